# Optimizing a Trainium2 kernel written in Bass

```python
import math
import jax
import jax.numpy as jnp
from jax import lax
import numpy as np

D_MODEL = 2048
BATCH = 2
SEQ = 8192
DEPTH = 2

F32 = jnp.float32
HEAD_DIM = 64
GROUP_WIDTH = D_MODEL // 4
D_MIX = 4 * GROUP_WIDTH
CONV_WIDTH = 3
RWKV_HEADS = GROUP_WIDTH // HEAD_DIM
RWKV_DECAY_RANK = 96
RWKV_A_RANK = 96
RWKV_GATE_RANK = 128
RWKV_GN_EPS = 64e-5
ATT_HEADS = GROUP_WIDTH // HEAD_DIM
ATT_KV_HEADS = max(1, ATT_HEADS // 4)
WINDOW = 128
ATT_BLOCK = 128
N_BUCKETS = 32
NEG_INF = -1e30
S5_CH = 16
S5_GROUPS = GROUP_WIDTH // S5_CH
S5_STATE = 64
N_EXPERT_GROUPS = 4
EXPERTS_PER_GROUP = 8
N_EXPERTS = N_EXPERT_GROUPS * EXPERTS_PER_GROUP
TOP_K = 2
D_EXPERT = D_MODEL // 4
MOE_BLOCK = 128
ALPHA = (2 * DEPTH) ** 0.25
BETA = (8 * DEPTH) ** -0.25
LN_EPS = 1e-5
RW_OFF = 3 * GROUP_WIDTH
RW_COLS = 3 * GROUP_WIDTH + RWKV_DECAY_RANK + RWKV_A_RANK + RWKV_GATE_RANK
ATT_OFF = RW_OFF + RW_COLS
ATT_Q = ATT_HEADS * HEAD_DIM
ATT_KV = ATT_KV_HEADS * HEAD_DIM
S5_OFF = ATT_OFF + ATT_Q + 2 * ATT_KV
N_IN = S5_OFF + GROUP_WIDTH

kernel_name = 'hybrid_parallel_heads_hmoe_deepnorm'


def layer_norm(x, gain=None, bias=None, eps=LN_EPS):
    x32 = x.astype(F32)
    mean = jnp.mean(x32, axis=-1, keepdims=True)
    var = jnp.mean(jnp.square(x32 - mean), axis=-1, keepdims=True)
    y = (x32 - mean) * lax.rsqrt(var + eps)
    if gain is not None:
        y = y * gain.astype(F32) + bias.astype(F32)
    return y


def ada_input(x, shift, scale):
    return (layer_norm(x) * (1.0 + scale[:, None, :]) + shift[:, None, :]).astype(x.dtype)


def short_conv_mixer(b_gate, c_gate, h, conv_w):
    z = c_gate * h
    z = lax.conv_general_dilated(z, conv_w[:, None, :].astype(z.dtype), window_strides=(1,),
                                 padding=[(CONV_WIDTH - 1, 0)],
                                 dimension_numbers=('NWC', 'WIO', 'NWC'),
                                 feature_group_count=z.shape[-1])
    return b_gate * z


def wkv7_scan(r, decay, k, v, a_vec, b_vec):
    Bsz, _, H, N = r.shape

    def step(state, inp):
        r_t, w_t, k_t, v_t, a_t, b_t = inp
        sa = jnp.einsum('bhvk,bhk->bhv', state, a_t)
        state = (state * w_t[:, :, None, :] + v_t[..., :, None] * k_t[..., None, :]
                 + sa[..., :, None] * b_t[..., None, :])
        return state, jnp.einsum('bhvk,bhk->bhv', state, r_t)

    xs = tuple(jnp.swapaxes(t, 0, 1) for t in (r, decay, k, v, a_vec, b_vec))
    _, ys = lax.scan(step, jnp.zeros((Bsz, H, N, N), F32), xs)
    return jnp.swapaxes(ys, 0, 1)


def rwkv7_mixer(cols, mu, w0, w2, a0, a2, g2, k_k, k_a, r_k, gn_g, gn_b):
    Bsz, S, _ = cols.shape
    G = GROUP_WIDTH
    p = cols.astype(F32)
    p_prev = jnp.pad(p, ((0, 0), (1, 0), (0, 0)))[:, :-1]
    p = p + (p_prev - p) * mu.astype(F32)
    r, k, v = p[..., :G], p[..., G:2 * G], p[..., 2 * G:3 * G]
    o = 3 * G
    w_lo = p[..., o:o + RWKV_DECAY_RANK]
    o += RWKV_DECAY_RANK
    a_lo = p[..., o:o + RWKV_A_RANK]
    o += RWKV_A_RANK
    g_lo = p[..., o:o + RWKV_GATE_RANK]
    w = -jax.nn.softplus(-(w0 + jnp.tanh(w_lo) @ w2)) - 0.5
    decay = jnp.exp(-jnp.exp(w))
    a = jax.nn.sigmoid(a0 + a_lo @ a2)
    g = jax.nn.sigmoid(g_lo) @ g2

    def heads(t):
        return t.reshape(Bsz, S, RWKV_HEADS, HEAD_DIM)

    kk = heads(k * k_k)
    kk = kk / jnp.maximum(jnp.sqrt(jnp.sum(kk * kk, axis=-1, keepdims=True)), 1e-12)
    k = k * (1.0 + (a - 1.0) * k_a)
    rh, kh, vh = heads(r), heads(k), heads(v)
    y = wkv7_scan(rh, heads(decay), kh, vh, -kk, kk * heads(a))
    mean = jnp.mean(y, axis=-1, keepdims=True)
    var = jnp.mean(jnp.square(y - mean), axis=-1, keepdims=True)
    y = ((y - mean) * lax.rsqrt(var + RWKV_GN_EPS)).reshape(Bsz, S, G) * gn_g + gn_b
    bonus = jnp.sum(rh * kh * r_k, axis=-1, keepdims=True) * vh
    return (y + bonus.reshape(Bsz, S, G)) * g


def t5_bucket(rel):
    n = jnp.maximum(rel, 0)
    max_exact = N_BUCKETS // 2
    n_f = jnp.maximum(n, 1).astype(F32)
    large = max_exact + (jnp.log(n_f / max_exact) / math.log(WINDOW / max_exact)
                         * (N_BUCKETS - max_exact)).astype(jnp.int32)
    return jnp.where(n < max_exact, n, jnp.minimum(large, N_BUCKETS - 1))


def swa_sink_attention(q, k, v, sinks, rel_bias):
    Bsz, S, _ = q.shape
    nb = S // ATT_BLOCK
    rep = ATT_HEADS // ATT_KV_HEADS
    qb = q.astype(F32).reshape(Bsz, nb, ATT_BLOCK, ATT_KV_HEADS, rep, HEAD_DIM)

    def with_prev(t):
        t = t.astype(F32).reshape(Bsz, nb, ATT_BLOCK, ATT_KV_HEADS, HEAD_DIM)
        prev = jnp.concatenate([jnp.zeros_like(t[:, :1]), t[:, :-1]], axis=1)
        return jnp.concatenate([prev, t], axis=2)

    kw, vw = with_prev(k), with_prev(v)
    scores = jnp.einsum('bnqgrd,bnkgd->bngrqk', qb, kw) * (HEAD_DIM ** -0.5)
    qi = jnp.arange(ATT_BLOCK)[:, None]
    kj = jnp.arange(2 * ATT_BLOCK)[None, :]
    rel = qi + ATT_BLOCK - kj
    bias = rel_bias.astype(F32)[t5_bucket(rel)]
    bias = jnp.transpose(bias, (2, 0, 1)).reshape(ATT_KV_HEADS, rep, ATT_BLOCK, 2 * ATT_BLOCK)
    key_pos = jnp.arange(nb)[:, None] * ATT_BLOCK + kj - ATT_BLOCK
    valid = ((rel >= 0) & (rel < WINDOW))[None] & (key_pos >= 0)[:, None, :]
    scores = jnp.where(valid[None, :, None, None], scores + bias, NEG_INF)
    sink = jnp.broadcast_to(sinks.astype(F32).reshape(1, 1, ATT_KV_HEADS, rep, 1, 1),
                            scores.shape[:-1] + (1,))
    probs = jax.nn.softmax(jnp.concatenate([scores, sink], axis=-1), axis=-1)[..., :-1]
    out = jnp.einsum('bngrqk,bnkgd->bnqgrd', probs, vw)
    return out.reshape(Bsz, S, ATT_Q)


def s5_mixer(u, lam_re, lam_im, log_dt, b_re, b_im, c_re, c_im, d_skip, glu_w, glu_b):
    Bsz, S, _ = u.shape
    u32 = u.astype(F32).reshape(Bsz, S, S5_GROUPS, S5_CH)
    lr, li = lam_re.astype(F32), lam_im.astype(F32)
    delta = jnp.exp(log_dt.astype(F32))[:, None]
    mag = jnp.exp(lr * delta)
    ab_re, ab_im = mag * jnp.cos(li * delta), mag * jnp.sin(li * delta)
    den = lr * lr + li * li
    z_re = ((ab_re - 1.0) * lr + ab_im * li) / den
    z_im = (ab_im * lr - (ab_re - 1.0) * li) / den
    br, bi = b_re.astype(F32), b_im.astype(F32)
    bb_re = z_re[..., None] * br - z_im[..., None] * bi
    bb_im = z_re[..., None] * bi + z_im[..., None] * br
    bu_re = jnp.einsum('gpc,bsgc->bsgp', bb_re, u32)
    bu_im = jnp.einsum('gpc,bsgc->bsgp', bb_im, u32)
    a_re = jnp.broadcast_to(ab_re, (1, S) + ab_re.shape)
    a_im = jnp.broadcast_to(ab_im, (1, S) + ab_im.shape)

    def combine(e1, e2):
        a1r, a1i, b1r, b1i = e1
        a2r, a2i, b2r, b2i = e2
        return (a2r * a1r - a2i * a1i, a2r * a1i + a2i * a1r,
                a2r * b1r - a2i * b1i + b2r, a2r * b1i + a2i * b1r + b2i)

    _, _, xr, xi = lax.associative_scan(combine, (a_re, a_im, bu_re, bu_im), axis=1)
    y = (jnp.einsum('gcp,bsgp->bsgc', c_re.astype(F32), xr)
         - jnp.einsum('gcp,bsgp->bsgc', c_im.astype(F32), xi)
         + d_skip.astype(F32) * u32)
    y = jax.nn.gelu(y.reshape(Bsz, S, GROUP_WIDTH))
    return y * jax.nn.sigmoid(y @ glu_w.astype(F32) + glu_b.astype(F32))


def hierarchical_moe(h, wg, bg, we, be, w1, w3, w2):
    Bsz, S, D = h.shape
    T = Bsz * S
    ht = h.reshape(T, D)
    h32 = ht.astype(F32)
    g_probs = jax.nn.softmax(h32 @ wg.astype(F32) + bg.astype(F32), axis=-1)
    g_val, g_idx = lax.top_k(g_probs, 1)
    e_logits = (h32 @ we.astype(F32) + be.astype(F32)).reshape(T, N_EXPERT_GROUPS, EXPERTS_PER_GROUP)
    e_logits = jnp.take_along_axis(e_logits, g_idx[:, :, None], axis=1)[:, 0]
    e_val, e_idx = lax.top_k(e_logits, TOP_K)
    weights = jax.nn.softmax(e_val, axis=-1) * g_val
    expert_id = g_idx * EXPERTS_PER_GROUP + e_idx
    n_assign = T * TOP_K
    n_rows = ((n_assign + N_EXPERTS * (MOE_BLOCK - 1) + MOE_BLOCK - 1) // MOE_BLOCK) * MOE_BLOCK
    n_blocks = n_rows // MOE_BLOCK
    flat_e = expert_id.reshape(n_assign)
    flat_tok = jnp.repeat(jnp.arange(T, dtype=jnp.int32), TOP_K)
    order = jnp.argsort(flat_e)
    se, stok, sw = flat_e[order], flat_tok[order], weights.reshape(n_assign)[order]
    counts = jnp.bincount(flat_e, length=N_EXPERTS)
    padded = ((counts + MOE_BLOCK - 1) // MOE_BLOCK) * MOE_BLOCK
    pad_end = jnp.cumsum(padded)
    pad_start = pad_end - padded
    raw_start = jnp.cumsum(counts) - counts
    dest = pad_start[se] + (jnp.arange(n_assign) - raw_start[se])
    row_tok = jnp.zeros((n_rows,), jnp.int32).at[dest].set(stok)
    block_expert = jnp.minimum(
        jnp.searchsorted(pad_end, jnp.arange(n_blocks) * MOE_BLOCK, side='right'), N_EXPERTS - 1)
    xb = ht[row_tok].reshape(n_blocks, MOE_BLOCK, D)

    def expert_block(args):
        xblk, e = args
        return (jax.nn.silu(xblk @ w1[e]) * (xblk @ w3[e])) @ w2[e]

    yr = lax.map(expert_block, (xb, block_expert)).reshape(n_rows, D)
    ya = yr[dest].astype(F32) * sw[:, None]
    return jax.ops.segment_sum(ya, stok, num_segments=T).reshape(Bsz, S, D)


def setup_inputs(seed: int = 0) -> dict:
    key = jax.random.key(seed)
    keys = jax.random.split(key, 48)
    counter = [0]

    def nxt():
        k = keys[counter[0]]
        counter[0] += 1
        return k

    def nrm(shape, scale):
        return jax.random.normal(nxt(), shape, F32) * scale

    def unif(shape, lo, hi):
        return jax.random.uniform(nxt(), shape, F32, lo, hi)

    L, D, G = DEPTH, D_MODEL, GROUP_WIDTH
    col_scale = np.ones((N_IN,), np.float32)
    col_scale[RW_OFF + 2 * G:RW_OFF + 3 * G] = BETA
    col_scale[ATT_OFF + ATT_Q + ATT_KV:ATT_OFF + ATT_Q + 2 * ATT_KV] = BETA
    return {
        'x': nrm((BATCH, SEQ, D), 1.0),
        'c': nrm((BATCH, D), 1.0),
        'w_ada': nrm((L, D, 6 * D), 0.2 * D ** -0.5),
        'b_ada': nrm((L, 6 * D), 0.02),
        'ln_g': 1.0 + nrm((L, 2, D), 0.02),
        'ln_b': nrm((L, 2, D), 0.02),
        'w_in': nrm((L, D, N_IN), D ** -0.5) * jnp.asarray(col_scale),
        'w_out': nrm((L, D_MIX, D), BETA * D_MIX ** -0.5),
        'conv_w': nrm((L, CONV_WIDTH, G), CONV_WIDTH ** -0.5),
        'rwkv_mu': unif((L, RW_COLS), 0.0, 1.0),
        'rwkv_w0': jnp.linspace(-6.0, -1.0, G, dtype=F32)[None] + nrm((L, G), 0.1),
        'rwkv_w2': nrm((L, RWKV_DECAY_RANK, G), 0.1 * RWKV_DECAY_RANK ** -0.5),
        'rwkv_a0': nrm((L, G), 0.1),
        'rwkv_a2': nrm((L, RWKV_A_RANK, G), 0.5 * RWKV_A_RANK ** -0.5),
        'rwkv_g2': nrm((L, RWKV_GATE_RANK, G), RWKV_GATE_RANK ** -0.5),
        'rwkv_kk': 0.85 + nrm((L, G), 0.02),
        'rwkv_ka': 1.0 + nrm((L, G), 0.02),
        'rwkv_rk': -0.04 + nrm((L, RWKV_HEADS, HEAD_DIM), 0.02),
        'rwkv_gn_g': 1.0 + nrm((L, G), 0.02),
        'rwkv_gn_b': nrm((L, G), 0.02),
        'attn_sinks': nrm((L, ATT_HEADS), 0.5),
        'rel_bias': nrm((N_BUCKETS, ATT_HEADS), 0.5),
        's5_lambda_re': -0.5 + nrm((L, S5_GROUPS, S5_STATE), 0.01),
        's5_lambda_im': math.pi * jnp.arange(S5_STATE, dtype=F32)[None, None] + nrm((L, S5_GROUPS, S5_STATE), 0.01),
        's5_log_dt': unif((L, S5_GROUPS), math.log(1e-3), math.log(1e-1)),
        's5_b_re': nrm((L, S5_GROUPS, S5_STATE, S5_CH), (2 * S5_CH) ** -0.5),
        's5_b_im': nrm((L, S5_GROUPS, S5_STATE, S5_CH), (2 * S5_CH) ** -0.5),
        's5_c_re': nrm((L, S5_GROUPS, S5_CH, S5_STATE), S5_STATE ** -0.5),
        's5_c_im': nrm((L, S5_GROUPS, S5_CH, S5_STATE), S5_STATE ** -0.5),
        's5_d': nrm((L, S5_GROUPS, S5_CH), 0.5),
        's5_glu_w': nrm((L, G, G), G ** -0.5),
        's5_glu_b': nrm((L, G), 0.02),
        'router_group_w': nrm((L, D, N_EXPERT_GROUPS), D ** -0.5),
        'router_group_b': nrm((L, N_EXPERT_GROUPS), 0.01),
        'router_expert_w': nrm((L, D, N_EXPERTS), D ** -0.5),
        'router_expert_b': nrm((L, N_EXPERTS), 0.01),
        'moe_w1': nrm((L, N_EXPERTS, D, D_EXPERT), BETA * D ** -0.5),
        'moe_w3': nrm((L, N_EXPERTS, D, D_EXPERT), BETA * D ** -0.5),
        'moe_w2': nrm((L, N_EXPERTS, D_EXPERT, D), BETA * D_EXPERT ** -0.5),
    }


def reference(x, c, w_ada, b_ada, ln_g, ln_b, w_in, w_out, conv_w, rwkv_mu, rwkv_w0, rwkv_w2,
              rwkv_a0, rwkv_a2, rwkv_g2, rwkv_kk, rwkv_ka, rwkv_rk, rwkv_gn_g, rwkv_gn_b,
              attn_sinks, rel_bias, s5_lambda_re, s5_lambda_im, s5_log_dt, s5_b_re, s5_b_im,
              s5_c_re, s5_c_im, s5_d, s5_glu_w, s5_glu_b, router_group_w, router_group_b,
              router_expert_w, router_expert_b, moe_w1, moe_w3, moe_w2):
    dt = x.dtype
    G = GROUP_WIDTH
    for l in range(DEPTH):
        mod = (jax.nn.silu(c) @ w_ada[l] + b_ada[l]).astype(F32)
        sh1, sc1, gt1, sh2, sc2, gt2 = jnp.split(mod, 6, axis=-1)
        h = ada_input(x, sh1, sc1)
        p = h @ w_in[l]
        y_conv = short_conv_mixer(p[..., 0:G], p[..., G:2 * G], p[..., 2 * G:3 * G], conv_w[l])
        y_rwkv = rwkv7_mixer(p[..., RW_OFF:ATT_OFF], rwkv_mu[l], rwkv_w0[l], rwkv_w2[l],
                             rwkv_a0[l], rwkv_a2[l], rwkv_g2[l], rwkv_kk[l], rwkv_ka[l],
                             rwkv_rk[l], rwkv_gn_g[l], rwkv_gn_b[l])
        y_att = swa_sink_attention(p[..., ATT_OFF:ATT_OFF + ATT_Q],
                                   p[..., ATT_OFF + ATT_Q:ATT_OFF + ATT_Q + ATT_KV],
                                   p[..., ATT_OFF + ATT_Q + ATT_KV:S5_OFF],
                                   attn_sinks[l], rel_bias)
        y_ssm = s5_mixer(p[..., S5_OFF:N_IN], s5_lambda_re[l], s5_lambda_im[l], s5_log_dt[l],
                         s5_b_re[l], s5_b_im[l], s5_c_re[l], s5_c_im[l], s5_d[l],
                         s5_glu_w[l], s5_glu_b[l])
        y_mix = jnp.concatenate([t.astype(dt) for t in (y_conv, y_rwkv, y_att, y_ssm)], axis=-1) @ w_out[l]
        x = layer_norm(ALPHA * x.astype(F32) + (1.0 + gt1)[:, None, :] * y_mix.astype(F32),
                       ln_g[l, 0], ln_b[l, 0]).astype(dt)
        h = ada_input(x, sh2, sc2)
        y_moe = hierarchical_moe(h, router_group_w[l], router_group_b[l], router_expert_w[l],
                                 router_expert_b[l], moe_w1[l], moe_w3[l], moe_w2[l])
        x = layer_norm(ALPHA * x.astype(F32) + (1.0 + gt2)[:, None, :] * y_moe.astype(F32),
                       ln_g[l, 1], ln_b[l, 1]).astype(dt)
    return x
```

```python
import numpy as np
from contextlib import ExitStack
import concourse.bass as bass
import concourse.mybir as mybir
from concourse.bass_utils import run_bass_kernel_spmd

F32 = mybir.dt.float32
BF16 = mybir.dt.bfloat16
I32 = mybir.dt.int32
U32 = mybir.dt.uint32
AF = mybir.ActivationFunctionType
ALU = mybir.AluOpType
AX = mybir.AxisListType

EPOCH = 1 << 20


class Buf:
    __slots__ = ("name", "w", "r")

    def __init__(self, name=""):
        self.name = name
        self.w = None
        self.r = {}


class MK:
    def __init__(self, nc, ctx, n_dma_sems=24):
        self.nc = nc
        self.ctx = ctx
        self.eng = {"pe": nc.tensor, "dve": nc.vector, "act": nc.scalar,
                    "pool": nc.gpsimd, "sp": nc.sync}
        self.sem = {}
        self.cnt = {e: 0 for e in self.eng}
        self.known = {e: {} for e in self.eng}
        for e in self.eng:
            self.sem[e] = ctx.enter_context(nc.semaphore("s_" + e))
        self.dma_keys = []
        self.dma_val = {}
        for i in range(n_dma_sems):
            k = ("dma", i)
            self.sem[k] = ctx.enter_context(nc.semaphore("s_dma%d" % i))
            self.dma_keys.append(k)
            self.dma_val[k] = 0
        self.dma_rr = 0
        self.nwaits = 0
        self.nops = 0
        self.out_events = []

    gen = 0

    def sb(self, name, shape, dt=F32):
        return self.ctx.enter_context(self.nc.sbuf_tensor("%s_g%d" % (name, self.gen), list(shape), dt))

    def ps(self, name, shape, dt=F32):
        return self.ctx.enter_context(self.nc.psum_tensor("%s_g%d" % (name, self.gen), list(shape), dt))

    def _wait(self, E, ev):
        if ev is None:
            return
        key, val = ev
        if self.known[E].get(key, 0) >= val:
            return
        self.eng[E].wait_ge(self.sem[key], val)
        self.known[E][key] = val
        self.nwaits += 1

    def _deps(self, E, reads, writes, skip_same=False):
        for b in reads:
            if b.w is not None and not (skip_same and b.w[0] == E):
                self._wait(E, b.w)
        for b in writes:
            if b.w is not None and not (skip_same and b.w[0] == E):
                self._wait(E, b.w)
            for ev in b.r.values():
                if not (skip_same and ev[0] == E):
                    self._wait(E, ev)

    def _mark(self, ev, reads, writes):
        for b in reads:
            b.r[ev[0]] = ev
        for b in writes:
            b.w = ev
            b.r = {}

    def op(self, E, fn, reads=(), writes=(), skip_same=False):
        self._deps(E, reads, writes, skip_same)
        inst = fn()
        self.cnt[E] += 1
        inst.then_inc(self.sem[E], 1)
        ev = (E, self.cnt[E])
        self._mark(ev, reads, writes)
        self.nops += 1
        return ev

    def dma(self, Q, out, in_, reads=(), writes=(), is_output=False, **kw):
        self._deps(Q, reads, writes)
        k = self.dma_keys[self.dma_rr]
        self.dma_rr = (self.dma_rr + 1) % len(self.dma_keys)
        self._wait(Q, (k, self.dma_val[k]) if self.dma_val[k] else None)
        self.dma_val[k] += 16
        inst = self.eng[Q].dma_start(out=out, in_=in_, **kw)
        inst.then_inc(self.sem[k], 16)
        ev = (k, self.dma_val[k])
        self._mark(ev, reads, writes)
        if is_output:
            self.out_events.append(ev)
        self.nops += 1
        return ev

    def finish(self, E="sp"):
        for k in self.dma_keys:
            if self.dma_val[k]:
                self._wait(E, (k, self.dma_val[k]))


def _idma(self, out, in_, idx_ap, scatter, bound, reads=(), writes=(), is_output=False):
    Q = "pool"
    self._deps(Q, reads, writes)
    k = self.dma_keys[self.dma_rr]
    self.dma_rr = (self.dma_rr + 1) % len(self.dma_keys)
    self._wait(Q, (k, self.dma_val[k]) if self.dma_val[k] else None)
    self.dma_val[k] += 16
    off = bass.IndirectOffsetOnAxis(ap=idx_ap, axis=0)
    if not hasattr(self, "_bregs"):
        self._bregs = {}
    if bound not in self._bregs:
        self._bregs[bound] = self.nc.gpsimd.to_reg(bound)
    bound = self._bregs[bound]
    if scatter:
        inst = self.nc.gpsimd.indirect_dma_start(out=out, out_offset=off, in_=in_, in_offset=None, bounds_check=bound, oob_is_err=False)
    else:
        inst = self.nc.gpsimd.indirect_dma_start(out=out, out_offset=None, in_=in_, in_offset=off, bounds_check=bound, oob_is_err=False)
    inst.then_inc(self.sem[k], 16)
    ev = (k, self.dma_val[k])
    self._mark(ev, reads, writes)
    if is_output:
        self.out_events.append(ev)
    self.nops += 1
    return ev


MK.idma = _idma


def _barrier(self):
    for E in self.eng:
        for F in self.eng:
            if self.cnt[F]:
                self._wait(E, (F, self.cnt[F]))
        for k in self.dma_keys:
            if self.dma_val[k]:
                self._wait(E, (k, self.dma_val[k]))
        if getattr(self, "cc_val", 0):
            self._wait(E, ("cc", self.cc_val))


MK.barrier = _barrier


from contextlib import contextmanager


@contextmanager
def _scope(self):
    old = self.ctx
    self.gen += 1
    with ExitStack() as s:
        self.ctx = s
        yield
        self.barrier()
    self.ctx = old


MK.scope = _scope


def _collective(self, kind, rg, in_ap, out_ap, reads=(), writes=()):
    Q = "pool"
    if "cc" not in self.sem:
        self.sem["cc"] = self.ctx.enter_context(self.nc.semaphore("s_cc"))
        self.cc_val = 0
    self._deps(Q, reads, writes)
    self.cc_val += 1
    inst = self.nc.gpsimd.collective_compute(kind, ALU.bypass, replica_groups=rg, ins=[in_ap.opt()], outs=[out_ap.opt()])
    inst.then_inc(self.sem["cc"], 1)
    ev = ("cc", self.cc_val)
    self._mark(ev, reads, writes)
    self.nops += 1
    return ev


MK.collective = _collective


import numpy as np
from contextlib import ExitStack

D = 2048
NIN = 4672
TOK = 2048


def build_k0():
    nc = bass.Bass("TRN2", target_bir_lowering=False)
    NCOL = 1536
    cT = nc.dram_tensor("cT", [128, 16, 2], F32, kind="ExternalInput").ap()
    w = nc.dram_tensor("w", [2, D, NCOL], F32, kind="ExternalInput").ap()
    bb = nc.dram_tensor("bb", [2, 2, NCOL], F32, kind="ExternalInput").ap()
    out = nc.dram_tensor("mod", [2, 2, NCOL], F32, kind="ExternalOutput").ap()
    with ExitStack() as ctx:
        mk = MK(nc, ctx)
        ct = mk.sb("ct", [128, 16, 2])
        sct = mk.sb("sct", [128, 16, 2])
        wt = [mk.sb("wt%d" % i, [128, 16, 512]) for i in range(2)]
        bt = mk.sb("bt", [2, 2, NCOL])
        ot = mk.sb("ot", [2, 2, NCOL])
        P = [mk.ps("P%d" % i, [2, 512]) for i in range(2)]
        b_c, b_b, b_o = Buf(), Buf(), Buf()
        b_w = [Buf(), Buf()]
        b_p = [Buf(), Buf()]
        mk.dma("sp", ct[:], cT, writes=[b_c])
        mk.dma("sp", bt[:], bb.rearrange("l b c -> b l c"), writes=[b_b])
        mk.op("act", lambda: nc.scalar.activation(out=sct[:], in_=ct[:], func=AF.Silu), reads=[b_c], writes=[b_c])
        it = 0
        for l in range(2):
            for n in range(3):
                i = it % 2
                it += 1
                mk.dma("sp", wt[i][:], w[l, :, n * 512:(n + 1) * 512].rearrange("(k p) c -> p k c", p=128), writes=[b_w[i]])
                for k in range(16):
                    mk.op("pe", lambda: nc.tensor.matmul(P[i][:], lhsT=sct[:, k, :], rhs=wt[i][:, k, :], start=(k == 0), stop=(k == 15)),
                          reads=[b_c, b_w[i]], writes=[b_p[i]], skip_same=True)
                mk.op("dve", lambda: nc.vector.tensor_tensor(out=ot[:, l, n * 512:(n + 1) * 512], in0=P[i][:], in1=bt[:, l, n * 512:(n + 1) * 512], op=ALU.add),
                      reads=[b_p[i], b_b], writes=[b_o])
        mk.dma("sp", out.rearrange("l b c -> b l c"), ot[:], reads=[b_o], is_output=True)
        mk.finish("sp")
    return nc


def build_k1():
    nc = bass.Bass("TRN2", target_bir_lowering=False)
    x = nc.dram_tensor("x", [TOK, D], F32, kind="ExternalInput").ap()
    sc = nc.dram_tensor("sc", [128, D], F32, kind="ExternalInput").ap()
    sh = nc.dram_tensor("sh", [128, D], F32, kind="ExternalInput").ap()
    w_in = nc.dram_tensor("w_in", [D, NIN], F32, kind="ExternalInput").ap()
    ident = nc.dram_tensor("ident", [128, 128], F32, kind="ExternalInput").ap()
    pT = nc.dram_tensor("pT", [NIN, TOK], F32, kind="ExternalOutput").ap()
    with ExitStack() as ctx:
        mk = MK(nc, ctx)
        emit_k1(nc, mk, x, sc, sh, w_in, ident, pT)
        mk.finish("sp")
        print("k1 ops", mk.nops, "waits", mk.nwaits)
    return nc


def ln_stats(nc, mk, xt, bx, st, mv, rs, nmr, bs, eps=1e-5):
    for c in range(4):
        mk.op("dve", lambda: nc.vector.bn_stats(out=st[:, c, :], in_=xt[:, c * 512:(c + 1) * 512]), reads=[bx], writes=[bs])
    mk.op("dve", lambda: nc.vector.bn_aggr(out=mv[:], in_=st[:].rearrange("p a b -> p (a b)")), reads=[bs], writes=[bs])
    mk.op("act", lambda: nc.scalar.activation(out=rs[:], in_=mv[:, 1:2], func=AF.Sqrt, bias=eps, scale=1.0), reads=[bs], writes=[bs])
    mk.op("dve", lambda: nc.vector.reciprocal(out=rs[:], in_=rs[:]), reads=[bs], writes=[bs])
    mk.op("dve", lambda: nc.vector.tensor_scalar(out=nmr[:], in0=mv[:, 0:1], scalar1=rs[:, 0:1], scalar2=-1.0, op0=ALU.mult, op1=ALU.mult),
          reads=[bs], writes=[bs])


def emit_k1(nc, mk, x, sc, sh, w_in, ident, pT):
    NT = TOK // 128
    xt = [mk.sb("xt%d" % i, [128, D]) for i in range(2)]
    xn = mk.sb("xn", [128, D])
    h1 = mk.sb("h1", [128, D])
    hb = [mk.sb("hb%d" % i, [128, D], BF16) for i in range(2)]
    hT = mk.sb("hT", [128, 16, TOK], BF16)
    sct = mk.sb("sct", [128, D])
    sht = mk.sb("sht", [128, D])
    idf = mk.sb("idf", [128, 128])
    idb = mk.sb("idb", [128, 128], BF16)
    st = mk.sb("st", [128, 4, 6])
    mv = mk.sb("mv", [128, 2])
    rs = mk.sb("rs", [128, 1])
    nmr = mk.sb("nmr", [128, 1])
    wt = [mk.sb("wt%d" % i, [128, 16, 128], BF16) for i in range(2)]
    ot = [mk.sb("ot%d" % i, [128, TOK]) for i in range(2)]
    PT = [mk.ps("PT%d" % i, [128, 8, 128], BF16) for i in range(2)]
    PM = [mk.ps("PM%d" % i, [128, 512]) for i in range(4)]
    b_x = [Buf(), Buf()]
    b_xn, b_h1, b_s, b_sc, b_sh, b_id, b_hT = Buf(), Buf(), Buf(), Buf(), Buf(), Buf(), Buf()
    b_hb = [Buf(), Buf()]
    b_pt = [Buf(), Buf()]
    b_pm = [Buf() for _ in range(4)]
    b_w = [Buf(), Buf()]
    b_o = [Buf(), Buf()]

    mk.dma("sp", sct[:], sc, writes=[b_sc])
    mk.dma("sp", sht[:], sh, writes=[b_sh])
    mk.dma("sp", idf[:], ident, writes=[b_id])
    mk.op("dve", lambda: nc.vector.tensor_copy(out=idb[:], in_=idf[:]), reads=[b_id], writes=[b_id])
    mk.op("pool", lambda: nc.gpsimd.tensor_scalar(out=sct[:], in0=sct[:], scalar1=1.0, scalar2=None, op0=ALU.add), reads=[b_sc], writes=[b_sc])

    NCB = (NIN + 127) // 128

    def load_w(cb):
        j = cb % 2
        c0 = cb * 128
        cw = min(128, NIN - c0)
        mk.dma("pool", wt[j][:, :, 0:cw], w_in[:, c0:c0 + cw].rearrange("(k p) c -> p k c", p=128), writes=[b_w[j]])

    load_w(0)
    load_w(1)
    for t in range(NT):
        i = t % 2
        mk.dma("sp", xt[i][:], x[t * 128:(t + 1) * 128, :], writes=[b_x[i]])
        ln_stats(nc, mk, xt[i], b_x[i], st, mv, rs, nmr, b_s)
        mk.op("act", lambda: nc.scalar.activation(out=xn[:], in_=xt[i][:], func=AF.Identity, bias=nmr[:, 0:1], scale=rs[:, 0:1]),
              reads=[b_x[i], b_s], writes=[b_xn])
        mk.op("dve", lambda: nc.vector.tensor_tensor(out=h1[:], in0=xn[:], in1=sct[:], op=ALU.mult), reads=[b_xn, b_sc], writes=[b_h1])
        mk.op("pool", lambda: nc.gpsimd.tensor_tensor(out=hb[i][:], in0=h1[:], in1=sht[:], op=ALU.add), reads=[b_h1, b_sh], writes=[b_hb[i]])
        for half in range(2):
            for kk in range(8):
                k = half * 8 + kk
                mk.op("pe", lambda: nc.tensor.transpose(PT[half][:, kk, :], hb[i][:, k * 128:(k + 1) * 128], idb[:]),
                      reads=[b_hb[i], b_id], writes=[b_pt[half]], skip_same=True)
            eng = "act" if half == 0 else "dve"
            if eng == "act":
                mk.op("act", lambda: nc.scalar.copy(out=hT[:, half * 8:(half + 1) * 8, t * 128:(t + 1) * 128], in_=PT[half][:]),
                      reads=[b_pt[half]], writes=[b_hT])
            else:
                mk.op("dve", lambda: nc.vector.tensor_copy(out=hT[:, half * 8:(half + 1) * 8, t * 128:(t + 1) * 128], in_=PT[half][:]),
                      reads=[b_pt[half]], writes=[b_hT])
    pi = 0
    for cb in range(NCB):
        j = cb % 2
        c0 = cb * 128
        cw = min(128, NIN - c0)
        for tc in range(TOK // 512):
            q = pi % 4
            pi += 1
            for k in range(16):
                mk.op("pe", lambda: nc.tensor.matmul(PM[q][0:cw, :], lhsT=wt[j][:, k, 0:cw], rhs=hT[:, k, tc * 512:(tc + 1) * 512],
                                                     start=(k == 0), stop=(k == 15)),
                      reads=[b_w[j], b_hT], writes=[b_pm[q]], skip_same=True)
            if tc % 2 == 0:
                mk.op("act", lambda: nc.scalar.copy(out=ot[j][0:cw, tc * 512:(tc + 1) * 512], in_=PM[q][0:cw, :]), reads=[b_pm[q]], writes=[b_o[j]])
            else:
                mk.op("dve", lambda: nc.vector.tensor_copy(out=ot[j][0:cw, tc * 512:(tc + 1) * 512], in_=PM[q][0:cw, :]), reads=[b_pm[q]], writes=[b_o[j]])
        mk.dma("sp", pT[c0:c0 + cw, :], ot[j][0:cw, :], reads=[b_o[j]], is_output=True)
        if cb + 2 < NCB:
            load_w(cb + 2)


import numpy as np
from contextlib import ExitStack

C = 64
TB = 512
NCH = TB // C


def rwkv_consts():
    s = np.arange(64)[:, None]
    t = np.arange(64)[None, :]
    m_su = (s < t).astype(np.float32)
    m_ui = (s <= t).astype(np.float32)
    m1 = np.concatenate([m_su, m_ui], axis=1)
    mask1 = np.tile(m1, (1, 4))
    m_sl = (t < s).astype(np.float32)
    mask3 = np.tile(m_sl, (1, 4))
    seg = np.ones((128, TB), np.float32)
    seg[:, ::C] = 0.0
    return {"mask1": mask1, "mask3": mask3, "seg": seg, "ident": np.eye(128, dtype=np.float32)}


def emit_rwkv(nc, mk, T, rwin, par64, par128, w2, a2, g2, gnt, cst, yT, odt=F32):
    import os
    LVL = int(os.environ.get("RW_LVL", "9"))
    NB = T // TB
    V = lambda fn, r=(), w=(): mk.op("dve", fn, r, w)
    A = lambda fn, r=(), w=(): mk.op("act", fn, r, w)
    G = lambda fn, r=(), w=(): mk.op("pool", fn, r, w)
    PE = lambda fn, r=(), w=(), ss=True: mk.op("pe", fn, r, w, skip_same=ss)
    import os
    F32R = mybir.dt.float32r
    USE_R = os.environ.get("RW_F32R", "1") == "1"
    RR = (lambda a: a.bitcast(F32R)) if USE_R else (lambda a: a)
    sb = mk.sb
    p64 = sb("rw_p64", [64, 2, 11]); p128 = sb("rw_p128", [128, 3])
    w2t = sb("rw_w2", [96, 128]); a2t = sb("rw_a2", [96, 128]); g2t = sb("rw_g2", [128, 128])
    gn = sb("rw_gn", [64, 2, 2, 64])
    mask1 = sb("rw_mask1", [64, 512]); mask3 = sb("rw_mask3", [64, 256]); seg = sb("rw_seg", [128, TB])
    ident = sb("rw_ident", [128, 128])
    ones64 = sb("rw_ones", [64, 64])
    bc = Buf("const")
    for dst, src in ((p64, par64), (p128, par128), (w2t, w2), (a2t, a2), (g2t, g2), (gn, gnt),
                     (mask1, cst["mask1"]), (mask3, cst["mask3"]), (seg, cst["seg"]), (ident, cst["ident"])):
        mk.dma("sp", dst[:], src, writes=[bc])
    V(lambda: nc.vector.memset(ones64[:], 1.0), w=[bc])
    raw = {}
    for nm in ("r0", "k0", "v0", "r1", "k1", "v1"):
        raw[nm] = sb("rw_raw_" + nm, [64, TB + 1])
    raw["w"] = sb("rw_raw_w", [96, TB + 1]); raw["a"] = sb("rw_raw_a", [96, TB + 1]); raw["g"] = sb("rw_raw_g", [128, TB + 1])
    b_raw = Buf("raw")
    tmp = sb("rw_tmp", [128, TB]); b_tmp = Buf("tmp")
    ws = sb("rw_ws", [96, TB]); as_ = sb("rw_as", [96, TB]); gs = sb("rw_gs", [128, TB]); b_lo = Buf("lo")
    gate = [sb("rw_gate%d" % i_, [128, TB]) for i_ in range(2)]; b_gate = [Buf("gate0"), Buf("gate1")]
    H = []
    for h in range(2):
        d = {}
        for nm in ("rs", "ks", "vs", "lw", "asg", "kkn", "kp", "bv", "cum", "e1", "e2", "BT", "KT", "BH", "KH", "rkr"):
            d[nm] = sb("rw_%s%d" % (nm, h), [64, TB])
        d["AR"] = sb("rw_AR%d" % h, [64, NCH, 128])
        d["cC"] = sb("rw_cC%d" % h, [64, NCH]); d["gC"] = sb("rw_gC%d" % h, [64, NCH])
        d["b"] = Buf("H%d" % h)
        d["bo"] = [Buf("Ho%d_0" % h), Buf("Ho%d_1" % h)]
        d["Vt"] = sb("rw_Vt%d" % h, [64, NCH, 64]); d["BHt"] = sb("rw_BHt%d" % h, [64, NCH, 64]); d["KHt"] = sb("rw_KHt%d" % h, [64, NCH, 64])
        d["bt"] = [Buf("Ht%d_0" % h), Buf("Ht%d_1" % h)]
        for nm_, shp_ in (("AR", [64, NCH, 128]), ("BT", [64, TB]), ("KT", [64, TB]), ("gC", [64, NCH]), ("Vt", [64, NCH, 64]), ("BHt", [64, NCH, 64]), ("KHt", [64, NCH, 64])):
            d[nm_] = [d[nm_], sb("rw_%s%d_b" % (nm_, h), shp_)]
        d["NG"] = sb("rw_NG%d" % h, [64, NCH, 128]); d["LG"] = sb("rw_LG%d" % h, [64, NCH, 128])
        d["L"] = sb("rw_L%d" % h, [64, NCH, 64]); d["bA"] = Buf("A%d" % h)
        d["P"] = [sb("rw_P%d_%d" % (h, i), [64, NCH, 64]) for i in range(2)]
        d["PT"] = [sb("rw_PT%d_%d" % (h, i), [64, NCH, 64]) for i in range(2)]
        d["ST"] = [sb("rw_ST%d_%d" % (h, i), [64, NCH, 64]) for i in range(2)]
        d["bD"] = Buf("D%d" % h)
        d["M"] = sb("rw_M%d" % h, [64, 64]); d["bM"] = Buf("M%d" % h)
        d["X1"] = sb("rw_X1%d" % h, [64, 64]); d["U"] = sb("rw_U%d" % h, [64, 64]); d["bX"] = Buf("X%d" % h); d["bU"] = Buf("U%d" % h)
        H.append(d)
    Yb = sb("rw_Yb", [64, NCH, 2, 64]); b_Y = Buf("Y")
    Ysq = sb("rw_Ysq", [64, NCH, 2, 64])
    st1 = sb("rw_st1", [64, NCH * 2]); st2 = sb("rw_st2", [64, NCH * 2]); st3 = sb("rw_st3", [64, NCH * 2]); b_st = Buf("st")
    sbon = [sb("rw_sbon%d" % i_, [64, NCH, 2]) for i_ in range(2)]; b_sb = [Buf("sbon0"), Buf("sbon1")]
    yo = sb("rw_yo", [128, TB], odt); b_yo = Buf("yo")
    ps_lo = mk.ps("rw_ps_lo", [128, 512]); b_pl = Buf()
    ps_tr = mk.ps("rw_ps_tr", [128, 512]); b_ptr = Buf()
    ps_a1 = mk.ps("rw_ps_a1", [64, 512]); b_pa1 = Buf()
    ps_a2 = mk.ps("rw_ps_a2", [64, 512]); b_pa2 = Buf()
    ps_a3f = mk.ps("rw_ps_a3", [128, 512]); ps_a3 = ps_a3f[0:64, :]; b_pa3 = Buf()
    ps_d = mk.ps("rw_ps_d", [64, 512]); b_pd = Buf()
    ps_d2 = ps_a3; b_pd2 = b_pa3
    ps_sh = [mk.ps("rw_ps_s%d" % h, [64, 512]) for h in range(2)]
    b_psh = [Buf(), Buf()]

    for h in range(2):
        V(lambda: nc.vector.tensor_scalar(out=RR(H[h]["M"][:]), in0=ident[0:64, 0:64], scalar1=0.0, scalar2=None, op0=ALU.mult), r=[bc], w=[H[h]["bM"]])

    rows = {"r0": 0, "r1": 64, "k0": 128, "k1": 192, "v0": 256, "v1": 320, "w": 384, "a": 480, "g": 576}
    nrow = {"r0": 64, "r1": 64, "k0": 64, "k1": 64, "v0": 64, "v1": 64, "w": 96, "a": 96, "g": 128}

    def stage1(blk):
        t0 = blk * TB
        par = blk % 2
        for nm in rows:
            r0, n = rows[nm], nrow[nm]
            if blk == 0:
                V(lambda: nc.vector.memset(raw[nm][0:n, 0:1], 0.0), w=[b_raw])
                yield
                mk.dma("sp", raw[nm][0:n, 1:TB + 1], rwin[r0:r0 + n, 0:TB], writes=[b_raw])
                yield
            else:
                mk.dma("sp", raw[nm][0:n, :], rwin[r0:r0 + n, t0 - 1:t0 + TB], writes=[b_raw])
                yield

        def shift(dst, src, n, mu_ap, bdst):
            V(lambda: nc.vector.tensor_tensor(out=tmp[0:n, :], in0=src[0:n, 0:TB], in1=src[0:n, 1:TB + 1], op=ALU.subtract), r=[b_raw], w=[b_tmp])
            V(lambda: nc.vector.scalar_tensor_tensor(out=dst[0:n, :], in0=tmp[0:n, :], scalar=mu_ap, in1=src[0:n, 1:TB + 1], op0=ALU.mult, op1=ALU.add),
              r=[b_tmp, b_raw, bc], w=[bdst])

        shift(ws, raw["w"], 96, p128[0:96, 0:1], b_lo)
        yield
        shift(as_, raw["a"], 96, p128[0:96, 1:2], b_lo)
        yield
        shift(gs, raw["g"], 128, p128[:, 2:3], b_lo)
        yield
        A(lambda: nc.scalar.activation(out=ws[:], in_=ws[:], func=AF.Tanh), r=[b_lo], w=[b_lo])
        yield
        A(lambda: nc.scalar.activation(out=gs[:], in_=gs[:], func=AF.Sigmoid), r=[b_lo], w=[b_lo])
        yield
        PE(lambda: nc.tensor.matmul(ps_lo[:, :], lhsT=g2t[:, :], rhs=gs[:, :], start=True, stop=True), r=[bc, b_lo], w=[b_pl])
        yield
        A(lambda: nc.scalar.copy(out=gate[par][:], in_=ps_lo[:, :]), r=[b_pl], w=[b_gate[par]])
        yield
        for h in range(2):
            d = H[h]; b = d["b"]; bo = d["bo"][par]
            hs = slice(64 * h, 64 * h + 64)
            shift(d["rs"], raw["r%d" % h], 64, p64[:, h, 0:1], b)
            yield
            shift(d["ks"], raw["k%d" % h], 64, p64[:, h, 1:2], b)
            yield
            shift(d["vs"], raw["v%d" % h], 64, p64[:, h, 2:3], b)
            yield
            PE(lambda: nc.tensor.matmul(ps_lo[0:64, :], lhsT=w2t[:, hs], rhs=ws[:, :], start=True, stop=True), r=[bc, b_lo], w=[b_pl])
            yield
            A(lambda: nc.scalar.activation(out=d["lw"][:], in_=ps_lo[0:64, :], func=AF.Sigmoid, bias=p64[:, h, 3:4], scale=1.0), r=[b_pl, bc], w=[b])
            yield
            V(lambda: nc.vector.tensor_scalar(out=d["lw"][:], in0=d["lw"][:], scalar1=-0.6065306597126334, scalar2=None, op0=ALU.mult), r=[b], w=[b])
            yield
            PE(lambda: nc.tensor.matmul(ps_lo[0:64, :], lhsT=a2t[:, hs], rhs=as_[:, :], start=True, stop=True), r=[bc, b_lo], w=[b_pl])
            yield
            A(lambda: nc.scalar.activation(out=d["asg"][:], in_=ps_lo[0:64, :], func=AF.Sigmoid, bias=p64[:, h, 4:5], scale=1.0), r=[b_pl, bc], w=[b])
            yield
            V(lambda: nc.vector.tensor_scalar(out=d["kkn"][:], in0=d["ks"][:], scalar1=p64[:, h, 5:6], scalar2=None, op0=ALU.mult), r=[b, bc], w=[b])
            yield
            A(lambda: nc.scalar.activation(out=tmp[0:64, :], in_=d["kkn"][:], func=AF.Square), r=[b], w=[b_tmp])
            yield
            PE(lambda: nc.tensor.matmul(ps_lo[0:64, :], lhsT=ones64[:, :], rhs=tmp[0:64, :], start=True, stop=True), r=[bc, b_tmp], w=[b_pl])
            yield
            A(lambda: nc.scalar.activation(out=tmp[0:64, :], in_=ps_lo[0:64, :], func=AF.Sqrt), r=[b_pl], w=[b_tmp])
            yield
            V(lambda: nc.vector.tensor_scalar(out=tmp[0:64, :], in0=tmp[0:64, :], scalar1=1e-12, scalar2=None, op0=ALU.max), r=[b_tmp], w=[b_tmp])
            yield
            V(lambda: nc.vector.reciprocal(out=tmp[0:64, :], in_=tmp[0:64, :]), r=[b_tmp], w=[b_tmp])
            yield
            V(lambda: nc.vector.tensor_tensor(out=d["kkn"][:], in0=d["kkn"][:], in1=tmp[0:64, :], op=ALU.mult), r=[b, b_tmp], w=[b])
            yield
            V(lambda: nc.vector.tensor_scalar(out=tmp[0:64, :], in0=d["asg"][:], scalar1=-1.0, scalar2=p64[:, h, 6:7], op0=ALU.add, op1=ALU.mult), r=[b, bc], w=[b_tmp])
            yield
            V(lambda: nc.vector.scalar_tensor_tensor(out=d["kp"][:], in0=tmp[0:64, :], scalar=1.0, in1=d["ks"][:], op0=ALU.add, op1=ALU.mult), r=[b_tmp, b], w=[b])
            yield
            V(lambda: nc.vector.tensor_tensor(out=d["bv"][:], in0=d["kkn"][:], in1=d["asg"][:], op=ALU.mult), r=[b], w=[b])
            yield
            V(lambda: nc.vector.scalar_tensor_tensor(out=d["rkr"][:], in0=d["rs"][:], scalar=p64[:, h, 7:8], in1=d["kp"][:], op0=ALU.mult, op1=ALU.mult), r=[b, bc], w=[b])
            yield
            V(lambda: nc.vector.tensor_tensor_scan(out=d["cum"][:], data0=seg[0:64, :], data1=d["lw"][:], initial=0.0, op0=ALU.mult, op1=ALU.add), r=[b, bc], w=[b])
            yield
            cum3 = d["cum"][:].rearrange("p (c t) -> p c t", t=C)
            V(lambda: nc.vector.tensor_copy(out=d["cC"][:], in_=cum3[:, :, C - 1]), r=[b], w=[b])
            yield
            A(lambda: nc.scalar.activation(out=d["gC"][par][:], in_=d["cC"][:], func=AF.Exp), r=[b], w=[bo])
            yield
            A(lambda: nc.scalar.activation(out=d["e1"][:], in_=d["cum"][:], func=AF.Exp), r=[b], w=[b])
            yield
            A(lambda: nc.scalar.activation(out=d["e2"][:], in_=d["cum"][:], func=AF.Exp, scale=-1.0), r=[b], w=[b])
            yield
            AR = d["AR"][par]
            V(lambda: nc.vector.tensor_tensor(out=RR(AR[:, :, 64:128]), in0=d["rs"][:].rearrange("p (c t) -> p c t", t=C),
                                              in1=d["e1"][:].rearrange("p (c t) -> p c t", t=C), op=ALU.mult), r=[b], w=[bo])
            yield
            V(lambda: nc.vector.tensor_tensor(out=RR(d["BT"][par][:]), in0=d["bv"][:], in1=d["e2"][:], op=ALU.mult), r=[b], w=[bo])
            yield
            V(lambda: nc.vector.tensor_tensor(out=RR(d["KT"][par][:]), in0=d["kp"][:], in1=d["e2"][:], op=ALU.mult), r=[b], w=[bo])
            yield
            V(lambda: nc.vector.tensor_tensor(out=tmp[0:64, :], in0=d["cum"][:], in1=d["lw"][:], op=ALU.subtract), r=[b], w=[b_tmp])
            yield
            A(lambda: nc.scalar.activation(out=tmp[0:64, :], in_=tmp[0:64, :], func=AF.Exp), r=[b_tmp], w=[b_tmp])
            yield
            V(lambda: nc.vector.scalar_tensor_tensor(out=RR(AR[:, :, 0:64]), in0=d["kkn"][:].rearrange("p (c t) -> p c t", t=C), scalar=-1.0,
                                                     in1=tmp[0:64, :].rearrange("p (c t) -> p c t", t=C), op0=ALU.mult, op1=ALU.mult), r=[b, b_tmp], w=[bo])
            yield
            V(lambda: nc.vector.tensor_tensor(out=tmp[0:64, :].rearrange("p (c t) -> p c t", t=C), in0=d["cC"][:].unsqueeze(2).to_broadcast([64, NCH, C]),
                                              in1=cum3, op=ALU.subtract), r=[b], w=[b_tmp])
            yield
            A(lambda: nc.scalar.activation(out=tmp[0:64, :], in_=tmp[0:64, :], func=AF.Exp), r=[b_tmp], w=[b_tmp])
            yield
            V(lambda: nc.vector.tensor_tensor(out=d["BH"][:], in0=d["bv"][:], in1=tmp[0:64, :], op=ALU.mult), r=[b, b_tmp], w=[b])
            yield
            V(lambda: nc.vector.tensor_tensor(out=d["KH"][:], in0=d["kp"][:], in1=tmp[0:64, :], op=ALU.mult), r=[b, b_tmp], w=[b])
            yield
            for src, dstn in (("vs", "Vt"), ("BH", "BHt"), ("KH", "KHt")):
                for c in range(NCH):
                    PE(lambda: nc.tensor.transpose(ps_tr[0:64, c * 64:(c + 1) * 64], d[src][:, c * C:(c + 1) * C], ident[0:64, 0:64]), r=[b, bc], w=[b_ptr])
                    yield
                A(lambda: nc.scalar.copy(out=RR(d[dstn][par][:].rearrange("p c k -> p (c k)")), in_=ps_tr[0:64, :]), r=[b_ptr], w=[d["bt"][par]])
                yield
            for c in range(NCH):
                PE(lambda: nc.tensor.matmul(ps_tr[0:64, 2 * c:2 * c + 2], lhsT=d["rkr"][:, c * C:(c + 1) * C], rhs=ones64[:, 0:2], start=True, stop=True), r=[b, bc], w=[b_ptr])
                yield
            V(lambda: nc.vector.tensor_copy(out=sbon[par][:, :, h], in_=ps_tr[0:64, 0:2 * NCH:2]), r=[b_ptr], w=[b_sb[par]])
            yield

    def rest(blk, tick):
        t0 = blk * TB
        par = blk % 2
        for h in range(2):
            d = H[h]; b = d["bo"][par]; AR = d["AR"][par]
            tick()
            for half in range(2):
                for cc in range(4):
                    c = half * 4 + cc
                    PE(lambda: nc.tensor.matmul(ps_a1[:, cc * 128:(cc + 1) * 128], lhsT=RR(d["BT"][par][:, c * C:(c + 1) * C]), rhs=RR(AR[:, c, :]), start=True, stop=True), r=[b], w=[b_pa1])
                    PE(lambda: nc.tensor.matmul(ps_a2[:, cc * 128:(cc + 1) * 128], lhsT=RR(d["KT"][par][:, c * C:(c + 1) * C]), rhs=RR(AR[:, c, :]), start=True, stop=True), r=[b], w=[b_pa2])
                    PE(lambda: nc.tensor.matmul(ps_a3[:, cc * 64:(cc + 1) * 64], lhsT=RR(AR[:, c, 0:64]), rhs=RR(d["BT"][par][:, c * C:(c + 1) * C]), start=True, stop=True), r=[b], w=[b_pa3])
                V(lambda: nc.vector.tensor_tensor(out=RR(d["NG"][:, half * 4:half * 4 + 4, :].rearrange("p c k -> p (c k)")), in0=ps_a1[:, :], in1=mask1[:, :], op=ALU.mult), r=[b_pa1, bc], w=[d["bA"]])
                V(lambda: nc.vector.tensor_tensor(out=RR(d["LG"][:, half * 4:half * 4 + 4, :].rearrange("p c k -> p (c k)")), in0=ps_a2[:, :], in1=mask1[:, :], op=ALU.mult), r=[b_pa2, bc], w=[d["bA"]])
                V(lambda: nc.vector.tensor_tensor(out=d["L"][:, half * 4:half * 4 + 4, :].rearrange("p c k -> p (c k)"), in0=ps_a3[:, 0:256], in1=mask3[:, :], op=ALU.mult), r=[b_pa3, bc], w=[d["bA"]])
        DPS = [(ps_d, b_pd, ps_d2, b_pd2), (ps_a1, b_pa1, ps_a2, b_pa2)]
        for h in range(2):
            d = H[h]; P, PT, ST = d["P"], d["PT"], d["ST"]; bD = d["bD"]
            V(lambda: nc.vector.tensor_copy(out=RR(P[0][:]), in_=d["L"][:]), r=[d["bA"]], w=[bD])
            V(lambda: nc.vector.tensor_copy(out=RR(PT[0][:]), in_=d["NG"][:, :, 0:64]), r=[d["bA"]], w=[bD])
            V(lambda: nc.vector.tensor_tensor(out=RR(ST[0][:]), in0=d["NG"][:, :, 0:64], in1=ident[0:64, 0:64].unsqueeze(1).to_broadcast([64, NCH, 64]), op=ALU.add), r=[d["bA"], bc], w=[bD])
        cur = 0
        for lev in range(5):
            nxt = 1 - cur
            tick()
            for h in range(2):
                d = H[h]; P, PT, ST = d["P"], d["PT"], d["ST"]; bD = d["bD"]
                pd, bpd, pd2, bpd2 = DPS[h]
                for c in range(NCH):
                    PE(lambda: nc.tensor.matmul(pd[:, c * 64:(c + 1) * 64], lhsT=RR(PT[cur][:, c, :]), rhs=RR(P[cur][:, c, :]), start=True, stop=True), r=[bD], w=[bpd])
                for c in range(NCH):
                    PE(lambda: nc.tensor.matmul(pd2[:, c * 64:(c + 1) * 64], lhsT=RR(P[cur][:, c, :]), rhs=RR(PT[cur][:, c, :]), start=True, stop=True), r=[bD], w=[bpd2])
            tick()
            for h in range(2):
                d = H[h]; P, PT, ST = d["P"], d["PT"], d["ST"]; bD = d["bD"]
                pd, bpd, pd2, bpd2 = DPS[h]
                V(lambda: nc.vector.tensor_copy(out=RR(P[nxt][:].rearrange("p c k -> p (c k)")), in_=pd[:, :]), r=[], w=[bpd, bD])
                A(lambda: nc.scalar.copy(out=RR(PT[nxt][:].rearrange("p c k -> p (c k)")), in_=pd2[:, :]), r=[], w=[bpd2, bD])
            tick()
            for h in range(2):
                d = H[h]; P, PT, ST = d["P"], d["PT"], d["ST"]; bD = d["bD"]
                pd, bpd, pd2, bpd2 = DPS[h]
                for c in range(NCH):
                    PE(lambda: nc.tensor.matmul(pd[:, c * 64:(c + 1) * 64], lhsT=RR(P[nxt][:, c, :]), rhs=RR(ST[cur][:, c, :]), start=True, stop=True), r=[bD], w=[bpd])
            tick()
            for h in range(2):
                d = H[h]; P, PT, ST = d["P"], d["PT"], d["ST"]; bD = d["bD"]
                pd, bpd, pd2, bpd2 = DPS[h]
                V(lambda: nc.vector.tensor_tensor(out=RR(ST[nxt][:].rearrange("p c k -> p (c k)")), in0=pd[:, :], in1=ST[cur][:].rearrange("p c k -> p (c k)"), op=ALU.add), r=[bD], w=[bpd, bD])
            tick()
            cur = nxt
        for h in range(2):
            H[h]["STf"] = H[h]["ST"][cur]
        for c in range(NCH):
            pp = lambda h, i: ps_sh[h][:, i * 64:(i + 1) * 64]
            tick()
            for h in range(2):
                d = H[h]
                PE(lambda: nc.tensor.matmul(pp(h, 0), lhsT=RR(d["LG"][:, c, 0:64]), rhs=RR(d["Vt"][par][:, c, :]), start=True, stop=False), r=[d["bA"], d["bt"][par]], w=[b_psh[h]])
                PE(lambda: nc.tensor.matmul(pp(h, 0), lhsT=RR(d["AR"][par][:, c, 0:64]), rhs=RR(d["M"][:, :]), start=False, stop=True), r=[d["bo"][par], d["bM"]], w=[b_psh[h]])
            tick()
            for h in range(2):
                d = H[h]
                if h == 0:
                    A(lambda: nc.scalar.copy(out=RR(d["X1"][:]), in_=pp(h, 0)), r=[], w=[b_psh[h], d["bX"]])
                else:
                    V(lambda: nc.vector.tensor_copy(out=RR(d["X1"][:]), in_=pp(h, 0)), r=[], w=[b_psh[h], d["bX"]])
            tick()
            for h in range(2):
                d = H[h]
                PE(lambda: nc.tensor.matmul(pp(h, 1), lhsT=RR(d["STf"][:, c, :]), rhs=RR(d["X1"][:, :]), start=True, stop=True), r=[d["bD"], d["bX"]], w=[b_psh[h]])
            tick()
            for h in range(2):
                d = H[h]
                if h == 0:
                    V(lambda: nc.vector.tensor_copy(out=RR(d["U"][:]), in_=pp(h, 1)), r=[], w=[b_psh[h], d["bU"]])
                else:
                    A(lambda: nc.scalar.copy(out=RR(d["U"][:]), in_=pp(h, 1)), r=[], w=[b_psh[h], d["bU"]])
            tick()
            for h in range(2):
                d = H[h]
                PE(lambda: nc.tensor.matmul(pp(h, 2), lhsT=RR(d["AR"][par][:, c, 64:128]), rhs=RR(d["M"][:, :]), start=True, stop=False), r=[d["bo"][par], d["bM"]], w=[b_psh[h]])
                PE(lambda: nc.tensor.matmul(pp(h, 2), lhsT=RR(d["LG"][:, c, 64:128]), rhs=RR(d["Vt"][par][:, c, :]), start=False, stop=False), r=[d["bA"], d["bt"][par]], w=[b_psh[h]])
                PE(lambda: nc.tensor.matmul(pp(h, 2), lhsT=RR(d["NG"][:, c, 64:128]), rhs=RR(d["U"][:, :]), start=False, stop=True), r=[d["bA"], d["bU"]], w=[b_psh[h]])
                PE(lambda: nc.tensor.matmul(pp(h, 3), lhsT=RR(d["KHt"][par][:, c, :]), rhs=RR(d["Vt"][par][:, c, :]), start=True, stop=False), r=[d["bt"][par]], w=[b_psh[h]])
                PE(lambda: nc.tensor.matmul(pp(h, 3), lhsT=RR(d["BHt"][par][:, c, :]), rhs=RR(d["U"][:, :]), start=False, stop=True), r=[d["bt"][par], d["bU"]], w=[b_psh[h]])
            tick()
            for h in range(2):
                d = H[h]
                V(lambda: nc.vector.scalar_tensor_tensor(out=RR(d["M"][:]), in0=d["M"][:], scalar=d["gC"][par][:, c:c + 1], in1=pp(h, 3), op0=ALU.mult, op1=ALU.add), r=[d["bo"][par], d["bM"]], w=[b_psh[h], d["bM"]])
                A(lambda: nc.scalar.copy(out=Yb[:, c, h, :], in_=pp(h, 2)), r=[], w=[b_psh[h], b_Y])
        Y2 = Yb[:].rearrange("p c h v -> p (c h) v")
        V(lambda: nc.vector.tensor_reduce(out=st1[:], in_=Y2, axis=AX.X, op=ALU.add), r=[b_Y], w=[b_st])
        A(lambda: nc.scalar.activation(out=Ysq[:].rearrange("p c h v -> p (c h v)"), in_=Yb[:].rearrange("p c h v -> p (c h v)"), func=AF.Square), r=[b_Y], w=[b_tmp])
        V(lambda: nc.vector.tensor_reduce(out=st2[:], in_=Ysq[:].rearrange("p c h v -> p (c h) v"), axis=AX.X, op=ALU.add), r=[b_tmp], w=[b_st])
        V(lambda: nc.vector.tensor_scalar(out=st1[:], in0=st1[:], scalar1=1.0 / 64, scalar2=None, op0=ALU.mult), r=[b_st], w=[b_st])
        V(lambda: nc.vector.tensor_tensor(out=st3[:], in0=st1[:], in1=st1[:], op=ALU.mult), r=[b_st], w=[b_st])
        V(lambda: nc.vector.scalar_tensor_tensor(out=st2[:], in0=st2[:], scalar=1.0 / 64, in1=st3[:], op0=ALU.mult, op1=ALU.subtract), r=[b_st], w=[b_st])
        A(lambda: nc.scalar.activation(out=st2[:], in_=st2[:], func=AF.Sqrt, bias=64e-5, scale=1.0), r=[b_st], w=[b_st])
        V(lambda: nc.vector.reciprocal(out=st2[:], in_=st2[:]), r=[b_st], w=[b_st])
        V(lambda: nc.vector.tensor_tensor(out=Y2, in0=Y2, in1=st1[:].unsqueeze(2).to_broadcast([64, NCH * 2, 64]), op=ALU.subtract), r=[b_st, b_Y], w=[b_Y])
        V(lambda: nc.vector.tensor_tensor(out=Y2, in0=Y2, in1=st2[:].unsqueeze(2).to_broadcast([64, NCH * 2, 64]), op=ALU.mult), r=[b_st, b_Y], w=[b_Y])
        for h in range(2):
            V(lambda: nc.vector.tensor_tensor(out=Yb[:, :, h, :], in0=Yb[:, :, h, :], in1=gn[:, 0, h, :].unsqueeze(1).to_broadcast([64, NCH, 64]), op=ALU.mult), r=[b_Y, bc], w=[b_Y])
            V(lambda: nc.vector.tensor_tensor(out=Yb[:, :, h, :], in0=Yb[:, :, h, :], in1=gn[:, 1, h, :].unsqueeze(1).to_broadcast([64, NCH, 64]), op=ALU.add), r=[b_Y, bc], w=[b_Y])
            V(lambda: nc.vector.tensor_tensor(out=Ysq[:, :, h, :], in0=H[h]["Vt"][par][:], in1=sbon[par][:, :, h].unsqueeze(2).to_broadcast([64, NCH, 64]), op=ALU.mult), r=[H[h]["bt"][par], b_sb[par]], w=[b_tmp])
        V(lambda: nc.vector.tensor_tensor(out=Yb[:].rearrange("p c h v -> p (c h v)"), in0=Yb[:].rearrange("p c h v -> p (c h v)"), in1=Ysq[:].rearrange("p c h v -> p (c h v)"), op=ALU.add), r=[b_Y, b_tmp], w=[b_Y])
        for c in range(NCH):
            PE(lambda: nc.tensor.transpose(ps_a3f[:, c * 64:(c + 1) * 64], Yb[:, c, :, :].rearrange("p h v -> p (h v)"), ident[0:64, 0:64]), r=[b_Y, bc], w=[b_pa3])
        V(lambda: nc.vector.tensor_tensor(out=yo[:], in0=ps_a3f[:, :], in1=gate[par][:], op=ALU.mult), r=[b_gate[par]], w=[b_pa3, b_yo])
        mk.dma("sp", yT[:, t0:t0 + TB], yo[:], reads=[b_yo], is_output=True)

    for _ in stage1(0):
        pass
    for blk in range(NB):
        nx = stage1(blk + 1) if blk + 1 < NB else None

        def tick(k=2):
            if nx is not None:
                for _ in range(k):
                    if next(nx, "END") == "END":
                        break
        rest(blk, tick)
        if nx is not None:
            for _ in nx:
                pass


def build_rwkv(T):
    nc = bass.Bass("TRN2", target_bir_lowering=False)
    dt = lambda n, s, k="ExternalInput": nc.dram_tensor(n, s, F32, kind=k).ap()
    rwin = dt("rwin", [704, T]); par64 = dt("par64", [64, 2, 11]); par128 = dt("par128", [128, 3])
    w2 = dt("w2", [96, 128]); a2 = dt("a2", [96, 128]); g2 = dt("g2", [128, 128]); gnt = dt("gnt", [64, 2, 2, 64])
    cst = {"mask1": dt("mask1", [64, 512]), "mask3": dt("mask3", [64, 256]), "seg": dt("seg", [128, TB]), "ident": dt("ident", [128, 128])}
    yT = dt("yT", [128, T], "ExternalOutput")
    with ExitStack() as ctx:
        mk = MK(nc, ctx)
        emit_rwkv(nc, mk, T, rwin, par64, par128, w2, a2, g2, gnt, cst, yT)
        mk.finish("sp")
        print("rwkv ops", mk.nops, "waits", mk.nwaits)
    return nc


def rwkv_host_inputs(prm, l, q):
    G = 512
    cs = slice(128 * q, 128 * q + 128)
    mu = prm["rwkv_mu"][l]
    par64 = np.zeros((64, 2, 11), np.float32)
    for h in range(2):
        c0 = 128 * q + 64 * h
        par64[:, h, 0] = mu[0 * G + c0:0 * G + c0 + 64]
        par64[:, h, 1] = mu[1 * G + c0:1 * G + c0 + 64]
        par64[:, h, 2] = mu[2 * G + c0:2 * G + c0 + 64]
        par64[:, h, 3] = prm["rwkv_w0"][l][c0:c0 + 64]
        par64[:, h, 4] = prm["rwkv_a0"][l][c0:c0 + 64]
        par64[:, h, 5] = prm["rwkv_kk"][l][c0:c0 + 64]
        par64[:, h, 6] = prm["rwkv_ka"][l][c0:c0 + 64]
        par64[:, h, 7] = prm["rwkv_rk"][l][2 * q + h]
    par128 = np.zeros((128, 3), np.float32)
    par128[0:96, 0] = mu[3 * G:3 * G + 96]
    par128[0:96, 1] = mu[3 * G + 96:3 * G + 192]
    par128[:, 2] = mu[3 * G + 192:3 * G + 320]
    gnt = np.zeros((64, 2, 2, 64), np.float32)
    for h in range(2):
        c0 = 128 * q + 64 * h
        gnt[:, 0, h, :] = prm["rwkv_gn_g"][l][c0:c0 + 64][None]
        gnt[:, 1, h, :] = prm["rwkv_gn_b"][l][c0:c0 + 64][None]
    d = {"par64": par64, "par128": par128, "gnt": gnt,
         "w2": np.ascontiguousarray(prm["rwkv_w2"][l][:, cs]), "a2": np.ascontiguousarray(prm["rwkv_a2"][l][:, cs]),
         "g2": np.ascontiguousarray(prm["rwkv_g2"][l][:, cs])}
    d.update(rwkv_consts())
    return d


def rwkv_rows(q):
    G = 512
    idx = []
    for base in (0, G, 2 * G):
        idx += list(range(base + 128 * q, base + 128 * q + 64))
        idx += list(range(base + 128 * q + 64, base + 128 * q + 128))
    idx += list(range(3 * G, 3 * G + 320))
    return np.array(idx)


import math
import numpy as np
from contextlib import ExitStack


def emit_conv(nc, mk, T, cvin, cw, yT, TB=2048, odt=F32):
    V = lambda fn, r=(), w=(): mk.op("dve", fn, r, w)
    G = lambda fn, r=(), w=(): mk.op("pool", fn, r, w)
    cwt = mk.sb("cv_w", [128, 3]); bc = Buf()
    mk.dma("sp", cwt[:], cw, writes=[bc])
    Bt = mk.sb("cv_B", [128, TB]); Ct = mk.sb("cv_C", [128, TB + 2]); Ht = mk.sb("cv_H", [128, TB + 2])
    z = mk.sb("cv_z", [128, TB + 2]); y = mk.sb("cv_y", [128, TB]); o = mk.sb("cv_o", [128, TB], odt)
    b_in, b_z, b_y, b_o = Buf(), Buf(), Buf(), Buf()
    for blk in range(T // TB):
        t0 = blk * TB
        mk.dma("sp", Bt[:], cvin[0:128, t0:t0 + TB], writes=[b_in])
        if blk == 0:
            V(lambda: nc.vector.memset(Ct[:, 0:2], 0.0), w=[b_in])
            V(lambda: nc.vector.memset(Ht[:, 0:2], 0.0), w=[b_in])
            mk.dma("sp", Ct[:, 2:], cvin[128:256, 0:TB], writes=[b_in])
            mk.dma("sp", Ht[:, 2:], cvin[256:384, 0:TB], writes=[b_in])
        else:
            mk.dma("sp", Ct[:], cvin[128:256, t0 - 2:t0 + TB], writes=[b_in])
            mk.dma("sp", Ht[:], cvin[256:384, t0 - 2:t0 + TB], writes=[b_in])
        G(lambda: nc.gpsimd.tensor_tensor(out=z[:], in0=Ct[:], in1=Ht[:], op=ALU.mult), r=[b_in], w=[b_z])
        V(lambda: nc.vector.tensor_scalar(out=y[:], in0=z[:, 2:TB + 2], scalar1=cwt[:, 2:3], scalar2=None, op0=ALU.mult), r=[b_z, bc], w=[b_y])
        V(lambda: nc.vector.scalar_tensor_tensor(out=y[:], in0=z[:, 1:TB + 1], scalar=cwt[:, 1:2], in1=y[:], op0=ALU.mult, op1=ALU.add), r=[b_z, bc], w=[b_y])
        V(lambda: nc.vector.scalar_tensor_tensor(out=y[:], in0=z[:, 0:TB], scalar=cwt[:, 0:1], in1=y[:], op0=ALU.mult, op1=ALU.add), r=[b_z, bc], w=[b_y])
        G(lambda: nc.gpsimd.tensor_tensor(out=o[:], in0=y[:], in1=Bt[:], op=ALU.mult), r=[b_y, b_in], w=[b_o])
        mk.dma("sp", yT[:, t0:t0 + TB], o[:], reads=[b_o], is_output=True)


def t5_bucket_np(rel):
    n = np.maximum(rel, 0)
    max_exact = 16
    n_f = np.maximum(n, 1).astype(np.float32)
    large = max_exact + (np.log(n_f / max_exact) / math.log(128 / max_exact) * (32 - max_exact)).astype(np.int32)
    return np.where(n < max_exact, n, np.minimum(large, 31))


def attn_tables(rel_bias, sinks_l, q):
    qi = np.arange(128)[:, None]
    kj = np.arange(256)[None, :]
    rel = qi + 128 - kj
    bucket = t5_bucket_np(rel)
    valid = (rel >= 0) & (rel < 128)
    tab = np.zeros((2, 128, 2, 256), np.float32)
    for h in range(2):
        bias = rel_bias[bucket, 2 * q + h]
        full = np.where(valid, bias, np.float32(-30000.0))
        tab[0, :, h, :] = full
        f0 = full.copy()
        f0[:, 0:128] = -30000.0
        tab[1, :, h, :] = f0
    sk = np.broadcast_to(sinks_l[2 * q:2 * q + 2][None, :], (128, 2)).astype(np.float32).copy()
    return tab, sk


def emit_attn(nc, mk, T, qkv, btab, sinkt_d, ident_d, yT, odt=F32):
    V = lambda fn, r=(), w=(): mk.op("dve", fn, r, w)
    A = lambda fn, r=(), w=(): mk.op("act", fn, r, w)
    PE = lambda fn, r=(), w=(): mk.op("pe", fn, r, w, skip_same=True)
    NBK = T // 128
    bt = mk.sb("at_bt", [128, 2, 2, 256]); sk = mk.sb("at_sk", [128, 2]); ident = mk.sb("at_id", [128, 128]); bc = Buf()
    mk.dma("sp", bt[:, 0, :, :], btab[0], writes=[bc])
    mk.dma("sp", bt[:, 1, :, :], btab[1], writes=[bc])
    mk.dma("sp", sk[:], sinkt_d, writes=[bc])
    mk.dma("sp", ident[:], ident_d, writes=[bc])
    CH = 1024
    qt = mk.sb("at_q", [128, CH]); kt = mk.sb("at_k", [128, 128 + CH]); vt = mk.sb("at_v", [64, CH])
    vtok = mk.sb("at_vtok", [128, CH // 128 + 1, 64])
    b_q, b_k, b_v, b_vt = Buf(), Buf(), Buf(), Buf()
    sc = [mk.sb("at_sc%d" % h, [128, 256]) for h in range(2)]; b_sc = [Buf(), Buf()]
    pr = [mk.sb("at_p%d" % h, [128, 256]) for h in range(2)]; b_p = [Buf(), Buf()]
    pT = [mk.sb("at_pT%d" % h, [128, 256]) for h in range(2)]; b_pT = [Buf(), Buf()]
    sm = [mk.sb("at_sm%d" % h, [128, 8]) for h in range(2)]; b_sm = [Buf(), Buf()]
    ot = mk.sb("at_o", [128, 128]); b_o = Buf()
    yo = mk.sb("at_yo", [128, CH], odt); b_yo = Buf()
    ps_s = [mk.ps("at_ps_s%d" % h, [128, 512]) for h in range(2)]; b_ps = [Buf(), Buf()]
    ps_t = [mk.ps("at_ps_t%d" % h, [128, 512]) for h in range(2)]; b_pt = [Buf(), Buf()]
    ps_o = mk.ps("at_ps_o", [128, 512]); b_po = Buf()
    ps_v = mk.ps("at_ps_v", [128, 512]); b_pv = Buf()
    for ch in range(T // CH):
        c0 = ch * CH
        mk.dma("sp", qt[:], qkv[0:128, c0:c0 + CH], writes=[b_q])
        if ch == 0:
            V(lambda: nc.vector.memset(kt[:, 0:128], 0.0), w=[b_k])
            V(lambda: nc.vector.memset(vtok[:, 0, :], 0.0), w=[b_vt])
            for hh in range(2):
                mk.dma("sp", kt[64 * hh:64 * hh + 64, 128:], qkv[128:192, 0:CH], writes=[b_k])
        else:
            for hh in range(2):
                mk.dma("sp", kt[64 * hh:64 * hh + 64, :], qkv[128:192, c0 - 128:c0 + CH], writes=[b_k])
            V(lambda: nc.vector.tensor_copy(out=vtok[:, 0, :], in_=vtok[:, CH // 128, :]), r=[b_vt], w=[b_vt])
        mk.dma("sp", vt[:], qkv[192:256, c0:c0 + CH], writes=[b_v])
        for j in range(CH // 128):
            PE(lambda: nc.tensor.transpose(ps_v[:, j * 64:(j + 1) * 64], vt[:, j * 128:(j + 1) * 128], ident[0:64, 0:64]), r=[b_v, bc], w=[b_pv])
        A(lambda: nc.scalar.copy(out=vtok[:, 1:, :].rearrange("p j d -> p (j d)"), in_=ps_v[:, 0:(CH // 128) * 64]), r=[], w=[b_pv, b_vt])
        for j in range(CH // 128):
            first = 1 if (ch == 0 and j == 0) else 0
            HS = [slice(0, 64), slice(64, 128)]
            for h in range(2):
                PE(lambda: nc.tensor.matmul(ps_s[h][:, 0:256], lhsT=qt[HS[h], j * 128:(j + 1) * 128], rhs=kt[HS[h], j * 128:j * 128 + 256], start=True, stop=True),
                   r=[b_q, b_k], w=[b_ps[h]])
            for h in range(2):
                V(lambda: nc.vector.scalar_tensor_tensor(out=sc[h][:], in0=ps_s[h][:, 0:256], scalar=0.125, in1=bt[:, first, h, :], op0=ALU.mult, op1=ALU.add),
                  r=[bc], w=[b_ps[h], b_sc[h]])
                s = sm[h]
                V(lambda: nc.vector.reduce_max(out=s[:, 0:1], in_=sc[h][:], axis=AX.X), r=[b_sc[h]], w=[b_sm[h]])
                V(lambda: nc.vector.tensor_tensor(out=s[:, 0:1], in0=s[:, 0:1], in1=sk[:, h:h + 1], op=ALU.max), r=[bc], w=[b_sm[h]])
                V(lambda: nc.vector.tensor_scalar(out=s[:, 1:2], in0=s[:, 0:1], scalar1=-1.0, scalar2=None, op0=ALU.mult), r=[], w=[b_sm[h]])
            for h in range(2):
                s = sm[h]
                A(lambda: nc.scalar.activation(out=pr[h][:], in_=sc[h][:], func=AF.Exp, bias=s[:, 1:2], scale=1.0, accum_out=s[:, 2:3]), r=[b_sc[h]], w=[b_sm[h], b_p[h]])
                A(lambda: nc.scalar.activation(out=s[:, 3:4], in_=sk[:, h:h + 1], func=AF.Exp, bias=s[:, 1:2], scale=1.0), r=[bc], w=[b_sm[h]])
            for h in range(2):
                for kb in range(2):
                    PE(lambda: nc.tensor.transpose(ps_t[h][:, kb * 128:(kb + 1) * 128], pr[h][:, kb * 128:(kb + 1) * 128], ident[:, :]), r=[b_p[h], bc], w=[b_pt[h]])
            for h in range(2):
                s = sm[h]
                V(lambda: nc.vector.tensor_tensor(out=s[:, 4:5], in0=s[:, 2:3], in1=s[:, 3:4], op=ALU.add), r=[], w=[b_sm[h]])
                V(lambda: nc.vector.reciprocal(out=s[:, 5:6], in_=s[:, 4:5]), r=[], w=[b_sm[h]])
                V(lambda: nc.vector.tensor_copy(out=pT[h][:], in_=ps_t[h][:, 0:256]), r=[], w=[b_pt[h], b_pT[h]])
            for h in range(2):
                for kb in range(2):
                    PE(lambda: nc.tensor.matmul(ps_o[:, h * 64:(h + 1) * 64], lhsT=pT[h][:, kb * 128:(kb + 1) * 128], rhs=vtok[:, j + kb, :], start=(kb == 0), stop=(kb == 1)),
                       r=[b_pT[h], b_vt], w=[b_po])
            for h in range(2):
                s = sm[h]
                A(lambda: nc.scalar.activation(out=ot[:, h * 64:(h + 1) * 64], in_=ps_o[:, h * 64:(h + 1) * 64], func=AF.Copy, scale=s[:, 5:6]), r=[b_sm[h]], w=[b_po, b_o])
            PE(lambda: nc.tensor.transpose(ps_o[:, 128:256], ot[:, :], ident[:, :]), r=[b_o, bc], w=[b_po])
            V(lambda: nc.vector.tensor_copy(out=yo[:, j * 128:(j + 1) * 128], in_=ps_o[:, 128:256]), r=[], w=[b_po, b_yo])
        mk.dma("sp", yT[:, c0:c0 + CH], yo[:], reads=[b_yo], is_output=True)


CS = 512


def s5_host_inputs(prm, l, q):
    g0 = 8 * q
    par = np.zeros((128, 4, 3), np.float32)
    bb = np.zeros((128, 4, 2, 16), np.float32)
    cc = np.zeros((128, 4, 2, 16), np.float32)
    for j in range(4):
        for gl in range(2):
            g = g0 + 2 * j + gl
            ps = slice(64 * gl, 64 * gl + 64)
            par[ps, j, 0] = prm["s5_lambda_re"][l][g]
            par[ps, j, 1] = prm["s5_lambda_im"][l][g]
            par[ps, j, 2] = prm["s5_log_dt"][l][g]
            bb[ps, j, 0, :] = prm["s5_b_re"][l][g]
            bb[ps, j, 1, :] = prm["s5_b_im"][l][g]
            cc[ps, j, 0, :] = prm["s5_c_re"][l][g].T
            cc[ps, j, 1, :] = prm["s5_c_im"][l][g].T
    dsk = np.ascontiguousarray(prm["s5_d"][l][g0:g0 + 8].reshape(128, 1))
    iot = np.broadcast_to(np.arange(CS, dtype=np.float32)[None, :], (128, CS)).copy()
    return {"s5par": par, "s5bb": bb, "s5cc": cc, "s5d": dsk, "s5iota": iot, "ident": np.eye(128, dtype=np.float32)}


def emit_s5(nc, mk, T, uT, par_d, bb_d, cc_d, d_d, iota_d, ident_d, yT, odt=F32):
    V = lambda fn, r=(), w=(): mk.op("dve", fn, r, w)
    A = lambda fn, r=(), w=(): mk.op("act", fn, r, w)
    G = lambda fn, r=(), w=(): mk.op("pool", fn, r, w)
    PE = lambda fn, r=(), w=(): mk.op("pe", fn, r, w, skip_same=True)
    sb = mk.sb
    TWO_PI = 2.0 * math.pi
    par = sb("s5_par", [128, 4, 3]); bb = sb("s5_bb", [128, 4, 2, 16]); cc = sb("s5_cc", [128, 4, 2, 16])
    dsk = sb("s5_d", [128, 1]); iot = sb("s5_iota", [128, CS]); ident = sb("s5_id", [128, 128])
    bc = Buf("c")
    for dst, src in ((par, par_d), (bb, bb_d), (cc, cc_d), (dsk, d_d), (iot, iota_d), (ident, ident_d)):
        mk.dma("sp", dst[:], src, writes=[bc])
    P = {}
    for nm in ("dl", "mag", "th", "cs", "sn", "are", "aim", "den", "zre", "zim", "t1", "t2", "t3", "cC", "sC"):
        P[nm] = sb("s5_p_" + nm, [128, 4])
    ti = sb("s5_ti", [128, 4 * CS], I32)
    bp = Buf("p")
    lr, li, ldt = par[:, :, 0], par[:, :, 1], par[:, :, 2]

    def sincos(sin_out, cos_out, x, n, tmpa, tmpb, tint):
        def wrap(r):
            V(lambda: nc.vector.tensor_scalar(out=tmpb, in0=r, scalar1=0.5, scalar2=None, op0=ALU.is_gt), r=[bp], w=[bp])
            V(lambda: nc.vector.tensor_tensor(out=r, in0=r, in1=tmpb, op=ALU.subtract), r=[bp], w=[bp])
            V(lambda: nc.vector.tensor_scalar(out=tmpb, in0=r, scalar1=-0.5, scalar2=None, op0=ALU.is_lt), r=[bp], w=[bp])
            V(lambda: nc.vector.tensor_tensor(out=r, in0=r, in1=tmpb, op=ALU.add), r=[bp], w=[bp])
        V(lambda: nc.vector.tensor_copy(out=tint, in_=x), r=[bp, bc], w=[bp])
        V(lambda: nc.vector.tensor_copy(out=tmpa, in_=tint), r=[bp], w=[bp])
        V(lambda: nc.vector.tensor_tensor(out=tmpa, in0=x, in1=tmpa, op=ALU.subtract), r=[bp, bc], w=[bp])
        wrap(tmpa)
        A(lambda: nc.scalar.activation(out=sin_out, in_=tmpa, func=AF.Sin, scale=TWO_PI), r=[bp], w=[bp])
        V(lambda: nc.vector.tensor_scalar(out=tmpa, in0=tmpa, scalar1=0.25, scalar2=None, op0=ALU.add), r=[bp], w=[bp])
        wrap(tmpa)
        A(lambda: nc.scalar.activation(out=cos_out, in_=tmpa, func=AF.Sin, scale=TWO_PI), r=[bp], w=[bp])

    A(lambda: nc.scalar.activation(out=P["dl"][:], in_=ldt, func=AF.Exp), r=[bc], w=[bp])
    V(lambda: nc.vector.tensor_tensor(out=P["mag"][:], in0=lr, in1=P["dl"][:], op=ALU.mult), r=[bc, bp], w=[bp])
    A(lambda: nc.scalar.activation(out=P["mag"][:], in_=P["mag"][:], func=AF.Exp), r=[bp], w=[bp])
    V(lambda: nc.vector.tensor_tensor(out=P["th"][:], in0=li, in1=P["dl"][:], op=ALU.mult), r=[bc, bp], w=[bp])
    V(lambda: nc.vector.tensor_scalar(out=P["th"][:], in0=P["th"][:], scalar1=1.0 / TWO_PI, scalar2=None, op0=ALU.mult), r=[bp], w=[bp])
    V(lambda: nc.vector.tensor_copy(out=ti[:, 0:4], in_=P["th"][:]), r=[bp], w=[bp])
    V(lambda: nc.vector.tensor_copy(out=P["t1"][:], in_=ti[:, 0:4]), r=[bp], w=[bp])
    V(lambda: nc.vector.tensor_tensor(out=P["th"][:], in0=P["th"][:], in1=P["t1"][:], op=ALU.subtract), r=[bp], w=[bp])
    V(lambda: nc.vector.tensor_scalar(out=P["t1"][:], in0=P["th"][:], scalar1=0.5, scalar2=None, op0=ALU.is_gt), r=[bp], w=[bp])
    V(lambda: nc.vector.tensor_tensor(out=P["th"][:], in0=P["th"][:], in1=P["t1"][:], op=ALU.subtract), r=[bp], w=[bp])
    V(lambda: nc.vector.tensor_scalar(out=P["t1"][:], in0=P["th"][:], scalar1=-0.5, scalar2=None, op0=ALU.is_lt), r=[bp], w=[bp])
    V(lambda: nc.vector.tensor_tensor(out=P["th"][:], in0=P["th"][:], in1=P["t1"][:], op=ALU.add), r=[bp], w=[bp])
    sincos(P["sn"][:], P["cs"][:], P["th"][:], 4, P["t1"][:], P["t2"][:], ti[:, 0:4])
    V(lambda: nc.vector.tensor_tensor(out=P["are"][:], in0=P["mag"][:], in1=P["cs"][:], op=ALU.mult), r=[bp], w=[bp])
    V(lambda: nc.vector.tensor_tensor(out=P["aim"][:], in0=P["mag"][:], in1=P["sn"][:], op=ALU.mult), r=[bp], w=[bp])
    V(lambda: nc.vector.tensor_tensor(out=P["den"][:], in0=lr, in1=lr, op=ALU.mult), r=[bc], w=[bp])
    V(lambda: nc.vector.tensor_tensor(out=P["t1"][:], in0=li, in1=li, op=ALU.mult), r=[bc], w=[bp])
    V(lambda: nc.vector.tensor_tensor(out=P["den"][:], in0=P["den"][:], in1=P["t1"][:], op=ALU.add), r=[bp], w=[bp])
    V(lambda: nc.vector.reciprocal(out=P["den"][:], in_=P["den"][:]), r=[bp], w=[bp])
    V(lambda: nc.vector.tensor_scalar(out=P["t3"][:], in0=P["are"][:], scalar1=-1.0, scalar2=None, op0=ALU.add), r=[bp], w=[bp])
    V(lambda: nc.vector.tensor_tensor(out=P["t1"][:], in0=P["t3"][:], in1=lr, op=ALU.mult), r=[bp, bc], w=[bp])
    V(lambda: nc.vector.tensor_tensor(out=P["t2"][:], in0=P["aim"][:], in1=li, op=ALU.mult), r=[bp, bc], w=[bp])
    V(lambda: nc.vector.tensor_tensor(out=P["t1"][:], in0=P["t1"][:], in1=P["t2"][:], op=ALU.add), r=[bp], w=[bp])
    V(lambda: nc.vector.tensor_tensor(out=P["zre"][:], in0=P["t1"][:], in1=P["den"][:], op=ALU.mult), r=[bp], w=[bp])
    V(lambda: nc.vector.tensor_tensor(out=P["t1"][:], in0=P["aim"][:], in1=lr, op=ALU.mult), r=[bp, bc], w=[bp])
    V(lambda: nc.vector.tensor_tensor(out=P["t2"][:], in0=P["t3"][:], in1=li, op=ALU.mult), r=[bp, bc], w=[bp])
    V(lambda: nc.vector.tensor_tensor(out=P["t1"][:], in0=P["t1"][:], in1=P["t2"][:], op=ALU.subtract), r=[bp], w=[bp])
    V(lambda: nc.vector.tensor_tensor(out=P["zim"][:], in0=P["t1"][:], in1=P["den"][:], op=ALU.mult), r=[bp], w=[bp])
    bbar = sb("s5_bbar", [128, 4, 2, 16]); tb1 = sb("s5_tb1", [128, 4, 16]); tb2 = sb("s5_tb2", [128, 4, 16])
    zre_b = P["zre"][:].unsqueeze(2).to_broadcast([128, 4, 16]); zim_b = P["zim"][:].unsqueeze(2).to_broadcast([128, 4, 16])
    V(lambda: nc.vector.tensor_tensor(out=tb1[:], in0=bb[:, :, 0, :], in1=zre_b, op=ALU.mult), r=[bp, bc], w=[bp])
    V(lambda: nc.vector.tensor_tensor(out=tb2[:], in0=bb[:, :, 1, :], in1=zim_b, op=ALU.mult), r=[bp, bc], w=[bp])
    V(lambda: nc.vector.tensor_tensor(out=bbar[:, :, 0, :], in0=tb1[:], in1=tb2[:], op=ALU.subtract), r=[bp], w=[bp])
    V(lambda: nc.vector.tensor_tensor(out=tb1[:], in0=bb[:, :, 1, :], in1=zre_b, op=ALU.mult), r=[bp, bc], w=[bp])
    V(lambda: nc.vector.tensor_tensor(out=tb2[:], in0=bb[:, :, 0, :], in1=zim_b, op=ALU.mult), r=[bp, bc], w=[bp])
    V(lambda: nc.vector.tensor_tensor(out=bbar[:, :, 1, :], in0=tb1[:], in1=tb2[:], op=ALU.add), r=[bp], w=[bp])
    BD = sb("s5_BD", [128, 4, 2, 128]); CM = sb("s5_CM", [128, 4, 4, 128]); BbT = sb("s5_BbT", [128, 4, 2, 128])
    V(lambda: nc.vector.memset(BD[:].rearrange("p a b c -> p (a b c)"), 0.0), w=[bp])
    V(lambda: nc.vector.memset(CM[:].rearrange("p a b c -> p (a b c)"), 0.0), w=[bp])
    for j in range(4):
        for gl in range(2):
            ps_ = slice(64 * gl, 64 * gl + 64)
            c0 = 32 * j + 16 * gl
            for ri in range(2):
                V(lambda: nc.vector.tensor_copy(out=BD[ps_, j, ri, c0:c0 + 16], in_=bbar[ps_, j, ri, :]), r=[bp], w=[bp])
            V(lambda: nc.vector.tensor_copy(out=CM[ps_, j, 0, c0:c0 + 16], in_=cc[ps_, j, 0, :]), r=[bc], w=[bp])
            V(lambda: nc.vector.tensor_scalar(out=CM[ps_, j, 1, c0:c0 + 16], in0=cc[ps_, j, 0, :], scalar1=-1.0, scalar2=None, op0=ALU.mult), r=[bc], w=[bp])
            V(lambda: nc.vector.tensor_scalar(out=CM[ps_, j, 2, c0:c0 + 16], in0=cc[ps_, j, 1, :], scalar1=-1.0, scalar2=None, op0=ALU.mult), r=[bc], w=[bp])
            V(lambda: nc.vector.tensor_scalar(out=CM[ps_, j, 3, c0:c0 + 16], in0=cc[ps_, j, 1, :], scalar1=-1.0, scalar2=None, op0=ALU.mult), r=[bc], w=[bp])
    ps_tmp = mk.ps("s5_ps_tmp", [128, 512]); b_pt = Buf()
    for j in range(4):
        for ri in range(2):
            PE(lambda: nc.tensor.transpose(ps_tmp[:, ri * 128:(ri + 1) * 128], BD[:, j, ri, :], ident[:, :]), r=[bp, bc], w=[b_pt])
        V(lambda: nc.vector.tensor_copy(out=BbT[:, j, :, :].rearrange("p a b -> p (a b)"), in_=ps_tmp[:, 0:256]), r=[], w=[b_pt, bp])
    cosT = sb("s5_cosT", [128, 4, CS]); sinT = sb("s5_sinT", [128, 4, CS])
    xa = sb("s5_xa", [128, 4 * CS]); xb = sb("s5_xb", [128, 4 * CS]); xc = sb("s5_xc", [128, 4 * CS])
    for j in range(4):
        V(lambda: nc.vector.tensor_scalar(out=xc[:, j * CS:(j + 1) * CS], in0=iot[:], scalar1=P["th"][:, j:j + 1], scalar2=None, op0=ALU.mult), r=[bp, bc], w=[bp])
    sincos(sinT[:].rearrange("p a b -> p (a b)"), cosT[:].rearrange("p a b -> p (a b)"), xc[:], 4 * CS, xa[:], xb[:], ti[:])
    V(lambda: nc.vector.tensor_scalar(out=P["t3"][:], in0=P["th"][:], scalar1=float(CS), scalar2=None, op0=ALU.mult), r=[bp], w=[bp])
    sincos(P["sC"][:], P["cC"][:], P["t3"][:], 4, P["t1"][:], P["t2"][:], ti[:, 0:4])
    WW = []
    for par in range(2):
        Wd = {}
        for nm in ("t1", "t2", "t3", "t4", "br", "bi", "zr", "zi", "q1", "q2", "q3", "q4"):
            Wd[nm] = sb("s5_w%d_%s" % (par, nm), [128, CS])
        WW.append(Wd)
    b_wl = [Buf("w0"), Buf("w1")]; b_zl = [Buf("z0"), Buf("z1")]; b_ql = [Buf("q0"), Buf("q1")]
    init = sb("s5_init", [128, 4, 2]); itmp = sb("s5_itmp", [128, 4, 2]); b_il = [Buf("init%d" % j) for j in range(4)]
    for j in range(4):
        V(lambda: nc.vector.memset(init[:, j, :], 0.0), w=[b_il[j]])
    yv = sb("s5_yv", [128, CS]); y2 = sb("s5_y2", [128, CS]); yo = sb("s5_yo", [128, CS], odt); b_y = Buf("y")
    ps_al = [mk.ps("s5_ps_a%d" % i, [128, 512]) for i in range(2)]; ps_bl = [mk.ps("s5_ps_b%d" % i, [128, 512]) for i in range(2)]
    ps_y = mk.ps("s5_ps_y", [128, 512])
    b_pal, b_pbl, b_py = [Buf(), Buf()], [Buf(), Buf()], Buf()
    uts = [sb("s5_u%d" % i, [128, CS]) for i in range(2)]; b_ul = [Buf(), Buf()]
    NCK = T // CS

    def stage1(n):
        chk, j = n // 4, n % 4
        par = n % 2
        ut = uts[chk % 2]; b_u = b_ul[chk % 2]
        if j == 0:
            mk.dma("sp", ut[:], uT[:, chk * CS:(chk + 1) * CS], writes=[b_u])
        W = WW[par]; b_w = b_wl[par]
        ps_a = ps_al[par]; ps_b = ps_bl[par]; b_pa = b_pal[par]; b_pb = b_pbl[par]
        PE(lambda: nc.tensor.matmul(ps_a[:, :], lhsT=BbT[:, j, 0, :], rhs=ut[:, :], start=True, stop=True), r=[bp, b_u], w=[b_pa])
        PE(lambda: nc.tensor.matmul(ps_b[:, :], lhsT=BbT[:, j, 1, :], rhs=ut[:, :], start=True, stop=True), r=[bp, b_u], w=[b_pb])

    def stage1v(n):
        chk, j = n // 4, n % 4
        par = n % 2
        W = WW[par]; b_w = b_wl[par]
        ps_a = ps_al[par]; ps_b = ps_bl[par]; b_pa = b_pal[par]; b_pb = b_pbl[par]
        cj, sj = cosT[:, j, :], sinT[:, j, :]
        V(lambda: nc.vector.tensor_tensor(out=W["t1"][:], in0=ps_a[:, :], in1=cj, op=ALU.mult), r=[bp], w=[b_pa, b_w])
        V(lambda: nc.vector.tensor_tensor(out=W["t4"][:], in0=ps_a[:, :], in1=sj, op=ALU.mult), r=[bp], w=[b_pa, b_w])
        V(lambda: nc.vector.tensor_tensor(out=W["t2"][:], in0=ps_b[:, :], in1=sj, op=ALU.mult), r=[bp], w=[b_pb, b_w])
        V(lambda: nc.vector.tensor_tensor(out=W["t3"][:], in0=ps_b[:, :], in1=cj, op=ALU.mult), r=[bp], w=[b_pb, b_w])
        G(lambda: nc.gpsimd.tensor_tensor(out=W["br"][:], in0=W["t1"][:], in1=W["t2"][:], op=ALU.add), r=[b_w], w=[b_w])
        G(lambda: nc.gpsimd.tensor_tensor(out=W["bi"][:], in0=W["t3"][:], in1=W["t4"][:], op=ALU.subtract), r=[b_w], w=[b_w])

    def stage2(n):
        chk, j = n // 4, n % 4
        par = n % 2
        t0 = chk * CS
        ut = uts[chk % 2]; b_u = b_ul[chk % 2]
        W = WW[par]; b_w = b_wl[par]; b_z = b_zl[par]; b_q = b_ql[par]; b_i = b_il[j]
        cj, sj = cosT[:, j, :], sinT[:, j, :]
        rho = P["mag"][:, j:j + 1].to_broadcast([128, CS])
        V(lambda: nc.vector.tensor_tensor_scan(out=W["zr"][:], data0=rho, data1=W["br"][:], initial=init[:, j, 0:1], op0=ALU.mult, op1=ALU.add), r=[b_w, bp, b_i, b_q], w=[b_z])
        V(lambda: nc.vector.tensor_tensor_scan(out=W["zi"][:], data0=rho, data1=W["bi"][:], initial=init[:, j, 1:2], op0=ALU.mult, op1=ALU.add), r=[b_w, bp, b_i, b_q], w=[b_z])
        zrl, zil = W["zr"][:, CS - 1:CS], W["zi"][:, CS - 1:CS]
        cC, sC = P["cC"][:, j:j + 1], P["sC"][:, j:j + 1]
        V(lambda: nc.vector.tensor_tensor(out=itmp[:, j, 0:1], in0=zil, in1=sC, op=ALU.mult), r=[b_z, bp], w=[b_i])
        V(lambda: nc.vector.scalar_tensor_tensor(out=init[:, j, 0:1], in0=zrl, scalar=cC, in1=itmp[:, j, 0:1], op0=ALU.mult, op1=ALU.subtract), r=[b_z, bp], w=[b_i])
        V(lambda: nc.vector.tensor_tensor(out=itmp[:, j, 1:2], in0=zrl, in1=sC, op=ALU.mult), r=[b_z, bp], w=[b_i])
        V(lambda: nc.vector.scalar_tensor_tensor(out=init[:, j, 1:2], in0=zil, scalar=cC, in1=itmp[:, j, 1:2], op0=ALU.mult, op1=ALU.add), r=[b_z, bp], w=[b_i])
        G(lambda: nc.gpsimd.tensor_tensor(out=W["q1"][:], in0=W["zr"][:], in1=cj, op=ALU.mult), r=[b_z, bp], w=[b_q])
        G(lambda: nc.gpsimd.tensor_tensor(out=W["q2"][:], in0=W["zi"][:], in1=sj, op=ALU.mult), r=[b_z, bp], w=[b_q])
        V(lambda: nc.vector.tensor_tensor(out=W["q3"][:], in0=W["zi"][:], in1=cj, op=ALU.mult), r=[b_z, bp], w=[b_q])
        V(lambda: nc.vector.tensor_tensor(out=W["q4"][:], in0=W["zr"][:], in1=sj, op=ALU.mult), r=[b_z, bp], w=[b_q])
        for qi, nm in enumerate(("q1", "q2", "q3", "q4")):
            PE(lambda: nc.tensor.matmul(ps_y[:, :], lhsT=CM[:, j, qi, :], rhs=W[nm][:, :], start=(j == 0 and qi == 0), stop=(j == 3 and qi == 3)), r=[bp, b_q], w=[b_py])
        if j == 3:
            V(lambda: nc.vector.scalar_tensor_tensor(out=yv[:], in0=ut[:], scalar=dsk[:, 0:1], in1=ps_y[:, :], op0=ALU.mult, op1=ALU.add), r=[b_u, bc], w=[b_py, b_y])
            A(lambda: nc.scalar.activation(out=y2[:], in_=yv[:], func=AF.Square), r=[b_y], w=[b_y])
            V(lambda: nc.vector.tensor_scalar(out=y2[:], in0=y2[:], scalar1=0.044715, scalar2=1.0, op0=ALU.mult, op1=ALU.add), r=[b_y], w=[b_y])
            V(lambda: nc.vector.tensor_tensor(out=y2[:], in0=y2[:], in1=yv[:], op=ALU.mult), r=[b_y], w=[b_y])
            A(lambda: nc.scalar.activation(out=y2[:], in_=y2[:], func=AF.Tanh, scale=0.7978845608028654), r=[b_y], w=[b_y])
            V(lambda: nc.vector.scalar_tensor_tensor(out=y2[:], in0=y2[:], scalar=1.0, in1=yv[:], op0=ALU.add, op1=ALU.mult), r=[b_y], w=[b_y])
            A(lambda: nc.scalar.mul(out=yo[:], in_=y2[:], mul=0.5), r=[b_y], w=[b_y])
            mk.dma("sp", yT[:, t0:t0 + CS], yo[:], reads=[b_y], is_output=True)

    NTL_ = NCK * 4
    stage1(0)
    for n in range(NTL_ + 1):
        if n + 1 < NTL_:
            stage1(n + 1)
        if n < NTL_:
            stage1v(n)
        if n >= 1:
            stage2(n - 1)


def build(which, T):
    nc = bass.Bass("TRN2", target_bir_lowering=False)
    dt = lambda n, s, k="ExternalInput": nc.dram_tensor(n, s, F32, kind=k).ap()
    with ExitStack() as ctx:
        mk = MK(nc, ctx)
        if which == "conv":
            emit_conv(nc, mk, T, dt("cvin", [384, T]), dt("cw", [128, 3]), dt("yT", [128, T], "ExternalOutput"), TB=min(T, 2048))
        elif which == "attn":
            emit_attn(nc, mk, T, dt("qkv", [256, T]), dt("btab", [2, 128, 2, 256]), dt("sinkt", [128, 2]), dt("ident", [128, 128]), dt("yT", [128, T], "ExternalOutput"))
        elif which == "s5":
            emit_s5(nc, mk, T, dt("uT", [128, T]), dt("s5par", [128, 4, 3]), dt("s5bb", [128, 4, 2, 16]), dt("s5cc", [128, 4, 2, 16]),
                    dt("s5d", [128, 1]), dt("s5iota", [128, CS]), dt("ident", [128, 128]), dt("yT", [128, T], "ExternalOutput"))
        mk.finish("sp")
        print(which, "ops", mk.nops, "waits", mk.nwaits)
    return nc


import math
import numpy as np
from contextlib import ExitStack

D = 2048
TOK = 2048
NT = TOK // 128
NE = 32
CAP = 256
ALPHA = 4 ** 0.25
DE = 512


def k3_consts():
    tp = np.arange(128)[:, None]
    t = np.arange(128)[None, :]
    U = (tp < t).astype(np.float32)
    ecap = np.broadcast_to((np.arange(NE) * CAP).astype(np.float32)[None, :], (128, NE)).copy()
    return {"U": U, "ecap": ecap, "ident": np.eye(128, dtype=np.float32)}


def emit_k3(nc, mk, ymixT, x, w_out, glu_w, glu_b, rows, wr, br, w1, w3, w2, cst, x1s, Xg, Yg, xout, ymload=None, rowload=None):
    V = lambda fn, r=(), w=(): mk.op("dve", fn, r, w)
    A = lambda fn, r=(), w=(): mk.op("act", fn, r, w)
    G = lambda fn, r=(), w=(): mk.op("pool", fn, r, w)
    PE = lambda fn, r=(), w=(): mk.op("pe", fn, r, w, skip_same=True)
    gw = mk.sb("k3_gw", [128, NT, 2]); slot = mk.sb("k3_slot", [128, NT, 2], I32); b_rt = Buf("route")
    ident = mk.sb("k3_ident", [128, 128]); identb = mk.sb("k3_identb", [128, 128], BF16); bc = Buf("c")
    mk.dma("sp", ident[:], cst["ident"], writes=[bc])
    V(lambda: nc.vector.tensor_copy(out=identb[:], in_=ident[:]), r=[bc], w=[bc])
    with ExitStack() as pa:
        sb = lambda n, s, dt=F32: pa.enter_context(nc.sbuf_tensor("k3a%d_" % mk.gen + n, list(s), dt))
        ps = lambda n, s, dt=F32: pa.enter_context(nc.psum_tensor("k3a%d_" % mk.gen + n, list(s), dt))
        wo = sb("wo", [128, 16, D], BF16); gluw = sb("gluw", [128, 4, 512], BF16); glub = sb("glub", [128, 4])
        R = [sb("row%d" % i, [128, D]) for i in range(5)]
        wrt = sb("wr", [128, 16, 36]); brt = sb("br", [128, 36]); Ut = sb("U", [128, 128]); ones = sb("ones", [128, 128]); ecap = sb("ecap", [128, NE])
        Srun = sb("Srun", [128, NE]); b_S = Buf("S")
        if ymload is None:
            ym = [sb("ym%d" % i, [128, 16, 128], BF16) for i in range(2)]; b_ym = [Buf(), Buf()]
        else:
            ymbig = sb("ymbig", [128, 16, 1024], BF16); _bym = Buf()
            ym = None; b_ym = [_bym, _bym]
        xt = [sb("xt%d" % i, [128, D]) for i in range(2)]; b_xt = [Buf(), Buf()]
        sg = sb("sg", [128, 4, 128]); b_sg = Buf()
        xr = sb("xr", [128, D]); b_xr = Buf()
        xn = xr; b_xn = b_xr
        _x1 = sb("x1_0", [128, D]); _bx1 = Buf()
        x1 = [_x1, _x1]; b_x1 = [_bx1, _bx1]
        _h2 = sb("h2_0", [128, D]); _bh2 = Buf()
        h2l = [_h2, _h2]; b_h2l = [_bh2, _bh2]
        hb = [sb("hb%d" % i, [128, D], BF16) for i in range(2)]; b_hb = [Buf(), Buf()]
        h2T = sb("h2T", [128, 16, 128]); b_h2T = Buf()
        st = sb("st", [128, 4, 6]); mv = sb("mv", [128, 2]); rs = sb("rs", [128, 1]); nmr = sb("nmr", [128, 1]); b_s = Buf()
        lg = sb("lg", [128, 36]); rt = sb("rt", [128, 16]); em = sb("em", [128, 32]); em2 = sb("em2", [128, 32])
        oh1 = sb("oh1", [128, 32]); oh2 = sb("oh2", [128, 32]); Mk = sb("Mk", [128, 32]); rank = sb("rank", [128, 32]); t32 = sb("t32", [128, 32]); pen = sb("pen", [128, 4]); ohg = sb("ohg", [128, 4]); eg = sb("eg", [128, 4])
        b_r = Buf("r")
        p_g = ps("p_g", [128, 512]); b_pg = Buf()
        p_o = [ps("p_o%d" % i, [128, 512]) for i in range(2)]; b_po = [Buf(), Buf()]
        p_t = [ps("p_t%d" % i, [128, 512]) for i in range(2)]; b_pt = [Buf(), Buf()]
        p_r = ps("p_r", [128, 512]); b_pr = Buf()
        p_k = ps("p_k", [128, 512]); b_pk = Buf()
        mk.dma("pool", wo[:], w_out.rearrange("(k p) c -> p k c", p=128), writes=[bc])
        mk.dma("pool", gluw[:], glu_w.rearrange("(k p) c -> p k c", p=128), writes=[bc])
        mk.dma("sp", glub[:], glu_b, writes=[bc])
        if rowload is None:
            rowload = lambda dst, ri, bcx: mk.dma("sp", dst[:], rows[ri], writes=[bcx])
        for i, ri in enumerate((0, 1, 2, 3, 4)):
            rowload(R[i], ri, bc)
        G(lambda: nc.gpsimd.tensor_scalar(out=R[0][:], in0=R[0][:], scalar1=1.0, scalar2=None, op0=ALU.add), r=[bc], w=[bc])
        G(lambda: nc.gpsimd.tensor_scalar(out=R[3][:], in0=R[3][:], scalar1=1.0, scalar2=None, op0=ALU.add), r=[bc], w=[bc])
        mk.dma("sp", wrt[:], wr.rearrange("(k p) c -> p k c", p=128), writes=[bc])
        mk.dma("sp", brt[:], br, writes=[bc])
        mk.dma("sp", Ut[:], cst["U"], writes=[bc])
        mk.dma("sp", ecap[:], cst["ecap"], writes=[bc])
        V(lambda: nc.vector.memset(ones[:], 1.0), w=[bc])
        V(lambda: nc.vector.memset(Srun[:], 0.0), w=[b_S])
        zt = hb[0]; b_z = b_hb[0]
        V(lambda: nc.vector.memset(zt[:], 0.0), w=[b_z])
        b_Xg = Buf("Xg")
        XgV = Xg.rearrange("(a p) c -> p a c", p=128)
        for a in range(NE * CAP // 128):
            mk.dma("sp", XgV[:, a, :], zt[:], reads=[b_z], writes=[b_Xg])

        def front(t):
            i = t % 2
            ts_ = slice(t * 128, (t + 1) * 128)
            if ymload is None:
                mk.dma("pool", ym[i][:], ymixT[:, ts_].rearrange("(k p) t -> p k t", p=128), writes=[b_ym[i]])
                ymt = ym[i]
            else:
                if t % 8 == 0:
                    ymload(t // 8, ymbig, b_ym[i])
                ymt = ymbig[:, :, (t % 8) * 128:(t % 8 + 1) * 128]
            mk.dma("sp", xt[i][:], x[ts_, :], writes=[b_xt[i]])
            for oc in range(4):
                for kc in range(4):
                    PE(lambda: nc.tensor.matmul(p_g[:, oc * 128:(oc + 1) * 128], lhsT=gluw[:, kc, oc * 128:(oc + 1) * 128], rhs=ymt[:, 12 + kc, :], start=(kc == 0), stop=(kc == 3)),
                       r=[bc, b_ym[i]], w=[b_pg])
            for oc in range(4):
                A(lambda: nc.scalar.activation(out=sg[:, oc, :], in_=p_g[:, oc * 128:(oc + 1) * 128], func=AF.Sigmoid, bias=glub[:, oc:oc + 1], scale=1.0), r=[bc], w=[b_pg, b_sg])
            V(lambda: nc.vector.tensor_tensor(out=ymt[:, 12:16, :], in0=ymt[:, 12:16, :], in1=sg[:], op=ALU.mult), r=[b_sg], w=[b_ym[i]])
            for cc in range(4):
                j = cc % 2
                for k in range(16):
                    PE(lambda: nc.tensor.matmul(p_o[j][:, :], lhsT=ymt[:, k, :], rhs=wo[:, k, cc * 512:(cc + 1) * 512], start=(k == 0), stop=(k == 15)), r=[b_ym[i], bc], w=[b_po[j]])
                V(lambda: nc.vector.tensor_tensor(out=xr[:, cc * 512:(cc + 1) * 512], in0=p_o[j][:, :], in1=R[0][:, cc * 512:(cc + 1) * 512], op=ALU.mult), r=[bc], w=[b_po[j], b_xr])
            V(lambda: nc.vector.scalar_tensor_tensor(out=xr[:], in0=xt[i][:], scalar=ALPHA, in1=xr[:], op0=ALU.mult, op1=ALU.add), r=[b_xt[i]], w=[b_xr])
            ln_stats(nc, mk, xr, b_xr, st, mv, rs, nmr, b_s)
            A(lambda: nc.scalar.activation(out=xn[:], in_=xr[:], func=AF.Identity, bias=nmr[:, 0:1], scale=rs[:, 0:1]), r=[b_s], w=[b_xn])
            V(lambda: nc.vector.tensor_tensor(out=xn[:], in0=xn[:], in1=R[1][:], op=ALU.mult), r=[bc], w=[b_xn])
            G(lambda: nc.gpsimd.tensor_tensor(out=x1[i][:], in0=xn[:], in1=R[2][:], op=ALU.add), r=[b_xn, bc], w=[b_x1[i]])
            mk.dma("sp", x1s[ts_, :], x1[i][:], reads=[b_x1[i]])
            ln_stats(nc, mk, x1[i], b_x1[i], st, mv, rs, nmr, b_s)
            A(lambda: nc.scalar.activation(out=xn[:], in_=x1[i][:], func=AF.Identity, bias=nmr[:, 0:1], scale=rs[:, 0:1]), r=[b_x1[i], b_s], w=[b_xn])
            V(lambda: nc.vector.tensor_tensor(out=xn[:], in0=xn[:], in1=R[3][:], op=ALU.mult), r=[bc], w=[b_xn])
            h2 = h2l[i]; b_h2 = b_h2l[i]
            G(lambda: nc.gpsimd.tensor_tensor(out=h2[:], in0=xn[:], in1=R[4][:], op=ALU.add), r=[b_xn, bc], w=[b_h2])
            A(lambda: nc.scalar.copy(out=hb[i][:], in_=h2[:]), r=[b_h2], w=[b_hb[i]])
        def tail(t):
            i = t % 2
            ts_ = slice(t * 128, (t + 1) * 128)
            h2 = h2l[i]; b_h2 = b_h2l[i]
            for half in range(4):
                j = half % 2
                for kk in range(4):
                    k = half * 4 + kk
                    PE(lambda: nc.tensor.transpose(p_t[j][:, kk * 128:(kk + 1) * 128], h2[:, k * 128:(k + 1) * 128], ident[:, :]), r=[b_h2, bc], w=[b_pt[j]])
                if j == 0:
                    A(lambda: nc.scalar.copy(out=h2T[:, half * 4:(half + 1) * 4, :].rearrange("p a b -> p (a b)"), in_=p_t[j][:, :]), r=[], w=[b_pt[j], b_h2T])
                else:
                    V(lambda: nc.vector.tensor_copy(out=h2T[:, half * 4:(half + 1) * 4, :].rearrange("p a b -> p (a b)"), in_=p_t[j][:, :]), r=[], w=[b_pt[j], b_h2T])
            for k in range(16):
                PE(lambda: nc.tensor.matmul(p_r[:, 0:36], lhsT=h2T[:, k, :], rhs=wrt[:, k, :], start=(k == 0), stop=(k == 15)), r=[b_h2T, bc], w=[b_pr])
            V(lambda: nc.vector.tensor_tensor(out=lg[:], in0=p_r[:, 0:36], in1=brt[:], op=ALU.add), r=[bc], w=[b_pr, b_r])
            R_ = lambda fn: V(fn, r=[b_r, bc], w=[b_r])
            R_(lambda: nc.vector.reduce_max(out=rt[:, 0:1], in_=lg[:, 0:4], axis=AX.X))
            R_(lambda: nc.vector.tensor_scalar(out=ohg[:], in0=lg[:, 0:4], scalar1=rt[:, 0:1], scalar2=None, op0=ALU.is_ge))
            R_(lambda: nc.vector.tensor_scalar(out=rt[:, 1:2], in0=rt[:, 0:1], scalar1=-1.0, scalar2=None, op0=ALU.mult))
            A(lambda: nc.scalar.activation(out=eg[:], in_=lg[:, 0:4], func=AF.Exp, bias=rt[:, 1:2], scale=1.0, accum_out=rt[:, 2:3]), r=[b_r], w=[b_r])
            R_(lambda: nc.vector.reciprocal(out=rt[:, 3:4], in_=rt[:, 2:3]))
            R_(lambda: nc.vector.tensor_scalar(out=pen[:], in0=ohg[:], scalar1=-1.0, scalar2=1e30, op0=ALU.add, op1=ALU.mult))
            R_(lambda: nc.vector.tensor_tensor(out=em[:].rearrange("p (g e) -> p g e", e=8), in0=lg[:, 4:36].rearrange("p (g e) -> p g e", e=8),
                                               in1=pen[:].unsqueeze(2).to_broadcast([128, 4, 8]), op=ALU.add))
            R_(lambda: nc.vector.reduce_max(out=rt[:, 4:5], in_=em[:], axis=AX.X))
            R_(lambda: nc.vector.tensor_scalar(out=oh1[:], in0=em[:], scalar1=rt[:, 4:5], scalar2=None, op0=ALU.is_ge))
            R_(lambda: nc.vector.scalar_tensor_tensor(out=em2[:], in0=oh1[:], scalar=-1e30, in1=em[:], op0=ALU.mult, op1=ALU.add))
            R_(lambda: nc.vector.reduce_max(out=rt[:, 5:6], in_=em2[:], axis=AX.X))
            R_(lambda: nc.vector.tensor_scalar(out=oh2[:], in0=em2[:], scalar1=rt[:, 5:6], scalar2=None, op0=ALU.is_ge))
            R_(lambda: nc.vector.tensor_tensor(out=rt[:, 6:7], in0=rt[:, 5:6], in1=rt[:, 4:5], op=ALU.subtract))
            A(lambda: nc.scalar.activation(out=rt[:, 7:8], in_=rt[:, 6:7], func=AF.Exp), r=[b_r], w=[b_r])
            R_(lambda: nc.vector.tensor_scalar(out=rt[:, 8:9], in0=rt[:, 7:8], scalar1=1.0, scalar2=None, op0=ALU.add))
            R_(lambda: nc.vector.reciprocal(out=rt[:, 8:9], in_=rt[:, 8:9]))
            R_(lambda: nc.vector.tensor_tensor(out=rt[:, 9:10], in0=rt[:, 7:8], in1=rt[:, 8:9], op=ALU.mult))
            R_(lambda: nc.vector.tensor_tensor(out=Mk[:], in0=oh1[:], in1=oh2[:], op=ALU.add))
            PE(lambda: nc.tensor.matmul(p_k[:, 0:32], lhsT=Ut[:, :], rhs=Mk[:, :], start=True, stop=True), r=[bc, b_r], w=[b_pk])
            PE(lambda: nc.tensor.matmul(p_k[:, 32:64], lhsT=ones[:, :], rhs=Mk[:, :], start=True, stop=True), r=[bc, b_r], w=[b_pk])
            V(lambda: nc.vector.tensor_tensor(out=rank[:], in0=p_k[:, 0:32], in1=Srun[:], op=ALU.add), r=[b_S, b_r], w=[b_pk, b_r])
            V(lambda: nc.vector.tensor_tensor(out=Srun[:], in0=p_k[:, 32:64], in1=Srun[:], op=ALU.add), r=[b_r], w=[b_pk, b_S])
            for kx, oh in enumerate((oh1, oh2)):
                R_(lambda: nc.vector.tensor_tensor(out=t32[:], in0=oh[:], in1=rank[:], op=ALU.mult))
                R_(lambda: nc.vector.reduce_sum(out=rt[:, 10:11], in_=t32[:], axis=AX.X))
                R_(lambda: nc.vector.tensor_tensor(out=t32[:], in0=oh[:], in1=ecap[:], op=ALU.mult))
                R_(lambda: nc.vector.reduce_sum(out=rt[:, 11:12], in_=t32[:], axis=AX.X))
                R_(lambda: nc.vector.tensor_scalar(out=rt[:, 12:13], in0=rt[:, 10:11], scalar1=float(CAP), scalar2=None, op0=ALU.is_ge))
                R_(lambda: nc.vector.tensor_tensor(out=rt[:, 11:12], in0=rt[:, 11:12], in1=rt[:, 10:11], op=ALU.add))
                R_(lambda: nc.vector.scalar_tensor_tensor(out=rt[:, 11:12], in0=rt[:, 12:13], scalar=1e6, in1=rt[:, 11:12], op0=ALU.mult, op1=ALU.add))
                V(lambda: nc.vector.tensor_copy(out=slot[:, t, kx:kx + 1], in_=rt[:, 11:12]), r=[b_r], w=[b_rt])
                R_(lambda: nc.vector.tensor_scalar(out=rt[:, 13:14], in0=rt[:, 12:13], scalar1=-1.0, scalar2=-1.0, op0=ALU.add, op1=ALU.mult))
                R_(lambda: nc.vector.tensor_tensor(out=rt[:, 13:14], in0=rt[:, 13:14], in1=rt[:, 3:4], op=ALU.mult))
                V(lambda: nc.vector.tensor_tensor(out=gw[:, t, kx:kx + 1], in0=rt[:, 13:14], in1=rt[:, 8 + kx:9 + kx], op=ALU.mult), r=[b_r], w=[b_rt])
                mk.idma(Xg, hb[i][:, :], slot[:, t, kx:kx + 1], True, NE * CAP - 1, reads=[b_hb[i], b_rt], writes=[b_Xg])
        for t in range(NT):
            front(t)
            tail(t)
        mk.barrier()
    b_Yg = Buf("Yg")
    with ExitStack() as pb:
        sb = lambda n, s, dt=F32: pb.enter_context(nc.sbuf_tensor("k3b%d_" % mk.gen + n, list(s), dt))
        ps = lambda n, s, dt=F32: pb.enter_context(nc.psum_tensor("k3b%d_" % mk.gen + n, list(s), dt))
        W1 = [sb("w1_%d" % i, [128, 16, DE], BF16) for i in range(2)]
        W3 = [sb("w3_%d" % i, [128, 16, DE], BF16) for i in range(2)]
        W2 = [sb("w2_%d" % i, [128, 4, D], BF16) for i in range(2)]
        b_w = [Buf(), Buf()]
        NTL = CAP // 128
        xg = sb("xg", [128, NTL, D], BF16); b_xg = Buf()
        xgT = sb("xgT", [128, 16, CAP], BF16); b_xgT = Buf()
        ga = sb("ga", [128, CAP]); b_ga = Buf()
        gh = sb("gh", [128, 4, CAP], BF16); b_gh = Buf()
        yo = [sb("yo%d" % i, [128, D]) for i in range(2)]; b_yo = [Buf(), Buf()]
        p_t = [ps("p_t%d" % i, [128, 1024], BF16) for i in range(2)]; b_pt = [Buf(), Buf()]
        p_a = ps("p_a", [128, 512]); p_b = ps("p_b", [128, 512]); b_pa, b_pb = Buf(), Buf()
        p_o = [ps("p_o%d" % i, [128, 512]) for i in range(2)]; b_po = [Buf(), Buf()]

        def load_w(e):
            j = e % 2
            mk.dma("pool", W1[j][:], w1[e].rearrange("(p k) c -> p k c", k=16), writes=[b_w[j]])
            mk.dma("pool", W3[j][:], w3[e].rearrange("(p k) c -> p k c", k=16), writes=[b_w[j]])
            mk.dma("pool", W2[j][:], w2[e].rearrange("(k p) c -> p k c", p=128), writes=[b_w[j]])

        load_w(0)
        yoi = 0
        for e in range(NE):
            j = e % 2
            if e + 1 < NE:
                load_w(e + 1)
            mk.dma("sp", xg[:], Xg[e * CAP:(e + 1) * CAP, :].rearrange("(a p) c -> p a c", p=128), reads=[b_Xg], writes=[b_xg])
            for a in range(NTL):
                for half in range(2):
                    for kk in range(8):
                        k = half * 8 + kk
                        PE(lambda: nc.tensor.transpose(p_t[half][:, kk * 128:(kk + 1) * 128], xg[:, a, :].rearrange("t (p k) -> t k p", k=16)[:, k, :], identb[:, :]), r=[b_xg, bc], w=[b_pt[half]])
                    if half == 0:
                        A(lambda: nc.scalar.copy(out=xgT[:, 0:8, a * 128:(a + 1) * 128], in_=p_t[half][:, :].rearrange("p (k t) -> p k t", t=128)), r=[], w=[b_pt[half], b_xgT])
                    else:
                        V(lambda: nc.vector.tensor_copy(out=xgT[:, 8:16, a * 128:(a + 1) * 128], in_=p_t[half][:, :].rearrange("p (k t) -> p k t", t=128)), r=[], w=[b_pt[half], b_xgT])
            for hc in range(4):
                for k in range(16):
                    PE(lambda: nc.tensor.matmul(p_a[:, 0:CAP], lhsT=W1[j][:, k, hc * 128:(hc + 1) * 128], rhs=xgT[:, k, :], start=(k == 0), stop=(k == 15)), r=[b_w[j], b_xgT], w=[b_pa])
                for k in range(16):
                    PE(lambda: nc.tensor.matmul(p_b[:, 0:CAP], lhsT=W3[j][:, k, hc * 128:(hc + 1) * 128], rhs=xgT[:, k, :], start=(k == 0), stop=(k == 15)), r=[b_w[j], b_xgT], w=[b_pb])
                A(lambda: nc.scalar.activation(out=ga[:], in_=p_a[:, 0:CAP], func=AF.Silu), r=[], w=[b_pa, b_ga])
                V(lambda: nc.vector.tensor_tensor(out=gh[:, hc, :], in0=p_b[:, 0:CAP], in1=ga[:], op=ALU.mult), r=[b_ga], w=[b_pb, b_gh])
            for a in range(NTL):
                y_ = yoi % 2
                yoi += 1
                for cc in range(4):
                    q = cc % 2
                    for hc in range(4):
                        PE(lambda: nc.tensor.matmul(p_o[q][:, :], lhsT=gh[:, hc, a * 128:(a + 1) * 128], rhs=W2[j][:, hc, cc * 512:(cc + 1) * 512], start=(hc == 0), stop=(hc == 3)), r=[b_gh, b_w[j]], w=[b_po[q]])
                    if q == 0:
                        A(lambda: nc.scalar.copy(out=yo[y_][:, cc * 512:(cc + 1) * 512], in_=p_o[q][:, :]), r=[], w=[b_po[q], b_yo[y_]])
                    else:
                        V(lambda: nc.vector.tensor_copy(out=yo[y_][:, cc * 512:(cc + 1) * 512], in_=p_o[q][:, :]), r=[], w=[b_po[q], b_yo[y_]])
                r0 = e * CAP + a * 128
                mk.dma("sp", Yg[r0:r0 + 128, :], yo[y_][:], reads=[b_yo[y_]], writes=[b_Yg])
        mk.barrier()
    with ExitStack() as pc:
        sb = lambda n, s, dt=F32: pc.enter_context(nc.sbuf_tensor("k3c%d_" % mk.gen + n, list(s), dt))
        R = [sb("row%d" % i, [128, D]) for i in range(3)]
        for i, ri in enumerate((5, 6, 7)):
            rowload(R[i], ri, bc)
        G(lambda: nc.gpsimd.tensor_scalar(out=R[0][:], in0=R[0][:], scalar1=1.0, scalar2=None, op0=ALU.add), r=[bc], w=[bc])
        Y = [[sb("Y%d_%d" % (k, i), [128, D]) for i in range(2)] for k in range(2)]
        b_Y = [[Buf(), Buf()], [Buf(), Buf()]]
        for k in range(2):
            for i in range(2):
                V(lambda: nc.vector.memset(Y[k][i][:], 0.0), w=[b_Y[k][i]])
        x1t = [sb("x1t%d" % i, [128, D]) for i in range(2)]; b_x1 = [Buf(), Buf()]
        yml = [sb("ym%d" % i, [128, D]) for i in range(2)]; b_yml = [Buf(), Buf()]
        xnl = [sb("xn%d" % i, [128, D]) for i in range(2)]; b_xnl = [Buf(), Buf()]
        ot = [sb("ot%d" % i, [128, D]) for i in range(2)]; b_ot = [Buf(), Buf()]
        stl = [sb("st%d" % i, [128, 4, 6]) for i in range(2)]; mvl = [sb("mv%d" % i, [128, 2]) for i in range(2)]
        rsl = [sb("rs%d" % i, [128, 1]) for i in range(2)]; nmrl = [sb("nmr%d" % i, [128, 1]) for i in range(2)]; b_sl = [Buf(), Buf()]
        for t in range(NT):
            i = t % 2
            ts_ = slice(t * 128, (t + 1) * 128)
            ym = yml[i]; b_ym = b_yml[i]; xn = xnl[i]; b_xn = b_xnl[i]
            st, mv, rs, nmr, b_s = stl[i], mvl[i], rsl[i], nmrl[i], b_sl[i]
            for k in range(2):
                mk.idma(Y[k][i][:, :], Yg, slot[:, t, k:k + 1], False, NE * CAP - 1, reads=[b_Yg, b_rt], writes=[b_Y[k][i]])
            mk.dma("sp", x1t[i][:], x1s[ts_, :], writes=[b_x1[i]])
            A(lambda: nc.scalar.activation(out=ym[:], in_=Y[0][i][:], func=AF.Copy, scale=gw[:, t, 0:1]), r=[b_Y[0][i], b_rt], w=[b_ym])
            V(lambda: nc.vector.scalar_tensor_tensor(out=ym[:], in0=Y[1][i][:], scalar=gw[:, t, 1:2], in1=ym[:], op0=ALU.mult, op1=ALU.add), r=[b_Y[1][i], b_rt], w=[b_ym])
            G(lambda: nc.gpsimd.tensor_tensor(out=ym[:], in0=ym[:], in1=R[0][:], op=ALU.mult), r=[bc], w=[b_ym])
            V(lambda: nc.vector.scalar_tensor_tensor(out=ym[:], in0=x1t[i][:], scalar=ALPHA, in1=ym[:], op0=ALU.mult, op1=ALU.add), r=[b_x1[i]], w=[b_ym])
            ln_stats(nc, mk, ym, b_ym, st, mv, rs, nmr, b_s)
            A(lambda: nc.scalar.activation(out=xn[:], in_=ym[:], func=AF.Identity, bias=nmr[:, 0:1], scale=rs[:, 0:1]), r=[b_ym, b_s], w=[b_xn])
            V(lambda: nc.vector.tensor_tensor(out=xn[:], in0=xn[:], in1=R[1][:], op=ALU.mult), r=[bc], w=[b_xn])
            G(lambda: nc.gpsimd.tensor_tensor(out=ot[i][:], in0=xn[:], in1=R[2][:], op=ALU.add), r=[b_xn, bc], w=[b_ot[i]])
            mk.dma("sp", xout[ts_, :], ot[i][:], reads=[b_ot[i]], is_output=True)
        mk.barrier()


def build_k3():
    nc = bass.Bass("TRN2", target_bir_lowering=False)
    dt = lambda n, s, k="ExternalInput", d=F32: nc.dram_tensor(n, s, d, kind=k).ap()
    ymixT = dt("ymixT", [D, TOK]); x = dt("x", [TOK, D]); w_out = dt("w_out", [D, D]); glu_w = dt("glu_w", [512, 512]); glu_b = dt("glu_b", [128, 4])
    rows = dt("rows", [8, 128, D]); wr = dt("wr", [D, 36]); br = dt("br", [128, 36])
    w1 = dt("w1", [NE, D, DE]); w3 = dt("w3", [NE, D, DE]); w2 = dt("w2", [NE, DE, D])
    cst = {"U": dt("U", [128, 128]), "ecap": dt("ecap", [128, NE]), "ident": dt("ident", [128, 128])}
    x1s = dt("x1s", [TOK, D], "Internal"); Xg = dt("Xg", [NE * CAP, D], "Internal", BF16); Yg = dt("Yg", [NE * CAP, D], "Internal")
    xout = dt("xout", [TOK, D], "ExternalOutput")
    with ExitStack() as ctx:
        mk = MK(nc, ctx)
        emit_k3(nc, mk, ymixT, x, w_out, glu_w, glu_b, rows, wr, br, w1, w3, w2, cst, x1s, Xg, Yg, xout)
        mk.finish("sp")
        print("k3 ops", mk.nops, "waits", mk.nwaits)
    return nc


def k3_host_inputs(prm, mod, l, b):
    rep = lambda v: np.ascontiguousarray(np.broadcast_to(v[None, :], (128, v.shape[0])))
    sh1, sc1, gt1, sh2, sc2, gt2 = [mod[l, b, i * D:(i + 1) * D] for i in range(6)]
    rows = np.stack([rep(gt1), rep(prm["ln_g"][l, 0]), rep(prm["ln_b"][l, 0]), rep(sc2), rep(sh2), rep(gt2), rep(prm["ln_g"][l, 1]), rep(prm["ln_b"][l, 1])])
    wr = np.ascontiguousarray(np.concatenate([prm["router_group_w"][l], prm["router_expert_w"][l]], axis=1))
    br = rep(np.concatenate([prm["router_group_b"][l], prm["router_expert_b"][l]]))
    d = {"rows": rows, "wr": wr, "br": br, "w_out": prm["w_out"][l], "glu_w": prm["s5_glu_w"][l],
         "glu_b": np.ascontiguousarray(prm["s5_glu_b"][l].reshape(4, 128).T),
         "w1": prm["moe_w1"][l], "w3": prm["moe_w3"][l], "w2": prm["moe_w2"][l]}
    d.update(k3_consts())
    return d


G_ = 512
RW_OFF = 3 * G_
RW_COLS = 3 * G_ + 96 + 96 + 128
ATT_OFF = RW_OFF + RW_COLS
S5_OFF = ATT_OFF + 512 + 2 * 128
SEQ = 8192
RG4 = [[0, 1, 2, 3], [4, 5, 6, 7]]
NMINE = 1472
MODC = 3072


def emit_k0f(nc, mk, cT, w, bb, modin):
    ct = mk.sb("k0_ct", [128, 16, 2]); sct = mk.sb("k0_sct", [128, 16, 2])
    wt = [mk.sb("k0_wt%d" % i, [128, 16, 512]) for i in range(2)]
    bt = mk.sb("k0_bt", [2, 2, MODC]); ot = mk.sb("k0_ot", [2, 2, MODC])
    P = [mk.ps("k0_P%d" % i, [2, 512]) for i in range(2)]
    b_c, b_b, b_o = Buf(), Buf(), Buf()
    b_w = [Buf(), Buf()]; b_p = [Buf(), Buf()]
    mk.dma("sp", ct[:], cT, writes=[b_c])
    mk.dma("sp", bt[:], bb.rearrange("l b c -> b l c"), writes=[b_b])
    mk.op("act", lambda: nc.scalar.activation(out=sct[:], in_=ct[:], func=AF.Silu), reads=[b_c], writes=[b_c])
    it = 0
    for l in range(2):
        for n in range(MODC // 512):
            i = it % 2
            it += 1
            mk.dma("sp", wt[i][:], w[l, :, n * 512:(n + 1) * 512].rearrange("(k p) c -> p k c", p=128), writes=[b_w[i]])
            for k in range(16):
                mk.op("pe", lambda: nc.tensor.matmul(P[i][:], lhsT=sct[:, k, :], rhs=wt[i][:, k, :], start=(k == 0), stop=(k == 15)),
                      reads=[b_c, b_w[i]], writes=[b_p[i]], skip_same=True)
            mk.op("dve", lambda: nc.vector.tensor_tensor(out=ot[:, l, n * 512:(n + 1) * 512], in0=P[i][:], in1=bt[:, l, n * 512:(n + 1) * 512], op=ALU.add),
                  reads=[b_b], writes=[b_p[i], b_o])
    mk.dma("sp", modin.rearrange("(o l) c -> o l c", o=1), ot[0:1, :, :], reads=[b_o])


def mod_row_load(nc, mk, dst, modall, l, chunk, bc):
    c0 = chunk * 2048
    done = 0
    while done < 2048:
        col = c0 + done
        r = col // MODC
        off = col % MODC
        n = min(2048 - done, MODC - off)
        src = modall[r * 2 + l:r * 2 + l + 1, off:off + n].partition_broadcast(128)
        mk.dma("sp", dst[:, done:done + n], src, writes=[bc])
        done += n


def emit_k1a(nc, mk, x, modall, l, ident_d, hTs):
    NT_ = TOK // 128
    xt = [mk.sb("a_xt%d" % i, [128, D]) for i in range(2)]
    xn = mk.sb("a_xn", [128, D]); h1 = mk.sb("a_h1", [128, D])
    hb = [mk.sb("a_hb%d" % i, [128, D], BF16) for i in range(2)]
    hT = mk.sb("a_hT", [128, 16, TOK], BF16)
    sct = mk.sb("a_sct", [128, D]); sht = mk.sb("a_sht", [128, D])
    idf = mk.sb("a_idf", [128, 128]); idb = mk.sb("a_idb", [128, 128], BF16)
    st = mk.sb("a_st", [128, 4, 6]); mv = mk.sb("a_mv", [128, 2]); rs = mk.sb("a_rs", [128, 1]); nmr = mk.sb("a_nmr", [128, 1])
    PT = [mk.ps("a_PT%d" % i, [128, 8, 128], BF16) for i in range(2)]
    b_x = [Buf(), Buf()]
    b_xn, b_h1, b_s, b_sc, b_sh, b_id, b_hT = Buf(), Buf(), Buf(), Buf(), Buf(), Buf(), Buf()
    b_hb = [Buf(), Buf()]; b_pt = [Buf(), Buf()]
    mod_row_load(nc, mk, sct, modall, l, 1, b_sc)
    mod_row_load(nc, mk, sht, modall, l, 0, b_sh)
    mk.dma("sp", idf[:], ident_d, writes=[b_id])
    mk.op("dve", lambda: nc.vector.tensor_copy(out=idb[:], in_=idf[:]), reads=[b_id], writes=[b_id])
    mk.op("pool", lambda: nc.gpsimd.tensor_scalar(out=sct[:], in0=sct[:], scalar1=1.0, scalar2=None, op0=ALU.add), reads=[b_sc], writes=[b_sc])
    for t in range(NT_):
        i = t % 2
        mk.dma("sp", xt[i][:], x[t * 128:(t + 1) * 128, :], writes=[b_x[i]])
        ln_stats(nc, mk, xt[i], b_x[i], st, mv, rs, nmr, b_s)
        mk.op("act", lambda: nc.scalar.activation(out=xn[:], in_=xt[i][:], func=AF.Identity, bias=nmr[:, 0:1], scale=rs[:, 0:1]),
              reads=[b_x[i], b_s], writes=[b_xn])
        mk.op("dve", lambda: nc.vector.tensor_tensor(out=h1[:], in0=xn[:], in1=sct[:], op=ALU.mult), reads=[b_xn, b_sc], writes=[b_h1])
        mk.op("pool", lambda: nc.gpsimd.tensor_tensor(out=hb[i][:], in0=h1[:], in1=sht[:], op=ALU.add), reads=[b_h1, b_sh], writes=[b_hb[i]])
        for half in range(2):
            for kk in range(8):
                k = half * 8 + kk
                mk.op("pe", lambda: nc.tensor.transpose(PT[half][:, kk, :], hb[i][:, k * 128:(k + 1) * 128], idb[:]),
                      reads=[b_hb[i], b_id], writes=[b_pt[half]], skip_same=True)
            if half == 0:
                mk.op("act", lambda: nc.scalar.copy(out=hT[:, 0:8, t * 128:(t + 1) * 128], in_=PT[half][:]), reads=[], writes=[b_pt[half], b_hT])
            else:
                mk.op("dve", lambda: nc.vector.tensor_copy(out=hT[:, 8:16, t * 128:(t + 1) * 128], in_=PT[half][:]), reads=[], writes=[b_pt[half], b_hT])
    mk.dma("sp", hTs.rearrange("(k p) t -> p k t", p=128), hT[:], reads=[b_hT])


def emit_k1b(nc, mk, hTg, wmine, pmine):
    NB_ = (NMINE + 127) // 128
    wt = mk.sb("b_wt", [128, 16, NB_ * 128], BF16); b_w = Buf()
    ht = [mk.sb("b_ht%d" % i, [128, 16, 512], BF16) for i in range(2)]; b_h = [Buf(), Buf()]
    ot = [mk.sb("b_ot%d" % i, [128, 512]) for i in range(4)]; b_o = [Buf() for _ in range(4)]
    PM = [mk.ps("b_PM%d" % i, [128, 512]) for i in range(4)]; b_pm = [Buf() for _ in range(4)]
    for j in range(NB_):
        c0 = j * 128
        cw = min(128, NMINE - c0)
        mk.dma("pool", wt[:, :, c0:c0 + cw], wmine[:, c0:c0 + cw].rearrange("(k p) c -> p k c", p=128), writes=[b_w])
    pi = 0
    for tc in range(SEQ // 512):
        i = tc % 2
        r = tc // 4
        t0 = (tc % 4) * 512
        src = hTg.rearrange("(c r h p) t -> r p c h t", c=8, r=4, h=2, p=128)[r]
        for c in range(8):
            mk.dma("sp", ht[i][:, 2 * c:2 * c + 2, :], src[:, c, :, t0:t0 + 512], writes=[b_h[i]])
        for j in range(NB_):
            c0 = j * 128
            cw = min(128, NMINE - c0)
            q = pi % 4
            pi += 1
            for k in range(16):
                mk.op("pe", lambda: nc.tensor.matmul(PM[q][0:cw, :], lhsT=wt[:, k, c0:c0 + cw], rhs=ht[i][:, k, :], start=(k == 0), stop=(k == 15)),
                      reads=[b_w, b_h[i]], writes=[b_pm[q]], skip_same=True)
            if q % 2 == 0:
                mk.op("act", lambda: nc.scalar.copy(out=ot[q][0:cw, :], in_=PM[q][0:cw, :]), reads=[], writes=[b_pm[q], b_o[q]])
            else:
                mk.op("dve", lambda: nc.vector.tensor_copy(out=ot[q][0:cw, :], in_=PM[q][0:cw, :]), reads=[], writes=[b_pm[q], b_o[q]])
            mk.dma("sp", pmine[c0:c0 + cw, tc * 512:(tc + 1) * 512], ot[q][0:cw, :], reads=[b_o[q]])


def build_fused():
    nc = bass.Bass("TRN2", target_bir_lowering=False)
    T = SEQ
    din = lambda n, s, d=F32: nc.dram_tensor(n, s, d, kind="ExternalInput").ap()
    scr = lambda n, s, d=F32: nc.dram_tensor(n, s, d).ap()
    x_in = din("x", [TOK, D]); cT = din("cT", [128, 16, 2]); w_ada = din("w_ada", [2, D, MODC]); bb = din("bb", [2, 2, MODC])
    wmine = din("wmine", [2, D, NMINE]); w_out = din("w_out", [2, D, D]); lnp = din("lnp", [2, 4, D])
    cw = din("cw", [2, 128, 3]); btab = din("btab", [2, 2, 128, 2, 256]); sinkt = din("sinkt", [2, 128, 2]); ident = din("ident", [128, 128])
    s5par = din("s5par", [2, 128, 4, 3]); s5bb = din("s5bb", [2, 128, 4, 2, 16]); s5cc = din("s5cc", [2, 128, 4, 2, 16]); s5d = din("s5d", [2, 128, 1]); s5iota = din("s5iota", [128, CS])
    par64 = din("par64", [2, 64, 2, 11]); par128 = din("par128", [2, 128, 3]); w2 = din("w2", [2, 96, 128]); a2 = din("a2", [2, 96, 128]); g2 = din("g2", [2, 128, 128]); gnt = din("gnt", [2, 64, 2, 2, 64])
    cst = {"mask1": din("mask1", [64, 512]), "mask3": din("mask3", [64, 256]), "seg": din("seg", [128, TB]), "ident": ident, "U": din("U", [128, 128]), "ecap": din("ecap", [128, NE])}
    glu_w = din("glu_w", [2, 512, 512]); glu_b = din("glu_b", [2, 128, 4]); wr = din("wr", [2, D, 36]); br = din("br", [2, 128, 36])
    w1 = din("w1", [2, NE, D, DE]); w3 = din("w3", [2, NE, D, DE]); w2m = din("w2m", [2, NE, DE, D])
    ymidx_d = din("ymidx", [128, 16, 2], I32)
    y_out = nc.dram_tensor("y", [TOK, D], F32, kind="ExternalOutput").ap()
    modin = scr("modin", [2, MODC]); modall = scr("modall", [8, MODC])
    hTs = scr("hTs", [D, TOK], BF16); hTg = scr("hTg", [4 * D, TOK], BF16)
    pmine = scr("pmine", [NMINE, T])
    yT16 = scr("yT16", [512, T], BF16); ymg = scr("ymg", [2048, T], BF16)
    x1s = scr("x1s", [TOK, D]); Xg = scr("Xg", [NE * CAP, D], BF16); Yg = scr("Yg", [NE * CAP, D]); xcur = scr("xcur", [TOK, D])
    ymg_rows = ymg.rearrange("r (tb t) -> (r tb) t", t=1024)
    with ExitStack() as ctx:
        mk = MK(nc, ctx)
        bD = Buf("dram")
        with mk.scope():
            emit_k0f(nc, mk, cT, w_ada, bb, modin)
        mk.collective("AllGather", RG4, modin, modall, reads=[bD], writes=[bD])
        mk.barrier()
        for l in range(2):
            xsrc = x_in if l == 0 else xcur
            xdst = xcur if l == 0 else y_out
            with mk.scope():
                emit_k1a(nc, mk, xsrc, modall, l, ident, hTs)
            for c in range(8):
                mk.collective("AllGather", RG4, hTs[c * 256:(c + 1) * 256, :], hTg[c * 1024:(c + 1) * 1024, :], reads=[bD], writes=[bD])
            mk.barrier()
            with mk.scope():
                emit_k1b(nc, mk, hTg, wmine[l], pmine)
            def ag(chunks):
                for c in chunks:
                    mk.collective("AllGather", RG4, yT16[c * 64:(c + 1) * 64, :], ymg[c * 256:(c + 1) * 256, :], reads=[], writes=[Buf()])
            with mk.scope():
                emit_conv(nc, mk, T, pmine[0:384, :], cw[l], yT16[0:128, :], odt=BF16)
            ag((0, 1))
            with mk.scope():
                emit_attn(nc, mk, T, pmine[1088:1344, :], btab[l], sinkt[l], ident, yT16[256:384, :], odt=BF16)
            ag((4, 5))
            with mk.scope():
                emit_s5(nc, mk, T, pmine[1344:1472, :], s5par[l], s5bb[l], s5cc[l], s5d[l], s5iota, ident, yT16[384:512, :], odt=BF16)
            ag((6, 7))
            with mk.scope():
                emit_rwkv(nc, mk, T, pmine[384:1088, :], par64[l], par128[l], w2[l], a2[l], g2[l], gnt[l], cst, yT16[128:256, :], odt=BF16)
            for c in (2, 3):
                mk.collective("AllGather", RG4, yT16[c * 64:(c + 1) * 64, :], ymg[c * 256:(c + 1) * 256, :], reads=[bD], writes=[bD])
            mk.barrier()
            with mk.scope():
                ymidx = mk.sb("ymidx_sb", [128, 16, 2], I32); b_idx = Buf()
                mk.dma("sp", ymidx[:], ymidx_d, writes=[b_idx])

                def ymload(hf, ymt, b_ymt):
                    for k in range(16):
                        mk.idma(ymt[:, k, :], ymg_rows, ymidx[:, k, hf:hf + 1], False, 2048 * 8 - 1, reads=[b_idx], writes=[b_ymt])

                def rowload(dst, ri, bcx, _l=l):
                    if ri in (1, 2, 6, 7):
                        j = {1: 0, 2: 1, 6: 2, 7: 3}[ri]
                        mk.dma("sp", dst[:], lnp[_l, j:j + 1, :].partition_broadcast(128), writes=[bcx])
                    else:
                        chunk = {0: 2, 3: 4, 4: 3, 5: 5}[ri]
                        mod_row_load(nc, mk, dst, modall, _l, chunk, bcx)

                emit_k3(nc, mk, None, xsrc, w_out[l], glu_w[l], glu_b[l], None, wr[l], br[l], w1[l], w3[l], w2m[l], cst, x1s, Xg, Yg, xdst,
                        ymload=ymload, rowload=rowload)
        mk.finish("sp")
        mk.barrier()
        print("fused ops", mk.nops, "waits", mk.nwaits)
    return nc


_NC_CACHE = {}


def _get(name, fn):
    if name not in _NC_CACHE:
        _NC_CACHE[name] = fn()
    return _NC_CACHE[name]


def _fused_inputs(prm, core):
    b, q = core // 4, core % 4
    eye = np.eye(128, dtype=np.float32)
    d = {}
    d["x"] = np.ascontiguousarray(prm["x"][b, q * TOK:(q + 1) * TOK])
    cb = prm["c"][b]
    d["cT"] = np.ascontiguousarray(np.stack([cb.reshape(16, 128).T, cb.reshape(16, 128).T], axis=-1))
    sl = slice(q * MODC, (q + 1) * MODC)
    d["w_ada"] = np.ascontiguousarray(prm["w_ada"][:, :, sl])
    d["bb"] = np.ascontiguousarray(np.broadcast_to(prm["b_ada"][:, None, sl], (2, 2, MODC)))
    kv = q // 2
    cols = np.concatenate([np.arange(128 * q, 128 * q + 128), np.arange(G_ + 128 * q, G_ + 128 * q + 128), np.arange(2 * G_ + 128 * q, 2 * G_ + 128 * q + 128),
                           RW_OFF + rwkv_rows(q),
                           np.arange(ATT_OFF + 128 * q, ATT_OFF + 128 * q + 128), np.arange(ATT_OFF + 512 + 64 * kv, ATT_OFF + 512 + 64 * kv + 64),
                           np.arange(ATT_OFF + 640 + 64 * kv, ATT_OFF + 640 + 64 * kv + 64),
                           np.arange(S5_OFF + 128 * q, S5_OFF + 128 * q + 128)])
    assert cols.shape[0] == NMINE
    d["wmine"] = np.ascontiguousarray(prm["w_in"][:, :, cols])
    d["w_out"] = prm["w_out"]
    d["lnp"] = np.ascontiguousarray(np.stack([np.stack([prm["ln_g"][l, 0], prm["ln_b"][l, 0], prm["ln_g"][l, 1], prm["ln_b"][l, 1]]) for l in range(2)]))
    d["cw"] = np.ascontiguousarray(np.stack([prm["conv_w"][l][:, 128 * q:128 * q + 128].T for l in range(2)]))
    tabs = [attn_tables(prm["rel_bias"], prm["attn_sinks"][l], q) for l in range(2)]
    d["btab"] = np.ascontiguousarray(np.stack([t[0] for t in tabs])); d["sinkt"] = np.ascontiguousarray(np.stack([t[1] for t in tabs]))
    d["ident"] = eye
    s5 = [s5_host_inputs(prm, l, q) for l in range(2)]
    for k_ in ("s5par", "s5bb", "s5cc", "s5d"):
        d[k_] = np.ascontiguousarray(np.stack([s[k_] for s in s5]))
    d["s5iota"] = s5[0]["s5iota"]
    rw = [rwkv_host_inputs(prm, l, q) for l in range(2)]
    for k_ in ("par64", "par128", "gnt", "w2", "a2", "g2"):
        d[k_] = np.ascontiguousarray(np.stack([r[k_] for r in rw]))
    for k_ in ("mask1", "mask3", "seg"):
        d[k_] = rw[0][k_]
    kc = k3_consts()
    d["U"] = kc["U"]; d["ecap"] = kc["ecap"]
    d["glu_w"] = prm["s5_glu_w"]
    d["glu_b"] = np.ascontiguousarray(np.stack([prm["s5_glu_b"][l].reshape(4, 128).T for l in range(2)]))
    d["wr"] = np.ascontiguousarray(np.stack([np.concatenate([prm["router_group_w"][l], prm["router_expert_w"][l]], axis=1) for l in range(2)]))
    d["br"] = np.ascontiguousarray(np.stack([np.broadcast_to(np.concatenate([prm["router_group_b"][l], prm["router_expert_b"][l]])[None, :], (128, 36)) for l in range(2)]))
    d["w1"] = prm["moe_w1"]; d["w3"] = prm["moe_w3"]; d["w2m"] = prm["moe_w2"]
    p_ = np.arange(128)[:, None, None]; k_i = np.arange(16)[None, :, None]; t_ = np.arange(2)[None, None, :]
    src_row = ((k_i // 4) * 2 + p_ // 64) * 256 + (k_i % 4) * 64 + p_ % 64
    d["ymidx"] = np.ascontiguousarray((src_row * 8 + q * 2 + t_).astype(np.int32))
    return d


def kernel(**inp):
    prm = {k: np.ascontiguousarray(np.asarray(v, dtype=np.float32)) for k, v in inp.items()}
    cores = list(range(8))
    in_maps = [_fused_inputs(prm, c) for c in cores]
    res = run_bass_kernel_spmd(_get("fused", build_fused), in_maps, core_ids=cores)
    out = np.stack([np.concatenate([res.results[b * 4 + q]["y"] for q in range(4)], axis=0) for b in range(2)])
    return out.astype(np.float32)
```

```python
import numpy as np
from contextlib import ExitStack
import concourse.bass as bass
import concourse.mybir as mybir
from concourse.bass_utils import run_bass_kernel_spmd

F32 = mybir.dt.float32
BF16 = mybir.dt.bfloat16
I32 = mybir.dt.int32
U32 = mybir.dt.uint32
AF = mybir.ActivationFunctionType
ALU = mybir.AluOpType
AX = mybir.AxisListType

EPOCH = 1 << 20


class Buf:
    __slots__ = ("name", "w", "r")

    def __init__(self, name=""):
        self.name = name
        self.w = None
        self.r = {}


class MK:
    def __init__(self, nc, ctx, n_dma_sems=24):
        self.nc = nc
        self.ctx = ctx
        self.eng = {"pe": nc.tensor, "dve": nc.vector, "act": nc.scalar,
                    "pool": nc.gpsimd, "sp": nc.sync}
        self.sem = {}
        self.cnt = {e: 0 for e in self.eng}
        self.known = {e: {} for e in self.eng}
        for e in self.eng:
            self.sem[e] = ctx.enter_context(nc.semaphore("s_" + e))
        self.dma_keys = []
        self.dma_val = {}
        for i in range(n_dma_sems):
            k = ("dma", i)
            self.sem[k] = ctx.enter_context(nc.semaphore("s_dma%d" % i))
            self.dma_keys.append(k)
            self.dma_val[k] = 0
        self.dma_rr = 0
        self.nwaits = 0
        self.nops = 0
        self.out_events = []

    gen = 0

    def sb(self, name, shape, dt=F32):
        return self.ctx.enter_context(self.nc.sbuf_tensor("%s_g%d" % (name, self.gen), list(shape), dt))

    def ps(self, name, shape, dt=F32):
        return self.ctx.enter_context(self.nc.psum_tensor("%s_g%d" % (name, self.gen), list(shape), dt))

    def _wait(self, E, ev):
        if ev is None:
            return
        key, val = ev
        if self.known[E].get(key, 0) >= val:
            return
        self.eng[E].wait_ge(self.sem[key], val)
        self.known[E][key] = val
        self.nwaits += 1

    def _deps(self, E, reads, writes, skip_same=False):
        for b in reads:
            if b.w is not None and not (skip_same and b.w[0] == E):
                self._wait(E, b.w)
        for b in writes:
            if b.w is not None and not (skip_same and b.w[0] == E):
                self._wait(E, b.w)
            for ev in b.r.values():
                if not (skip_same and ev[0] == E):
                    self._wait(E, ev)

    def _mark(self, ev, reads, writes):
        for b in reads:
            b.r[ev[0]] = ev
        for b in writes:
            b.w = ev
            b.r = {}

    def op(self, E, fn, reads=(), writes=(), skip_same=False):
        self._deps(E, reads, writes, skip_same)
        inst = fn()
        self.cnt[E] += 1
        inst.then_inc(self.sem[E], 1)
        ev = (E, self.cnt[E])
        self._mark(ev, reads, writes)
        self.nops += 1
        return ev

    def dma(self, Q, out, in_, reads=(), writes=(), is_output=False, **kw):
        self._deps(Q, reads, writes)
        k = self.dma_keys[self.dma_rr]
        self.dma_rr = (self.dma_rr + 1) % len(self.dma_keys)
        self._wait(Q, (k, self.dma_val[k]) if self.dma_val[k] else None)
        self.dma_val[k] += 16
        inst = self.eng[Q].dma_start(out=out, in_=in_, **kw)
        inst.then_inc(self.sem[k], 16)
        ev = (k, self.dma_val[k])
        self._mark(ev, reads, writes)
        if is_output:
            self.out_events.append(ev)
        self.nops += 1
        return ev

    def finish(self, E="sp"):
        for k in self.dma_keys:
            if self.dma_val[k]:
                self._wait(E, (k, self.dma_val[k]))


def _idma(self, out, in_, idx_ap, scatter, bound, reads=(), writes=(), is_output=False):
    Q = "pool"
    self._deps(Q, reads, writes)
    k = self.dma_keys[self.dma_rr]
    self.dma_rr = (self.dma_rr + 1) % len(self.dma_keys)
    self._wait(Q, (k, self.dma_val[k]) if self.dma_val[k] else None)
    self.dma_val[k] += 16
    off = bass.IndirectOffsetOnAxis(ap=idx_ap, axis=0)
    if not hasattr(self, "_bregs"):
        self._bregs = {}
    if bound not in self._bregs:
        self._bregs[bound] = self.nc.gpsimd.to_reg(bound)
    bound = self._bregs[bound]
    if scatter:
        inst = self.nc.gpsimd.indirect_dma_start(out=out, out_offset=off, in_=in_, in_offset=None, bounds_check=bound, oob_is_err=False)
    else:
        inst = self.nc.gpsimd.indirect_dma_start(out=out, out_offset=None, in_=in_, in_offset=off, bounds_check=bound, oob_is_err=False)
    inst.then_inc(self.sem[k], 16)
    ev = (k, self.dma_val[k])
    self._mark(ev, reads, writes)
    if is_output:
        self.out_events.append(ev)
    self.nops += 1
    return ev


MK.idma = _idma


def _barrier(self):
    for E in self.eng:
        for F in self.eng:
            if self.cnt[F]:
                self._wait(E, (F, self.cnt[F]))
        for k in self.dma_keys:
            if self.dma_val[k]:
                self._wait(E, (k, self.dma_val[k]))
        if getattr(self, "cc_val", 0):
            self._wait(E, ("cc", self.cc_val))


MK.barrier = _barrier


from contextlib import contextmanager


@contextmanager
def _scope(self):
    old = self.ctx
    self.gen += 1
    with ExitStack() as s:
        self.ctx = s
        yield
        self.barrier()
    self.ctx = old


MK.scope = _scope


def _collective(self, kind, rg, in_ap, out_ap, reads=(), writes=()):
    Q = "pool"
    if "cc" not in self.sem:
        self.sem["cc"] = self.ctx.enter_context(self.nc.semaphore("s_cc"))
        self.cc_val = 0
    self._deps(Q, reads, writes)
    self.cc_val += 1
    inst = self.nc.gpsimd.collective_compute(kind, ALU.bypass, replica_groups=rg, ins=[in_ap.opt()], outs=[out_ap.opt()])
    inst.then_inc(self.sem["cc"], 1)
    ev = ("cc", self.cc_val)
    self._mark(ev, reads, writes)
    self.nops += 1
    return ev


MK.collective = _collective


import numpy as np
from contextlib import ExitStack

D = 2048
NIN = 4672
TOK = 2048


def build_k0():
    nc = bass.Bass("TRN2", target_bir_lowering=False)
    NCOL = 1536
    cT = nc.dram_tensor("cT", [128, 16, 2], F32, kind="ExternalInput").ap()
    w = nc.dram_tensor("w", [2, D, NCOL], F32, kind="ExternalInput").ap()
    bb = nc.dram_tensor("bb", [2, 2, NCOL], F32, kind="ExternalInput").ap()
    out = nc.dram_tensor("mod", [2, 2, NCOL], F32, kind="ExternalOutput").ap()
    with ExitStack() as ctx:
        mk = MK(nc, ctx)
        ct = mk.sb("ct", [128, 16, 2])
        sct = mk.sb("sct", [128, 16, 2])
        wt = [mk.sb("wt%d" % i, [128, 16, 512]) for i in range(2)]
        bt = mk.sb("bt", [2, 2, NCOL])
        ot = mk.sb("ot", [2, 2, NCOL])
        P = [mk.ps("P%d" % i, [2, 512]) for i in range(2)]
        b_c, b_b, b_o = Buf(), Buf(), Buf()
        b_w = [Buf(), Buf()]
        b_p = [Buf(), Buf()]
        mk.dma("sp", ct[:], cT, writes=[b_c])
        mk.dma("sp", bt[:], bb.rearrange("l b c -> b l c"), writes=[b_b])
        mk.op("act", lambda: nc.scalar.activation(out=sct[:], in_=ct[:], func=AF.Silu), reads=[b_c], writes=[b_c])
        it = 0
        for l in range(2):
            for n in range(3):
                i = it % 2
                it += 1
                mk.dma("sp", wt[i][:], w[l, :, n * 512:(n + 1) * 512].rearrange("(k p) c -> p k c", p=128), writes=[b_w[i]])
                for k in range(16):
                    mk.op("pe", lambda: nc.tensor.matmul(P[i][:], lhsT=sct[:, k, :], rhs=wt[i][:, k, :], start=(k == 0), stop=(k == 15)),
                          reads=[b_c, b_w[i]], writes=[b_p[i]], skip_same=True)
                mk.op("dve", lambda: nc.vector.tensor_tensor(out=ot[:, l, n * 512:(n + 1) * 512], in0=P[i][:], in1=bt[:, l, n * 512:(n + 1) * 512], op=ALU.add),
                      reads=[b_p[i], b_b], writes=[b_o])
        mk.dma("sp", out.rearrange("l b c -> b l c"), ot[:], reads=[b_o], is_output=True)
        mk.finish("sp")
    return nc


def build_k1():
    nc = bass.Bass("TRN2", target_bir_lowering=False)
    x = nc.dram_tensor("x", [TOK, D], F32, kind="ExternalInput").ap()
    sc = nc.dram_tensor("sc", [128, D], F32, kind="ExternalInput").ap()
    sh = nc.dram_tensor("sh", [128, D], F32, kind="ExternalInput").ap()
    w_in = nc.dram_tensor("w_in", [D, NIN], F32, kind="ExternalInput").ap()
    ident = nc.dram_tensor("ident", [128, 128], F32, kind="ExternalInput").ap()
    pT = nc.dram_tensor("pT", [NIN, TOK], F32, kind="ExternalOutput").ap()
    with ExitStack() as ctx:
        mk = MK(nc, ctx)
        emit_k1(nc, mk, x, sc, sh, w_in, ident, pT)
        mk.finish("sp")
        print("k1 ops", mk.nops, "waits", mk.nwaits)
    return nc


def ln_stats(nc, mk, xt, bx, st, mv, rs, nmr, bs, eps=1e-5):
    for c in range(4):
        mk.op("dve", lambda: nc.vector.bn_stats(out=st[:, c, :], in_=xt[:, c * 512:(c + 1) * 512]), reads=[bx], writes=[bs])
    mk.op("dve", lambda: nc.vector.bn_aggr(out=mv[:], in_=st[:].rearrange("p a b -> p (a b)")), reads=[bs], writes=[bs])
    mk.op("act", lambda: nc.scalar.activation(out=rs[:], in_=mv[:, 1:2], func=AF.Sqrt, bias=eps, scale=1.0), reads=[bs], writes=[bs])
    mk.op("dve", lambda: nc.vector.reciprocal(out=rs[:], in_=rs[:]), reads=[bs], writes=[bs])
    mk.op("dve", lambda: nc.vector.tensor_scalar(out=nmr[:], in0=mv[:, 0:1], scalar1=rs[:, 0:1], scalar2=-1.0, op0=ALU.mult, op1=ALU.mult),
          reads=[bs], writes=[bs])


def emit_k1(nc, mk, x, sc, sh, w_in, ident, pT):
    NT = TOK // 128
    xt = [mk.sb("xt%d" % i, [128, D]) for i in range(2)]
    xn = mk.sb("xn", [128, D])
    h1 = mk.sb("h1", [128, D])
    hb = [mk.sb("hb%d" % i, [128, D], BF16) for i in range(2)]
    hT = mk.sb("hT", [128, 16, TOK], BF16)
    sct = mk.sb("sct", [128, D])
    sht = mk.sb("sht", [128, D])
    idf = mk.sb("idf", [128, 128])
    idb = mk.sb("idb", [128, 128], BF16)
    st = mk.sb("st", [128, 4, 6])
    mv = mk.sb("mv", [128, 2])
    rs = mk.sb("rs", [128, 1])
    nmr = mk.sb("nmr", [128, 1])
    wt = [mk.sb("wt%d" % i, [128, 16, 128], BF16) for i in range(2)]
    ot = [mk.sb("ot%d" % i, [128, TOK]) for i in range(2)]
    PT = [mk.ps("PT%d" % i, [128, 8, 128], BF16) for i in range(2)]
    PM = [mk.ps("PM%d" % i, [128, 512]) for i in range(4)]
    b_x = [Buf(), Buf()]
    b_xn, b_h1, b_s, b_sc, b_sh, b_id, b_hT = Buf(), Buf(), Buf(), Buf(), Buf(), Buf(), Buf()
    b_hb = [Buf(), Buf()]
    b_pt = [Buf(), Buf()]
    b_pm = [Buf() for _ in range(4)]
    b_w = [Buf(), Buf()]
    b_o = [Buf(), Buf()]

    mk.dma("sp", sct[:], sc, writes=[b_sc])
    mk.dma("sp", sht[:], sh, writes=[b_sh])
    mk.dma("sp", idf[:], ident, writes=[b_id])
    mk.op("dve", lambda: nc.vector.tensor_copy(out=idb[:], in_=idf[:]), reads=[b_id], writes=[b_id])
    mk.op("pool", lambda: nc.gpsimd.tensor_scalar(out=sct[:], in0=sct[:], scalar1=1.0, scalar2=None, op0=ALU.add), reads=[b_sc], writes=[b_sc])

    NCB = (NIN + 127) // 128

    def load_w(cb):
        j = cb % 2
        c0 = cb * 128
        cw = min(128, NIN - c0)
        mk.dma("pool", wt[j][:, :, 0:cw], w_in[:, c0:c0 + cw].rearrange("(k p) c -> p k c", p=128), writes=[b_w[j]])

    load_w(0)
    load_w(1)
    for t in range(NT):
        i = t % 2
        mk.dma("sp", xt[i][:], x[t * 128:(t + 1) * 128, :], writes=[b_x[i]])
        ln_stats(nc, mk, xt[i], b_x[i], st, mv, rs, nmr, b_s)
        mk.op("act", lambda: nc.scalar.activation(out=xn[:], in_=xt[i][:], func=AF.Identity, bias=nmr[:, 0:1], scale=rs[:, 0:1]),
              reads=[b_x[i], b_s], writes=[b_xn])
        mk.op("dve", lambda: nc.vector.tensor_tensor(out=h1[:], in0=xn[:], in1=sct[:], op=ALU.mult), reads=[b_xn, b_sc], writes=[b_h1])
        mk.op("pool", lambda: nc.gpsimd.tensor_tensor(out=hb[i][:], in0=h1[:], in1=sht[:], op=ALU.add), reads=[b_h1, b_sh], writes=[b_hb[i]])
        for half in range(2):
            for kk in range(8):
                k = half * 8 + kk
                mk.op("pe", lambda: nc.tensor.transpose(PT[half][:, kk, :], hb[i][:, k * 128:(k + 1) * 128], idb[:]),
                      reads=[b_hb[i], b_id], writes=[b_pt[half]], skip_same=True)
            eng = "act" if half == 0 else "dve"
            if eng == "act":
                mk.op("act", lambda: nc.scalar.copy(out=hT[:, half * 8:(half + 1) * 8, t * 128:(t + 1) * 128], in_=PT[half][:]),
                      reads=[b_pt[half]], writes=[b_hT])
            else:
                mk.op("dve", lambda: nc.vector.tensor_copy(out=hT[:, half * 8:(half + 1) * 8, t * 128:(t + 1) * 128], in_=PT[half][:]),
                      reads=[b_pt[half]], writes=[b_hT])
    pi = 0
    for cb in range(NCB):
        j = cb % 2
        c0 = cb * 128
        cw = min(128, NIN - c0)
        for tc in range(TOK // 512):
            q = pi % 4
            pi += 1
            for k in range(16):
                mk.op("pe", lambda: nc.tensor.matmul(PM[q][0:cw, :], lhsT=wt[j][:, k, 0:cw], rhs=hT[:, k, tc * 512:(tc + 1) * 512],
                                                     start=(k == 0), stop=(k == 15)),
                      reads=[b_w[j], b_hT], writes=[b_pm[q]], skip_same=True)
            if tc % 2 == 0:
                mk.op("act", lambda: nc.scalar.copy(out=ot[j][0:cw, tc * 512:(tc + 1) * 512], in_=PM[q][0:cw, :]), reads=[b_pm[q]], writes=[b_o[j]])
            else:
                mk.op("dve", lambda: nc.vector.tensor_copy(out=ot[j][0:cw, tc * 512:(tc + 1) * 512], in_=PM[q][0:cw, :]), reads=[b_pm[q]], writes=[b_o[j]])
        mk.dma("sp", pT[c0:c0 + cw, :], ot[j][0:cw, :], reads=[b_o[j]], is_output=True)
        if cb + 2 < NCB:
            load_w(cb + 2)


import numpy as np
from contextlib import ExitStack

C = 64
TB = 512
NCH = TB // C


def rwkv_consts():
    s = np.arange(64)[:, None]
    t = np.arange(64)[None, :]
    m_su = (s < t).astype(np.float32)
    m_ui = (s <= t).astype(np.float32)
    m1 = np.concatenate([m_su, m_ui], axis=1)
    mask1 = np.tile(m1, (1, 4))
    m_sl = (t < s).astype(np.float32)
    mask3 = np.tile(m_sl, (1, 4))
    seg = np.ones((128, TB), np.float32)
    seg[:, ::C] = 0.0
    return {"mask1": mask1, "mask3": mask3, "seg": seg, "ident": np.eye(128, dtype=np.float32)}


def emit_rwkv(nc, mk, T, rwin, par64, par128, w2, a2, g2, gnt, cst, yT, odt=F32):
    import os
    LVL = int(os.environ.get("RW_LVL", "9"))
    NB = T // TB
    V = lambda fn, r=(), w=(): mk.op("dve", fn, r, w)
    A = lambda fn, r=(), w=(): mk.op("act", fn, r, w)
    G = lambda fn, r=(), w=(): mk.op("pool", fn, r, w)
    PE = lambda fn, r=(), w=(), ss=True: mk.op("pe", fn, r, w, skip_same=ss)
    import os
    F32R = mybir.dt.float32r
    USE_R = os.environ.get("RW_F32R", "1") == "1"
    RR = (lambda a: a.bitcast(F32R)) if USE_R else (lambda a: a)
    sb = mk.sb
    p64 = sb("rw_p64", [64, 2, 11]); p128 = sb("rw_p128", [128, 3])
    w2t = sb("rw_w2", [96, 128]); a2t = sb("rw_a2", [96, 128]); g2t = sb("rw_g2", [128, 128])
    gn = sb("rw_gn", [64, 2, 2, 64])
    mask1 = sb("rw_mask1", [64, 512]); mask3 = sb("rw_mask3", [64, 256]); seg = sb("rw_seg", [128, TB])
    ident = sb("rw_ident", [128, 128])
    ones64 = sb("rw_ones", [64, 64])
    bc = Buf("const")
    for dst, src in ((p64, par64), (p128, par128), (w2t, w2), (a2t, a2), (g2t, g2), (gn, gnt),
                     (mask1, cst["mask1"]), (mask3, cst["mask3"]), (seg, cst["seg"]), (ident, cst["ident"])):
        mk.dma("sp", dst[:], src, writes=[bc])
    V(lambda: nc.vector.memset(ones64[:], 1.0), w=[bc])
    raw = {}
    for nm in ("r0", "k0", "v0", "r1", "k1", "v1"):
        raw[nm] = sb("rw_raw_" + nm, [64, TB + 1])
    raw["w"] = sb("rw_raw_w", [96, TB + 1]); raw["a"] = sb("rw_raw_a", [96, TB + 1]); raw["g"] = sb("rw_raw_g", [128, TB + 1])
    b_raw = Buf("raw")
    tmp = sb("rw_tmp", [128, TB]); b_tmp = Buf("tmp")
    ws = sb("rw_ws", [96, TB]); as_ = sb("rw_as", [96, TB]); gs = sb("rw_gs", [128, TB]); b_lo = Buf("lo")
    gate = [sb("rw_gate%d" % i_, [128, TB]) for i_ in range(2)]; b_gate = [Buf("gate0"), Buf("gate1")]
    H = []
    for h in range(2):
        d = {}
        for nm in ("rs", "ks", "vs", "lw", "asg", "kkn", "kp", "bv", "cum", "e1", "e2", "BT", "KT", "BH", "KH", "rkr"):
            d[nm] = sb("rw_%s%d" % (nm, h), [64, TB])
        d["AR"] = sb("rw_AR%d" % h, [64, NCH, 128])
        d["cC"] = sb("rw_cC%d" % h, [64, NCH]); d["gC"] = sb("rw_gC%d" % h, [64, NCH])
        d["b"] = Buf("H%d" % h)
        d["bo"] = [Buf("Ho%d_0" % h), Buf("Ho%d_1" % h)]
        d["Vt"] = sb("rw_Vt%d" % h, [64, NCH, 64]); d["BHt"] = sb("rw_BHt%d" % h, [64, NCH, 64]); d["KHt"] = sb("rw_KHt%d" % h, [64, NCH, 64])
        d["bt"] = [Buf("Ht%d_0" % h), Buf("Ht%d_1" % h)]
        for nm_, shp_ in (("AR", [64, NCH, 128]), ("BT", [64, TB]), ("KT", [64, TB]), ("gC", [64, NCH]), ("Vt", [64, NCH, 64]), ("BHt", [64, NCH, 64]), ("KHt", [64, NCH, 64])):
            d[nm_] = [d[nm_], sb("rw_%s%d_b" % (nm_, h), shp_)]
        d["NG"] = sb("rw_NG%d" % h, [64, NCH, 128]); d["LG"] = sb("rw_LG%d" % h, [64, NCH, 128])
        d["L"] = sb("rw_L%d" % h, [64, NCH, 64]); d["bA"] = Buf("A%d" % h)
        d["P"] = [sb("rw_P%d_%d" % (h, i), [64, NCH, 64]) for i in range(2)]
        d["PT"] = [sb("rw_PT%d_%d" % (h, i), [64, NCH, 64]) for i in range(2)]
        d["ST"] = [sb("rw_ST%d_%d" % (h, i), [64, NCH, 64]) for i in range(2)]
        d["bD"] = Buf("D%d" % h)
        d["M"] = sb("rw_M%d" % h, [64, 64]); d["bM"] = Buf("M%d" % h)
        d["X1"] = sb("rw_X1%d" % h, [64, 64]); d["U"] = sb("rw_U%d" % h, [64, 64]); d["bX"] = Buf("X%d" % h); d["bU"] = Buf("U%d" % h)
        H.append(d)
    Yb = sb("rw_Yb", [64, NCH, 2, 64]); b_Y = Buf("Y")
    Ysq = sb("rw_Ysq", [64, NCH, 2, 64])
    st1 = sb("rw_st1", [64, NCH * 2]); st2 = sb("rw_st2", [64, NCH * 2]); st3 = sb("rw_st3", [64, NCH * 2]); b_st = Buf("st")
    sbon = [sb("rw_sbon%d" % i_, [64, NCH, 2]) for i_ in range(2)]; b_sb = [Buf("sbon0"), Buf("sbon1")]
    yo = sb("rw_yo", [128, TB], odt); b_yo = Buf("yo")
    ps_lo = mk.ps("rw_ps_lo", [128, 512]); b_pl = Buf()
    ps_tr = mk.ps("rw_ps_tr", [128, 512]); b_ptr = Buf()
    ps_a1 = mk.ps("rw_ps_a1", [64, 512]); b_pa1 = Buf()
    ps_a2 = mk.ps("rw_ps_a2", [64, 512]); b_pa2 = Buf()
    ps_a3f = mk.ps("rw_ps_a3", [128, 512]); ps_a3 = ps_a3f[0:64, :]; b_pa3 = Buf()
    ps_d = mk.ps("rw_ps_d", [64, 512]); b_pd = Buf()
    ps_d2 = ps_a3; b_pd2 = b_pa3
    ps_sh = [mk.ps("rw_ps_s%d" % h, [64, 512]) for h in range(2)]
    b_psh = [Buf(), Buf()]

    for h in range(2):
        V(lambda: nc.vector.tensor_scalar(out=RR(H[h]["M"][:]), in0=ident[0:64, 0:64], scalar1=0.0, scalar2=None, op0=ALU.mult), r=[bc], w=[H[h]["bM"]])

    rows = {"r0": 0, "r1": 64, "k0": 128, "k1": 192, "v0": 256, "v1": 320, "w": 384, "a": 480, "g": 576}
    nrow = {"r0": 64, "r1": 64, "k0": 64, "k1": 64, "v0": 64, "v1": 64, "w": 96, "a": 96, "g": 128}

    def stage1(blk):
        t0 = blk * TB
        par = blk % 2
        for nm in rows:
            r0, n = rows[nm], nrow[nm]
            if blk == 0:
                V(lambda: nc.vector.memset(raw[nm][0:n, 0:1], 0.0), w=[b_raw])
                yield
                mk.dma("sp", raw[nm][0:n, 1:TB + 1], rwin[r0:r0 + n, 0:TB], writes=[b_raw])
                yield
            else:
                mk.dma("sp", raw[nm][0:n, :], rwin[r0:r0 + n, t0 - 1:t0 + TB], writes=[b_raw])
                yield

        def shift(dst, src, n, mu_ap, bdst):
            V(lambda: nc.vector.tensor_tensor(out=tmp[0:n, :], in0=src[0:n, 0:TB], in1=src[0:n, 1:TB + 1], op=ALU.subtract), r=[b_raw], w=[b_tmp])
            V(lambda: nc.vector.scalar_tensor_tensor(out=dst[0:n, :], in0=tmp[0:n, :], scalar=mu_ap, in1=src[0:n, 1:TB + 1], op0=ALU.mult, op1=ALU.add),
              r=[b_tmp, b_raw, bc], w=[bdst])

        shift(ws, raw["w"], 96, p128[0:96, 0:1], b_lo)
        yield
        shift(as_, raw["a"], 96, p128[0:96, 1:2], b_lo)
        yield
        shift(gs, raw["g"], 128, p128[:, 2:3], b_lo)
        yield
        A(lambda: nc.scalar.activation(out=ws[:], in_=ws[:], func=AF.Tanh), r=[b_lo], w=[b_lo])
        yield
        A(lambda: nc.scalar.activation(out=gs[:], in_=gs[:], func=AF.Sigmoid), r=[b_lo], w=[b_lo])
        yield
        PE(lambda: nc.tensor.matmul(ps_lo[:, :], lhsT=g2t[:, :], rhs=gs[:, :], start=True, stop=True), r=[bc, b_lo], w=[b_pl])
        yield
        A(lambda: nc.scalar.copy(out=gate[par][:], in_=ps_lo[:, :]), r=[b_pl], w=[b_gate[par]])
        yield
        for h in range(2):
            d = H[h]; b = d["b"]; bo = d["bo"][par]
            hs = slice(64 * h, 64 * h + 64)
            shift(d["rs"], raw["r%d" % h], 64, p64[:, h, 0:1], b)
            yield
            shift(d["ks"], raw["k%d" % h], 64, p64[:, h, 1:2], b)
            yield
            shift(d["vs"], raw["v%d" % h], 64, p64[:, h, 2:3], b)
            yield
            PE(lambda: nc.tensor.matmul(ps_lo[0:64, :], lhsT=w2t[:, hs], rhs=ws[:, :], start=True, stop=True), r=[bc, b_lo], w=[b_pl])
            yield
            A(lambda: nc.scalar.activation(out=d["lw"][:], in_=ps_lo[0:64, :], func=AF.Sigmoid, bias=p64[:, h, 3:4], scale=1.0), r=[b_pl, bc], w=[b])
            yield
            V(lambda: nc.vector.tensor_scalar(out=d["lw"][:], in0=d["lw"][:], scalar1=-0.6065306597126334, scalar2=None, op0=ALU.mult), r=[b], w=[b])
            yield
            PE(lambda: nc.tensor.matmul(ps_lo[0:64, :], lhsT=a2t[:, hs], rhs=as_[:, :], start=True, stop=True), r=[bc, b_lo], w=[b_pl])
            yield
            A(lambda: nc.scalar.activation(out=d["asg"][:], in_=ps_lo[0:64, :], func=AF.Sigmoid, bias=p64[:, h, 4:5], scale=1.0), r=[b_pl, bc], w=[b])
            yield
            V(lambda: nc.vector.tensor_scalar(out=d["kkn"][:], in0=d["ks"][:], scalar1=p64[:, h, 5:6], scalar2=None, op0=ALU.mult), r=[b, bc], w=[b])
            yield
            A(lambda: nc.scalar.activation(out=tmp[0:64, :], in_=d["kkn"][:], func=AF.Square), r=[b], w=[b_tmp])
            yield
            PE(lambda: nc.tensor.matmul(ps_lo[0:64, :], lhsT=ones64[:, :], rhs=tmp[0:64, :], start=True, stop=True), r=[bc, b_tmp], w=[b_pl])
            yield
            A(lambda: nc.scalar.activation(out=tmp[0:64, :], in_=ps_lo[0:64, :], func=AF.Sqrt), r=[b_pl], w=[b_tmp])
            yield
            V(lambda: nc.vector.tensor_scalar(out=tmp[0:64, :], in0=tmp[0:64, :], scalar1=1e-12, scalar2=None, op0=ALU.max), r=[b_tmp], w=[b_tmp])
            yield
            V(lambda: nc.vector.reciprocal(out=tmp[0:64, :], in_=tmp[0:64, :]), r=[b_tmp], w=[b_tmp])
            yield
            V(lambda: nc.vector.tensor_tensor(out=d["kkn"][:], in0=d["kkn"][:], in1=tmp[0:64, :], op=ALU.mult), r=[b, b_tmp], w=[b])
            yield
            V(lambda: nc.vector.tensor_scalar(out=tmp[0:64, :], in0=d["asg"][:], scalar1=-1.0, scalar2=p64[:, h, 6:7], op0=ALU.add, op1=ALU.mult), r=[b, bc], w=[b_tmp])
            yield
            V(lambda: nc.vector.scalar_tensor_tensor(out=d["kp"][:], in0=tmp[0:64, :], scalar=1.0, in1=d["ks"][:], op0=ALU.add, op1=ALU.mult), r=[b_tmp, b], w=[b])
            yield
            V(lambda: nc.vector.tensor_tensor(out=d["bv"][:], in0=d["kkn"][:], in1=d["asg"][:], op=ALU.mult), r=[b], w=[b])
            yield
            V(lambda: nc.vector.scalar_tensor_tensor(out=d["rkr"][:], in0=d["rs"][:], scalar=p64[:, h, 7:8], in1=d["kp"][:], op0=ALU.mult, op1=ALU.mult), r=[b, bc], w=[b])
            yield
            V(lambda: nc.vector.tensor_tensor_scan(out=d["cum"][:], data0=seg[0:64, :], data1=d["lw"][:], initial=0.0, op0=ALU.mult, op1=ALU.add), r=[b, bc], w=[b])
            yield
            cum3 = d["cum"][:].rearrange("p (c t) -> p c t", t=C)
            V(lambda: nc.vector.tensor_copy(out=d["cC"][:], in_=cum3[:, :, C - 1]), r=[b], w=[b])
            yield
            A(lambda: nc.scalar.activation(out=d["gC"][par][:], in_=d["cC"][:], func=AF.Exp), r=[b], w=[bo])
            yield
            A(lambda: nc.scalar.activation(out=d["e1"][:], in_=d["cum"][:], func=AF.Exp), r=[b], w=[b])
            yield
            A(lambda: nc.scalar.activation(out=d["e2"][:], in_=d["cum"][:], func=AF.Exp, scale=-1.0), r=[b], w=[b])
            yield
            AR = d["AR"][par]
            V(lambda: nc.vector.tensor_tensor(out=RR(AR[:, :, 64:128]), in0=d["rs"][:].rearrange("p (c t) -> p c t", t=C),
                                              in1=d["e1"][:].rearrange("p (c t) -> p c t", t=C), op=ALU.mult), r=[b], w=[bo])
            yield
            V(lambda: nc.vector.tensor_tensor(out=RR(d["BT"][par][:]), in0=d["bv"][:], in1=d["e2"][:], op=ALU.mult), r=[b], w=[bo])
            yield
            V(lambda: nc.vector.tensor_tensor(out=RR(d["KT"][par][:]), in0=d["kp"][:], in1=d["e2"][:], op=ALU.mult), r=[b], w=[bo])
            yield
            V(lambda: nc.vector.tensor_tensor(out=tmp[0:64, :], in0=d["cum"][:], in1=d["lw"][:], op=ALU.subtract), r=[b], w=[b_tmp])
            yield
            A(lambda: nc.scalar.activation(out=tmp[0:64, :], in_=tmp[0:64, :], func=AF.Exp), r=[b_tmp], w=[b_tmp])
            yield
            V(lambda: nc.vector.scalar_tensor_tensor(out=RR(AR[:, :, 0:64]), in0=d["kkn"][:].rearrange("p (c t) -> p c t", t=C), scalar=-1.0,
                                                     in1=tmp[0:64, :].rearrange("p (c t) -> p c t", t=C), op0=ALU.mult, op1=ALU.mult), r=[b, b_tmp], w=[bo])
            yield
            V(lambda: nc.vector.tensor_tensor(out=tmp[0:64, :].rearrange("p (c t) -> p c t", t=C), in0=d["cC"][:].unsqueeze(2).to_broadcast([64, NCH, C]),
                                              in1=cum3, op=ALU.subtract), r=[b], w=[b_tmp])
            yield
            A(lambda: nc.scalar.activation(out=tmp[0:64, :], in_=tmp[0:64, :], func=AF.Exp), r=[b_tmp], w=[b_tmp])
            yield
            V(lambda: nc.vector.tensor_tensor(out=d["BH"][:], in0=d["bv"][:], in1=tmp[0:64, :], op=ALU.mult), r=[b, b_tmp], w=[b])
            yield
            V(lambda: nc.vector.tensor_tensor(out=d["KH"][:], in0=d["kp"][:], in1=tmp[0:64, :], op=ALU.mult), r=[b, b_tmp], w=[b])
            yield
            for src, dstn in (("vs", "Vt"), ("BH", "BHt"), ("KH", "KHt")):
                for c in range(NCH):
                    PE(lambda: nc.tensor.transpose(ps_tr[0:64, c * 64:(c + 1) * 64], d[src][:, c * C:(c + 1) * C], ident[0:64, 0:64]), r=[b, bc], w=[b_ptr])
                    yield
                A(lambda: nc.scalar.copy(out=RR(d[dstn][par][:].rearrange("p c k -> p (c k)")), in_=ps_tr[0:64, :]), r=[b_ptr], w=[d["bt"][par]])
                yield
            for c in range(NCH):
                PE(lambda: nc.tensor.matmul(ps_tr[0:64, 2 * c:2 * c + 2], lhsT=d["rkr"][:, c * C:(c + 1) * C], rhs=ones64[:, 0:2], start=True, stop=True), r=[b, bc], w=[b_ptr])
                yield
            V(lambda: nc.vector.tensor_copy(out=sbon[par][:, :, h], in_=ps_tr[0:64, 0:2 * NCH:2]), r=[b_ptr], w=[b_sb[par]])
            yield

    def rest(blk, tick):
        t0 = blk * TB
        par = blk % 2
        for h in range(2):
            d = H[h]; b = d["bo"][par]; AR = d["AR"][par]
            tick()
            for half in range(2):
                for cc in range(4):
                    c = half * 4 + cc
                    PE(lambda: nc.tensor.matmul(ps_a1[:, cc * 128:(cc + 1) * 128], lhsT=RR(d["BT"][par][:, c * C:(c + 1) * C]), rhs=RR(AR[:, c, :]), start=True, stop=True), r=[b], w=[b_pa1])
                    PE(lambda: nc.tensor.matmul(ps_a2[:, cc * 128:(cc + 1) * 128], lhsT=RR(d["KT"][par][:, c * C:(c + 1) * C]), rhs=RR(AR[:, c, :]), start=True, stop=True), r=[b], w=[b_pa2])
                    PE(lambda: nc.tensor.matmul(ps_a3[:, cc * 64:(cc + 1) * 64], lhsT=RR(AR[:, c, 0:64]), rhs=RR(d["BT"][par][:, c * C:(c + 1) * C]), start=True, stop=True), r=[b], w=[b_pa3])
                V(lambda: nc.vector.tensor_tensor(out=RR(d["NG"][:, half * 4:half * 4 + 4, :].rearrange("p c k -> p (c k)")), in0=ps_a1[:, :], in1=mask1[:, :], op=ALU.mult), r=[b_pa1, bc], w=[d["bA"]])
                V(lambda: nc.vector.tensor_tensor(out=RR(d["LG"][:, half * 4:half * 4 + 4, :].rearrange("p c k -> p (c k)")), in0=ps_a2[:, :], in1=mask1[:, :], op=ALU.mult), r=[b_pa2, bc], w=[d["bA"]])
                V(lambda: nc.vector.tensor_tensor(out=d["L"][:, half * 4:half * 4 + 4, :].rearrange("p c k -> p (c k)"), in0=ps_a3[:, 0:256], in1=mask3[:, :], op=ALU.mult), r=[b_pa3, bc], w=[d["bA"]])
        DPS = [(ps_d, b_pd, ps_d2, b_pd2), (ps_a1, b_pa1, ps_a2, b_pa2)]
        for h in range(2):
            d = H[h]; P, PT, ST = d["P"], d["PT"], d["ST"]; bD = d["bD"]
            V(lambda: nc.vector.tensor_copy(out=RR(P[0][:]), in_=d["L"][:]), r=[d["bA"]], w=[bD])
            V(lambda: nc.vector.tensor_copy(out=RR(PT[0][:]), in_=d["NG"][:, :, 0:64]), r=[d["bA"]], w=[bD])
            V(lambda: nc.vector.tensor_tensor(out=RR(ST[0][:]), in0=d["NG"][:, :, 0:64], in1=ident[0:64, 0:64].unsqueeze(1).to_broadcast([64, NCH, 64]), op=ALU.add), r=[d["bA"], bc], w=[bD])
        cur = 0
        for lev in range(5):
            nxt = 1 - cur
            tick()
            for h in range(2):
                d = H[h]; P, PT, ST = d["P"], d["PT"], d["ST"]; bD = d["bD"]
                pd, bpd, pd2, bpd2 = DPS[h]
                for c in range(NCH):
                    PE(lambda: nc.tensor.matmul(pd[:, c * 64:(c + 1) * 64], lhsT=RR(PT[cur][:, c, :]), rhs=RR(P[cur][:, c, :]), start=True, stop=True), r=[bD], w=[bpd])
                for c in range(NCH):
                    PE(lambda: nc.tensor.matmul(pd2[:, c * 64:(c + 1) * 64], lhsT=RR(P[cur][:, c, :]), rhs=RR(PT[cur][:, c, :]), start=True, stop=True), r=[bD], w=[bpd2])
            tick()
            for h in range(2):
                d = H[h]; P, PT, ST = d["P"], d["PT"], d["ST"]; bD = d["bD"]
                pd, bpd, pd2, bpd2 = DPS[h]
                V(lambda: nc.vector.tensor_copy(out=RR(P[nxt][:].rearrange("p c k -> p (c k)")), in_=pd[:, :]), r=[], w=[bpd, bD])
                A(lambda: nc.scalar.copy(out=RR(PT[nxt][:].rearrange("p c k -> p (c k)")), in_=pd2[:, :]), r=[], w=[bpd2, bD])
            tick()
            for h in range(2):
                d = H[h]; P, PT, ST = d["P"], d["PT"], d["ST"]; bD = d["bD"]
                pd, bpd, pd2, bpd2 = DPS[h]
                for c in range(NCH):
                    PE(lambda: nc.tensor.matmul(pd[:, c * 64:(c + 1) * 64], lhsT=RR(P[nxt][:, c, :]), rhs=RR(ST[cur][:, c, :]), start=True, stop=True), r=[bD], w=[bpd])
            tick()
            for h in range(2):
                d = H[h]; P, PT, ST = d["P"], d["PT"], d["ST"]; bD = d["bD"]
                pd, bpd, pd2, bpd2 = DPS[h]
                V(lambda: nc.vector.tensor_tensor(out=RR(ST[nxt][:].rearrange("p c k -> p (c k)")), in0=pd[:, :], in1=ST[cur][:].rearrange("p c k -> p (c k)"), op=ALU.add), r=[bD], w=[bpd, bD])
            tick()
            cur = nxt
        for h in range(2):
            H[h]["STf"] = H[h]["ST"][cur]
        for c in range(NCH):
            pp = lambda h, i: ps_sh[h][:, i * 64:(i + 1) * 64]
            tick()
            for h in range(2):
                d = H[h]
                PE(lambda: nc.tensor.matmul(pp(h, 0), lhsT=RR(d["LG"][:, c, 0:64]), rhs=RR(d["Vt"][par][:, c, :]), start=True, stop=False), r=[d["bA"], d["bt"][par]], w=[b_psh[h]])
                PE(lambda: nc.tensor.matmul(pp(h, 0), lhsT=RR(d["AR"][par][:, c, 0:64]), rhs=RR(d["M"][:, :]), start=False, stop=True), r=[d["bo"][par], d["bM"]], w=[b_psh[h]])
            tick()
            for h in range(2):
                d = H[h]
                if h == 0:
                    A(lambda: nc.scalar.copy(out=RR(d["X1"][:]), in_=pp(h, 0)), r=[], w=[b_psh[h], d["bX"]])
                else:
                    V(lambda: nc.vector.tensor_copy(out=RR(d["X1"][:]), in_=pp(h, 0)), r=[], w=[b_psh[h], d["bX"]])
            tick()
            for h in range(2):
                d = H[h]
                PE(lambda: nc.tensor.matmul(pp(h, 1), lhsT=RR(d["STf"][:, c, :]), rhs=RR(d["X1"][:, :]), start=True, stop=True), r=[d["bD"], d["bX"]], w=[b_psh[h]])
            tick()
            for h in range(2):
                d = H[h]
                if h == 0:
                    V(lambda: nc.vector.tensor_copy(out=RR(d["U"][:]), in_=pp(h, 1)), r=[], w=[b_psh[h], d["bU"]])
                else:
                    A(lambda: nc.scalar.copy(out=RR(d["U"][:]), in_=pp(h, 1)), r=[], w=[b_psh[h], d["bU"]])
            tick()
            for h in range(2):
                d = H[h]
                PE(lambda: nc.tensor.matmul(pp(h, 2), lhsT=RR(d["AR"][par][:, c, 64:128]), rhs=RR(d["M"][:, :]), start=True, stop=False), r=[d["bo"][par], d["bM"]], w=[b_psh[h]])
                PE(lambda: nc.tensor.matmul(pp(h, 2), lhsT=RR(d["LG"][:, c, 64:128]), rhs=RR(d["Vt"][par][:, c, :]), start=False, stop=False), r=[d["bA"], d["bt"][par]], w=[b_psh[h]])
                PE(lambda: nc.tensor.matmul(pp(h, 2), lhsT=RR(d["NG"][:, c, 64:128]), rhs=RR(d["U"][:, :]), start=False, stop=True), r=[d["bA"], d["bU"]], w=[b_psh[h]])
                PE(lambda: nc.tensor.matmul(pp(h, 3), lhsT=RR(d["KHt"][par][:, c, :]), rhs=RR(d["Vt"][par][:, c, :]), start=True, stop=False), r=[d["bt"][par]], w=[b_psh[h]])
                PE(lambda: nc.tensor.matmul(pp(h, 3), lhsT=RR(d["BHt"][par][:, c, :]), rhs=RR(d["U"][:, :]), start=False, stop=True), r=[d["bt"][par], d["bU"]], w=[b_psh[h]])
            tick()
            for h in range(2):
                d = H[h]
                V(lambda: nc.vector.scalar_tensor_tensor(out=RR(d["M"][:]), in0=d["M"][:], scalar=d["gC"][par][:, c:c + 1], in1=pp(h, 3), op0=ALU.mult, op1=ALU.add), r=[d["bo"][par], d["bM"]], w=[b_psh[h], d["bM"]])
                A(lambda: nc.scalar.copy(out=Yb[:, c, h, :], in_=pp(h, 2)), r=[], w=[b_psh[h], b_Y])
        Y2 = Yb[:].rearrange("p c h v -> p (c h) v")
        V(lambda: nc.vector.tensor_reduce(out=st1[:], in_=Y2, axis=AX.X, op=ALU.add), r=[b_Y], w=[b_st])
        A(lambda: nc.scalar.activation(out=Ysq[:].rearrange("p c h v -> p (c h v)"), in_=Yb[:].rearrange("p c h v -> p (c h v)"), func=AF.Square), r=[b_Y], w=[b_tmp])
        V(lambda: nc.vector.tensor_reduce(out=st2[:], in_=Ysq[:].rearrange("p c h v -> p (c h) v"), axis=AX.X, op=ALU.add), r=[b_tmp], w=[b_st])
        V(lambda: nc.vector.tensor_scalar(out=st1[:], in0=st1[:], scalar1=1.0 / 64, scalar2=None, op0=ALU.mult), r=[b_st], w=[b_st])
        V(lambda: nc.vector.tensor_tensor(out=st3[:], in0=st1[:], in1=st1[:], op=ALU.mult), r=[b_st], w=[b_st])
        V(lambda: nc.vector.scalar_tensor_tensor(out=st2[:], in0=st2[:], scalar=1.0 / 64, in1=st3[:], op0=ALU.mult, op1=ALU.subtract), r=[b_st], w=[b_st])
        A(lambda: nc.scalar.activation(out=st2[:], in_=st2[:], func=AF.Sqrt, bias=64e-5, scale=1.0), r=[b_st], w=[b_st])
        V(lambda: nc.vector.reciprocal(out=st2[:], in_=st2[:]), r=[b_st], w=[b_st])
        V(lambda: nc.vector.tensor_tensor(out=Y2, in0=Y2, in1=st1[:].unsqueeze(2).to_broadcast([64, NCH * 2, 64]), op=ALU.subtract), r=[b_st, b_Y], w=[b_Y])
        V(lambda: nc.vector.tensor_tensor(out=Y2, in0=Y2, in1=st2[:].unsqueeze(2).to_broadcast([64, NCH * 2, 64]), op=ALU.mult), r=[b_st, b_Y], w=[b_Y])
        for h in range(2):
            V(lambda: nc.vector.tensor_tensor(out=Yb[:, :, h, :], in0=Yb[:, :, h, :], in1=gn[:, 0, h, :].unsqueeze(1).to_broadcast([64, NCH, 64]), op=ALU.mult), r=[b_Y, bc], w=[b_Y])
            V(lambda: nc.vector.tensor_tensor(out=Yb[:, :, h, :], in0=Yb[:, :, h, :], in1=gn[:, 1, h, :].unsqueeze(1).to_broadcast([64, NCH, 64]), op=ALU.add), r=[b_Y, bc], w=[b_Y])
            V(lambda: nc.vector.tensor_tensor(out=Ysq[:, :, h, :], in0=H[h]["Vt"][par][:], in1=sbon[par][:, :, h].unsqueeze(2).to_broadcast([64, NCH, 64]), op=ALU.mult), r=[H[h]["bt"][par], b_sb[par]], w=[b_tmp])
        V(lambda: nc.vector.tensor_tensor(out=Yb[:].rearrange("p c h v -> p (c h v)"), in0=Yb[:].rearrange("p c h v -> p (c h v)"), in1=Ysq[:].rearrange("p c h v -> p (c h v)"), op=ALU.add), r=[b_Y, b_tmp], w=[b_Y])
        for c in range(NCH):
            PE(lambda: nc.tensor.transpose(ps_a3f[:, c * 64:(c + 1) * 64], Yb[:, c, :, :].rearrange("p h v -> p (h v)"), ident[0:64, 0:64]), r=[b_Y, bc], w=[b_pa3])
        V(lambda: nc.vector.tensor_tensor(out=yo[:], in0=ps_a3f[:, :], in1=gate[par][:], op=ALU.mult), r=[b_gate[par]], w=[b_pa3, b_yo])
        mk.dma("sp", yT[:, t0:t0 + TB], yo[:], reads=[b_yo], is_output=True)

    for _ in stage1(0):
        pass
    for blk in range(NB):
        nx = stage1(blk + 1) if blk + 1 < NB else None

        def tick(k=2):
            if nx is not None:
                for _ in range(k):
                    if next(nx, "END") == "END":
                        break
        rest(blk, tick)
        if nx is not None:
            for _ in nx:
                pass


def build_rwkv(T):
    nc = bass.Bass("TRN2", target_bir_lowering=False)
    dt = lambda n, s, k="ExternalInput": nc.dram_tensor(n, s, F32, kind=k).ap()
    rwin = dt("rwin", [704, T]); par64 = dt("par64", [64, 2, 11]); par128 = dt("par128", [128, 3])
    w2 = dt("w2", [96, 128]); a2 = dt("a2", [96, 128]); g2 = dt("g2", [128, 128]); gnt = dt("gnt", [64, 2, 2, 64])
    cst = {"mask1": dt("mask1", [64, 512]), "mask3": dt("mask3", [64, 256]), "seg": dt("seg", [128, TB]), "ident": dt("ident", [128, 128])}
    yT = dt("yT", [128, T], "ExternalOutput")
    with ExitStack() as ctx:
        mk = MK(nc, ctx)
        emit_rwkv(nc, mk, T, rwin, par64, par128, w2, a2, g2, gnt, cst, yT)
        mk.finish("sp")
        print("rwkv ops", mk.nops, "waits", mk.nwaits)
    return nc


def rwkv_host_inputs(prm, l, q):
    G = 512
    cs = slice(128 * q, 128 * q + 128)
    mu = prm["rwkv_mu"][l]
    par64 = np.zeros((64, 2, 11), np.float32)
    for h in range(2):
        c0 = 128 * q + 64 * h
        par64[:, h, 0] = mu[0 * G + c0:0 * G + c0 + 64]
        par64[:, h, 1] = mu[1 * G + c0:1 * G + c0 + 64]
        par64[:, h, 2] = mu[2 * G + c0:2 * G + c0 + 64]
        par64[:, h, 3] = prm["rwkv_w0"][l][c0:c0 + 64]
        par64[:, h, 4] = prm["rwkv_a0"][l][c0:c0 + 64]
        par64[:, h, 5] = prm["rwkv_kk"][l][c0:c0 + 64]
        par64[:, h, 6] = prm["rwkv_ka"][l][c0:c0 + 64]
        par64[:, h, 7] = prm["rwkv_rk"][l][2 * q + h]
    par128 = np.zeros((128, 3), np.float32)
    par128[0:96, 0] = mu[3 * G:3 * G + 96]
    par128[0:96, 1] = mu[3 * G + 96:3 * G + 192]
    par128[:, 2] = mu[3 * G + 192:3 * G + 320]
    gnt = np.zeros((64, 2, 2, 64), np.float32)
    for h in range(2):
        c0 = 128 * q + 64 * h
        gnt[:, 0, h, :] = prm["rwkv_gn_g"][l][c0:c0 + 64][None]
        gnt[:, 1, h, :] = prm["rwkv_gn_b"][l][c0:c0 + 64][None]
    d = {"par64": par64, "par128": par128, "gnt": gnt,
         "w2": np.ascontiguousarray(prm["rwkv_w2"][l][:, cs]), "a2": np.ascontiguousarray(prm["rwkv_a2"][l][:, cs]),
         "g2": np.ascontiguousarray(prm["rwkv_g2"][l][:, cs])}
    d.update(rwkv_consts())
    return d


def rwkv_rows(q):
    G = 512
    idx = []
    for base in (0, G, 2 * G):
        idx += list(range(base + 128 * q, base + 128 * q + 64))
        idx += list(range(base + 128 * q + 64, base + 128 * q + 128))
    idx += list(range(3 * G, 3 * G + 320))
    return np.array(idx)


import math
import numpy as np
from contextlib import ExitStack


def emit_conv(nc, mk, T, cvin, cw, yT, TB=2048, odt=F32):
    V = lambda fn, r=(), w=(): mk.op("dve", fn, r, w)
    G = lambda fn, r=(), w=(): mk.op("pool", fn, r, w)
    cwt = mk.sb("cv_w", [128, 3]); bc = Buf()
    mk.dma("sp", cwt[:], cw, writes=[bc])
    Bt = mk.sb("cv_B", [128, TB]); Ct = mk.sb("cv_C", [128, TB + 2]); Ht = mk.sb("cv_H", [128, TB + 2])
    z = mk.sb("cv_z", [128, TB + 2]); y = mk.sb("cv_y", [128, TB]); o = mk.sb("cv_o", [128, TB], odt)
    b_in, b_z, b_y, b_o = Buf(), Buf(), Buf(), Buf()
    for blk in range(T // TB):
        t0 = blk * TB
        mk.dma("sp", Bt[:], cvin[0:128, t0:t0 + TB], writes=[b_in])
        if blk == 0:
            V(lambda: nc.vector.memset(Ct[:, 0:2], 0.0), w=[b_in])
            V(lambda: nc.vector.memset(Ht[:, 0:2], 0.0), w=[b_in])
            mk.dma("sp", Ct[:, 2:], cvin[128:256, 0:TB], writes=[b_in])
            mk.dma("sp", Ht[:, 2:], cvin[256:384, 0:TB], writes=[b_in])
        else:
            mk.dma("sp", Ct[:], cvin[128:256, t0 - 2:t0 + TB], writes=[b_in])
            mk.dma("sp", Ht[:], cvin[256:384, t0 - 2:t0 + TB], writes=[b_in])
        G(lambda: nc.gpsimd.tensor_tensor(out=z[:], in0=Ct[:], in1=Ht[:], op=ALU.mult), r=[b_in], w=[b_z])
        V(lambda: nc.vector.tensor_scalar(out=y[:], in0=z[:, 2:TB + 2], scalar1=cwt[:, 2:3], scalar2=None, op0=ALU.mult), r=[b_z, bc], w=[b_y])
        V(lambda: nc.vector.scalar_tensor_tensor(out=y[:], in0=z[:, 1:TB + 1], scalar=cwt[:, 1:2], in1=y[:], op0=ALU.mult, op1=ALU.add), r=[b_z, bc], w=[b_y])
        V(lambda: nc.vector.scalar_tensor_tensor(out=y[:], in0=z[:, 0:TB], scalar=cwt[:, 0:1], in1=y[:], op0=ALU.mult, op1=ALU.add), r=[b_z, bc], w=[b_y])
        G(lambda: nc.gpsimd.tensor_tensor(out=o[:], in0=y[:], in1=Bt[:], op=ALU.mult), r=[b_y, b_in], w=[b_o])
        mk.dma("sp", yT[:, t0:t0 + TB], o[:], reads=[b_o], is_output=True)


def t5_bucket_np(rel):
    n = np.maximum(rel, 0)
    max_exact = 16
    n_f = np.maximum(n, 1).astype(np.float32)
    large = max_exact + (np.log(n_f / max_exact) / math.log(128 / max_exact) * (32 - max_exact)).astype(np.int32)
    return np.where(n < max_exact, n, np.minimum(large, 31))


def attn_tables(rel_bias, sinks_l, q):
    qi = np.arange(128)[:, None]
    kj = np.arange(256)[None, :]
    rel = qi + 128 - kj
    bucket = t5_bucket_np(rel)
    valid = (rel >= 0) & (rel < 128)
    tab = np.zeros((2, 128, 2, 256), np.float32)
    for h in range(2):
        bias = rel_bias[bucket, 2 * q + h]
        full = np.where(valid, bias, np.float32(-30000.0))
        tab[0, :, h, :] = full
        f0 = full.copy()
        f0[:, 0:128] = -30000.0
        tab[1, :, h, :] = f0
    sk = np.broadcast_to(sinks_l[2 * q:2 * q + 2][None, :], (128, 2)).astype(np.float32).copy()
    return tab, sk


def emit_attn(nc, mk, T, qkv, btab, sinkt_d, ident_d, yT, odt=F32):
    V = lambda fn, r=(), w=(): mk.op("dve", fn, r, w)
    A = lambda fn, r=(), w=(): mk.op("act", fn, r, w)
    PE = lambda fn, r=(), w=(): mk.op("pe", fn, r, w, skip_same=True)
    NBK = T // 128
    bt = mk.sb("at_bt", [128, 2, 2, 256]); sk = mk.sb("at_sk", [128, 2]); ident = mk.sb("at_id", [128, 128]); bc = Buf()
    mk.dma("sp", bt[:, 0, :, :], btab[0], writes=[bc])
    mk.dma("sp", bt[:, 1, :, :], btab[1], writes=[bc])
    mk.dma("sp", sk[:], sinkt_d, writes=[bc])
    mk.dma("sp", ident[:], ident_d, writes=[bc])
    CH = 1024
    qt = mk.sb("at_q", [128, CH]); kt = mk.sb("at_k", [128, 128 + CH]); vt = mk.sb("at_v", [64, CH])
    vtok = mk.sb("at_vtok", [128, CH // 128 + 1, 64])
    b_q, b_k, b_v, b_vt = Buf(), Buf(), Buf(), Buf()
    sc = [mk.sb("at_sc%d" % h, [128, 256]) for h in range(2)]; b_sc = [Buf(), Buf()]
    pr = [mk.sb("at_p%d" % h, [128, 256]) for h in range(2)]; b_p = [Buf(), Buf()]
    pT = [mk.sb("at_pT%d" % h, [128, 256]) for h in range(2)]; b_pT = [Buf(), Buf()]
    sm = [mk.sb("at_sm%d" % h, [128, 8]) for h in range(2)]; b_sm = [Buf(), Buf()]
    ot = mk.sb("at_o", [128, 128]); b_o = Buf()
    yo = mk.sb("at_yo", [128, CH], odt); b_yo = Buf()
    ps_s = [mk.ps("at_ps_s%d" % h, [128, 512]) for h in range(2)]; b_ps = [Buf(), Buf()]
    ps_t = [mk.ps("at_ps_t%d" % h, [128, 512]) for h in range(2)]; b_pt = [Buf(), Buf()]
    ps_o = mk.ps("at_ps_o", [128, 512]); b_po = Buf()
    ps_v = mk.ps("at_ps_v", [128, 512]); b_pv = Buf()
    for ch in range(T // CH):
        c0 = ch * CH
        mk.dma("sp", qt[:], qkv[0:128, c0:c0 + CH], writes=[b_q])
        if ch == 0:
            V(lambda: nc.vector.memset(kt[:, 0:128], 0.0), w=[b_k])
            V(lambda: nc.vector.memset(vtok[:, 0, :], 0.0), w=[b_vt])
            for hh in range(2):
                mk.dma("sp", kt[64 * hh:64 * hh + 64, 128:], qkv[128:192, 0:CH], writes=[b_k])
        else:
            for hh in range(2):
                mk.dma("sp", kt[64 * hh:64 * hh + 64, :], qkv[128:192, c0 - 128:c0 + CH], writes=[b_k])
            V(lambda: nc.vector.tensor_copy(out=vtok[:, 0, :], in_=vtok[:, CH // 128, :]), r=[b_vt], w=[b_vt])
        mk.dma("sp", vt[:], qkv[192:256, c0:c0 + CH], writes=[b_v])
        for j in range(CH // 128):
            PE(lambda: nc.tensor.transpose(ps_v[:, j * 64:(j + 1) * 64], vt[:, j * 128:(j + 1) * 128], ident[0:64, 0:64]), r=[b_v, bc], w=[b_pv])
        A(lambda: nc.scalar.copy(out=vtok[:, 1:, :].rearrange("p j d -> p (j d)"), in_=ps_v[:, 0:(CH // 128) * 64]), r=[], w=[b_pv, b_vt])
        for j in range(CH // 128):
            first = 1 if (ch == 0 and j == 0) else 0
            HS = [slice(0, 64), slice(64, 128)]
            for h in range(2):
                PE(lambda: nc.tensor.matmul(ps_s[h][:, 0:256], lhsT=qt[HS[h], j * 128:(j + 1) * 128], rhs=kt[HS[h], j * 128:j * 128 + 256], start=True, stop=True),
                   r=[b_q, b_k], w=[b_ps[h]])
            for h in range(2):
                V(lambda: nc.vector.scalar_tensor_tensor(out=sc[h][:], in0=ps_s[h][:, 0:256], scalar=0.125, in1=bt[:, first, h, :], op0=ALU.mult, op1=ALU.add),
                  r=[bc], w=[b_ps[h], b_sc[h]])
                s = sm[h]
                V(lambda: nc.vector.reduce_max(out=s[:, 0:1], in_=sc[h][:], axis=AX.X), r=[b_sc[h]], w=[b_sm[h]])
                V(lambda: nc.vector.tensor_tensor(out=s[:, 0:1], in0=s[:, 0:1], in1=sk[:, h:h + 1], op=ALU.max), r=[bc], w=[b_sm[h]])
                V(lambda: nc.vector.tensor_scalar(out=s[:, 1:2], in0=s[:, 0:1], scalar1=-1.0, scalar2=None, op0=ALU.mult), r=[], w=[b_sm[h]])
            for h in range(2):
                s = sm[h]
                A(lambda: nc.scalar.activation(out=pr[h][:], in_=sc[h][:], func=AF.Exp, bias=s[:, 1:2], scale=1.0, accum_out=s[:, 2:3]), r=[b_sc[h]], w=[b_sm[h], b_p[h]])
                A(lambda: nc.scalar.activation(out=s[:, 3:4], in_=sk[:, h:h + 1], func=AF.Exp, bias=s[:, 1:2], scale=1.0), r=[bc], w=[b_sm[h]])
            for h in range(2):
                for kb in range(2):
                    PE(lambda: nc.tensor.transpose(ps_t[h][:, kb * 128:(kb + 1) * 128], pr[h][:, kb * 128:(kb + 1) * 128], ident[:, :]), r=[b_p[h], bc], w=[b_pt[h]])
            for h in range(2):
                s = sm[h]
                V(lambda: nc.vector.tensor_tensor(out=s[:, 4:5], in0=s[:, 2:3], in1=s[:, 3:4], op=ALU.add), r=[], w=[b_sm[h]])
                V(lambda: nc.vector.reciprocal(out=s[:, 5:6], in_=s[:, 4:5]), r=[], w=[b_sm[h]])
                V(lambda: nc.vector.tensor_copy(out=pT[h][:], in_=ps_t[h][:, 0:256]), r=[], w=[b_pt[h], b_pT[h]])
            for h in range(2):
                for kb in range(2):
                    PE(lambda: nc.tensor.matmul(ps_o[:, h * 64:(h + 1) * 64], lhsT=pT[h][:, kb * 128:(kb + 1) * 128], rhs=vtok[:, j + kb, :], start=(kb == 0), stop=(kb == 1)),
                       r=[b_pT[h], b_vt], w=[b_po])
            for h in range(2):
                s = sm[h]
                A(lambda: nc.scalar.activation(out=ot[:, h * 64:(h + 1) * 64], in_=ps_o[:, h * 64:(h + 1) * 64], func=AF.Copy, scale=s[:, 5:6]), r=[b_sm[h]], w=[b_po, b_o])
            PE(lambda: nc.tensor.transpose(ps_o[:, 128:256], ot[:, :], ident[:, :]), r=[b_o, bc], w=[b_po])
            V(lambda: nc.vector.tensor_copy(out=yo[:, j * 128:(j + 1) * 128], in_=ps_o[:, 128:256]), r=[], w=[b_po, b_yo])
        mk.dma("sp", yT[:, c0:c0 + CH], yo[:], reads=[b_yo], is_output=True)


CS = 512


def s5_host_inputs(prm, l, q):
    g0 = 8 * q
    par = np.zeros((128, 4, 3), np.float32)
    bb = np.zeros((128, 4, 2, 16), np.float32)
    cc = np.zeros((128, 4, 2, 16), np.float32)
    for j in range(4):
        for gl in range(2):
            g = g0 + 2 * j + gl
            ps = slice(64 * gl, 64 * gl + 64)
            par[ps, j, 0] = prm["s5_lambda_re"][l][g]
            par[ps, j, 1] = prm["s5_lambda_im"][l][g]
            par[ps, j, 2] = prm["s5_log_dt"][l][g]
            bb[ps, j, 0, :] = prm["s5_b_re"][l][g]
            bb[ps, j, 1, :] = prm["s5_b_im"][l][g]
            cc[ps, j, 0, :] = prm["s5_c_re"][l][g].T
            cc[ps, j, 1, :] = prm["s5_c_im"][l][g].T
    dsk = np.ascontiguousarray(prm["s5_d"][l][g0:g0 + 8].reshape(128, 1))
    iot = np.broadcast_to(np.arange(CS, dtype=np.float32)[None, :], (128, CS)).copy()
    return {"s5par": par, "s5bb": bb, "s5cc": cc, "s5d": dsk, "s5iota": iot, "ident": np.eye(128, dtype=np.float32)}


def emit_s5(nc, mk, T, uT, par_d, bb_d, cc_d, d_d, iota_d, ident_d, yT, odt=F32):
    V = lambda fn, r=(), w=(): mk.op("dve", fn, r, w)
    A = lambda fn, r=(), w=(): mk.op("act", fn, r, w)
    G = lambda fn, r=(), w=(): mk.op("pool", fn, r, w)
    PE = lambda fn, r=(), w=(): mk.op("pe", fn, r, w, skip_same=True)
    sb = mk.sb
    TWO_PI = 2.0 * math.pi
    par = sb("s5_par", [128, 4, 3]); bb = sb("s5_bb", [128, 4, 2, 16]); cc = sb("s5_cc", [128, 4, 2, 16])
    dsk = sb("s5_d", [128, 1]); iot = sb("s5_iota", [128, CS]); ident = sb("s5_id", [128, 128])
    bc = Buf("c")
    for dst, src in ((par, par_d), (bb, bb_d), (cc, cc_d), (dsk, d_d), (iot, iota_d), (ident, ident_d)):
        mk.dma("sp", dst[:], src, writes=[bc])
    P = {}
    for nm in ("dl", "mag", "th", "cs", "sn", "are", "aim", "den", "zre", "zim", "t1", "t2", "t3", "cC", "sC"):
        P[nm] = sb("s5_p_" + nm, [128, 4])
    ti = sb("s5_ti", [128, 4 * CS], I32)
    bp = Buf("p")
    lr, li, ldt = par[:, :, 0], par[:, :, 1], par[:, :, 2]

    def sincos(sin_out, cos_out, x, n, tmpa, tmpb, tint):
        def wrap(r):
            V(lambda: nc.vector.tensor_scalar(out=tmpb, in0=r, scalar1=0.5, scalar2=None, op0=ALU.is_gt), r=[bp], w=[bp])
            V(lambda: nc.vector.tensor_tensor(out=r, in0=r, in1=tmpb, op=ALU.subtract), r=[bp], w=[bp])
            V(lambda: nc.vector.tensor_scalar(out=tmpb, in0=r, scalar1=-0.5, scalar2=None, op0=ALU.is_lt), r=[bp], w=[bp])
            V(lambda: nc.vector.tensor_tensor(out=r, in0=r, in1=tmpb, op=ALU.add), r=[bp], w=[bp])
        V(lambda: nc.vector.tensor_copy(out=tint, in_=x), r=[bp, bc], w=[bp])
        V(lambda: nc.vector.tensor_copy(out=tmpa, in_=tint), r=[bp], w=[bp])
        V(lambda: nc.vector.tensor_tensor(out=tmpa, in0=x, in1=tmpa, op=ALU.subtract), r=[bp, bc], w=[bp])
        wrap(tmpa)
        A(lambda: nc.scalar.activation(out=sin_out, in_=tmpa, func=AF.Sin, scale=TWO_PI), r=[bp], w=[bp])
        V(lambda: nc.vector.tensor_scalar(out=tmpa, in0=tmpa, scalar1=0.25, scalar2=None, op0=ALU.add), r=[bp], w=[bp])
        wrap(tmpa)
        A(lambda: nc.scalar.activation(out=cos_out, in_=tmpa, func=AF.Sin, scale=TWO_PI), r=[bp], w=[bp])

    A(lambda: nc.scalar.activation(out=P["dl"][:], in_=ldt, func=AF.Exp), r=[bc], w=[bp])
    V(lambda: nc.vector.tensor_tensor(out=P["mag"][:], in0=lr, in1=P["dl"][:], op=ALU.mult), r=[bc, bp], w=[bp])
    A(lambda: nc.scalar.activation(out=P["mag"][:], in_=P["mag"][:], func=AF.Exp), r=[bp], w=[bp])
    V(lambda: nc.vector.tensor_tensor(out=P["th"][:], in0=li, in1=P["dl"][:], op=ALU.mult), r=[bc, bp], w=[bp])
    V(lambda: nc.vector.tensor_scalar(out=P["th"][:], in0=P["th"][:], scalar1=1.0 / TWO_PI, scalar2=None, op0=ALU.mult), r=[bp], w=[bp])
    V(lambda: nc.vector.tensor_copy(out=ti[:, 0:4], in_=P["th"][:]), r=[bp], w=[bp])
    V(lambda: nc.vector.tensor_copy(out=P["t1"][:], in_=ti[:, 0:4]), r=[bp], w=[bp])
    V(lambda: nc.vector.tensor_tensor(out=P["th"][:], in0=P["th"][:], in1=P["t1"][:], op=ALU.subtract), r=[bp], w=[bp])
    V(lambda: nc.vector.tensor_scalar(out=P["t1"][:], in0=P["th"][:], scalar1=0.5, scalar2=None, op0=ALU.is_gt), r=[bp], w=[bp])
    V(lambda: nc.vector.tensor_tensor(out=P["th"][:], in0=P["th"][:], in1=P["t1"][:], op=ALU.subtract), r=[bp], w=[bp])
    V(lambda: nc.vector.tensor_scalar(out=P["t1"][:], in0=P["th"][:], scalar1=-0.5, scalar2=None, op0=ALU.is_lt), r=[bp], w=[bp])
    V(lambda: nc.vector.tensor_tensor(out=P["th"][:], in0=P["th"][:], in1=P["t1"][:], op=ALU.add), r=[bp], w=[bp])
    sincos(P["sn"][:], P["cs"][:], P["th"][:], 4, P["t1"][:], P["t2"][:], ti[:, 0:4])
    V(lambda: nc.vector.tensor_tensor(out=P["are"][:], in0=P["mag"][:], in1=P["cs"][:], op=ALU.mult), r=[bp], w=[bp])
    V(lambda: nc.vector.tensor_tensor(out=P["aim"][:], in0=P["mag"][:], in1=P["sn"][:], op=ALU.mult), r=[bp], w=[bp])
    V(lambda: nc.vector.tensor_tensor(out=P["den"][:], in0=lr, in1=lr, op=ALU.mult), r=[bc], w=[bp])
    V(lambda: nc.vector.tensor_tensor(out=P["t1"][:], in0=li, in1=li, op=ALU.mult), r=[bc], w=[bp])
    V(lambda: nc.vector.tensor_tensor(out=P["den"][:], in0=P["den"][:], in1=P["t1"][:], op=ALU.add), r=[bp], w=[bp])
    V(lambda: nc.vector.reciprocal(out=P["den"][:], in_=P["den"][:]), r=[bp], w=[bp])
    V(lambda: nc.vector.tensor_scalar(out=P["t3"][:], in0=P["are"][:], scalar1=-1.0, scalar2=None, op0=ALU.add), r=[bp], w=[bp])
    V(lambda: nc.vector.tensor_tensor(out=P["t1"][:], in0=P["t3"][:], in1=lr, op=ALU.mult), r=[bp, bc], w=[bp])
    V(lambda: nc.vector.tensor_tensor(out=P["t2"][:], in0=P["aim"][:], in1=li, op=ALU.mult), r=[bp, bc], w=[bp])
    V(lambda: nc.vector.tensor_tensor(out=P["t1"][:], in0=P["t1"][:], in1=P["t2"][:], op=ALU.add), r=[bp], w=[bp])
    V(lambda: nc.vector.tensor_tensor(out=P["zre"][:], in0=P["t1"][:], in1=P["den"][:], op=ALU.mult), r=[bp], w=[bp])
    V(lambda: nc.vector.tensor_tensor(out=P["t1"][:], in0=P["aim"][:], in1=lr, op=ALU.mult), r=[bp, bc], w=[bp])
    V(lambda: nc.vector.tensor_tensor(out=P["t2"][:], in0=P["t3"][:], in1=li, op=ALU.mult), r=[bp, bc], w=[bp])
    V(lambda: nc.vector.tensor_tensor(out=P["t1"][:], in0=P["t1"][:], in1=P["t2"][:], op=ALU.subtract), r=[bp], w=[bp])
    V(lambda: nc.vector.tensor_tensor(out=P["zim"][:], in0=P["t1"][:], in1=P["den"][:], op=ALU.mult), r=[bp], w=[bp])
    bbar = sb("s5_bbar", [128, 4, 2, 16]); tb1 = sb("s5_tb1", [128, 4, 16]); tb2 = sb("s5_tb2", [128, 4, 16])
    zre_b = P["zre"][:].unsqueeze(2).to_broadcast([128, 4, 16]); zim_b = P["zim"][:].unsqueeze(2).to_broadcast([128, 4, 16])
    V(lambda: nc.vector.tensor_tensor(out=tb1[:], in0=bb[:, :, 0, :], in1=zre_b, op=ALU.mult), r=[bp, bc], w=[bp])
    V(lambda: nc.vector.tensor_tensor(out=tb2[:], in0=bb[:, :, 1, :], in1=zim_b, op=ALU.mult), r=[bp, bc], w=[bp])
    V(lambda: nc.vector.tensor_tensor(out=bbar[:, :, 0, :], in0=tb1[:], in1=tb2[:], op=ALU.subtract), r=[bp], w=[bp])
    V(lambda: nc.vector.tensor_tensor(out=tb1[:], in0=bb[:, :, 1, :], in1=zre_b, op=ALU.mult), r=[bp, bc], w=[bp])
    V(lambda: nc.vector.tensor_tensor(out=tb2[:], in0=bb[:, :, 0, :], in1=zim_b, op=ALU.mult), r=[bp, bc], w=[bp])
    V(lambda: nc.vector.tensor_tensor(out=bbar[:, :, 1, :], in0=tb1[:], in1=tb2[:], op=ALU.add), r=[bp], w=[bp])
    BD = sb("s5_BD", [128, 4, 2, 128]); CM = sb("s5_CM", [128, 4, 4, 128]); BbT = sb("s5_BbT", [128, 4, 2, 128])
    V(lambda: nc.vector.memset(BD[:].rearrange("p a b c -> p (a b c)"), 0.0), w=[bp])
    V(lambda: nc.vector.memset(CM[:].rearrange("p a b c -> p (a b c)"), 0.0), w=[bp])
    for j in range(4):
        for gl in range(2):
            ps_ = slice(64 * gl, 64 * gl + 64)
            c0 = 32 * j + 16 * gl
            for ri in range(2):
                V(lambda: nc.vector.tensor_copy(out=BD[ps_, j, ri, c0:c0 + 16], in_=bbar[ps_, j, ri, :]), r=[bp], w=[bp])
            V(lambda: nc.vector.tensor_copy(out=CM[ps_, j, 0, c0:c0 + 16], in_=cc[ps_, j, 0, :]), r=[bc], w=[bp])
            V(lambda: nc.vector.tensor_scalar(out=CM[ps_, j, 1, c0:c0 + 16], in0=cc[ps_, j, 0, :], scalar1=-1.0, scalar2=None, op0=ALU.mult), r=[bc], w=[bp])
            V(lambda: nc.vector.tensor_scalar(out=CM[ps_, j, 2, c0:c0 + 16], in0=cc[ps_, j, 1, :], scalar1=-1.0, scalar2=None, op0=ALU.mult), r=[bc], w=[bp])
            V(lambda: nc.vector.tensor_scalar(out=CM[ps_, j, 3, c0:c0 + 16], in0=cc[ps_, j, 1, :], scalar1=-1.0, scalar2=None, op0=ALU.mult), r=[bc], w=[bp])
    ps_tmp = mk.ps("s5_ps_tmp", [128, 512]); b_pt = Buf()
    for j in range(4):
        for ri in range(2):
            PE(lambda: nc.tensor.transpose(ps_tmp[:, ri * 128:(ri + 1) * 128], BD[:, j, ri, :], ident[:, :]), r=[bp, bc], w=[b_pt])
        V(lambda: nc.vector.tensor_copy(out=BbT[:, j, :, :].rearrange("p a b -> p (a b)"), in_=ps_tmp[:, 0:256]), r=[], w=[b_pt, bp])
    cosT = sb("s5_cosT", [128, 4, CS]); sinT = sb("s5_sinT", [128, 4, CS])
    xa = sb("s5_xa", [128, 4 * CS]); xb = sb("s5_xb", [128, 4 * CS]); xc = sb("s5_xc", [128, 4 * CS])
    for j in range(4):
        V(lambda: nc.vector.tensor_scalar(out=xc[:, j * CS:(j + 1) * CS], in0=iot[:], scalar1=P["th"][:, j:j + 1], scalar2=None, op0=ALU.mult), r=[bp, bc], w=[bp])
    sincos(sinT[:].rearrange("p a b -> p (a b)"), cosT[:].rearrange("p a b -> p (a b)"), xc[:], 4 * CS, xa[:], xb[:], ti[:])
    V(lambda: nc.vector.tensor_scalar(out=P["t3"][:], in0=P["th"][:], scalar1=float(CS), scalar2=None, op0=ALU.mult), r=[bp], w=[bp])
    sincos(P["sC"][:], P["cC"][:], P["t3"][:], 4, P["t1"][:], P["t2"][:], ti[:, 0:4])
    WW = []
    for par in range(2):
        Wd = {}
        for nm in ("t1", "t2", "t3", "t4", "br", "bi", "zr", "zi", "q1", "q2", "q3", "q4"):
            Wd[nm] = sb("s5_w%d_%s" % (par, nm), [128, CS])
        WW.append(Wd)
    b_wl = [Buf("w0"), Buf("w1")]; b_zl = [Buf("z0"), Buf("z1")]; b_ql = [Buf("q0"), Buf("q1")]
    init = sb("s5_init", [128, 4, 2]); itmp = sb("s5_itmp", [128, 4, 2]); b_il = [Buf("init%d" % j) for j in range(4)]
    for j in range(4):
        V(lambda: nc.vector.memset(init[:, j, :], 0.0), w=[b_il[j]])
    yv = sb("s5_yv", [128, CS]); y2 = sb("s5_y2", [128, CS]); yo = sb("s5_yo", [128, CS], odt); b_y = Buf("y")
    ps_al = [mk.ps("s5_ps_a%d" % i, [128, 512]) for i in range(2)]; ps_bl = [mk.ps("s5_ps_b%d" % i, [128, 512]) for i in range(2)]
    ps_y = mk.ps("s5_ps_y", [128, 512])
    b_pal, b_pbl, b_py = [Buf(), Buf()], [Buf(), Buf()], Buf()
    uts = [sb("s5_u%d" % i, [128, CS]) for i in range(2)]; b_ul = [Buf(), Buf()]
    NCK = T // CS

    def stage1(n):
        chk, j = n // 4, n % 4
        par = n % 2
        ut = uts[chk % 2]; b_u = b_ul[chk % 2]
        if j == 0:
            mk.dma("sp", ut[:], uT[:, chk * CS:(chk + 1) * CS], writes=[b_u])
        W = WW[par]; b_w = b_wl[par]
        ps_a = ps_al[par]; ps_b = ps_bl[par]; b_pa = b_pal[par]; b_pb = b_pbl[par]
        PE(lambda: nc.tensor.matmul(ps_a[:, :], lhsT=BbT[:, j, 0, :], rhs=ut[:, :], start=True, stop=True), r=[bp, b_u], w=[b_pa])
        PE(lambda: nc.tensor.matmul(ps_b[:, :], lhsT=BbT[:, j, 1, :], rhs=ut[:, :], start=True, stop=True), r=[bp, b_u], w=[b_pb])

    def stage1v(n):
        chk, j = n // 4, n % 4
        par = n % 2
        W = WW[par]; b_w = b_wl[par]
        ps_a = ps_al[par]; ps_b = ps_bl[par]; b_pa = b_pal[par]; b_pb = b_pbl[par]
        cj, sj = cosT[:, j, :], sinT[:, j, :]
        V(lambda: nc.vector.tensor_tensor(out=W["t1"][:], in0=ps_a[:, :], in1=cj, op=ALU.mult), r=[bp], w=[b_pa, b_w])
        V(lambda: nc.vector.tensor_tensor(out=W["t4"][:], in0=ps_a[:, :], in1=sj, op=ALU.mult), r=[bp], w=[b_pa, b_w])
        V(lambda: nc.vector.tensor_tensor(out=W["t2"][:], in0=ps_b[:, :], in1=sj, op=ALU.mult), r=[bp], w=[b_pb, b_w])
        V(lambda: nc.vector.tensor_tensor(out=W["t3"][:], in0=ps_b[:, :], in1=cj, op=ALU.mult), r=[bp], w=[b_pb, b_w])
        G(lambda: nc.gpsimd.tensor_tensor(out=W["br"][:], in0=W["t1"][:], in1=W["t2"][:], op=ALU.add), r=[b_w], w=[b_w])
        G(lambda: nc.gpsimd.tensor_tensor(out=W["bi"][:], in0=W["t3"][:], in1=W["t4"][:], op=ALU.subtract), r=[b_w], w=[b_w])

    def stage2(n):
        chk, j = n // 4, n % 4
        par = n % 2
        t0 = chk * CS
        ut = uts[chk % 2]; b_u = b_ul[chk % 2]
        W = WW[par]; b_w = b_wl[par]; b_z = b_zl[par]; b_q = b_ql[par]; b_i = b_il[j]
        cj, sj = cosT[:, j, :], sinT[:, j, :]
        rho = P["mag"][:, j:j + 1].to_broadcast([128, CS])
        V(lambda: nc.vector.tensor_tensor_scan(out=W["zr"][:], data0=rho, data1=W["br"][:], initial=init[:, j, 0:1], op0=ALU.mult, op1=ALU.add), r=[b_w, bp, b_i, b_q], w=[b_z])
        V(lambda: nc.vector.tensor_tensor_scan(out=W["zi"][:], data0=rho, data1=W["bi"][:], initial=init[:, j, 1:2], op0=ALU.mult, op1=ALU.add), r=[b_w, bp, b_i, b_q], w=[b_z])
        zrl, zil = W["zr"][:, CS - 1:CS], W["zi"][:, CS - 1:CS]
        cC, sC = P["cC"][:, j:j + 1], P["sC"][:, j:j + 1]
        V(lambda: nc.vector.tensor_tensor(out=itmp[:, j, 0:1], in0=zil, in1=sC, op=ALU.mult), r=[b_z, bp], w=[b_i])
        V(lambda: nc.vector.scalar_tensor_tensor(out=init[:, j, 0:1], in0=zrl, scalar=cC, in1=itmp[:, j, 0:1], op0=ALU.mult, op1=ALU.subtract), r=[b_z, bp], w=[b_i])
        V(lambda: nc.vector.tensor_tensor(out=itmp[:, j, 1:2], in0=zrl, in1=sC, op=ALU.mult), r=[b_z, bp], w=[b_i])
        V(lambda: nc.vector.scalar_tensor_tensor(out=init[:, j, 1:2], in0=zil, scalar=cC, in1=itmp[:, j, 1:2], op0=ALU.mult, op1=ALU.add), r=[b_z, bp], w=[b_i])
        G(lambda: nc.gpsimd.tensor_tensor(out=W["q1"][:], in0=W["zr"][:], in1=cj, op=ALU.mult), r=[b_z, bp], w=[b_q])
        G(lambda: nc.gpsimd.tensor_tensor(out=W["q2"][:], in0=W["zi"][:], in1=sj, op=ALU.mult), r=[b_z, bp], w=[b_q])
        V(lambda: nc.vector.tensor_tensor(out=W["q3"][:], in0=W["zi"][:], in1=cj, op=ALU.mult), r=[b_z, bp], w=[b_q])
        V(lambda: nc.vector.tensor_tensor(out=W["q4"][:], in0=W["zr"][:], in1=sj, op=ALU.mult), r=[b_z, bp], w=[b_q])
        for qi, nm in enumerate(("q1", "q2", "q3", "q4")):
            PE(lambda: nc.tensor.matmul(ps_y[:, :], lhsT=CM[:, j, qi, :], rhs=W[nm][:, :], start=(j == 0 and qi == 0), stop=(j == 3 and qi == 3)), r=[bp, b_q], w=[b_py])
        if j == 3:
            V(lambda: nc.vector.scalar_tensor_tensor(out=yv[:], in0=ut[:], scalar=dsk[:, 0:1], in1=ps_y[:, :], op0=ALU.mult, op1=ALU.add), r=[b_u, bc], w=[b_py, b_y])
            A(lambda: nc.scalar.activation(out=y2[:], in_=yv[:], func=AF.Square), r=[b_y], w=[b_y])
            V(lambda: nc.vector.tensor_scalar(out=y2[:], in0=y2[:], scalar1=0.044715, scalar2=1.0, op0=ALU.mult, op1=ALU.add), r=[b_y], w=[b_y])
            V(lambda: nc.vector.tensor_tensor(out=y2[:], in0=y2[:], in1=yv[:], op=ALU.mult), r=[b_y], w=[b_y])
            A(lambda: nc.scalar.activation(out=y2[:], in_=y2[:], func=AF.Tanh, scale=0.7978845608028654), r=[b_y], w=[b_y])
            V(lambda: nc.vector.scalar_tensor_tensor(out=y2[:], in0=y2[:], scalar=1.0, in1=yv[:], op0=ALU.add, op1=ALU.mult), r=[b_y], w=[b_y])
            A(lambda: nc.scalar.mul(out=yo[:], in_=y2[:], mul=0.5), r=[b_y], w=[b_y])
            mk.dma("sp", yT[:, t0:t0 + CS], yo[:], reads=[b_y], is_output=True)

    NTL_ = NCK * 4
    stage1(0)
    for n in range(NTL_ + 1):
        if n + 1 < NTL_:
            stage1(n + 1)
        if n < NTL_:
            stage1v(n)
        if n >= 1:
            stage2(n - 1)


def build(which, T):
    nc = bass.Bass("TRN2", target_bir_lowering=False)
    dt = lambda n, s, k="ExternalInput": nc.dram_tensor(n, s, F32, kind=k).ap()
    with ExitStack() as ctx:
        mk = MK(nc, ctx)
        if which == "conv":
            emit_conv(nc, mk, T, dt("cvin", [384, T]), dt("cw", [128, 3]), dt("yT", [128, T], "ExternalOutput"), TB=min(T, 2048))
        elif which == "attn":
            emit_attn(nc, mk, T, dt("qkv", [256, T]), dt("btab", [2, 128, 2, 256]), dt("sinkt", [128, 2]), dt("ident", [128, 128]), dt("yT", [128, T], "ExternalOutput"))
        elif which == "s5":
            emit_s5(nc, mk, T, dt("uT", [128, T]), dt("s5par", [128, 4, 3]), dt("s5bb", [128, 4, 2, 16]), dt("s5cc", [128, 4, 2, 16]),
                    dt("s5d", [128, 1]), dt("s5iota", [128, CS]), dt("ident", [128, 128]), dt("yT", [128, T], "ExternalOutput"))
        mk.finish("sp")
        print(which, "ops", mk.nops, "waits", mk.nwaits)
    return nc


import math
import numpy as np
from contextlib import ExitStack

D = 2048
TOK = 2048
NT = TOK // 128
NE = 32
CAP = 256
ALPHA = 4 ** 0.25
DE = 512


def k3_consts():
    tp = np.arange(128)[:, None]
    t = np.arange(128)[None, :]
    U = (tp < t).astype(np.float32)
    ecap = np.broadcast_to((np.arange(NE) * CAP).astype(np.float32)[None, :], (128, NE)).copy()
    return {"U": U, "ecap": ecap, "ident": np.eye(128, dtype=np.float32)}


def emit_k3(nc, mk, ymixT, x, w_out, glu_w, glu_b, rows, wr, br, w1, w3, w2, cst, x1s, Xg, Yg, xout, ymload=None, rowload=None):
    V = lambda fn, r=(), w=(): mk.op("dve", fn, r, w)
    A = lambda fn, r=(), w=(): mk.op("act", fn, r, w)
    G = lambda fn, r=(), w=(): mk.op("pool", fn, r, w)
    PE = lambda fn, r=(), w=(): mk.op("pe", fn, r, w, skip_same=True)
    gw = mk.sb("k3_gw", [128, NT, 2]); slot = mk.sb("k3_slot", [128, NT, 2], I32); b_rt = Buf("route")
    ident = mk.sb("k3_ident", [128, 128]); identb = mk.sb("k3_identb", [128, 128], BF16); bc = Buf("c")
    mk.dma("sp", ident[:], cst["ident"], writes=[bc])
    V(lambda: nc.vector.tensor_copy(out=identb[:], in_=ident[:]), r=[bc], w=[bc])
    with ExitStack() as pa:
        sb = lambda n, s, dt=F32: pa.enter_context(nc.sbuf_tensor("k3a%d_" % mk.gen + n, list(s), dt))
        ps = lambda n, s, dt=F32: pa.enter_context(nc.psum_tensor("k3a%d_" % mk.gen + n, list(s), dt))
        wo = sb("wo", [128, 16, D], BF16); gluw = sb("gluw", [128, 4, 512], BF16); glub = sb("glub", [128, 4])
        R = [sb("row%d" % i, [128, D]) for i in range(5)]
        wrt = sb("wr", [128, 16, 36]); brt = sb("br", [128, 36]); Ut = sb("U", [128, 128]); ones = sb("ones", [128, 128]); ecap = sb("ecap", [128, NE])
        Srun = sb("Srun", [128, NE]); b_S = Buf("S")
        if ymload is None:
            ym = [sb("ym%d" % i, [128, 16, 128], BF16) for i in range(2)]; b_ym = [Buf(), Buf()]
        else:
            ymbig = sb("ymbig", [128, 16, 1024], BF16); _bym = Buf()
            ym = None; b_ym = [_bym, _bym]
        xt = [sb("xt%d" % i, [128, D]) for i in range(2)]; b_xt = [Buf(), Buf()]
        sg = sb("sg", [128, 4, 128]); b_sg = Buf()
        xr = sb("xr", [128, D]); b_xr = Buf()
        xn = xr; b_xn = b_xr
        _x1 = sb("x1_0", [128, D]); _bx1 = Buf()
        x1 = [_x1, _x1]; b_x1 = [_bx1, _bx1]
        _h2 = sb("h2_0", [128, D]); _bh2 = Buf()
        h2l = [_h2, _h2]; b_h2l = [_bh2, _bh2]
        hb = [sb("hb%d" % i, [128, D], BF16) for i in range(2)]; b_hb = [Buf(), Buf()]
        h2T = sb("h2T", [128, 16, 128]); b_h2T = Buf()
        st = sb("st", [128, 4, 6]); mv = sb("mv", [128, 2]); rs = sb("rs", [128, 1]); nmr = sb("nmr", [128, 1]); b_s = Buf()
        lg = sb("lg", [128, 36]); rt = sb("rt", [128, 16]); em = sb("em", [128, 32]); em2 = sb("em2", [128, 32])
        oh1 = sb("oh1", [128, 32]); oh2 = sb("oh2", [128, 32]); Mk = sb("Mk", [128, 32]); rank = sb("rank", [128, 32]); t32 = sb("t32", [128, 32]); pen = sb("pen", [128, 4]); ohg = sb("ohg", [128, 4]); eg = sb("eg", [128, 4])
        b_r = Buf("r")
        p_g = ps("p_g", [128, 512]); b_pg = Buf()
        p_o = [ps("p_o%d" % i, [128, 512]) for i in range(2)]; b_po = [Buf(), Buf()]
        p_t = [ps("p_t%d" % i, [128, 512]) for i in range(2)]; b_pt = [Buf(), Buf()]
        p_r = ps("p_r", [128, 512]); b_pr = Buf()
        p_k = ps("p_k", [128, 512]); b_pk = Buf()
        mk.dma("pool", wo[:], w_out.rearrange("(k p) c -> p k c", p=128), writes=[bc])
        mk.dma("pool", gluw[:], glu_w.rearrange("(k p) c -> p k c", p=128), writes=[bc])
        mk.dma("sp", glub[:], glu_b, writes=[bc])
        if rowload is None:
            rowload = lambda dst, ri, bcx: mk.dma("sp", dst[:], rows[ri], writes=[bcx])
        for i, ri in enumerate((0, 1, 2, 3, 4)):
            rowload(R[i], ri, bc)
        G(lambda: nc.gpsimd.tensor_scalar(out=R[0][:], in0=R[0][:], scalar1=1.0, scalar2=None, op0=ALU.add), r=[bc], w=[bc])
        G(lambda: nc.gpsimd.tensor_scalar(out=R[3][:], in0=R[3][:], scalar1=1.0, scalar2=None, op0=ALU.add), r=[bc], w=[bc])
        mk.dma("sp", wrt[:], wr.rearrange("(k p) c -> p k c", p=128), writes=[bc])
        mk.dma("sp", brt[:], br, writes=[bc])
        mk.dma("sp", Ut[:], cst["U"], writes=[bc])
        mk.dma("sp", ecap[:], cst["ecap"], writes=[bc])
        V(lambda: nc.vector.memset(ones[:], 1.0), w=[bc])
        V(lambda: nc.vector.memset(Srun[:], 0.0), w=[b_S])
        zt = hb[0]; b_z = b_hb[0]
        V(lambda: nc.vector.memset(zt[:], 0.0), w=[b_z])
        b_Xg = Buf("Xg")
        XgV = Xg.rearrange("(a p) c -> p a c", p=128)
        for a in range(NE * CAP // 128):
            mk.dma("sp", XgV[:, a, :], zt[:], reads=[b_z], writes=[b_Xg])

        def front(t):
            i = t % 2
            ts_ = slice(t * 128, (t + 1) * 128)
            if ymload is None:
                mk.dma("pool", ym[i][:], ymixT[:, ts_].rearrange("(k p) t -> p k t", p=128), writes=[b_ym[i]])
                ymt = ym[i]
            else:
                if t % 8 == 0:
                    ymload(t // 8, ymbig, b_ym[i])
                ymt = ymbig[:, :, (t % 8) * 128:(t % 8 + 1) * 128]
            mk.dma("sp", xt[i][:], x[ts_, :], writes=[b_xt[i]])
            for oc in range(4):
                for kc in range(4):
                    PE(lambda: nc.tensor.matmul(p_g[:, oc * 128:(oc + 1) * 128], lhsT=gluw[:, kc, oc * 128:(oc + 1) * 128], rhs=ymt[:, 12 + kc, :], start=(kc == 0), stop=(kc == 3)),
                       r=[bc, b_ym[i]], w=[b_pg])
            for oc in range(4):
                A(lambda: nc.scalar.activation(out=sg[:, oc, :], in_=p_g[:, oc * 128:(oc + 1) * 128], func=AF.Sigmoid, bias=glub[:, oc:oc + 1], scale=1.0), r=[bc], w=[b_pg, b_sg])
            V(lambda: nc.vector.tensor_tensor(out=ymt[:, 12:16, :], in0=ymt[:, 12:16, :], in1=sg[:], op=ALU.mult), r=[b_sg], w=[b_ym[i]])
            for cc in range(4):
                j = cc % 2
                for k in range(16):
                    PE(lambda: nc.tensor.matmul(p_o[j][:, :], lhsT=ymt[:, k, :], rhs=wo[:, k, cc * 512:(cc + 1) * 512], start=(k == 0), stop=(k == 15)), r=[b_ym[i], bc], w=[b_po[j]])
                V(lambda: nc.vector.tensor_tensor(out=xr[:, cc * 512:(cc + 1) * 512], in0=p_o[j][:, :], in1=R[0][:, cc * 512:(cc + 1) * 512], op=ALU.mult), r=[bc], w=[b_po[j], b_xr])
            V(lambda: nc.vector.scalar_tensor_tensor(out=xr[:], in0=xt[i][:], scalar=ALPHA, in1=xr[:], op0=ALU.mult, op1=ALU.add), r=[b_xt[i]], w=[b_xr])
            ln_stats(nc, mk, xr, b_xr, st, mv, rs, nmr, b_s)
            A(lambda: nc.scalar.activation(out=xn[:], in_=xr[:], func=AF.Identity, bias=nmr[:, 0:1], scale=rs[:, 0:1]), r=[b_s], w=[b_xn])
            V(lambda: nc.vector.tensor_tensor(out=xn[:], in0=xn[:], in1=R[1][:], op=ALU.mult), r=[bc], w=[b_xn])
            V(lambda: nc.vector.tensor_tensor(out=x1[i][:], in0=xn[:], in1=R[2][:], op=ALU.add), r=[b_xn, bc], w=[b_x1[i]])
            mk.dma("sp", x1s[ts_, :], x1[i][:], reads=[b_x1[i]])
            ln_stats(nc, mk, x1[i], b_x1[i], st, mv, rs, nmr, b_s)
            A(lambda: nc.scalar.activation(out=xn[:], in_=x1[i][:], func=AF.Identity, bias=nmr[:, 0:1], scale=rs[:, 0:1]), r=[b_x1[i], b_s], w=[b_xn])
            V(lambda: nc.vector.tensor_tensor(out=xn[:], in0=xn[:], in1=R[3][:], op=ALU.mult), r=[bc], w=[b_xn])
            h2 = h2l[i]; b_h2 = b_h2l[i]
            V(lambda: nc.vector.tensor_tensor(out=h2[:], in0=xn[:], in1=R[4][:], op=ALU.add), r=[b_xn, bc], w=[b_h2])
            A(lambda: nc.scalar.copy(out=hb[i][:], in_=h2[:]), r=[b_h2], w=[b_hb[i]])
        def tail(t):
            i = t % 2
            ts_ = slice(t * 128, (t + 1) * 128)
            h2 = h2l[i]; b_h2 = b_h2l[i]
            for half in range(4):
                j = half % 2
                for kk in range(4):
                    k = half * 4 + kk
                    PE(lambda: nc.tensor.transpose(p_t[j][:, kk * 128:(kk + 1) * 128], h2[:, k * 128:(k + 1) * 128], ident[:, :]), r=[b_h2, bc], w=[b_pt[j]])
                if j == 0:
                    A(lambda: nc.scalar.copy(out=h2T[:, half * 4:(half + 1) * 4, :].rearrange("p a b -> p (a b)"), in_=p_t[j][:, :]), r=[], w=[b_pt[j], b_h2T])
                else:
                    V(lambda: nc.vector.tensor_copy(out=h2T[:, half * 4:(half + 1) * 4, :].rearrange("p a b -> p (a b)"), in_=p_t[j][:, :]), r=[], w=[b_pt[j], b_h2T])
            for k in range(16):
                PE(lambda: nc.tensor.matmul(p_r[:, 0:36], lhsT=h2T[:, k, :], rhs=wrt[:, k, :], start=(k == 0), stop=(k == 15)), r=[b_h2T, bc], w=[b_pr])
            V(lambda: nc.vector.tensor_tensor(out=lg[:], in0=p_r[:, 0:36], in1=brt[:], op=ALU.add), r=[bc], w=[b_pr, b_r])
            R_ = lambda fn: V(fn, r=[b_r, bc], w=[b_r])
            R_(lambda: nc.vector.reduce_max(out=rt[:, 0:1], in_=lg[:, 0:4], axis=AX.X))
            R_(lambda: nc.vector.tensor_scalar(out=ohg[:], in0=lg[:, 0:4], scalar1=rt[:, 0:1], scalar2=None, op0=ALU.is_ge))
            R_(lambda: nc.vector.tensor_scalar(out=rt[:, 1:2], in0=rt[:, 0:1], scalar1=-1.0, scalar2=None, op0=ALU.mult))
            A(lambda: nc.scalar.activation(out=eg[:], in_=lg[:, 0:4], func=AF.Exp, bias=rt[:, 1:2], scale=1.0, accum_out=rt[:, 2:3]), r=[b_r], w=[b_r])
            R_(lambda: nc.vector.reciprocal(out=rt[:, 3:4], in_=rt[:, 2:3]))
            R_(lambda: nc.vector.tensor_scalar(out=pen[:], in0=ohg[:], scalar1=-1.0, scalar2=1e30, op0=ALU.add, op1=ALU.mult))
            R_(lambda: nc.vector.tensor_tensor(out=em[:].rearrange("p (g e) -> p g e", e=8), in0=lg[:, 4:36].rearrange("p (g e) -> p g e", e=8),
                                               in1=pen[:].unsqueeze(2).to_broadcast([128, 4, 8]), op=ALU.add))
            R_(lambda: nc.vector.reduce_max(out=rt[:, 4:5], in_=em[:], axis=AX.X))
            R_(lambda: nc.vector.tensor_scalar(out=oh1[:], in0=em[:], scalar1=rt[:, 4:5], scalar2=None, op0=ALU.is_ge))
            R_(lambda: nc.vector.scalar_tensor_tensor(out=em2[:], in0=oh1[:], scalar=-1e30, in1=em[:], op0=ALU.mult, op1=ALU.add))
            R_(lambda: nc.vector.reduce_max(out=rt[:, 5:6], in_=em2[:], axis=AX.X))
            R_(lambda: nc.vector.tensor_scalar(out=oh2[:], in0=em2[:], scalar1=rt[:, 5:6], scalar2=None, op0=ALU.is_ge))
            R_(lambda: nc.vector.tensor_tensor(out=rt[:, 6:7], in0=rt[:, 5:6], in1=rt[:, 4:5], op=ALU.subtract))
            A(lambda: nc.scalar.activation(out=rt[:, 7:8], in_=rt[:, 6:7], func=AF.Exp), r=[b_r], w=[b_r])
            R_(lambda: nc.vector.tensor_scalar(out=rt[:, 8:9], in0=rt[:, 7:8], scalar1=1.0, scalar2=None, op0=ALU.add))
            R_(lambda: nc.vector.reciprocal(out=rt[:, 8:9], in_=rt[:, 8:9]))
            R_(lambda: nc.vector.tensor_tensor(out=rt[:, 9:10], in0=rt[:, 7:8], in1=rt[:, 8:9], op=ALU.mult))
            R_(lambda: nc.vector.tensor_tensor(out=Mk[:], in0=oh1[:], in1=oh2[:], op=ALU.add))
            PE(lambda: nc.tensor.matmul(p_k[:, 0:32], lhsT=Ut[:, :], rhs=Mk[:, :], start=True, stop=True), r=[bc, b_r], w=[b_pk])
            PE(lambda: nc.tensor.matmul(p_k[:, 32:64], lhsT=ones[:, :], rhs=Mk[:, :], start=True, stop=True), r=[bc, b_r], w=[b_pk])
            V(lambda: nc.vector.tensor_tensor(out=rank[:], in0=p_k[:, 0:32], in1=Srun[:], op=ALU.add), r=[b_S, b_r], w=[b_pk, b_r])
            V(lambda: nc.vector.tensor_tensor(out=Srun[:], in0=p_k[:, 32:64], in1=Srun[:], op=ALU.add), r=[b_r], w=[b_pk, b_S])
            for kx, oh in enumerate((oh1, oh2)):
                R_(lambda: nc.vector.tensor_tensor(out=t32[:], in0=oh[:], in1=rank[:], op=ALU.mult))
                R_(lambda: nc.vector.reduce_sum(out=rt[:, 10:11], in_=t32[:], axis=AX.X))
                R_(lambda: nc.vector.tensor_tensor(out=t32[:], in0=oh[:], in1=ecap[:], op=ALU.mult))
                R_(lambda: nc.vector.reduce_sum(out=rt[:, 11:12], in_=t32[:], axis=AX.X))
                R_(lambda: nc.vector.tensor_scalar(out=rt[:, 12:13], in0=rt[:, 10:11], scalar1=float(CAP), scalar2=None, op0=ALU.is_ge))
                R_(lambda: nc.vector.tensor_tensor(out=rt[:, 11:12], in0=rt[:, 11:12], in1=rt[:, 10:11], op=ALU.add))
                R_(lambda: nc.vector.scalar_tensor_tensor(out=rt[:, 11:12], in0=rt[:, 12:13], scalar=1e6, in1=rt[:, 11:12], op0=ALU.mult, op1=ALU.add))
                V(lambda: nc.vector.tensor_copy(out=slot[:, t, kx:kx + 1], in_=rt[:, 11:12]), r=[b_r], w=[b_rt])
                R_(lambda: nc.vector.tensor_scalar(out=rt[:, 13:14], in0=rt[:, 12:13], scalar1=-1.0, scalar2=-1.0, op0=ALU.add, op1=ALU.mult))
                R_(lambda: nc.vector.tensor_tensor(out=rt[:, 13:14], in0=rt[:, 13:14], in1=rt[:, 3:4], op=ALU.mult))
                V(lambda: nc.vector.tensor_tensor(out=gw[:, t, kx:kx + 1], in0=rt[:, 13:14], in1=rt[:, 8 + kx:9 + kx], op=ALU.mult), r=[b_r], w=[b_rt])
                mk.idma(Xg, hb[i][:, :], slot[:, t, kx:kx + 1], True, NE * CAP - 1, reads=[b_hb[i], b_rt], writes=[b_Xg])
        for t in range(NT):
            front(t)
            tail(t)
        mk.barrier()
    b_Yg = Buf("Yg")
    with ExitStack() as pb:
        sb = lambda n, s, dt=F32: pb.enter_context(nc.sbuf_tensor("k3b%d_" % mk.gen + n, list(s), dt))
        ps = lambda n, s, dt=F32: pb.enter_context(nc.psum_tensor("k3b%d_" % mk.gen + n, list(s), dt))
        W1 = [sb("w1_%d" % i, [128, 16, DE], BF16) for i in range(2)]
        W3 = [sb("w3_%d" % i, [128, 16, DE], BF16) for i in range(2)]
        W2 = [sb("w2_%d" % i, [128, 4, D], BF16) for i in range(2)]
        b_w = [Buf(), Buf()]
        NTL = CAP // 128
        xg = sb("xg", [128, NTL, D], BF16); b_xg = Buf()
        xgT = sb("xgT", [128, 16, CAP], BF16); b_xgT = Buf()
        ga = sb("ga", [128, CAP]); b_ga = Buf()
        gh = sb("gh", [128, 4, CAP], BF16); b_gh = Buf()
        yo = [sb("yo%d" % i, [128, D]) for i in range(2)]; b_yo = [Buf(), Buf()]
        p_t = [ps("p_t%d" % i, [128, 1024], BF16) for i in range(2)]; b_pt = [Buf(), Buf()]
        p_a = ps("p_a", [128, 512]); p_b = ps("p_b", [128, 512]); b_pa, b_pb = Buf(), Buf()
        p_o = [ps("p_o%d" % i, [128, 512]) for i in range(2)]; b_po = [Buf(), Buf()]

        def load_w(e):
            j = e % 2
            mk.dma("pool", W1[j][:], w1[e].rearrange("(p k) c -> p k c", k=16), writes=[b_w[j]])
            mk.dma("pool", W3[j][:], w3[e].rearrange("(p k) c -> p k c", k=16), writes=[b_w[j]])
            mk.dma("pool", W2[j][:], w2[e].rearrange("(k p) c -> p k c", p=128), writes=[b_w[j]])

        load_w(0)
        yoi = 0
        for e in range(NE):
            j = e % 2
            if e + 1 < NE:
                load_w(e + 1)
            mk.dma("sp", xg[:], Xg[e * CAP:(e + 1) * CAP, :].rearrange("(a p) c -> p a c", p=128), reads=[b_Xg], writes=[b_xg])
            for a in range(NTL):
                for half in range(2):
                    for kk in range(8):
                        k = half * 8 + kk
                        PE(lambda: nc.tensor.transpose(p_t[half][:, kk * 128:(kk + 1) * 128], xg[:, a, :].rearrange("t (p k) -> t k p", k=16)[:, k, :], identb[:, :]), r=[b_xg, bc], w=[b_pt[half]])
                    if half == 0:
                        A(lambda: nc.scalar.copy(out=xgT[:, 0:8, a * 128:(a + 1) * 128], in_=p_t[half][:, :].rearrange("p (k t) -> p k t", t=128)), r=[], w=[b_pt[half], b_xgT])
                    else:
                        V(lambda: nc.vector.tensor_copy(out=xgT[:, 8:16, a * 128:(a + 1) * 128], in_=p_t[half][:, :].rearrange("p (k t) -> p k t", t=128)), r=[], w=[b_pt[half], b_xgT])
            for hc in range(4):
                for k in range(16):
                    PE(lambda: nc.tensor.matmul(p_a[:, 0:CAP], lhsT=W1[j][:, k, hc * 128:(hc + 1) * 128], rhs=xgT[:, k, :], start=(k == 0), stop=(k == 15)), r=[b_w[j], b_xgT], w=[b_pa])
                for k in range(16):
                    PE(lambda: nc.tensor.matmul(p_b[:, 0:CAP], lhsT=W3[j][:, k, hc * 128:(hc + 1) * 128], rhs=xgT[:, k, :], start=(k == 0), stop=(k == 15)), r=[b_w[j], b_xgT], w=[b_pb])
                A(lambda: nc.scalar.activation(out=ga[:], in_=p_a[:, 0:CAP], func=AF.Silu), r=[], w=[b_pa, b_ga])
                V(lambda: nc.vector.tensor_tensor(out=gh[:, hc, :], in0=p_b[:, 0:CAP], in1=ga[:], op=ALU.mult), r=[b_ga], w=[b_pb, b_gh])
            for a in range(NTL):
                y_ = yoi % 2
                yoi += 1
                for cc in range(4):
                    q = cc % 2
                    for hc in range(4):
                        PE(lambda: nc.tensor.matmul(p_o[q][:, :], lhsT=gh[:, hc, a * 128:(a + 1) * 128], rhs=W2[j][:, hc, cc * 512:(cc + 1) * 512], start=(hc == 0), stop=(hc == 3)), r=[b_gh, b_w[j]], w=[b_po[q]])
                    if q == 0:
                        A(lambda: nc.scalar.copy(out=yo[y_][:, cc * 512:(cc + 1) * 512], in_=p_o[q][:, :]), r=[], w=[b_po[q], b_yo[y_]])
                    else:
                        V(lambda: nc.vector.tensor_copy(out=yo[y_][:, cc * 512:(cc + 1) * 512], in_=p_o[q][:, :]), r=[], w=[b_po[q], b_yo[y_]])
                r0 = e * CAP + a * 128
                mk.dma("sp", Yg[r0:r0 + 128, :], yo[y_][:], reads=[b_yo[y_]], writes=[b_Yg])
        mk.barrier()
    with ExitStack() as pc:
        sb = lambda n, s, dt=F32: pc.enter_context(nc.sbuf_tensor("k3c%d_" % mk.gen + n, list(s), dt))
        R = [sb("row%d" % i, [128, D]) for i in range(3)]
        for i, ri in enumerate((5, 6, 7)):
            rowload(R[i], ri, bc)
        G(lambda: nc.gpsimd.tensor_scalar(out=R[0][:], in0=R[0][:], scalar1=1.0, scalar2=None, op0=ALU.add), r=[bc], w=[bc])
        Y = [[sb("Y%d_%d" % (k, i), [128, D]) for i in range(2)] for k in range(2)]
        b_Y = [[Buf(), Buf()], [Buf(), Buf()]]
        for k in range(2):
            for i in range(2):
                V(lambda: nc.vector.memset(Y[k][i][:], 0.0), w=[b_Y[k][i]])
        x1t = [sb("x1t%d" % i, [128, D]) for i in range(2)]; b_x1 = [Buf(), Buf()]
        yml = [sb("ym%d" % i, [128, D]) for i in range(2)]; b_yml = [Buf(), Buf()]
        xnl = [sb("xn%d" % i, [128, D]) for i in range(2)]; b_xnl = [Buf(), Buf()]
        ot = [sb("ot%d" % i, [128, D]) for i in range(2)]; b_ot = [Buf(), Buf()]
        stl = [sb("st%d" % i, [128, 4, 6]) for i in range(2)]; mvl = [sb("mv%d" % i, [128, 2]) for i in range(2)]
        rsl = [sb("rs%d" % i, [128, 1]) for i in range(2)]; nmrl = [sb("nmr%d" % i, [128, 1]) for i in range(2)]; b_sl = [Buf(), Buf()]
        for t in range(NT):
            i = t % 2
            ts_ = slice(t * 128, (t + 1) * 128)
            ym = yml[i]; b_ym = b_yml[i]; xn = xnl[i]; b_xn = b_xnl[i]
            st, mv, rs, nmr, b_s = stl[i], mvl[i], rsl[i], nmrl[i], b_sl[i]
            for k in range(2):
                mk.idma(Y[k][i][:, :], Yg, slot[:, t, k:k + 1], False, NE * CAP - 1, reads=[b_Yg, b_rt], writes=[b_Y[k][i]])
            mk.dma("sp", x1t[i][:], x1s[ts_, :], writes=[b_x1[i]])
            A(lambda: nc.scalar.activation(out=ym[:], in_=Y[0][i][:], func=AF.Copy, scale=gw[:, t, 0:1]), r=[b_Y[0][i], b_rt], w=[b_ym])
            V(lambda: nc.vector.scalar_tensor_tensor(out=ym[:], in0=Y[1][i][:], scalar=gw[:, t, 1:2], in1=ym[:], op0=ALU.mult, op1=ALU.add), r=[b_Y[1][i], b_rt], w=[b_ym])
            V(lambda: nc.vector.tensor_tensor(out=ym[:], in0=ym[:], in1=R[0][:], op=ALU.mult), r=[bc], w=[b_ym])
            V(lambda: nc.vector.scalar_tensor_tensor(out=ym[:], in0=x1t[i][:], scalar=ALPHA, in1=ym[:], op0=ALU.mult, op1=ALU.add), r=[b_x1[i]], w=[b_ym])
            ln_stats(nc, mk, ym, b_ym, st, mv, rs, nmr, b_s)
            A(lambda: nc.scalar.activation(out=xn[:], in_=ym[:], func=AF.Identity, bias=nmr[:, 0:1], scale=rs[:, 0:1]), r=[b_ym, b_s], w=[b_xn])
            V(lambda: nc.vector.tensor_tensor(out=xn[:], in0=xn[:], in1=R[1][:], op=ALU.mult), r=[bc], w=[b_xn])
            V(lambda: nc.vector.tensor_tensor(out=ot[i][:], in0=xn[:], in1=R[2][:], op=ALU.add), r=[b_xn, bc], w=[b_ot[i]])
            mk.dma("sp", xout[ts_, :], ot[i][:], reads=[b_ot[i]], is_output=True)
        mk.barrier()


def build_k3():
    nc = bass.Bass("TRN2", target_bir_lowering=False)
    dt = lambda n, s, k="ExternalInput", d=F32: nc.dram_tensor(n, s, d, kind=k).ap()
    ymixT = dt("ymixT", [D, TOK]); x = dt("x", [TOK, D]); w_out = dt("w_out", [D, D]); glu_w = dt("glu_w", [512, 512]); glu_b = dt("glu_b", [128, 4])
    rows = dt("rows", [8, 128, D]); wr = dt("wr", [D, 36]); br = dt("br", [128, 36])
    w1 = dt("w1", [NE, D, DE]); w3 = dt("w3", [NE, D, DE]); w2 = dt("w2", [NE, DE, D])
    cst = {"U": dt("U", [128, 128]), "ecap": dt("ecap", [128, NE]), "ident": dt("ident", [128, 128])}
    x1s = dt("x1s", [TOK, D], "Internal"); Xg = dt("Xg", [NE * CAP, D], "Internal", BF16); Yg = dt("Yg", [NE * CAP, D], "Internal")
    xout = dt("xout", [TOK, D], "ExternalOutput")
    with ExitStack() as ctx:
        mk = MK(nc, ctx)
        emit_k3(nc, mk, ymixT, x, w_out, glu_w, glu_b, rows, wr, br, w1, w3, w2, cst, x1s, Xg, Yg, xout)
        mk.finish("sp")
        print("k3 ops", mk.nops, "waits", mk.nwaits)
    return nc


def k3_host_inputs(prm, mod, l, b):
    rep = lambda v: np.ascontiguousarray(np.broadcast_to(v[None, :], (128, v.shape[0])))
    sh1, sc1, gt1, sh2, sc2, gt2 = [mod[l, b, i * D:(i + 1) * D] for i in range(6)]
    rows = np.stack([rep(gt1), rep(prm["ln_g"][l, 0]), rep(prm["ln_b"][l, 0]), rep(sc2), rep(sh2), rep(gt2), rep(prm["ln_g"][l, 1]), rep(prm["ln_b"][l, 1])])
    wr = np.ascontiguousarray(np.concatenate([prm["router_group_w"][l], prm["router_expert_w"][l]], axis=1))
    br = rep(np.concatenate([prm["router_group_b"][l], prm["router_expert_b"][l]]))
    d = {"rows": rows, "wr": wr, "br": br, "w_out": prm["w_out"][l], "glu_w": prm["s5_glu_w"][l],
         "glu_b": np.ascontiguousarray(prm["s5_glu_b"][l].reshape(4, 128).T),
         "w1": prm["moe_w1"][l], "w3": prm["moe_w3"][l], "w2": prm["moe_w2"][l]}
    d.update(k3_consts())
    return d


G_ = 512
RW_OFF = 3 * G_
RW_COLS = 3 * G_ + 96 + 96 + 128
ATT_OFF = RW_OFF + RW_COLS
S5_OFF = ATT_OFF + 512 + 2 * 128
SEQ = 8192
RG4 = [[0, 1, 2, 3], [4, 5, 6, 7]]
NMINE = 1472
MODC = 3072


def emit_k0f(nc, mk, cT, w, bb, modin):
    ct = mk.sb("k0_ct", [128, 16, 2]); sct = mk.sb("k0_sct", [128, 16, 2])
    wt = [mk.sb("k0_wt%d" % i, [128, 16, 512]) for i in range(2)]
    bt = mk.sb("k0_bt", [2, 2, MODC]); ot = mk.sb("k0_ot", [2, 2, MODC])
    P = [mk.ps("k0_P%d" % i, [2, 512]) for i in range(2)]
    b_c, b_b, b_o = Buf(), Buf(), Buf()
    b_w = [Buf(), Buf()]; b_p = [Buf(), Buf()]
    mk.dma("sp", ct[:], cT, writes=[b_c])
    mk.dma("sp", bt[:], bb.rearrange("l b c -> b l c"), writes=[b_b])
    mk.op("act", lambda: nc.scalar.activation(out=sct[:], in_=ct[:], func=AF.Silu), reads=[b_c], writes=[b_c])
    it = 0
    for l in range(2):
        for n in range(MODC // 512):
            i = it % 2
            it += 1
            mk.dma("sp", wt[i][:], w[l, :, n * 512:(n + 1) * 512].rearrange("(k p) c -> p k c", p=128), writes=[b_w[i]])
            for k in range(16):
                mk.op("pe", lambda: nc.tensor.matmul(P[i][:], lhsT=sct[:, k, :], rhs=wt[i][:, k, :], start=(k == 0), stop=(k == 15)),
                      reads=[b_c, b_w[i]], writes=[b_p[i]], skip_same=True)
            mk.op("dve", lambda: nc.vector.tensor_tensor(out=ot[:, l, n * 512:(n + 1) * 512], in0=P[i][:], in1=bt[:, l, n * 512:(n + 1) * 512], op=ALU.add),
                  reads=[b_b], writes=[b_p[i], b_o])
    mk.dma("sp", modin.rearrange("(o l) c -> o l c", o=1), ot[0:1, :, :], reads=[b_o])


def mod_row_load(nc, mk, dst, modall, l, chunk, bc):
    c0 = chunk * 2048
    done = 0
    while done < 2048:
        col = c0 + done
        r = col // MODC
        off = col % MODC
        n = min(2048 - done, MODC - off)
        src = modall[r * 2 + l:r * 2 + l + 1, off:off + n].partition_broadcast(128)
        mk.dma("sp", dst[:, done:done + n], src, writes=[bc])
        done += n


def emit_k1a(nc, mk, x, modall, l, ident_d, hTs):
    NT_ = TOK // 128
    xt = [mk.sb("a_xt%d" % i, [128, D]) for i in range(2)]
    xn = mk.sb("a_xn", [128, D]); h1 = mk.sb("a_h1", [128, D])
    hb = [mk.sb("a_hb%d" % i, [128, D], BF16) for i in range(2)]
    hT = mk.sb("a_hT", [128, 16, TOK], BF16)
    sct = mk.sb("a_sct", [128, D]); sht = mk.sb("a_sht", [128, D])
    idf = mk.sb("a_idf", [128, 128]); idb = mk.sb("a_idb", [128, 128], BF16)
    st = mk.sb("a_st", [128, 4, 6]); mv = mk.sb("a_mv", [128, 2]); rs = mk.sb("a_rs", [128, 1]); nmr = mk.sb("a_nmr", [128, 1])
    PT = [mk.ps("a_PT%d" % i, [128, 8, 128], BF16) for i in range(2)]
    b_x = [Buf(), Buf()]
    b_xn, b_h1, b_s, b_sc, b_sh, b_id, b_hT = Buf(), Buf(), Buf(), Buf(), Buf(), Buf(), Buf()
    b_hb = [Buf(), Buf()]; b_pt = [Buf(), Buf()]
    mod_row_load(nc, mk, sct, modall, l, 1, b_sc)
    mod_row_load(nc, mk, sht, modall, l, 0, b_sh)
    mk.dma("sp", idf[:], ident_d, writes=[b_id])
    mk.op("dve", lambda: nc.vector.tensor_copy(out=idb[:], in_=idf[:]), reads=[b_id], writes=[b_id])
    mk.op("pool", lambda: nc.gpsimd.tensor_scalar(out=sct[:], in0=sct[:], scalar1=1.0, scalar2=None, op0=ALU.add), reads=[b_sc], writes=[b_sc])
    for t in range(NT_):
        i = t % 2
        mk.dma("sp", xt[i][:], x[t * 128:(t + 1) * 128, :], writes=[b_x[i]])
        ln_stats(nc, mk, xt[i], b_x[i], st, mv, rs, nmr, b_s)
        mk.op("act", lambda: nc.scalar.activation(out=xn[:], in_=xt[i][:], func=AF.Identity, bias=nmr[:, 0:1], scale=rs[:, 0:1]),
              reads=[b_x[i], b_s], writes=[b_xn])
        mk.op("dve", lambda: nc.vector.tensor_tensor(out=h1[:], in0=xn[:], in1=sct[:], op=ALU.mult), reads=[b_xn, b_sc], writes=[b_h1])
        mk.op("pool", lambda: nc.gpsimd.tensor_tensor(out=hb[i][:], in0=h1[:], in1=sht[:], op=ALU.add), reads=[b_h1, b_sh], writes=[b_hb[i]])
        for half in range(2):
            for kk in range(8):
                k = half * 8 + kk
                mk.op("pe", lambda: nc.tensor.transpose(PT[half][:, kk, :], hb[i][:, k * 128:(k + 1) * 128], idb[:]),
                      reads=[b_hb[i], b_id], writes=[b_pt[half]], skip_same=True)
            if half == 0:
                mk.op("act", lambda: nc.scalar.copy(out=hT[:, 0:8, t * 128:(t + 1) * 128], in_=PT[half][:]), reads=[], writes=[b_pt[half], b_hT])
            else:
                mk.op("dve", lambda: nc.vector.tensor_copy(out=hT[:, 8:16, t * 128:(t + 1) * 128], in_=PT[half][:]), reads=[], writes=[b_pt[half], b_hT])
    mk.dma("sp", hTs.rearrange("(k p) t -> p k t", p=128), hT[:], reads=[b_hT])


def emit_k1b(nc, mk, hTg, wmine, pmine):
    NB_ = (NMINE + 127) // 128
    wt = mk.sb("b_wt", [128, 16, NB_ * 128], BF16); b_w = Buf()
    ht = [mk.sb("b_ht%d" % i, [128, 16, 512], BF16) for i in range(2)]; b_h = [Buf(), Buf()]
    ot = [mk.sb("b_ot%d" % i, [128, 512]) for i in range(4)]; b_o = [Buf() for _ in range(4)]
    PM = [mk.ps("b_PM%d" % i, [128, 512]) for i in range(4)]; b_pm = [Buf() for _ in range(4)]
    for j in range(NB_):
        c0 = j * 128
        cw = min(128, NMINE - c0)
        mk.dma("pool", wt[:, :, c0:c0 + cw], wmine[:, c0:c0 + cw].rearrange("(k p) c -> p k c", p=128), writes=[b_w])
    pi = 0
    for tc in range(SEQ // 512):
        i = tc % 2
        r = tc // 4
        t0 = (tc % 4) * 512
        src = hTg.rearrange("(c r h p) t -> r p c h t", c=8, r=4, h=2, p=128)[r]
        for c in range(8):
            mk.dma("sp", ht[i][:, 2 * c:2 * c + 2, :], src[:, c, :, t0:t0 + 512], writes=[b_h[i]])
        for j in range(NB_):
            c0 = j * 128
            cw = min(128, NMINE - c0)
            q = pi % 4
            pi += 1
            for k in range(16):
                mk.op("pe", lambda: nc.tensor.matmul(PM[q][0:cw, :], lhsT=wt[:, k, c0:c0 + cw], rhs=ht[i][:, k, :], start=(k == 0), stop=(k == 15)),
                      reads=[b_w, b_h[i]], writes=[b_pm[q]], skip_same=True)
            if q % 2 == 0:
                mk.op("act", lambda: nc.scalar.copy(out=ot[q][0:cw, :], in_=PM[q][0:cw, :]), reads=[], writes=[b_pm[q], b_o[q]])
            else:
                mk.op("dve", lambda: nc.vector.tensor_copy(out=ot[q][0:cw, :], in_=PM[q][0:cw, :]), reads=[], writes=[b_pm[q], b_o[q]])
            mk.dma("sp", pmine[c0:c0 + cw, tc * 512:(tc + 1) * 512], ot[q][0:cw, :], reads=[b_o[q]])


def build_fused():
    nc = bass.Bass("TRN2", target_bir_lowering=False)
    T = SEQ
    din = lambda n, s, d=F32: nc.dram_tensor(n, s, d, kind="ExternalInput").ap()
    scr = lambda n, s, d=F32: nc.dram_tensor(n, s, d).ap()
    x_in = din("x", [TOK, D]); cT = din("cT", [128, 16, 2]); w_ada = din("w_ada", [2, D, MODC]); bb = din("bb", [2, 2, MODC])
    wmine = din("wmine", [2, D, NMINE]); w_out = din("w_out", [2, D, D]); lnp = din("lnp", [2, 4, D])
    cw = din("cw", [2, 128, 3]); btab = din("btab", [2, 2, 128, 2, 256]); sinkt = din("sinkt", [2, 128, 2]); ident = din("ident", [128, 128])
    s5par = din("s5par", [2, 128, 4, 3]); s5bb = din("s5bb", [2, 128, 4, 2, 16]); s5cc = din("s5cc", [2, 128, 4, 2, 16]); s5d = din("s5d", [2, 128, 1]); s5iota = din("s5iota", [128, CS])
    par64 = din("par64", [2, 64, 2, 11]); par128 = din("par128", [2, 128, 3]); w2 = din("w2", [2, 96, 128]); a2 = din("a2", [2, 96, 128]); g2 = din("g2", [2, 128, 128]); gnt = din("gnt", [2, 64, 2, 2, 64])
    cst = {"mask1": din("mask1", [64, 512]), "mask3": din("mask3", [64, 256]), "seg": din("seg", [128, TB]), "ident": ident, "U": din("U", [128, 128]), "ecap": din("ecap", [128, NE])}
    glu_w = din("glu_w", [2, 512, 512]); glu_b = din("glu_b", [2, 128, 4]); wr = din("wr", [2, D, 36]); br = din("br", [2, 128, 36])
    w1 = din("w1", [2, NE, D, DE]); w3 = din("w3", [2, NE, D, DE]); w2m = din("w2m", [2, NE, DE, D])
    ymidx_d = din("ymidx", [128, 16, 2], I32)
    y_out = nc.dram_tensor("y", [TOK, D], F32, kind="ExternalOutput").ap()
    modin = scr("modin", [2, MODC]); modall = scr("modall", [8, MODC])
    hTs = scr("hTs", [D, TOK], BF16); hTg = scr("hTg", [4 * D, TOK], BF16)
    pmine = scr("pmine", [NMINE, T])
    yT16 = scr("yT16", [512, T], BF16); ymg = scr("ymg", [2048, T], BF16)
    x1s = scr("x1s", [TOK, D]); Xg = scr("Xg", [NE * CAP, D], BF16); Yg = scr("Yg", [NE * CAP, D]); xcur = scr("xcur", [TOK, D])
    ymg_rows = ymg.rearrange("r (tb t) -> (r tb) t", t=1024)
    with ExitStack() as ctx:
        mk = MK(nc, ctx)
        bD = Buf("dram")
        with mk.scope():
            emit_k0f(nc, mk, cT, w_ada, bb, modin)
        mk.collective("AllGather", RG4, modin, modall, reads=[bD], writes=[bD])
        mk.barrier()
        for l in range(2):
            xsrc = x_in if l == 0 else xcur
            xdst = xcur if l == 0 else y_out
            with mk.scope():
                emit_k1a(nc, mk, xsrc, modall, l, ident, hTs)
            for c in range(8):
                mk.collective("AllGather", RG4, hTs[c * 256:(c + 1) * 256, :], hTg[c * 1024:(c + 1) * 1024, :], reads=[bD], writes=[bD])
            mk.barrier()
            with mk.scope():
                emit_k1b(nc, mk, hTg, wmine[l], pmine)
            def ag(chunks):
                for c in chunks:
                    mk.collective("AllGather", RG4, yT16[c * 64:(c + 1) * 64, :], ymg[c * 256:(c + 1) * 256, :], reads=[], writes=[Buf()])
            with mk.scope():
                emit_conv(nc, mk, T, pmine[0:384, :], cw[l], yT16[0:128, :], odt=BF16)
            ag((0, 1))
            with mk.scope():
                emit_attn(nc, mk, T, pmine[1088:1344, :], btab[l], sinkt[l], ident, yT16[256:384, :], odt=BF16)
            ag((4, 5))
            with mk.scope():
                emit_s5(nc, mk, T, pmine[1344:1472, :], s5par[l], s5bb[l], s5cc[l], s5d[l], s5iota, ident, yT16[384:512, :], odt=BF16)
            ag((6, 7))
            with mk.scope():
                emit_rwkv(nc, mk, T, pmine[384:1088, :], par64[l], par128[l], w2[l], a2[l], g2[l], gnt[l], cst, yT16[128:256, :], odt=BF16)
            for c in (2, 3):
                mk.collective("AllGather", RG4, yT16[c * 64:(c + 1) * 64, :], ymg[c * 256:(c + 1) * 256, :], reads=[bD], writes=[bD])
            mk.barrier()
            with mk.scope():
                ymidx = mk.sb("ymidx_sb", [128, 16, 2], I32); b_idx = Buf()
                mk.dma("sp", ymidx[:], ymidx_d, writes=[b_idx])

                def ymload(hf, ymt, b_ymt):
                    for k in range(16):
                        mk.idma(ymt[:, k, :], ymg_rows, ymidx[:, k, hf:hf + 1], False, 2048 * 8 - 1, reads=[b_idx], writes=[b_ymt])

                def rowload(dst, ri, bcx, _l=l):
                    if ri in (1, 2, 6, 7):
                        j = {1: 0, 2: 1, 6: 2, 7: 3}[ri]
                        mk.dma("sp", dst[:], lnp[_l, j:j + 1, :].partition_broadcast(128), writes=[bcx])
                    else:
                        chunk = {0: 2, 3: 4, 4: 3, 5: 5}[ri]
                        mod_row_load(nc, mk, dst, modall, _l, chunk, bcx)

                emit_k3(nc, mk, None, xsrc, w_out[l], glu_w[l], glu_b[l], None, wr[l], br[l], w1[l], w3[l], w2m[l], cst, x1s, Xg, Yg, xdst,
                        ymload=ymload, rowload=rowload)
        mk.finish("sp")
        mk.barrier()
        print("fused ops", mk.nops, "waits", mk.nwaits)
    return nc


_NC_CACHE = {}


def _get(name, fn):
    if name not in _NC_CACHE:
        _NC_CACHE[name] = fn()
    return _NC_CACHE[name]


def _fused_inputs(prm, core):
    b, q = core // 4, core % 4
    eye = np.eye(128, dtype=np.float32)
    d = {}
    d["x"] = np.ascontiguousarray(prm["x"][b, q * TOK:(q + 1) * TOK])
    cb = prm["c"][b]
    d["cT"] = np.ascontiguousarray(np.stack([cb.reshape(16, 128).T, cb.reshape(16, 128).T], axis=-1))
    sl = slice(q * MODC, (q + 1) * MODC)
    d["w_ada"] = np.ascontiguousarray(prm["w_ada"][:, :, sl])
    d["bb"] = np.ascontiguousarray(np.broadcast_to(prm["b_ada"][:, None, sl], (2, 2, MODC)))
    kv = q // 2
    cols = np.concatenate([np.arange(128 * q, 128 * q + 128), np.arange(G_ + 128 * q, G_ + 128 * q + 128), np.arange(2 * G_ + 128 * q, 2 * G_ + 128 * q + 128),
                           RW_OFF + rwkv_rows(q),
                           np.arange(ATT_OFF + 128 * q, ATT_OFF + 128 * q + 128), np.arange(ATT_OFF + 512 + 64 * kv, ATT_OFF + 512 + 64 * kv + 64),
                           np.arange(ATT_OFF + 640 + 64 * kv, ATT_OFF + 640 + 64 * kv + 64),
                           np.arange(S5_OFF + 128 * q, S5_OFF + 128 * q + 128)])
    assert cols.shape[0] == NMINE
    d["wmine"] = np.ascontiguousarray(prm["w_in"][:, :, cols])
    d["w_out"] = prm["w_out"]
    d["lnp"] = np.ascontiguousarray(np.stack([np.stack([prm["ln_g"][l, 0], prm["ln_b"][l, 0], prm["ln_g"][l, 1], prm["ln_b"][l, 1]]) for l in range(2)]))
    d["cw"] = np.ascontiguousarray(np.stack([prm["conv_w"][l][:, 128 * q:128 * q + 128].T for l in range(2)]))
    tabs = [attn_tables(prm["rel_bias"], prm["attn_sinks"][l], q) for l in range(2)]
    d["btab"] = np.ascontiguousarray(np.stack([t[0] for t in tabs])); d["sinkt"] = np.ascontiguousarray(np.stack([t[1] for t in tabs]))
    d["ident"] = eye
    s5 = [s5_host_inputs(prm, l, q) for l in range(2)]
    for k_ in ("s5par", "s5bb", "s5cc", "s5d"):
        d[k_] = np.ascontiguousarray(np.stack([s[k_] for s in s5]))
    d["s5iota"] = s5[0]["s5iota"]
    rw = [rwkv_host_inputs(prm, l, q) for l in range(2)]
    for k_ in ("par64", "par128", "gnt", "w2", "a2", "g2"):
        d[k_] = np.ascontiguousarray(np.stack([r[k_] for r in rw]))
    for k_ in ("mask1", "mask3", "seg"):
        d[k_] = rw[0][k_]
    kc = k3_consts()
    d["U"] = kc["U"]; d["ecap"] = kc["ecap"]
    d["glu_w"] = prm["s5_glu_w"]
    d["glu_b"] = np.ascontiguousarray(np.stack([prm["s5_glu_b"][l].reshape(4, 128).T for l in range(2)]))
    d["wr"] = np.ascontiguousarray(np.stack([np.concatenate([prm["router_group_w"][l], prm["router_expert_w"][l]], axis=1) for l in range(2)]))
    d["br"] = np.ascontiguousarray(np.stack([np.broadcast_to(np.concatenate([prm["router_group_b"][l], prm["router_expert_b"][l]])[None, :], (128, 36)) for l in range(2)]))
    d["w1"] = prm["moe_w1"]; d["w3"] = prm["moe_w3"]; d["w2m"] = prm["moe_w2"]
    p_ = np.arange(128)[:, None, None]; k_i = np.arange(16)[None, :, None]; t_ = np.arange(2)[None, None, :]
    src_row = ((k_i // 4) * 2 + p_ // 64) * 256 + (k_i % 4) * 64 + p_ % 64
    d["ymidx"] = np.ascontiguousarray((src_row * 8 + q * 2 + t_).astype(np.int32))
    return d


def kernel(**inp):
    prm = {k: np.ascontiguousarray(np.asarray(v, dtype=np.float32)) for k, v in inp.items()}
    cores = list(range(8))
    in_maps = [_fused_inputs(prm, c) for c in cores]
    res = run_bass_kernel_spmd(_get("fused", build_fused), in_maps, core_ids=cores)
    out = np.stack([np.concatenate([res.results[b * 4 + q]["y"] for q in range(4)], axis=0) for b in range(2)])
    return out.astype(np.float32)
```

```python
import numpy as np
from contextlib import ExitStack
import concourse.bass as bass
import concourse.mybir as mybir
from concourse.bass_utils import run_bass_kernel_spmd

F32 = mybir.dt.float32
BF16 = mybir.dt.bfloat16
I32 = mybir.dt.int32
U32 = mybir.dt.uint32
AF = mybir.ActivationFunctionType
ALU = mybir.AluOpType
AX = mybir.AxisListType

EPOCH = 1 << 20


class Buf:
    __slots__ = ("name", "w", "r")

    def __init__(self, name=""):
        self.name = name
        self.w = None
        self.r = {}


class MK:
    def __init__(self, nc, ctx, n_dma_sems=24):
        self.nc = nc
        self.ctx = ctx
        self.eng = {"pe": nc.tensor, "dve": nc.vector, "act": nc.scalar,
                    "pool": nc.gpsimd, "sp": nc.sync}
        self.sem = {}
        self.cnt = {e: 0 for e in self.eng}
        self.known = {e: {} for e in self.eng}
        for e in self.eng:
            self.sem[e] = ctx.enter_context(nc.semaphore("s_" + e))
        self.dma_keys = []
        self.dma_val = {}
        for i in range(n_dma_sems):
            k = ("dma", i)
            self.sem[k] = ctx.enter_context(nc.semaphore("s_dma%d" % i))
            self.dma_keys.append(k)
            self.dma_val[k] = 0
        self.dma_rr = 0
        self.nwaits = 0
        self.nops = 0
        self.out_events = []

    gen = 0

    def sb(self, name, shape, dt=F32):
        return self.ctx.enter_context(self.nc.sbuf_tensor("%s_g%d" % (name, self.gen), list(shape), dt))

    def ps(self, name, shape, dt=F32):
        return self.ctx.enter_context(self.nc.psum_tensor("%s_g%d" % (name, self.gen), list(shape), dt))

    def _wait(self, E, ev):
        if ev is None:
            return
        key, val = ev
        if self.known[E].get(key, 0) >= val:
            return
        self.eng[E].wait_ge(self.sem[key], val)
        self.known[E][key] = val
        self.nwaits += 1

    def _deps(self, E, reads, writes, skip_same=False):
        for b in reads:
            if b.w is not None and not (skip_same and b.w[0] == E):
                self._wait(E, b.w)
        for b in writes:
            if b.w is not None and not (skip_same and b.w[0] == E):
                self._wait(E, b.w)
            for ev in b.r.values():
                if not (skip_same and ev[0] == E):
                    self._wait(E, ev)

    def _mark(self, ev, reads, writes):
        for b in reads:
            b.r[ev[0]] = ev
        for b in writes:
            b.w = ev
            b.r = {}

    def op(self, E, fn, reads=(), writes=(), skip_same=False):
        self._deps(E, reads, writes, skip_same)
        inst = fn()
        self.cnt[E] += 1
        inst.then_inc(self.sem[E], 1)
        ev = (E, self.cnt[E])
        self._mark(ev, reads, writes)
        self.nops += 1
        return ev

    def dma(self, Q, out, in_, reads=(), writes=(), is_output=False, **kw):
        self._deps(Q, reads, writes)
        k = self.dma_keys[self.dma_rr]
        self.dma_rr = (self.dma_rr + 1) % len(self.dma_keys)
        self._wait(Q, (k, self.dma_val[k]) if self.dma_val[k] else None)
        self.dma_val[k] += 16
        inst = self.eng[Q].dma_start(out=out, in_=in_, **kw)
        inst.then_inc(self.sem[k], 16)
        ev = (k, self.dma_val[k])
        self._mark(ev, reads, writes)
        if is_output:
            self.out_events.append(ev)
        self.nops += 1
        return ev

    def finish(self, E="sp"):
        for k in self.dma_keys:
            if self.dma_val[k]:
                self._wait(E, (k, self.dma_val[k]))


def _idma(self, out, in_, idx_ap, scatter, bound, reads=(), writes=(), is_output=False):
    Q = "pool"
    self._deps(Q, reads, writes)
    k = self.dma_keys[self.dma_rr]
    self.dma_rr = (self.dma_rr + 1) % len(self.dma_keys)
    self._wait(Q, (k, self.dma_val[k]) if self.dma_val[k] else None)
    self.dma_val[k] += 16
    off = bass.IndirectOffsetOnAxis(ap=idx_ap, axis=0)
    if not hasattr(self, "_bregs"):
        self._bregs = {}
    if bound not in self._bregs:
        self._bregs[bound] = self.nc.gpsimd.to_reg(bound)
    bound = self._bregs[bound]
    if scatter:
        inst = self.nc.gpsimd.indirect_dma_start(out=out, out_offset=off, in_=in_, in_offset=None, bounds_check=bound, oob_is_err=False)
    else:
        inst = self.nc.gpsimd.indirect_dma_start(out=out, out_offset=None, in_=in_, in_offset=off, bounds_check=bound, oob_is_err=False)
    inst.then_inc(self.sem[k], 16)
    ev = (k, self.dma_val[k])
    self._mark(ev, reads, writes)
    if is_output:
        self.out_events.append(ev)
    self.nops += 1
    return ev


MK.idma = _idma


def _barrier(self):
    for E in self.eng:
        for F in self.eng:
            if self.cnt[F]:
                self._wait(E, (F, self.cnt[F]))
        for k in self.dma_keys:
            if self.dma_val[k]:
                self._wait(E, (k, self.dma_val[k]))
        if getattr(self, "cc_val", 0):
            self._wait(E, ("cc", self.cc_val))


MK.barrier = _barrier


from contextlib import contextmanager


@contextmanager
def _scope(self):
    old = self.ctx
    self.gen += 1
    with ExitStack() as s:
        self.ctx = s
        yield
        self.barrier()
    self.ctx = old


MK.scope = _scope


def _collective(self, kind, rg, in_ap, out_ap, reads=(), writes=()):
    Q = "pool"
    if "cc" not in self.sem:
        self.sem["cc"] = self.ctx.enter_context(self.nc.semaphore("s_cc"))
        self.cc_val = 0
    self._deps(Q, reads, writes)
    self.cc_val += 1
    inst = self.nc.gpsimd.collective_compute(kind, ALU.bypass, replica_groups=rg, ins=[in_ap.opt()], outs=[out_ap.opt()])
    inst.then_inc(self.sem["cc"], 1)
    ev = ("cc", self.cc_val)
    self._mark(ev, reads, writes)
    self.nops += 1
    return ev


MK.collective = _collective


import numpy as np
from contextlib import ExitStack

D = 2048
NIN = 4672
TOK = 2048


def build_k0():
    nc = bass.Bass("TRN2", target_bir_lowering=False)
    NCOL = 1536
    cT = nc.dram_tensor("cT", [128, 16, 2], F32, kind="ExternalInput").ap()
    w = nc.dram_tensor("w", [2, D, NCOL], F32, kind="ExternalInput").ap()
    bb = nc.dram_tensor("bb", [2, 2, NCOL], F32, kind="ExternalInput").ap()
    out = nc.dram_tensor("mod", [2, 2, NCOL], F32, kind="ExternalOutput").ap()
    with ExitStack() as ctx:
        mk = MK(nc, ctx)
        ct = mk.sb("ct", [128, 16, 2])
        sct = mk.sb("sct", [128, 16, 2])
        wt = [mk.sb("wt%d" % i, [128, 16, 512]) for i in range(2)]
        bt = mk.sb("bt", [2, 2, NCOL])
        ot = mk.sb("ot", [2, 2, NCOL])
        P = [mk.ps("P%d" % i, [2, 512]) for i in range(2)]
        b_c, b_b, b_o = Buf(), Buf(), Buf()
        b_w = [Buf(), Buf()]
        b_p = [Buf(), Buf()]
        mk.dma("sp", ct[:], cT, writes=[b_c])
        mk.dma("sp", bt[:], bb.rearrange("l b c -> b l c"), writes=[b_b])
        mk.op("act", lambda: nc.scalar.activation(out=sct[:], in_=ct[:], func=AF.Silu), reads=[b_c], writes=[b_c])
        it = 0
        for l in range(2):
            for n in range(3):
                i = it % 2
                it += 1
                mk.dma("sp", wt[i][:], w[l, :, n * 512:(n + 1) * 512].rearrange("(k p) c -> p k c", p=128), writes=[b_w[i]])
                for k in range(16):
                    mk.op("pe", lambda: nc.tensor.matmul(P[i][:], lhsT=sct[:, k, :], rhs=wt[i][:, k, :], start=(k == 0), stop=(k == 15)),
                          reads=[b_c, b_w[i]], writes=[b_p[i]], skip_same=True)
                mk.op("dve", lambda: nc.vector.tensor_tensor(out=ot[:, l, n * 512:(n + 1) * 512], in0=P[i][:], in1=bt[:, l, n * 512:(n + 1) * 512], op=ALU.add),
                      reads=[b_p[i], b_b], writes=[b_o])
        mk.dma("sp", out.rearrange("l b c -> b l c"), ot[:], reads=[b_o], is_output=True)
        mk.finish("sp")
    return nc


def build_k1():
    nc = bass.Bass("TRN2", target_bir_lowering=False)
    x = nc.dram_tensor("x", [TOK, D], F32, kind="ExternalInput").ap()
    sc = nc.dram_tensor("sc", [128, D], F32, kind="ExternalInput").ap()
    sh = nc.dram_tensor("sh", [128, D], F32, kind="ExternalInput").ap()
    w_in = nc.dram_tensor("w_in", [D, NIN], F32, kind="ExternalInput").ap()
    ident = nc.dram_tensor("ident", [128, 128], F32, kind="ExternalInput").ap()
    pT = nc.dram_tensor("pT", [NIN, TOK], F32, kind="ExternalOutput").ap()
    with ExitStack() as ctx:
        mk = MK(nc, ctx)
        emit_k1(nc, mk, x, sc, sh, w_in, ident, pT)
        mk.finish("sp")
        print("k1 ops", mk.nops, "waits", mk.nwaits)
    return nc


def ln_stats(nc, mk, xt, bx, st, mv, rs, nmr, bs, eps=1e-5):
    for c in range(4):
        mk.op("dve", lambda: nc.vector.bn_stats(out=st[:, c, :], in_=xt[:, c * 512:(c + 1) * 512]), reads=[bx], writes=[bs])
    mk.op("dve", lambda: nc.vector.bn_aggr(out=mv[:], in_=st[:].rearrange("p a b -> p (a b)")), reads=[bs], writes=[bs])
    mk.op("act", lambda: nc.scalar.activation(out=rs[:], in_=mv[:, 1:2], func=AF.Sqrt, bias=eps, scale=1.0), reads=[bs], writes=[bs])
    mk.op("dve", lambda: nc.vector.reciprocal(out=rs[:], in_=rs[:]), reads=[bs], writes=[bs])
    mk.op("dve", lambda: nc.vector.tensor_scalar(out=nmr[:], in0=mv[:, 0:1], scalar1=rs[:, 0:1], scalar2=-1.0, op0=ALU.mult, op1=ALU.mult),
          reads=[bs], writes=[bs])


def emit_k1(nc, mk, x, sc, sh, w_in, ident, pT):
    NT = TOK // 128
    xt = [mk.sb("xt%d" % i, [128, D]) for i in range(2)]
    xn = mk.sb("xn", [128, D])
    h1 = mk.sb("h1", [128, D])
    hb = [mk.sb("hb%d" % i, [128, D], BF16) for i in range(2)]
    hT = mk.sb("hT", [128, 16, TOK], BF16)
    sct = mk.sb("sct", [128, D])
    sht = mk.sb("sht", [128, D])
    idf = mk.sb("idf", [128, 128])
    idb = mk.sb("idb", [128, 128], BF16)
    st = mk.sb("st", [128, 4, 6])
    mv = mk.sb("mv", [128, 2])
    rs = mk.sb("rs", [128, 1])
    nmr = mk.sb("nmr", [128, 1])
    wt = [mk.sb("wt%d" % i, [128, 16, 128], BF16) for i in range(2)]
    ot = [mk.sb("ot%d" % i, [128, TOK]) for i in range(2)]
    PT = [mk.ps("PT%d" % i, [128, 8, 128], BF16) for i in range(2)]
    PM = [mk.ps("PM%d" % i, [128, 512]) for i in range(4)]
    b_x = [Buf(), Buf()]
    b_xn, b_h1, b_s, b_sc, b_sh, b_id, b_hT = Buf(), Buf(), Buf(), Buf(), Buf(), Buf(), Buf()
    b_hb = [Buf(), Buf()]
    b_pt = [Buf(), Buf()]
    b_pm = [Buf() for _ in range(4)]
    b_w = [Buf(), Buf()]
    b_o = [Buf(), Buf()]

    mk.dma("sp", sct[:], sc, writes=[b_sc])
    mk.dma("sp", sht[:], sh, writes=[b_sh])
    mk.dma("sp", idf[:], ident, writes=[b_id])
    mk.op("dve", lambda: nc.vector.tensor_copy(out=idb[:], in_=idf[:]), reads=[b_id], writes=[b_id])
    mk.op("pool", lambda: nc.gpsimd.tensor_scalar(out=sct[:], in0=sct[:], scalar1=1.0, scalar2=None, op0=ALU.add), reads=[b_sc], writes=[b_sc])

    NCB = (NIN + 127) // 128

    def load_w(cb):
        j = cb % 2
        c0 = cb * 128
        cw = min(128, NIN - c0)
        mk.dma("pool", wt[j][:, :, 0:cw], w_in[:, c0:c0 + cw].rearrange("(k p) c -> p k c", p=128), writes=[b_w[j]])

    load_w(0)
    load_w(1)
    for t in range(NT):
        i = t % 2
        mk.dma("sp", xt[i][:], x[t * 128:(t + 1) * 128, :], writes=[b_x[i]])
        ln_stats(nc, mk, xt[i], b_x[i], st, mv, rs, nmr, b_s)
        mk.op("act", lambda: nc.scalar.activation(out=xn[:], in_=xt[i][:], func=AF.Identity, bias=nmr[:, 0:1], scale=rs[:, 0:1]),
              reads=[b_x[i], b_s], writes=[b_xn])
        mk.op("dve", lambda: nc.vector.tensor_tensor(out=h1[:], in0=xn[:], in1=sct[:], op=ALU.mult), reads=[b_xn, b_sc], writes=[b_h1])
        mk.op("pool", lambda: nc.gpsimd.tensor_tensor(out=hb[i][:], in0=h1[:], in1=sht[:], op=ALU.add), reads=[b_h1, b_sh], writes=[b_hb[i]])
        for half in range(2):
            for kk in range(8):
                k = half * 8 + kk
                mk.op("pe", lambda: nc.tensor.transpose(PT[half][:, kk, :], hb[i][:, k * 128:(k + 1) * 128], idb[:]),
                      reads=[b_hb[i], b_id], writes=[b_pt[half]], skip_same=True)
            eng = "act" if half == 0 else "dve"
            if eng == "act":
                mk.op("act", lambda: nc.scalar.copy(out=hT[:, half * 8:(half + 1) * 8, t * 128:(t + 1) * 128], in_=PT[half][:]),
                      reads=[b_pt[half]], writes=[b_hT])
            else:
                mk.op("dve", lambda: nc.vector.tensor_copy(out=hT[:, half * 8:(half + 1) * 8, t * 128:(t + 1) * 128], in_=PT[half][:]),
                      reads=[b_pt[half]], writes=[b_hT])
    pi = 0
    for cb in range(NCB):
        j = cb % 2
        c0 = cb * 128
        cw = min(128, NIN - c0)
        for tc in range(TOK // 512):
            q = pi % 4
            pi += 1
            for k in range(16):
                mk.op("pe", lambda: nc.tensor.matmul(PM[q][0:cw, :], lhsT=wt[j][:, k, 0:cw], rhs=hT[:, k, tc * 512:(tc + 1) * 512],
                                                     start=(k == 0), stop=(k == 15)),
                      reads=[b_w[j], b_hT], writes=[b_pm[q]], skip_same=True)
            if tc % 2 == 0:
                mk.op("act", lambda: nc.scalar.copy(out=ot[j][0:cw, tc * 512:(tc + 1) * 512], in_=PM[q][0:cw, :]), reads=[b_pm[q]], writes=[b_o[j]])
            else:
                mk.op("dve", lambda: nc.vector.tensor_copy(out=ot[j][0:cw, tc * 512:(tc + 1) * 512], in_=PM[q][0:cw, :]), reads=[b_pm[q]], writes=[b_o[j]])
        mk.dma("sp", pT[c0:c0 + cw, :], ot[j][0:cw, :], reads=[b_o[j]], is_output=True)
        if cb + 2 < NCB:
            load_w(cb + 2)


import numpy as np
from contextlib import ExitStack

C = 64
TB = 512
NCH = TB // C


def rwkv_consts():
    s = np.arange(64)[:, None]
    t = np.arange(64)[None, :]
    m_su = (s < t).astype(np.float32)
    m_ui = (s <= t).astype(np.float32)
    m1 = np.concatenate([m_su, m_ui], axis=1)
    mask1 = np.tile(m1, (1, 4))
    m_sl = (t < s).astype(np.float32)
    mask3 = np.tile(m_sl, (1, 4))
    seg = np.ones((128, TB), np.float32)
    seg[:, ::C] = 0.0
    return {"mask1": mask1, "mask3": mask3, "seg": seg, "ident": np.eye(128, dtype=np.float32)}


def emit_rwkv(nc, mk, T, rwin, par64, par128, w2, a2, g2, gnt, cst, yT, odt=F32):
    import os
    LVL = int(os.environ.get("RW_LVL", "9"))
    NB = T // TB
    V = lambda fn, r=(), w=(): mk.op("dve", fn, r, w)
    A = lambda fn, r=(), w=(): mk.op("act", fn, r, w)
    G = lambda fn, r=(), w=(): mk.op("pool", fn, r, w)
    PE = lambda fn, r=(), w=(), ss=True: mk.op("pe", fn, r, w, skip_same=ss)
    import os
    F32R = mybir.dt.float32r
    USE_R = os.environ.get("RW_F32R", "1") == "1"
    RR = (lambda a: a.bitcast(F32R)) if USE_R else (lambda a: a)
    sb = mk.sb
    p64 = sb("rw_p64", [64, 2, 11]); p128 = sb("rw_p128", [128, 3])
    w2t = sb("rw_w2", [96, 128]); a2t = sb("rw_a2", [96, 128]); g2t = sb("rw_g2", [128, 128])
    gn = sb("rw_gn", [64, 2, 2, 64])
    mask1 = sb("rw_mask1", [64, 512]); mask3 = sb("rw_mask3", [64, 256]); seg = sb("rw_seg", [128, TB])
    ident = sb("rw_ident", [128, 128])
    ones64 = sb("rw_ones", [64, 64])
    bc = Buf("const")
    for dst, src in ((p64, par64), (p128, par128), (w2t, w2), (a2t, a2), (g2t, g2), (gn, gnt),
                     (mask1, cst["mask1"]), (mask3, cst["mask3"]), (seg, cst["seg"]), (ident, cst["ident"])):
        mk.dma("sp", dst[:], src, writes=[bc])
    V(lambda: nc.vector.memset(ones64[:], 1.0), w=[bc])
    raw = {}
    for nm in ("r0", "k0", "v0", "r1", "k1", "v1"):
        raw[nm] = sb("rw_raw_" + nm, [64, TB + 1])
    raw["w"] = sb("rw_raw_w", [96, TB + 1]); raw["a"] = sb("rw_raw_a", [96, TB + 1]); raw["g"] = sb("rw_raw_g", [128, TB + 1])
    b_raw = Buf("raw")
    tmp = sb("rw_tmp", [128, TB]); b_tmp = Buf("tmp")
    ws = sb("rw_ws", [96, TB]); as_ = sb("rw_as", [96, TB]); gs = sb("rw_gs", [128, TB]); b_lo = Buf("lo")
    gate = [sb("rw_gate%d" % i_, [128, TB]) for i_ in range(2)]; b_gate = [Buf("gate0"), Buf("gate1")]
    H = []
    for h in range(2):
        d = {}
        for nm in ("rs", "ks", "vs", "lw", "asg", "kkn", "kp", "bv", "cum", "e1", "e2", "BT", "KT", "BH", "KH", "rkr"):
            d[nm] = sb("rw_%s%d" % (nm, h), [64, TB])
        d["AR"] = sb("rw_AR%d" % h, [64, NCH, 128])
        d["cC"] = sb("rw_cC%d" % h, [64, NCH]); d["gC"] = sb("rw_gC%d" % h, [64, NCH])
        d["b"] = Buf("H%d" % h)
        d["bo"] = [Buf("Ho%d_0" % h), Buf("Ho%d_1" % h)]
        d["Vt"] = sb("rw_Vt%d" % h, [64, NCH, 64]); d["BHt"] = sb("rw_BHt%d" % h, [64, NCH, 64]); d["KHt"] = sb("rw_KHt%d" % h, [64, NCH, 64])
        d["bt"] = [Buf("Ht%d_0" % h), Buf("Ht%d_1" % h)]
        for nm_, shp_ in (("AR", [64, NCH, 128]), ("BT", [64, TB]), ("KT", [64, TB]), ("gC", [64, NCH]), ("Vt", [64, NCH, 64]), ("BHt", [64, NCH, 64]), ("KHt", [64, NCH, 64])):
            d[nm_] = [d[nm_], sb("rw_%s%d_b" % (nm_, h), shp_)]
        d["NG"] = sb("rw_NG%d" % h, [64, NCH, 128]); d["LG"] = sb("rw_LG%d" % h, [64, NCH, 128])
        d["L"] = sb("rw_L%d" % h, [64, NCH, 64]); d["bA"] = Buf("A%d" % h)
        d["P"] = [sb("rw_P%d_%d" % (h, i), [64, NCH, 64]) for i in range(2)]
        d["PT"] = [sb("rw_PT%d_%d" % (h, i), [64, NCH, 64]) for i in range(2)]
        d["ST"] = [sb("rw_ST%d_%d" % (h, i), [64, NCH, 64]) for i in range(2)]
        d["bD"] = Buf("D%d" % h)
        d["M"] = sb("rw_M%d" % h, [64, 64]); d["bM"] = Buf("M%d" % h)
        d["X1"] = sb("rw_X1%d" % h, [64, 64]); d["U"] = sb("rw_U%d" % h, [64, 64]); d["bX"] = Buf("X%d" % h); d["bU"] = Buf("U%d" % h)
        H.append(d)
    Yb = sb("rw_Yb", [64, NCH, 2, 64]); b_Y = Buf("Y")
    Ysq = sb("rw_Ysq", [64, NCH, 2, 64])
    st1 = sb("rw_st1", [64, NCH * 2]); st2 = sb("rw_st2", [64, NCH * 2]); st3 = sb("rw_st3", [64, NCH * 2]); b_st = Buf("st")
    sbon = [sb("rw_sbon%d" % i_, [64, NCH, 2]) for i_ in range(2)]; b_sb = [Buf("sbon0"), Buf("sbon1")]
    yo = sb("rw_yo", [128, TB], odt); b_yo = Buf("yo")
    ps_lo = mk.ps("rw_ps_lo", [128, 512]); b_pl = Buf()
    ps_tr = mk.ps("rw_ps_tr", [128, 512]); b_ptr = Buf()
    ps_a1 = mk.ps("rw_ps_a1", [64, 512]); b_pa1 = Buf()
    ps_a2 = mk.ps("rw_ps_a2", [64, 512]); b_pa2 = Buf()
    ps_a3f = mk.ps("rw_ps_a3", [128, 512]); ps_a3 = ps_a3f[0:64, :]; b_pa3 = Buf()
    ps_d = mk.ps("rw_ps_d", [64, 512]); b_pd = Buf()
    ps_d2 = ps_a3; b_pd2 = b_pa3
    ps_sh = [mk.ps("rw_ps_s%d" % h, [64, 512]) for h in range(2)]
    b_psh = [Buf(), Buf()]

    for h in range(2):
        V(lambda: nc.vector.tensor_scalar(out=RR(H[h]["M"][:]), in0=ident[0:64, 0:64], scalar1=0.0, scalar2=None, op0=ALU.mult), r=[bc], w=[H[h]["bM"]])

    rows = {"r0": 0, "r1": 64, "k0": 128, "k1": 192, "v0": 256, "v1": 320, "w": 384, "a": 480, "g": 576}
    nrow = {"r0": 64, "r1": 64, "k0": 64, "k1": 64, "v0": 64, "v1": 64, "w": 96, "a": 96, "g": 128}

    def stage1(blk):
        t0 = blk * TB
        par = blk % 2
        for nm in rows:
            r0, n = rows[nm], nrow[nm]
            if blk == 0:
                V(lambda: nc.vector.memset(raw[nm][0:n, 0:1], 0.0), w=[b_raw])
                yield
                mk.dma("sp", raw[nm][0:n, 1:TB + 1], rwin[r0:r0 + n, 0:TB], writes=[b_raw])
                yield
            else:
                mk.dma("sp", raw[nm][0:n, :], rwin[r0:r0 + n, t0 - 1:t0 + TB], writes=[b_raw])
                yield

        def shift(dst, src, n, mu_ap, bdst):
            V(lambda: nc.vector.tensor_tensor(out=tmp[0:n, :], in0=src[0:n, 0:TB], in1=src[0:n, 1:TB + 1], op=ALU.subtract), r=[b_raw], w=[b_tmp])
            V(lambda: nc.vector.scalar_tensor_tensor(out=dst[0:n, :], in0=tmp[0:n, :], scalar=mu_ap, in1=src[0:n, 1:TB + 1], op0=ALU.mult, op1=ALU.add),
              r=[b_tmp, b_raw, bc], w=[bdst])

        shift(ws, raw["w"], 96, p128[0:96, 0:1], b_lo)
        yield
        shift(as_, raw["a"], 96, p128[0:96, 1:2], b_lo)
        yield
        shift(gs, raw["g"], 128, p128[:, 2:3], b_lo)
        yield
        A(lambda: nc.scalar.activation(out=ws[:], in_=ws[:], func=AF.Tanh), r=[b_lo], w=[b_lo])
        yield
        A(lambda: nc.scalar.activation(out=gs[:], in_=gs[:], func=AF.Sigmoid), r=[b_lo], w=[b_lo])
        yield
        PE(lambda: nc.tensor.matmul(ps_lo[:, :], lhsT=g2t[:, :], rhs=gs[:, :], start=True, stop=True), r=[bc, b_lo], w=[b_pl])
        yield
        A(lambda: nc.scalar.copy(out=gate[par][:], in_=ps_lo[:, :]), r=[b_pl], w=[b_gate[par]])
        yield
        for h in range(2):
            d = H[h]; b = d["b"]; bo = d["bo"][par]
            hs = slice(64 * h, 64 * h + 64)
            shift(d["rs"], raw["r%d" % h], 64, p64[:, h, 0:1], b)
            yield
            shift(d["ks"], raw["k%d" % h], 64, p64[:, h, 1:2], b)
            yield
            shift(d["vs"], raw["v%d" % h], 64, p64[:, h, 2:3], b)
            yield
            PE(lambda: nc.tensor.matmul(ps_lo[0:64, :], lhsT=w2t[:, hs], rhs=ws[:, :], start=True, stop=True), r=[bc, b_lo], w=[b_pl])
            yield
            A(lambda: nc.scalar.activation(out=d["lw"][:], in_=ps_lo[0:64, :], func=AF.Sigmoid, bias=p64[:, h, 3:4], scale=1.0), r=[b_pl, bc], w=[b])
            yield
            V(lambda: nc.vector.tensor_scalar(out=d["lw"][:], in0=d["lw"][:], scalar1=-0.6065306597126334, scalar2=None, op0=ALU.mult), r=[b], w=[b])
            yield
            PE(lambda: nc.tensor.matmul(ps_lo[0:64, :], lhsT=a2t[:, hs], rhs=as_[:, :], start=True, stop=True), r=[bc, b_lo], w=[b_pl])
            yield
            A(lambda: nc.scalar.activation(out=d["asg"][:], in_=ps_lo[0:64, :], func=AF.Sigmoid, bias=p64[:, h, 4:5], scale=1.0), r=[b_pl, bc], w=[b])
            yield
            V(lambda: nc.vector.tensor_scalar(out=d["kkn"][:], in0=d["ks"][:], scalar1=p64[:, h, 5:6], scalar2=None, op0=ALU.mult), r=[b, bc], w=[b])
            yield
            A(lambda: nc.scalar.activation(out=tmp[0:64, :], in_=d["kkn"][:], func=AF.Square), r=[b], w=[b_tmp])
            yield
            PE(lambda: nc.tensor.matmul(ps_lo[0:64, :], lhsT=ones64[:, :], rhs=tmp[0:64, :], start=True, stop=True), r=[bc, b_tmp], w=[b_pl])
            yield
            A(lambda: nc.scalar.activation(out=tmp[0:64, :], in_=ps_lo[0:64, :], func=AF.Sqrt), r=[b_pl], w=[b_tmp])
            yield
            V(lambda: nc.vector.tensor_scalar(out=tmp[0:64, :], in0=tmp[0:64, :], scalar1=1e-12, scalar2=None, op0=ALU.max), r=[b_tmp], w=[b_tmp])
            yield
            V(lambda: nc.vector.reciprocal(out=tmp[0:64, :], in_=tmp[0:64, :]), r=[b_tmp], w=[b_tmp])
            yield
            V(lambda: nc.vector.tensor_tensor(out=d["kkn"][:], in0=d["kkn"][:], in1=tmp[0:64, :], op=ALU.mult), r=[b, b_tmp], w=[b])
            yield
            V(lambda: nc.vector.tensor_scalar(out=tmp[0:64, :], in0=d["asg"][:], scalar1=-1.0, scalar2=p64[:, h, 6:7], op0=ALU.add, op1=ALU.mult), r=[b, bc], w=[b_tmp])
            yield
            V(lambda: nc.vector.scalar_tensor_tensor(out=d["kp"][:], in0=tmp[0:64, :], scalar=1.0, in1=d["ks"][:], op0=ALU.add, op1=ALU.mult), r=[b_tmp, b], w=[b])
            yield
            V(lambda: nc.vector.tensor_tensor(out=d["bv"][:], in0=d["kkn"][:], in1=d["asg"][:], op=ALU.mult), r=[b], w=[b])
            yield
            V(lambda: nc.vector.scalar_tensor_tensor(out=d["rkr"][:], in0=d["rs"][:], scalar=p64[:, h, 7:8], in1=d["kp"][:], op0=ALU.mult, op1=ALU.mult), r=[b, bc], w=[b])
            yield
            V(lambda: nc.vector.tensor_tensor_scan(out=d["cum"][:], data0=seg[0:64, :], data1=d["lw"][:], initial=0.0, op0=ALU.mult, op1=ALU.add), r=[b, bc], w=[b])
            yield
            cum3 = d["cum"][:].rearrange("p (c t) -> p c t", t=C)
            V(lambda: nc.vector.tensor_copy(out=d["cC"][:], in_=cum3[:, :, C - 1]), r=[b], w=[b])
            yield
            A(lambda: nc.scalar.activation(out=d["gC"][par][:], in_=d["cC"][:], func=AF.Exp), r=[b], w=[bo])
            yield
            A(lambda: nc.scalar.activation(out=d["e1"][:], in_=d["cum"][:], func=AF.Exp), r=[b], w=[b])
            yield
            A(lambda: nc.scalar.activation(out=d["e2"][:], in_=d["cum"][:], func=AF.Exp, scale=-1.0), r=[b], w=[b])
            yield
            AR = d["AR"][par]
            V(lambda: nc.vector.tensor_tensor(out=RR(AR[:, :, 64:128]), in0=d["rs"][:].rearrange("p (c t) -> p c t", t=C),
                                              in1=d["e1"][:].rearrange("p (c t) -> p c t", t=C), op=ALU.mult), r=[b], w=[bo])
            yield
            V(lambda: nc.vector.tensor_tensor(out=RR(d["BT"][par][:]), in0=d["bv"][:], in1=d["e2"][:], op=ALU.mult), r=[b], w=[bo])
            yield
            V(lambda: nc.vector.tensor_tensor(out=RR(d["KT"][par][:]), in0=d["kp"][:], in1=d["e2"][:], op=ALU.mult), r=[b], w=[bo])
            yield
            V(lambda: nc.vector.tensor_tensor(out=tmp[0:64, :], in0=d["cum"][:], in1=d["lw"][:], op=ALU.subtract), r=[b], w=[b_tmp])
            yield
            A(lambda: nc.scalar.activation(out=tmp[0:64, :], in_=tmp[0:64, :], func=AF.Exp), r=[b_tmp], w=[b_tmp])
            yield
            V(lambda: nc.vector.scalar_tensor_tensor(out=RR(AR[:, :, 0:64]), in0=d["kkn"][:].rearrange("p (c t) -> p c t", t=C), scalar=-1.0,
                                                     in1=tmp[0:64, :].rearrange("p (c t) -> p c t", t=C), op0=ALU.mult, op1=ALU.mult), r=[b, b_tmp], w=[bo])
            yield
            V(lambda: nc.vector.tensor_tensor(out=tmp[0:64, :].rearrange("p (c t) -> p c t", t=C), in0=d["cC"][:].unsqueeze(2).to_broadcast([64, NCH, C]),
                                              in1=cum3, op=ALU.subtract), r=[b], w=[b_tmp])
            yield
            A(lambda: nc.scalar.activation(out=tmp[0:64, :], in_=tmp[0:64, :], func=AF.Exp), r=[b_tmp], w=[b_tmp])
            yield
            V(lambda: nc.vector.tensor_tensor(out=d["BH"][:], in0=d["bv"][:], in1=tmp[0:64, :], op=ALU.mult), r=[b, b_tmp], w=[b])
            yield
            V(lambda: nc.vector.tensor_tensor(out=d["KH"][:], in0=d["kp"][:], in1=tmp[0:64, :], op=ALU.mult), r=[b, b_tmp], w=[b])
            yield
            for src, dstn in (("vs", "Vt"), ("BH", "BHt"), ("KH", "KHt")):
                for c in range(NCH):
                    PE(lambda: nc.tensor.transpose(ps_tr[0:64, c * 64:(c + 1) * 64], d[src][:, c * C:(c + 1) * C], ident[0:64, 0:64]), r=[b, bc], w=[b_ptr])
                    yield
                A(lambda: nc.scalar.copy(out=RR(d[dstn][par][:].rearrange("p c k -> p (c k)")), in_=ps_tr[0:64, :]), r=[b_ptr], w=[d["bt"][par]])
                yield
            for c in range(NCH):
                PE(lambda: nc.tensor.matmul(ps_tr[0:64, 2 * c:2 * c + 2], lhsT=d["rkr"][:, c * C:(c + 1) * C], rhs=ones64[:, 0:2], start=True, stop=True), r=[b, bc], w=[b_ptr])
                yield
            V(lambda: nc.vector.tensor_copy(out=sbon[par][:, :, h], in_=ps_tr[0:64, 0:2 * NCH:2]), r=[b_ptr], w=[b_sb[par]])
            yield

    def rest(blk, tick):
        t0 = blk * TB
        par = blk % 2
        for h in range(2):
            d = H[h]; b = d["bo"][par]; AR = d["AR"][par]
            tick()
            for half in range(2):
                for cc in range(4):
                    c = half * 4 + cc
                    PE(lambda: nc.tensor.matmul(ps_a1[:, cc * 128:(cc + 1) * 128], lhsT=RR(d["BT"][par][:, c * C:(c + 1) * C]), rhs=RR(AR[:, c, :]), start=True, stop=True), r=[b], w=[b_pa1])
                    PE(lambda: nc.tensor.matmul(ps_a2[:, cc * 128:(cc + 1) * 128], lhsT=RR(d["KT"][par][:, c * C:(c + 1) * C]), rhs=RR(AR[:, c, :]), start=True, stop=True), r=[b], w=[b_pa2])
                    PE(lambda: nc.tensor.matmul(ps_a3[:, cc * 64:(cc + 1) * 64], lhsT=RR(AR[:, c, 0:64]), rhs=RR(d["BT"][par][:, c * C:(c + 1) * C]), start=True, stop=True), r=[b], w=[b_pa3])
                V(lambda: nc.vector.tensor_tensor(out=RR(d["NG"][:, half * 4:half * 4 + 4, :].rearrange("p c k -> p (c k)")), in0=ps_a1[:, :], in1=mask1[:, :], op=ALU.mult), r=[b_pa1, bc], w=[d["bA"]])
                V(lambda: nc.vector.tensor_tensor(out=RR(d["LG"][:, half * 4:half * 4 + 4, :].rearrange("p c k -> p (c k)")), in0=ps_a2[:, :], in1=mask1[:, :], op=ALU.mult), r=[b_pa2, bc], w=[d["bA"]])
                V(lambda: nc.vector.tensor_tensor(out=d["L"][:, half * 4:half * 4 + 4, :].rearrange("p c k -> p (c k)"), in0=ps_a3[:, 0:256], in1=mask3[:, :], op=ALU.mult), r=[b_pa3, bc], w=[d["bA"]])
        DPS = [(ps_d, b_pd, ps_d2, b_pd2), (ps_a1, b_pa1, ps_a2, b_pa2)]
        for h in range(2):
            d = H[h]; P, PT, ST = d["P"], d["PT"], d["ST"]; bD = d["bD"]
            V(lambda: nc.vector.tensor_copy(out=RR(P[0][:]), in_=d["L"][:]), r=[d["bA"]], w=[bD])
            V(lambda: nc.vector.tensor_copy(out=RR(PT[0][:]), in_=d["NG"][:, :, 0:64]), r=[d["bA"]], w=[bD])
            V(lambda: nc.vector.tensor_tensor(out=RR(ST[0][:]), in0=d["NG"][:, :, 0:64], in1=ident[0:64, 0:64].unsqueeze(1).to_broadcast([64, NCH, 64]), op=ALU.add), r=[d["bA"], bc], w=[bD])
        cur = 0
        for lev in range(5):
            nxt = 1 - cur
            tick()
            for h in range(2):
                d = H[h]; P, PT, ST = d["P"], d["PT"], d["ST"]; bD = d["bD"]
                pd, bpd, pd2, bpd2 = DPS[h]
                for c in range(NCH):
                    PE(lambda: nc.tensor.matmul(pd[:, c * 64:(c + 1) * 64], lhsT=RR(PT[cur][:, c, :]), rhs=RR(P[cur][:, c, :]), start=True, stop=True), r=[bD], w=[bpd])
                for c in range(NCH):
                    PE(lambda: nc.tensor.matmul(pd2[:, c * 64:(c + 1) * 64], lhsT=RR(P[cur][:, c, :]), rhs=RR(PT[cur][:, c, :]), start=True, stop=True), r=[bD], w=[bpd2])
            tick()
            for h in range(2):
                d = H[h]; P, PT, ST = d["P"], d["PT"], d["ST"]; bD = d["bD"]
                pd, bpd, pd2, bpd2 = DPS[h]
                V(lambda: nc.vector.tensor_copy(out=RR(P[nxt][:].rearrange("p c k -> p (c k)")), in_=pd[:, :]), r=[], w=[bpd, bD])
                A(lambda: nc.scalar.copy(out=RR(PT[nxt][:].rearrange("p c k -> p (c k)")), in_=pd2[:, :]), r=[], w=[bpd2, bD])
            tick()
            for h in range(2):
                d = H[h]; P, PT, ST = d["P"], d["PT"], d["ST"]; bD = d["bD"]
                pd, bpd, pd2, bpd2 = DPS[h]
                for c in range(NCH):
                    PE(lambda: nc.tensor.matmul(pd[:, c * 64:(c + 1) * 64], lhsT=RR(P[nxt][:, c, :]), rhs=RR(ST[cur][:, c, :]), start=True, stop=True), r=[bD], w=[bpd])
            tick()
            for h in range(2):
                d = H[h]; P, PT, ST = d["P"], d["PT"], d["ST"]; bD = d["bD"]
                pd, bpd, pd2, bpd2 = DPS[h]
                V(lambda: nc.vector.tensor_tensor(out=RR(ST[nxt][:].rearrange("p c k -> p (c k)")), in0=pd[:, :], in1=ST[cur][:].rearrange("p c k -> p (c k)"), op=ALU.add), r=[bD], w=[bpd, bD])
            tick()
            cur = nxt
        for h in range(2):
            H[h]["STf"] = H[h]["ST"][cur]
        for c in range(NCH):
            pp = lambda h, i: ps_sh[h][:, i * 64:(i + 1) * 64]
            tick()
            for h in range(2):
                d = H[h]
                PE(lambda: nc.tensor.matmul(pp(h, 0), lhsT=RR(d["LG"][:, c, 0:64]), rhs=RR(d["Vt"][par][:, c, :]), start=True, stop=False), r=[d["bA"], d["bt"][par]], w=[b_psh[h]])
                PE(lambda: nc.tensor.matmul(pp(h, 0), lhsT=RR(d["AR"][par][:, c, 0:64]), rhs=RR(d["M"][:, :]), start=False, stop=True), r=[d["bo"][par], d["bM"]], w=[b_psh[h]])
            tick()
            for h in range(2):
                d = H[h]
                if h == 0:
                    A(lambda: nc.scalar.copy(out=RR(d["X1"][:]), in_=pp(h, 0)), r=[], w=[b_psh[h], d["bX"]])
                else:
                    V(lambda: nc.vector.tensor_copy(out=RR(d["X1"][:]), in_=pp(h, 0)), r=[], w=[b_psh[h], d["bX"]])
            tick()
            for h in range(2):
                d = H[h]
                PE(lambda: nc.tensor.matmul(pp(h, 1), lhsT=RR(d["STf"][:, c, :]), rhs=RR(d["X1"][:, :]), start=True, stop=True), r=[d["bD"], d["bX"]], w=[b_psh[h]])
            tick()
            for h in range(2):
                d = H[h]
                if h == 0:
                    V(lambda: nc.vector.tensor_copy(out=RR(d["U"][:]), in_=pp(h, 1)), r=[], w=[b_psh[h], d["bU"]])
                else:
                    A(lambda: nc.scalar.copy(out=RR(d["U"][:]), in_=pp(h, 1)), r=[], w=[b_psh[h], d["bU"]])
            tick()
            for h in range(2):
                d = H[h]
                PE(lambda: nc.tensor.matmul(pp(h, 2), lhsT=RR(d["AR"][par][:, c, 64:128]), rhs=RR(d["M"][:, :]), start=True, stop=False), r=[d["bo"][par], d["bM"]], w=[b_psh[h]])
                PE(lambda: nc.tensor.matmul(pp(h, 2), lhsT=RR(d["LG"][:, c, 64:128]), rhs=RR(d["Vt"][par][:, c, :]), start=False, stop=False), r=[d["bA"], d["bt"][par]], w=[b_psh[h]])
                PE(lambda: nc.tensor.matmul(pp(h, 2), lhsT=RR(d["NG"][:, c, 64:128]), rhs=RR(d["U"][:, :]), start=False, stop=True), r=[d["bA"], d["bU"]], w=[b_psh[h]])
                PE(lambda: nc.tensor.matmul(pp(h, 3), lhsT=RR(d["KHt"][par][:, c, :]), rhs=RR(d["Vt"][par][:, c, :]), start=True, stop=False), r=[d["bt"][par]], w=[b_psh[h]])
                PE(lambda: nc.tensor.matmul(pp(h, 3), lhsT=RR(d["BHt"][par][:, c, :]), rhs=RR(d["U"][:, :]), start=False, stop=True), r=[d["bt"][par], d["bU"]], w=[b_psh[h]])
            tick()
            for h in range(2):
                d = H[h]
                V(lambda: nc.vector.scalar_tensor_tensor(out=RR(d["M"][:]), in0=d["M"][:], scalar=d["gC"][par][:, c:c + 1], in1=pp(h, 3), op0=ALU.mult, op1=ALU.add), r=[d["bo"][par], d["bM"]], w=[b_psh[h], d["bM"]])
                A(lambda: nc.scalar.copy(out=Yb[:, c, h, :], in_=pp(h, 2)), r=[], w=[b_psh[h], b_Y])
        Y2 = Yb[:].rearrange("p c h v -> p (c h) v")
        V(lambda: nc.vector.tensor_reduce(out=st1[:], in_=Y2, axis=AX.X, op=ALU.add), r=[b_Y], w=[b_st])
        A(lambda: nc.scalar.activation(out=Ysq[:].rearrange("p c h v -> p (c h v)"), in_=Yb[:].rearrange("p c h v -> p (c h v)"), func=AF.Square), r=[b_Y], w=[b_tmp])
        V(lambda: nc.vector.tensor_reduce(out=st2[:], in_=Ysq[:].rearrange("p c h v -> p (c h) v"), axis=AX.X, op=ALU.add), r=[b_tmp], w=[b_st])
        V(lambda: nc.vector.tensor_scalar(out=st1[:], in0=st1[:], scalar1=1.0 / 64, scalar2=None, op0=ALU.mult), r=[b_st], w=[b_st])
        V(lambda: nc.vector.tensor_tensor(out=st3[:], in0=st1[:], in1=st1[:], op=ALU.mult), r=[b_st], w=[b_st])
        V(lambda: nc.vector.scalar_tensor_tensor(out=st2[:], in0=st2[:], scalar=1.0 / 64, in1=st3[:], op0=ALU.mult, op1=ALU.subtract), r=[b_st], w=[b_st])
        A(lambda: nc.scalar.activation(out=st2[:], in_=st2[:], func=AF.Sqrt, bias=64e-5, scale=1.0), r=[b_st], w=[b_st])
        V(lambda: nc.vector.reciprocal(out=st2[:], in_=st2[:]), r=[b_st], w=[b_st])
        V(lambda: nc.vector.tensor_tensor(out=Y2, in0=Y2, in1=st1[:].unsqueeze(2).to_broadcast([64, NCH * 2, 64]), op=ALU.subtract), r=[b_st, b_Y], w=[b_Y])
        V(lambda: nc.vector.tensor_tensor(out=Y2, in0=Y2, in1=st2[:].unsqueeze(2).to_broadcast([64, NCH * 2, 64]), op=ALU.mult), r=[b_st, b_Y], w=[b_Y])
        for h in range(2):
            V(lambda: nc.vector.tensor_tensor(out=Yb[:, :, h, :], in0=Yb[:, :, h, :], in1=gn[:, 0, h, :].unsqueeze(1).to_broadcast([64, NCH, 64]), op=ALU.mult), r=[b_Y, bc], w=[b_Y])
            V(lambda: nc.vector.tensor_tensor(out=Yb[:, :, h, :], in0=Yb[:, :, h, :], in1=gn[:, 1, h, :].unsqueeze(1).to_broadcast([64, NCH, 64]), op=ALU.add), r=[b_Y, bc], w=[b_Y])
            V(lambda: nc.vector.tensor_tensor(out=Ysq[:, :, h, :], in0=H[h]["Vt"][par][:], in1=sbon[par][:, :, h].unsqueeze(2).to_broadcast([64, NCH, 64]), op=ALU.mult), r=[H[h]["bt"][par], b_sb[par]], w=[b_tmp])
        V(lambda: nc.vector.tensor_tensor(out=Yb[:].rearrange("p c h v -> p (c h v)"), in0=Yb[:].rearrange("p c h v -> p (c h v)"), in1=Ysq[:].rearrange("p c h v -> p (c h v)"), op=ALU.add), r=[b_Y, b_tmp], w=[b_Y])
        for c in range(NCH):
            PE(lambda: nc.tensor.transpose(ps_a3f[:, c * 64:(c + 1) * 64], Yb[:, c, :, :].rearrange("p h v -> p (h v)"), ident[0:64, 0:64]), r=[b_Y, bc], w=[b_pa3])
        V(lambda: nc.vector.tensor_tensor(out=yo[:], in0=ps_a3f[:, :], in1=gate[par][:], op=ALU.mult), r=[b_gate[par]], w=[b_pa3, b_yo])
        mk.dma("sp", yT[:, t0:t0 + TB], yo[:], reads=[b_yo], is_output=True)

    for _ in stage1(0):
        pass
    for blk in range(NB):
        nx = stage1(blk + 1) if blk + 1 < NB else None

        def tick(k=2):
            if nx is not None:
                for _ in range(k):
                    if next(nx, "END") == "END":
                        break
        rest(blk, tick)
        if nx is not None:
            for _ in nx:
                pass


def build_rwkv(T):
    nc = bass.Bass("TRN2", target_bir_lowering=False)
    dt = lambda n, s, k="ExternalInput": nc.dram_tensor(n, s, F32, kind=k).ap()
    rwin = dt("rwin", [704, T]); par64 = dt("par64", [64, 2, 11]); par128 = dt("par128", [128, 3])
    w2 = dt("w2", [96, 128]); a2 = dt("a2", [96, 128]); g2 = dt("g2", [128, 128]); gnt = dt("gnt", [64, 2, 2, 64])
    cst = {"mask1": dt("mask1", [64, 512]), "mask3": dt("mask3", [64, 256]), "seg": dt("seg", [128, TB]), "ident": dt("ident", [128, 128])}
    yT = dt("yT", [128, T], "ExternalOutput")
    with ExitStack() as ctx:
        mk = MK(nc, ctx)
        emit_rwkv(nc, mk, T, rwin, par64, par128, w2, a2, g2, gnt, cst, yT)
        mk.finish("sp")
        print("rwkv ops", mk.nops, "waits", mk.nwaits)
    return nc


def rwkv_host_inputs(prm, l, q):
    G = 512
    cs = slice(128 * q, 128 * q + 128)
    mu = prm["rwkv_mu"][l]
    par64 = np.zeros((64, 2, 11), np.float32)
    for h in range(2):
        c0 = 128 * q + 64 * h
        par64[:, h, 0] = mu[0 * G + c0:0 * G + c0 + 64]
        par64[:, h, 1] = mu[1 * G + c0:1 * G + c0 + 64]
        par64[:, h, 2] = mu[2 * G + c0:2 * G + c0 + 64]
        par64[:, h, 3] = prm["rwkv_w0"][l][c0:c0 + 64]
        par64[:, h, 4] = prm["rwkv_a0"][l][c0:c0 + 64]
        par64[:, h, 5] = prm["rwkv_kk"][l][c0:c0 + 64]
        par64[:, h, 6] = prm["rwkv_ka"][l][c0:c0 + 64]
        par64[:, h, 7] = prm["rwkv_rk"][l][2 * q + h]
    par128 = np.zeros((128, 3), np.float32)
    par128[0:96, 0] = mu[3 * G:3 * G + 96]
    par128[0:96, 1] = mu[3 * G + 96:3 * G + 192]
    par128[:, 2] = mu[3 * G + 192:3 * G + 320]
    gnt = np.zeros((64, 2, 2, 64), np.float32)
    for h in range(2):
        c0 = 128 * q + 64 * h
        gnt[:, 0, h, :] = prm["rwkv_gn_g"][l][c0:c0 + 64][None]
        gnt[:, 1, h, :] = prm["rwkv_gn_b"][l][c0:c0 + 64][None]
    d = {"par64": par64, "par128": par128, "gnt": gnt,
         "w2": np.ascontiguousarray(prm["rwkv_w2"][l][:, cs]), "a2": np.ascontiguousarray(prm["rwkv_a2"][l][:, cs]),
         "g2": np.ascontiguousarray(prm["rwkv_g2"][l][:, cs])}
    d.update(rwkv_consts())
    return d


def rwkv_rows(q):
    G = 512
    idx = []
    for base in (0, G, 2 * G):
        idx += list(range(base + 128 * q, base + 128 * q + 64))
        idx += list(range(base + 128 * q + 64, base + 128 * q + 128))
    idx += list(range(3 * G, 3 * G + 320))
    return np.array(idx)


import math
import numpy as np
from contextlib import ExitStack


def emit_conv(nc, mk, T, cvin, cw, yT, TB=2048, odt=F32):
    V = lambda fn, r=(), w=(): mk.op("dve", fn, r, w)
    G = lambda fn, r=(), w=(): mk.op("pool", fn, r, w)
    cwt = mk.sb("cv_w", [128, 3]); bc = Buf()
    mk.dma("sp", cwt[:], cw, writes=[bc])
    Bt = mk.sb("cv_B", [128, TB]); Ct = mk.sb("cv_C", [128, TB + 2]); Ht = mk.sb("cv_H", [128, TB + 2])
    z = mk.sb("cv_z", [128, TB + 2]); y = mk.sb("cv_y", [128, TB]); o = mk.sb("cv_o", [128, TB], odt)
    b_in, b_z, b_y, b_o = Buf(), Buf(), Buf(), Buf()
    for blk in range(T // TB):
        t0 = blk * TB
        mk.dma("sp", Bt[:], cvin[0:128, t0:t0 + TB], writes=[b_in])
        if blk == 0:
            V(lambda: nc.vector.memset(Ct[:, 0:2], 0.0), w=[b_in])
            V(lambda: nc.vector.memset(Ht[:, 0:2], 0.0), w=[b_in])
            mk.dma("sp", Ct[:, 2:], cvin[128:256, 0:TB], writes=[b_in])
            mk.dma("sp", Ht[:, 2:], cvin[256:384, 0:TB], writes=[b_in])
        else:
            mk.dma("sp", Ct[:], cvin[128:256, t0 - 2:t0 + TB], writes=[b_in])
            mk.dma("sp", Ht[:], cvin[256:384, t0 - 2:t0 + TB], writes=[b_in])
        G(lambda: nc.gpsimd.tensor_tensor(out=z[:], in0=Ct[:], in1=Ht[:], op=ALU.mult), r=[b_in], w=[b_z])
        V(lambda: nc.vector.tensor_scalar(out=y[:], in0=z[:, 2:TB + 2], scalar1=cwt[:, 2:3], scalar2=None, op0=ALU.mult), r=[b_z, bc], w=[b_y])
        V(lambda: nc.vector.scalar_tensor_tensor(out=y[:], in0=z[:, 1:TB + 1], scalar=cwt[:, 1:2], in1=y[:], op0=ALU.mult, op1=ALU.add), r=[b_z, bc], w=[b_y])
        V(lambda: nc.vector.scalar_tensor_tensor(out=y[:], in0=z[:, 0:TB], scalar=cwt[:, 0:1], in1=y[:], op0=ALU.mult, op1=ALU.add), r=[b_z, bc], w=[b_y])
        G(lambda: nc.gpsimd.tensor_tensor(out=o[:], in0=y[:], in1=Bt[:], op=ALU.mult), r=[b_y, b_in], w=[b_o])
        mk.dma("sp", yT[:, t0:t0 + TB], o[:], reads=[b_o], is_output=True)


def t5_bucket_np(rel):
    n = np.maximum(rel, 0)
    max_exact = 16
    n_f = np.maximum(n, 1).astype(np.float32)
    large = max_exact + (np.log(n_f / max_exact) / math.log(128 / max_exact) * (32 - max_exact)).astype(np.int32)
    return np.where(n < max_exact, n, np.minimum(large, 31))


def attn_tables(rel_bias, sinks_l, q):
    qi = np.arange(128)[:, None]
    kj = np.arange(256)[None, :]
    rel = qi + 128 - kj
    bucket = t5_bucket_np(rel)
    valid = (rel >= 0) & (rel < 128)
    tab = np.zeros((2, 128, 2, 256), np.float32)
    for h in range(2):
        bias = rel_bias[bucket, 2 * q + h]
        full = np.where(valid, bias, np.float32(-30000.0))
        tab[0, :, h, :] = full
        f0 = full.copy()
        f0[:, 0:128] = -30000.0
        tab[1, :, h, :] = f0
    sk = np.broadcast_to(sinks_l[2 * q:2 * q + 2][None, :], (128, 2)).astype(np.float32).copy()
    return tab, sk


def emit_attn(nc, mk, T, qkv, btab, sinkt_d, ident_d, yT, odt=F32):
    V = lambda fn, r=(), w=(): mk.op("dve", fn, r, w)
    A = lambda fn, r=(), w=(): mk.op("act", fn, r, w)
    PE = lambda fn, r=(), w=(): mk.op("pe", fn, r, w, skip_same=True)
    NBK = T // 128
    bt = mk.sb("at_bt", [128, 2, 2, 256]); sk = mk.sb("at_sk", [128, 2]); ident = mk.sb("at_id", [128, 128]); bc = Buf()
    mk.dma("sp", bt[:, 0, :, :], btab[0], writes=[bc])
    mk.dma("sp", bt[:, 1, :, :], btab[1], writes=[bc])
    mk.dma("sp", sk[:], sinkt_d, writes=[bc])
    mk.dma("sp", ident[:], ident_d, writes=[bc])
    CH = 1024
    qt = mk.sb("at_q", [128, CH]); kt = mk.sb("at_k", [128, 128 + CH]); vt = mk.sb("at_v", [64, CH])
    vtok = mk.sb("at_vtok", [128, CH // 128 + 1, 64])
    b_q, b_k, b_v, b_vt = Buf(), Buf(), Buf(), Buf()
    sc = [mk.sb("at_sc%d" % h, [128, 256]) for h in range(2)]; b_sc = [Buf(), Buf()]
    pr = [mk.sb("at_p%d" % h, [128, 256]) for h in range(2)]; b_p = [Buf(), Buf()]
    pT = [mk.sb("at_pT%d" % h, [128, 256]) for h in range(2)]; b_pT = [Buf(), Buf()]
    sm = [mk.sb("at_sm%d" % h, [128, 8]) for h in range(2)]; b_sm = [Buf(), Buf()]
    ot = mk.sb("at_o", [128, 128]); b_o = Buf()
    yo = mk.sb("at_yo", [128, CH], odt); b_yo = Buf()
    ps_s = [mk.ps("at_ps_s%d" % h, [128, 512]) for h in range(2)]; b_ps = [Buf(), Buf()]
    ps_t = [mk.ps("at_ps_t%d" % h, [128, 512]) for h in range(2)]; b_pt = [Buf(), Buf()]
    ps_o = mk.ps("at_ps_o", [128, 512]); b_po = Buf()
    ps_v = mk.ps("at_ps_v", [128, 512]); b_pv = Buf()
    for ch in range(T // CH):
        c0 = ch * CH
        mk.dma("sp", qt[:], qkv[0:128, c0:c0 + CH], writes=[b_q])
        if ch == 0:
            V(lambda: nc.vector.memset(kt[:, 0:128], 0.0), w=[b_k])
            V(lambda: nc.vector.memset(vtok[:, 0, :], 0.0), w=[b_vt])
            for hh in range(2):
                mk.dma("sp", kt[64 * hh:64 * hh + 64, 128:], qkv[128:192, 0:CH], writes=[b_k])
        else:
            for hh in range(2):
                mk.dma("sp", kt[64 * hh:64 * hh + 64, :], qkv[128:192, c0 - 128:c0 + CH], writes=[b_k])
            V(lambda: nc.vector.tensor_copy(out=vtok[:, 0, :], in_=vtok[:, CH // 128, :]), r=[b_vt], w=[b_vt])
        mk.dma("sp", vt[:], qkv[192:256, c0:c0 + CH], writes=[b_v])
        for j in range(CH // 128):
            PE(lambda: nc.tensor.transpose(ps_v[:, j * 64:(j + 1) * 64], vt[:, j * 128:(j + 1) * 128], ident[0:64, 0:64]), r=[b_v, bc], w=[b_pv])
        A(lambda: nc.scalar.copy(out=vtok[:, 1:, :].rearrange("p j d -> p (j d)"), in_=ps_v[:, 0:(CH // 128) * 64]), r=[], w=[b_pv, b_vt])
        for j in range(CH // 128):
            first = 1 if (ch == 0 and j == 0) else 0
            HS = [slice(0, 64), slice(64, 128)]
            for h in range(2):
                PE(lambda: nc.tensor.matmul(ps_s[h][:, 0:256], lhsT=qt[HS[h], j * 128:(j + 1) * 128], rhs=kt[HS[h], j * 128:j * 128 + 256], start=True, stop=True),
                   r=[b_q, b_k], w=[b_ps[h]])
            for h in range(2):
                V(lambda: nc.vector.scalar_tensor_tensor(out=sc[h][:], in0=ps_s[h][:, 0:256], scalar=0.125, in1=bt[:, first, h, :], op0=ALU.mult, op1=ALU.add),
                  r=[bc], w=[b_ps[h], b_sc[h]])
                s = sm[h]
                V(lambda: nc.vector.reduce_max(out=s[:, 0:1], in_=sc[h][:], axis=AX.X), r=[b_sc[h]], w=[b_sm[h]])
                V(lambda: nc.vector.tensor_tensor(out=s[:, 0:1], in0=s[:, 0:1], in1=sk[:, h:h + 1], op=ALU.max), r=[bc], w=[b_sm[h]])
                V(lambda: nc.vector.tensor_scalar(out=s[:, 1:2], in0=s[:, 0:1], scalar1=-1.0, scalar2=None, op0=ALU.mult), r=[], w=[b_sm[h]])
            for h in range(2):
                s = sm[h]
                A(lambda: nc.scalar.activation(out=pr[h][:], in_=sc[h][:], func=AF.Exp, bias=s[:, 1:2], scale=1.0, accum_out=s[:, 2:3]), r=[b_sc[h]], w=[b_sm[h], b_p[h]])
                A(lambda: nc.scalar.activation(out=s[:, 3:4], in_=sk[:, h:h + 1], func=AF.Exp, bias=s[:, 1:2], scale=1.0), r=[bc], w=[b_sm[h]])
            for h in range(2):
                for kb in range(2):
                    PE(lambda: nc.tensor.transpose(ps_t[h][:, kb * 128:(kb + 1) * 128], pr[h][:, kb * 128:(kb + 1) * 128], ident[:, :]), r=[b_p[h], bc], w=[b_pt[h]])
            for h in range(2):
                s = sm[h]
                V(lambda: nc.vector.tensor_tensor(out=s[:, 4:5], in0=s[:, 2:3], in1=s[:, 3:4], op=ALU.add), r=[], w=[b_sm[h]])
                V(lambda: nc.vector.reciprocal(out=s[:, 5:6], in_=s[:, 4:5]), r=[], w=[b_sm[h]])
                V(lambda: nc.vector.tensor_copy(out=pT[h][:], in_=ps_t[h][:, 0:256]), r=[], w=[b_pt[h], b_pT[h]])
            for h in range(2):
                for kb in range(2):
                    PE(lambda: nc.tensor.matmul(ps_o[:, h * 64:(h + 1) * 64], lhsT=pT[h][:, kb * 128:(kb + 1) * 128], rhs=vtok[:, j + kb, :], start=(kb == 0), stop=(kb == 1)),
                       r=[b_pT[h], b_vt], w=[b_po])
            for h in range(2):
                s = sm[h]
                A(lambda: nc.scalar.activation(out=ot[:, h * 64:(h + 1) * 64], in_=ps_o[:, h * 64:(h + 1) * 64], func=AF.Copy, scale=s[:, 5:6]), r=[b_sm[h]], w=[b_po, b_o])
            PE(lambda: nc.tensor.transpose(ps_o[:, 128:256], ot[:, :], ident[:, :]), r=[b_o, bc], w=[b_po])
            V(lambda: nc.vector.tensor_copy(out=yo[:, j * 128:(j + 1) * 128], in_=ps_o[:, 128:256]), r=[], w=[b_po, b_yo])
        mk.dma("sp", yT[:, c0:c0 + CH], yo[:], reads=[b_yo], is_output=True)


CS = 512


def s5_host_inputs(prm, l, q):
    g0 = 8 * q
    par = np.zeros((128, 4, 3), np.float32)
    bb = np.zeros((128, 4, 2, 16), np.float32)
    cc = np.zeros((128, 4, 2, 16), np.float32)
    for j in range(4):
        for gl in range(2):
            g = g0 + 2 * j + gl
            ps = slice(64 * gl, 64 * gl + 64)
            par[ps, j, 0] = prm["s5_lambda_re"][l][g]
            par[ps, j, 1] = prm["s5_lambda_im"][l][g]
            par[ps, j, 2] = prm["s5_log_dt"][l][g]
            bb[ps, j, 0, :] = prm["s5_b_re"][l][g]
            bb[ps, j, 1, :] = prm["s5_b_im"][l][g]
            cc[ps, j, 0, :] = prm["s5_c_re"][l][g].T
            cc[ps, j, 1, :] = prm["s5_c_im"][l][g].T
    dsk = np.ascontiguousarray(prm["s5_d"][l][g0:g0 + 8].reshape(128, 1))
    iot = np.broadcast_to(np.arange(CS, dtype=np.float32)[None, :], (128, CS)).copy()
    return {"s5par": par, "s5bb": bb, "s5cc": cc, "s5d": dsk, "s5iota": iot, "ident": np.eye(128, dtype=np.float32)}


def emit_s5(nc, mk, T, uT, par_d, bb_d, cc_d, d_d, iota_d, ident_d, yT, odt=F32):
    V = lambda fn, r=(), w=(): mk.op("dve", fn, r, w)
    A = lambda fn, r=(), w=(): mk.op("act", fn, r, w)
    G = lambda fn, r=(), w=(): mk.op("pool", fn, r, w)
    PE = lambda fn, r=(), w=(): mk.op("pe", fn, r, w, skip_same=True)
    sb = mk.sb
    TWO_PI = 2.0 * math.pi
    par = sb("s5_par", [128, 4, 3]); bb = sb("s5_bb", [128, 4, 2, 16]); cc = sb("s5_cc", [128, 4, 2, 16])
    dsk = sb("s5_d", [128, 1]); iot = sb("s5_iota", [128, CS]); ident = sb("s5_id", [128, 128])
    bc = Buf("c")
    for dst, src in ((par, par_d), (bb, bb_d), (cc, cc_d), (dsk, d_d), (iot, iota_d), (ident, ident_d)):
        mk.dma("sp", dst[:], src, writes=[bc])
    P = {}
    for nm in ("dl", "mag", "th", "cs", "sn", "are", "aim", "den", "zre", "zim", "t1", "t2", "t3", "cC", "sC"):
        P[nm] = sb("s5_p_" + nm, [128, 4])
    ti = sb("s5_ti", [128, 4 * CS], I32)
    bp = Buf("p")
    lr, li, ldt = par[:, :, 0], par[:, :, 1], par[:, :, 2]

    def sincos(sin_out, cos_out, x, n, tmpa, tmpb, tint):
        def wrap(r):
            V(lambda: nc.vector.tensor_scalar(out=tmpb, in0=r, scalar1=0.5, scalar2=None, op0=ALU.is_gt), r=[bp], w=[bp])
            V(lambda: nc.vector.tensor_tensor(out=r, in0=r, in1=tmpb, op=ALU.subtract), r=[bp], w=[bp])
            V(lambda: nc.vector.tensor_scalar(out=tmpb, in0=r, scalar1=-0.5, scalar2=None, op0=ALU.is_lt), r=[bp], w=[bp])
            V(lambda: nc.vector.tensor_tensor(out=r, in0=r, in1=tmpb, op=ALU.add), r=[bp], w=[bp])
        V(lambda: nc.vector.tensor_copy(out=tint, in_=x), r=[bp, bc], w=[bp])
        V(lambda: nc.vector.tensor_copy(out=tmpa, in_=tint), r=[bp], w=[bp])
        V(lambda: nc.vector.tensor_tensor(out=tmpa, in0=x, in1=tmpa, op=ALU.subtract), r=[bp, bc], w=[bp])
        wrap(tmpa)
        A(lambda: nc.scalar.activation(out=sin_out, in_=tmpa, func=AF.Sin, scale=TWO_PI), r=[bp], w=[bp])
        V(lambda: nc.vector.tensor_scalar(out=tmpa, in0=tmpa, scalar1=0.25, scalar2=None, op0=ALU.add), r=[bp], w=[bp])
        wrap(tmpa)
        A(lambda: nc.scalar.activation(out=cos_out, in_=tmpa, func=AF.Sin, scale=TWO_PI), r=[bp], w=[bp])

    A(lambda: nc.scalar.activation(out=P["dl"][:], in_=ldt, func=AF.Exp), r=[bc], w=[bp])
    V(lambda: nc.vector.tensor_tensor(out=P["mag"][:], in0=lr, in1=P["dl"][:], op=ALU.mult), r=[bc, bp], w=[bp])
    A(lambda: nc.scalar.activation(out=P["mag"][:], in_=P["mag"][:], func=AF.Exp), r=[bp], w=[bp])
    V(lambda: nc.vector.tensor_tensor(out=P["th"][:], in0=li, in1=P["dl"][:], op=ALU.mult), r=[bc, bp], w=[bp])
    V(lambda: nc.vector.tensor_scalar(out=P["th"][:], in0=P["th"][:], scalar1=1.0 / TWO_PI, scalar2=None, op0=ALU.mult), r=[bp], w=[bp])
    V(lambda: nc.vector.tensor_copy(out=ti[:, 0:4], in_=P["th"][:]), r=[bp], w=[bp])
    V(lambda: nc.vector.tensor_copy(out=P["t1"][:], in_=ti[:, 0:4]), r=[bp], w=[bp])
    V(lambda: nc.vector.tensor_tensor(out=P["th"][:], in0=P["th"][:], in1=P["t1"][:], op=ALU.subtract), r=[bp], w=[bp])
    V(lambda: nc.vector.tensor_scalar(out=P["t1"][:], in0=P["th"][:], scalar1=0.5, scalar2=None, op0=ALU.is_gt), r=[bp], w=[bp])
    V(lambda: nc.vector.tensor_tensor(out=P["th"][:], in0=P["th"][:], in1=P["t1"][:], op=ALU.subtract), r=[bp], w=[bp])
    V(lambda: nc.vector.tensor_scalar(out=P["t1"][:], in0=P["th"][:], scalar1=-0.5, scalar2=None, op0=ALU.is_lt), r=[bp], w=[bp])
    V(lambda: nc.vector.tensor_tensor(out=P["th"][:], in0=P["th"][:], in1=P["t1"][:], op=ALU.add), r=[bp], w=[bp])
    sincos(P["sn"][:], P["cs"][:], P["th"][:], 4, P["t1"][:], P["t2"][:], ti[:, 0:4])
    V(lambda: nc.vector.tensor_tensor(out=P["are"][:], in0=P["mag"][:], in1=P["cs"][:], op=ALU.mult), r=[bp], w=[bp])
    V(lambda: nc.vector.tensor_tensor(out=P["aim"][:], in0=P["mag"][:], in1=P["sn"][:], op=ALU.mult), r=[bp], w=[bp])
    V(lambda: nc.vector.tensor_tensor(out=P["den"][:], in0=lr, in1=lr, op=ALU.mult), r=[bc], w=[bp])
    V(lambda: nc.vector.tensor_tensor(out=P["t1"][:], in0=li, in1=li, op=ALU.mult), r=[bc], w=[bp])
    V(lambda: nc.vector.tensor_tensor(out=P["den"][:], in0=P["den"][:], in1=P["t1"][:], op=ALU.add), r=[bp], w=[bp])
    V(lambda: nc.vector.reciprocal(out=P["den"][:], in_=P["den"][:]), r=[bp], w=[bp])
    V(lambda: nc.vector.tensor_scalar(out=P["t3"][:], in0=P["are"][:], scalar1=-1.0, scalar2=None, op0=ALU.add), r=[bp], w=[bp])
    V(lambda: nc.vector.tensor_tensor(out=P["t1"][:], in0=P["t3"][:], in1=lr, op=ALU.mult), r=[bp, bc], w=[bp])
    V(lambda: nc.vector.tensor_tensor(out=P["t2"][:], in0=P["aim"][:], in1=li, op=ALU.mult), r=[bp, bc], w=[bp])
    V(lambda: nc.vector.tensor_tensor(out=P["t1"][:], in0=P["t1"][:], in1=P["t2"][:], op=ALU.add), r=[bp], w=[bp])
    V(lambda: nc.vector.tensor_tensor(out=P["zre"][:], in0=P["t1"][:], in1=P["den"][:], op=ALU.mult), r=[bp], w=[bp])
    V(lambda: nc.vector.tensor_tensor(out=P["t1"][:], in0=P["aim"][:], in1=lr, op=ALU.mult), r=[bp, bc], w=[bp])
    V(lambda: nc.vector.tensor_tensor(out=P["t2"][:], in0=P["t3"][:], in1=li, op=ALU.mult), r=[bp, bc], w=[bp])
    V(lambda: nc.vector.tensor_tensor(out=P["t1"][:], in0=P["t1"][:], in1=P["t2"][:], op=ALU.subtract), r=[bp], w=[bp])
    V(lambda: nc.vector.tensor_tensor(out=P["zim"][:], in0=P["t1"][:], in1=P["den"][:], op=ALU.mult), r=[bp], w=[bp])
    bbar = sb("s5_bbar", [128, 4, 2, 16]); tb1 = sb("s5_tb1", [128, 4, 16]); tb2 = sb("s5_tb2", [128, 4, 16])
    zre_b = P["zre"][:].unsqueeze(2).to_broadcast([128, 4, 16]); zim_b = P["zim"][:].unsqueeze(2).to_broadcast([128, 4, 16])
    V(lambda: nc.vector.tensor_tensor(out=tb1[:], in0=bb[:, :, 0, :], in1=zre_b, op=ALU.mult), r=[bp, bc], w=[bp])
    V(lambda: nc.vector.tensor_tensor(out=tb2[:], in0=bb[:, :, 1, :], in1=zim_b, op=ALU.mult), r=[bp, bc], w=[bp])
    V(lambda: nc.vector.tensor_tensor(out=bbar[:, :, 0, :], in0=tb1[:], in1=tb2[:], op=ALU.subtract), r=[bp], w=[bp])
    V(lambda: nc.vector.tensor_tensor(out=tb1[:], in0=bb[:, :, 1, :], in1=zre_b, op=ALU.mult), r=[bp, bc], w=[bp])
    V(lambda: nc.vector.tensor_tensor(out=tb2[:], in0=bb[:, :, 0, :], in1=zim_b, op=ALU.mult), r=[bp, bc], w=[bp])
    V(lambda: nc.vector.tensor_tensor(out=bbar[:, :, 1, :], in0=tb1[:], in1=tb2[:], op=ALU.add), r=[bp], w=[bp])
    BD = sb("s5_BD", [128, 4, 2, 128]); CM = sb("s5_CM", [128, 4, 4, 128]); BbT = sb("s5_BbT", [128, 4, 2, 128])
    V(lambda: nc.vector.memset(BD[:].rearrange("p a b c -> p (a b c)"), 0.0), w=[bp])
    V(lambda: nc.vector.memset(CM[:].rearrange("p a b c -> p (a b c)"), 0.0), w=[bp])
    for j in range(4):
        for gl in range(2):
            ps_ = slice(64 * gl, 64 * gl + 64)
            c0 = 32 * j + 16 * gl
            for ri in range(2):
                V(lambda: nc.vector.tensor_copy(out=BD[ps_, j, ri, c0:c0 + 16], in_=bbar[ps_, j, ri, :]), r=[bp], w=[bp])
            V(lambda: nc.vector.tensor_copy(out=CM[ps_, j, 0, c0:c0 + 16], in_=cc[ps_, j, 0, :]), r=[bc], w=[bp])
            V(lambda: nc.vector.tensor_scalar(out=CM[ps_, j, 1, c0:c0 + 16], in0=cc[ps_, j, 0, :], scalar1=-1.0, scalar2=None, op0=ALU.mult), r=[bc], w=[bp])
            V(lambda: nc.vector.tensor_scalar(out=CM[ps_, j, 2, c0:c0 + 16], in0=cc[ps_, j, 1, :], scalar1=-1.0, scalar2=None, op0=ALU.mult), r=[bc], w=[bp])
            V(lambda: nc.vector.tensor_scalar(out=CM[ps_, j, 3, c0:c0 + 16], in0=cc[ps_, j, 1, :], scalar1=-1.0, scalar2=None, op0=ALU.mult), r=[bc], w=[bp])
    ps_tmp = mk.ps("s5_ps_tmp", [128, 512]); b_pt = Buf()
    for j in range(4):
        for ri in range(2):
            PE(lambda: nc.tensor.transpose(ps_tmp[:, ri * 128:(ri + 1) * 128], BD[:, j, ri, :], ident[:, :]), r=[bp, bc], w=[b_pt])
        V(lambda: nc.vector.tensor_copy(out=BbT[:, j, :, :].rearrange("p a b -> p (a b)"), in_=ps_tmp[:, 0:256]), r=[], w=[b_pt, bp])
    cosT = sb("s5_cosT", [128, 4, CS]); sinT = sb("s5_sinT", [128, 4, CS])
    xa = sb("s5_xa", [128, 4 * CS]); xb = sb("s5_xb", [128, 4 * CS]); xc = sb("s5_xc", [128, 4 * CS])
    for j in range(4):
        V(lambda: nc.vector.tensor_scalar(out=xc[:, j * CS:(j + 1) * CS], in0=iot[:], scalar1=P["th"][:, j:j + 1], scalar2=None, op0=ALU.mult), r=[bp, bc], w=[bp])
    sincos(sinT[:].rearrange("p a b -> p (a b)"), cosT[:].rearrange("p a b -> p (a b)"), xc[:], 4 * CS, xa[:], xb[:], ti[:])
    V(lambda: nc.vector.tensor_scalar(out=P["t3"][:], in0=P["th"][:], scalar1=float(CS), scalar2=None, op0=ALU.mult), r=[bp], w=[bp])
    sincos(P["sC"][:], P["cC"][:], P["t3"][:], 4, P["t1"][:], P["t2"][:], ti[:, 0:4])
    WW = []
    for par in range(2):
        Wd = {}
        for nm in ("t1", "t2", "t3", "t4", "br", "bi", "zr", "zi", "q1", "q2", "q3", "q4"):
            Wd[nm] = sb("s5_w%d_%s" % (par, nm), [128, CS])
        WW.append(Wd)
    b_wl = [Buf("w0"), Buf("w1")]; b_zl = [Buf("z0"), Buf("z1")]; b_ql = [Buf("q0"), Buf("q1")]
    init = sb("s5_init", [128, 4, 2]); itmp = sb("s5_itmp", [128, 4, 2]); b_il = [Buf("init%d" % j) for j in range(4)]
    for j in range(4):
        V(lambda: nc.vector.memset(init[:, j, :], 0.0), w=[b_il[j]])
    yv = sb("s5_yv", [128, CS]); y2 = sb("s5_y2", [128, CS]); yo = sb("s5_yo", [128, CS], odt); b_y = Buf("y")
    ps_al = [mk.ps("s5_ps_a%d" % i, [128, 512]) for i in range(2)]; ps_bl = [mk.ps("s5_ps_b%d" % i, [128, 512]) for i in range(2)]
    ps_y = mk.ps("s5_ps_y", [128, 512])
    b_pal, b_pbl, b_py = [Buf(), Buf()], [Buf(), Buf()], Buf()
    uts = [sb("s5_u%d" % i, [128, CS]) for i in range(2)]; b_ul = [Buf(), Buf()]
    NCK = T // CS

    def stage1(n):
        chk, j = n // 4, n % 4
        par = n % 2
        ut = uts[chk % 2]; b_u = b_ul[chk % 2]
        if j == 0:
            mk.dma("sp", ut[:], uT[:, chk * CS:(chk + 1) * CS], writes=[b_u])
        W = WW[par]; b_w = b_wl[par]
        ps_a = ps_al[par]; ps_b = ps_bl[par]; b_pa = b_pal[par]; b_pb = b_pbl[par]
        PE(lambda: nc.tensor.matmul(ps_a[:, :], lhsT=BbT[:, j, 0, :], rhs=ut[:, :], start=True, stop=True), r=[bp, b_u], w=[b_pa])
        PE(lambda: nc.tensor.matmul(ps_b[:, :], lhsT=BbT[:, j, 1, :], rhs=ut[:, :], start=True, stop=True), r=[bp, b_u], w=[b_pb])

    def stage1v(n):
        chk, j = n // 4, n % 4
        par = n % 2
        W = WW[par]; b_w = b_wl[par]
        ps_a = ps_al[par]; ps_b = ps_bl[par]; b_pa = b_pal[par]; b_pb = b_pbl[par]
        cj, sj = cosT[:, j, :], sinT[:, j, :]
        V(lambda: nc.vector.tensor_tensor(out=W["t1"][:], in0=ps_a[:, :], in1=cj, op=ALU.mult), r=[bp], w=[b_pa, b_w])
        V(lambda: nc.vector.tensor_tensor(out=W["t4"][:], in0=ps_a[:, :], in1=sj, op=ALU.mult), r=[bp], w=[b_pa, b_w])
        V(lambda: nc.vector.tensor_tensor(out=W["t2"][:], in0=ps_b[:, :], in1=sj, op=ALU.mult), r=[bp], w=[b_pb, b_w])
        V(lambda: nc.vector.tensor_tensor(out=W["t3"][:], in0=ps_b[:, :], in1=cj, op=ALU.mult), r=[bp], w=[b_pb, b_w])
        G(lambda: nc.gpsimd.tensor_tensor(out=W["br"][:], in0=W["t1"][:], in1=W["t2"][:], op=ALU.add), r=[b_w], w=[b_w])
        G(lambda: nc.gpsimd.tensor_tensor(out=W["bi"][:], in0=W["t3"][:], in1=W["t4"][:], op=ALU.subtract), r=[b_w], w=[b_w])

    def stage2(n):
        chk, j = n // 4, n % 4
        par = n % 2
        t0 = chk * CS
        ut = uts[chk % 2]; b_u = b_ul[chk % 2]
        W = WW[par]; b_w = b_wl[par]; b_z = b_zl[par]; b_q = b_ql[par]; b_i = b_il[j]
        cj, sj = cosT[:, j, :], sinT[:, j, :]
        rho = P["mag"][:, j:j + 1].to_broadcast([128, CS])
        V(lambda: nc.vector.tensor_tensor_scan(out=W["zr"][:], data0=rho, data1=W["br"][:], initial=init[:, j, 0:1], op0=ALU.mult, op1=ALU.add), r=[b_w, bp, b_i, b_q], w=[b_z])
        V(lambda: nc.vector.tensor_tensor_scan(out=W["zi"][:], data0=rho, data1=W["bi"][:], initial=init[:, j, 1:2], op0=ALU.mult, op1=ALU.add), r=[b_w, bp, b_i, b_q], w=[b_z])
        zrl, zil = W["zr"][:, CS - 1:CS], W["zi"][:, CS - 1:CS]
        cC, sC = P["cC"][:, j:j + 1], P["sC"][:, j:j + 1]
        V(lambda: nc.vector.tensor_tensor(out=itmp[:, j, 0:1], in0=zil, in1=sC, op=ALU.mult), r=[b_z, bp], w=[b_i])
        V(lambda: nc.vector.scalar_tensor_tensor(out=init[:, j, 0:1], in0=zrl, scalar=cC, in1=itmp[:, j, 0:1], op0=ALU.mult, op1=ALU.subtract), r=[b_z, bp], w=[b_i])
        V(lambda: nc.vector.tensor_tensor(out=itmp[:, j, 1:2], in0=zrl, in1=sC, op=ALU.mult), r=[b_z, bp], w=[b_i])
        V(lambda: nc.vector.scalar_tensor_tensor(out=init[:, j, 1:2], in0=zil, scalar=cC, in1=itmp[:, j, 1:2], op0=ALU.mult, op1=ALU.add), r=[b_z, bp], w=[b_i])
        G(lambda: nc.gpsimd.tensor_tensor(out=W["q1"][:], in0=W["zr"][:], in1=cj, op=ALU.mult), r=[b_z, bp], w=[b_q])
        G(lambda: nc.gpsimd.tensor_tensor(out=W["q2"][:], in0=W["zi"][:], in1=sj, op=ALU.mult), r=[b_z, bp], w=[b_q])
        V(lambda: nc.vector.tensor_tensor(out=W["q3"][:], in0=W["zi"][:], in1=cj, op=ALU.mult), r=[b_z, bp], w=[b_q])
        V(lambda: nc.vector.tensor_tensor(out=W["q4"][:], in0=W["zr"][:], in1=sj, op=ALU.mult), r=[b_z, bp], w=[b_q])
        for qi, nm in enumerate(("q1", "q2", "q3", "q4")):
            PE(lambda: nc.tensor.matmul(ps_y[:, :], lhsT=CM[:, j, qi, :], rhs=W[nm][:, :], start=(j == 0 and qi == 0), stop=(j == 3 and qi == 3)), r=[bp, b_q], w=[b_py])
        if j == 3:
            V(lambda: nc.vector.scalar_tensor_tensor(out=yv[:], in0=ut[:], scalar=dsk[:, 0:1], in1=ps_y[:, :], op0=ALU.mult, op1=ALU.add), r=[b_u, bc], w=[b_py, b_y])
            A(lambda: nc.scalar.activation(out=y2[:], in_=yv[:], func=AF.Square), r=[b_y], w=[b_y])
            V(lambda: nc.vector.tensor_scalar(out=y2[:], in0=y2[:], scalar1=0.044715, scalar2=1.0, op0=ALU.mult, op1=ALU.add), r=[b_y], w=[b_y])
            V(lambda: nc.vector.tensor_tensor(out=y2[:], in0=y2[:], in1=yv[:], op=ALU.mult), r=[b_y], w=[b_y])
            A(lambda: nc.scalar.activation(out=y2[:], in_=y2[:], func=AF.Tanh, scale=0.7978845608028654), r=[b_y], w=[b_y])
            V(lambda: nc.vector.scalar_tensor_tensor(out=y2[:], in0=y2[:], scalar=1.0, in1=yv[:], op0=ALU.add, op1=ALU.mult), r=[b_y], w=[b_y])
            A(lambda: nc.scalar.mul(out=yo[:], in_=y2[:], mul=0.5), r=[b_y], w=[b_y])
            mk.dma("sp", yT[:, t0:t0 + CS], yo[:], reads=[b_y], is_output=True)

    NTL_ = NCK * 4
    stage1(0)
    for n in range(NTL_ + 1):
        if n + 1 < NTL_:
            stage1(n + 1)
        if n < NTL_:
            stage1v(n)
        if n >= 1:
            stage2(n - 1)


def build(which, T):
    nc = bass.Bass("TRN2", target_bir_lowering=False)
    dt = lambda n, s, k="ExternalInput": nc.dram_tensor(n, s, F32, kind=k).ap()
    with ExitStack() as ctx:
        mk = MK(nc, ctx)
        if which == "conv":
            emit_conv(nc, mk, T, dt("cvin", [384, T]), dt("cw", [128, 3]), dt("yT", [128, T], "ExternalOutput"), TB=min(T, 2048))
        elif which == "attn":
            emit_attn(nc, mk, T, dt("qkv", [256, T]), dt("btab", [2, 128, 2, 256]), dt("sinkt", [128, 2]), dt("ident", [128, 128]), dt("yT", [128, T], "ExternalOutput"))
        elif which == "s5":
            emit_s5(nc, mk, T, dt("uT", [128, T]), dt("s5par", [128, 4, 3]), dt("s5bb", [128, 4, 2, 16]), dt("s5cc", [128, 4, 2, 16]),
                    dt("s5d", [128, 1]), dt("s5iota", [128, CS]), dt("ident", [128, 128]), dt("yT", [128, T], "ExternalOutput"))
        mk.finish("sp")
        print(which, "ops", mk.nops, "waits", mk.nwaits)
    return nc


import math
import numpy as np
from contextlib import ExitStack

D = 2048
TOK = 2048
NT = TOK // 128
NE = 32
CAP = 256
ALPHA = 4 ** 0.25
DE = 512


def k3_consts():
    tp = np.arange(128)[:, None]
    t = np.arange(128)[None, :]
    U = (tp < t).astype(np.float32)
    ecap = np.broadcast_to((np.arange(NE) * CAP).astype(np.float32)[None, :], (128, NE)).copy()
    return {"U": U, "ecap": ecap, "ident": np.eye(128, dtype=np.float32)}


def emit_k3(nc, mk, ymixT, x, w_out, glu_w, glu_b, rows, wr, br, w1, w3, w2, cst, x1s, Xg, Yg, xout, ymload=None, rowload=None):
    V = lambda fn, r=(), w=(): mk.op("dve", fn, r, w)
    A = lambda fn, r=(), w=(): mk.op("act", fn, r, w)
    G = lambda fn, r=(), w=(): mk.op("pool", fn, r, w)
    PE = lambda fn, r=(), w=(): mk.op("pe", fn, r, w, skip_same=True)
    gw = mk.sb("k3_gw", [128, NT, 2]); slot = mk.sb("k3_slot", [128, NT, 2], I32); b_rt = Buf("route")
    ident = mk.sb("k3_ident", [128, 128]); identb = mk.sb("k3_identb", [128, 128], BF16); bc = Buf("c")
    mk.dma("sp", ident[:], cst["ident"], writes=[bc])
    V(lambda: nc.vector.tensor_copy(out=identb[:], in_=ident[:]), r=[bc], w=[bc])
    with ExitStack() as pa:
        sb = lambda n, s, dt=F32: pa.enter_context(nc.sbuf_tensor("k3a%d_" % mk.gen + n, list(s), dt))
        ps = lambda n, s, dt=F32: pa.enter_context(nc.psum_tensor("k3a%d_" % mk.gen + n, list(s), dt))
        wo = sb("wo", [128, 16, D], BF16); gluw = sb("gluw", [128, 4, 512], BF16); glub = sb("glub", [128, 4])
        R = [sb("row%d" % i, [128, D]) for i in range(5)]
        wrt = sb("wr", [128, 16, 36]); brt = sb("br", [128, 36]); Ut = sb("U", [128, 128]); ones = sb("ones", [128, 128]); ecap = sb("ecap", [128, NE])
        Srun = sb("Srun", [128, NE]); b_S = Buf("S")
        if ymload is None:
            ym = [sb("ym%d" % i, [128, 16, 128], BF16) for i in range(2)]; b_ym = [Buf(), Buf()]
        else:
            ymbig = sb("ymbig", [128, 16, 1024], BF16); _bym = Buf()
            ym = None; b_ym = [_bym, _bym]
        xt = [sb("xt%d" % i, [128, D]) for i in range(2)]; b_xt = [Buf(), Buf()]
        sg = sb("sg", [128, 4, 128]); b_sg = Buf()
        xr = sb("xr", [128, D]); b_xr = Buf()
        xn = xr; b_xn = b_xr
        _x1 = sb("x1_0", [128, D]); _bx1 = Buf()
        x1 = [_x1, _x1]; b_x1 = [_bx1, _bx1]
        _h2 = sb("h2_0", [128, D]); _bh2 = Buf()
        h2l = [_h2, _h2]; b_h2l = [_bh2, _bh2]
        hb = [sb("hb%d" % i, [128, D], BF16) for i in range(2)]; b_hb = [Buf(), Buf()]
        h2T = sb("h2T", [128, 16, 128]); b_h2T = Buf()
        st = sb("st", [128, 4, 6]); mv = sb("mv", [128, 2]); rs = sb("rs", [128, 1]); nmr = sb("nmr", [128, 1]); b_s = Buf()
        lg = sb("lg", [128, 36]); rt = sb("rt", [128, 16]); em = sb("em", [128, 32]); em2 = sb("em2", [128, 32])
        oh1 = sb("oh1", [128, 32]); oh2 = sb("oh2", [128, 32]); Mk = sb("Mk", [128, 32]); rank = sb("rank", [128, 32]); t32 = sb("t32", [128, 32]); pen = sb("pen", [128, 4]); ohg = sb("ohg", [128, 4]); eg = sb("eg", [128, 4])
        b_r = Buf("r")
        p_g = ps("p_g", [128, 512]); b_pg = Buf()
        p_o = [ps("p_o%d" % i, [128, 512]) for i in range(2)]; b_po = [Buf(), Buf()]
        p_t = [ps("p_t%d" % i, [128, 512]) for i in range(2)]; b_pt = [Buf(), Buf()]
        p_r = ps("p_r", [128, 512]); b_pr = Buf()
        p_k = ps("p_k", [128, 512]); b_pk = Buf()
        mk.dma("pool", wo[:], w_out.rearrange("(k p) c -> p k c", p=128), writes=[bc])
        mk.dma("pool", gluw[:], glu_w.rearrange("(k p) c -> p k c", p=128), writes=[bc])
        mk.dma("sp", glub[:], glu_b, writes=[bc])
        if rowload is None:
            rowload = lambda dst, ri, bcx: mk.dma("sp", dst[:], rows[ri], writes=[bcx])
        for i, ri in enumerate((0, 1, 2, 3, 4)):
            rowload(R[i], ri, bc)
        G(lambda: nc.gpsimd.tensor_scalar(out=R[0][:], in0=R[0][:], scalar1=1.0, scalar2=None, op0=ALU.add), r=[bc], w=[bc])
        G(lambda: nc.gpsimd.tensor_scalar(out=R[3][:], in0=R[3][:], scalar1=1.0, scalar2=None, op0=ALU.add), r=[bc], w=[bc])
        mk.dma("sp", wrt[:], wr.rearrange("(k p) c -> p k c", p=128), writes=[bc])
        mk.dma("sp", brt[:], br, writes=[bc])
        mk.dma("sp", Ut[:], cst["U"], writes=[bc])
        mk.dma("sp", ecap[:], cst["ecap"], writes=[bc])
        V(lambda: nc.vector.memset(ones[:], 1.0), w=[bc])
        V(lambda: nc.vector.memset(Srun[:], 0.0), w=[b_S])
        zt = hb[0]; b_z = b_hb[0]
        V(lambda: nc.vector.memset(zt[:], 0.0), w=[b_z])
        b_Xg = Buf("Xg")
        XgV = Xg.rearrange("(a p) c -> p a c", p=128)
        for a in range(NE * CAP // 128):
            mk.dma("sp", XgV[:, a, :], zt[:], reads=[b_z], writes=[b_Xg])

        def front(t):
            i = t % 2
            ts_ = slice(t * 128, (t + 1) * 128)
            if ymload is None:
                mk.dma("pool", ym[i][:], ymixT[:, ts_].rearrange("(k p) t -> p k t", p=128), writes=[b_ym[i]])
                ymt = ym[i]
            else:
                if t % 8 == 0:
                    ymload(t // 8, ymbig, b_ym[i])
                ymt = ymbig[:, :, (t % 8) * 128:(t % 8 + 1) * 128]
            mk.dma("sp", xt[i][:], x[ts_, :], writes=[b_xt[i]])
            for oc in range(4):
                for kc in range(4):
                    PE(lambda: nc.tensor.matmul(p_g[:, oc * 128:(oc + 1) * 128], lhsT=gluw[:, kc, oc * 128:(oc + 1) * 128], rhs=ymt[:, 12 + kc, :], start=(kc == 0), stop=(kc == 3)),
                       r=[bc, b_ym[i]], w=[b_pg])
            for oc in range(4):
                A(lambda: nc.scalar.activation(out=sg[:, oc, :], in_=p_g[:, oc * 128:(oc + 1) * 128], func=AF.Sigmoid, bias=glub[:, oc:oc + 1], scale=1.0), r=[bc], w=[b_pg, b_sg])
            V(lambda: nc.vector.tensor_tensor(out=ymt[:, 12:16, :], in0=ymt[:, 12:16, :], in1=sg[:], op=ALU.mult), r=[b_sg], w=[b_ym[i]])
            for cc in range(4):
                j = cc % 2
                for k in range(16):
                    PE(lambda: nc.tensor.matmul(p_o[j][:, :], lhsT=ymt[:, k, :], rhs=wo[:, k, cc * 512:(cc + 1) * 512], start=(k == 0), stop=(k == 15)), r=[b_ym[i], bc], w=[b_po[j]])
                V(lambda: nc.vector.tensor_tensor(out=xr[:, cc * 512:(cc + 1) * 512], in0=p_o[j][:, :], in1=R[0][:, cc * 512:(cc + 1) * 512], op=ALU.mult), r=[bc], w=[b_po[j], b_xr])
            V(lambda: nc.vector.scalar_tensor_tensor(out=xr[:], in0=xt[i][:], scalar=ALPHA, in1=xr[:], op0=ALU.mult, op1=ALU.add), r=[b_xt[i]], w=[b_xr])
            ln_stats(nc, mk, xr, b_xr, st, mv, rs, nmr, b_s)
            A(lambda: nc.scalar.activation(out=xn[:], in_=xr[:], func=AF.Identity, bias=nmr[:, 0:1], scale=rs[:, 0:1]), r=[b_s], w=[b_xn])
            V(lambda: nc.vector.tensor_tensor(out=xn[:], in0=xn[:], in1=R[1][:], op=ALU.mult), r=[bc], w=[b_xn])
            V(lambda: nc.vector.tensor_tensor(out=x1[i][:], in0=xn[:], in1=R[2][:], op=ALU.add), r=[b_xn, bc], w=[b_x1[i]])
            mk.dma("sp", x1s[ts_, :], x1[i][:], reads=[b_x1[i]])
            ln_stats(nc, mk, x1[i], b_x1[i], st, mv, rs, nmr, b_s)
            A(lambda: nc.scalar.activation(out=xn[:], in_=x1[i][:], func=AF.Identity, bias=nmr[:, 0:1], scale=rs[:, 0:1]), r=[b_x1[i], b_s], w=[b_xn])
            V(lambda: nc.vector.tensor_tensor(out=xn[:], in0=xn[:], in1=R[3][:], op=ALU.mult), r=[bc], w=[b_xn])
            h2 = h2l[i]; b_h2 = b_h2l[i]
            V(lambda: nc.vector.tensor_tensor(out=h2[:], in0=xn[:], in1=R[4][:], op=ALU.add), r=[b_xn, bc], w=[b_h2])
            A(lambda: nc.scalar.copy(out=hb[i][:], in_=h2[:]), r=[b_h2], w=[b_hb[i]])
        def tail(t):
            i = t % 2
            ts_ = slice(t * 128, (t + 1) * 128)
            h2 = h2l[i]; b_h2 = b_h2l[i]
            for half in range(4):
                j = half % 2
                for kk in range(4):
                    k = half * 4 + kk
                    PE(lambda: nc.tensor.transpose(p_t[j][:, kk * 128:(kk + 1) * 128], h2[:, k * 128:(k + 1) * 128], ident[:, :]), r=[b_h2, bc], w=[b_pt[j]])
                if j == 0:
                    A(lambda: nc.scalar.copy(out=h2T[:, half * 4:(half + 1) * 4, :].rearrange("p a b -> p (a b)"), in_=p_t[j][:, :]), r=[], w=[b_pt[j], b_h2T])
                else:
                    V(lambda: nc.vector.tensor_copy(out=h2T[:, half * 4:(half + 1) * 4, :].rearrange("p a b -> p (a b)"), in_=p_t[j][:, :]), r=[], w=[b_pt[j], b_h2T])
            for k in range(16):
                PE(lambda: nc.tensor.matmul(p_r[:, 0:36], lhsT=h2T[:, k, :], rhs=wrt[:, k, :], start=(k == 0), stop=(k == 15)), r=[b_h2T, bc], w=[b_pr])
            V(lambda: nc.vector.tensor_tensor(out=lg[:], in0=p_r[:, 0:36], in1=brt[:], op=ALU.add), r=[bc], w=[b_pr, b_r])
            R_ = lambda fn: V(fn, r=[b_r, bc], w=[b_r])
            R_(lambda: nc.vector.reduce_max(out=rt[:, 0:1], in_=lg[:, 0:4], axis=AX.X))
            R_(lambda: nc.vector.tensor_scalar(out=ohg[:], in0=lg[:, 0:4], scalar1=rt[:, 0:1], scalar2=None, op0=ALU.is_ge))
            R_(lambda: nc.vector.tensor_scalar(out=rt[:, 1:2], in0=rt[:, 0:1], scalar1=-1.0, scalar2=None, op0=ALU.mult))
            A(lambda: nc.scalar.activation(out=eg[:], in_=lg[:, 0:4], func=AF.Exp, bias=rt[:, 1:2], scale=1.0, accum_out=rt[:, 2:3]), r=[b_r], w=[b_r])
            R_(lambda: nc.vector.reciprocal(out=rt[:, 3:4], in_=rt[:, 2:3]))
            R_(lambda: nc.vector.tensor_scalar(out=pen[:], in0=ohg[:], scalar1=-1.0, scalar2=1e30, op0=ALU.add, op1=ALU.mult))
            R_(lambda: nc.vector.tensor_tensor(out=em[:].rearrange("p (g e) -> p g e", e=8), in0=lg[:, 4:36].rearrange("p (g e) -> p g e", e=8),
                                               in1=pen[:].unsqueeze(2).to_broadcast([128, 4, 8]), op=ALU.add))
            R_(lambda: nc.vector.reduce_max(out=rt[:, 4:5], in_=em[:], axis=AX.X))
            R_(lambda: nc.vector.tensor_scalar(out=oh1[:], in0=em[:], scalar1=rt[:, 4:5], scalar2=None, op0=ALU.is_ge))
            R_(lambda: nc.vector.scalar_tensor_tensor(out=em2[:], in0=oh1[:], scalar=-1e30, in1=em[:], op0=ALU.mult, op1=ALU.add))
            R_(lambda: nc.vector.reduce_max(out=rt[:, 5:6], in_=em2[:], axis=AX.X))
            R_(lambda: nc.vector.tensor_scalar(out=oh2[:], in0=em2[:], scalar1=rt[:, 5:6], scalar2=None, op0=ALU.is_ge))
            R_(lambda: nc.vector.tensor_tensor(out=rt[:, 6:7], in0=rt[:, 5:6], in1=rt[:, 4:5], op=ALU.subtract))
            A(lambda: nc.scalar.activation(out=rt[:, 7:8], in_=rt[:, 6:7], func=AF.Exp), r=[b_r], w=[b_r])
            R_(lambda: nc.vector.tensor_scalar(out=rt[:, 8:9], in0=rt[:, 7:8], scalar1=1.0, scalar2=None, op0=ALU.add))
            R_(lambda: nc.vector.reciprocal(out=rt[:, 8:9], in_=rt[:, 8:9]))
            R_(lambda: nc.vector.tensor_tensor(out=rt[:, 9:10], in0=rt[:, 7:8], in1=rt[:, 8:9], op=ALU.mult))
            R_(lambda: nc.vector.tensor_tensor(out=Mk[:], in0=oh1[:], in1=oh2[:], op=ALU.add))
            PE(lambda: nc.tensor.matmul(p_k[:, 0:32], lhsT=Ut[:, :], rhs=Mk[:, :], start=True, stop=True), r=[bc, b_r], w=[b_pk])
            PE(lambda: nc.tensor.matmul(p_k[:, 32:64], lhsT=ones[:, :], rhs=Mk[:, :], start=True, stop=True), r=[bc, b_r], w=[b_pk])
            V(lambda: nc.vector.tensor_tensor(out=rank[:], in0=p_k[:, 0:32], in1=Srun[:], op=ALU.add), r=[b_S, b_r], w=[b_pk, b_r])
            V(lambda: nc.vector.tensor_tensor(out=Srun[:], in0=p_k[:, 32:64], in1=Srun[:], op=ALU.add), r=[b_r], w=[b_pk, b_S])
            for kx, oh in enumerate((oh1, oh2)):
                R_(lambda: nc.vector.tensor_tensor(out=t32[:], in0=oh[:], in1=rank[:], op=ALU.mult))
                R_(lambda: nc.vector.reduce_sum(out=rt[:, 10:11], in_=t32[:], axis=AX.X))
                R_(lambda: nc.vector.tensor_tensor(out=t32[:], in0=oh[:], in1=ecap[:], op=ALU.mult))
                R_(lambda: nc.vector.reduce_sum(out=rt[:, 11:12], in_=t32[:], axis=AX.X))
                R_(lambda: nc.vector.tensor_scalar(out=rt[:, 12:13], in0=rt[:, 10:11], scalar1=float(CAP), scalar2=None, op0=ALU.is_ge))
                R_(lambda: nc.vector.tensor_tensor(out=rt[:, 11:12], in0=rt[:, 11:12], in1=rt[:, 10:11], op=ALU.add))
                R_(lambda: nc.vector.scalar_tensor_tensor(out=rt[:, 11:12], in0=rt[:, 12:13], scalar=1e6, in1=rt[:, 11:12], op0=ALU.mult, op1=ALU.add))
                V(lambda: nc.vector.tensor_copy(out=slot[:, t, kx:kx + 1], in_=rt[:, 11:12]), r=[b_r], w=[b_rt])
                R_(lambda: nc.vector.tensor_scalar(out=rt[:, 13:14], in0=rt[:, 12:13], scalar1=-1.0, scalar2=-1.0, op0=ALU.add, op1=ALU.mult))
                R_(lambda: nc.vector.tensor_tensor(out=rt[:, 13:14], in0=rt[:, 13:14], in1=rt[:, 3:4], op=ALU.mult))
                V(lambda: nc.vector.tensor_tensor(out=gw[:, t, kx:kx + 1], in0=rt[:, 13:14], in1=rt[:, 8 + kx:9 + kx], op=ALU.mult), r=[b_r], w=[b_rt])
                mk.idma(Xg, hb[i][:, :], slot[:, t, kx:kx + 1], True, NE * CAP - 1, reads=[b_hb[i], b_rt], writes=[b_Xg])
        for t in range(NT):
            front(t)
            tail(t)
        mk.barrier()
    b_Yg = Buf("Yg")
    with ExitStack() as pb:
        sb = lambda n, s, dt=F32: pb.enter_context(nc.sbuf_tensor("k3b%d_" % mk.gen + n, list(s), dt))
        ps = lambda n, s, dt=F32: pb.enter_context(nc.psum_tensor("k3b%d_" % mk.gen + n, list(s), dt))
        W1 = [sb("w1_%d" % i, [128, 16, DE], BF16) for i in range(2)]
        W3 = [sb("w3_%d" % i, [128, 16, DE], BF16) for i in range(2)]
        W2 = [sb("w2_%d" % i, [128, 4, D], BF16) for i in range(2)]
        b_w = [Buf(), Buf()]
        NTL = CAP // 128
        xg = sb("xg", [128, NTL, D], BF16); b_xg = Buf()
        xgT = sb("xgT", [128, 16, CAP], BF16); b_xgT = Buf()
        ga = sb("ga", [128, CAP]); b_ga = Buf()
        gh = sb("gh", [128, 4, CAP], BF16); b_gh = Buf()
        yo = [sb("yo%d" % i, [128, D]) for i in range(2)]; b_yo = [Buf(), Buf()]
        p_t = [ps("p_t%d" % i, [128, 1024], BF16) for i in range(2)]; b_pt = [Buf(), Buf()]
        p_a = ps("p_a", [128, 512]); p_b = ps("p_b", [128, 512]); b_pa, b_pb = Buf(), Buf()
        p_o = [ps("p_o%d" % i, [128, 512]) for i in range(2)]; b_po = [Buf(), Buf()]

        def load_w(e):
            j = e % 2
            mk.dma("pool", W1[j][:], w1[e].rearrange("(p k) c -> p k c", k=16), writes=[b_w[j]])
            mk.dma("pool", W3[j][:], w3[e].rearrange("(p k) c -> p k c", k=16), writes=[b_w[j]])
            mk.dma("pool", W2[j][:], w2[e].rearrange("(k p) c -> p k c", p=128), writes=[b_w[j]])

        load_w(0)
        yoi = 0
        for e in range(NE):
            j = e % 2
            if e + 1 < NE:
                load_w(e + 1)
            mk.dma("sp", xg[:], Xg[e * CAP:(e + 1) * CAP, :].rearrange("(a p) c -> p a c", p=128), reads=[b_Xg], writes=[b_xg])
            for a in range(NTL):
                for half in range(2):
                    for kk in range(8):
                        k = half * 8 + kk
                        PE(lambda: nc.tensor.transpose(p_t[half][:, kk * 128:(kk + 1) * 128], xg[:, a, :].rearrange("t (p k) -> t k p", k=16)[:, k, :], identb[:, :]), r=[b_xg, bc], w=[b_pt[half]])
                    if half == 0:
                        A(lambda: nc.scalar.copy(out=xgT[:, 0:8, a * 128:(a + 1) * 128], in_=p_t[half][:, :].rearrange("p (k t) -> p k t", t=128)), r=[], w=[b_pt[half], b_xgT])
                    else:
                        V(lambda: nc.vector.tensor_copy(out=xgT[:, 8:16, a * 128:(a + 1) * 128], in_=p_t[half][:, :].rearrange("p (k t) -> p k t", t=128)), r=[], w=[b_pt[half], b_xgT])
            for hc in range(4):
                for k in range(16):
                    PE(lambda: nc.tensor.matmul(p_a[:, 0:CAP], lhsT=W1[j][:, k, hc * 128:(hc + 1) * 128], rhs=xgT[:, k, :], start=(k == 0), stop=(k == 15)), r=[b_w[j], b_xgT], w=[b_pa])
                for k in range(16):
                    PE(lambda: nc.tensor.matmul(p_b[:, 0:CAP], lhsT=W3[j][:, k, hc * 128:(hc + 1) * 128], rhs=xgT[:, k, :], start=(k == 0), stop=(k == 15)), r=[b_w[j], b_xgT], w=[b_pb])
                A(lambda: nc.scalar.activation(out=ga[:], in_=p_a[:, 0:CAP], func=AF.Silu), r=[], w=[b_pa, b_ga])
                V(lambda: nc.vector.tensor_tensor(out=gh[:, hc, :], in0=p_b[:, 0:CAP], in1=ga[:], op=ALU.mult), r=[b_ga], w=[b_pb, b_gh])
            for a in range(NTL):
                y_ = yoi % 2
                yoi += 1
                for cc in range(4):
                    q = cc % 2
                    for hc in range(4):
                        PE(lambda: nc.tensor.matmul(p_o[q][:, :], lhsT=gh[:, hc, a * 128:(a + 1) * 128], rhs=W2[j][:, hc, cc * 512:(cc + 1) * 512], start=(hc == 0), stop=(hc == 3)), r=[b_gh, b_w[j]], w=[b_po[q]])
                    if q == 0:
                        A(lambda: nc.scalar.copy(out=yo[y_][:, cc * 512:(cc + 1) * 512], in_=p_o[q][:, :]), r=[], w=[b_po[q], b_yo[y_]])
                    else:
                        V(lambda: nc.vector.tensor_copy(out=yo[y_][:, cc * 512:(cc + 1) * 512], in_=p_o[q][:, :]), r=[], w=[b_po[q], b_yo[y_]])
                r0 = e * CAP + a * 128
                mk.dma("sp", Yg[r0:r0 + 128, :], yo[y_][:], reads=[b_yo[y_]], writes=[b_Yg])
        mk.barrier()
    with ExitStack() as pc:
        sb = lambda n, s, dt=F32: pc.enter_context(nc.sbuf_tensor("k3c%d_" % mk.gen + n, list(s), dt))
        R = [sb("row%d" % i, [128, D]) for i in range(3)]
        for i, ri in enumerate((5, 6, 7)):
            rowload(R[i], ri, bc)
        G(lambda: nc.gpsimd.tensor_scalar(out=R[0][:], in0=R[0][:], scalar1=1.0, scalar2=None, op0=ALU.add), r=[bc], w=[bc])
        Y = [[sb("Y%d_%d" % (k, i), [128, D]) for i in range(2)] for k in range(2)]
        b_Y = [[Buf(), Buf()], [Buf(), Buf()]]
        for k in range(2):
            for i in range(2):
                V(lambda: nc.vector.memset(Y[k][i][:], 0.0), w=[b_Y[k][i]])
        x1t = [sb("x1t%d" % i, [128, D]) for i in range(2)]; b_x1 = [Buf(), Buf()]
        yml = [sb("ym%d" % i, [128, D]) for i in range(2)]; b_yml = [Buf(), Buf()]
        xnl = [sb("xn%d" % i, [128, D]) for i in range(2)]; b_xnl = [Buf(), Buf()]
        ot = [sb("ot%d" % i, [128, D]) for i in range(2)]; b_ot = [Buf(), Buf()]
        stl = [sb("st%d" % i, [128, 4, 6]) for i in range(2)]; mvl = [sb("mv%d" % i, [128, 2]) for i in range(2)]
        rsl = [sb("rs%d" % i, [128, 1]) for i in range(2)]; nmrl = [sb("nmr%d" % i, [128, 1]) for i in range(2)]; b_sl = [Buf(), Buf()]
        for t in range(NT):
            i = t % 2
            ts_ = slice(t * 128, (t + 1) * 128)
            ym = yml[i]; b_ym = b_yml[i]; xn = xnl[i]; b_xn = b_xnl[i]
            st, mv, rs, nmr, b_s = stl[i], mvl[i], rsl[i], nmrl[i], b_sl[i]
            for k in range(2):
                mk.idma(Y[k][i][:, :], Yg, slot[:, t, k:k + 1], False, NE * CAP - 1, reads=[b_Yg, b_rt], writes=[b_Y[k][i]])
            mk.dma("sp", x1t[i][:], x1s[ts_, :], writes=[b_x1[i]])
            A(lambda: nc.scalar.activation(out=ym[:], in_=Y[0][i][:], func=AF.Copy, scale=gw[:, t, 0:1]), r=[b_Y[0][i], b_rt], w=[b_ym])
            V(lambda: nc.vector.scalar_tensor_tensor(out=ym[:], in0=Y[1][i][:], scalar=gw[:, t, 1:2], in1=ym[:], op0=ALU.mult, op1=ALU.add), r=[b_Y[1][i], b_rt], w=[b_ym])
            V(lambda: nc.vector.tensor_tensor(out=ym[:], in0=ym[:], in1=R[0][:], op=ALU.mult), r=[bc], w=[b_ym])
            V(lambda: nc.vector.scalar_tensor_tensor(out=ym[:], in0=x1t[i][:], scalar=ALPHA, in1=ym[:], op0=ALU.mult, op1=ALU.add), r=[b_x1[i]], w=[b_ym])
            ln_stats(nc, mk, ym, b_ym, st, mv, rs, nmr, b_s)
            A(lambda: nc.scalar.activation(out=xn[:], in_=ym[:], func=AF.Identity, bias=nmr[:, 0:1], scale=rs[:, 0:1]), r=[b_ym, b_s], w=[b_xn])
            V(lambda: nc.vector.tensor_tensor(out=xn[:], in0=xn[:], in1=R[1][:], op=ALU.mult), r=[bc], w=[b_xn])
            V(lambda: nc.vector.tensor_tensor(out=ot[i][:], in0=xn[:], in1=R[2][:], op=ALU.add), r=[b_xn, bc], w=[b_ot[i]])
            mk.dma("sp", xout[ts_, :], ot[i][:], reads=[b_ot[i]], is_output=True)
        mk.barrier()


def build_k3():
    nc = bass.Bass("TRN2", target_bir_lowering=False)
    dt = lambda n, s, k="ExternalInput", d=F32: nc.dram_tensor(n, s, d, kind=k).ap()
    ymixT = dt("ymixT", [D, TOK]); x = dt("x", [TOK, D]); w_out = dt("w_out", [D, D]); glu_w = dt("glu_w", [512, 512]); glu_b = dt("glu_b", [128, 4])
    rows = dt("rows", [8, 128, D]); wr = dt("wr", [D, 36]); br = dt("br", [128, 36])
    w1 = dt("w1", [NE, D, DE]); w3 = dt("w3", [NE, D, DE]); w2 = dt("w2", [NE, DE, D])
    cst = {"U": dt("U", [128, 128]), "ecap": dt("ecap", [128, NE]), "ident": dt("ident", [128, 128])}
    x1s = dt("x1s", [TOK, D], "Internal"); Xg = dt("Xg", [NE * CAP, D], "Internal", BF16); Yg = dt("Yg", [NE * CAP, D], "Internal")
    xout = dt("xout", [TOK, D], "ExternalOutput")
    with ExitStack() as ctx:
        mk = MK(nc, ctx)
        emit_k3(nc, mk, ymixT, x, w_out, glu_w, glu_b, rows, wr, br, w1, w3, w2, cst, x1s, Xg, Yg, xout)
        mk.finish("sp")
        print("k3 ops", mk.nops, "waits", mk.nwaits)
    return nc


def k3_host_inputs(prm, mod, l, b):
    rep = lambda v: np.ascontiguousarray(np.broadcast_to(v[None, :], (128, v.shape[0])))
    sh1, sc1, gt1, sh2, sc2, gt2 = [mod[l, b, i * D:(i + 1) * D] for i in range(6)]
    rows = np.stack([rep(gt1), rep(prm["ln_g"][l, 0]), rep(prm["ln_b"][l, 0]), rep(sc2), rep(sh2), rep(gt2), rep(prm["ln_g"][l, 1]), rep(prm["ln_b"][l, 1])])
    wr = np.ascontiguousarray(np.concatenate([prm["router_group_w"][l], prm["router_expert_w"][l]], axis=1))
    br = rep(np.concatenate([prm["router_group_b"][l], prm["router_expert_b"][l]]))
    d = {"rows": rows, "wr": wr, "br": br, "w_out": prm["w_out"][l], "glu_w": prm["s5_glu_w"][l],
         "glu_b": np.ascontiguousarray(prm["s5_glu_b"][l].reshape(4, 128).T),
         "w1": prm["moe_w1"][l], "w3": prm["moe_w3"][l], "w2": prm["moe_w2"][l]}
    d.update(k3_consts())
    return d


G_ = 512
RW_OFF = 3 * G_
RW_COLS = 3 * G_ + 96 + 96 + 128
ATT_OFF = RW_OFF + RW_COLS
S5_OFF = ATT_OFF + 512 + 2 * 128
SEQ = 8192
RG4 = [[0, 1, 2, 3], [4, 5, 6, 7]]
NMINE = 1472
MODC = 3072


def emit_k0f(nc, mk, cT, w, bb, modin):
    ct = mk.sb("k0_ct", [128, 16, 2]); sct = mk.sb("k0_sct", [128, 16, 2])
    wt = [mk.sb("k0_wt%d" % i, [128, 16, 512]) for i in range(2)]
    bt = mk.sb("k0_bt", [2, 2, MODC]); ot = mk.sb("k0_ot", [2, 2, MODC])
    P = [mk.ps("k0_P%d" % i, [2, 512]) for i in range(2)]
    b_c, b_b, b_o = Buf(), Buf(), Buf()
    b_w = [Buf(), Buf()]; b_p = [Buf(), Buf()]
    mk.dma("sp", ct[:], cT, writes=[b_c])
    mk.dma("sp", bt[:], bb.rearrange("l b c -> b l c"), writes=[b_b])
    mk.op("act", lambda: nc.scalar.activation(out=sct[:], in_=ct[:], func=AF.Silu), reads=[b_c], writes=[b_c])
    it = 0
    for l in range(2):
        for n in range(MODC // 512):
            i = it % 2
            it += 1
            mk.dma("sp", wt[i][:], w[l, :, n * 512:(n + 1) * 512].rearrange("(k p) c -> p k c", p=128), writes=[b_w[i]])
            for k in range(16):
                mk.op("pe", lambda: nc.tensor.matmul(P[i][:], lhsT=sct[:, k, :], rhs=wt[i][:, k, :], start=(k == 0), stop=(k == 15)),
                      reads=[b_c, b_w[i]], writes=[b_p[i]], skip_same=True)
            mk.op("dve", lambda: nc.vector.tensor_tensor(out=ot[:, l, n * 512:(n + 1) * 512], in0=P[i][:], in1=bt[:, l, n * 512:(n + 1) * 512], op=ALU.add),
                  reads=[b_b], writes=[b_p[i], b_o])
    mk.dma("sp", modin.rearrange("(o l) c -> o l c", o=1), ot[0:1, :, :], reads=[b_o])


def mod_row_load(nc, mk, dst, modall, l, chunk, bc):
    c0 = chunk * 2048
    done = 0
    while done < 2048:
        col = c0 + done
        r = col // MODC
        off = col % MODC
        n = min(2048 - done, MODC - off)
        src = modall[r * 2 + l:r * 2 + l + 1, off:off + n].partition_broadcast(128)
        mk.dma("sp", dst[:, done:done + n], src, writes=[bc])
        done += n


def emit_k1a(nc, mk, x, modall, l, ident_d, hTs):
    NT_ = TOK // 128
    xt = [mk.sb("a_xt%d" % i, [128, D]) for i in range(2)]
    xn = mk.sb("a_xn", [128, D]); h1 = mk.sb("a_h1", [128, D])
    hb = [mk.sb("a_hb%d" % i, [128, D], BF16) for i in range(2)]
    hT = mk.sb("a_hT", [128, 16, TOK], BF16)
    sct = mk.sb("a_sct", [128, D]); sht = mk.sb("a_sht", [128, D])
    idf = mk.sb("a_idf", [128, 128]); idb = mk.sb("a_idb", [128, 128], BF16)
    st = mk.sb("a_st", [128, 4, 6]); mv = mk.sb("a_mv", [128, 2]); rs = mk.sb("a_rs", [128, 1]); nmr = mk.sb("a_nmr", [128, 1])
    PT = [mk.ps("a_PT%d" % i, [128, 8, 128], BF16) for i in range(2)]
    b_x = [Buf(), Buf()]
    b_xn, b_h1, b_s, b_sc, b_sh, b_id, b_hT = Buf(), Buf(), Buf(), Buf(), Buf(), Buf(), Buf()
    b_hb = [Buf(), Buf()]; b_pt = [Buf(), Buf()]
    mod_row_load(nc, mk, sct, modall, l, 1, b_sc)
    mod_row_load(nc, mk, sht, modall, l, 0, b_sh)
    mk.dma("sp", idf[:], ident_d, writes=[b_id])
    mk.op("dve", lambda: nc.vector.tensor_copy(out=idb[:], in_=idf[:]), reads=[b_id], writes=[b_id])
    mk.op("pool", lambda: nc.gpsimd.tensor_scalar(out=sct[:], in0=sct[:], scalar1=1.0, scalar2=None, op0=ALU.add), reads=[b_sc], writes=[b_sc])
    for t in range(NT_):
        i = t % 2
        mk.dma("sp", xt[i][:], x[t * 128:(t + 1) * 128, :], writes=[b_x[i]])
        ln_stats(nc, mk, xt[i], b_x[i], st, mv, rs, nmr, b_s)
        mk.op("act", lambda: nc.scalar.activation(out=xn[:], in_=xt[i][:], func=AF.Identity, bias=nmr[:, 0:1], scale=rs[:, 0:1]),
              reads=[b_x[i], b_s], writes=[b_xn])
        mk.op("dve", lambda: nc.vector.tensor_tensor(out=h1[:], in0=xn[:], in1=sct[:], op=ALU.mult), reads=[b_xn, b_sc], writes=[b_h1])
        mk.op("dve", lambda: nc.vector.tensor_tensor(out=hb[i][:], in0=h1[:], in1=sht[:], op=ALU.add), reads=[b_h1, b_sh], writes=[b_hb[i]])
        for half in range(2):
            for kk in range(8):
                k = half * 8 + kk
                mk.op("pe", lambda: nc.tensor.transpose(PT[half][:, kk, :], hb[i][:, k * 128:(k + 1) * 128], idb[:]),
                      reads=[b_hb[i], b_id], writes=[b_pt[half]], skip_same=True)
            if half == 0:
                mk.op("act", lambda: nc.scalar.copy(out=hT[:, 0:8, t * 128:(t + 1) * 128], in_=PT[half][:]), reads=[], writes=[b_pt[half], b_hT])
            else:
                mk.op("dve", lambda: nc.vector.tensor_copy(out=hT[:, 8:16, t * 128:(t + 1) * 128], in_=PT[half][:]), reads=[], writes=[b_pt[half], b_hT])
    mk.dma("sp", hTs.rearrange("(k p) t -> p k t", p=128), hT[:], reads=[b_hT])


def emit_k1b(nc, mk, hTg, wmine, pmine):
    NB_ = (NMINE + 127) // 128
    wt = mk.sb("b_wt", [128, 16, NB_ * 128], BF16); b_w = Buf()
    ht = [mk.sb("b_ht%d" % i, [128, 16, 512], BF16) for i in range(2)]; b_h = [Buf(), Buf()]
    ot = [mk.sb("b_ot%d" % i, [128, 512]) for i in range(4)]; b_o = [Buf() for _ in range(4)]
    PM = [mk.ps("b_PM%d" % i, [128, 512]) for i in range(4)]; b_pm = [Buf() for _ in range(4)]
    for j in range(NB_):
        c0 = j * 128
        cw = min(128, NMINE - c0)
        mk.dma("pool", wt[:, :, c0:c0 + cw], wmine[:, c0:c0 + cw].rearrange("(k p) c -> p k c", p=128), writes=[b_w])
    pi = 0
    for tc in range(SEQ // 512):
        i = tc % 2
        r = tc // 4
        t0 = (tc % 4) * 512
        src = hTg.rearrange("(c r h p) t -> r p c h t", c=8, r=4, h=2, p=128)[r]
        for c in range(8):
            mk.dma("sp", ht[i][:, 2 * c:2 * c + 2, :], src[:, c, :, t0:t0 + 512], writes=[b_h[i]])
        for j in range(NB_):
            c0 = j * 128
            cw = min(128, NMINE - c0)
            q = pi % 4
            pi += 1
            for k in range(16):
                mk.op("pe", lambda: nc.tensor.matmul(PM[q][0:cw, :], lhsT=wt[:, k, c0:c0 + cw], rhs=ht[i][:, k, :], start=(k == 0), stop=(k == 15)),
                      reads=[b_w, b_h[i]], writes=[b_pm[q]], skip_same=True)
            if q % 2 == 0:
                mk.op("act", lambda: nc.scalar.copy(out=ot[q][0:cw, :], in_=PM[q][0:cw, :]), reads=[], writes=[b_pm[q], b_o[q]])
            else:
                mk.op("dve", lambda: nc.vector.tensor_copy(out=ot[q][0:cw, :], in_=PM[q][0:cw, :]), reads=[], writes=[b_pm[q], b_o[q]])
            mk.dma("sp", pmine[c0:c0 + cw, tc * 512:(tc + 1) * 512], ot[q][0:cw, :], reads=[b_o[q]])


def build_fused():
    nc = bass.Bass("TRN2", target_bir_lowering=False)
    T = SEQ
    din = lambda n, s, d=F32: nc.dram_tensor(n, s, d, kind="ExternalInput").ap()
    scr = lambda n, s, d=F32: nc.dram_tensor(n, s, d).ap()
    x_in = din("x", [TOK, D]); cT = din("cT", [128, 16, 2]); w_ada = din("w_ada", [2, D, MODC]); bb = din("bb", [2, 2, MODC])
    wmine = din("wmine", [2, D, NMINE]); w_out = din("w_out", [2, D, D]); lnp = din("lnp", [2, 4, D])
    cw = din("cw", [2, 128, 3]); btab = din("btab", [2, 2, 128, 2, 256]); sinkt = din("sinkt", [2, 128, 2]); ident = din("ident", [128, 128])
    s5par = din("s5par", [2, 128, 4, 3]); s5bb = din("s5bb", [2, 128, 4, 2, 16]); s5cc = din("s5cc", [2, 128, 4, 2, 16]); s5d = din("s5d", [2, 128, 1]); s5iota = din("s5iota", [128, CS])
    par64 = din("par64", [2, 64, 2, 11]); par128 = din("par128", [2, 128, 3]); w2 = din("w2", [2, 96, 128]); a2 = din("a2", [2, 96, 128]); g2 = din("g2", [2, 128, 128]); gnt = din("gnt", [2, 64, 2, 2, 64])
    cst = {"mask1": din("mask1", [64, 512]), "mask3": din("mask3", [64, 256]), "seg": din("seg", [128, TB]), "ident": ident, "U": din("U", [128, 128]), "ecap": din("ecap", [128, NE])}
    glu_w = din("glu_w", [2, 512, 512]); glu_b = din("glu_b", [2, 128, 4]); wr = din("wr", [2, D, 36]); br = din("br", [2, 128, 36])
    w1 = din("w1", [2, NE, D, DE]); w3 = din("w3", [2, NE, D, DE]); w2m = din("w2m", [2, NE, DE, D])
    ymidx_d = din("ymidx", [128, 16, 2], I32)
    y_out = nc.dram_tensor("y", [TOK, D], F32, kind="ExternalOutput").ap()
    modin = scr("modin", [2, MODC]); modall = scr("modall", [8, MODC])
    hTs = scr("hTs", [D, TOK], BF16); hTg = scr("hTg", [4 * D, TOK], BF16)
    pmine = scr("pmine", [NMINE, T])
    yT16 = scr("yT16", [512, T], BF16); ymg = scr("ymg", [2048, T], BF16)
    x1s = scr("x1s", [TOK, D]); Xg = scr("Xg", [NE * CAP, D], BF16); Yg = scr("Yg", [NE * CAP, D]); xcur = scr("xcur", [TOK, D])
    ymg_rows = ymg.rearrange("r (tb t) -> (r tb) t", t=1024)
    with ExitStack() as ctx:
        mk = MK(nc, ctx)
        bD = Buf("dram")
        with mk.scope():
            emit_k0f(nc, mk, cT, w_ada, bb, modin)
        mk.collective("AllGather", RG4, modin, modall, reads=[bD], writes=[bD])
        mk.barrier()
        for l in range(2):
            xsrc = x_in if l == 0 else xcur
            xdst = xcur if l == 0 else y_out
            with mk.scope():
                emit_k1a(nc, mk, xsrc, modall, l, ident, hTs)
            for c in range(8):
                mk.collective("AllGather", RG4, hTs[c * 256:(c + 1) * 256, :], hTg[c * 1024:(c + 1) * 1024, :], reads=[bD], writes=[bD])
            mk.barrier()
            with mk.scope():
                emit_k1b(nc, mk, hTg, wmine[l], pmine)
            def ag(chunks):
                for c in chunks:
                    mk.collective("AllGather", RG4, yT16[c * 64:(c + 1) * 64, :], ymg[c * 256:(c + 1) * 256, :], reads=[], writes=[Buf()])
            with mk.scope():
                emit_conv(nc, mk, T, pmine[0:384, :], cw[l], yT16[0:128, :], odt=BF16)
            ag((0, 1))
            with mk.scope():
                emit_attn(nc, mk, T, pmine[1088:1344, :], btab[l], sinkt[l], ident, yT16[256:384, :], odt=BF16)
            ag((4, 5))
            with mk.scope():
                emit_s5(nc, mk, T, pmine[1344:1472, :], s5par[l], s5bb[l], s5cc[l], s5d[l], s5iota, ident, yT16[384:512, :], odt=BF16)
            ag((6, 7))
            with mk.scope():
                emit_rwkv(nc, mk, T, pmine[384:1088, :], par64[l], par128[l], w2[l], a2[l], g2[l], gnt[l], cst, yT16[128:256, :], odt=BF16)
            for c in (2, 3):
                mk.collective("AllGather", RG4, yT16[c * 64:(c + 1) * 64, :], ymg[c * 256:(c + 1) * 256, :], reads=[bD], writes=[bD])
            mk.barrier()
            with mk.scope():
                ymidx = mk.sb("ymidx_sb", [128, 16, 2], I32); b_idx = Buf()
                mk.dma("sp", ymidx[:], ymidx_d, writes=[b_idx])

                def ymload(hf, ymt, b_ymt):
                    for k in range(16):
                        mk.idma(ymt[:, k, :], ymg_rows, ymidx[:, k, hf:hf + 1], False, 2048 * 8 - 1, reads=[b_idx], writes=[b_ymt])

                def rowload(dst, ri, bcx, _l=l):
                    if ri in (1, 2, 6, 7):
                        j = {1: 0, 2: 1, 6: 2, 7: 3}[ri]
                        mk.dma("sp", dst[:], lnp[_l, j:j + 1, :].partition_broadcast(128), writes=[bcx])
                    else:
                        chunk = {0: 2, 3: 4, 4: 3, 5: 5}[ri]
                        mod_row_load(nc, mk, dst, modall, _l, chunk, bcx)

                emit_k3(nc, mk, None, xsrc, w_out[l], glu_w[l], glu_b[l], None, wr[l], br[l], w1[l], w3[l], w2m[l], cst, x1s, Xg, Yg, xdst,
                        ymload=ymload, rowload=rowload)
        mk.finish("sp")
        mk.barrier()
        print("fused ops", mk.nops, "waits", mk.nwaits)
    return nc


_NC_CACHE = {}


def _get(name, fn):
    if name not in _NC_CACHE:
        _NC_CACHE[name] = fn()
    return _NC_CACHE[name]


def _fused_inputs(prm, core):
    b, q = core // 4, core % 4
    eye = np.eye(128, dtype=np.float32)
    d = {}
    d["x"] = np.ascontiguousarray(prm["x"][b, q * TOK:(q + 1) * TOK])
    cb = prm["c"][b]
    d["cT"] = np.ascontiguousarray(np.stack([cb.reshape(16, 128).T, cb.reshape(16, 128).T], axis=-1))
    sl = slice(q * MODC, (q + 1) * MODC)
    d["w_ada"] = np.ascontiguousarray(prm["w_ada"][:, :, sl])
    d["bb"] = np.ascontiguousarray(np.broadcast_to(prm["b_ada"][:, None, sl], (2, 2, MODC)))
    kv = q // 2
    cols = np.concatenate([np.arange(128 * q, 128 * q + 128), np.arange(G_ + 128 * q, G_ + 128 * q + 128), np.arange(2 * G_ + 128 * q, 2 * G_ + 128 * q + 128),
                           RW_OFF + rwkv_rows(q),
                           np.arange(ATT_OFF + 128 * q, ATT_OFF + 128 * q + 128), np.arange(ATT_OFF + 512 + 64 * kv, ATT_OFF + 512 + 64 * kv + 64),
                           np.arange(ATT_OFF + 640 + 64 * kv, ATT_OFF + 640 + 64 * kv + 64),
                           np.arange(S5_OFF + 128 * q, S5_OFF + 128 * q + 128)])
    assert cols.shape[0] == NMINE
    d["wmine"] = np.ascontiguousarray(prm["w_in"][:, :, cols])
    d["w_out"] = prm["w_out"]
    d["lnp"] = np.ascontiguousarray(np.stack([np.stack([prm["ln_g"][l, 0], prm["ln_b"][l, 0], prm["ln_g"][l, 1], prm["ln_b"][l, 1]]) for l in range(2)]))
    d["cw"] = np.ascontiguousarray(np.stack([prm["conv_w"][l][:, 128 * q:128 * q + 128].T for l in range(2)]))
    tabs = [attn_tables(prm["rel_bias"], prm["attn_sinks"][l], q) for l in range(2)]
    d["btab"] = np.ascontiguousarray(np.stack([t[0] for t in tabs])); d["sinkt"] = np.ascontiguousarray(np.stack([t[1] for t in tabs]))
    d["ident"] = eye
    s5 = [s5_host_inputs(prm, l, q) for l in range(2)]
    for k_ in ("s5par", "s5bb", "s5cc", "s5d"):
        d[k_] = np.ascontiguousarray(np.stack([s[k_] for s in s5]))
    d["s5iota"] = s5[0]["s5iota"]
    rw = [rwkv_host_inputs(prm, l, q) for l in range(2)]
    for k_ in ("par64", "par128", "gnt", "w2", "a2", "g2"):
        d[k_] = np.ascontiguousarray(np.stack([r[k_] for r in rw]))
    for k_ in ("mask1", "mask3", "seg"):
        d[k_] = rw[0][k_]
    kc = k3_consts()
    d["U"] = kc["U"]; d["ecap"] = kc["ecap"]
    d["glu_w"] = prm["s5_glu_w"]
    d["glu_b"] = np.ascontiguousarray(np.stack([prm["s5_glu_b"][l].reshape(4, 128).T for l in range(2)]))
    d["wr"] = np.ascontiguousarray(np.stack([np.concatenate([prm["router_group_w"][l], prm["router_expert_w"][l]], axis=1) for l in range(2)]))
    d["br"] = np.ascontiguousarray(np.stack([np.broadcast_to(np.concatenate([prm["router_group_b"][l], prm["router_expert_b"][l]])[None, :], (128, 36)) for l in range(2)]))
    d["w1"] = prm["moe_w1"]; d["w3"] = prm["moe_w3"]; d["w2m"] = prm["moe_w2"]
    p_ = np.arange(128)[:, None, None]; k_i = np.arange(16)[None, :, None]; t_ = np.arange(2)[None, None, :]
    src_row = ((k_i // 4) * 2 + p_ // 64) * 256 + (k_i % 4) * 64 + p_ % 64
    d["ymidx"] = np.ascontiguousarray((src_row * 8 + q * 2 + t_).astype(np.int32))
    return d


def kernel(**inp):
    prm = {k: np.ascontiguousarray(np.asarray(v, dtype=np.float32)) for k, v in inp.items()}
    cores = list(range(8))
    in_maps = [_fused_inputs(prm, c) for c in cores]
    res = run_bass_kernel_spmd(_get("fused", build_fused), in_maps, core_ids=cores)
    out = np.stack([np.concatenate([res.results[b * 4 + q]["y"] for q in range(4)], axis=0) for b in range(2)])
    return out.astype(np.float32)
```

```python
import numpy as np
from contextlib import ExitStack
import concourse.bass as bass
import concourse.mybir as mybir
from concourse.bass_utils import run_bass_kernel_spmd

F32 = mybir.dt.float32
BF16 = mybir.dt.bfloat16
I32 = mybir.dt.int32
U32 = mybir.dt.uint32
AF = mybir.ActivationFunctionType
ALU = mybir.AluOpType
AX = mybir.AxisListType

EPOCH = 1 << 20


class Buf:
    __slots__ = ("name", "w", "r")

    def __init__(self, name=""):
        self.name = name
        self.w = None
        self.r = {}


class MK:
    def __init__(self, nc, ctx, n_dma_sems=24):
        self.nc = nc
        self.ctx = ctx
        self.eng = {"pe": nc.tensor, "dve": nc.vector, "act": nc.scalar,
                    "pool": nc.gpsimd, "sp": nc.sync}
        self.sem = {}
        self.cnt = {e: 0 for e in self.eng}
        self.known = {e: {} for e in self.eng}
        for e in self.eng:
            self.sem[e] = ctx.enter_context(nc.semaphore("s_" + e))
        self.dma_keys = []
        self.dma_val = {}
        for i in range(n_dma_sems):
            k = ("dma", i)
            self.sem[k] = ctx.enter_context(nc.semaphore("s_dma%d" % i))
            self.dma_keys.append(k)
            self.dma_val[k] = 0
        self.dma_rr = 0
        self.nwaits = 0
        self.nops = 0
        self.out_events = []

    gen = 0

    def sb(self, name, shape, dt=F32):
        return self.ctx.enter_context(self.nc.sbuf_tensor("%s_g%d" % (name, self.gen), list(shape), dt))

    def ps(self, name, shape, dt=F32):
        return self.ctx.enter_context(self.nc.psum_tensor("%s_g%d" % (name, self.gen), list(shape), dt))

    def _wait(self, E, ev):
        if ev is None:
            return
        key, val = ev
        if self.known[E].get(key, 0) >= val:
            return
        self.eng[E].wait_ge(self.sem[key], val)
        self.known[E][key] = val
        self.nwaits += 1

    def _deps(self, E, reads, writes, skip_same=False):
        for b in reads:
            if b.w is not None and not (skip_same and b.w[0] == E):
                self._wait(E, b.w)
        for b in writes:
            if b.w is not None and not (skip_same and b.w[0] == E):
                self._wait(E, b.w)
            for ev in b.r.values():
                if not (skip_same and ev[0] == E):
                    self._wait(E, ev)

    def _mark(self, ev, reads, writes):
        for b in reads:
            b.r[ev[0]] = ev
        for b in writes:
            b.w = ev
            b.r = {}

    def op(self, E, fn, reads=(), writes=(), skip_same=False):
        self._deps(E, reads, writes, skip_same)
        inst = fn()
        self.cnt[E] += 1
        inst.then_inc(self.sem[E], 1)
        ev = (E, self.cnt[E])
        self._mark(ev, reads, writes)
        self.nops += 1
        return ev

    def dma(self, Q, out, in_, reads=(), writes=(), is_output=False, **kw):
        self._deps(Q, reads, writes)
        k = self.dma_keys[self.dma_rr]
        self.dma_rr = (self.dma_rr + 1) % len(self.dma_keys)
        self._wait(Q, (k, self.dma_val[k]) if self.dma_val[k] else None)
        self.dma_val[k] += 16
        inst = self.eng[Q].dma_start(out=out, in_=in_, **kw)
        inst.then_inc(self.sem[k], 16)
        ev = (k, self.dma_val[k])
        self._mark(ev, reads, writes)
        if is_output:
            self.out_events.append(ev)
        self.nops += 1
        return ev

    def finish(self, E="sp"):
        for k in self.dma_keys:
            if self.dma_val[k]:
                self._wait(E, (k, self.dma_val[k]))


def _idma(self, out, in_, idx_ap, scatter, bound, reads=(), writes=(), is_output=False):
    Q = "pool"
    self._deps(Q, reads, writes)
    k = self.dma_keys[self.dma_rr]
    self.dma_rr = (self.dma_rr + 1) % len(self.dma_keys)
    self._wait(Q, (k, self.dma_val[k]) if self.dma_val[k] else None)
    self.dma_val[k] += 16
    off = bass.IndirectOffsetOnAxis(ap=idx_ap, axis=0)
    if not hasattr(self, "_bregs"):
        self._bregs = {}
    if bound not in self._bregs:
        self._bregs[bound] = self.nc.gpsimd.to_reg(bound)
    bound = self._bregs[bound]
    if scatter:
        inst = self.nc.gpsimd.indirect_dma_start(out=out, out_offset=off, in_=in_, in_offset=None, bounds_check=bound, oob_is_err=False)
    else:
        inst = self.nc.gpsimd.indirect_dma_start(out=out, out_offset=None, in_=in_, in_offset=off, bounds_check=bound, oob_is_err=False)
    inst.then_inc(self.sem[k], 16)
    ev = (k, self.dma_val[k])
    self._mark(ev, reads, writes)
    if is_output:
        self.out_events.append(ev)
    self.nops += 1
    return ev


MK.idma = _idma


def _barrier(self):
    for E in self.eng:
        for F in self.eng:
            if self.cnt[F]:
                self._wait(E, (F, self.cnt[F]))
        for k in self.dma_keys:
            if self.dma_val[k]:
                self._wait(E, (k, self.dma_val[k]))
        if getattr(self, "cc_val", 0):
            self._wait(E, ("cc", self.cc_val))


MK.barrier = _barrier


from contextlib import contextmanager


@contextmanager
def _scope(self):
    old = self.ctx
    self.gen += 1
    with ExitStack() as s:
        self.ctx = s
        yield
        self.barrier()
    self.ctx = old


MK.scope = _scope


def _collective(self, kind, rg, in_ap, out_ap, reads=(), writes=()):
    Q = "pool"
    if "cc" not in self.sem:
        self.sem["cc"] = self.ctx.enter_context(self.nc.semaphore("s_cc"))
        self.cc_val = 0
    self._deps(Q, reads, writes)
    self.cc_val += 1
    inst = self.nc.gpsimd.collective_compute(kind, ALU.bypass, replica_groups=rg, ins=[in_ap.opt()], outs=[out_ap.opt()])
    inst.then_inc(self.sem["cc"], 1)
    ev = ("cc", self.cc_val)
    self._mark(ev, reads, writes)
    self.nops += 1
    return ev


MK.collective = _collective


import numpy as np
from contextlib import ExitStack

D = 2048
NIN = 4672
TOK = 2048


def build_k0():
    nc = bass.Bass("TRN2", target_bir_lowering=False)
    NCOL = 1536
    cT = nc.dram_tensor("cT", [128, 16, 2], F32, kind="ExternalInput").ap()
    w = nc.dram_tensor("w", [2, D, NCOL], F32, kind="ExternalInput").ap()
    bb = nc.dram_tensor("bb", [2, 2, NCOL], F32, kind="ExternalInput").ap()
    out = nc.dram_tensor("mod", [2, 2, NCOL], F32, kind="ExternalOutput").ap()
    with ExitStack() as ctx:
        mk = MK(nc, ctx)
        ct = mk.sb("ct", [128, 16, 2])
        sct = mk.sb("sct", [128, 16, 2])
        wt = [mk.sb("wt%d" % i, [128, 16, 512]) for i in range(2)]
        bt = mk.sb("bt", [2, 2, NCOL])
        ot = mk.sb("ot", [2, 2, NCOL])
        P = [mk.ps("P%d" % i, [2, 512]) for i in range(2)]
        b_c, b_b, b_o = Buf(), Buf(), Buf()
        b_w = [Buf(), Buf()]
        b_p = [Buf(), Buf()]
        mk.dma("sp", ct[:], cT, writes=[b_c])
        mk.dma("sp", bt[:], bb.rearrange("l b c -> b l c"), writes=[b_b])
        mk.op("act", lambda: nc.scalar.activation(out=sct[:], in_=ct[:], func=AF.Silu), reads=[b_c], writes=[b_c])
        it = 0
        for l in range(2):
            for n in range(3):
                i = it % 2
                it += 1
                mk.dma("sp", wt[i][:], w[l, :, n * 512:(n + 1) * 512].rearrange("(k p) c -> p k c", p=128), writes=[b_w[i]])
                for k in range(16):
                    mk.op("pe", lambda: nc.tensor.matmul(P[i][:], lhsT=sct[:, k, :], rhs=wt[i][:, k, :], start=(k == 0), stop=(k == 15)),
                          reads=[b_c, b_w[i]], writes=[b_p[i]], skip_same=True)
                mk.op("dve", lambda: nc.vector.tensor_tensor(out=ot[:, l, n * 512:(n + 1) * 512], in0=P[i][:], in1=bt[:, l, n * 512:(n + 1) * 512], op=ALU.add),
                      reads=[b_p[i], b_b], writes=[b_o])
        mk.dma("sp", out.rearrange("l b c -> b l c"), ot[:], reads=[b_o], is_output=True)
        mk.finish("sp")
    return nc


def build_k1():
    nc = bass.Bass("TRN2", target_bir_lowering=False)
    x = nc.dram_tensor("x", [TOK, D], F32, kind="ExternalInput").ap()
    sc = nc.dram_tensor("sc", [128, D], F32, kind="ExternalInput").ap()
    sh = nc.dram_tensor("sh", [128, D], F32, kind="ExternalInput").ap()
    w_in = nc.dram_tensor("w_in", [D, NIN], F32, kind="ExternalInput").ap()
    ident = nc.dram_tensor("ident", [128, 128], F32, kind="ExternalInput").ap()
    pT = nc.dram_tensor("pT", [NIN, TOK], F32, kind="ExternalOutput").ap()
    with ExitStack() as ctx:
        mk = MK(nc, ctx)
        emit_k1(nc, mk, x, sc, sh, w_in, ident, pT)
        mk.finish("sp")
        print("k1 ops", mk.nops, "waits", mk.nwaits)
    return nc


def ln_stats(nc, mk, xt, bx, st, mv, rs, nmr, bs, eps=1e-5):
    for c in range(4):
        mk.op("dve", lambda: nc.vector.bn_stats(out=st[:, c, :], in_=xt[:, c * 512:(c + 1) * 512]), reads=[bx], writes=[bs])
    mk.op("dve", lambda: nc.vector.bn_aggr(out=mv[:], in_=st[:].rearrange("p a b -> p (a b)")), reads=[bs], writes=[bs])
    mk.op("act", lambda: nc.scalar.activation(out=rs[:], in_=mv[:, 1:2], func=AF.Sqrt, bias=eps, scale=1.0), reads=[bs], writes=[bs])
    mk.op("dve", lambda: nc.vector.reciprocal(out=rs[:], in_=rs[:]), reads=[bs], writes=[bs])
    mk.op("dve", lambda: nc.vector.tensor_scalar(out=nmr[:], in0=mv[:, 0:1], scalar1=rs[:, 0:1], scalar2=-1.0, op0=ALU.mult, op1=ALU.mult),
          reads=[bs], writes=[bs])


def emit_k1(nc, mk, x, sc, sh, w_in, ident, pT):
    NT = TOK // 128
    xt = [mk.sb("xt%d" % i, [128, D]) for i in range(2)]
    xn = mk.sb("xn", [128, D])
    h1 = mk.sb("h1", [128, D])
    hb = [mk.sb("hb%d" % i, [128, D], BF16) for i in range(2)]
    hT = mk.sb("hT", [128, 16, TOK], BF16)
    sct = mk.sb("sct", [128, D])
    sht = mk.sb("sht", [128, D])
    idf = mk.sb("idf", [128, 128])
    idb = mk.sb("idb", [128, 128], BF16)
    st = mk.sb("st", [128, 4, 6])
    mv = mk.sb("mv", [128, 2])
    rs = mk.sb("rs", [128, 1])
    nmr = mk.sb("nmr", [128, 1])
    wt = [mk.sb("wt%d" % i, [128, 16, 128], BF16) for i in range(2)]
    ot = [mk.sb("ot%d" % i, [128, TOK]) for i in range(2)]
    PT = [mk.ps("PT%d" % i, [128, 8, 128], BF16) for i in range(2)]
    PM = [mk.ps("PM%d" % i, [128, 512]) for i in range(4)]
    b_x = [Buf(), Buf()]
    b_xn, b_h1, b_s, b_sc, b_sh, b_id, b_hT = Buf(), Buf(), Buf(), Buf(), Buf(), Buf(), Buf()
    b_hb = [Buf(), Buf()]
    b_pt = [Buf(), Buf()]
    b_pm = [Buf() for _ in range(4)]
    b_w = [Buf(), Buf()]
    b_o = [Buf(), Buf()]

    mk.dma("sp", sct[:], sc, writes=[b_sc])
    mk.dma("sp", sht[:], sh, writes=[b_sh])
    mk.dma("sp", idf[:], ident, writes=[b_id])
    mk.op("dve", lambda: nc.vector.tensor_copy(out=idb[:], in_=idf[:]), reads=[b_id], writes=[b_id])
    mk.op("pool", lambda: nc.gpsimd.tensor_scalar(out=sct[:], in0=sct[:], scalar1=1.0, scalar2=None, op0=ALU.add), reads=[b_sc], writes=[b_sc])

    NCB = (NIN + 127) // 128

    def load_w(cb):
        j = cb % 2
        c0 = cb * 128
        cw = min(128, NIN - c0)
        mk.dma("pool", wt[j][:, :, 0:cw], w_in[:, c0:c0 + cw].rearrange("(k p) c -> p k c", p=128), writes=[b_w[j]])

    load_w(0)
    load_w(1)
    for t in range(NT):
        i = t % 2
        mk.dma("sp", xt[i][:], x[t * 128:(t + 1) * 128, :], writes=[b_x[i]])
        ln_stats(nc, mk, xt[i], b_x[i], st, mv, rs, nmr, b_s)
        mk.op("act", lambda: nc.scalar.activation(out=xn[:], in_=xt[i][:], func=AF.Identity, bias=nmr[:, 0:1], scale=rs[:, 0:1]),
              reads=[b_x[i], b_s], writes=[b_xn])
        mk.op("dve", lambda: nc.vector.tensor_tensor(out=h1[:], in0=xn[:], in1=sct[:], op=ALU.mult), reads=[b_xn, b_sc], writes=[b_h1])
        mk.op("pool", lambda: nc.gpsimd.tensor_tensor(out=hb[i][:], in0=h1[:], in1=sht[:], op=ALU.add), reads=[b_h1, b_sh], writes=[b_hb[i]])
        for half in range(2):
            for kk in range(8):
                k = half * 8 + kk
                mk.op("pe", lambda: nc.tensor.transpose(PT[half][:, kk, :], hb[i][:, k * 128:(k + 1) * 128], idb[:]),
                      reads=[b_hb[i], b_id], writes=[b_pt[half]], skip_same=True)
            eng = "act" if half == 0 else "dve"
            if eng == "act":
                mk.op("act", lambda: nc.scalar.copy(out=hT[:, half * 8:(half + 1) * 8, t * 128:(t + 1) * 128], in_=PT[half][:]),
                      reads=[b_pt[half]], writes=[b_hT])
            else:
                mk.op("dve", lambda: nc.vector.tensor_copy(out=hT[:, half * 8:(half + 1) * 8, t * 128:(t + 1) * 128], in_=PT[half][:]),
                      reads=[b_pt[half]], writes=[b_hT])
    pi = 0
    for cb in range(NCB):
        j = cb % 2
        c0 = cb * 128
        cw = min(128, NIN - c0)
        for tc in range(TOK // 512):
            q = pi % 4
            pi += 1
            for k in range(16):
                mk.op("pe", lambda: nc.tensor.matmul(PM[q][0:cw, :], lhsT=wt[j][:, k, 0:cw], rhs=hT[:, k, tc * 512:(tc + 1) * 512],
                                                     start=(k == 0), stop=(k == 15)),
                      reads=[b_w[j], b_hT], writes=[b_pm[q]], skip_same=True)
            if tc % 2 == 0:
                mk.op("act", lambda: nc.scalar.copy(out=ot[j][0:cw, tc * 512:(tc + 1) * 512], in_=PM[q][0:cw, :]), reads=[b_pm[q]], writes=[b_o[j]])
            else:
                mk.op("dve", lambda: nc.vector.tensor_copy(out=ot[j][0:cw, tc * 512:(tc + 1) * 512], in_=PM[q][0:cw, :]), reads=[b_pm[q]], writes=[b_o[j]])
        mk.dma("sp", pT[c0:c0 + cw, :], ot[j][0:cw, :], reads=[b_o[j]], is_output=True)
        if cb + 2 < NCB:
            load_w(cb + 2)


import numpy as np
from contextlib import ExitStack

C = 64
TB = 512
NCH = TB // C


def rwkv_consts():
    s = np.arange(64)[:, None]
    t = np.arange(64)[None, :]
    m_su = (s < t).astype(np.float32)
    m_ui = (s <= t).astype(np.float32)
    m1 = np.concatenate([m_su, m_ui], axis=1)
    mask1 = np.tile(m1, (1, 4))
    m_sl = (t < s).astype(np.float32)
    mask3 = np.tile(m_sl, (1, 4))
    seg = np.ones((128, TB), np.float32)
    seg[:, ::C] = 0.0
    return {"mask1": mask1, "mask3": mask3, "seg": seg, "ident": np.eye(128, dtype=np.float32)}


def emit_rwkv(nc, mk, T, rwin, par64, par128, w2, a2, g2, gnt, cst, yT, odt=F32):
    import os
    LVL = int(os.environ.get("RW_LVL", "9"))
    NB = T // TB
    V = lambda fn, r=(), w=(): mk.op("dve", fn, r, w)
    A = lambda fn, r=(), w=(): mk.op("act", fn, r, w)
    G = lambda fn, r=(), w=(): mk.op("pool", fn, r, w)
    PE = lambda fn, r=(), w=(), ss=True: mk.op("pe", fn, r, w, skip_same=ss)
    import os
    F32R = mybir.dt.float32r
    USE_R = os.environ.get("RW_F32R", "1") == "1"
    RR = (lambda a: a.bitcast(F32R)) if USE_R else (lambda a: a)
    sb = mk.sb
    p64 = sb("rw_p64", [64, 2, 11]); p128 = sb("rw_p128", [128, 3])
    w2t = sb("rw_w2", [96, 128]); a2t = sb("rw_a2", [96, 128]); g2t = sb("rw_g2", [128, 128])
    gn = sb("rw_gn", [64, 2, 2, 64])
    mask1 = sb("rw_mask1", [64, 512]); mask3 = sb("rw_mask3", [64, 256]); seg = sb("rw_seg", [128, TB])
    ident = sb("rw_ident", [128, 128])
    ones64 = sb("rw_ones", [64, 64])
    bc = Buf("const")
    for dst, src in ((p64, par64), (p128, par128), (w2t, w2), (a2t, a2), (g2t, g2), (gn, gnt),
                     (mask1, cst["mask1"]), (mask3, cst["mask3"]), (seg, cst["seg"]), (ident, cst["ident"])):
        mk.dma("sp", dst[:], src, writes=[bc])
    V(lambda: nc.vector.memset(ones64[:], 1.0), w=[bc])
    raw = {}
    for nm in ("r0", "k0", "v0", "r1", "k1", "v1"):
        raw[nm] = sb("rw_raw_" + nm, [64, TB + 1])
    raw["w"] = sb("rw_raw_w", [96, TB + 1]); raw["a"] = sb("rw_raw_a", [96, TB + 1]); raw["g"] = sb("rw_raw_g", [128, TB + 1])
    b_raw = Buf("raw")
    tmp = sb("rw_tmp", [128, TB]); b_tmp = Buf("tmp")
    ws = sb("rw_ws", [96, TB]); as_ = sb("rw_as", [96, TB]); gs = sb("rw_gs", [128, TB]); b_lo = Buf("lo")
    gate = [sb("rw_gate%d" % i_, [128, TB]) for i_ in range(2)]; b_gate = [Buf("gate0"), Buf("gate1")]
    H = []
    for h in range(2):
        d = {}
        for nm in ("rs", "ks", "vs", "lw", "asg", "kkn", "kp", "bv", "cum", "e1", "e2", "BT", "KT", "BH", "KH", "rkr"):
            d[nm] = sb("rw_%s%d" % (nm, h), [64, TB])
        d["AR"] = sb("rw_AR%d" % h, [64, NCH, 128])
        d["cC"] = sb("rw_cC%d" % h, [64, NCH]); d["gC"] = sb("rw_gC%d" % h, [64, NCH])
        d["b"] = Buf("H%d" % h)
        d["bo"] = [Buf("Ho%d_0" % h), Buf("Ho%d_1" % h)]
        d["Vt"] = sb("rw_Vt%d" % h, [64, NCH, 64]); d["BHt"] = sb("rw_BHt%d" % h, [64, NCH, 64]); d["KHt"] = sb("rw_KHt%d" % h, [64, NCH, 64])
        d["bt"] = [Buf("Ht%d_0" % h), Buf("Ht%d_1" % h)]
        for nm_, shp_ in (("AR", [64, NCH, 128]), ("BT", [64, TB]), ("KT", [64, TB]), ("gC", [64, NCH]), ("Vt", [64, NCH, 64]), ("BHt", [64, NCH, 64]), ("KHt", [64, NCH, 64])):
            d[nm_] = [d[nm_], sb("rw_%s%d_b" % (nm_, h), shp_)]
        d["NG"] = sb("rw_NG%d" % h, [64, NCH, 128]); d["LG"] = sb("rw_LG%d" % h, [64, NCH, 128])
        d["L"] = sb("rw_L%d" % h, [64, NCH, 64]); d["bA"] = Buf("A%d" % h)
        d["P"] = [sb("rw_P%d_%d" % (h, i), [64, NCH, 64]) for i in range(2)]
        d["PT"] = [sb("rw_PT%d_%d" % (h, i), [64, NCH, 64]) for i in range(2)]
        d["ST"] = [sb("rw_ST%d_%d" % (h, i), [64, NCH, 64]) for i in range(2)]
        d["bD"] = Buf("D%d" % h)
        d["M"] = sb("rw_M%d" % h, [64, 64]); d["bM"] = Buf("M%d" % h)
        d["X1"] = sb("rw_X1%d" % h, [64, 64]); d["U"] = sb("rw_U%d" % h, [64, 64]); d["bX"] = Buf("X%d" % h); d["bU"] = Buf("U%d" % h)
        H.append(d)
    Yb = sb("rw_Yb", [64, NCH, 2, 64]); b_Y = Buf("Y")
    Ysq = sb("rw_Ysq", [64, NCH, 2, 64])
    st1 = sb("rw_st1", [64, NCH * 2]); st2 = sb("rw_st2", [64, NCH * 2]); st3 = sb("rw_st3", [64, NCH * 2]); b_st = Buf("st")
    sbon = [sb("rw_sbon%d" % i_, [64, NCH, 2]) for i_ in range(2)]; b_sb = [Buf("sbon0"), Buf("sbon1")]
    yo = sb("rw_yo", [128, TB], odt); b_yo = Buf("yo")
    ps_lo = mk.ps("rw_ps_lo", [128, 512]); b_pl = Buf()
    ps_tr = mk.ps("rw_ps_tr", [128, 512]); b_ptr = Buf()
    ps_a1 = mk.ps("rw_ps_a1", [64, 512]); b_pa1 = Buf()
    ps_a2 = mk.ps("rw_ps_a2", [64, 512]); b_pa2 = Buf()
    ps_a3f = mk.ps("rw_ps_a3", [128, 512]); ps_a3 = ps_a3f[0:64, :]; b_pa3 = Buf()
    ps_d = mk.ps("rw_ps_d", [64, 512]); b_pd = Buf()
    ps_d2 = ps_a3; b_pd2 = b_pa3
    ps_sh = [mk.ps("rw_ps_s%d" % h, [64, 512]) for h in range(2)]
    b_psh = [Buf(), Buf()]

    for h in range(2):
        V(lambda: nc.vector.tensor_scalar(out=RR(H[h]["M"][:]), in0=ident[0:64, 0:64], scalar1=0.0, scalar2=None, op0=ALU.mult), r=[bc], w=[H[h]["bM"]])

    rows = {"r0": 0, "r1": 64, "k0": 128, "k1": 192, "v0": 256, "v1": 320, "w": 384, "a": 480, "g": 576}
    nrow = {"r0": 64, "r1": 64, "k0": 64, "k1": 64, "v0": 64, "v1": 64, "w": 96, "a": 96, "g": 128}

    def stage1(blk):
        t0 = blk * TB
        par = blk % 2
        for nm in rows:
            r0, n = rows[nm], nrow[nm]
            if blk == 0:
                V(lambda: nc.vector.memset(raw[nm][0:n, 0:1], 0.0), w=[b_raw])
                yield
                mk.dma("sp", raw[nm][0:n, 1:TB + 1], rwin[r0:r0 + n, 0:TB], writes=[b_raw])
                yield
            else:
                mk.dma("sp", raw[nm][0:n, :], rwin[r0:r0 + n, t0 - 1:t0 + TB], writes=[b_raw])
                yield

        def shift(dst, src, n, mu_ap, bdst):
            V(lambda: nc.vector.tensor_tensor(out=tmp[0:n, :], in0=src[0:n, 0:TB], in1=src[0:n, 1:TB + 1], op=ALU.subtract), r=[b_raw], w=[b_tmp])
            V(lambda: nc.vector.scalar_tensor_tensor(out=dst[0:n, :], in0=tmp[0:n, :], scalar=mu_ap, in1=src[0:n, 1:TB + 1], op0=ALU.mult, op1=ALU.add),
              r=[b_tmp, b_raw, bc], w=[bdst])

        shift(ws, raw["w"], 96, p128[0:96, 0:1], b_lo)
        yield
        shift(as_, raw["a"], 96, p128[0:96, 1:2], b_lo)
        yield
        shift(gs, raw["g"], 128, p128[:, 2:3], b_lo)
        yield
        A(lambda: nc.scalar.activation(out=ws[:], in_=ws[:], func=AF.Tanh), r=[b_lo], w=[b_lo])
        yield
        A(lambda: nc.scalar.activation(out=gs[:], in_=gs[:], func=AF.Sigmoid), r=[b_lo], w=[b_lo])
        yield
        PE(lambda: nc.tensor.matmul(ps_lo[:, :], lhsT=g2t[:, :], rhs=gs[:, :], start=True, stop=True), r=[bc, b_lo], w=[b_pl])
        yield
        A(lambda: nc.scalar.copy(out=gate[par][:], in_=ps_lo[:, :]), r=[b_pl], w=[b_gate[par]])
        yield
        for h in range(2):
            d = H[h]; b = d["b"]; bo = d["bo"][par]
            hs = slice(64 * h, 64 * h + 64)
            shift(d["rs"], raw["r%d" % h], 64, p64[:, h, 0:1], b)
            yield
            shift(d["ks"], raw["k%d" % h], 64, p64[:, h, 1:2], b)
            yield
            shift(d["vs"], raw["v%d" % h], 64, p64[:, h, 2:3], b)
            yield
            PE(lambda: nc.tensor.matmul(ps_lo[0:64, :], lhsT=w2t[:, hs], rhs=ws[:, :], start=True, stop=True), r=[bc, b_lo], w=[b_pl])
            yield
            A(lambda: nc.scalar.activation(out=d["lw"][:], in_=ps_lo[0:64, :], func=AF.Sigmoid, bias=p64[:, h, 3:4], scale=1.0), r=[b_pl, bc], w=[b])
            yield
            V(lambda: nc.vector.tensor_scalar(out=d["lw"][:], in0=d["lw"][:], scalar1=-0.6065306597126334, scalar2=None, op0=ALU.mult), r=[b], w=[b])
            yield
            PE(lambda: nc.tensor.matmul(ps_lo[0:64, :], lhsT=a2t[:, hs], rhs=as_[:, :], start=True, stop=True), r=[bc, b_lo], w=[b_pl])
            yield
            A(lambda: nc.scalar.activation(out=d["asg"][:], in_=ps_lo[0:64, :], func=AF.Sigmoid, bias=p64[:, h, 4:5], scale=1.0), r=[b_pl, bc], w=[b])
            yield
            V(lambda: nc.vector.tensor_scalar(out=d["kkn"][:], in0=d["ks"][:], scalar1=p64[:, h, 5:6], scalar2=None, op0=ALU.mult), r=[b, bc], w=[b])
            yield
            A(lambda: nc.scalar.activation(out=tmp[0:64, :], in_=d["kkn"][:], func=AF.Square), r=[b], w=[b_tmp])
            yield
            PE(lambda: nc.tensor.matmul(ps_lo[0:64, :], lhsT=ones64[:, :], rhs=tmp[0:64, :], start=True, stop=True), r=[bc, b_tmp], w=[b_pl])
            yield
            A(lambda: nc.scalar.activation(out=tmp[0:64, :], in_=ps_lo[0:64, :], func=AF.Sqrt), r=[b_pl], w=[b_tmp])
            yield
            V(lambda: nc.vector.tensor_scalar(out=tmp[0:64, :], in0=tmp[0:64, :], scalar1=1e-12, scalar2=None, op0=ALU.max), r=[b_tmp], w=[b_tmp])
            yield
            V(lambda: nc.vector.reciprocal(out=tmp[0:64, :], in_=tmp[0:64, :]), r=[b_tmp], w=[b_tmp])
            yield
            V(lambda: nc.vector.tensor_tensor(out=d["kkn"][:], in0=d["kkn"][:], in1=tmp[0:64, :], op=ALU.mult), r=[b, b_tmp], w=[b])
            yield
            V(lambda: nc.vector.tensor_scalar(out=tmp[0:64, :], in0=d["asg"][:], scalar1=-1.0, scalar2=p64[:, h, 6:7], op0=ALU.add, op1=ALU.mult), r=[b, bc], w=[b_tmp])
            yield
            V(lambda: nc.vector.scalar_tensor_tensor(out=d["kp"][:], in0=tmp[0:64, :], scalar=1.0, in1=d["ks"][:], op0=ALU.add, op1=ALU.mult), r=[b_tmp, b], w=[b])
            yield
            V(lambda: nc.vector.tensor_tensor(out=d["bv"][:], in0=d["kkn"][:], in1=d["asg"][:], op=ALU.mult), r=[b], w=[b])
            yield
            V(lambda: nc.vector.scalar_tensor_tensor(out=d["rkr"][:], in0=d["rs"][:], scalar=p64[:, h, 7:8], in1=d["kp"][:], op0=ALU.mult, op1=ALU.mult), r=[b, bc], w=[b])
            yield
            V(lambda: nc.vector.tensor_tensor_scan(out=d["cum"][:], data0=seg[0:64, :], data1=d["lw"][:], initial=0.0, op0=ALU.mult, op1=ALU.add), r=[b, bc], w=[b])
            yield
            cum3 = d["cum"][:].rearrange("p (c t) -> p c t", t=C)
            V(lambda: nc.vector.tensor_copy(out=d["cC"][:], in_=cum3[:, :, C - 1]), r=[b], w=[b])
            yield
            A(lambda: nc.scalar.activation(out=d["gC"][par][:], in_=d["cC"][:], func=AF.Exp), r=[b], w=[bo])
            yield
            A(lambda: nc.scalar.activation(out=d["e1"][:], in_=d["cum"][:], func=AF.Exp), r=[b], w=[b])
            yield
            A(lambda: nc.scalar.activation(out=d["e2"][:], in_=d["cum"][:], func=AF.Exp, scale=-1.0), r=[b], w=[b])
            yield
            AR = d["AR"][par]
            V(lambda: nc.vector.tensor_tensor(out=RR(AR[:, :, 64:128]), in0=d["rs"][:].rearrange("p (c t) -> p c t", t=C),
                                              in1=d["e1"][:].rearrange("p (c t) -> p c t", t=C), op=ALU.mult), r=[b], w=[bo])
            yield
            V(lambda: nc.vector.tensor_tensor(out=RR(d["BT"][par][:]), in0=d["bv"][:], in1=d["e2"][:], op=ALU.mult), r=[b], w=[bo])
            yield
            V(lambda: nc.vector.tensor_tensor(out=RR(d["KT"][par][:]), in0=d["kp"][:], in1=d["e2"][:], op=ALU.mult), r=[b], w=[bo])
            yield
            V(lambda: nc.vector.tensor_tensor(out=tmp[0:64, :], in0=d["cum"][:], in1=d["lw"][:], op=ALU.subtract), r=[b], w=[b_tmp])
            yield
            A(lambda: nc.scalar.activation(out=tmp[0:64, :], in_=tmp[0:64, :], func=AF.Exp), r=[b_tmp], w=[b_tmp])
            yield
            V(lambda: nc.vector.scalar_tensor_tensor(out=RR(AR[:, :, 0:64]), in0=d["kkn"][:].rearrange("p (c t) -> p c t", t=C), scalar=-1.0,
                                                     in1=tmp[0:64, :].rearrange("p (c t) -> p c t", t=C), op0=ALU.mult, op1=ALU.mult), r=[b, b_tmp], w=[bo])
            yield
            V(lambda: nc.vector.tensor_tensor(out=tmp[0:64, :].rearrange("p (c t) -> p c t", t=C), in0=d["cC"][:].unsqueeze(2).to_broadcast([64, NCH, C]),
                                              in1=cum3, op=ALU.subtract), r=[b], w=[b_tmp])
            yield
            A(lambda: nc.scalar.activation(out=tmp[0:64, :], in_=tmp[0:64, :], func=AF.Exp), r=[b_tmp], w=[b_tmp])
            yield
            V(lambda: nc.vector.tensor_tensor(out=d["BH"][:], in0=d["bv"][:], in1=tmp[0:64, :], op=ALU.mult), r=[b, b_tmp], w=[b])
            yield
            V(lambda: nc.vector.tensor_tensor(out=d["KH"][:], in0=d["kp"][:], in1=tmp[0:64, :], op=ALU.mult), r=[b, b_tmp], w=[b])
            yield
            for src, dstn in (("vs", "Vt"), ("BH", "BHt"), ("KH", "KHt")):
                for c in range(NCH):
                    PE(lambda: nc.tensor.transpose(ps_tr[0:64, c * 64:(c + 1) * 64], d[src][:, c * C:(c + 1) * C], ident[0:64, 0:64]), r=[b, bc], w=[b_ptr])
                    yield
                A(lambda: nc.scalar.copy(out=RR(d[dstn][par][:].rearrange("p c k -> p (c k)")), in_=ps_tr[0:64, :]), r=[b_ptr], w=[d["bt"][par]])
                yield
            for c in range(NCH):
                PE(lambda: nc.tensor.matmul(ps_tr[0:64, 2 * c:2 * c + 2], lhsT=d["rkr"][:, c * C:(c + 1) * C], rhs=ones64[:, 0:2], start=True, stop=True), r=[b, bc], w=[b_ptr])
                yield
            V(lambda: nc.vector.tensor_copy(out=sbon[par][:, :, h], in_=ps_tr[0:64, 0:2 * NCH:2]), r=[b_ptr], w=[b_sb[par]])
            yield

    def rest(blk, tick):
        t0 = blk * TB
        par = blk % 2
        for h in range(2):
            d = H[h]; b = d["bo"][par]; AR = d["AR"][par]
            tick()
            for half in range(2):
                for cc in range(4):
                    c = half * 4 + cc
                    PE(lambda: nc.tensor.matmul(ps_a1[:, cc * 128:(cc + 1) * 128], lhsT=RR(d["BT"][par][:, c * C:(c + 1) * C]), rhs=RR(AR[:, c, :]), start=True, stop=True), r=[b], w=[b_pa1])
                    PE(lambda: nc.tensor.matmul(ps_a2[:, cc * 128:(cc + 1) * 128], lhsT=RR(d["KT"][par][:, c * C:(c + 1) * C]), rhs=RR(AR[:, c, :]), start=True, stop=True), r=[b], w=[b_pa2])
                    PE(lambda: nc.tensor.matmul(ps_a3[:, cc * 64:(cc + 1) * 64], lhsT=RR(AR[:, c, 0:64]), rhs=RR(d["BT"][par][:, c * C:(c + 1) * C]), start=True, stop=True), r=[b], w=[b_pa3])
                V(lambda: nc.vector.tensor_tensor(out=RR(d["NG"][:, half * 4:half * 4 + 4, :].rearrange("p c k -> p (c k)")), in0=ps_a1[:, :], in1=mask1[:, :], op=ALU.mult), r=[b_pa1, bc], w=[d["bA"]])
                V(lambda: nc.vector.tensor_tensor(out=RR(d["LG"][:, half * 4:half * 4 + 4, :].rearrange("p c k -> p (c k)")), in0=ps_a2[:, :], in1=mask1[:, :], op=ALU.mult), r=[b_pa2, bc], w=[d["bA"]])
                V(lambda: nc.vector.tensor_tensor(out=d["L"][:, half * 4:half * 4 + 4, :].rearrange("p c k -> p (c k)"), in0=ps_a3[:, 0:256], in1=mask3[:, :], op=ALU.mult), r=[b_pa3, bc], w=[d["bA"]])
        DPS = [(ps_d, b_pd, ps_d2, b_pd2), (ps_a1, b_pa1, ps_a2, b_pa2)]
        for h in range(2):
            d = H[h]; P, PT, ST = d["P"], d["PT"], d["ST"]; bD = d["bD"]
            V(lambda: nc.vector.tensor_copy(out=RR(P[0][:]), in_=d["L"][:]), r=[d["bA"]], w=[bD])
            V(lambda: nc.vector.tensor_copy(out=RR(PT[0][:]), in_=d["NG"][:, :, 0:64]), r=[d["bA"]], w=[bD])
            V(lambda: nc.vector.tensor_tensor(out=RR(ST[0][:]), in0=d["NG"][:, :, 0:64], in1=ident[0:64, 0:64].unsqueeze(1).to_broadcast([64, NCH, 64]), op=ALU.add), r=[d["bA"], bc], w=[bD])
        cur = 0
        for lev in range(5):
            nxt = 1 - cur
            tick()
            for h in range(2):
                d = H[h]; P, PT, ST = d["P"], d["PT"], d["ST"]; bD = d["bD"]
                pd, bpd, pd2, bpd2 = DPS[h]
                for c in range(NCH):
                    PE(lambda: nc.tensor.matmul(pd[:, c * 64:(c + 1) * 64], lhsT=RR(PT[cur][:, c, :]), rhs=RR(P[cur][:, c, :]), start=True, stop=True), r=[bD], w=[bpd])
                for c in range(NCH):
                    PE(lambda: nc.tensor.matmul(pd2[:, c * 64:(c + 1) * 64], lhsT=RR(P[cur][:, c, :]), rhs=RR(PT[cur][:, c, :]), start=True, stop=True), r=[bD], w=[bpd2])
            tick()
            for h in range(2):
                d = H[h]; P, PT, ST = d["P"], d["PT"], d["ST"]; bD = d["bD"]
                pd, bpd, pd2, bpd2 = DPS[h]
                V(lambda: nc.vector.tensor_copy(out=RR(P[nxt][:].rearrange("p c k -> p (c k)")), in_=pd[:, :]), r=[], w=[bpd, bD])
                A(lambda: nc.scalar.copy(out=RR(PT[nxt][:].rearrange("p c k -> p (c k)")), in_=pd2[:, :]), r=[], w=[bpd2, bD])
            tick()
            for h in range(2):
                d = H[h]; P, PT, ST = d["P"], d["PT"], d["ST"]; bD = d["bD"]
                pd, bpd, pd2, bpd2 = DPS[h]
                for c in range(NCH):
                    PE(lambda: nc.tensor.matmul(pd[:, c * 64:(c + 1) * 64], lhsT=RR(P[nxt][:, c, :]), rhs=RR(ST[cur][:, c, :]), start=True, stop=True), r=[bD], w=[bpd])
            tick()
            for h in range(2):
                d = H[h]; P, PT, ST = d["P"], d["PT"], d["ST"]; bD = d["bD"]
                pd, bpd, pd2, bpd2 = DPS[h]
                V(lambda: nc.vector.tensor_tensor(out=RR(ST[nxt][:].rearrange("p c k -> p (c k)")), in0=pd[:, :], in1=ST[cur][:].rearrange("p c k -> p (c k)"), op=ALU.add), r=[bD], w=[bpd, bD])
            tick()
            cur = nxt
        for h in range(2):
            H[h]["STf"] = H[h]["ST"][cur]
        for c in range(NCH):
            pp = lambda h, i: ps_sh[h][:, i * 64:(i + 1) * 64]
            tick()
            for h in range(2):
                d = H[h]
                PE(lambda: nc.tensor.matmul(pp(h, 0), lhsT=RR(d["LG"][:, c, 0:64]), rhs=RR(d["Vt"][par][:, c, :]), start=True, stop=False), r=[d["bA"], d["bt"][par]], w=[b_psh[h]])
                PE(lambda: nc.tensor.matmul(pp(h, 0), lhsT=RR(d["AR"][par][:, c, 0:64]), rhs=RR(d["M"][:, :]), start=False, stop=True), r=[d["bo"][par], d["bM"]], w=[b_psh[h]])
            tick()
            for h in range(2):
                d = H[h]
                if h == 0:
                    A(lambda: nc.scalar.copy(out=RR(d["X1"][:]), in_=pp(h, 0)), r=[], w=[b_psh[h], d["bX"]])
                else:
                    V(lambda: nc.vector.tensor_copy(out=RR(d["X1"][:]), in_=pp(h, 0)), r=[], w=[b_psh[h], d["bX"]])
            tick()
            for h in range(2):
                d = H[h]
                PE(lambda: nc.tensor.matmul(pp(h, 1), lhsT=RR(d["STf"][:, c, :]), rhs=RR(d["X1"][:, :]), start=True, stop=True), r=[d["bD"], d["bX"]], w=[b_psh[h]])
            tick()
            for h in range(2):
                d = H[h]
                if h == 0:
                    V(lambda: nc.vector.tensor_copy(out=RR(d["U"][:]), in_=pp(h, 1)), r=[], w=[b_psh[h], d["bU"]])
                else:
                    A(lambda: nc.scalar.copy(out=RR(d["U"][:]), in_=pp(h, 1)), r=[], w=[b_psh[h], d["bU"]])
            tick()
            for h in range(2):
                d = H[h]
                PE(lambda: nc.tensor.matmul(pp(h, 2), lhsT=RR(d["AR"][par][:, c, 64:128]), rhs=RR(d["M"][:, :]), start=True, stop=False), r=[d["bo"][par], d["bM"]], w=[b_psh[h]])
                PE(lambda: nc.tensor.matmul(pp(h, 2), lhsT=RR(d["LG"][:, c, 64:128]), rhs=RR(d["Vt"][par][:, c, :]), start=False, stop=False), r=[d["bA"], d["bt"][par]], w=[b_psh[h]])
                PE(lambda: nc.tensor.matmul(pp(h, 2), lhsT=RR(d["NG"][:, c, 64:128]), rhs=RR(d["U"][:, :]), start=False, stop=True), r=[d["bA"], d["bU"]], w=[b_psh[h]])
                PE(lambda: nc.tensor.matmul(pp(h, 3), lhsT=RR(d["KHt"][par][:, c, :]), rhs=RR(d["Vt"][par][:, c, :]), start=True, stop=False), r=[d["bt"][par]], w=[b_psh[h]])
                PE(lambda: nc.tensor.matmul(pp(h, 3), lhsT=RR(d["BHt"][par][:, c, :]), rhs=RR(d["U"][:, :]), start=False, stop=True), r=[d["bt"][par], d["bU"]], w=[b_psh[h]])
            tick()
            for h in range(2):
                d = H[h]
                V(lambda: nc.vector.scalar_tensor_tensor(out=RR(d["M"][:]), in0=d["M"][:], scalar=d["gC"][par][:, c:c + 1], in1=pp(h, 3), op0=ALU.mult, op1=ALU.add), r=[d["bo"][par], d["bM"]], w=[b_psh[h], d["bM"]])
                A(lambda: nc.scalar.copy(out=Yb[:, c, h, :], in_=pp(h, 2)), r=[], w=[b_psh[h], b_Y])
        Y2 = Yb[:].rearrange("p c h v -> p (c h) v")
        V(lambda: nc.vector.tensor_reduce(out=st1[:], in_=Y2, axis=AX.X, op=ALU.add), r=[b_Y], w=[b_st])
        A(lambda: nc.scalar.activation(out=Ysq[:].rearrange("p c h v -> p (c h v)"), in_=Yb[:].rearrange("p c h v -> p (c h v)"), func=AF.Square), r=[b_Y], w=[b_tmp])
        V(lambda: nc.vector.tensor_reduce(out=st2[:], in_=Ysq[:].rearrange("p c h v -> p (c h) v"), axis=AX.X, op=ALU.add), r=[b_tmp], w=[b_st])
        V(lambda: nc.vector.tensor_scalar(out=st1[:], in0=st1[:], scalar1=1.0 / 64, scalar2=None, op0=ALU.mult), r=[b_st], w=[b_st])
        V(lambda: nc.vector.tensor_tensor(out=st3[:], in0=st1[:], in1=st1[:], op=ALU.mult), r=[b_st], w=[b_st])
        V(lambda: nc.vector.scalar_tensor_tensor(out=st2[:], in0=st2[:], scalar=1.0 / 64, in1=st3[:], op0=ALU.mult, op1=ALU.subtract), r=[b_st], w=[b_st])
        A(lambda: nc.scalar.activation(out=st2[:], in_=st2[:], func=AF.Sqrt, bias=64e-5, scale=1.0), r=[b_st], w=[b_st])
        V(lambda: nc.vector.reciprocal(out=st2[:], in_=st2[:]), r=[b_st], w=[b_st])
        V(lambda: nc.vector.tensor_tensor(out=Y2, in0=Y2, in1=st1[:].unsqueeze(2).to_broadcast([64, NCH * 2, 64]), op=ALU.subtract), r=[b_st, b_Y], w=[b_Y])
        V(lambda: nc.vector.tensor_tensor(out=Y2, in0=Y2, in1=st2[:].unsqueeze(2).to_broadcast([64, NCH * 2, 64]), op=ALU.mult), r=[b_st, b_Y], w=[b_Y])
        for h in range(2):
            V(lambda: nc.vector.tensor_tensor(out=Yb[:, :, h, :], in0=Yb[:, :, h, :], in1=gn[:, 0, h, :].unsqueeze(1).to_broadcast([64, NCH, 64]), op=ALU.mult), r=[b_Y, bc], w=[b_Y])
            V(lambda: nc.vector.tensor_tensor(out=Yb[:, :, h, :], in0=Yb[:, :, h, :], in1=gn[:, 1, h, :].unsqueeze(1).to_broadcast([64, NCH, 64]), op=ALU.add), r=[b_Y, bc], w=[b_Y])
            V(lambda: nc.vector.tensor_tensor(out=Ysq[:, :, h, :], in0=H[h]["Vt"][par][:], in1=sbon[par][:, :, h].unsqueeze(2).to_broadcast([64, NCH, 64]), op=ALU.mult), r=[H[h]["bt"][par], b_sb[par]], w=[b_tmp])
        V(lambda: nc.vector.tensor_tensor(out=Yb[:].rearrange("p c h v -> p (c h v)"), in0=Yb[:].rearrange("p c h v -> p (c h v)"), in1=Ysq[:].rearrange("p c h v -> p (c h v)"), op=ALU.add), r=[b_Y, b_tmp], w=[b_Y])
        for c in range(NCH):
            PE(lambda: nc.tensor.transpose(ps_a3f[:, c * 64:(c + 1) * 64], Yb[:, c, :, :].rearrange("p h v -> p (h v)"), ident[0:64, 0:64]), r=[b_Y, bc], w=[b_pa3])
        V(lambda: nc.vector.tensor_tensor(out=yo[:], in0=ps_a3f[:, :], in1=gate[par][:], op=ALU.mult), r=[b_gate[par]], w=[b_pa3, b_yo])
        mk.dma("sp", yT[:, t0:t0 + TB], yo[:], reads=[b_yo], is_output=True)

    for _ in stage1(0):
        pass
    for blk in range(NB):
        nx = stage1(blk + 1) if blk + 1 < NB else None

        def tick(k=2):
            if nx is not None:
                for _ in range(k):
                    if next(nx, "END") == "END":
                        break
        rest(blk, tick)
        if nx is not None:
            for _ in nx:
                pass


def build_rwkv(T):
    nc = bass.Bass("TRN2", target_bir_lowering=False)
    dt = lambda n, s, k="ExternalInput": nc.dram_tensor(n, s, F32, kind=k).ap()
    rwin = dt("rwin", [704, T]); par64 = dt("par64", [64, 2, 11]); par128 = dt("par128", [128, 3])
    w2 = dt("w2", [96, 128]); a2 = dt("a2", [96, 128]); g2 = dt("g2", [128, 128]); gnt = dt("gnt", [64, 2, 2, 64])
    cst = {"mask1": dt("mask1", [64, 512]), "mask3": dt("mask3", [64, 256]), "seg": dt("seg", [128, TB]), "ident": dt("ident", [128, 128])}
    yT = dt("yT", [128, T], "ExternalOutput")
    with ExitStack() as ctx:
        mk = MK(nc, ctx)
        emit_rwkv(nc, mk, T, rwin, par64, par128, w2, a2, g2, gnt, cst, yT)
        mk.finish("sp")
        print("rwkv ops", mk.nops, "waits", mk.nwaits)
    return nc


def rwkv_host_inputs(prm, l, q):
    G = 512
    cs = slice(128 * q, 128 * q + 128)
    mu = prm["rwkv_mu"][l]
    par64 = np.zeros((64, 2, 11), np.float32)
    for h in range(2):
        c0 = 128 * q + 64 * h
        par64[:, h, 0] = mu[0 * G + c0:0 * G + c0 + 64]
        par64[:, h, 1] = mu[1 * G + c0:1 * G + c0 + 64]
        par64[:, h, 2] = mu[2 * G + c0:2 * G + c0 + 64]
        par64[:, h, 3] = prm["rwkv_w0"][l][c0:c0 + 64]
        par64[:, h, 4] = prm["rwkv_a0"][l][c0:c0 + 64]
        par64[:, h, 5] = prm["rwkv_kk"][l][c0:c0 + 64]
        par64[:, h, 6] = prm["rwkv_ka"][l][c0:c0 + 64]
        par64[:, h, 7] = prm["rwkv_rk"][l][2 * q + h]
    par128 = np.zeros((128, 3), np.float32)
    par128[0:96, 0] = mu[3 * G:3 * G + 96]
    par128[0:96, 1] = mu[3 * G + 96:3 * G + 192]
    par128[:, 2] = mu[3 * G + 192:3 * G + 320]
    gnt = np.zeros((64, 2, 2, 64), np.float32)
    for h in range(2):
        c0 = 128 * q + 64 * h
        gnt[:, 0, h, :] = prm["rwkv_gn_g"][l][c0:c0 + 64][None]
        gnt[:, 1, h, :] = prm["rwkv_gn_b"][l][c0:c0 + 64][None]
    d = {"par64": par64, "par128": par128, "gnt": gnt,
         "w2": np.ascontiguousarray(prm["rwkv_w2"][l][:, cs]), "a2": np.ascontiguousarray(prm["rwkv_a2"][l][:, cs]),
         "g2": np.ascontiguousarray(prm["rwkv_g2"][l][:, cs])}
    d.update(rwkv_consts())
    return d


def rwkv_rows(q):
    G = 512
    idx = []
    for base in (0, G, 2 * G):
        idx += list(range(base + 128 * q, base + 128 * q + 64))
        idx += list(range(base + 128 * q + 64, base + 128 * q + 128))
    idx += list(range(3 * G, 3 * G + 320))
    return np.array(idx)


import math
import numpy as np
from contextlib import ExitStack


def emit_conv(nc, mk, T, cvin, cw, yT, TB=2048, odt=F32):
    V = lambda fn, r=(), w=(): mk.op("dve", fn, r, w)
    G = lambda fn, r=(), w=(): mk.op("pool", fn, r, w)
    cwt = mk.sb("cv_w", [128, 3]); bc = Buf()
    mk.dma("sp", cwt[:], cw, writes=[bc])
    Bt = mk.sb("cv_B", [128, TB]); Ct = mk.sb("cv_C", [128, TB + 2]); Ht = mk.sb("cv_H", [128, TB + 2])
    z = mk.sb("cv_z", [128, TB + 2]); y = mk.sb("cv_y", [128, TB]); o = mk.sb("cv_o", [128, TB], odt)
    b_in, b_z, b_y, b_o = Buf(), Buf(), Buf(), Buf()
    for blk in range(T // TB):
        t0 = blk * TB
        mk.dma("sp", Bt[:], cvin[0:128, t0:t0 + TB], writes=[b_in])
        if blk == 0:
            V(lambda: nc.vector.memset(Ct[:, 0:2], 0.0), w=[b_in])
            V(lambda: nc.vector.memset(Ht[:, 0:2], 0.0), w=[b_in])
            mk.dma("sp", Ct[:, 2:], cvin[128:256, 0:TB], writes=[b_in])
            mk.dma("sp", Ht[:, 2:], cvin[256:384, 0:TB], writes=[b_in])
        else:
            mk.dma("sp", Ct[:], cvin[128:256, t0 - 2:t0 + TB], writes=[b_in])
            mk.dma("sp", Ht[:], cvin[256:384, t0 - 2:t0 + TB], writes=[b_in])
        G(lambda: nc.gpsimd.tensor_tensor(out=z[:], in0=Ct[:], in1=Ht[:], op=ALU.mult), r=[b_in], w=[b_z])
        V(lambda: nc.vector.tensor_scalar(out=y[:], in0=z[:, 2:TB + 2], scalar1=cwt[:, 2:3], scalar2=None, op0=ALU.mult), r=[b_z, bc], w=[b_y])
        V(lambda: nc.vector.scalar_tensor_tensor(out=y[:], in0=z[:, 1:TB + 1], scalar=cwt[:, 1:2], in1=y[:], op0=ALU.mult, op1=ALU.add), r=[b_z, bc], w=[b_y])
        V(lambda: nc.vector.scalar_tensor_tensor(out=y[:], in0=z[:, 0:TB], scalar=cwt[:, 0:1], in1=y[:], op0=ALU.mult, op1=ALU.add), r=[b_z, bc], w=[b_y])
        G(lambda: nc.gpsimd.tensor_tensor(out=o[:], in0=y[:], in1=Bt[:], op=ALU.mult), r=[b_y, b_in], w=[b_o])
        mk.dma("sp", yT[:, t0:t0 + TB], o[:], reads=[b_o], is_output=True)


def t5_bucket_np(rel):
    n = np.maximum(rel, 0)
    max_exact = 16
    n_f = np.maximum(n, 1).astype(np.float32)
    large = max_exact + (np.log(n_f / max_exact) / math.log(128 / max_exact) * (32 - max_exact)).astype(np.int32)
    return np.where(n < max_exact, n, np.minimum(large, 31))


def attn_tables(rel_bias, sinks_l, q):
    qi = np.arange(128)[:, None]
    kj = np.arange(256)[None, :]
    rel = qi + 128 - kj
    bucket = t5_bucket_np(rel)
    valid = (rel >= 0) & (rel < 128)
    tab = np.zeros((2, 128, 2, 256), np.float32)
    for h in range(2):
        bias = rel_bias[bucket, 2 * q + h]
        full = np.where(valid, bias, np.float32(-30000.0))
        tab[0, :, h, :] = full
        f0 = full.copy()
        f0[:, 0:128] = -30000.0
        tab[1, :, h, :] = f0
    sk = np.broadcast_to(sinks_l[2 * q:2 * q + 2][None, :], (128, 2)).astype(np.float32).copy()
    return tab, sk


def emit_attn(nc, mk, T, qkv, btab, sinkt_d, ident_d, yT, odt=F32):
    V = lambda fn, r=(), w=(): mk.op("dve", fn, r, w)
    A = lambda fn, r=(), w=(): mk.op("act", fn, r, w)
    PE = lambda fn, r=(), w=(): mk.op("pe", fn, r, w, skip_same=True)
    NBK = T // 128
    bt = mk.sb("at_bt", [128, 2, 2, 256]); sk = mk.sb("at_sk", [128, 2]); ident = mk.sb("at_id", [128, 128]); bc = Buf()
    mk.dma("sp", bt[:, 0, :, :], btab[0], writes=[bc])
    mk.dma("sp", bt[:, 1, :, :], btab[1], writes=[bc])
    mk.dma("sp", sk[:], sinkt_d, writes=[bc])
    mk.dma("sp", ident[:], ident_d, writes=[bc])
    CH = 1024
    qt = mk.sb("at_q", [128, CH]); kt = mk.sb("at_k", [128, 128 + CH]); vt = mk.sb("at_v", [64, CH])
    vtok = mk.sb("at_vtok", [128, CH // 128 + 1, 64])
    b_q, b_k, b_v, b_vt = Buf(), Buf(), Buf(), Buf()
    sc = [mk.sb("at_sc%d" % h, [128, 256]) for h in range(2)]; b_sc = [Buf(), Buf()]
    pr = [mk.sb("at_p%d" % h, [128, 256]) for h in range(2)]; b_p = [Buf(), Buf()]
    pT = [mk.sb("at_pT%d" % h, [128, 256]) for h in range(2)]; b_pT = [Buf(), Buf()]
    sm = [mk.sb("at_sm%d" % h, [128, 8]) for h in range(2)]; b_sm = [Buf(), Buf()]
    ot = mk.sb("at_o", [128, 128]); b_o = Buf()
    yo = mk.sb("at_yo", [128, CH], odt); b_yo = Buf()
    ps_s = [mk.ps("at_ps_s%d" % h, [128, 512]) for h in range(2)]; b_ps = [Buf(), Buf()]
    ps_t = [mk.ps("at_ps_t%d" % h, [128, 512]) for h in range(2)]; b_pt = [Buf(), Buf()]
    ps_o = mk.ps("at_ps_o", [128, 512]); b_po = Buf()
    ps_v = mk.ps("at_ps_v", [128, 512]); b_pv = Buf()
    for ch in range(T // CH):
        c0 = ch * CH
        mk.dma("sp", qt[:], qkv[0:128, c0:c0 + CH], writes=[b_q])
        if ch == 0:
            V(lambda: nc.vector.memset(kt[:, 0:128], 0.0), w=[b_k])
            V(lambda: nc.vector.memset(vtok[:, 0, :], 0.0), w=[b_vt])
            for hh in range(2):
                mk.dma("sp", kt[64 * hh:64 * hh + 64, 128:], qkv[128:192, 0:CH], writes=[b_k])
        else:
            for hh in range(2):
                mk.dma("sp", kt[64 * hh:64 * hh + 64, :], qkv[128:192, c0 - 128:c0 + CH], writes=[b_k])
            V(lambda: nc.vector.tensor_copy(out=vtok[:, 0, :], in_=vtok[:, CH // 128, :]), r=[b_vt], w=[b_vt])
        mk.dma("sp", vt[:], qkv[192:256, c0:c0 + CH], writes=[b_v])
        for j in range(CH // 128):
            PE(lambda: nc.tensor.transpose(ps_v[:, j * 64:(j + 1) * 64], vt[:, j * 128:(j + 1) * 128], ident[0:64, 0:64]), r=[b_v, bc], w=[b_pv])
        A(lambda: nc.scalar.copy(out=vtok[:, 1:, :].rearrange("p j d -> p (j d)"), in_=ps_v[:, 0:(CH // 128) * 64]), r=[], w=[b_pv, b_vt])
        for j in range(CH // 128):
            first = 1 if (ch == 0 and j == 0) else 0
            HS = [slice(0, 64), slice(64, 128)]
            for h in range(2):
                PE(lambda: nc.tensor.matmul(ps_s[h][:, 0:256], lhsT=qt[HS[h], j * 128:(j + 1) * 128], rhs=kt[HS[h], j * 128:j * 128 + 256], start=True, stop=True),
                   r=[b_q, b_k], w=[b_ps[h]])
            for h in range(2):
                V(lambda: nc.vector.scalar_tensor_tensor(out=sc[h][:], in0=ps_s[h][:, 0:256], scalar=0.125, in1=bt[:, first, h, :], op0=ALU.mult, op1=ALU.add),
                  r=[bc], w=[b_ps[h], b_sc[h]])
                s = sm[h]
                V(lambda: nc.vector.reduce_max(out=s[:, 0:1], in_=sc[h][:], axis=AX.X), r=[b_sc[h]], w=[b_sm[h]])
                V(lambda: nc.vector.tensor_tensor(out=s[:, 0:1], in0=s[:, 0:1], in1=sk[:, h:h + 1], op=ALU.max), r=[bc], w=[b_sm[h]])
                V(lambda: nc.vector.tensor_scalar(out=s[:, 1:2], in0=s[:, 0:1], scalar1=-1.0, scalar2=None, op0=ALU.mult), r=[], w=[b_sm[h]])
            for h in range(2):
                s = sm[h]
                A(lambda: nc.scalar.activation(out=pr[h][:], in_=sc[h][:], func=AF.Exp, bias=s[:, 1:2], scale=1.0, accum_out=s[:, 2:3]), r=[b_sc[h]], w=[b_sm[h], b_p[h]])
                A(lambda: nc.scalar.activation(out=s[:, 3:4], in_=sk[:, h:h + 1], func=AF.Exp, bias=s[:, 1:2], scale=1.0), r=[bc], w=[b_sm[h]])
            for h in range(2):
                for kb in range(2):
                    PE(lambda: nc.tensor.transpose(ps_t[h][:, kb * 128:(kb + 1) * 128], pr[h][:, kb * 128:(kb + 1) * 128], ident[:, :]), r=[b_p[h], bc], w=[b_pt[h]])
            for h in range(2):
                s = sm[h]
                V(lambda: nc.vector.tensor_tensor(out=s[:, 4:5], in0=s[:, 2:3], in1=s[:, 3:4], op=ALU.add), r=[], w=[b_sm[h]])
                V(lambda: nc.vector.reciprocal(out=s[:, 5:6], in_=s[:, 4:5]), r=[], w=[b_sm[h]])
                V(lambda: nc.vector.tensor_copy(out=pT[h][:], in_=ps_t[h][:, 0:256]), r=[], w=[b_pt[h], b_pT[h]])
            for h in range(2):
                for kb in range(2):
                    PE(lambda: nc.tensor.matmul(ps_o[:, h * 64:(h + 1) * 64], lhsT=pT[h][:, kb * 128:(kb + 1) * 128], rhs=vtok[:, j + kb, :], start=(kb == 0), stop=(kb == 1)),
                       r=[b_pT[h], b_vt], w=[b_po])
            for h in range(2):
                s = sm[h]
                A(lambda: nc.scalar.activation(out=ot[:, h * 64:(h + 1) * 64], in_=ps_o[:, h * 64:(h + 1) * 64], func=AF.Copy, scale=s[:, 5:6]), r=[b_sm[h]], w=[b_po, b_o])
            PE(lambda: nc.tensor.transpose(ps_o[:, 128:256], ot[:, :], ident[:, :]), r=[b_o, bc], w=[b_po])
            V(lambda: nc.vector.tensor_copy(out=yo[:, j * 128:(j + 1) * 128], in_=ps_o[:, 128:256]), r=[], w=[b_po, b_yo])
        mk.dma("sp", yT[:, c0:c0 + CH], yo[:], reads=[b_yo], is_output=True)


CS = 512


def s5_host_inputs(prm, l, q):
    g0 = 8 * q
    par = np.zeros((128, 4, 3), np.float32)
    bb = np.zeros((128, 4, 2, 16), np.float32)
    cc = np.zeros((128, 4, 2, 16), np.float32)
    for j in range(4):
        for gl in range(2):
            g = g0 + 2 * j + gl
            ps = slice(64 * gl, 64 * gl + 64)
            par[ps, j, 0] = prm["s5_lambda_re"][l][g]
            par[ps, j, 1] = prm["s5_lambda_im"][l][g]
            par[ps, j, 2] = prm["s5_log_dt"][l][g]
            bb[ps, j, 0, :] = prm["s5_b_re"][l][g]
            bb[ps, j, 1, :] = prm["s5_b_im"][l][g]
            cc[ps, j, 0, :] = prm["s5_c_re"][l][g].T
            cc[ps, j, 1, :] = prm["s5_c_im"][l][g].T
    dsk = np.ascontiguousarray(prm["s5_d"][l][g0:g0 + 8].reshape(128, 1))
    iot = np.broadcast_to(np.arange(CS, dtype=np.float32)[None, :], (128, CS)).copy()
    return {"s5par": par, "s5bb": bb, "s5cc": cc, "s5d": dsk, "s5iota": iot, "ident": np.eye(128, dtype=np.float32)}


def emit_s5(nc, mk, T, uT, par_d, bb_d, cc_d, d_d, iota_d, ident_d, yT, odt=F32):
    V = lambda fn, r=(), w=(): mk.op("dve", fn, r, w)
    A = lambda fn, r=(), w=(): mk.op("act", fn, r, w)
    G = lambda fn, r=(), w=(): mk.op("pool", fn, r, w)
    PE = lambda fn, r=(), w=(): mk.op("pe", fn, r, w, skip_same=True)
    sb = mk.sb
    TWO_PI = 2.0 * math.pi
    par = sb("s5_par", [128, 4, 3]); bb = sb("s5_bb", [128, 4, 2, 16]); cc = sb("s5_cc", [128, 4, 2, 16])
    dsk = sb("s5_d", [128, 1]); iot = sb("s5_iota", [128, CS]); ident = sb("s5_id", [128, 128])
    bc = Buf("c")
    for dst, src in ((par, par_d), (bb, bb_d), (cc, cc_d), (dsk, d_d), (iot, iota_d), (ident, ident_d)):
        mk.dma("sp", dst[:], src, writes=[bc])
    P = {}
    for nm in ("dl", "mag", "th", "cs", "sn", "are", "aim", "den", "zre", "zim", "t1", "t2", "t3", "cC", "sC"):
        P[nm] = sb("s5_p_" + nm, [128, 4])
    ti = sb("s5_ti", [128, 4 * CS], I32)
    bp = Buf("p")
    lr, li, ldt = par[:, :, 0], par[:, :, 1], par[:, :, 2]

    def sincos(sin_out, cos_out, x, n, tmpa, tmpb, tint):
        def wrap(r):
            V(lambda: nc.vector.tensor_scalar(out=tmpb, in0=r, scalar1=0.5, scalar2=None, op0=ALU.is_gt), r=[bp], w=[bp])
            V(lambda: nc.vector.tensor_tensor(out=r, in0=r, in1=tmpb, op=ALU.subtract), r=[bp], w=[bp])
            V(lambda: nc.vector.tensor_scalar(out=tmpb, in0=r, scalar1=-0.5, scalar2=None, op0=ALU.is_lt), r=[bp], w=[bp])
            V(lambda: nc.vector.tensor_tensor(out=r, in0=r, in1=tmpb, op=ALU.add), r=[bp], w=[bp])
        V(lambda: nc.vector.tensor_copy(out=tint, in_=x), r=[bp, bc], w=[bp])
        V(lambda: nc.vector.tensor_copy(out=tmpa, in_=tint), r=[bp], w=[bp])
        V(lambda: nc.vector.tensor_tensor(out=tmpa, in0=x, in1=tmpa, op=ALU.subtract), r=[bp, bc], w=[bp])
        wrap(tmpa)
        A(lambda: nc.scalar.activation(out=sin_out, in_=tmpa, func=AF.Sin, scale=TWO_PI), r=[bp], w=[bp])
        V(lambda: nc.vector.tensor_scalar(out=tmpa, in0=tmpa, scalar1=0.25, scalar2=None, op0=ALU.add), r=[bp], w=[bp])
        wrap(tmpa)
        A(lambda: nc.scalar.activation(out=cos_out, in_=tmpa, func=AF.Sin, scale=TWO_PI), r=[bp], w=[bp])

    A(lambda: nc.scalar.activation(out=P["dl"][:], in_=ldt, func=AF.Exp), r=[bc], w=[bp])
    V(lambda: nc.vector.tensor_tensor(out=P["mag"][:], in0=lr, in1=P["dl"][:], op=ALU.mult), r=[bc, bp], w=[bp])
    A(lambda: nc.scalar.activation(out=P["mag"][:], in_=P["mag"][:], func=AF.Exp), r=[bp], w=[bp])
    V(lambda: nc.vector.tensor_tensor(out=P["th"][:], in0=li, in1=P["dl"][:], op=ALU.mult), r=[bc, bp], w=[bp])
    V(lambda: nc.vector.tensor_scalar(out=P["th"][:], in0=P["th"][:], scalar1=1.0 / TWO_PI, scalar2=None, op0=ALU.mult), r=[bp], w=[bp])
    V(lambda: nc.vector.tensor_copy(out=ti[:, 0:4], in_=P["th"][:]), r=[bp], w=[bp])
    V(lambda: nc.vector.tensor_copy(out=P["t1"][:], in_=ti[:, 0:4]), r=[bp], w=[bp])
    V(lambda: nc.vector.tensor_tensor(out=P["th"][:], in0=P["th"][:], in1=P["t1"][:], op=ALU.subtract), r=[bp], w=[bp])
    V(lambda: nc.vector.tensor_scalar(out=P["t1"][:], in0=P["th"][:], scalar1=0.5, scalar2=None, op0=ALU.is_gt), r=[bp], w=[bp])
    V(lambda: nc.vector.tensor_tensor(out=P["th"][:], in0=P["th"][:], in1=P["t1"][:], op=ALU.subtract), r=[bp], w=[bp])
    V(lambda: nc.vector.tensor_scalar(out=P["t1"][:], in0=P["th"][:], scalar1=-0.5, scalar2=None, op0=ALU.is_lt), r=[bp], w=[bp])
    V(lambda: nc.vector.tensor_tensor(out=P["th"][:], in0=P["th"][:], in1=P["t1"][:], op=ALU.add), r=[bp], w=[bp])
    sincos(P["sn"][:], P["cs"][:], P["th"][:], 4, P["t1"][:], P["t2"][:], ti[:, 0:4])
    V(lambda: nc.vector.tensor_tensor(out=P["are"][:], in0=P["mag"][:], in1=P["cs"][:], op=ALU.mult), r=[bp], w=[bp])
    V(lambda: nc.vector.tensor_tensor(out=P["aim"][:], in0=P["mag"][:], in1=P["sn"][:], op=ALU.mult), r=[bp], w=[bp])
    V(lambda: nc.vector.tensor_tensor(out=P["den"][:], in0=lr, in1=lr, op=ALU.mult), r=[bc], w=[bp])
    V(lambda: nc.vector.tensor_tensor(out=P["t1"][:], in0=li, in1=li, op=ALU.mult), r=[bc], w=[bp])
    V(lambda: nc.vector.tensor_tensor(out=P["den"][:], in0=P["den"][:], in1=P["t1"][:], op=ALU.add), r=[bp], w=[bp])
    V(lambda: nc.vector.reciprocal(out=P["den"][:], in_=P["den"][:]), r=[bp], w=[bp])
    V(lambda: nc.vector.tensor_scalar(out=P["t3"][:], in0=P["are"][:], scalar1=-1.0, scalar2=None, op0=ALU.add), r=[bp], w=[bp])
    V(lambda: nc.vector.tensor_tensor(out=P["t1"][:], in0=P["t3"][:], in1=lr, op=ALU.mult), r=[bp, bc], w=[bp])
    V(lambda: nc.vector.tensor_tensor(out=P["t2"][:], in0=P["aim"][:], in1=li, op=ALU.mult), r=[bp, bc], w=[bp])
    V(lambda: nc.vector.tensor_tensor(out=P["t1"][:], in0=P["t1"][:], in1=P["t2"][:], op=ALU.add), r=[bp], w=[bp])
    V(lambda: nc.vector.tensor_tensor(out=P["zre"][:], in0=P["t1"][:], in1=P["den"][:], op=ALU.mult), r=[bp], w=[bp])
    V(lambda: nc.vector.tensor_tensor(out=P["t1"][:], in0=P["aim"][:], in1=lr, op=ALU.mult), r=[bp, bc], w=[bp])
    V(lambda: nc.vector.tensor_tensor(out=P["t2"][:], in0=P["t3"][:], in1=li, op=ALU.mult), r=[bp, bc], w=[bp])
    V(lambda: nc.vector.tensor_tensor(out=P["t1"][:], in0=P["t1"][:], in1=P["t2"][:], op=ALU.subtract), r=[bp], w=[bp])
    V(lambda: nc.vector.tensor_tensor(out=P["zim"][:], in0=P["t1"][:], in1=P["den"][:], op=ALU.mult), r=[bp], w=[bp])
    bbar = sb("s5_bbar", [128, 4, 2, 16]); tb1 = sb("s5_tb1", [128, 4, 16]); tb2 = sb("s5_tb2", [128, 4, 16])
    zre_b = P["zre"][:].unsqueeze(2).to_broadcast([128, 4, 16]); zim_b = P["zim"][:].unsqueeze(2).to_broadcast([128, 4, 16])
    V(lambda: nc.vector.tensor_tensor(out=tb1[:], in0=bb[:, :, 0, :], in1=zre_b, op=ALU.mult), r=[bp, bc], w=[bp])
    V(lambda: nc.vector.tensor_tensor(out=tb2[:], in0=bb[:, :, 1, :], in1=zim_b, op=ALU.mult), r=[bp, bc], w=[bp])
    V(lambda: nc.vector.tensor_tensor(out=bbar[:, :, 0, :], in0=tb1[:], in1=tb2[:], op=ALU.subtract), r=[bp], w=[bp])
    V(lambda: nc.vector.tensor_tensor(out=tb1[:], in0=bb[:, :, 1, :], in1=zre_b, op=ALU.mult), r=[bp, bc], w=[bp])
    V(lambda: nc.vector.tensor_tensor(out=tb2[:], in0=bb[:, :, 0, :], in1=zim_b, op=ALU.mult), r=[bp, bc], w=[bp])
    V(lambda: nc.vector.tensor_tensor(out=bbar[:, :, 1, :], in0=tb1[:], in1=tb2[:], op=ALU.add), r=[bp], w=[bp])
    BD = sb("s5_BD", [128, 4, 2, 128]); CM = sb("s5_CM", [128, 4, 4, 128]); BbT = sb("s5_BbT", [128, 4, 2, 128])
    V(lambda: nc.vector.memset(BD[:].rearrange("p a b c -> p (a b c)"), 0.0), w=[bp])
    V(lambda: nc.vector.memset(CM[:].rearrange("p a b c -> p (a b c)"), 0.0), w=[bp])
    for j in range(4):
        for gl in range(2):
            ps_ = slice(64 * gl, 64 * gl + 64)
            c0 = 32 * j + 16 * gl
            for ri in range(2):
                V(lambda: nc.vector.tensor_copy(out=BD[ps_, j, ri, c0:c0 + 16], in_=bbar[ps_, j, ri, :]), r=[bp], w=[bp])
            V(lambda: nc.vector.tensor_copy(out=CM[ps_, j, 0, c0:c0 + 16], in_=cc[ps_, j, 0, :]), r=[bc], w=[bp])
            V(lambda: nc.vector.tensor_scalar(out=CM[ps_, j, 1, c0:c0 + 16], in0=cc[ps_, j, 0, :], scalar1=-1.0, scalar2=None, op0=ALU.mult), r=[bc], w=[bp])
            V(lambda: nc.vector.tensor_scalar(out=CM[ps_, j, 2, c0:c0 + 16], in0=cc[ps_, j, 1, :], scalar1=-1.0, scalar2=None, op0=ALU.mult), r=[bc], w=[bp])
            V(lambda: nc.vector.tensor_scalar(out=CM[ps_, j, 3, c0:c0 + 16], in0=cc[ps_, j, 1, :], scalar1=-1.0, scalar2=None, op0=ALU.mult), r=[bc], w=[bp])
    ps_tmp = mk.ps("s5_ps_tmp", [128, 512]); b_pt = Buf()
    for j in range(4):
        for ri in range(2):
            PE(lambda: nc.tensor.transpose(ps_tmp[:, ri * 128:(ri + 1) * 128], BD[:, j, ri, :], ident[:, :]), r=[bp, bc], w=[b_pt])
        V(lambda: nc.vector.tensor_copy(out=BbT[:, j, :, :].rearrange("p a b -> p (a b)"), in_=ps_tmp[:, 0:256]), r=[], w=[b_pt, bp])
    cosT = sb("s5_cosT", [128, 4, CS]); sinT = sb("s5_sinT", [128, 4, CS])
    xa = sb("s5_xa", [128, 4 * CS]); xb = sb("s5_xb", [128, 4 * CS]); xc = sb("s5_xc", [128, 4 * CS])
    for j in range(4):
        V(lambda: nc.vector.tensor_scalar(out=xc[:, j * CS:(j + 1) * CS], in0=iot[:], scalar1=P["th"][:, j:j + 1], scalar2=None, op0=ALU.mult), r=[bp, bc], w=[bp])
    sincos(sinT[:].rearrange("p a b -> p (a b)"), cosT[:].rearrange("p a b -> p (a b)"), xc[:], 4 * CS, xa[:], xb[:], ti[:])
    V(lambda: nc.vector.tensor_scalar(out=P["t3"][:], in0=P["th"][:], scalar1=float(CS), scalar2=None, op0=ALU.mult), r=[bp], w=[bp])
    sincos(P["sC"][:], P["cC"][:], P["t3"][:], 4, P["t1"][:], P["t2"][:], ti[:, 0:4])
    WW = []
    for par in range(2):
        Wd = {}
        for nm in ("t1", "t2", "t3", "t4", "br", "bi", "zr", "zi", "q1", "q2", "q3", "q4"):
            Wd[nm] = sb("s5_w%d_%s" % (par, nm), [128, CS])
        WW.append(Wd)
    b_wl = [Buf("w0"), Buf("w1")]; b_zl = [Buf("z0"), Buf("z1")]; b_ql = [Buf("q0"), Buf("q1")]
    init = sb("s5_init", [128, 4, 2]); itmp = sb("s5_itmp", [128, 4, 2]); b_il = [Buf("init%d" % j) for j in range(4)]
    for j in range(4):
        V(lambda: nc.vector.memset(init[:, j, :], 0.0), w=[b_il[j]])
    yv = sb("s5_yv", [128, CS]); y2 = sb("s5_y2", [128, CS]); yo = sb("s5_yo", [128, CS], odt); b_y = Buf("y")
    ps_al = [mk.ps("s5_ps_a%d" % i, [128, 512]) for i in range(2)]; ps_bl = [mk.ps("s5_ps_b%d" % i, [128, 512]) for i in range(2)]
    ps_y = mk.ps("s5_ps_y", [128, 512])
    b_pal, b_pbl, b_py = [Buf(), Buf()], [Buf(), Buf()], Buf()
    uts = [sb("s5_u%d" % i, [128, CS]) for i in range(2)]; b_ul = [Buf(), Buf()]
    NCK = T // CS

    def stage1(n):
        chk, j = n // 4, n % 4
        par = n % 2
        ut = uts[chk % 2]; b_u = b_ul[chk % 2]
        if j == 0:
            mk.dma("sp", ut[:], uT[:, chk * CS:(chk + 1) * CS], writes=[b_u])
        W = WW[par]; b_w = b_wl[par]
        ps_a = ps_al[par]; ps_b = ps_bl[par]; b_pa = b_pal[par]; b_pb = b_pbl[par]
        PE(lambda: nc.tensor.matmul(ps_a[:, :], lhsT=BbT[:, j, 0, :], rhs=ut[:, :], start=True, stop=True), r=[bp, b_u], w=[b_pa])
        PE(lambda: nc.tensor.matmul(ps_b[:, :], lhsT=BbT[:, j, 1, :], rhs=ut[:, :], start=True, stop=True), r=[bp, b_u], w=[b_pb])

    def stage1v(n):
        chk, j = n // 4, n % 4
        par = n % 2
        W = WW[par]; b_w = b_wl[par]
        ps_a = ps_al[par]; ps_b = ps_bl[par]; b_pa = b_pal[par]; b_pb = b_pbl[par]
        cj, sj = cosT[:, j, :], sinT[:, j, :]
        V(lambda: nc.vector.tensor_tensor(out=W["t1"][:], in0=ps_a[:, :], in1=cj, op=ALU.mult), r=[bp], w=[b_pa, b_w])
        V(lambda: nc.vector.tensor_tensor(out=W["t4"][:], in0=ps_a[:, :], in1=sj, op=ALU.mult), r=[bp], w=[b_pa, b_w])
        V(lambda: nc.vector.tensor_tensor(out=W["t2"][:], in0=ps_b[:, :], in1=sj, op=ALU.mult), r=[bp], w=[b_pb, b_w])
        V(lambda: nc.vector.tensor_tensor(out=W["t3"][:], in0=ps_b[:, :], in1=cj, op=ALU.mult), r=[bp], w=[b_pb, b_w])
        G(lambda: nc.gpsimd.tensor_tensor(out=W["br"][:], in0=W["t1"][:], in1=W["t2"][:], op=ALU.add), r=[b_w], w=[b_w])
        G(lambda: nc.gpsimd.tensor_tensor(out=W["bi"][:], in0=W["t3"][:], in1=W["t4"][:], op=ALU.subtract), r=[b_w], w=[b_w])

    def stage2(n):
        chk, j = n // 4, n % 4
        par = n % 2
        t0 = chk * CS
        ut = uts[chk % 2]; b_u = b_ul[chk % 2]
        W = WW[par]; b_w = b_wl[par]; b_z = b_zl[par]; b_q = b_ql[par]; b_i = b_il[j]
        cj, sj = cosT[:, j, :], sinT[:, j, :]
        rho = P["mag"][:, j:j + 1].to_broadcast([128, CS])
        V(lambda: nc.vector.tensor_tensor_scan(out=W["zr"][:], data0=rho, data1=W["br"][:], initial=init[:, j, 0:1], op0=ALU.mult, op1=ALU.add), r=[b_w, bp, b_i, b_q], w=[b_z])
        V(lambda: nc.vector.tensor_tensor_scan(out=W["zi"][:], data0=rho, data1=W["bi"][:], initial=init[:, j, 1:2], op0=ALU.mult, op1=ALU.add), r=[b_w, bp, b_i, b_q], w=[b_z])
        zrl, zil = W["zr"][:, CS - 1:CS], W["zi"][:, CS - 1:CS]
        cC, sC = P["cC"][:, j:j + 1], P["sC"][:, j:j + 1]
        V(lambda: nc.vector.tensor_tensor(out=itmp[:, j, 0:1], in0=zil, in1=sC, op=ALU.mult), r=[b_z, bp], w=[b_i])
        V(lambda: nc.vector.scalar_tensor_tensor(out=init[:, j, 0:1], in0=zrl, scalar=cC, in1=itmp[:, j, 0:1], op0=ALU.mult, op1=ALU.subtract), r=[b_z, bp], w=[b_i])
        V(lambda: nc.vector.tensor_tensor(out=itmp[:, j, 1:2], in0=zrl, in1=sC, op=ALU.mult), r=[b_z, bp], w=[b_i])
        V(lambda: nc.vector.scalar_tensor_tensor(out=init[:, j, 1:2], in0=zil, scalar=cC, in1=itmp[:, j, 1:2], op0=ALU.mult, op1=ALU.add), r=[b_z, bp], w=[b_i])
        G(lambda: nc.gpsimd.tensor_tensor(out=W["q1"][:], in0=W["zr"][:], in1=cj, op=ALU.mult), r=[b_z, bp], w=[b_q])
        G(lambda: nc.gpsimd.tensor_tensor(out=W["q2"][:], in0=W["zi"][:], in1=sj, op=ALU.mult), r=[b_z, bp], w=[b_q])
        V(lambda: nc.vector.tensor_tensor(out=W["q3"][:], in0=W["zi"][:], in1=cj, op=ALU.mult), r=[b_z, bp], w=[b_q])
        V(lambda: nc.vector.tensor_tensor(out=W["q4"][:], in0=W["zr"][:], in1=sj, op=ALU.mult), r=[b_z, bp], w=[b_q])
        for qi, nm in enumerate(("q1", "q2", "q3", "q4")):
            PE(lambda: nc.tensor.matmul(ps_y[:, :], lhsT=CM[:, j, qi, :], rhs=W[nm][:, :], start=(j == 0 and qi == 0), stop=(j == 3 and qi == 3)), r=[bp, b_q], w=[b_py])
        if j == 3:
            V(lambda: nc.vector.scalar_tensor_tensor(out=yv[:], in0=ut[:], scalar=dsk[:, 0:1], in1=ps_y[:, :], op0=ALU.mult, op1=ALU.add), r=[b_u, bc], w=[b_py, b_y])
            A(lambda: nc.scalar.activation(out=y2[:], in_=yv[:], func=AF.Square), r=[b_y], w=[b_y])
            V(lambda: nc.vector.tensor_scalar(out=y2[:], in0=y2[:], scalar1=0.044715, scalar2=1.0, op0=ALU.mult, op1=ALU.add), r=[b_y], w=[b_y])
            V(lambda: nc.vector.tensor_tensor(out=y2[:], in0=y2[:], in1=yv[:], op=ALU.mult), r=[b_y], w=[b_y])
            A(lambda: nc.scalar.activation(out=y2[:], in_=y2[:], func=AF.Tanh, scale=0.7978845608028654), r=[b_y], w=[b_y])
            V(lambda: nc.vector.scalar_tensor_tensor(out=y2[:], in0=y2[:], scalar=1.0, in1=yv[:], op0=ALU.add, op1=ALU.mult), r=[b_y], w=[b_y])
            A(lambda: nc.scalar.mul(out=yo[:], in_=y2[:], mul=0.5), r=[b_y], w=[b_y])
            mk.dma("sp", yT[:, t0:t0 + CS], yo[:], reads=[b_y], is_output=True)

    NTL_ = NCK * 4
    stage1(0)
    for n in range(NTL_ + 1):
        if n + 1 < NTL_:
            stage1(n + 1)
        if n < NTL_:
            stage1v(n)
        if n >= 1:
            stage2(n - 1)


def build(which, T):
    nc = bass.Bass("TRN2", target_bir_lowering=False)
    dt = lambda n, s, k="ExternalInput": nc.dram_tensor(n, s, F32, kind=k).ap()
    with ExitStack() as ctx:
        mk = MK(nc, ctx)
        if which == "conv":
            emit_conv(nc, mk, T, dt("cvin", [384, T]), dt("cw", [128, 3]), dt("yT", [128, T], "ExternalOutput"), TB=min(T, 2048))
        elif which == "attn":
            emit_attn(nc, mk, T, dt("qkv", [256, T]), dt("btab", [2, 128, 2, 256]), dt("sinkt", [128, 2]), dt("ident", [128, 128]), dt("yT", [128, T], "ExternalOutput"))
        elif which == "s5":
            emit_s5(nc, mk, T, dt("uT", [128, T]), dt("s5par", [128, 4, 3]), dt("s5bb", [128, 4, 2, 16]), dt("s5cc", [128, 4, 2, 16]),
                    dt("s5d", [128, 1]), dt("s5iota", [128, CS]), dt("ident", [128, 128]), dt("yT", [128, T], "ExternalOutput"))
        mk.finish("sp")
        print(which, "ops", mk.nops, "waits", mk.nwaits)
    return nc


import math
import numpy as np
from contextlib import ExitStack

D = 2048
TOK = 2048
NT = TOK // 128
NE = 32
CAP = 256
ALPHA = 4 ** 0.25
DE = 512


def k3_consts():
    tp = np.arange(128)[:, None]
    t = np.arange(128)[None, :]
    U = (tp < t).astype(np.float32)
    ecap = np.broadcast_to((np.arange(NE) * CAP).astype(np.float32)[None, :], (128, NE)).copy()
    return {"U": U, "ecap": ecap, "ident": np.eye(128, dtype=np.float32)}


def emit_k3(nc, mk, ymixT, x, w_out, glu_w, glu_b, rows, wr, br, w1, w3, w2, cst, x1s, Xg, Yg, xout, ymload=None, rowload=None):
    V = lambda fn, r=(), w=(): mk.op("dve", fn, r, w)
    A = lambda fn, r=(), w=(): mk.op("act", fn, r, w)
    G = lambda fn, r=(), w=(): mk.op("pool", fn, r, w)
    PE = lambda fn, r=(), w=(): mk.op("pe", fn, r, w, skip_same=True)
    gw = mk.sb("k3_gw", [128, NT, 2]); slot = mk.sb("k3_slot", [128, NT, 2], I32); b_rt = Buf("route")
    ident = mk.sb("k3_ident", [128, 128]); identb = mk.sb("k3_identb", [128, 128], BF16); bc = Buf("c")
    mk.dma("sp", ident[:], cst["ident"], writes=[bc])
    V(lambda: nc.vector.tensor_copy(out=identb[:], in_=ident[:]), r=[bc], w=[bc])
    with ExitStack() as pa:
        sb = lambda n, s, dt=F32: pa.enter_context(nc.sbuf_tensor("k3a%d_" % mk.gen + n, list(s), dt))
        ps = lambda n, s, dt=F32: pa.enter_context(nc.psum_tensor("k3a%d_" % mk.gen + n, list(s), dt))
        wo = sb("wo", [128, 16, D], BF16); gluw = sb("gluw", [128, 4, 512], BF16); glub = sb("glub", [128, 4])
        R = [sb("row%d" % i, [128, D]) for i in range(5)]
        wrt = sb("wr", [128, 16, 36]); brt = sb("br", [128, 36]); Ut = sb("U", [128, 128]); ones = sb("ones", [128, 128]); ecap = sb("ecap", [128, NE])
        Srun = sb("Srun", [128, NE]); b_S = Buf("S")
        if ymload is None:
            ym = [sb("ym%d" % i, [128, 16, 128], BF16) for i in range(2)]; b_ym = [Buf(), Buf()]
        else:
            ymbig = sb("ymbig", [128, 16, 1024], BF16); _bym = Buf()
            ym = None; b_ym = [_bym, _bym]
        xt = [sb("xt%d" % i, [128, D]) for i in range(2)]; b_xt = [Buf(), Buf()]
        sg = sb("sg", [128, 4, 128]); b_sg = Buf()
        xr = sb("xr", [128, D]); b_xr = Buf()
        xn = xr; b_xn = b_xr
        _x1 = sb("x1_0", [128, D]); _bx1 = Buf()
        x1 = [_x1, _x1]; b_x1 = [_bx1, _bx1]
        _h2 = sb("h2_0", [128, D]); _bh2 = Buf()
        h2l = [_h2, _h2]; b_h2l = [_bh2, _bh2]
        hb = [sb("hb%d" % i, [128, D], BF16) for i in range(2)]; b_hb = [Buf(), Buf()]
        h2T = sb("h2T", [128, 16, 128]); b_h2T = Buf()
        st = sb("st", [128, 4, 6]); mv = sb("mv", [128, 2]); rs = sb("rs", [128, 1]); nmr = sb("nmr", [128, 1]); b_s = Buf()
        lg = sb("lg", [128, 36]); rt = sb("rt", [128, 16]); em = sb("em", [128, 32]); em2 = sb("em2", [128, 32])
        oh1 = sb("oh1", [128, 32]); oh2 = sb("oh2", [128, 32]); Mk = sb("Mk", [128, 32]); rank = sb("rank", [128, 32]); t32 = sb("t32", [128, 32]); pen = sb("pen", [128, 4]); ohg = sb("ohg", [128, 4]); eg = sb("eg", [128, 4])
        b_r = Buf("r")
        p_g = ps("p_g", [128, 512]); b_pg = Buf()
        p_o = [ps("p_o%d" % i, [128, 512]) for i in range(2)]; b_po = [Buf(), Buf()]
        p_t = [ps("p_t%d" % i, [128, 512]) for i in range(2)]; b_pt = [Buf(), Buf()]
        p_r = ps("p_r", [128, 512]); b_pr = Buf()
        p_k = ps("p_k", [128, 512]); b_pk = Buf()
        mk.dma("pool", wo[:], w_out.rearrange("(k p) c -> p k c", p=128), writes=[bc])
        mk.dma("pool", gluw[:], glu_w.rearrange("(k p) c -> p k c", p=128), writes=[bc])
        mk.dma("sp", glub[:], glu_b, writes=[bc])
        if rowload is None:
            rowload = lambda dst, ri, bcx: mk.dma("sp", dst[:], rows[ri], writes=[bcx])
        for i, ri in enumerate((0, 1, 2, 3, 4)):
            rowload(R[i], ri, bc)
        G(lambda: nc.gpsimd.tensor_scalar(out=R[0][:], in0=R[0][:], scalar1=1.0, scalar2=None, op0=ALU.add), r=[bc], w=[bc])
        G(lambda: nc.gpsimd.tensor_scalar(out=R[3][:], in0=R[3][:], scalar1=1.0, scalar2=None, op0=ALU.add), r=[bc], w=[bc])
        mk.dma("sp", wrt[:], wr.rearrange("(k p) c -> p k c", p=128), writes=[bc])
        mk.dma("sp", brt[:], br, writes=[bc])
        mk.dma("sp", Ut[:], cst["U"], writes=[bc])
        mk.dma("sp", ecap[:], cst["ecap"], writes=[bc])
        V(lambda: nc.vector.memset(ones[:], 1.0), w=[bc])
        V(lambda: nc.vector.memset(Srun[:], 0.0), w=[b_S])
        zt = hb[0]; b_z = b_hb[0]
        V(lambda: nc.vector.memset(zt[:], 0.0), w=[b_z])
        b_Xg = Buf("Xg")
        XgV = Xg.rearrange("(a p) c -> p a c", p=128)
        for a in range(NE * CAP // 128):
            mk.dma("sp", XgV[:, a, :], zt[:], reads=[b_z], writes=[b_Xg])

        def front(t):
            i = t % 2
            ts_ = slice(t * 128, (t + 1) * 128)
            if ymload is None:
                mk.dma("pool", ym[i][:], ymixT[:, ts_].rearrange("(k p) t -> p k t", p=128), writes=[b_ym[i]])
                ymt = ym[i]
            else:
                if t % 8 == 0:
                    ymload(t // 8, ymbig, b_ym[i])
                ymt = ymbig[:, :, (t % 8) * 128:(t % 8 + 1) * 128]
            mk.dma("sp", xt[i][:], x[ts_, :], writes=[b_xt[i]])
            for oc in range(4):
                for kc in range(4):
                    PE(lambda: nc.tensor.matmul(p_g[:, oc * 128:(oc + 1) * 128], lhsT=gluw[:, kc, oc * 128:(oc + 1) * 128], rhs=ymt[:, 12 + kc, :], start=(kc == 0), stop=(kc == 3)),
                       r=[bc, b_ym[i]], w=[b_pg])
            for oc in range(4):
                A(lambda: nc.scalar.activation(out=sg[:, oc, :], in_=p_g[:, oc * 128:(oc + 1) * 128], func=AF.Sigmoid, bias=glub[:, oc:oc + 1], scale=1.0), r=[bc], w=[b_pg, b_sg])
            V(lambda: nc.vector.tensor_tensor(out=ymt[:, 12:16, :], in0=ymt[:, 12:16, :], in1=sg[:], op=ALU.mult), r=[b_sg], w=[b_ym[i]])
            for cc in range(4):
                j = cc % 2
                for k in range(16):
                    PE(lambda: nc.tensor.matmul(p_o[j][:, :], lhsT=ymt[:, k, :], rhs=wo[:, k, cc * 512:(cc + 1) * 512], start=(k == 0), stop=(k == 15)), r=[b_ym[i], bc], w=[b_po[j]])
                V(lambda: nc.vector.tensor_tensor(out=xr[:, cc * 512:(cc + 1) * 512], in0=p_o[j][:, :], in1=R[0][:, cc * 512:(cc + 1) * 512], op=ALU.mult), r=[bc], w=[b_po[j], b_xr])
            V(lambda: nc.vector.scalar_tensor_tensor(out=xr[:], in0=xt[i][:], scalar=ALPHA, in1=xr[:], op0=ALU.mult, op1=ALU.add), r=[b_xt[i]], w=[b_xr])
            ln_stats(nc, mk, xr, b_xr, st, mv, rs, nmr, b_s)
            A(lambda: nc.scalar.activation(out=xn[:], in_=xr[:], func=AF.Identity, bias=nmr[:, 0:1], scale=rs[:, 0:1]), r=[b_s], w=[b_xn])
            V(lambda: nc.vector.tensor_tensor(out=xn[:], in0=xn[:], in1=R[1][:], op=ALU.mult), r=[bc], w=[b_xn])
            V(lambda: nc.vector.tensor_tensor(out=x1[i][:], in0=xn[:], in1=R[2][:], op=ALU.add), r=[b_xn, bc], w=[b_x1[i]])
            mk.dma("sp", x1s[ts_, :], x1[i][:], reads=[b_x1[i]])
            ln_stats(nc, mk, x1[i], b_x1[i], st, mv, rs, nmr, b_s)
            A(lambda: nc.scalar.activation(out=xn[:], in_=x1[i][:], func=AF.Identity, bias=nmr[:, 0:1], scale=rs[:, 0:1]), r=[b_x1[i], b_s], w=[b_xn])
            V(lambda: nc.vector.tensor_tensor(out=xn[:], in0=xn[:], in1=R[3][:], op=ALU.mult), r=[bc], w=[b_xn])
            h2 = h2l[i]; b_h2 = b_h2l[i]
            V(lambda: nc.vector.tensor_tensor(out=h2[:], in0=xn[:], in1=R[4][:], op=ALU.add), r=[b_xn, bc], w=[b_h2])
            A(lambda: nc.scalar.copy(out=hb[i][:], in_=h2[:]), r=[b_h2], w=[b_hb[i]])
        def tail(t):
            i = t % 2
            ts_ = slice(t * 128, (t + 1) * 128)
            h2 = h2l[i]; b_h2 = b_h2l[i]
            for half in range(4):
                j = half % 2
                for kk in range(4):
                    k = half * 4 + kk
                    PE(lambda: nc.tensor.transpose(p_t[j][:, kk * 128:(kk + 1) * 128], h2[:, k * 128:(k + 1) * 128], ident[:, :]), r=[b_h2, bc], w=[b_pt[j]])
                if j == 0:
                    A(lambda: nc.scalar.copy(out=h2T[:, half * 4:(half + 1) * 4, :].rearrange("p a b -> p (a b)"), in_=p_t[j][:, :]), r=[], w=[b_pt[j], b_h2T])
                else:
                    V(lambda: nc.vector.tensor_copy(out=h2T[:, half * 4:(half + 1) * 4, :].rearrange("p a b -> p (a b)"), in_=p_t[j][:, :]), r=[], w=[b_pt[j], b_h2T])
            for k in range(16):
                PE(lambda: nc.tensor.matmul(p_r[:, 0:36], lhsT=h2T[:, k, :], rhs=wrt[:, k, :], start=(k == 0), stop=(k == 15)), r=[b_h2T, bc], w=[b_pr])
            V(lambda: nc.vector.tensor_tensor(out=lg[:], in0=p_r[:, 0:36], in1=brt[:], op=ALU.add), r=[bc], w=[b_pr, b_r])
            R_ = lambda fn: V(fn, r=[b_r, bc], w=[b_r])
            R_(lambda: nc.vector.reduce_max(out=rt[:, 0:1], in_=lg[:, 0:4], axis=AX.X))
            R_(lambda: nc.vector.tensor_scalar(out=ohg[:], in0=lg[:, 0:4], scalar1=rt[:, 0:1], scalar2=None, op0=ALU.is_ge))
            R_(lambda: nc.vector.tensor_scalar(out=rt[:, 1:2], in0=rt[:, 0:1], scalar1=-1.0, scalar2=None, op0=ALU.mult))
            A(lambda: nc.scalar.activation(out=eg[:], in_=lg[:, 0:4], func=AF.Exp, bias=rt[:, 1:2], scale=1.0, accum_out=rt[:, 2:3]), r=[b_r], w=[b_r])
            R_(lambda: nc.vector.reciprocal(out=rt[:, 3:4], in_=rt[:, 2:3]))
            R_(lambda: nc.vector.tensor_scalar(out=pen[:], in0=ohg[:], scalar1=-1.0, scalar2=1e30, op0=ALU.add, op1=ALU.mult))
            R_(lambda: nc.vector.tensor_tensor(out=em[:].rearrange("p (g e) -> p g e", e=8), in0=lg[:, 4:36].rearrange("p (g e) -> p g e", e=8),
                                               in1=pen[:].unsqueeze(2).to_broadcast([128, 4, 8]), op=ALU.add))
            R_(lambda: nc.vector.reduce_max(out=rt[:, 4:5], in_=em[:], axis=AX.X))
            R_(lambda: nc.vector.tensor_scalar(out=oh1[:], in0=em[:], scalar1=rt[:, 4:5], scalar2=None, op0=ALU.is_ge))
            R_(lambda: nc.vector.scalar_tensor_tensor(out=em2[:], in0=oh1[:], scalar=-1e30, in1=em[:], op0=ALU.mult, op1=ALU.add))
            R_(lambda: nc.vector.reduce_max(out=rt[:, 5:6], in_=em2[:], axis=AX.X))
            R_(lambda: nc.vector.tensor_scalar(out=oh2[:], in0=em2[:], scalar1=rt[:, 5:6], scalar2=None, op0=ALU.is_ge))
            R_(lambda: nc.vector.tensor_tensor(out=rt[:, 6:7], in0=rt[:, 5:6], in1=rt[:, 4:5], op=ALU.subtract))
            A(lambda: nc.scalar.activation(out=rt[:, 7:8], in_=rt[:, 6:7], func=AF.Exp), r=[b_r], w=[b_r])
            R_(lambda: nc.vector.tensor_scalar(out=rt[:, 8:9], in0=rt[:, 7:8], scalar1=1.0, scalar2=None, op0=ALU.add))
            R_(lambda: nc.vector.reciprocal(out=rt[:, 8:9], in_=rt[:, 8:9]))
            R_(lambda: nc.vector.tensor_tensor(out=rt[:, 9:10], in0=rt[:, 7:8], in1=rt[:, 8:9], op=ALU.mult))
            R_(lambda: nc.vector.tensor_tensor(out=Mk[:], in0=oh1[:], in1=oh2[:], op=ALU.add))
            PE(lambda: nc.tensor.matmul(p_k[:, 0:32], lhsT=Ut[:, :], rhs=Mk[:, :], start=True, stop=True), r=[bc, b_r], w=[b_pk])
            PE(lambda: nc.tensor.matmul(p_k[:, 32:64], lhsT=ones[:, :], rhs=Mk[:, :], start=True, stop=True), r=[bc, b_r], w=[b_pk])
            V(lambda: nc.vector.tensor_tensor(out=rank[:], in0=p_k[:, 0:32], in1=Srun[:], op=ALU.add), r=[b_S, b_r], w=[b_pk, b_r])
            V(lambda: nc.vector.tensor_tensor(out=Srun[:], in0=p_k[:, 32:64], in1=Srun[:], op=ALU.add), r=[b_r], w=[b_pk, b_S])
            for kx, oh in enumerate((oh1, oh2)):
                R_(lambda: nc.vector.tensor_tensor(out=t32[:], in0=oh[:], in1=rank[:], op=ALU.mult))
                R_(lambda: nc.vector.reduce_sum(out=rt[:, 10:11], in_=t32[:], axis=AX.X))
                R_(lambda: nc.vector.tensor_tensor(out=t32[:], in0=oh[:], in1=ecap[:], op=ALU.mult))
                R_(lambda: nc.vector.reduce_sum(out=rt[:, 11:12], in_=t32[:], axis=AX.X))
                R_(lambda: nc.vector.tensor_scalar(out=rt[:, 12:13], in0=rt[:, 10:11], scalar1=float(CAP), scalar2=None, op0=ALU.is_ge))
                R_(lambda: nc.vector.tensor_tensor(out=rt[:, 11:12], in0=rt[:, 11:12], in1=rt[:, 10:11], op=ALU.add))
                R_(lambda: nc.vector.scalar_tensor_tensor(out=rt[:, 11:12], in0=rt[:, 12:13], scalar=1e6, in1=rt[:, 11:12], op0=ALU.mult, op1=ALU.add))
                V(lambda: nc.vector.tensor_copy(out=slot[:, t, kx:kx + 1], in_=rt[:, 11:12]), r=[b_r], w=[b_rt])
                R_(lambda: nc.vector.tensor_scalar(out=rt[:, 13:14], in0=rt[:, 12:13], scalar1=-1.0, scalar2=-1.0, op0=ALU.add, op1=ALU.mult))
                R_(lambda: nc.vector.tensor_tensor(out=rt[:, 13:14], in0=rt[:, 13:14], in1=rt[:, 3:4], op=ALU.mult))
                V(lambda: nc.vector.tensor_tensor(out=gw[:, t, kx:kx + 1], in0=rt[:, 13:14], in1=rt[:, 8 + kx:9 + kx], op=ALU.mult), r=[b_r], w=[b_rt])
                mk.idma(Xg, hb[i][:, :], slot[:, t, kx:kx + 1], True, NE * CAP - 1, reads=[b_hb[i], b_rt], writes=[b_Xg])
        for t in range(NT):
            front(t)
            tail(t)
        mk.barrier()
    b_Yg = Buf("Yg")
    with ExitStack() as pb:
        sb = lambda n, s, dt=F32: pb.enter_context(nc.sbuf_tensor("k3b%d_" % mk.gen + n, list(s), dt))
        ps = lambda n, s, dt=F32: pb.enter_context(nc.psum_tensor("k3b%d_" % mk.gen + n, list(s), dt))
        W1 = [sb("w1_%d" % i, [128, 16, DE], BF16) for i in range(2)]
        W3 = [sb("w3_%d" % i, [128, 16, DE], BF16) for i in range(2)]
        W2 = [sb("w2_%d" % i, [128, 4, D], BF16) for i in range(2)]
        b_w = [Buf(), Buf()]
        NTL = CAP // 128
        xg = sb("xg", [128, NTL, D], BF16); b_xg = Buf()
        xgT = sb("xgT", [128, 16, CAP], BF16); b_xgT = Buf()
        ga = sb("ga", [128, CAP]); b_ga = Buf()
        gh = sb("gh", [128, 4, CAP], BF16); b_gh = Buf()
        yo = [sb("yo%d" % i, [128, D]) for i in range(2)]; b_yo = [Buf(), Buf()]
        p_t = [ps("p_t%d" % i, [128, 1024], BF16) for i in range(2)]; b_pt = [Buf(), Buf()]
        p_a = ps("p_a", [128, 512]); p_b = ps("p_b", [128, 512]); b_pa, b_pb = Buf(), Buf()
        p_o = [ps("p_o%d" % i, [128, 512]) for i in range(2)]; b_po = [Buf(), Buf()]

        def load_w(e):
            j = e % 2
            mk.dma("pool", W1[j][:], w1[e].rearrange("(p k) c -> p k c", k=16), writes=[b_w[j]])
            mk.dma("pool", W3[j][:], w3[e].rearrange("(p k) c -> p k c", k=16), writes=[b_w[j]])
            mk.dma("pool", W2[j][:], w2[e].rearrange("(k p) c -> p k c", p=128), writes=[b_w[j]])

        xgl = [xg, sb("xg_b", [128, NTL, D], BF16)]; b_xgl = [b_xg, Buf()]
        xgTl = [xgT, sb("xgT_b", [128, 16, CAP], BF16)]; b_xgTl = [b_xgT, Buf()]

        def prep(e):
            q_ = e % 2
            xg_, bxg_, xgT_, bxgT_ = xgl[q_], b_xgl[q_], xgTl[q_], b_xgTl[q_]
            mk.dma("sp", xg_[:], Xg[e * CAP:(e + 1) * CAP, :].rearrange("(a p) c -> p a c", p=128), reads=[b_Xg], writes=[bxg_])
            yield
            for a in range(NTL):
                for half in range(2):
                    for kk in range(8):
                        k = half * 8 + kk
                        PE(lambda: nc.tensor.transpose(p_t[half][:, kk * 128:(kk + 1) * 128], xg_[:, a, :].rearrange("t (p k) -> t k p", k=16)[:, k, :], identb[:, :]), r=[bxg_, bc], w=[b_pt[half]])
                    if half == 0:
                        A(lambda: nc.scalar.copy(out=xgT_[:, 0:8, a * 128:(a + 1) * 128], in_=p_t[half][:, :].rearrange("p (k t) -> p k t", t=128)), r=[], w=[b_pt[half], bxgT_])
                    else:
                        V(lambda: nc.vector.tensor_copy(out=xgT_[:, 8:16, a * 128:(a + 1) * 128], in_=p_t[half][:, :].rearrange("p (k t) -> p k t", t=128)), r=[], w=[b_pt[half], bxgT_])
                    yield

        load_w(0)
        yoi = 0
        for _ in prep(0):
            pass
        for e in range(NE):
            j = e % 2
            if e + 1 < NE:
                load_w(e + 1)
            nx = prep(e + 1) if e + 1 < NE else None

            def tick():
                if nx is not None:
                    next(nx, None)
            xgT_ = xgTl[e % 2]; bxgT_ = b_xgTl[e % 2]
            for hc in range(4):
                for k in range(16):
                    PE(lambda: nc.tensor.matmul(p_a[:, 0:CAP], lhsT=W1[j][:, k, hc * 128:(hc + 1) * 128], rhs=xgT_[:, k, :], start=(k == 0), stop=(k == 15)), r=[b_w[j], bxgT_], w=[b_pa])
                for k in range(16):
                    PE(lambda: nc.tensor.matmul(p_b[:, 0:CAP], lhsT=W3[j][:, k, hc * 128:(hc + 1) * 128], rhs=xgT_[:, k, :], start=(k == 0), stop=(k == 15)), r=[b_w[j], bxgT_], w=[b_pb])
                tick()
                A(lambda: nc.scalar.activation(out=ga[:], in_=p_a[:, 0:CAP], func=AF.Silu), r=[], w=[b_pa, b_ga])
                V(lambda: nc.vector.tensor_tensor(out=gh[:, hc, :], in0=p_b[:, 0:CAP], in1=ga[:], op=ALU.mult), r=[b_ga], w=[b_pb, b_gh])
            for a in range(NTL):
                y_ = yoi % 2
                yoi += 1
                for cc in range(4):
                    q = cc % 2
                    for hc in range(4):
                        PE(lambda: nc.tensor.matmul(p_o[q][:, :], lhsT=gh[:, hc, a * 128:(a + 1) * 128], rhs=W2[j][:, hc, cc * 512:(cc + 1) * 512], start=(hc == 0), stop=(hc == 3)), r=[b_gh, b_w[j]], w=[b_po[q]])
                    if q == 0:
                        A(lambda: nc.scalar.copy(out=yo[y_][:, cc * 512:(cc + 1) * 512], in_=p_o[q][:, :]), r=[], w=[b_po[q], b_yo[y_]])
                    else:
                        V(lambda: nc.vector.tensor_copy(out=yo[y_][:, cc * 512:(cc + 1) * 512], in_=p_o[q][:, :]), r=[], w=[b_po[q], b_yo[y_]])
                r0 = e * CAP + a * 128
                mk.dma("sp", Yg[r0:r0 + 128, :], yo[y_][:], reads=[b_yo[y_]], writes=[b_Yg])
            if nx is not None:
                for _ in nx:
                    pass
        mk.barrier()
    with ExitStack() as pc:
        sb = lambda n, s, dt=F32: pc.enter_context(nc.sbuf_tensor("k3c%d_" % mk.gen + n, list(s), dt))
        R = [sb("row%d" % i, [128, D]) for i in range(3)]
        for i, ri in enumerate((5, 6, 7)):
            rowload(R[i], ri, bc)
        G(lambda: nc.gpsimd.tensor_scalar(out=R[0][:], in0=R[0][:], scalar1=1.0, scalar2=None, op0=ALU.add), r=[bc], w=[bc])
        Y = [[sb("Y%d_%d" % (k, i), [128, D]) for i in range(2)] for k in range(2)]
        b_Y = [[Buf(), Buf()], [Buf(), Buf()]]
        for k in range(2):
            for i in range(2):
                V(lambda: nc.vector.memset(Y[k][i][:], 0.0), w=[b_Y[k][i]])
        x1t = [sb("x1t%d" % i, [128, D]) for i in range(2)]; b_x1 = [Buf(), Buf()]
        yml = [sb("ym%d" % i, [128, D]) for i in range(2)]; b_yml = [Buf(), Buf()]
        xnl = [sb("xn%d" % i, [128, D]) for i in range(2)]; b_xnl = [Buf(), Buf()]
        ot = [sb("ot%d" % i, [128, D]) for i in range(2)]; b_ot = [Buf(), Buf()]
        stl = [sb("st%d" % i, [128, 4, 6]) for i in range(2)]; mvl = [sb("mv%d" % i, [128, 2]) for i in range(2)]
        rsl = [sb("rs%d" % i, [128, 1]) for i in range(2)]; nmrl = [sb("nmr%d" % i, [128, 1]) for i in range(2)]; b_sl = [Buf(), Buf()]
        for t in range(NT):
            i = t % 2
            ts_ = slice(t * 128, (t + 1) * 128)
            ym = yml[i]; b_ym = b_yml[i]; xn = xnl[i]; b_xn = b_xnl[i]
            st, mv, rs, nmr, b_s = stl[i], mvl[i], rsl[i], nmrl[i], b_sl[i]
            for k in range(2):
                mk.idma(Y[k][i][:, :], Yg, slot[:, t, k:k + 1], False, NE * CAP - 1, reads=[b_Yg, b_rt], writes=[b_Y[k][i]])
            mk.dma("sp", x1t[i][:], x1s[ts_, :], writes=[b_x1[i]])
            A(lambda: nc.scalar.activation(out=ym[:], in_=Y[0][i][:], func=AF.Copy, scale=gw[:, t, 0:1]), r=[b_Y[0][i], b_rt], w=[b_ym])
            V(lambda: nc.vector.scalar_tensor_tensor(out=ym[:], in0=Y[1][i][:], scalar=gw[:, t, 1:2], in1=ym[:], op0=ALU.mult, op1=ALU.add), r=[b_Y[1][i], b_rt], w=[b_ym])
            V(lambda: nc.vector.tensor_tensor(out=ym[:], in0=ym[:], in1=R[0][:], op=ALU.mult), r=[bc], w=[b_ym])
            V(lambda: nc.vector.scalar_tensor_tensor(out=ym[:], in0=x1t[i][:], scalar=ALPHA, in1=ym[:], op0=ALU.mult, op1=ALU.add), r=[b_x1[i]], w=[b_ym])
            ln_stats(nc, mk, ym, b_ym, st, mv, rs, nmr, b_s)
            A(lambda: nc.scalar.activation(out=xn[:], in_=ym[:], func=AF.Identity, bias=nmr[:, 0:1], scale=rs[:, 0:1]), r=[b_ym, b_s], w=[b_xn])
            V(lambda: nc.vector.tensor_tensor(out=xn[:], in0=xn[:], in1=R[1][:], op=ALU.mult), r=[bc], w=[b_xn])
            V(lambda: nc.vector.tensor_tensor(out=ot[i][:], in0=xn[:], in1=R[2][:], op=ALU.add), r=[b_xn, bc], w=[b_ot[i]])
            mk.dma("sp", xout[ts_, :], ot[i][:], reads=[b_ot[i]], is_output=True)
        mk.barrier()


def build_k3():
    nc = bass.Bass("TRN2", target_bir_lowering=False)
    dt = lambda n, s, k="ExternalInput", d=F32: nc.dram_tensor(n, s, d, kind=k).ap()
    ymixT = dt("ymixT", [D, TOK]); x = dt("x", [TOK, D]); w_out = dt("w_out", [D, D]); glu_w = dt("glu_w", [512, 512]); glu_b = dt("glu_b", [128, 4])
    rows = dt("rows", [8, 128, D]); wr = dt("wr", [D, 36]); br = dt("br", [128, 36])
    w1 = dt("w1", [NE, D, DE]); w3 = dt("w3", [NE, D, DE]); w2 = dt("w2", [NE, DE, D])
    cst = {"U": dt("U", [128, 128]), "ecap": dt("ecap", [128, NE]), "ident": dt("ident", [128, 128])}
    x1s = dt("x1s", [TOK, D], "Internal"); Xg = dt("Xg", [NE * CAP, D], "Internal", BF16); Yg = dt("Yg", [NE * CAP, D], "Internal")
    xout = dt("xout", [TOK, D], "ExternalOutput")
    with ExitStack() as ctx:
        mk = MK(nc, ctx)
        emit_k3(nc, mk, ymixT, x, w_out, glu_w, glu_b, rows, wr, br, w1, w3, w2, cst, x1s, Xg, Yg, xout)
        mk.finish("sp")
        print("k3 ops", mk.nops, "waits", mk.nwaits)
    return nc


def k3_host_inputs(prm, mod, l, b):
    rep = lambda v: np.ascontiguousarray(np.broadcast_to(v[None, :], (128, v.shape[0])))
    sh1, sc1, gt1, sh2, sc2, gt2 = [mod[l, b, i * D:(i + 1) * D] for i in range(6)]
    rows = np.stack([rep(gt1), rep(prm["ln_g"][l, 0]), rep(prm["ln_b"][l, 0]), rep(sc2), rep(sh2), rep(gt2), rep(prm["ln_g"][l, 1]), rep(prm["ln_b"][l, 1])])
    wr = np.ascontiguousarray(np.concatenate([prm["router_group_w"][l], prm["router_expert_w"][l]], axis=1))
    br = rep(np.concatenate([prm["router_group_b"][l], prm["router_expert_b"][l]]))
    d = {"rows": rows, "wr": wr, "br": br, "w_out": prm["w_out"][l], "glu_w": prm["s5_glu_w"][l],
         "glu_b": np.ascontiguousarray(prm["s5_glu_b"][l].reshape(4, 128).T),
         "w1": prm["moe_w1"][l], "w3": prm["moe_w3"][l], "w2": prm["moe_w2"][l]}
    d.update(k3_consts())
    return d


G_ = 512
RW_OFF = 3 * G_
RW_COLS = 3 * G_ + 96 + 96 + 128
ATT_OFF = RW_OFF + RW_COLS
S5_OFF = ATT_OFF + 512 + 2 * 128
SEQ = 8192
RG4 = [[0, 1, 2, 3], [4, 5, 6, 7]]
NMINE = 1472
MODC = 3072


def emit_k0f(nc, mk, cT, w, bb, modin):
    ct = mk.sb("k0_ct", [128, 16, 2]); sct = mk.sb("k0_sct", [128, 16, 2])
    wt = [mk.sb("k0_wt%d" % i, [128, 16, 512]) for i in range(2)]
    bt = mk.sb("k0_bt", [2, 2, MODC]); ot = mk.sb("k0_ot", [2, 2, MODC])
    P = [mk.ps("k0_P%d" % i, [2, 512]) for i in range(2)]
    b_c, b_b, b_o = Buf(), Buf(), Buf()
    b_w = [Buf(), Buf()]; b_p = [Buf(), Buf()]
    mk.dma("sp", ct[:], cT, writes=[b_c])
    mk.dma("sp", bt[:], bb.rearrange("l b c -> b l c"), writes=[b_b])
    mk.op("act", lambda: nc.scalar.activation(out=sct[:], in_=ct[:], func=AF.Silu), reads=[b_c], writes=[b_c])
    it = 0
    for l in range(2):
        for n in range(MODC // 512):
            i = it % 2
            it += 1
            mk.dma("sp", wt[i][:], w[l, :, n * 512:(n + 1) * 512].rearrange("(k p) c -> p k c", p=128), writes=[b_w[i]])
            for k in range(16):
                mk.op("pe", lambda: nc.tensor.matmul(P[i][:], lhsT=sct[:, k, :], rhs=wt[i][:, k, :], start=(k == 0), stop=(k == 15)),
                      reads=[b_c, b_w[i]], writes=[b_p[i]], skip_same=True)
            mk.op("dve", lambda: nc.vector.tensor_tensor(out=ot[:, l, n * 512:(n + 1) * 512], in0=P[i][:], in1=bt[:, l, n * 512:(n + 1) * 512], op=ALU.add),
                  reads=[b_b], writes=[b_p[i], b_o])
    mk.dma("sp", modin.rearrange("(o l) c -> o l c", o=1), ot[0:1, :, :], reads=[b_o])


def mod_row_load(nc, mk, dst, modall, l, chunk, bc):
    c0 = chunk * 2048
    done = 0
    while done < 2048:
        col = c0 + done
        r = col // MODC
        off = col % MODC
        n = min(2048 - done, MODC - off)
        src = modall[r * 2 + l:r * 2 + l + 1, off:off + n].partition_broadcast(128)
        mk.dma("sp", dst[:, done:done + n], src, writes=[bc])
        done += n


def emit_k1a(nc, mk, x, modall, l, ident_d, hTs):
    NT_ = TOK // 128
    xt = [mk.sb("a_xt%d" % i, [128, D]) for i in range(2)]
    xn = mk.sb("a_xn", [128, D]); h1 = mk.sb("a_h1", [128, D])
    hb = [mk.sb("a_hb%d" % i, [128, D], BF16) for i in range(2)]
    hT = mk.sb("a_hT", [128, 16, TOK], BF16)
    sct = mk.sb("a_sct", [128, D]); sht = mk.sb("a_sht", [128, D])
    idf = mk.sb("a_idf", [128, 128]); idb = mk.sb("a_idb", [128, 128], BF16)
    st = mk.sb("a_st", [128, 4, 6]); mv = mk.sb("a_mv", [128, 2]); rs = mk.sb("a_rs", [128, 1]); nmr = mk.sb("a_nmr", [128, 1])
    PT = [mk.ps("a_PT%d" % i, [128, 8, 128], BF16) for i in range(2)]
    b_x = [Buf(), Buf()]
    b_xn, b_h1, b_s, b_sc, b_sh, b_id, b_hT = Buf(), Buf(), Buf(), Buf(), Buf(), Buf(), Buf()
    b_hb = [Buf(), Buf()]; b_pt = [Buf(), Buf()]
    mod_row_load(nc, mk, sct, modall, l, 1, b_sc)
    mod_row_load(nc, mk, sht, modall, l, 0, b_sh)
    mk.dma("sp", idf[:], ident_d, writes=[b_id])
    mk.op("dve", lambda: nc.vector.tensor_copy(out=idb[:], in_=idf[:]), reads=[b_id], writes=[b_id])
    mk.op("pool", lambda: nc.gpsimd.tensor_scalar(out=sct[:], in0=sct[:], scalar1=1.0, scalar2=None, op0=ALU.add), reads=[b_sc], writes=[b_sc])
    for t in range(NT_):
        i = t % 2
        mk.dma("sp", xt[i][:], x[t * 128:(t + 1) * 128, :], writes=[b_x[i]])
        ln_stats(nc, mk, xt[i], b_x[i], st, mv, rs, nmr, b_s)
        mk.op("act", lambda: nc.scalar.activation(out=xn[:], in_=xt[i][:], func=AF.Identity, bias=nmr[:, 0:1], scale=rs[:, 0:1]),
              reads=[b_x[i], b_s], writes=[b_xn])
        mk.op("dve", lambda: nc.vector.tensor_tensor(out=h1[:], in0=xn[:], in1=sct[:], op=ALU.mult), reads=[b_xn, b_sc], writes=[b_h1])
        mk.op("dve", lambda: nc.vector.tensor_tensor(out=hb[i][:], in0=h1[:], in1=sht[:], op=ALU.add), reads=[b_h1, b_sh], writes=[b_hb[i]])
        for half in range(2):
            for kk in range(8):
                k = half * 8 + kk
                mk.op("pe", lambda: nc.tensor.transpose(PT[half][:, kk, :], hb[i][:, k * 128:(k + 1) * 128], idb[:]),
                      reads=[b_hb[i], b_id], writes=[b_pt[half]], skip_same=True)
            if half == 0:
                mk.op("act", lambda: nc.scalar.copy(out=hT[:, 0:8, t * 128:(t + 1) * 128], in_=PT[half][:]), reads=[], writes=[b_pt[half], b_hT])
            else:
                mk.op("dve", lambda: nc.vector.tensor_copy(out=hT[:, 8:16, t * 128:(t + 1) * 128], in_=PT[half][:]), reads=[], writes=[b_pt[half], b_hT])
    mk.dma("sp", hTs.rearrange("(k p) t -> p k t", p=128), hT[:], reads=[b_hT])


def emit_k1b(nc, mk, hTg, wmine, pmine):
    NB_ = (NMINE + 127) // 128
    wt = mk.sb("b_wt", [128, 16, NB_ * 128], BF16); b_w = Buf()
    ht = [mk.sb("b_ht%d" % i, [128, 16, 512], BF16) for i in range(2)]; b_h = [Buf(), Buf()]
    ot = [mk.sb("b_ot%d" % i, [128, 512]) for i in range(4)]; b_o = [Buf() for _ in range(4)]
    PM = [mk.ps("b_PM%d" % i, [128, 512]) for i in range(4)]; b_pm = [Buf() for _ in range(4)]
    for j in range(NB_):
        c0 = j * 128
        cw = min(128, NMINE - c0)
        mk.dma("pool", wt[:, :, c0:c0 + cw], wmine[:, c0:c0 + cw].rearrange("(k p) c -> p k c", p=128), writes=[b_w])
    pi = 0
    for tc in range(SEQ // 512):
        i = tc % 2
        r = tc // 4
        t0 = (tc % 4) * 512
        src = hTg.rearrange("(c r h p) t -> r p c h t", c=8, r=4, h=2, p=128)[r]
        for c in range(8):
            mk.dma("sp", ht[i][:, 2 * c:2 * c + 2, :], src[:, c, :, t0:t0 + 512], writes=[b_h[i]])
        for j in range(NB_):
            c0 = j * 128
            cw = min(128, NMINE - c0)
            q = pi % 4
            pi += 1
            for k in range(16):
                mk.op("pe", lambda: nc.tensor.matmul(PM[q][0:cw, :], lhsT=wt[:, k, c0:c0 + cw], rhs=ht[i][:, k, :], start=(k == 0), stop=(k == 15)),
                      reads=[b_w, b_h[i]], writes=[b_pm[q]], skip_same=True)
            if q % 2 == 0:
                mk.op("act", lambda: nc.scalar.copy(out=ot[q][0:cw, :], in_=PM[q][0:cw, :]), reads=[], writes=[b_pm[q], b_o[q]])
            else:
                mk.op("dve", lambda: nc.vector.tensor_copy(out=ot[q][0:cw, :], in_=PM[q][0:cw, :]), reads=[], writes=[b_pm[q], b_o[q]])
            mk.dma("sp", pmine[c0:c0 + cw, tc * 512:(tc + 1) * 512], ot[q][0:cw, :], reads=[b_o[q]])


def build_fused():
    nc = bass.Bass("TRN2", target_bir_lowering=False)
    T = SEQ
    din = lambda n, s, d=F32: nc.dram_tensor(n, s, d, kind="ExternalInput").ap()
    scr = lambda n, s, d=F32: nc.dram_tensor(n, s, d).ap()
    x_in = din("x", [TOK, D]); cT = din("cT", [128, 16, 2]); w_ada = din("w_ada", [2, D, MODC]); bb = din("bb", [2, 2, MODC])
    wmine = din("wmine", [2, D, NMINE]); w_out = din("w_out", [2, D, D]); lnp = din("lnp", [2, 4, D])
    cw = din("cw", [2, 128, 3]); btab = din("btab", [2, 2, 128, 2, 256]); sinkt = din("sinkt", [2, 128, 2]); ident = din("ident", [128, 128])
    s5par = din("s5par", [2, 128, 4, 3]); s5bb = din("s5bb", [2, 128, 4, 2, 16]); s5cc = din("s5cc", [2, 128, 4, 2, 16]); s5d = din("s5d", [2, 128, 1]); s5iota = din("s5iota", [128, CS])
    par64 = din("par64", [2, 64, 2, 11]); par128 = din("par128", [2, 128, 3]); w2 = din("w2", [2, 96, 128]); a2 = din("a2", [2, 96, 128]); g2 = din("g2", [2, 128, 128]); gnt = din("gnt", [2, 64, 2, 2, 64])
    cst = {"mask1": din("mask1", [64, 512]), "mask3": din("mask3", [64, 256]), "seg": din("seg", [128, TB]), "ident": ident, "U": din("U", [128, 128]), "ecap": din("ecap", [128, NE])}
    glu_w = din("glu_w", [2, 512, 512]); glu_b = din("glu_b", [2, 128, 4]); wr = din("wr", [2, D, 36]); br = din("br", [2, 128, 36])
    w1 = din("w1", [2, NE, D, DE]); w3 = din("w3", [2, NE, D, DE]); w2m = din("w2m", [2, NE, DE, D])
    ymidx_d = din("ymidx", [128, 16, 2], I32)
    y_out = nc.dram_tensor("y", [TOK, D], F32, kind="ExternalOutput").ap()
    modin = scr("modin", [2, MODC]); modall = scr("modall", [8, MODC])
    hTs = scr("hTs", [D, TOK], BF16); hTg = scr("hTg", [4 * D, TOK], BF16)
    pmine = scr("pmine", [NMINE, T])
    yT16 = scr("yT16", [512, T], BF16); ymg = scr("ymg", [2048, T], BF16)
    x1s = scr("x1s", [TOK, D]); Xg = scr("Xg", [NE * CAP, D], BF16); Yg = scr("Yg", [NE * CAP, D]); xcur = scr("xcur", [TOK, D])
    ymg_rows = ymg.rearrange("r (tb t) -> (r tb) t", t=1024)
    with ExitStack() as ctx:
        mk = MK(nc, ctx)
        bD = Buf("dram")
        with mk.scope():
            emit_k0f(nc, mk, cT, w_ada, bb, modin)
        mk.collective("AllGather", RG4, modin, modall, reads=[bD], writes=[bD])
        mk.barrier()
        for l in range(2):
            xsrc = x_in if l == 0 else xcur
            xdst = xcur if l == 0 else y_out
            with mk.scope():
                emit_k1a(nc, mk, xsrc, modall, l, ident, hTs)
            for c in range(8):
                mk.collective("AllGather", RG4, hTs[c * 256:(c + 1) * 256, :], hTg[c * 1024:(c + 1) * 1024, :], reads=[bD], writes=[bD])
            mk.barrier()
            with mk.scope():
                emit_k1b(nc, mk, hTg, wmine[l], pmine)
            def ag(chunks):
                for c in chunks:
                    mk.collective("AllGather", RG4, yT16[c * 64:(c + 1) * 64, :], ymg[c * 256:(c + 1) * 256, :], reads=[], writes=[Buf()])
            with mk.scope():
                emit_conv(nc, mk, T, pmine[0:384, :], cw[l], yT16[0:128, :], odt=BF16)
            ag((0, 1))
            with mk.scope():
                emit_attn(nc, mk, T, pmine[1088:1344, :], btab[l], sinkt[l], ident, yT16[256:384, :], odt=BF16)
            ag((4, 5))
            with mk.scope():
                emit_s5(nc, mk, T, pmine[1344:1472, :], s5par[l], s5bb[l], s5cc[l], s5d[l], s5iota, ident, yT16[384:512, :], odt=BF16)
            ag((6, 7))
            with mk.scope():
                emit_rwkv(nc, mk, T, pmine[384:1088, :], par64[l], par128[l], w2[l], a2[l], g2[l], gnt[l], cst, yT16[128:256, :], odt=BF16)
            for c in (2, 3):
                mk.collective("AllGather", RG4, yT16[c * 64:(c + 1) * 64, :], ymg[c * 256:(c + 1) * 256, :], reads=[bD], writes=[bD])
            mk.barrier()
            with mk.scope():
                ymidx = mk.sb("ymidx_sb", [128, 16, 2], I32); b_idx = Buf()
                mk.dma("sp", ymidx[:], ymidx_d, writes=[b_idx])

                def ymload(hf, ymt, b_ymt):
                    for k in range(16):
                        mk.idma(ymt[:, k, :], ymg_rows, ymidx[:, k, hf:hf + 1], False, 2048 * 8 - 1, reads=[b_idx], writes=[b_ymt])

                def rowload(dst, ri, bcx, _l=l):
                    if ri in (1, 2, 6, 7):
                        j = {1: 0, 2: 1, 6: 2, 7: 3}[ri]
                        mk.dma("sp", dst[:], lnp[_l, j:j + 1, :].partition_broadcast(128), writes=[bcx])
                    else:
                        chunk = {0: 2, 3: 4, 4: 3, 5: 5}[ri]
                        mod_row_load(nc, mk, dst, modall, _l, chunk, bcx)

                emit_k3(nc, mk, None, xsrc, w_out[l], glu_w[l], glu_b[l], None, wr[l], br[l], w1[l], w3[l], w2m[l], cst, x1s, Xg, Yg, xdst,
                        ymload=ymload, rowload=rowload)
        mk.finish("sp")
        mk.barrier()
        print("fused ops", mk.nops, "waits", mk.nwaits)
    return nc


_NC_CACHE = {}


def _get(name, fn):
    if name not in _NC_CACHE:
        _NC_CACHE[name] = fn()
    return _NC_CACHE[name]


def _fused_inputs(prm, core):
    b, q = core // 4, core % 4
    eye = np.eye(128, dtype=np.float32)
    d = {}
    d["x"] = np.ascontiguousarray(prm["x"][b, q * TOK:(q + 1) * TOK])
    cb = prm["c"][b]
    d["cT"] = np.ascontiguousarray(np.stack([cb.reshape(16, 128).T, cb.reshape(16, 128).T], axis=-1))
    sl = slice(q * MODC, (q + 1) * MODC)
    d["w_ada"] = np.ascontiguousarray(prm["w_ada"][:, :, sl])
    d["bb"] = np.ascontiguousarray(np.broadcast_to(prm["b_ada"][:, None, sl], (2, 2, MODC)))
    kv = q // 2
    cols = np.concatenate([np.arange(128 * q, 128 * q + 128), np.arange(G_ + 128 * q, G_ + 128 * q + 128), np.arange(2 * G_ + 128 * q, 2 * G_ + 128 * q + 128),
                           RW_OFF + rwkv_rows(q),
                           np.arange(ATT_OFF + 128 * q, ATT_OFF + 128 * q + 128), np.arange(ATT_OFF + 512 + 64 * kv, ATT_OFF + 512 + 64 * kv + 64),
                           np.arange(ATT_OFF + 640 + 64 * kv, ATT_OFF + 640 + 64 * kv + 64),
                           np.arange(S5_OFF + 128 * q, S5_OFF + 128 * q + 128)])
    assert cols.shape[0] == NMINE
    d["wmine"] = np.ascontiguousarray(prm["w_in"][:, :, cols])
    d["w_out"] = prm["w_out"]
    d["lnp"] = np.ascontiguousarray(np.stack([np.stack([prm["ln_g"][l, 0], prm["ln_b"][l, 0], prm["ln_g"][l, 1], prm["ln_b"][l, 1]]) for l in range(2)]))
    d["cw"] = np.ascontiguousarray(np.stack([prm["conv_w"][l][:, 128 * q:128 * q + 128].T for l in range(2)]))
    tabs = [attn_tables(prm["rel_bias"], prm["attn_sinks"][l], q) for l in range(2)]
    d["btab"] = np.ascontiguousarray(np.stack([t[0] for t in tabs])); d["sinkt"] = np.ascontiguousarray(np.stack([t[1] for t in tabs]))
    d["ident"] = eye
    s5 = [s5_host_inputs(prm, l, q) for l in range(2)]
    for k_ in ("s5par", "s5bb", "s5cc", "s5d"):
        d[k_] = np.ascontiguousarray(np.stack([s[k_] for s in s5]))
    d["s5iota"] = s5[0]["s5iota"]
    rw = [rwkv_host_inputs(prm, l, q) for l in range(2)]
    for k_ in ("par64", "par128", "gnt", "w2", "a2", "g2"):
        d[k_] = np.ascontiguousarray(np.stack([r[k_] for r in rw]))
    for k_ in ("mask1", "mask3", "seg"):
        d[k_] = rw[0][k_]
    kc = k3_consts()
    d["U"] = kc["U"]; d["ecap"] = kc["ecap"]
    d["glu_w"] = prm["s5_glu_w"]
    d["glu_b"] = np.ascontiguousarray(np.stack([prm["s5_glu_b"][l].reshape(4, 128).T for l in range(2)]))
    d["wr"] = np.ascontiguousarray(np.stack([np.concatenate([prm["router_group_w"][l], prm["router_expert_w"][l]], axis=1) for l in range(2)]))
    d["br"] = np.ascontiguousarray(np.stack([np.broadcast_to(np.concatenate([prm["router_group_b"][l], prm["router_expert_b"][l]])[None, :], (128, 36)) for l in range(2)]))
    d["w1"] = prm["moe_w1"]; d["w3"] = prm["moe_w3"]; d["w2m"] = prm["moe_w2"]
    p_ = np.arange(128)[:, None, None]; k_i = np.arange(16)[None, :, None]; t_ = np.arange(2)[None, None, :]
    src_row = ((k_i // 4) * 2 + p_ // 64) * 256 + (k_i % 4) * 64 + p_ % 64
    d["ymidx"] = np.ascontiguousarray((src_row * 8 + q * 2 + t_).astype(np.int32))
    return d


def kernel(**inp):
    prm = {k: np.ascontiguousarray(np.asarray(v, dtype=np.float32)) for k, v in inp.items()}
    cores = list(range(8))
    in_maps = [_fused_inputs(prm, c) for c in cores]
    res = run_bass_kernel_spmd(_get("fused", build_fused), in_maps, core_ids=cores)
    out = np.stack([np.concatenate([res.results[b * 4 + q]["y"] for q in range(4)], axis=0) for b in range(2)])
    return out.astype(np.float32)
```

```python
import numpy as np
from contextlib import ExitStack
import concourse.bass as bass
import concourse.mybir as mybir
from concourse.bass_utils import run_bass_kernel_spmd

F32 = mybir.dt.float32
BF16 = mybir.dt.bfloat16
I32 = mybir.dt.int32
U32 = mybir.dt.uint32
AF = mybir.ActivationFunctionType
ALU = mybir.AluOpType
AX = mybir.AxisListType

EPOCH = 1 << 20


class Buf:
    __slots__ = ("name", "w", "r")

    def __init__(self, name=""):
        self.name = name
        self.w = None
        self.r = {}


class MK:
    def __init__(self, nc, ctx, n_dma_sems=24):
        self.nc = nc
        self.ctx = ctx
        self.eng = {"pe": nc.tensor, "dve": nc.vector, "act": nc.scalar,
                    "pool": nc.gpsimd, "sp": nc.sync}
        self.sem = {}
        self.cnt = {e: 0 for e in self.eng}
        self.known = {e: {} for e in self.eng}
        for e in self.eng:
            self.sem[e] = ctx.enter_context(nc.semaphore("s_" + e))
        self.dma_keys = []
        self.dma_val = {}
        for i in range(n_dma_sems):
            k = ("dma", i)
            self.sem[k] = ctx.enter_context(nc.semaphore("s_dma%d" % i))
            self.dma_keys.append(k)
            self.dma_val[k] = 0
        self.dma_rr = 0
        self.nwaits = 0
        self.nops = 0
        self.out_events = []

    gen = 0

    def sb(self, name, shape, dt=F32):
        return self.ctx.enter_context(self.nc.sbuf_tensor("%s_g%d" % (name, self.gen), list(shape), dt))

    def ps(self, name, shape, dt=F32):
        return self.ctx.enter_context(self.nc.psum_tensor("%s_g%d" % (name, self.gen), list(shape), dt))

    def _wait(self, E, ev):
        if ev is None:
            return
        key, val = ev
        if self.known[E].get(key, 0) >= val:
            return
        self.eng[E].wait_ge(self.sem[key], val)
        self.known[E][key] = val
        self.nwaits += 1

    def _deps(self, E, reads, writes, skip_same=False):
        for b in reads:
            if b.w is not None and not (skip_same and b.w[0] == E):
                self._wait(E, b.w)
        for b in writes:
            if b.w is not None and not (skip_same and b.w[0] == E):
                self._wait(E, b.w)
            for ev in b.r.values():
                if not (skip_same and ev[0] == E):
                    self._wait(E, ev)

    def _mark(self, ev, reads, writes):
        for b in reads:
            b.r[ev[0]] = ev
        for b in writes:
            b.w = ev
            b.r = {}

    def op(self, E, fn, reads=(), writes=(), skip_same=False):
        self._deps(E, reads, writes, skip_same)
        inst = fn()
        self.cnt[E] += 1
        inst.then_inc(self.sem[E], 1)
        ev = (E, self.cnt[E])
        self._mark(ev, reads, writes)
        self.nops += 1
        return ev

    def dma(self, Q, out, in_, reads=(), writes=(), is_output=False, **kw):
        self._deps(Q, reads, writes)
        k = self.dma_keys[self.dma_rr]
        self.dma_rr = (self.dma_rr + 1) % len(self.dma_keys)
        self._wait(Q, (k, self.dma_val[k]) if self.dma_val[k] else None)
        self.dma_val[k] += 16
        inst = self.eng[Q].dma_start(out=out, in_=in_, **kw)
        inst.then_inc(self.sem[k], 16)
        ev = (k, self.dma_val[k])
        self._mark(ev, reads, writes)
        if is_output:
            self.out_events.append(ev)
        self.nops += 1
        return ev

    def finish(self, E="sp"):
        for k in self.dma_keys:
            if self.dma_val[k]:
                self._wait(E, (k, self.dma_val[k]))


def _idma(self, out, in_, idx_ap, scatter, bound, reads=(), writes=(), is_output=False):
    Q = "pool"
    self._deps(Q, reads, writes)
    k = self.dma_keys[self.dma_rr]
    self.dma_rr = (self.dma_rr + 1) % len(self.dma_keys)
    self._wait(Q, (k, self.dma_val[k]) if self.dma_val[k] else None)
    self.dma_val[k] += 16
    off = bass.IndirectOffsetOnAxis(ap=idx_ap, axis=0)
    if not hasattr(self, "_bregs"):
        self._bregs = {}
    if bound not in self._bregs:
        self._bregs[bound] = self.nc.gpsimd.to_reg(bound)
    bound = self._bregs[bound]
    if scatter:
        inst = self.nc.gpsimd.indirect_dma_start(out=out, out_offset=off, in_=in_, in_offset=None, bounds_check=bound, oob_is_err=False)
    else:
        inst = self.nc.gpsimd.indirect_dma_start(out=out, out_offset=None, in_=in_, in_offset=off, bounds_check=bound, oob_is_err=False)
    inst.then_inc(self.sem[k], 16)
    ev = (k, self.dma_val[k])
    self._mark(ev, reads, writes)
    if is_output:
        self.out_events.append(ev)
    self.nops += 1
    return ev


MK.idma = _idma


def _barrier(self):
    for E in self.eng:
        for F in self.eng:
            if self.cnt[F]:
                self._wait(E, (F, self.cnt[F]))
        for k in self.dma_keys:
            if self.dma_val[k]:
                self._wait(E, (k, self.dma_val[k]))
        if getattr(self, "cc_val", 0):
            self._wait(E, ("cc", self.cc_val))


MK.barrier = _barrier


from contextlib import contextmanager


@contextmanager
def _scope(self):
    old = self.ctx
    self.gen += 1
    with ExitStack() as s:
        self.ctx = s
        yield
        self.barrier()
    self.ctx = old


MK.scope = _scope


def _collective(self, kind, rg, in_ap, out_ap, reads=(), writes=()):
    Q = "pool"
    if "cc" not in self.sem:
        self.sem["cc"] = self.ctx.enter_context(self.nc.semaphore("s_cc"))
        self.cc_val = 0
    self._deps(Q, reads, writes)
    self.cc_val += 1
    inst = self.nc.gpsimd.collective_compute(kind, ALU.bypass, replica_groups=rg, ins=[in_ap.opt()], outs=[out_ap.opt()])
    inst.then_inc(self.sem["cc"], 1)
    ev = ("cc", self.cc_val)
    self._mark(ev, reads, writes)
    self.nops += 1
    return ev


MK.collective = _collective


import numpy as np
from contextlib import ExitStack

D = 2048
NIN = 4672
TOK = 2048


def build_k0():
    nc = bass.Bass("TRN2", target_bir_lowering=False)
    NCOL = 1536
    cT = nc.dram_tensor("cT", [128, 16, 2], F32, kind="ExternalInput").ap()
    w = nc.dram_tensor("w", [2, D, NCOL], F32, kind="ExternalInput").ap()
    bb = nc.dram_tensor("bb", [2, 2, NCOL], F32, kind="ExternalInput").ap()
    out = nc.dram_tensor("mod", [2, 2, NCOL], F32, kind="ExternalOutput").ap()
    with ExitStack() as ctx:
        mk = MK(nc, ctx)
        ct = mk.sb("ct", [128, 16, 2])
        sct = mk.sb("sct", [128, 16, 2])
        wt = [mk.sb("wt%d" % i, [128, 16, 512]) for i in range(2)]
        bt = mk.sb("bt", [2, 2, NCOL])
        ot = mk.sb("ot", [2, 2, NCOL])
        P = [mk.ps("P%d" % i, [2, 512]) for i in range(2)]
        b_c, b_b, b_o = Buf(), Buf(), Buf()
        b_w = [Buf(), Buf()]
        b_p = [Buf(), Buf()]
        mk.dma("sp", ct[:], cT, writes=[b_c])
        mk.dma("sp", bt[:], bb.rearrange("l b c -> b l c"), writes=[b_b])
        mk.op("act", lambda: nc.scalar.activation(out=sct[:], in_=ct[:], func=AF.Silu), reads=[b_c], writes=[b_c])
        it = 0
        for l in range(2):
            for n in range(3):
                i = it % 2
                it += 1
                mk.dma("sp", wt[i][:], w[l, :, n * 512:(n + 1) * 512].rearrange("(k p) c -> p k c", p=128), writes=[b_w[i]])
                for k in range(16):
                    mk.op("pe", lambda: nc.tensor.matmul(P[i][:], lhsT=sct[:, k, :], rhs=wt[i][:, k, :], start=(k == 0), stop=(k == 15)),
                          reads=[b_c, b_w[i]], writes=[b_p[i]], skip_same=True)
                mk.op("dve", lambda: nc.vector.tensor_tensor(out=ot[:, l, n * 512:(n + 1) * 512], in0=P[i][:], in1=bt[:, l, n * 512:(n + 1) * 512], op=ALU.add),
                      reads=[b_p[i], b_b], writes=[b_o])
        mk.dma("sp", out.rearrange("l b c -> b l c"), ot[:], reads=[b_o], is_output=True)
        mk.finish("sp")
    return nc


def build_k1():
    nc = bass.Bass("TRN2", target_bir_lowering=False)
    x = nc.dram_tensor("x", [TOK, D], F32, kind="ExternalInput").ap()
    sc = nc.dram_tensor("sc", [128, D], F32, kind="ExternalInput").ap()
    sh = nc.dram_tensor("sh", [128, D], F32, kind="ExternalInput").ap()
    w_in = nc.dram_tensor("w_in", [D, NIN], F32, kind="ExternalInput").ap()
    ident = nc.dram_tensor("ident", [128, 128], F32, kind="ExternalInput").ap()
    pT = nc.dram_tensor("pT", [NIN, TOK], F32, kind="ExternalOutput").ap()
    with ExitStack() as ctx:
        mk = MK(nc, ctx)
        emit_k1(nc, mk, x, sc, sh, w_in, ident, pT)
        mk.finish("sp")
        print("k1 ops", mk.nops, "waits", mk.nwaits)
    return nc


def ln_stats(nc, mk, xt, bx, st, mv, rs, nmr, bs, eps=1e-5):
    for c in range(4):
        mk.op("dve", lambda: nc.vector.bn_stats(out=st[:, c, :], in_=xt[:, c * 512:(c + 1) * 512]), reads=[bx], writes=[bs])
    mk.op("dve", lambda: nc.vector.bn_aggr(out=mv[:], in_=st[:].rearrange("p a b -> p (a b)")), reads=[bs], writes=[bs])
    mk.op("act", lambda: nc.scalar.activation(out=rs[:], in_=mv[:, 1:2], func=AF.Sqrt, bias=eps, scale=1.0), reads=[bs], writes=[bs])
    mk.op("dve", lambda: nc.vector.reciprocal(out=rs[:], in_=rs[:]), reads=[bs], writes=[bs])
    mk.op("dve", lambda: nc.vector.tensor_scalar(out=nmr[:], in0=mv[:, 0:1], scalar1=rs[:, 0:1], scalar2=-1.0, op0=ALU.mult, op1=ALU.mult),
          reads=[bs], writes=[bs])


def emit_k1(nc, mk, x, sc, sh, w_in, ident, pT):
    NT = TOK // 128
    xt = [mk.sb("xt%d" % i, [128, D]) for i in range(2)]
    xn = mk.sb("xn", [128, D])
    h1 = mk.sb("h1", [128, D])
    hb = [mk.sb("hb%d" % i, [128, D], BF16) for i in range(2)]
    hT = mk.sb("hT", [128, 16, TOK], BF16)
    sct = mk.sb("sct", [128, D])
    sht = mk.sb("sht", [128, D])
    idf = mk.sb("idf", [128, 128])
    idb = mk.sb("idb", [128, 128], BF16)
    st = mk.sb("st", [128, 4, 6])
    mv = mk.sb("mv", [128, 2])
    rs = mk.sb("rs", [128, 1])
    nmr = mk.sb("nmr", [128, 1])
    wt = [mk.sb("wt%d" % i, [128, 16, 128], BF16) for i in range(2)]
    ot = [mk.sb("ot%d" % i, [128, TOK]) for i in range(2)]
    PT = [mk.ps("PT%d" % i, [128, 8, 128], BF16) for i in range(2)]
    PM = [mk.ps("PM%d" % i, [128, 512]) for i in range(4)]
    b_x = [Buf(), Buf()]
    b_xn, b_h1, b_s, b_sc, b_sh, b_id, b_hT = Buf(), Buf(), Buf(), Buf(), Buf(), Buf(), Buf()
    b_hb = [Buf(), Buf()]
    b_pt = [Buf(), Buf()]
    b_pm = [Buf() for _ in range(4)]
    b_w = [Buf(), Buf()]
    b_o = [Buf(), Buf()]

    mk.dma("sp", sct[:], sc, writes=[b_sc])
    mk.dma("sp", sht[:], sh, writes=[b_sh])
    mk.dma("sp", idf[:], ident, writes=[b_id])
    mk.op("dve", lambda: nc.vector.tensor_copy(out=idb[:], in_=idf[:]), reads=[b_id], writes=[b_id])
    mk.op("pool", lambda: nc.gpsimd.tensor_scalar(out=sct[:], in0=sct[:], scalar1=1.0, scalar2=None, op0=ALU.add), reads=[b_sc], writes=[b_sc])

    NCB = (NIN + 127) // 128

    def load_w(cb):
        j = cb % 2
        c0 = cb * 128
        cw = min(128, NIN - c0)
        mk.dma("pool", wt[j][:, :, 0:cw], w_in[:, c0:c0 + cw].rearrange("(k p) c -> p k c", p=128), writes=[b_w[j]])

    load_w(0)
    load_w(1)
    for t in range(NT):
        i = t % 2
        mk.dma("sp", xt[i][:], x[t * 128:(t + 1) * 128, :], writes=[b_x[i]])
        ln_stats(nc, mk, xt[i], b_x[i], st, mv, rs, nmr, b_s)
        mk.op("act", lambda: nc.scalar.activation(out=xn[:], in_=xt[i][:], func=AF.Identity, bias=nmr[:, 0:1], scale=rs[:, 0:1]),
              reads=[b_x[i], b_s], writes=[b_xn])
        mk.op("dve", lambda: nc.vector.tensor_tensor(out=h1[:], in0=xn[:], in1=sct[:], op=ALU.mult), reads=[b_xn, b_sc], writes=[b_h1])
        mk.op("pool", lambda: nc.gpsimd.tensor_tensor(out=hb[i][:], in0=h1[:], in1=sht[:], op=ALU.add), reads=[b_h1, b_sh], writes=[b_hb[i]])
        for half in range(2):
            for kk in range(8):
                k = half * 8 + kk
                mk.op("pe", lambda: nc.tensor.transpose(PT[half][:, kk, :], hb[i][:, k * 128:(k + 1) * 128], idb[:]),
                      reads=[b_hb[i], b_id], writes=[b_pt[half]], skip_same=True)
            eng = "act" if half == 0 else "dve"
            if eng == "act":
                mk.op("act", lambda: nc.scalar.copy(out=hT[:, half * 8:(half + 1) * 8, t * 128:(t + 1) * 128], in_=PT[half][:]),
                      reads=[b_pt[half]], writes=[b_hT])
            else:
                mk.op("dve", lambda: nc.vector.tensor_copy(out=hT[:, half * 8:(half + 1) * 8, t * 128:(t + 1) * 128], in_=PT[half][:]),
                      reads=[b_pt[half]], writes=[b_hT])
    pi = 0
    for cb in range(NCB):
        j = cb % 2
        c0 = cb * 128
        cw = min(128, NIN - c0)
        for tc in range(TOK // 512):
            q = pi % 4
            pi += 1
            for k in range(16):
                mk.op("pe", lambda: nc.tensor.matmul(PM[q][0:cw, :], lhsT=wt[j][:, k, 0:cw], rhs=hT[:, k, tc * 512:(tc + 1) * 512],
                                                     start=(k == 0), stop=(k == 15)),
                      reads=[b_w[j], b_hT], writes=[b_pm[q]], skip_same=True)
            if tc % 2 == 0:
                mk.op("act", lambda: nc.scalar.copy(out=ot[j][0:cw, tc * 512:(tc + 1) * 512], in_=PM[q][0:cw, :]), reads=[b_pm[q]], writes=[b_o[j]])
            else:
                mk.op("dve", lambda: nc.vector.tensor_copy(out=ot[j][0:cw, tc * 512:(tc + 1) * 512], in_=PM[q][0:cw, :]), reads=[b_pm[q]], writes=[b_o[j]])
        mk.dma("sp", pT[c0:c0 + cw, :], ot[j][0:cw, :], reads=[b_o[j]], is_output=True)
        if cb + 2 < NCB:
            load_w(cb + 2)


import numpy as np
from contextlib import ExitStack

C = 64
TB = 512
NCH = TB // C


def rwkv_consts():
    s = np.arange(64)[:, None]
    t = np.arange(64)[None, :]
    m_su = (s < t).astype(np.float32)
    m_ui = (s <= t).astype(np.float32)
    m1 = np.concatenate([m_su, m_ui], axis=1)
    mask1 = np.tile(m1, (1, 4))
    m_sl = (t < s).astype(np.float32)
    mask3 = np.tile(m_sl, (1, 4))
    seg = np.ones((128, TB), np.float32)
    seg[:, ::C] = 0.0
    return {"mask1": mask1, "mask3": mask3, "seg": seg, "ident": np.eye(128, dtype=np.float32)}


def emit_rwkv(nc, mk, T, rwin, par64, par128, w2, a2, g2, gnt, cst, yT, odt=F32):
    import os
    LVL = int(os.environ.get("RW_LVL", "9"))
    NB = T // TB
    V = lambda fn, r=(), w=(): mk.op("dve", fn, r, w)
    A = lambda fn, r=(), w=(): mk.op("act", fn, r, w)
    G = lambda fn, r=(), w=(): mk.op("pool", fn, r, w)
    PE = lambda fn, r=(), w=(), ss=True: mk.op("pe", fn, r, w, skip_same=ss)
    import os
    F32R = mybir.dt.float32r
    USE_R = os.environ.get("RW_F32R", "1") == "1"
    RR = (lambda a: a.bitcast(F32R)) if USE_R else (lambda a: a)
    sb = mk.sb
    p64 = sb("rw_p64", [64, 2, 11]); p128 = sb("rw_p128", [128, 3])
    w2t = sb("rw_w2", [96, 128]); a2t = sb("rw_a2", [96, 128]); g2t = sb("rw_g2", [128, 128])
    gn = sb("rw_gn", [64, 2, 2, 64])
    mask1 = sb("rw_mask1", [64, 512]); mask3 = sb("rw_mask3", [64, 256]); seg = sb("rw_seg", [128, TB])
    ident = sb("rw_ident", [128, 128])
    ones64 = sb("rw_ones", [64, 64])
    bc = Buf("const")
    for dst, src in ((p64, par64), (p128, par128), (w2t, w2), (a2t, a2), (g2t, g2), (gn, gnt),
                     (mask1, cst["mask1"]), (mask3, cst["mask3"]), (seg, cst["seg"]), (ident, cst["ident"])):
        mk.dma("sp", dst[:], src, writes=[bc])
    V(lambda: nc.vector.memset(ones64[:], 1.0), w=[bc])
    raw = {}
    for nm in ("r0", "k0", "v0", "r1", "k1", "v1"):
        raw[nm] = sb("rw_raw_" + nm, [64, TB + 1])
    raw["w"] = sb("rw_raw_w", [96, TB + 1]); raw["a"] = sb("rw_raw_a", [96, TB + 1]); raw["g"] = sb("rw_raw_g", [128, TB + 1])
    b_raw = Buf("raw")
    tmp = sb("rw_tmp", [128, TB]); b_tmp = Buf("tmp")
    ws = sb("rw_ws", [96, TB]); as_ = sb("rw_as", [96, TB]); gs = sb("rw_gs", [128, TB]); b_lo = Buf("lo")
    gate = [sb("rw_gate%d" % i_, [128, TB]) for i_ in range(2)]; b_gate = [Buf("gate0"), Buf("gate1")]
    H = []
    for h in range(2):
        d = {}
        for nm in ("rs", "ks", "vs", "lw", "asg", "kkn", "kp", "bv", "cum", "e1", "e2", "BT", "KT", "BH", "KH", "rkr"):
            d[nm] = sb("rw_%s%d" % (nm, h), [64, TB])
        d["AR"] = sb("rw_AR%d" % h, [64, NCH, 128])
        d["cC"] = sb("rw_cC%d" % h, [64, NCH]); d["gC"] = sb("rw_gC%d" % h, [64, NCH])
        d["b"] = Buf("H%d" % h)
        d["bo"] = [Buf("Ho%d_0" % h), Buf("Ho%d_1" % h)]
        d["Vt"] = sb("rw_Vt%d" % h, [64, NCH, 64]); d["BHt"] = sb("rw_BHt%d" % h, [64, NCH, 64]); d["KHt"] = sb("rw_KHt%d" % h, [64, NCH, 64])
        d["bt"] = [Buf("Ht%d_0" % h), Buf("Ht%d_1" % h)]
        for nm_, shp_ in (("AR", [64, NCH, 128]), ("BT", [64, TB]), ("KT", [64, TB]), ("gC", [64, NCH]), ("Vt", [64, NCH, 64]), ("BHt", [64, NCH, 64]), ("KHt", [64, NCH, 64])):
            d[nm_] = [d[nm_], sb("rw_%s%d_b" % (nm_, h), shp_)]
        d["NG"] = sb("rw_NG%d" % h, [64, NCH, 128]); d["LG"] = sb("rw_LG%d" % h, [64, NCH, 128])
        d["L"] = sb("rw_L%d" % h, [64, NCH, 64]); d["bA"] = Buf("A%d" % h)
        d["P"] = [sb("rw_P%d_%d" % (h, i), [64, NCH, 64]) for i in range(2)]
        d["PT"] = [sb("rw_PT%d_%d" % (h, i), [64, NCH, 64]) for i in range(2)]
        d["ST"] = [sb("rw_ST%d_%d" % (h, i), [64, NCH, 64]) for i in range(2)]
        d["bD"] = Buf("D%d" % h)
        d["M"] = sb("rw_M%d" % h, [64, 64]); d["bM"] = Buf("M%d" % h)
        d["X1"] = sb("rw_X1%d" % h, [64, 64]); d["U"] = sb("rw_U%d" % h, [64, 64]); d["bX"] = Buf("X%d" % h); d["bU"] = Buf("U%d" % h)
        H.append(d)
    Yb = sb("rw_Yb", [64, NCH, 2, 64]); b_Y = Buf("Y")
    Ysq = sb("rw_Ysq", [64, NCH, 2, 64])
    st1 = sb("rw_st1", [64, NCH * 2]); st2 = sb("rw_st2", [64, NCH * 2]); st3 = sb("rw_st3", [64, NCH * 2]); b_st = Buf("st")
    sbon = [sb("rw_sbon%d" % i_, [64, NCH, 2]) for i_ in range(2)]; b_sb = [Buf("sbon0"), Buf("sbon1")]
    yo = sb("rw_yo", [128, TB], odt); b_yo = Buf("yo")
    ps_lo = mk.ps("rw_ps_lo", [128, 512]); b_pl = Buf()
    ps_tr = mk.ps("rw_ps_tr", [128, 512]); b_ptr = Buf()
    ps_a1 = mk.ps("rw_ps_a1", [64, 512]); b_pa1 = Buf()
    ps_a2 = mk.ps("rw_ps_a2", [64, 512]); b_pa2 = Buf()
    ps_a3f = mk.ps("rw_ps_a3", [128, 512]); ps_a3 = ps_a3f[0:64, :]; b_pa3 = Buf()
    ps_d = mk.ps("rw_ps_d", [64, 512]); b_pd = Buf()
    ps_d2 = ps_a3; b_pd2 = b_pa3
    ps_sh = [mk.ps("rw_ps_s%d" % h, [64, 512]) for h in range(2)]
    b_psh = [Buf(), Buf()]

    for h in range(2):
        V(lambda: nc.vector.tensor_scalar(out=RR(H[h]["M"][:]), in0=ident[0:64, 0:64], scalar1=0.0, scalar2=None, op0=ALU.mult), r=[bc], w=[H[h]["bM"]])

    rows = {"r0": 0, "r1": 64, "k0": 128, "k1": 192, "v0": 256, "v1": 320, "w": 384, "a": 480, "g": 576}
    nrow = {"r0": 64, "r1": 64, "k0": 64, "k1": 64, "v0": 64, "v1": 64, "w": 96, "a": 96, "g": 128}

    def stage1(blk):
        t0 = blk * TB
        par = blk % 2
        for nm in rows:
            r0, n = rows[nm], nrow[nm]
            if blk == 0:
                V(lambda: nc.vector.memset(raw[nm][0:n, 0:1], 0.0), w=[b_raw])
                yield
                mk.dma("sp", raw[nm][0:n, 1:TB + 1], rwin[r0:r0 + n, 0:TB], writes=[b_raw])
                yield
            else:
                mk.dma("sp", raw[nm][0:n, :], rwin[r0:r0 + n, t0 - 1:t0 + TB], writes=[b_raw])
                yield

        def shift(dst, src, n, mu_ap, bdst):
            V(lambda: nc.vector.tensor_tensor(out=tmp[0:n, :], in0=src[0:n, 0:TB], in1=src[0:n, 1:TB + 1], op=ALU.subtract), r=[b_raw], w=[b_tmp])
            V(lambda: nc.vector.scalar_tensor_tensor(out=dst[0:n, :], in0=tmp[0:n, :], scalar=mu_ap, in1=src[0:n, 1:TB + 1], op0=ALU.mult, op1=ALU.add),
              r=[b_tmp, b_raw, bc], w=[bdst])

        shift(ws, raw["w"], 96, p128[0:96, 0:1], b_lo)
        yield
        shift(as_, raw["a"], 96, p128[0:96, 1:2], b_lo)
        yield
        shift(gs, raw["g"], 128, p128[:, 2:3], b_lo)
        yield
        A(lambda: nc.scalar.activation(out=ws[:], in_=ws[:], func=AF.Tanh), r=[b_lo], w=[b_lo])
        yield
        A(lambda: nc.scalar.activation(out=gs[:], in_=gs[:], func=AF.Sigmoid), r=[b_lo], w=[b_lo])
        yield
        PE(lambda: nc.tensor.matmul(ps_lo[:, :], lhsT=g2t[:, :], rhs=gs[:, :], start=True, stop=True), r=[bc, b_lo], w=[b_pl])
        yield
        A(lambda: nc.scalar.copy(out=gate[par][:], in_=ps_lo[:, :]), r=[b_pl], w=[b_gate[par]])
        yield
        for h in range(2):
            d = H[h]; b = d["b"]; bo = d["bo"][par]
            hs = slice(64 * h, 64 * h + 64)
            shift(d["rs"], raw["r%d" % h], 64, p64[:, h, 0:1], b)
            yield
            shift(d["ks"], raw["k%d" % h], 64, p64[:, h, 1:2], b)
            yield
            shift(d["vs"], raw["v%d" % h], 64, p64[:, h, 2:3], b)
            yield
            PE(lambda: nc.tensor.matmul(ps_lo[0:64, :], lhsT=w2t[:, hs], rhs=ws[:, :], start=True, stop=True), r=[bc, b_lo], w=[b_pl])
            yield
            A(lambda: nc.scalar.activation(out=d["lw"][:], in_=ps_lo[0:64, :], func=AF.Sigmoid, bias=p64[:, h, 3:4], scale=1.0), r=[b_pl, bc], w=[b])
            yield
            V(lambda: nc.vector.tensor_scalar(out=d["lw"][:], in0=d["lw"][:], scalar1=-0.6065306597126334, scalar2=None, op0=ALU.mult), r=[b], w=[b])
            yield
            PE(lambda: nc.tensor.matmul(ps_lo[0:64, :], lhsT=a2t[:, hs], rhs=as_[:, :], start=True, stop=True), r=[bc, b_lo], w=[b_pl])
            yield
            A(lambda: nc.scalar.activation(out=d["asg"][:], in_=ps_lo[0:64, :], func=AF.Sigmoid, bias=p64[:, h, 4:5], scale=1.0), r=[b_pl, bc], w=[b])
            yield
            V(lambda: nc.vector.tensor_scalar(out=d["kkn"][:], in0=d["ks"][:], scalar1=p64[:, h, 5:6], scalar2=None, op0=ALU.mult), r=[b, bc], w=[b])
            yield
            A(lambda: nc.scalar.activation(out=tmp[0:64, :], in_=d["kkn"][:], func=AF.Square), r=[b], w=[b_tmp])
            yield
            PE(lambda: nc.tensor.matmul(ps_lo[0:64, :], lhsT=ones64[:, :], rhs=tmp[0:64, :], start=True, stop=True), r=[bc, b_tmp], w=[b_pl])
            yield
            A(lambda: nc.scalar.activation(out=tmp[0:64, :], in_=ps_lo[0:64, :], func=AF.Sqrt), r=[b_pl], w=[b_tmp])
            yield
            V(lambda: nc.vector.tensor_scalar(out=tmp[0:64, :], in0=tmp[0:64, :], scalar1=1e-12, scalar2=None, op0=ALU.max), r=[b_tmp], w=[b_tmp])
            yield
            V(lambda: nc.vector.reciprocal(out=tmp[0:64, :], in_=tmp[0:64, :]), r=[b_tmp], w=[b_tmp])
            yield
            V(lambda: nc.vector.tensor_tensor(out=d["kkn"][:], in0=d["kkn"][:], in1=tmp[0:64, :], op=ALU.mult), r=[b, b_tmp], w=[b])
            yield
            V(lambda: nc.vector.tensor_scalar(out=tmp[0:64, :], in0=d["asg"][:], scalar1=-1.0, scalar2=p64[:, h, 6:7], op0=ALU.add, op1=ALU.mult), r=[b, bc], w=[b_tmp])
            yield
            V(lambda: nc.vector.scalar_tensor_tensor(out=d["kp"][:], in0=tmp[0:64, :], scalar=1.0, in1=d["ks"][:], op0=ALU.add, op1=ALU.mult), r=[b_tmp, b], w=[b])
            yield
            V(lambda: nc.vector.tensor_tensor(out=d["bv"][:], in0=d["kkn"][:], in1=d["asg"][:], op=ALU.mult), r=[b], w=[b])
            yield
            V(lambda: nc.vector.scalar_tensor_tensor(out=d["rkr"][:], in0=d["rs"][:], scalar=p64[:, h, 7:8], in1=d["kp"][:], op0=ALU.mult, op1=ALU.mult), r=[b, bc], w=[b])
            yield
            V(lambda: nc.vector.tensor_tensor_scan(out=d["cum"][:], data0=seg[0:64, :], data1=d["lw"][:], initial=0.0, op0=ALU.mult, op1=ALU.add), r=[b, bc], w=[b])
            yield
            cum3 = d["cum"][:].rearrange("p (c t) -> p c t", t=C)
            V(lambda: nc.vector.tensor_copy(out=d["cC"][:], in_=cum3[:, :, C - 1]), r=[b], w=[b])
            yield
            A(lambda: nc.scalar.activation(out=d["gC"][par][:], in_=d["cC"][:], func=AF.Exp), r=[b], w=[bo])
            yield
            A(lambda: nc.scalar.activation(out=d["e1"][:], in_=d["cum"][:], func=AF.Exp), r=[b], w=[b])
            yield
            A(lambda: nc.scalar.activation(out=d["e2"][:], in_=d["cum"][:], func=AF.Exp, scale=-1.0), r=[b], w=[b])
            yield
            AR = d["AR"][par]
            V(lambda: nc.vector.tensor_tensor(out=RR(AR[:, :, 64:128]), in0=d["rs"][:].rearrange("p (c t) -> p c t", t=C),
                                              in1=d["e1"][:].rearrange("p (c t) -> p c t", t=C), op=ALU.mult), r=[b], w=[bo])
            yield
            V(lambda: nc.vector.tensor_tensor(out=RR(d["BT"][par][:]), in0=d["bv"][:], in1=d["e2"][:], op=ALU.mult), r=[b], w=[bo])
            yield
            V(lambda: nc.vector.tensor_tensor(out=RR(d["KT"][par][:]), in0=d["kp"][:], in1=d["e2"][:], op=ALU.mult), r=[b], w=[bo])
            yield
            V(lambda: nc.vector.tensor_tensor(out=tmp[0:64, :], in0=d["cum"][:], in1=d["lw"][:], op=ALU.subtract), r=[b], w=[b_tmp])
            yield
            A(lambda: nc.scalar.activation(out=tmp[0:64, :], in_=tmp[0:64, :], func=AF.Exp), r=[b_tmp], w=[b_tmp])
            yield
            V(lambda: nc.vector.scalar_tensor_tensor(out=RR(AR[:, :, 0:64]), in0=d["kkn"][:].rearrange("p (c t) -> p c t", t=C), scalar=-1.0,
                                                     in1=tmp[0:64, :].rearrange("p (c t) -> p c t", t=C), op0=ALU.mult, op1=ALU.mult), r=[b, b_tmp], w=[bo])
            yield
            V(lambda: nc.vector.tensor_tensor(out=tmp[0:64, :].rearrange("p (c t) -> p c t", t=C), in0=d["cC"][:].unsqueeze(2).to_broadcast([64, NCH, C]),
                                              in1=cum3, op=ALU.subtract), r=[b], w=[b_tmp])
            yield
            A(lambda: nc.scalar.activation(out=tmp[0:64, :], in_=tmp[0:64, :], func=AF.Exp), r=[b_tmp], w=[b_tmp])
            yield
            V(lambda: nc.vector.tensor_tensor(out=d["BH"][:], in0=d["bv"][:], in1=tmp[0:64, :], op=ALU.mult), r=[b, b_tmp], w=[b])
            yield
            V(lambda: nc.vector.tensor_tensor(out=d["KH"][:], in0=d["kp"][:], in1=tmp[0:64, :], op=ALU.mult), r=[b, b_tmp], w=[b])
            yield
            for src, dstn in (("vs", "Vt"), ("BH", "BHt"), ("KH", "KHt")):
                for c in range(NCH):
                    PE(lambda: nc.tensor.transpose(ps_tr[0:64, c * 64:(c + 1) * 64], d[src][:, c * C:(c + 1) * C], ident[0:64, 0:64]), r=[b, bc], w=[b_ptr])
                    yield
                A(lambda: nc.scalar.copy(out=RR(d[dstn][par][:].rearrange("p c k -> p (c k)")), in_=ps_tr[0:64, :]), r=[b_ptr], w=[d["bt"][par]])
                yield
            for c in range(NCH):
                PE(lambda: nc.tensor.matmul(ps_tr[0:64, 2 * c:2 * c + 2], lhsT=d["rkr"][:, c * C:(c + 1) * C], rhs=ones64[:, 0:2], start=True, stop=True), r=[b, bc], w=[b_ptr])
                yield
            V(lambda: nc.vector.tensor_copy(out=sbon[par][:, :, h], in_=ps_tr[0:64, 0:2 * NCH:2]), r=[b_ptr], w=[b_sb[par]])
            yield

    def rest(blk, tick):
        t0 = blk * TB
        par = blk % 2
        for h in range(2):
            d = H[h]; b = d["bo"][par]; AR = d["AR"][par]
            tick()
            for half in range(2):
                for cc in range(4):
                    c = half * 4 + cc
                    PE(lambda: nc.tensor.matmul(ps_a1[:, cc * 128:(cc + 1) * 128], lhsT=RR(d["BT"][par][:, c * C:(c + 1) * C]), rhs=RR(AR[:, c, :]), start=True, stop=True), r=[b], w=[b_pa1])
                    PE(lambda: nc.tensor.matmul(ps_a2[:, cc * 128:(cc + 1) * 128], lhsT=RR(d["KT"][par][:, c * C:(c + 1) * C]), rhs=RR(AR[:, c, :]), start=True, stop=True), r=[b], w=[b_pa2])
                    PE(lambda: nc.tensor.matmul(ps_a3[:, cc * 64:(cc + 1) * 64], lhsT=RR(AR[:, c, 0:64]), rhs=RR(d["BT"][par][:, c * C:(c + 1) * C]), start=True, stop=True), r=[b], w=[b_pa3])
                V(lambda: nc.vector.tensor_tensor(out=RR(d["NG"][:, half * 4:half * 4 + 4, :].rearrange("p c k -> p (c k)")), in0=ps_a1[:, :], in1=mask1[:, :], op=ALU.mult), r=[b_pa1, bc], w=[d["bA"]])
                V(lambda: nc.vector.tensor_tensor(out=RR(d["LG"][:, half * 4:half * 4 + 4, :].rearrange("p c k -> p (c k)")), in0=ps_a2[:, :], in1=mask1[:, :], op=ALU.mult), r=[b_pa2, bc], w=[d["bA"]])
                V(lambda: nc.vector.tensor_tensor(out=d["L"][:, half * 4:half * 4 + 4, :].rearrange("p c k -> p (c k)"), in0=ps_a3[:, 0:256], in1=mask3[:, :], op=ALU.mult), r=[b_pa3, bc], w=[d["bA"]])
        DPS = [(ps_d, b_pd, ps_d2, b_pd2), (ps_a1, b_pa1, ps_a2, b_pa2)]
        for h in range(2):
            d = H[h]; P, PT, ST = d["P"], d["PT"], d["ST"]; bD = d["bD"]
            V(lambda: nc.vector.tensor_copy(out=RR(P[0][:]), in_=d["L"][:]), r=[d["bA"]], w=[bD])
            V(lambda: nc.vector.tensor_copy(out=RR(PT[0][:]), in_=d["NG"][:, :, 0:64]), r=[d["bA"]], w=[bD])
            V(lambda: nc.vector.tensor_tensor(out=RR(ST[0][:]), in0=d["NG"][:, :, 0:64], in1=ident[0:64, 0:64].unsqueeze(1).to_broadcast([64, NCH, 64]), op=ALU.add), r=[d["bA"], bc], w=[bD])
        cur = 0
        for lev in range(5):
            nxt = 1 - cur
            tick()
            for h in range(2):
                d = H[h]; P, PT, ST = d["P"], d["PT"], d["ST"]; bD = d["bD"]
                pd, bpd, pd2, bpd2 = DPS[h]
                for c in range(NCH):
                    PE(lambda: nc.tensor.matmul(pd[:, c * 64:(c + 1) * 64], lhsT=RR(PT[cur][:, c, :]), rhs=RR(P[cur][:, c, :]), start=True, stop=True), r=[bD], w=[bpd])
                for c in range(NCH):
                    PE(lambda: nc.tensor.matmul(pd2[:, c * 64:(c + 1) * 64], lhsT=RR(P[cur][:, c, :]), rhs=RR(PT[cur][:, c, :]), start=True, stop=True), r=[bD], w=[bpd2])
            tick()
            for h in range(2):
                d = H[h]; P, PT, ST = d["P"], d["PT"], d["ST"]; bD = d["bD"]
                pd, bpd, pd2, bpd2 = DPS[h]
                V(lambda: nc.vector.tensor_copy(out=RR(P[nxt][:].rearrange("p c k -> p (c k)")), in_=pd[:, :]), r=[], w=[bpd, bD])
                A(lambda: nc.scalar.copy(out=RR(PT[nxt][:].rearrange("p c k -> p (c k)")), in_=pd2[:, :]), r=[], w=[bpd2, bD])
            tick()
            for h in range(2):
                d = H[h]; P, PT, ST = d["P"], d["PT"], d["ST"]; bD = d["bD"]
                pd, bpd, pd2, bpd2 = DPS[h]
                for c in range(NCH):
                    PE(lambda: nc.tensor.matmul(pd[:, c * 64:(c + 1) * 64], lhsT=RR(P[nxt][:, c, :]), rhs=RR(ST[cur][:, c, :]), start=True, stop=True), r=[bD], w=[bpd])
            tick()
            for h in range(2):
                d = H[h]; P, PT, ST = d["P"], d["PT"], d["ST"]; bD = d["bD"]
                pd, bpd, pd2, bpd2 = DPS[h]
                V(lambda: nc.vector.tensor_tensor(out=RR(ST[nxt][:].rearrange("p c k -> p (c k)")), in0=pd[:, :], in1=ST[cur][:].rearrange("p c k -> p (c k)"), op=ALU.add), r=[bD], w=[bpd, bD])
            tick()
            cur = nxt
        for h in range(2):
            H[h]["STf"] = H[h]["ST"][cur]
        for c in range(NCH):
            pp = lambda h, i: ps_sh[h][:, i * 64:(i + 1) * 64]
            tick()
            for h in range(2):
                d = H[h]
                PE(lambda: nc.tensor.matmul(pp(h, 0), lhsT=RR(d["LG"][:, c, 0:64]), rhs=RR(d["Vt"][par][:, c, :]), start=True, stop=False), r=[d["bA"], d["bt"][par]], w=[b_psh[h]])
                PE(lambda: nc.tensor.matmul(pp(h, 0), lhsT=RR(d["AR"][par][:, c, 0:64]), rhs=RR(d["M"][:, :]), start=False, stop=True), r=[d["bo"][par], d["bM"]], w=[b_psh[h]])
            tick()
            for h in range(2):
                d = H[h]
                if h == 0:
                    A(lambda: nc.scalar.copy(out=RR(d["X1"][:]), in_=pp(h, 0)), r=[], w=[b_psh[h], d["bX"]])
                else:
                    V(lambda: nc.vector.tensor_copy(out=RR(d["X1"][:]), in_=pp(h, 0)), r=[], w=[b_psh[h], d["bX"]])
            tick()
            for h in range(2):
                d = H[h]
                PE(lambda: nc.tensor.matmul(pp(h, 1), lhsT=RR(d["STf"][:, c, :]), rhs=RR(d["X1"][:, :]), start=True, stop=True), r=[d["bD"], d["bX"]], w=[b_psh[h]])
            tick()
            for h in range(2):
                d = H[h]
                if h == 0:
                    V(lambda: nc.vector.tensor_copy(out=RR(d["U"][:]), in_=pp(h, 1)), r=[], w=[b_psh[h], d["bU"]])
                else:
                    A(lambda: nc.scalar.copy(out=RR(d["U"][:]), in_=pp(h, 1)), r=[], w=[b_psh[h], d["bU"]])
            tick()
            for h in range(2):
                d = H[h]
                PE(lambda: nc.tensor.matmul(pp(h, 2), lhsT=RR(d["AR"][par][:, c, 64:128]), rhs=RR(d["M"][:, :]), start=True, stop=False), r=[d["bo"][par], d["bM"]], w=[b_psh[h]])
                PE(lambda: nc.tensor.matmul(pp(h, 2), lhsT=RR(d["LG"][:, c, 64:128]), rhs=RR(d["Vt"][par][:, c, :]), start=False, stop=False), r=[d["bA"], d["bt"][par]], w=[b_psh[h]])
                PE(lambda: nc.tensor.matmul(pp(h, 2), lhsT=RR(d["NG"][:, c, 64:128]), rhs=RR(d["U"][:, :]), start=False, stop=True), r=[d["bA"], d["bU"]], w=[b_psh[h]])
                PE(lambda: nc.tensor.matmul(pp(h, 3), lhsT=RR(d["KHt"][par][:, c, :]), rhs=RR(d["Vt"][par][:, c, :]), start=True, stop=False), r=[d["bt"][par]], w=[b_psh[h]])
                PE(lambda: nc.tensor.matmul(pp(h, 3), lhsT=RR(d["BHt"][par][:, c, :]), rhs=RR(d["U"][:, :]), start=False, stop=True), r=[d["bt"][par], d["bU"]], w=[b_psh[h]])
            tick()
            for h in range(2):
                d = H[h]
                V(lambda: nc.vector.scalar_tensor_tensor(out=RR(d["M"][:]), in0=d["M"][:], scalar=d["gC"][par][:, c:c + 1], in1=pp(h, 3), op0=ALU.mult, op1=ALU.add), r=[d["bo"][par], d["bM"]], w=[b_psh[h], d["bM"]])
                A(lambda: nc.scalar.copy(out=Yb[:, c, h, :], in_=pp(h, 2)), r=[], w=[b_psh[h], b_Y])
        Y2 = Yb[:].rearrange("p c h v -> p (c h) v")
        V(lambda: nc.vector.tensor_reduce(out=st1[:], in_=Y2, axis=AX.X, op=ALU.add), r=[b_Y], w=[b_st])
        A(lambda: nc.scalar.activation(out=Ysq[:].rearrange("p c h v -> p (c h v)"), in_=Yb[:].rearrange("p c h v -> p (c h v)"), func=AF.Square), r=[b_Y], w=[b_tmp])
        V(lambda: nc.vector.tensor_reduce(out=st2[:], in_=Ysq[:].rearrange("p c h v -> p (c h) v"), axis=AX.X, op=ALU.add), r=[b_tmp], w=[b_st])
        V(lambda: nc.vector.tensor_scalar(out=st1[:], in0=st1[:], scalar1=1.0 / 64, scalar2=None, op0=ALU.mult), r=[b_st], w=[b_st])
        V(lambda: nc.vector.tensor_tensor(out=st3[:], in0=st1[:], in1=st1[:], op=ALU.mult), r=[b_st], w=[b_st])
        V(lambda: nc.vector.scalar_tensor_tensor(out=st2[:], in0=st2[:], scalar=1.0 / 64, in1=st3[:], op0=ALU.mult, op1=ALU.subtract), r=[b_st], w=[b_st])
        A(lambda: nc.scalar.activation(out=st2[:], in_=st2[:], func=AF.Sqrt, bias=64e-5, scale=1.0), r=[b_st], w=[b_st])
        V(lambda: nc.vector.reciprocal(out=st2[:], in_=st2[:]), r=[b_st], w=[b_st])
        V(lambda: nc.vector.tensor_tensor(out=Y2, in0=Y2, in1=st1[:].unsqueeze(2).to_broadcast([64, NCH * 2, 64]), op=ALU.subtract), r=[b_st, b_Y], w=[b_Y])
        V(lambda: nc.vector.tensor_tensor(out=Y2, in0=Y2, in1=st2[:].unsqueeze(2).to_broadcast([64, NCH * 2, 64]), op=ALU.mult), r=[b_st, b_Y], w=[b_Y])
        for h in range(2):
            V(lambda: nc.vector.tensor_tensor(out=Yb[:, :, h, :], in0=Yb[:, :, h, :], in1=gn[:, 0, h, :].unsqueeze(1).to_broadcast([64, NCH, 64]), op=ALU.mult), r=[b_Y, bc], w=[b_Y])
            V(lambda: nc.vector.tensor_tensor(out=Yb[:, :, h, :], in0=Yb[:, :, h, :], in1=gn[:, 1, h, :].unsqueeze(1).to_broadcast([64, NCH, 64]), op=ALU.add), r=[b_Y, bc], w=[b_Y])
            V(lambda: nc.vector.tensor_tensor(out=Ysq[:, :, h, :], in0=H[h]["Vt"][par][:], in1=sbon[par][:, :, h].unsqueeze(2).to_broadcast([64, NCH, 64]), op=ALU.mult), r=[H[h]["bt"][par], b_sb[par]], w=[b_tmp])
        V(lambda: nc.vector.tensor_tensor(out=Yb[:].rearrange("p c h v -> p (c h v)"), in0=Yb[:].rearrange("p c h v -> p (c h v)"), in1=Ysq[:].rearrange("p c h v -> p (c h v)"), op=ALU.add), r=[b_Y, b_tmp], w=[b_Y])
        for c in range(NCH):
            PE(lambda: nc.tensor.transpose(ps_a3f[:, c * 64:(c + 1) * 64], Yb[:, c, :, :].rearrange("p h v -> p (h v)"), ident[0:64, 0:64]), r=[b_Y, bc], w=[b_pa3])
        V(lambda: nc.vector.tensor_tensor(out=yo[:], in0=ps_a3f[:, :], in1=gate[par][:], op=ALU.mult), r=[b_gate[par]], w=[b_pa3, b_yo])
        mk.dma("sp", yT[:, t0:t0 + TB], yo[:], reads=[b_yo], is_output=True)

    for _ in stage1(0):
        pass
    for blk in range(NB):
        nx = stage1(blk + 1) if blk + 1 < NB else None

        def tick(k=2):
            if nx is not None:
                for _ in range(k):
                    if next(nx, "END") == "END":
                        break
        rest(blk, tick)
        if nx is not None:
            for _ in nx:
                pass


def build_rwkv(T):
    nc = bass.Bass("TRN2", target_bir_lowering=False)
    dt = lambda n, s, k="ExternalInput": nc.dram_tensor(n, s, F32, kind=k).ap()
    rwin = dt("rwin", [704, T]); par64 = dt("par64", [64, 2, 11]); par128 = dt("par128", [128, 3])
    w2 = dt("w2", [96, 128]); a2 = dt("a2", [96, 128]); g2 = dt("g2", [128, 128]); gnt = dt("gnt", [64, 2, 2, 64])
    cst = {"mask1": dt("mask1", [64, 512]), "mask3": dt("mask3", [64, 256]), "seg": dt("seg", [128, TB]), "ident": dt("ident", [128, 128])}
    yT = dt("yT", [128, T], "ExternalOutput")
    with ExitStack() as ctx:
        mk = MK(nc, ctx)
        emit_rwkv(nc, mk, T, rwin, par64, par128, w2, a2, g2, gnt, cst, yT)
        mk.finish("sp")
        print("rwkv ops", mk.nops, "waits", mk.nwaits)
    return nc


def rwkv_host_inputs(prm, l, q):
    G = 512
    cs = slice(128 * q, 128 * q + 128)
    mu = prm["rwkv_mu"][l]
    par64 = np.zeros((64, 2, 11), np.float32)
    for h in range(2):
        c0 = 128 * q + 64 * h
        par64[:, h, 0] = mu[0 * G + c0:0 * G + c0 + 64]
        par64[:, h, 1] = mu[1 * G + c0:1 * G + c0 + 64]
        par64[:, h, 2] = mu[2 * G + c0:2 * G + c0 + 64]
        par64[:, h, 3] = prm["rwkv_w0"][l][c0:c0 + 64]
        par64[:, h, 4] = prm["rwkv_a0"][l][c0:c0 + 64]
        par64[:, h, 5] = prm["rwkv_kk"][l][c0:c0 + 64]
        par64[:, h, 6] = prm["rwkv_ka"][l][c0:c0 + 64]
        par64[:, h, 7] = prm["rwkv_rk"][l][2 * q + h]
    par128 = np.zeros((128, 3), np.float32)
    par128[0:96, 0] = mu[3 * G:3 * G + 96]
    par128[0:96, 1] = mu[3 * G + 96:3 * G + 192]
    par128[:, 2] = mu[3 * G + 192:3 * G + 320]
    gnt = np.zeros((64, 2, 2, 64), np.float32)
    for h in range(2):
        c0 = 128 * q + 64 * h
        gnt[:, 0, h, :] = prm["rwkv_gn_g"][l][c0:c0 + 64][None]
        gnt[:, 1, h, :] = prm["rwkv_gn_b"][l][c0:c0 + 64][None]
    d = {"par64": par64, "par128": par128, "gnt": gnt,
         "w2": np.ascontiguousarray(prm["rwkv_w2"][l][:, cs]), "a2": np.ascontiguousarray(prm["rwkv_a2"][l][:, cs]),
         "g2": np.ascontiguousarray(prm["rwkv_g2"][l][:, cs])}
    d.update(rwkv_consts())
    return d


def rwkv_rows(q):
    G = 512
    idx = []
    for base in (0, G, 2 * G):
        idx += list(range(base + 128 * q, base + 128 * q + 64))
        idx += list(range(base + 128 * q + 64, base + 128 * q + 128))
    idx += list(range(3 * G, 3 * G + 320))
    return np.array(idx)


import math
import numpy as np
from contextlib import ExitStack


def emit_conv(nc, mk, T, cvin, cw, yT, TB=2048, odt=F32):
    V = lambda fn, r=(), w=(): mk.op("dve", fn, r, w)
    G = lambda fn, r=(), w=(): mk.op("pool", fn, r, w)
    cwt = mk.sb("cv_w", [128, 3]); bc = Buf()
    mk.dma("sp", cwt[:], cw, writes=[bc])
    Bt = mk.sb("cv_B", [128, TB]); Ct = mk.sb("cv_C", [128, TB + 2]); Ht = mk.sb("cv_H", [128, TB + 2])
    z = mk.sb("cv_z", [128, TB + 2]); y = mk.sb("cv_y", [128, TB]); o = mk.sb("cv_o", [128, TB], odt)
    b_in, b_z, b_y, b_o = Buf(), Buf(), Buf(), Buf()
    for blk in range(T // TB):
        t0 = blk * TB
        mk.dma("sp", Bt[:], cvin[0:128, t0:t0 + TB], writes=[b_in])
        if blk == 0:
            V(lambda: nc.vector.memset(Ct[:, 0:2], 0.0), w=[b_in])
            V(lambda: nc.vector.memset(Ht[:, 0:2], 0.0), w=[b_in])
            mk.dma("sp", Ct[:, 2:], cvin[128:256, 0:TB], writes=[b_in])
            mk.dma("sp", Ht[:, 2:], cvin[256:384, 0:TB], writes=[b_in])
        else:
            mk.dma("sp", Ct[:], cvin[128:256, t0 - 2:t0 + TB], writes=[b_in])
            mk.dma("sp", Ht[:], cvin[256:384, t0 - 2:t0 + TB], writes=[b_in])
        G(lambda: nc.gpsimd.tensor_tensor(out=z[:], in0=Ct[:], in1=Ht[:], op=ALU.mult), r=[b_in], w=[b_z])
        V(lambda: nc.vector.tensor_scalar(out=y[:], in0=z[:, 2:TB + 2], scalar1=cwt[:, 2:3], scalar2=None, op0=ALU.mult), r=[b_z, bc], w=[b_y])
        V(lambda: nc.vector.scalar_tensor_tensor(out=y[:], in0=z[:, 1:TB + 1], scalar=cwt[:, 1:2], in1=y[:], op0=ALU.mult, op1=ALU.add), r=[b_z, bc], w=[b_y])
        V(lambda: nc.vector.scalar_tensor_tensor(out=y[:], in0=z[:, 0:TB], scalar=cwt[:, 0:1], in1=y[:], op0=ALU.mult, op1=ALU.add), r=[b_z, bc], w=[b_y])
        G(lambda: nc.gpsimd.tensor_tensor(out=o[:], in0=y[:], in1=Bt[:], op=ALU.mult), r=[b_y, b_in], w=[b_o])
        mk.dma("sp", yT[:, t0:t0 + TB], o[:], reads=[b_o], is_output=True)


def t5_bucket_np(rel):
    n = np.maximum(rel, 0)
    max_exact = 16
    n_f = np.maximum(n, 1).astype(np.float32)
    large = max_exact + (np.log(n_f / max_exact) / math.log(128 / max_exact) * (32 - max_exact)).astype(np.int32)
    return np.where(n < max_exact, n, np.minimum(large, 31))


def attn_tables(rel_bias, sinks_l, q):
    qi = np.arange(128)[:, None]
    kj = np.arange(256)[None, :]
    rel = qi + 128 - kj
    bucket = t5_bucket_np(rel)
    valid = (rel >= 0) & (rel < 128)
    tab = np.zeros((2, 128, 2, 256), np.float32)
    for h in range(2):
        bias = rel_bias[bucket, 2 * q + h]
        full = np.where(valid, bias, np.float32(-30000.0))
        tab[0, :, h, :] = full
        f0 = full.copy()
        f0[:, 0:128] = -30000.0
        tab[1, :, h, :] = f0
    sk = np.broadcast_to(sinks_l[2 * q:2 * q + 2][None, :], (128, 2)).astype(np.float32).copy()
    return tab, sk


def emit_attn(nc, mk, T, qkv, btab, sinkt_d, ident_d, yT, odt=F32):
    V = lambda fn, r=(), w=(): mk.op("dve", fn, r, w)
    A = lambda fn, r=(), w=(): mk.op("act", fn, r, w)
    PE = lambda fn, r=(), w=(): mk.op("pe", fn, r, w, skip_same=True)
    NBK = T // 128
    bt = mk.sb("at_bt", [128, 2, 2, 256]); sk = mk.sb("at_sk", [128, 2]); ident = mk.sb("at_id", [128, 128]); bc = Buf()
    mk.dma("sp", bt[:, 0, :, :], btab[0], writes=[bc])
    mk.dma("sp", bt[:, 1, :, :], btab[1], writes=[bc])
    mk.dma("sp", sk[:], sinkt_d, writes=[bc])
    mk.dma("sp", ident[:], ident_d, writes=[bc])
    CH = 1024
    qt = mk.sb("at_q", [128, CH]); kt = mk.sb("at_k", [128, 128 + CH]); vt = mk.sb("at_v", [64, CH])
    vtok = mk.sb("at_vtok", [128, CH // 128 + 1, 64])
    b_q, b_k, b_v, b_vt = Buf(), Buf(), Buf(), Buf()
    sc = [mk.sb("at_sc%d" % h, [128, 256]) for h in range(2)]; b_sc = [Buf(), Buf()]
    pr = [mk.sb("at_p%d" % h, [128, 256]) for h in range(2)]; b_p = [Buf(), Buf()]
    pT = [mk.sb("at_pT%d" % h, [128, 256]) for h in range(2)]; b_pT = [Buf(), Buf()]
    sm = [mk.sb("at_sm%d" % h, [128, 8]) for h in range(2)]; b_sm = [Buf(), Buf()]
    ot = mk.sb("at_o", [128, 128]); b_o = Buf()
    yo = mk.sb("at_yo", [128, CH], odt); b_yo = Buf()
    ps_s = [mk.ps("at_ps_s%d" % h, [128, 512]) for h in range(2)]; b_ps = [Buf(), Buf()]
    ps_t = [mk.ps("at_ps_t%d" % h, [128, 512]) for h in range(2)]; b_pt = [Buf(), Buf()]
    ps_o = mk.ps("at_ps_o", [128, 512]); b_po = Buf()
    ps_v = mk.ps("at_ps_v", [128, 512]); b_pv = Buf()
    for ch in range(T // CH):
        c0 = ch * CH
        mk.dma("sp", qt[:], qkv[0:128, c0:c0 + CH], writes=[b_q])
        if ch == 0:
            V(lambda: nc.vector.memset(kt[:, 0:128], 0.0), w=[b_k])
            V(lambda: nc.vector.memset(vtok[:, 0, :], 0.0), w=[b_vt])
            for hh in range(2):
                mk.dma("sp", kt[64 * hh:64 * hh + 64, 128:], qkv[128:192, 0:CH], writes=[b_k])
        else:
            for hh in range(2):
                mk.dma("sp", kt[64 * hh:64 * hh + 64, :], qkv[128:192, c0 - 128:c0 + CH], writes=[b_k])
            V(lambda: nc.vector.tensor_copy(out=vtok[:, 0, :], in_=vtok[:, CH // 128, :]), r=[b_vt], w=[b_vt])
        mk.dma("sp", vt[:], qkv[192:256, c0:c0 + CH], writes=[b_v])
        for j in range(CH // 128):
            PE(lambda: nc.tensor.transpose(ps_v[:, j * 64:(j + 1) * 64], vt[:, j * 128:(j + 1) * 128], ident[0:64, 0:64]), r=[b_v, bc], w=[b_pv])
        A(lambda: nc.scalar.copy(out=vtok[:, 1:, :].rearrange("p j d -> p (j d)"), in_=ps_v[:, 0:(CH // 128) * 64]), r=[], w=[b_pv, b_vt])
        for j in range(CH // 128):
            first = 1 if (ch == 0 and j == 0) else 0
            HS = [slice(0, 64), slice(64, 128)]
            for h in range(2):
                PE(lambda: nc.tensor.matmul(ps_s[h][:, 0:256], lhsT=qt[HS[h], j * 128:(j + 1) * 128], rhs=kt[HS[h], j * 128:j * 128 + 256], start=True, stop=True),
                   r=[b_q, b_k], w=[b_ps[h]])
            for h in range(2):
                V(lambda: nc.vector.scalar_tensor_tensor(out=sc[h][:], in0=ps_s[h][:, 0:256], scalar=0.125, in1=bt[:, first, h, :], op0=ALU.mult, op1=ALU.add),
                  r=[bc], w=[b_ps[h], b_sc[h]])
                s = sm[h]
                V(lambda: nc.vector.reduce_max(out=s[:, 0:1], in_=sc[h][:], axis=AX.X), r=[b_sc[h]], w=[b_sm[h]])
                V(lambda: nc.vector.tensor_tensor(out=s[:, 0:1], in0=s[:, 0:1], in1=sk[:, h:h + 1], op=ALU.max), r=[bc], w=[b_sm[h]])
                V(lambda: nc.vector.tensor_scalar(out=s[:, 1:2], in0=s[:, 0:1], scalar1=-1.0, scalar2=None, op0=ALU.mult), r=[], w=[b_sm[h]])
            for h in range(2):
                s = sm[h]
                A(lambda: nc.scalar.activation(out=pr[h][:], in_=sc[h][:], func=AF.Exp, bias=s[:, 1:2], scale=1.0, accum_out=s[:, 2:3]), r=[b_sc[h]], w=[b_sm[h], b_p[h]])
                A(lambda: nc.scalar.activation(out=s[:, 3:4], in_=sk[:, h:h + 1], func=AF.Exp, bias=s[:, 1:2], scale=1.0), r=[bc], w=[b_sm[h]])
            for h in range(2):
                for kb in range(2):
                    PE(lambda: nc.tensor.transpose(ps_t[h][:, kb * 128:(kb + 1) * 128], pr[h][:, kb * 128:(kb + 1) * 128], ident[:, :]), r=[b_p[h], bc], w=[b_pt[h]])
            for h in range(2):
                s = sm[h]
                V(lambda: nc.vector.tensor_tensor(out=s[:, 4:5], in0=s[:, 2:3], in1=s[:, 3:4], op=ALU.add), r=[], w=[b_sm[h]])
                V(lambda: nc.vector.reciprocal(out=s[:, 5:6], in_=s[:, 4:5]), r=[], w=[b_sm[h]])
                V(lambda: nc.vector.tensor_copy(out=pT[h][:], in_=ps_t[h][:, 0:256]), r=[], w=[b_pt[h], b_pT[h]])
            for h in range(2):
                for kb in range(2):
                    PE(lambda: nc.tensor.matmul(ps_o[:, h * 64:(h + 1) * 64], lhsT=pT[h][:, kb * 128:(kb + 1) * 128], rhs=vtok[:, j + kb, :], start=(kb == 0), stop=(kb == 1)),
                       r=[b_pT[h], b_vt], w=[b_po])
            for h in range(2):
                s = sm[h]
                A(lambda: nc.scalar.activation(out=ot[:, h * 64:(h + 1) * 64], in_=ps_o[:, h * 64:(h + 1) * 64], func=AF.Copy, scale=s[:, 5:6]), r=[b_sm[h]], w=[b_po, b_o])
            PE(lambda: nc.tensor.transpose(ps_o[:, 128:256], ot[:, :], ident[:, :]), r=[b_o, bc], w=[b_po])
            V(lambda: nc.vector.tensor_copy(out=yo[:, j * 128:(j + 1) * 128], in_=ps_o[:, 128:256]), r=[], w=[b_po, b_yo])
        mk.dma("sp", yT[:, c0:c0 + CH], yo[:], reads=[b_yo], is_output=True)


CS = 512


def s5_host_inputs(prm, l, q):
    g0 = 8 * q
    par = np.zeros((128, 4, 3), np.float32)
    bb = np.zeros((128, 4, 2, 16), np.float32)
    cc = np.zeros((128, 4, 2, 16), np.float32)
    for j in range(4):
        for gl in range(2):
            g = g0 + 2 * j + gl
            ps = slice(64 * gl, 64 * gl + 64)
            par[ps, j, 0] = prm["s5_lambda_re"][l][g]
            par[ps, j, 1] = prm["s5_lambda_im"][l][g]
            par[ps, j, 2] = prm["s5_log_dt"][l][g]
            bb[ps, j, 0, :] = prm["s5_b_re"][l][g]
            bb[ps, j, 1, :] = prm["s5_b_im"][l][g]
            cc[ps, j, 0, :] = prm["s5_c_re"][l][g].T
            cc[ps, j, 1, :] = prm["s5_c_im"][l][g].T
    dsk = np.ascontiguousarray(prm["s5_d"][l][g0:g0 + 8].reshape(128, 1))
    iot = np.broadcast_to(np.arange(CS, dtype=np.float32)[None, :], (128, CS)).copy()
    return {"s5par": par, "s5bb": bb, "s5cc": cc, "s5d": dsk, "s5iota": iot, "ident": np.eye(128, dtype=np.float32)}


def emit_s5(nc, mk, T, uT, par_d, bb_d, cc_d, d_d, iota_d, ident_d, yT, odt=F32):
    V = lambda fn, r=(), w=(): mk.op("dve", fn, r, w)
    A = lambda fn, r=(), w=(): mk.op("act", fn, r, w)
    G = lambda fn, r=(), w=(): mk.op("pool", fn, r, w)
    PE = lambda fn, r=(), w=(): mk.op("pe", fn, r, w, skip_same=True)
    sb = mk.sb
    TWO_PI = 2.0 * math.pi
    par = sb("s5_par", [128, 4, 3]); bb = sb("s5_bb", [128, 4, 2, 16]); cc = sb("s5_cc", [128, 4, 2, 16])
    dsk = sb("s5_d", [128, 1]); iot = sb("s5_iota", [128, CS]); ident = sb("s5_id", [128, 128])
    bc = Buf("c")
    for dst, src in ((par, par_d), (bb, bb_d), (cc, cc_d), (dsk, d_d), (iot, iota_d), (ident, ident_d)):
        mk.dma("sp", dst[:], src, writes=[bc])
    P = {}
    for nm in ("dl", "mag", "th", "cs", "sn", "are", "aim", "den", "zre", "zim", "t1", "t2", "t3", "cC", "sC"):
        P[nm] = sb("s5_p_" + nm, [128, 4])
    ti = sb("s5_ti", [128, 4 * CS], I32)
    bp = Buf("p")
    lr, li, ldt = par[:, :, 0], par[:, :, 1], par[:, :, 2]

    def sincos(sin_out, cos_out, x, n, tmpa, tmpb, tint):
        def wrap(r):
            V(lambda: nc.vector.tensor_scalar(out=tmpb, in0=r, scalar1=0.5, scalar2=None, op0=ALU.is_gt), r=[bp], w=[bp])
            V(lambda: nc.vector.tensor_tensor(out=r, in0=r, in1=tmpb, op=ALU.subtract), r=[bp], w=[bp])
            V(lambda: nc.vector.tensor_scalar(out=tmpb, in0=r, scalar1=-0.5, scalar2=None, op0=ALU.is_lt), r=[bp], w=[bp])
            V(lambda: nc.vector.tensor_tensor(out=r, in0=r, in1=tmpb, op=ALU.add), r=[bp], w=[bp])
        V(lambda: nc.vector.tensor_copy(out=tint, in_=x), r=[bp, bc], w=[bp])
        V(lambda: nc.vector.tensor_copy(out=tmpa, in_=tint), r=[bp], w=[bp])
        V(lambda: nc.vector.tensor_tensor(out=tmpa, in0=x, in1=tmpa, op=ALU.subtract), r=[bp, bc], w=[bp])
        wrap(tmpa)
        A(lambda: nc.scalar.activation(out=sin_out, in_=tmpa, func=AF.Sin, scale=TWO_PI), r=[bp], w=[bp])
        V(lambda: nc.vector.tensor_scalar(out=tmpa, in0=tmpa, scalar1=0.25, scalar2=None, op0=ALU.add), r=[bp], w=[bp])
        wrap(tmpa)
        A(lambda: nc.scalar.activation(out=cos_out, in_=tmpa, func=AF.Sin, scale=TWO_PI), r=[bp], w=[bp])

    A(lambda: nc.scalar.activation(out=P["dl"][:], in_=ldt, func=AF.Exp), r=[bc], w=[bp])
    V(lambda: nc.vector.tensor_tensor(out=P["mag"][:], in0=lr, in1=P["dl"][:], op=ALU.mult), r=[bc, bp], w=[bp])
    A(lambda: nc.scalar.activation(out=P["mag"][:], in_=P["mag"][:], func=AF.Exp), r=[bp], w=[bp])
    V(lambda: nc.vector.tensor_tensor(out=P["th"][:], in0=li, in1=P["dl"][:], op=ALU.mult), r=[bc, bp], w=[bp])
    V(lambda: nc.vector.tensor_scalar(out=P["th"][:], in0=P["th"][:], scalar1=1.0 / TWO_PI, scalar2=None, op0=ALU.mult), r=[bp], w=[bp])
    V(lambda: nc.vector.tensor_copy(out=ti[:, 0:4], in_=P["th"][:]), r=[bp], w=[bp])
    V(lambda: nc.vector.tensor_copy(out=P["t1"][:], in_=ti[:, 0:4]), r=[bp], w=[bp])
    V(lambda: nc.vector.tensor_tensor(out=P["th"][:], in0=P["th"][:], in1=P["t1"][:], op=ALU.subtract), r=[bp], w=[bp])
    V(lambda: nc.vector.tensor_scalar(out=P["t1"][:], in0=P["th"][:], scalar1=0.5, scalar2=None, op0=ALU.is_gt), r=[bp], w=[bp])
    V(lambda: nc.vector.tensor_tensor(out=P["th"][:], in0=P["th"][:], in1=P["t1"][:], op=ALU.subtract), r=[bp], w=[bp])
    V(lambda: nc.vector.tensor_scalar(out=P["t1"][:], in0=P["th"][:], scalar1=-0.5, scalar2=None, op0=ALU.is_lt), r=[bp], w=[bp])
    V(lambda: nc.vector.tensor_tensor(out=P["th"][:], in0=P["th"][:], in1=P["t1"][:], op=ALU.add), r=[bp], w=[bp])
    sincos(P["sn"][:], P["cs"][:], P["th"][:], 4, P["t1"][:], P["t2"][:], ti[:, 0:4])
    V(lambda: nc.vector.tensor_tensor(out=P["are"][:], in0=P["mag"][:], in1=P["cs"][:], op=ALU.mult), r=[bp], w=[bp])
    V(lambda: nc.vector.tensor_tensor(out=P["aim"][:], in0=P["mag"][:], in1=P["sn"][:], op=ALU.mult), r=[bp], w=[bp])
    V(lambda: nc.vector.tensor_tensor(out=P["den"][:], in0=lr, in1=lr, op=ALU.mult), r=[bc], w=[bp])
    V(lambda: nc.vector.tensor_tensor(out=P["t1"][:], in0=li, in1=li, op=ALU.mult), r=[bc], w=[bp])
    V(lambda: nc.vector.tensor_tensor(out=P["den"][:], in0=P["den"][:], in1=P["t1"][:], op=ALU.add), r=[bp], w=[bp])
    V(lambda: nc.vector.reciprocal(out=P["den"][:], in_=P["den"][:]), r=[bp], w=[bp])
    V(lambda: nc.vector.tensor_scalar(out=P["t3"][:], in0=P["are"][:], scalar1=-1.0, scalar2=None, op0=ALU.add), r=[bp], w=[bp])
    V(lambda: nc.vector.tensor_tensor(out=P["t1"][:], in0=P["t3"][:], in1=lr, op=ALU.mult), r=[bp, bc], w=[bp])
    V(lambda: nc.vector.tensor_tensor(out=P["t2"][:], in0=P["aim"][:], in1=li, op=ALU.mult), r=[bp, bc], w=[bp])
    V(lambda: nc.vector.tensor_tensor(out=P["t1"][:], in0=P["t1"][:], in1=P["t2"][:], op=ALU.add), r=[bp], w=[bp])
    V(lambda: nc.vector.tensor_tensor(out=P["zre"][:], in0=P["t1"][:], in1=P["den"][:], op=ALU.mult), r=[bp], w=[bp])
    V(lambda: nc.vector.tensor_tensor(out=P["t1"][:], in0=P["aim"][:], in1=lr, op=ALU.mult), r=[bp, bc], w=[bp])
    V(lambda: nc.vector.tensor_tensor(out=P["t2"][:], in0=P["t3"][:], in1=li, op=ALU.mult), r=[bp, bc], w=[bp])
    V(lambda: nc.vector.tensor_tensor(out=P["t1"][:], in0=P["t1"][:], in1=P["t2"][:], op=ALU.subtract), r=[bp], w=[bp])
    V(lambda: nc.vector.tensor_tensor(out=P["zim"][:], in0=P["t1"][:], in1=P["den"][:], op=ALU.mult), r=[bp], w=[bp])
    bbar = sb("s5_bbar", [128, 4, 2, 16]); tb1 = sb("s5_tb1", [128, 4, 16]); tb2 = sb("s5_tb2", [128, 4, 16])
    zre_b = P["zre"][:].unsqueeze(2).to_broadcast([128, 4, 16]); zim_b = P["zim"][:].unsqueeze(2).to_broadcast([128, 4, 16])
    V(lambda: nc.vector.tensor_tensor(out=tb1[:], in0=bb[:, :, 0, :], in1=zre_b, op=ALU.mult), r=[bp, bc], w=[bp])
    V(lambda: nc.vector.tensor_tensor(out=tb2[:], in0=bb[:, :, 1, :], in1=zim_b, op=ALU.mult), r=[bp, bc], w=[bp])
    V(lambda: nc.vector.tensor_tensor(out=bbar[:, :, 0, :], in0=tb1[:], in1=tb2[:], op=ALU.subtract), r=[bp], w=[bp])
    V(lambda: nc.vector.tensor_tensor(out=tb1[:], in0=bb[:, :, 1, :], in1=zre_b, op=ALU.mult), r=[bp, bc], w=[bp])
    V(lambda: nc.vector.tensor_tensor(out=tb2[:], in0=bb[:, :, 0, :], in1=zim_b, op=ALU.mult), r=[bp, bc], w=[bp])
    V(lambda: nc.vector.tensor_tensor(out=bbar[:, :, 1, :], in0=tb1[:], in1=tb2[:], op=ALU.add), r=[bp], w=[bp])
    BD = sb("s5_BD", [128, 4, 2, 128]); CM = sb("s5_CM", [128, 4, 4, 128]); BbT = sb("s5_BbT", [128, 4, 2, 128])
    V(lambda: nc.vector.memset(BD[:].rearrange("p a b c -> p (a b c)"), 0.0), w=[bp])
    V(lambda: nc.vector.memset(CM[:].rearrange("p a b c -> p (a b c)"), 0.0), w=[bp])
    for j in range(4):
        for gl in range(2):
            ps_ = slice(64 * gl, 64 * gl + 64)
            c0 = 32 * j + 16 * gl
            for ri in range(2):
                V(lambda: nc.vector.tensor_copy(out=BD[ps_, j, ri, c0:c0 + 16], in_=bbar[ps_, j, ri, :]), r=[bp], w=[bp])
            V(lambda: nc.vector.tensor_copy(out=CM[ps_, j, 0, c0:c0 + 16], in_=cc[ps_, j, 0, :]), r=[bc], w=[bp])
            V(lambda: nc.vector.tensor_scalar(out=CM[ps_, j, 1, c0:c0 + 16], in0=cc[ps_, j, 0, :], scalar1=-1.0, scalar2=None, op0=ALU.mult), r=[bc], w=[bp])
            V(lambda: nc.vector.tensor_scalar(out=CM[ps_, j, 2, c0:c0 + 16], in0=cc[ps_, j, 1, :], scalar1=-1.0, scalar2=None, op0=ALU.mult), r=[bc], w=[bp])
            V(lambda: nc.vector.tensor_scalar(out=CM[ps_, j, 3, c0:c0 + 16], in0=cc[ps_, j, 1, :], scalar1=-1.0, scalar2=None, op0=ALU.mult), r=[bc], w=[bp])
    ps_tmp = mk.ps("s5_ps_tmp", [128, 512]); b_pt = Buf()
    for j in range(4):
        for ri in range(2):
            PE(lambda: nc.tensor.transpose(ps_tmp[:, ri * 128:(ri + 1) * 128], BD[:, j, ri, :], ident[:, :]), r=[bp, bc], w=[b_pt])
        V(lambda: nc.vector.tensor_copy(out=BbT[:, j, :, :].rearrange("p a b -> p (a b)"), in_=ps_tmp[:, 0:256]), r=[], w=[b_pt, bp])
    cosT = sb("s5_cosT", [128, 4, CS]); sinT = sb("s5_sinT", [128, 4, CS])
    xa = sb("s5_xa", [128, 4 * CS]); xb = sb("s5_xb", [128, 4 * CS]); xc = sb("s5_xc", [128, 4 * CS])
    for j in range(4):
        V(lambda: nc.vector.tensor_scalar(out=xc[:, j * CS:(j + 1) * CS], in0=iot[:], scalar1=P["th"][:, j:j + 1], scalar2=None, op0=ALU.mult), r=[bp, bc], w=[bp])
    sincos(sinT[:].rearrange("p a b -> p (a b)"), cosT[:].rearrange("p a b -> p (a b)"), xc[:], 4 * CS, xa[:], xb[:], ti[:])
    V(lambda: nc.vector.tensor_scalar(out=P["t3"][:], in0=P["th"][:], scalar1=float(CS), scalar2=None, op0=ALU.mult), r=[bp], w=[bp])
    sincos(P["sC"][:], P["cC"][:], P["t3"][:], 4, P["t1"][:], P["t2"][:], ti[:, 0:4])
    WW = []
    for par in range(2):
        Wd = {}
        for nm in ("t1", "t2", "t3", "t4", "br", "bi", "zr", "zi", "q1", "q2", "q3", "q4"):
            Wd[nm] = sb("s5_w%d_%s" % (par, nm), [128, CS])
        WW.append(Wd)
    b_wl = [Buf("w0"), Buf("w1")]; b_zl = [Buf("z0"), Buf("z1")]; b_ql = [Buf("q0"), Buf("q1")]
    init = sb("s5_init", [128, 4, 2]); itmp = sb("s5_itmp", [128, 4, 2]); b_il = [Buf("init%d" % j) for j in range(4)]
    for j in range(4):
        V(lambda: nc.vector.memset(init[:, j, :], 0.0), w=[b_il[j]])
    yv = sb("s5_yv", [128, CS]); y2 = sb("s5_y2", [128, CS]); yo = sb("s5_yo", [128, CS], odt); b_y = Buf("y")
    ps_al = [mk.ps("s5_ps_a%d" % i, [128, 512]) for i in range(2)]; ps_bl = [mk.ps("s5_ps_b%d" % i, [128, 512]) for i in range(2)]
    ps_y = mk.ps("s5_ps_y", [128, 512])
    b_pal, b_pbl, b_py = [Buf(), Buf()], [Buf(), Buf()], Buf()
    uts = [sb("s5_u%d" % i, [128, CS]) for i in range(2)]; b_ul = [Buf(), Buf()]
    NCK = T // CS

    def stage1(n):
        chk, j = n // 4, n % 4
        par = n % 2
        ut = uts[chk % 2]; b_u = b_ul[chk % 2]
        if j == 0:
            mk.dma("sp", ut[:], uT[:, chk * CS:(chk + 1) * CS], writes=[b_u])
        W = WW[par]; b_w = b_wl[par]
        ps_a = ps_al[par]; ps_b = ps_bl[par]; b_pa = b_pal[par]; b_pb = b_pbl[par]
        PE(lambda: nc.tensor.matmul(ps_a[:, :], lhsT=BbT[:, j, 0, :], rhs=ut[:, :], start=True, stop=True), r=[bp, b_u], w=[b_pa])
        PE(lambda: nc.tensor.matmul(ps_b[:, :], lhsT=BbT[:, j, 1, :], rhs=ut[:, :], start=True, stop=True), r=[bp, b_u], w=[b_pb])

    def stage1v(n):
        chk, j = n // 4, n % 4
        par = n % 2
        W = WW[par]; b_w = b_wl[par]
        ps_a = ps_al[par]; ps_b = ps_bl[par]; b_pa = b_pal[par]; b_pb = b_pbl[par]
        cj, sj = cosT[:, j, :], sinT[:, j, :]
        V(lambda: nc.vector.tensor_tensor(out=W["t1"][:], in0=ps_a[:, :], in1=cj, op=ALU.mult), r=[bp], w=[b_pa, b_w])
        V(lambda: nc.vector.tensor_tensor(out=W["t4"][:], in0=ps_a[:, :], in1=sj, op=ALU.mult), r=[bp], w=[b_pa, b_w])
        V(lambda: nc.vector.tensor_tensor(out=W["t2"][:], in0=ps_b[:, :], in1=sj, op=ALU.mult), r=[bp], w=[b_pb, b_w])
        V(lambda: nc.vector.tensor_tensor(out=W["t3"][:], in0=ps_b[:, :], in1=cj, op=ALU.mult), r=[bp], w=[b_pb, b_w])
        G(lambda: nc.gpsimd.tensor_tensor(out=W["br"][:], in0=W["t1"][:], in1=W["t2"][:], op=ALU.add), r=[b_w], w=[b_w])
        G(lambda: nc.gpsimd.tensor_tensor(out=W["bi"][:], in0=W["t3"][:], in1=W["t4"][:], op=ALU.subtract), r=[b_w], w=[b_w])

    def stage2(n):
        chk, j = n // 4, n % 4
        par = n % 2
        t0 = chk * CS
        ut = uts[chk % 2]; b_u = b_ul[chk % 2]
        W = WW[par]; b_w = b_wl[par]; b_z = b_zl[par]; b_q = b_ql[par]; b_i = b_il[j]
        cj, sj = cosT[:, j, :], sinT[:, j, :]
        rho = P["mag"][:, j:j + 1].to_broadcast([128, CS])
        V(lambda: nc.vector.tensor_tensor_scan(out=W["zr"][:], data0=rho, data1=W["br"][:], initial=init[:, j, 0:1], op0=ALU.mult, op1=ALU.add), r=[b_w, bp, b_i, b_q], w=[b_z])
        V(lambda: nc.vector.tensor_tensor_scan(out=W["zi"][:], data0=rho, data1=W["bi"][:], initial=init[:, j, 1:2], op0=ALU.mult, op1=ALU.add), r=[b_w, bp, b_i, b_q], w=[b_z])
        zrl, zil = W["zr"][:, CS - 1:CS], W["zi"][:, CS - 1:CS]
        cC, sC = P["cC"][:, j:j + 1], P["sC"][:, j:j + 1]
        V(lambda: nc.vector.tensor_tensor(out=itmp[:, j, 0:1], in0=zil, in1=sC, op=ALU.mult), r=[b_z, bp], w=[b_i])
        V(lambda: nc.vector.scalar_tensor_tensor(out=init[:, j, 0:1], in0=zrl, scalar=cC, in1=itmp[:, j, 0:1], op0=ALU.mult, op1=ALU.subtract), r=[b_z, bp], w=[b_i])
        V(lambda: nc.vector.tensor_tensor(out=itmp[:, j, 1:2], in0=zrl, in1=sC, op=ALU.mult), r=[b_z, bp], w=[b_i])
        V(lambda: nc.vector.scalar_tensor_tensor(out=init[:, j, 1:2], in0=zil, scalar=cC, in1=itmp[:, j, 1:2], op0=ALU.mult, op1=ALU.add), r=[b_z, bp], w=[b_i])
        G(lambda: nc.gpsimd.tensor_tensor(out=W["q1"][:], in0=W["zr"][:], in1=cj, op=ALU.mult), r=[b_z, bp], w=[b_q])
        G(lambda: nc.gpsimd.tensor_tensor(out=W["q2"][:], in0=W["zi"][:], in1=sj, op=ALU.mult), r=[b_z, bp], w=[b_q])
        V(lambda: nc.vector.tensor_tensor(out=W["q3"][:], in0=W["zi"][:], in1=cj, op=ALU.mult), r=[b_z, bp], w=[b_q])
        V(lambda: nc.vector.tensor_tensor(out=W["q4"][:], in0=W["zr"][:], in1=sj, op=ALU.mult), r=[b_z, bp], w=[b_q])
        for qi, nm in enumerate(("q1", "q2", "q3", "q4")):
            PE(lambda: nc.tensor.matmul(ps_y[:, :], lhsT=CM[:, j, qi, :], rhs=W[nm][:, :], start=(j == 0 and qi == 0), stop=(j == 3 and qi == 3)), r=[bp, b_q], w=[b_py])
        if j == 3:
            V(lambda: nc.vector.scalar_tensor_tensor(out=yv[:], in0=ut[:], scalar=dsk[:, 0:1], in1=ps_y[:, :], op0=ALU.mult, op1=ALU.add), r=[b_u, bc], w=[b_py, b_y])
            A(lambda: nc.scalar.activation(out=y2[:], in_=yv[:], func=AF.Square), r=[b_y], w=[b_y])
            V(lambda: nc.vector.tensor_scalar(out=y2[:], in0=y2[:], scalar1=0.044715, scalar2=1.0, op0=ALU.mult, op1=ALU.add), r=[b_y], w=[b_y])
            V(lambda: nc.vector.tensor_tensor(out=y2[:], in0=y2[:], in1=yv[:], op=ALU.mult), r=[b_y], w=[b_y])
            A(lambda: nc.scalar.activation(out=y2[:], in_=y2[:], func=AF.Tanh, scale=0.7978845608028654), r=[b_y], w=[b_y])
            V(lambda: nc.vector.scalar_tensor_tensor(out=y2[:], in0=y2[:], scalar=1.0, in1=yv[:], op0=ALU.add, op1=ALU.mult), r=[b_y], w=[b_y])
            A(lambda: nc.scalar.mul(out=yo[:], in_=y2[:], mul=0.5), r=[b_y], w=[b_y])
            mk.dma("sp", yT[:, t0:t0 + CS], yo[:], reads=[b_y], is_output=True)

    NTL_ = NCK * 4
    stage1(0)
    for n in range(NTL_ + 1):
        if n + 1 < NTL_:
            stage1(n + 1)
        if n < NTL_:
            stage1v(n)
        if n >= 1:
            stage2(n - 1)


def build(which, T):
    nc = bass.Bass("TRN2", target_bir_lowering=False)
    dt = lambda n, s, k="ExternalInput": nc.dram_tensor(n, s, F32, kind=k).ap()
    with ExitStack() as ctx:
        mk = MK(nc, ctx)
        if which == "conv":
            emit_conv(nc, mk, T, dt("cvin", [384, T]), dt("cw", [128, 3]), dt("yT", [128, T], "ExternalOutput"), TB=min(T, 2048))
        elif which == "attn":
            emit_attn(nc, mk, T, dt("qkv", [256, T]), dt("btab", [2, 128, 2, 256]), dt("sinkt", [128, 2]), dt("ident", [128, 128]), dt("yT", [128, T], "ExternalOutput"))
        elif which == "s5":
            emit_s5(nc, mk, T, dt("uT", [128, T]), dt("s5par", [128, 4, 3]), dt("s5bb", [128, 4, 2, 16]), dt("s5cc", [128, 4, 2, 16]),
                    dt("s5d", [128, 1]), dt("s5iota", [128, CS]), dt("ident", [128, 128]), dt("yT", [128, T], "ExternalOutput"))
        mk.finish("sp")
        print(which, "ops", mk.nops, "waits", mk.nwaits)
    return nc


import math
import numpy as np
from contextlib import ExitStack

D = 2048
TOK = 2048
NT = TOK // 128
NE = 32
CAP = 256
ALPHA = 4 ** 0.25
DE = 512


def k3_consts():
    tp = np.arange(128)[:, None]
    t = np.arange(128)[None, :]
    U = (tp < t).astype(np.float32)
    ecap = np.broadcast_to((np.arange(NE) * CAP).astype(np.float32)[None, :], (128, NE)).copy()
    return {"U": U, "ecap": ecap, "ident": np.eye(128, dtype=np.float32)}


def emit_k3(nc, mk, ymixT, x, w_out, glu_w, glu_b, rows, wr, br, w1, w3, w2, cst, x1s, Xg, Yg, xout, ymload=None, rowload=None):
    V = lambda fn, r=(), w=(): mk.op("dve", fn, r, w)
    A = lambda fn, r=(), w=(): mk.op("act", fn, r, w)
    G = lambda fn, r=(), w=(): mk.op("pool", fn, r, w)
    PE = lambda fn, r=(), w=(): mk.op("pe", fn, r, w, skip_same=True)
    gw = mk.sb("k3_gw", [128, NT, 2]); slot = mk.sb("k3_slot", [128, NT, 2], I32); b_rt = Buf("route")
    ident = mk.sb("k3_ident", [128, 128]); identb = mk.sb("k3_identb", [128, 128], BF16); bc = Buf("c")
    mk.dma("sp", ident[:], cst["ident"], writes=[bc])
    V(lambda: nc.vector.tensor_copy(out=identb[:], in_=ident[:]), r=[bc], w=[bc])
    with ExitStack() as pa:
        sb = lambda n, s, dt=F32: pa.enter_context(nc.sbuf_tensor("k3a%d_" % mk.gen + n, list(s), dt))
        ps = lambda n, s, dt=F32: pa.enter_context(nc.psum_tensor("k3a%d_" % mk.gen + n, list(s), dt))
        wo = sb("wo", [128, 16, D], BF16); gluw = sb("gluw", [128, 4, 512], BF16); glub = sb("glub", [128, 4])
        R = [sb("row%d" % i, [128, D]) for i in range(5)]
        wrt = sb("wr", [128, 16, 36]); brt = sb("br", [128, 36]); Ut = sb("U", [128, 128]); ones = sb("ones", [128, 128]); ecap = sb("ecap", [128, NE])
        Srun = sb("Srun", [128, NE]); b_S = Buf("S")
        if ymload is None:
            ym = [sb("ym%d" % i, [128, 16, 128], BF16) for i in range(2)]; b_ym = [Buf(), Buf()]
        else:
            ymbig = sb("ymbig", [128, 16, 1024], BF16); _bym = Buf()
            ym = None; b_ym = [_bym, _bym]
        _xt = sb("xt0", [128, D]); _bxt = Buf()
        xt = [_xt, _xt]; b_xt = [_bxt, _bxt]
        sg = sb("sg", [128, 4, 128]); b_sg = Buf()
        xr = sb("xr", [128, D]); b_xr = Buf()
        xn = xr; b_xn = b_xr
        _x1 = sb("x1_0", [128, D]); _bx1 = Buf()
        x1 = [_x1, _x1]; b_x1 = [_bx1, _bx1]
        h2l = [sb("h2_%d" % i_, [128, D]) for i_ in range(2)]; b_h2l = [Buf(), Buf()]
        hb = [sb("hb%d" % i, [128, D], BF16) for i in range(2)]; b_hb = [Buf(), Buf()]
        h2T = sb("h2T", [128, 16, 128]); b_h2T = Buf()
        st = sb("st", [128, 4, 6]); mv = sb("mv", [128, 2]); rs = sb("rs", [128, 1]); nmr = sb("nmr", [128, 1]); b_s = Buf()
        lg = sb("lg", [128, 36]); rt = sb("rt", [128, 16]); em = sb("em", [128, 32]); em2 = sb("em2", [128, 32])
        oh1 = sb("oh1", [128, 32]); oh2 = sb("oh2", [128, 32]); Mk = sb("Mk", [128, 32]); rank = sb("rank", [128, 32]); t32 = sb("t32", [128, 32]); pen = sb("pen", [128, 4]); ohg = sb("ohg", [128, 4]); eg = sb("eg", [128, 4])
        b_r = Buf("r")
        p_g = ps("p_g", [128, 512]); b_pg = Buf()
        p_o = [ps("p_o%d" % i, [128, 512]) for i in range(2)]; b_po = [Buf(), Buf()]
        p_t = [ps("p_t%d" % i, [128, 512]) for i in range(2)]; b_pt = [Buf(), Buf()]
        p_r = ps("p_r", [128, 512]); b_pr = Buf()
        p_k = ps("p_k", [128, 512]); b_pk = Buf()
        mk.dma("pool", wo[:], w_out.rearrange("(k p) c -> p k c", p=128), writes=[bc])
        mk.dma("pool", gluw[:], glu_w.rearrange("(k p) c -> p k c", p=128), writes=[bc])
        mk.dma("sp", glub[:], glu_b, writes=[bc])
        if rowload is None:
            rowload = lambda dst, ri, bcx: mk.dma("sp", dst[:], rows[ri], writes=[bcx])
        for i, ri in enumerate((0, 1, 2, 3, 4)):
            rowload(R[i], ri, bc)
        G(lambda: nc.gpsimd.tensor_scalar(out=R[0][:], in0=R[0][:], scalar1=1.0, scalar2=None, op0=ALU.add), r=[bc], w=[bc])
        G(lambda: nc.gpsimd.tensor_scalar(out=R[3][:], in0=R[3][:], scalar1=1.0, scalar2=None, op0=ALU.add), r=[bc], w=[bc])
        mk.dma("sp", wrt[:], wr.rearrange("(k p) c -> p k c", p=128), writes=[bc])
        mk.dma("sp", brt[:], br, writes=[bc])
        mk.dma("sp", Ut[:], cst["U"], writes=[bc])
        mk.dma("sp", ecap[:], cst["ecap"], writes=[bc])
        V(lambda: nc.vector.memset(ones[:], 1.0), w=[bc])
        V(lambda: nc.vector.memset(Srun[:], 0.0), w=[b_S])
        zt = hb[0]; b_z = b_hb[0]
        V(lambda: nc.vector.memset(zt[:], 0.0), w=[b_z])
        b_Xg = Buf("Xg")
        XgV = Xg.rearrange("(a p) c -> p a c", p=128)
        for a in range(NE * CAP // 128):
            mk.dma("sp", XgV[:, a, :], zt[:], reads=[b_z], writes=[b_Xg])

        def front(t, tick=lambda: None):
            i = t % 2
            ts_ = slice(t * 128, (t + 1) * 128)
            if ymload is None:
                mk.dma("pool", ym[i][:], ymixT[:, ts_].rearrange("(k p) t -> p k t", p=128), writes=[b_ym[i]])
                tick()
                ymt = ym[i]
            else:
                if t % 8 == 0:
                    ymload(t // 8, ymbig, b_ym[i])
                    tick()
                ymt = ymbig[:, :, (t % 8) * 128:(t % 8 + 1) * 128]
            mk.dma("sp", xt[i][:], x[ts_, :], writes=[b_xt[i]])
            tick()
            for oc in range(4):
                for kc in range(4):
                    PE(lambda: nc.tensor.matmul(p_g[:, oc * 128:(oc + 1) * 128], lhsT=gluw[:, kc, oc * 128:(oc + 1) * 128], rhs=ymt[:, 12 + kc, :], start=(kc == 0), stop=(kc == 3)),
                       r=[bc, b_ym[i]], w=[b_pg])
                    tick()
            for oc in range(4):
                A(lambda: nc.scalar.activation(out=sg[:, oc, :], in_=p_g[:, oc * 128:(oc + 1) * 128], func=AF.Sigmoid, bias=glub[:, oc:oc + 1], scale=1.0), r=[bc], w=[b_pg, b_sg])
                tick()
            V(lambda: nc.vector.tensor_tensor(out=ymt[:, 12:16, :], in0=ymt[:, 12:16, :], in1=sg[:], op=ALU.mult), r=[b_sg], w=[b_ym[i]])
            tick()
            for cc in range(4):
                j = cc % 2
                for k in range(16):
                    PE(lambda: nc.tensor.matmul(p_o[j][:, :], lhsT=ymt[:, k, :], rhs=wo[:, k, cc * 512:(cc + 1) * 512], start=(k == 0), stop=(k == 15)), r=[b_ym[i], bc], w=[b_po[j]])
                    tick()
                V(lambda: nc.vector.tensor_tensor(out=xr[:, cc * 512:(cc + 1) * 512], in0=p_o[j][:, :], in1=R[0][:, cc * 512:(cc + 1) * 512], op=ALU.mult), r=[bc], w=[b_po[j], b_xr])
                tick()
            V(lambda: nc.vector.scalar_tensor_tensor(out=xr[:], in0=xt[i][:], scalar=ALPHA, in1=xr[:], op0=ALU.mult, op1=ALU.add), r=[b_xt[i]], w=[b_xr])
            tick()
            ln_stats(nc, mk, xr, b_xr, st, mv, rs, nmr, b_s)
            tick()
            A(lambda: nc.scalar.activation(out=xn[:], in_=xr[:], func=AF.Identity, bias=nmr[:, 0:1], scale=rs[:, 0:1]), r=[b_s], w=[b_xn])
            tick()
            V(lambda: nc.vector.tensor_tensor(out=xn[:], in0=xn[:], in1=R[1][:], op=ALU.mult), r=[bc], w=[b_xn])
            tick()
            V(lambda: nc.vector.tensor_tensor(out=x1[i][:], in0=xn[:], in1=R[2][:], op=ALU.add), r=[b_xn, bc], w=[b_x1[i]])
            tick()
            mk.dma("sp", x1s[ts_, :], x1[i][:], reads=[b_x1[i]])
            tick()
            ln_stats(nc, mk, x1[i], b_x1[i], st, mv, rs, nmr, b_s)
            tick()
            A(lambda: nc.scalar.activation(out=xn[:], in_=x1[i][:], func=AF.Identity, bias=nmr[:, 0:1], scale=rs[:, 0:1]), r=[b_x1[i], b_s], w=[b_xn])
            tick()
            V(lambda: nc.vector.tensor_tensor(out=xn[:], in0=xn[:], in1=R[3][:], op=ALU.mult), r=[bc], w=[b_xn])
            tick()
            h2 = h2l[i]; b_h2 = b_h2l[i]
            V(lambda: nc.vector.tensor_tensor(out=h2[:], in0=xn[:], in1=R[4][:], op=ALU.add), r=[b_xn, bc], w=[b_h2])
            tick()
            A(lambda: nc.scalar.copy(out=hb[i][:], in_=h2[:]), r=[b_h2], w=[b_hb[i]])
            tick()
        def tail(t):
            i = t % 2
            ts_ = slice(t * 128, (t + 1) * 128)
            h2 = h2l[i]; b_h2 = b_h2l[i]
            for half in range(4):
                j = half % 2
                for kk in range(4):
                    k = half * 4 + kk
                    PE(lambda: nc.tensor.transpose(p_t[j][:, kk * 128:(kk + 1) * 128], h2[:, k * 128:(k + 1) * 128], ident[:, :]), r=[b_h2, bc], w=[b_pt[j]])
                    yield
                if j == 0:
                    A(lambda: nc.scalar.copy(out=h2T[:, half * 4:(half + 1) * 4, :].rearrange("p a b -> p (a b)"), in_=p_t[j][:, :]), r=[], w=[b_pt[j], b_h2T])
                    yield
                else:
                    V(lambda: nc.vector.tensor_copy(out=h2T[:, half * 4:(half + 1) * 4, :].rearrange("p a b -> p (a b)"), in_=p_t[j][:, :]), r=[], w=[b_pt[j], b_h2T])
                    yield
            for k in range(16):
                PE(lambda: nc.tensor.matmul(p_r[:, 0:36], lhsT=h2T[:, k, :], rhs=wrt[:, k, :], start=(k == 0), stop=(k == 15)), r=[b_h2T, bc], w=[b_pr])
                yield
            V(lambda: nc.vector.tensor_tensor(out=lg[:], in0=p_r[:, 0:36], in1=brt[:], op=ALU.add), r=[bc], w=[b_pr, b_r])
            yield
            R_ = lambda fn: V(fn, r=[b_r, bc], w=[b_r])
            R_(lambda: nc.vector.reduce_max(out=rt[:, 0:1], in_=lg[:, 0:4], axis=AX.X))
            yield
            R_(lambda: nc.vector.tensor_scalar(out=ohg[:], in0=lg[:, 0:4], scalar1=rt[:, 0:1], scalar2=None, op0=ALU.is_ge))
            yield
            R_(lambda: nc.vector.tensor_scalar(out=rt[:, 1:2], in0=rt[:, 0:1], scalar1=-1.0, scalar2=None, op0=ALU.mult))
            yield
            A(lambda: nc.scalar.activation(out=eg[:], in_=lg[:, 0:4], func=AF.Exp, bias=rt[:, 1:2], scale=1.0, accum_out=rt[:, 2:3]), r=[b_r], w=[b_r])
            yield
            R_(lambda: nc.vector.reciprocal(out=rt[:, 3:4], in_=rt[:, 2:3]))
            yield
            R_(lambda: nc.vector.tensor_scalar(out=pen[:], in0=ohg[:], scalar1=-1.0, scalar2=1e30, op0=ALU.add, op1=ALU.mult))
            yield
            R_(lambda: nc.vector.tensor_tensor(out=em[:].rearrange("p (g e) -> p g e", e=8), in0=lg[:, 4:36].rearrange("p (g e) -> p g e", e=8),
                                               in1=pen[:].unsqueeze(2).to_broadcast([128, 4, 8]), op=ALU.add))
            yield
            R_(lambda: nc.vector.reduce_max(out=rt[:, 4:5], in_=em[:], axis=AX.X))
            yield
            R_(lambda: nc.vector.tensor_scalar(out=oh1[:], in0=em[:], scalar1=rt[:, 4:5], scalar2=None, op0=ALU.is_ge))
            yield
            R_(lambda: nc.vector.scalar_tensor_tensor(out=em2[:], in0=oh1[:], scalar=-1e30, in1=em[:], op0=ALU.mult, op1=ALU.add))
            yield
            R_(lambda: nc.vector.reduce_max(out=rt[:, 5:6], in_=em2[:], axis=AX.X))
            yield
            R_(lambda: nc.vector.tensor_scalar(out=oh2[:], in0=em2[:], scalar1=rt[:, 5:6], scalar2=None, op0=ALU.is_ge))
            yield
            R_(lambda: nc.vector.tensor_tensor(out=rt[:, 6:7], in0=rt[:, 5:6], in1=rt[:, 4:5], op=ALU.subtract))
            yield
            A(lambda: nc.scalar.activation(out=rt[:, 7:8], in_=rt[:, 6:7], func=AF.Exp), r=[b_r], w=[b_r])
            yield
            R_(lambda: nc.vector.tensor_scalar(out=rt[:, 8:9], in0=rt[:, 7:8], scalar1=1.0, scalar2=None, op0=ALU.add))
            yield
            R_(lambda: nc.vector.reciprocal(out=rt[:, 8:9], in_=rt[:, 8:9]))
            yield
            R_(lambda: nc.vector.tensor_tensor(out=rt[:, 9:10], in0=rt[:, 7:8], in1=rt[:, 8:9], op=ALU.mult))
            yield
            R_(lambda: nc.vector.tensor_tensor(out=Mk[:], in0=oh1[:], in1=oh2[:], op=ALU.add))
            yield
            PE(lambda: nc.tensor.matmul(p_k[:, 0:32], lhsT=Ut[:, :], rhs=Mk[:, :], start=True, stop=True), r=[bc, b_r], w=[b_pk])
            yield
            PE(lambda: nc.tensor.matmul(p_k[:, 32:64], lhsT=ones[:, :], rhs=Mk[:, :], start=True, stop=True), r=[bc, b_r], w=[b_pk])
            yield
            V(lambda: nc.vector.tensor_tensor(out=rank[:], in0=p_k[:, 0:32], in1=Srun[:], op=ALU.add), r=[b_S, b_r], w=[b_pk, b_r])
            yield
            V(lambda: nc.vector.tensor_tensor(out=Srun[:], in0=p_k[:, 32:64], in1=Srun[:], op=ALU.add), r=[b_r], w=[b_pk, b_S])
            yield
            for kx, oh in enumerate((oh1, oh2)):
                R_(lambda: nc.vector.tensor_tensor(out=t32[:], in0=oh[:], in1=rank[:], op=ALU.mult))
                yield
                R_(lambda: nc.vector.reduce_sum(out=rt[:, 10:11], in_=t32[:], axis=AX.X))
                yield
                R_(lambda: nc.vector.tensor_tensor(out=t32[:], in0=oh[:], in1=ecap[:], op=ALU.mult))
                yield
                R_(lambda: nc.vector.reduce_sum(out=rt[:, 11:12], in_=t32[:], axis=AX.X))
                yield
                R_(lambda: nc.vector.tensor_scalar(out=rt[:, 12:13], in0=rt[:, 10:11], scalar1=float(CAP), scalar2=None, op0=ALU.is_ge))
                yield
                R_(lambda: nc.vector.tensor_tensor(out=rt[:, 11:12], in0=rt[:, 11:12], in1=rt[:, 10:11], op=ALU.add))
                yield
                R_(lambda: nc.vector.scalar_tensor_tensor(out=rt[:, 11:12], in0=rt[:, 12:13], scalar=1e6, in1=rt[:, 11:12], op0=ALU.mult, op1=ALU.add))
                yield
                V(lambda: nc.vector.tensor_copy(out=slot[:, t, kx:kx + 1], in_=rt[:, 11:12]), r=[b_r], w=[b_rt])
                yield
                R_(lambda: nc.vector.tensor_scalar(out=rt[:, 13:14], in0=rt[:, 12:13], scalar1=-1.0, scalar2=-1.0, op0=ALU.add, op1=ALU.mult))
                yield
                R_(lambda: nc.vector.tensor_tensor(out=rt[:, 13:14], in0=rt[:, 13:14], in1=rt[:, 3:4], op=ALU.mult))
                yield
                V(lambda: nc.vector.tensor_tensor(out=gw[:, t, kx:kx + 1], in0=rt[:, 13:14], in1=rt[:, 8 + kx:9 + kx], op=ALU.mult), r=[b_r], w=[b_rt])
                yield
                mk.idma(Xg, hb[i][:, :], slot[:, t, kx:kx + 1], True, NE * CAP - 1, reads=[b_hb[i], b_rt], writes=[b_Xg])
                yield
        for t in range(NT):
            if t == 0:
                front(0)
            g = tail(t)
            if t + 1 < NT:
                def tick(_g=g):
                    next(_g, None); next(_g, None); next(_g, None)
                front(t + 1, tick)
            for _ in g:
                pass
        mk.barrier()
    b_Yg = Buf("Yg")
    with ExitStack() as pb:
        sb = lambda n, s, dt=F32: pb.enter_context(nc.sbuf_tensor("k3b%d_" % mk.gen + n, list(s), dt))
        ps = lambda n, s, dt=F32: pb.enter_context(nc.psum_tensor("k3b%d_" % mk.gen + n, list(s), dt))
        W1 = [sb("w1_%d" % i, [128, 16, DE], BF16) for i in range(2)]
        W3 = [sb("w3_%d" % i, [128, 16, DE], BF16) for i in range(2)]
        W2 = [sb("w2_%d" % i, [128, 4, D], BF16) for i in range(2)]
        b_w = [Buf(), Buf()]
        NTL = CAP // 128
        xg = sb("xg", [128, NTL, D], BF16); b_xg = Buf()
        xgT = sb("xgT", [128, 16, CAP], BF16); b_xgT = Buf()
        ga = sb("ga", [128, CAP]); b_ga = Buf()
        gh = sb("gh", [128, 4, CAP], BF16); b_gh = Buf()
        yo = [sb("yo%d" % i, [128, D]) for i in range(2)]; b_yo = [Buf(), Buf()]
        p_t = [ps("p_t%d" % i, [128, 1024], BF16) for i in range(2)]; b_pt = [Buf(), Buf()]
        p_a = ps("p_a", [128, 512]); p_b = ps("p_b", [128, 512]); b_pa, b_pb = Buf(), Buf()
        p_o = [ps("p_o%d" % i, [128, 512]) for i in range(2)]; b_po = [Buf(), Buf()]

        def load_w(e):
            j = e % 2
            mk.dma("pool", W1[j][:], w1[e].rearrange("(p k) c -> p k c", k=16), writes=[b_w[j]])
            mk.dma("pool", W3[j][:], w3[e].rearrange("(p k) c -> p k c", k=16), writes=[b_w[j]])
            mk.dma("pool", W2[j][:], w2[e].rearrange("(k p) c -> p k c", p=128), writes=[b_w[j]])

        xgl = [xg, sb("xg_b", [128, NTL, D], BF16)]; b_xgl = [b_xg, Buf()]
        xgTl = [xgT, sb("xgT_b", [128, 16, CAP], BF16)]; b_xgTl = [b_xgT, Buf()]

        def prep(e):
            q_ = e % 2
            xg_, bxg_, xgT_, bxgT_ = xgl[q_], b_xgl[q_], xgTl[q_], b_xgTl[q_]
            mk.dma("sp", xg_[:], Xg[e * CAP:(e + 1) * CAP, :].rearrange("(a p) c -> p a c", p=128), reads=[b_Xg], writes=[bxg_])
            yield
            for a in range(NTL):
                for half in range(2):
                    for kk in range(8):
                        k = half * 8 + kk
                        PE(lambda: nc.tensor.transpose(p_t[half][:, kk * 128:(kk + 1) * 128], xg_[:, a, :].rearrange("t (p k) -> t k p", k=16)[:, k, :], identb[:, :]), r=[bxg_, bc], w=[b_pt[half]])
                    if half == 0:
                        A(lambda: nc.scalar.copy(out=xgT_[:, 0:8, a * 128:(a + 1) * 128], in_=p_t[half][:, :].rearrange("p (k t) -> p k t", t=128)), r=[], w=[b_pt[half], bxgT_])
                    else:
                        V(lambda: nc.vector.tensor_copy(out=xgT_[:, 8:16, a * 128:(a + 1) * 128], in_=p_t[half][:, :].rearrange("p (k t) -> p k t", t=128)), r=[], w=[b_pt[half], bxgT_])
                    yield

        load_w(0)
        yoi = 0
        for _ in prep(0):
            pass
        for e in range(NE):
            j = e % 2
            if e + 1 < NE:
                load_w(e + 1)
            nx = prep(e + 1) if e + 1 < NE else None

            def tick():
                if nx is not None:
                    next(nx, None)
            xgT_ = xgTl[e % 2]; bxgT_ = b_xgTl[e % 2]
            for hc in range(4):
                for k in range(16):
                    PE(lambda: nc.tensor.matmul(p_a[:, 0:CAP], lhsT=W1[j][:, k, hc * 128:(hc + 1) * 128], rhs=xgT_[:, k, :], start=(k == 0), stop=(k == 15)), r=[b_w[j], bxgT_], w=[b_pa])
                for k in range(16):
                    PE(lambda: nc.tensor.matmul(p_b[:, 0:CAP], lhsT=W3[j][:, k, hc * 128:(hc + 1) * 128], rhs=xgT_[:, k, :], start=(k == 0), stop=(k == 15)), r=[b_w[j], bxgT_], w=[b_pb])
                tick()
                A(lambda: nc.scalar.activation(out=ga[:], in_=p_a[:, 0:CAP], func=AF.Silu), r=[], w=[b_pa, b_ga])
                V(lambda: nc.vector.tensor_tensor(out=gh[:, hc, :], in0=p_b[:, 0:CAP], in1=ga[:], op=ALU.mult), r=[b_ga], w=[b_pb, b_gh])
            for a in range(NTL):
                y_ = yoi % 2
                yoi += 1
                for cc in range(4):
                    q = cc % 2
                    for hc in range(4):
                        PE(lambda: nc.tensor.matmul(p_o[q][:, :], lhsT=gh[:, hc, a * 128:(a + 1) * 128], rhs=W2[j][:, hc, cc * 512:(cc + 1) * 512], start=(hc == 0), stop=(hc == 3)), r=[b_gh, b_w[j]], w=[b_po[q]])
                    if q == 0:
                        A(lambda: nc.scalar.copy(out=yo[y_][:, cc * 512:(cc + 1) * 512], in_=p_o[q][:, :]), r=[], w=[b_po[q], b_yo[y_]])
                    else:
                        V(lambda: nc.vector.tensor_copy(out=yo[y_][:, cc * 512:(cc + 1) * 512], in_=p_o[q][:, :]), r=[], w=[b_po[q], b_yo[y_]])
                r0 = e * CAP + a * 128
                mk.dma("sp", Yg[r0:r0 + 128, :], yo[y_][:], reads=[b_yo[y_]], writes=[b_Yg])
            if nx is not None:
                for _ in nx:
                    pass
        mk.barrier()
    with ExitStack() as pc:
        sb = lambda n, s, dt=F32: pc.enter_context(nc.sbuf_tensor("k3c%d_" % mk.gen + n, list(s), dt))
        R = [sb("row%d" % i, [128, D]) for i in range(3)]
        for i, ri in enumerate((5, 6, 7)):
            rowload(R[i], ri, bc)
        G(lambda: nc.gpsimd.tensor_scalar(out=R[0][:], in0=R[0][:], scalar1=1.0, scalar2=None, op0=ALU.add), r=[bc], w=[bc])
        Y = [[sb("Y%d_%d" % (k, i), [128, D]) for i in range(2)] for k in range(2)]
        b_Y = [[Buf(), Buf()], [Buf(), Buf()]]
        for k in range(2):
            for i in range(2):
                V(lambda: nc.vector.memset(Y[k][i][:], 0.0), w=[b_Y[k][i]])
        x1t = [sb("x1t%d" % i, [128, D]) for i in range(2)]; b_x1 = [Buf(), Buf()]
        yml = [sb("ym%d" % i, [128, D]) for i in range(2)]; b_yml = [Buf(), Buf()]
        xnl = [sb("xn%d" % i, [128, D]) for i in range(2)]; b_xnl = [Buf(), Buf()]
        ot = [sb("ot%d" % i, [128, D]) for i in range(2)]; b_ot = [Buf(), Buf()]
        stl = [sb("st%d" % i, [128, 4, 6]) for i in range(2)]; mvl = [sb("mv%d" % i, [128, 2]) for i in range(2)]
        rsl = [sb("rs%d" % i, [128, 1]) for i in range(2)]; nmrl = [sb("nmr%d" % i, [128, 1]) for i in range(2)]; b_sl = [Buf(), Buf()]
        for t in range(NT):
            i = t % 2
            ts_ = slice(t * 128, (t + 1) * 128)
            ym = yml[i]; b_ym = b_yml[i]; xn = xnl[i]; b_xn = b_xnl[i]
            st, mv, rs, nmr, b_s = stl[i], mvl[i], rsl[i], nmrl[i], b_sl[i]
            for k in range(2):
                mk.idma(Y[k][i][:, :], Yg, slot[:, t, k:k + 1], False, NE * CAP - 1, reads=[b_Yg, b_rt], writes=[b_Y[k][i]])
            mk.dma("sp", x1t[i][:], x1s[ts_, :], writes=[b_x1[i]])
            A(lambda: nc.scalar.activation(out=ym[:], in_=Y[0][i][:], func=AF.Copy, scale=gw[:, t, 0:1]), r=[b_Y[0][i], b_rt], w=[b_ym])
            V(lambda: nc.vector.scalar_tensor_tensor(out=ym[:], in0=Y[1][i][:], scalar=gw[:, t, 1:2], in1=ym[:], op0=ALU.mult, op1=ALU.add), r=[b_Y[1][i], b_rt], w=[b_ym])
            V(lambda: nc.vector.tensor_tensor(out=ym[:], in0=ym[:], in1=R[0][:], op=ALU.mult), r=[bc], w=[b_ym])
            V(lambda: nc.vector.scalar_tensor_tensor(out=ym[:], in0=x1t[i][:], scalar=ALPHA, in1=ym[:], op0=ALU.mult, op1=ALU.add), r=[b_x1[i]], w=[b_ym])
            ln_stats(nc, mk, ym, b_ym, st, mv, rs, nmr, b_s)
            A(lambda: nc.scalar.activation(out=xn[:], in_=ym[:], func=AF.Identity, bias=nmr[:, 0:1], scale=rs[:, 0:1]), r=[b_ym, b_s], w=[b_xn])
            V(lambda: nc.vector.tensor_tensor(out=xn[:], in0=xn[:], in1=R[1][:], op=ALU.mult), r=[bc], w=[b_xn])
            V(lambda: nc.vector.tensor_tensor(out=ot[i][:], in0=xn[:], in1=R[2][:], op=ALU.add), r=[b_xn, bc], w=[b_ot[i]])
            mk.dma("sp", xout[ts_, :], ot[i][:], reads=[b_ot[i]], is_output=True)
        mk.barrier()


def build_k3():
    nc = bass.Bass("TRN2", target_bir_lowering=False)
    dt = lambda n, s, k="ExternalInput", d=F32: nc.dram_tensor(n, s, d, kind=k).ap()
    ymixT = dt("ymixT", [D, TOK]); x = dt("x", [TOK, D]); w_out = dt("w_out", [D, D]); glu_w = dt("glu_w", [512, 512]); glu_b = dt("glu_b", [128, 4])
    rows = dt("rows", [8, 128, D]); wr = dt("wr", [D, 36]); br = dt("br", [128, 36])
    w1 = dt("w1", [NE, D, DE]); w3 = dt("w3", [NE, D, DE]); w2 = dt("w2", [NE, DE, D])
    cst = {"U": dt("U", [128, 128]), "ecap": dt("ecap", [128, NE]), "ident": dt("ident", [128, 128])}
    x1s = dt("x1s", [TOK, D], "Internal"); Xg = dt("Xg", [NE * CAP, D], "Internal", BF16); Yg = dt("Yg", [NE * CAP, D], "Internal")
    xout = dt("xout", [TOK, D], "ExternalOutput")
    with ExitStack() as ctx:
        mk = MK(nc, ctx)
        emit_k3(nc, mk, ymixT, x, w_out, glu_w, glu_b, rows, wr, br, w1, w3, w2, cst, x1s, Xg, Yg, xout)
        mk.finish("sp")
        print("k3 ops", mk.nops, "waits", mk.nwaits)
    return nc


def k3_host_inputs(prm, mod, l, b):
    rep = lambda v: np.ascontiguousarray(np.broadcast_to(v[None, :], (128, v.shape[0])))
    sh1, sc1, gt1, sh2, sc2, gt2 = [mod[l, b, i * D:(i + 1) * D] for i in range(6)]
    rows = np.stack([rep(gt1), rep(prm["ln_g"][l, 0]), rep(prm["ln_b"][l, 0]), rep(sc2), rep(sh2), rep(gt2), rep(prm["ln_g"][l, 1]), rep(prm["ln_b"][l, 1])])
    wr = np.ascontiguousarray(np.concatenate([prm["router_group_w"][l], prm["router_expert_w"][l]], axis=1))
    br = rep(np.concatenate([prm["router_group_b"][l], prm["router_expert_b"][l]]))
    d = {"rows": rows, "wr": wr, "br": br, "w_out": prm["w_out"][l], "glu_w": prm["s5_glu_w"][l],
         "glu_b": np.ascontiguousarray(prm["s5_glu_b"][l].reshape(4, 128).T),
         "w1": prm["moe_w1"][l], "w3": prm["moe_w3"][l], "w2": prm["moe_w2"][l]}
    d.update(k3_consts())
    return d


G_ = 512
RW_OFF = 3 * G_
RW_COLS = 3 * G_ + 96 + 96 + 128
ATT_OFF = RW_OFF + RW_COLS
S5_OFF = ATT_OFF + 512 + 2 * 128
SEQ = 8192
RG4 = [[0, 1, 2, 3], [4, 5, 6, 7]]
NMINE = 1472
MODC = 3072


def emit_k0f(nc, mk, cT, w, bb, modin):
    ct = mk.sb("k0_ct", [128, 16, 2]); sct = mk.sb("k0_sct", [128, 16, 2])
    wt = [mk.sb("k0_wt%d" % i, [128, 16, 512]) for i in range(2)]
    bt = mk.sb("k0_bt", [2, 2, MODC]); ot = mk.sb("k0_ot", [2, 2, MODC])
    P = [mk.ps("k0_P%d" % i, [2, 512]) for i in range(2)]
    b_c, b_b, b_o = Buf(), Buf(), Buf()
    b_w = [Buf(), Buf()]; b_p = [Buf(), Buf()]
    mk.dma("sp", ct[:], cT, writes=[b_c])
    mk.dma("sp", bt[:], bb.rearrange("l b c -> b l c"), writes=[b_b])
    mk.op("act", lambda: nc.scalar.activation(out=sct[:], in_=ct[:], func=AF.Silu), reads=[b_c], writes=[b_c])
    it = 0
    for l in range(2):
        for n in range(MODC // 512):
            i = it % 2
            it += 1
            mk.dma("sp", wt[i][:], w[l, :, n * 512:(n + 1) * 512].rearrange("(k p) c -> p k c", p=128), writes=[b_w[i]])
            for k in range(16):
                mk.op("pe", lambda: nc.tensor.matmul(P[i][:], lhsT=sct[:, k, :], rhs=wt[i][:, k, :], start=(k == 0), stop=(k == 15)),
                      reads=[b_c, b_w[i]], writes=[b_p[i]], skip_same=True)
            mk.op("dve", lambda: nc.vector.tensor_tensor(out=ot[:, l, n * 512:(n + 1) * 512], in0=P[i][:], in1=bt[:, l, n * 512:(n + 1) * 512], op=ALU.add),
                  reads=[b_b], writes=[b_p[i], b_o])
    mk.dma("sp", modin.rearrange("(o l) c -> o l c", o=1), ot[0:1, :, :], reads=[b_o])


def mod_row_load(nc, mk, dst, modall, l, chunk, bc):
    c0 = chunk * 2048
    done = 0
    while done < 2048:
        col = c0 + done
        r = col // MODC
        off = col % MODC
        n = min(2048 - done, MODC - off)
        src = modall[r * 2 + l:r * 2 + l + 1, off:off + n].partition_broadcast(128)
        mk.dma("sp", dst[:, done:done + n], src, writes=[bc])
        done += n


def emit_k1a(nc, mk, x, modall, l, ident_d, hTs):
    NT_ = TOK // 128
    xt = [mk.sb("a_xt%d" % i, [128, D]) for i in range(2)]
    xn = mk.sb("a_xn", [128, D]); h1 = mk.sb("a_h1", [128, D])
    hb = [mk.sb("a_hb%d" % i, [128, D], BF16) for i in range(2)]
    hT = mk.sb("a_hT", [128, 16, TOK], BF16)
    sct = mk.sb("a_sct", [128, D]); sht = mk.sb("a_sht", [128, D])
    idf = mk.sb("a_idf", [128, 128]); idb = mk.sb("a_idb", [128, 128], BF16)
    st = mk.sb("a_st", [128, 4, 6]); mv = mk.sb("a_mv", [128, 2]); rs = mk.sb("a_rs", [128, 1]); nmr = mk.sb("a_nmr", [128, 1])
    PT = [mk.ps("a_PT%d" % i, [128, 8, 128], BF16) for i in range(2)]
    b_x = [Buf(), Buf()]
    b_xn, b_h1, b_s, b_sc, b_sh, b_id, b_hT = Buf(), Buf(), Buf(), Buf(), Buf(), Buf(), Buf()
    b_hb = [Buf(), Buf()]; b_pt = [Buf(), Buf()]
    mod_row_load(nc, mk, sct, modall, l, 1, b_sc)
    mod_row_load(nc, mk, sht, modall, l, 0, b_sh)
    mk.dma("sp", idf[:], ident_d, writes=[b_id])
    mk.op("dve", lambda: nc.vector.tensor_copy(out=idb[:], in_=idf[:]), reads=[b_id], writes=[b_id])
    mk.op("pool", lambda: nc.gpsimd.tensor_scalar(out=sct[:], in0=sct[:], scalar1=1.0, scalar2=None, op0=ALU.add), reads=[b_sc], writes=[b_sc])
    for t in range(NT_):
        i = t % 2
        mk.dma("sp", xt[i][:], x[t * 128:(t + 1) * 128, :], writes=[b_x[i]])
        ln_stats(nc, mk, xt[i], b_x[i], st, mv, rs, nmr, b_s)
        mk.op("act", lambda: nc.scalar.activation(out=xn[:], in_=xt[i][:], func=AF.Identity, bias=nmr[:, 0:1], scale=rs[:, 0:1]),
              reads=[b_x[i], b_s], writes=[b_xn])
        mk.op("dve", lambda: nc.vector.tensor_tensor(out=h1[:], in0=xn[:], in1=sct[:], op=ALU.mult), reads=[b_xn, b_sc], writes=[b_h1])
        mk.op("dve", lambda: nc.vector.tensor_tensor(out=hb[i][:], in0=h1[:], in1=sht[:], op=ALU.add), reads=[b_h1, b_sh], writes=[b_hb[i]])
        for half in range(2):
            for kk in range(8):
                k = half * 8 + kk
                mk.op("pe", lambda: nc.tensor.transpose(PT[half][:, kk, :], hb[i][:, k * 128:(k + 1) * 128], idb[:]),
                      reads=[b_hb[i], b_id], writes=[b_pt[half]], skip_same=True)
            if half == 0:
                mk.op("act", lambda: nc.scalar.copy(out=hT[:, 0:8, t * 128:(t + 1) * 128], in_=PT[half][:]), reads=[], writes=[b_pt[half], b_hT])
            else:
                mk.op("dve", lambda: nc.vector.tensor_copy(out=hT[:, 8:16, t * 128:(t + 1) * 128], in_=PT[half][:]), reads=[], writes=[b_pt[half], b_hT])
    mk.dma("sp", hTs.rearrange("(k p) t -> p k t", p=128), hT[:], reads=[b_hT])


def emit_k1b(nc, mk, hTg, wmine, pmine):
    NB_ = (NMINE + 127) // 128
    wt = mk.sb("b_wt", [128, 16, NB_ * 128], BF16); b_w = Buf()
    ht = [mk.sb("b_ht%d" % i, [128, 16, 512], BF16) for i in range(2)]; b_h = [Buf(), Buf()]
    ot = [mk.sb("b_ot%d" % i, [128, 512]) for i in range(4)]; b_o = [Buf() for _ in range(4)]
    PM = [mk.ps("b_PM%d" % i, [128, 512]) for i in range(4)]; b_pm = [Buf() for _ in range(4)]
    for j in range(NB_):
        c0 = j * 128
        cw = min(128, NMINE - c0)
        mk.dma("pool", wt[:, :, c0:c0 + cw], wmine[:, c0:c0 + cw].rearrange("(k p) c -> p k c", p=128), writes=[b_w])
    pi = 0
    for tc in range(SEQ // 512):
        i = tc % 2
        r = tc // 4
        t0 = (tc % 4) * 512
        src = hTg.rearrange("(c r h p) t -> r p c h t", c=8, r=4, h=2, p=128)[r]
        for c in range(8):
            mk.dma("sp", ht[i][:, 2 * c:2 * c + 2, :], src[:, c, :, t0:t0 + 512], writes=[b_h[i]])
        for j in range(NB_):
            c0 = j * 128
            cw = min(128, NMINE - c0)
            q = pi % 4
            pi += 1
            for k in range(16):
                mk.op("pe", lambda: nc.tensor.matmul(PM[q][0:cw, :], lhsT=wt[:, k, c0:c0 + cw], rhs=ht[i][:, k, :], start=(k == 0), stop=(k == 15)),
                      reads=[b_w, b_h[i]], writes=[b_pm[q]], skip_same=True)
            if q % 2 == 0:
                mk.op("act", lambda: nc.scalar.copy(out=ot[q][0:cw, :], in_=PM[q][0:cw, :]), reads=[], writes=[b_pm[q], b_o[q]])
            else:
                mk.op("dve", lambda: nc.vector.tensor_copy(out=ot[q][0:cw, :], in_=PM[q][0:cw, :]), reads=[], writes=[b_pm[q], b_o[q]])
            mk.dma("sp", pmine[c0:c0 + cw, tc * 512:(tc + 1) * 512], ot[q][0:cw, :], reads=[b_o[q]])


def build_fused():
    nc = bass.Bass("TRN2", target_bir_lowering=False)
    T = SEQ
    din = lambda n, s, d=F32: nc.dram_tensor(n, s, d, kind="ExternalInput").ap()
    scr = lambda n, s, d=F32: nc.dram_tensor(n, s, d).ap()
    x_in = din("x", [TOK, D]); cT = din("cT", [128, 16, 2]); w_ada = din("w_ada", [2, D, MODC]); bb = din("bb", [2, 2, MODC])
    wmine = din("wmine", [2, D, NMINE]); w_out = din("w_out", [2, D, D]); lnp = din("lnp", [2, 4, D])
    cw = din("cw", [2, 128, 3]); btab = din("btab", [2, 2, 128, 2, 256]); sinkt = din("sinkt", [2, 128, 2]); ident = din("ident", [128, 128])
    s5par = din("s5par", [2, 128, 4, 3]); s5bb = din("s5bb", [2, 128, 4, 2, 16]); s5cc = din("s5cc", [2, 128, 4, 2, 16]); s5d = din("s5d", [2, 128, 1]); s5iota = din("s5iota", [128, CS])
    par64 = din("par64", [2, 64, 2, 11]); par128 = din("par128", [2, 128, 3]); w2 = din("w2", [2, 96, 128]); a2 = din("a2", [2, 96, 128]); g2 = din("g2", [2, 128, 128]); gnt = din("gnt", [2, 64, 2, 2, 64])
    cst = {"mask1": din("mask1", [64, 512]), "mask3": din("mask3", [64, 256]), "seg": din("seg", [128, TB]), "ident": ident, "U": din("U", [128, 128]), "ecap": din("ecap", [128, NE])}
    glu_w = din("glu_w", [2, 512, 512]); glu_b = din("glu_b", [2, 128, 4]); wr = din("wr", [2, D, 36]); br = din("br", [2, 128, 36])
    w1 = din("w1", [2, NE, D, DE]); w3 = din("w3", [2, NE, D, DE]); w2m = din("w2m", [2, NE, DE, D])
    ymidx_d = din("ymidx", [128, 16, 2], I32)
    y_out = nc.dram_tensor("y", [TOK, D], F32, kind="ExternalOutput").ap()
    modin = scr("modin", [2, MODC]); modall = scr("modall", [8, MODC])
    hTs = scr("hTs", [D, TOK], BF16); hTg = scr("hTg", [4 * D, TOK], BF16)
    pmine = scr("pmine", [NMINE, T])
    yT16 = scr("yT16", [512, T], BF16); ymg = scr("ymg", [2048, T], BF16)
    x1s = scr("x1s", [TOK, D]); Xg = scr("Xg", [NE * CAP, D], BF16); Yg = scr("Yg", [NE * CAP, D]); xcur = scr("xcur", [TOK, D])
    ymg_rows = ymg.rearrange("r (tb t) -> (r tb) t", t=1024)
    with ExitStack() as ctx:
        mk = MK(nc, ctx)
        bD = Buf("dram")
        with mk.scope():
            emit_k0f(nc, mk, cT, w_ada, bb, modin)
        mk.collective("AllGather", RG4, modin, modall, reads=[bD], writes=[bD])
        mk.barrier()
        for l in range(2):
            xsrc = x_in if l == 0 else xcur
            xdst = xcur if l == 0 else y_out
            with mk.scope():
                emit_k1a(nc, mk, xsrc, modall, l, ident, hTs)
            for c in range(8):
                mk.collective("AllGather", RG4, hTs[c * 256:(c + 1) * 256, :], hTg[c * 1024:(c + 1) * 1024, :], reads=[bD], writes=[bD])
            mk.barrier()
            with mk.scope():
                emit_k1b(nc, mk, hTg, wmine[l], pmine)
            def ag(chunks):
                for c in chunks:
                    mk.collective("AllGather", RG4, yT16[c * 64:(c + 1) * 64, :], ymg[c * 256:(c + 1) * 256, :], reads=[], writes=[Buf()])
            with mk.scope():
                emit_conv(nc, mk, T, pmine[0:384, :], cw[l], yT16[0:128, :], odt=BF16)
            ag((0, 1))
            with mk.scope():
                emit_attn(nc, mk, T, pmine[1088:1344, :], btab[l], sinkt[l], ident, yT16[256:384, :], odt=BF16)
            ag((4, 5))
            with mk.scope():
                emit_s5(nc, mk, T, pmine[1344:1472, :], s5par[l], s5bb[l], s5cc[l], s5d[l], s5iota, ident, yT16[384:512, :], odt=BF16)
            ag((6, 7))
            with mk.scope():
                emit_rwkv(nc, mk, T, pmine[384:1088, :], par64[l], par128[l], w2[l], a2[l], g2[l], gnt[l], cst, yT16[128:256, :], odt=BF16)
            for c in (2, 3):
                mk.collective("AllGather", RG4, yT16[c * 64:(c + 1) * 64, :], ymg[c * 256:(c + 1) * 256, :], reads=[bD], writes=[bD])
            mk.barrier()
            with mk.scope():
                ymidx = mk.sb("ymidx_sb", [128, 16, 2], I32); b_idx = Buf()
                mk.dma("sp", ymidx[:], ymidx_d, writes=[b_idx])

                def ymload(hf, ymt, b_ymt):
                    for k in range(16):
                        mk.idma(ymt[:, k, :], ymg_rows, ymidx[:, k, hf:hf + 1], False, 2048 * 8 - 1, reads=[b_idx], writes=[b_ymt])

                def rowload(dst, ri, bcx, _l=l):
                    if ri in (1, 2, 6, 7):
                        j = {1: 0, 2: 1, 6: 2, 7: 3}[ri]
                        mk.dma("sp", dst[:], lnp[_l, j:j + 1, :].partition_broadcast(128), writes=[bcx])
                    else:
                        chunk = {0: 2, 3: 4, 4: 3, 5: 5}[ri]
                        mod_row_load(nc, mk, dst, modall, _l, chunk, bcx)

                emit_k3(nc, mk, None, xsrc, w_out[l], glu_w[l], glu_b[l], None, wr[l], br[l], w1[l], w3[l], w2m[l], cst, x1s, Xg, Yg, xdst,
                        ymload=ymload, rowload=rowload)
        mk.finish("sp")
        mk.barrier()
        print("fused ops", mk.nops, "waits", mk.nwaits)
    return nc


_NC_CACHE = {}


def _get(name, fn):
    if name not in _NC_CACHE:
        _NC_CACHE[name] = fn()
    return _NC_CACHE[name]


def _fused_inputs(prm, core):
    b, q = core // 4, core % 4
    eye = np.eye(128, dtype=np.float32)
    d = {}
    d["x"] = np.ascontiguousarray(prm["x"][b, q * TOK:(q + 1) * TOK])
    cb = prm["c"][b]
    d["cT"] = np.ascontiguousarray(np.stack([cb.reshape(16, 128).T, cb.reshape(16, 128).T], axis=-1))
    sl = slice(q * MODC, (q + 1) * MODC)
    d["w_ada"] = np.ascontiguousarray(prm["w_ada"][:, :, sl])
    d["bb"] = np.ascontiguousarray(np.broadcast_to(prm["b_ada"][:, None, sl], (2, 2, MODC)))
    kv = q // 2
    cols = np.concatenate([np.arange(128 * q, 128 * q + 128), np.arange(G_ + 128 * q, G_ + 128 * q + 128), np.arange(2 * G_ + 128 * q, 2 * G_ + 128 * q + 128),
                           RW_OFF + rwkv_rows(q),
                           np.arange(ATT_OFF + 128 * q, ATT_OFF + 128 * q + 128), np.arange(ATT_OFF + 512 + 64 * kv, ATT_OFF + 512 + 64 * kv + 64),
                           np.arange(ATT_OFF + 640 + 64 * kv, ATT_OFF + 640 + 64 * kv + 64),
                           np.arange(S5_OFF + 128 * q, S5_OFF + 128 * q + 128)])
    assert cols.shape[0] == NMINE
    d["wmine"] = np.ascontiguousarray(prm["w_in"][:, :, cols])
    d["w_out"] = prm["w_out"]
    d["lnp"] = np.ascontiguousarray(np.stack([np.stack([prm["ln_g"][l, 0], prm["ln_b"][l, 0], prm["ln_g"][l, 1], prm["ln_b"][l, 1]]) for l in range(2)]))
    d["cw"] = np.ascontiguousarray(np.stack([prm["conv_w"][l][:, 128 * q:128 * q + 128].T for l in range(2)]))
    tabs = [attn_tables(prm["rel_bias"], prm["attn_sinks"][l], q) for l in range(2)]
    d["btab"] = np.ascontiguousarray(np.stack([t[0] for t in tabs])); d["sinkt"] = np.ascontiguousarray(np.stack([t[1] for t in tabs]))
    d["ident"] = eye
    s5 = [s5_host_inputs(prm, l, q) for l in range(2)]
    for k_ in ("s5par", "s5bb", "s5cc", "s5d"):
        d[k_] = np.ascontiguousarray(np.stack([s[k_] for s in s5]))
    d["s5iota"] = s5[0]["s5iota"]
    rw = [rwkv_host_inputs(prm, l, q) for l in range(2)]
    for k_ in ("par64", "par128", "gnt", "w2", "a2", "g2"):
        d[k_] = np.ascontiguousarray(np.stack([r[k_] for r in rw]))
    for k_ in ("mask1", "mask3", "seg"):
        d[k_] = rw[0][k_]
    kc = k3_consts()
    d["U"] = kc["U"]; d["ecap"] = kc["ecap"]
    d["glu_w"] = prm["s5_glu_w"]
    d["glu_b"] = np.ascontiguousarray(np.stack([prm["s5_glu_b"][l].reshape(4, 128).T for l in range(2)]))
    d["wr"] = np.ascontiguousarray(np.stack([np.concatenate([prm["router_group_w"][l], prm["router_expert_w"][l]], axis=1) for l in range(2)]))
    d["br"] = np.ascontiguousarray(np.stack([np.broadcast_to(np.concatenate([prm["router_group_b"][l], prm["router_expert_b"][l]])[None, :], (128, 36)) for l in range(2)]))
    d["w1"] = prm["moe_w1"]; d["w3"] = prm["moe_w3"]; d["w2m"] = prm["moe_w2"]
    p_ = np.arange(128)[:, None, None]; k_i = np.arange(16)[None, :, None]; t_ = np.arange(2)[None, None, :]
    src_row = ((k_i // 4) * 2 + p_ // 64) * 256 + (k_i % 4) * 64 + p_ % 64
    d["ymidx"] = np.ascontiguousarray((src_row * 8 + q * 2 + t_).astype(np.int32))
    return d


def kernel(**inp):
    prm = {k: np.ascontiguousarray(np.asarray(v, dtype=np.float32)) for k, v in inp.items()}
    cores = list(range(8))
    in_maps = [_fused_inputs(prm, c) for c in cores]
    res = run_bass_kernel_spmd(_get("fused", build_fused), in_maps, core_ids=cores)
    out = np.stack([np.concatenate([res.results[b * 4 + q]["y"] for q in range(4)], axis=0) for b in range(2)])
    return out.astype(np.float32)
```

```python
import numpy as np
from contextlib import ExitStack
import concourse.bass as bass
import concourse.mybir as mybir
from concourse.bass_utils import run_bass_kernel_spmd

F32 = mybir.dt.float32
BF16 = mybir.dt.bfloat16
I32 = mybir.dt.int32
U32 = mybir.dt.uint32
AF = mybir.ActivationFunctionType
ALU = mybir.AluOpType
AX = mybir.AxisListType

EPOCH = 1 << 20


class Buf:
    __slots__ = ("name", "w", "r")

    def __init__(self, name=""):
        self.name = name
        self.w = None
        self.r = {}


class MK:
    def __init__(self, nc, ctx, n_dma_sems=24):
        self.nc = nc
        self.ctx = ctx
        self.eng = {"pe": nc.tensor, "dve": nc.vector, "act": nc.scalar,
                    "pool": nc.gpsimd, "sp": nc.sync}
        self.sem = {}
        self.cnt = {e: 0 for e in self.eng}
        self.known = {e: {} for e in self.eng}
        for e in self.eng:
            self.sem[e] = ctx.enter_context(nc.semaphore("s_" + e))
        self.dma_keys = []
        self.dma_val = {}
        for i in range(n_dma_sems):
            k = ("dma", i)
            self.sem[k] = ctx.enter_context(nc.semaphore("s_dma%d" % i))
            self.dma_keys.append(k)
            self.dma_val[k] = 0
        self.dma_rr = 0
        self.nwaits = 0
        self.nops = 0
        self.out_events = []

    gen = 0

    def sb(self, name, shape, dt=F32):
        return self.ctx.enter_context(self.nc.sbuf_tensor("%s_g%d" % (name, self.gen), list(shape), dt))

    def ps(self, name, shape, dt=F32):
        return self.ctx.enter_context(self.nc.psum_tensor("%s_g%d" % (name, self.gen), list(shape), dt))

    def _wait(self, E, ev):
        if ev is None:
            return
        key, val = ev
        if self.known[E].get(key, 0) >= val:
            return
        self.eng[E].wait_ge(self.sem[key], val)
        self.known[E][key] = val
        self.nwaits += 1

    def _deps(self, E, reads, writes, skip_same=False):
        for b in reads:
            if b.w is not None and not (skip_same and b.w[0] == E):
                self._wait(E, b.w)
        for b in writes:
            if b.w is not None and not (skip_same and b.w[0] == E):
                self._wait(E, b.w)
            for ev in b.r.values():
                if not (skip_same and ev[0] == E):
                    self._wait(E, ev)

    def _mark(self, ev, reads, writes):
        for b in reads:
            b.r[ev[0]] = ev
        for b in writes:
            b.w = ev
            b.r = {}

    def op(self, E, fn, reads=(), writes=(), skip_same=False):
        self._deps(E, reads, writes, skip_same)
        inst = fn()
        self.cnt[E] += 1
        inst.then_inc(self.sem[E], 1)
        ev = (E, self.cnt[E])
        self._mark(ev, reads, writes)
        self.nops += 1
        return ev

    def dma(self, Q, out, in_, reads=(), writes=(), is_output=False, **kw):
        self._deps(Q, reads, writes)
        k = self.dma_keys[self.dma_rr]
        self.dma_rr = (self.dma_rr + 1) % len(self.dma_keys)
        self._wait(Q, (k, self.dma_val[k]) if self.dma_val[k] else None)
        self.dma_val[k] += 16
        inst = self.eng[Q].dma_start(out=out, in_=in_, **kw)
        inst.then_inc(self.sem[k], 16)
        ev = (k, self.dma_val[k])
        self._mark(ev, reads, writes)
        if is_output:
            self.out_events.append(ev)
        self.nops += 1
        return ev

    def finish(self, E="sp"):
        for k in self.dma_keys:
            if self.dma_val[k]:
                self._wait(E, (k, self.dma_val[k]))


def _idma(self, out, in_, idx_ap, scatter, bound, reads=(), writes=(), is_output=False):
    Q = "pool"
    self._deps(Q, reads, writes)
    k = self.dma_keys[self.dma_rr]
    self.dma_rr = (self.dma_rr + 1) % len(self.dma_keys)
    self._wait(Q, (k, self.dma_val[k]) if self.dma_val[k] else None)
    self.dma_val[k] += 16
    off = bass.IndirectOffsetOnAxis(ap=idx_ap, axis=0)
    if not hasattr(self, "_bregs"):
        self._bregs = {}
    if bound not in self._bregs:
        self._bregs[bound] = self.nc.gpsimd.to_reg(bound)
    bound = self._bregs[bound]
    if scatter:
        inst = self.nc.gpsimd.indirect_dma_start(out=out, out_offset=off, in_=in_, in_offset=None, bounds_check=bound, oob_is_err=False)
    else:
        inst = self.nc.gpsimd.indirect_dma_start(out=out, out_offset=None, in_=in_, in_offset=off, bounds_check=bound, oob_is_err=False)
    inst.then_inc(self.sem[k], 16)
    ev = (k, self.dma_val[k])
    self._mark(ev, reads, writes)
    if is_output:
        self.out_events.append(ev)
    self.nops += 1
    return ev


MK.idma = _idma


def _barrier(self):
    for E in self.eng:
        for F in self.eng:
            if self.cnt[F]:
                self._wait(E, (F, self.cnt[F]))
        for k in self.dma_keys:
            if self.dma_val[k]:
                self._wait(E, (k, self.dma_val[k]))
        if getattr(self, "cc_val", 0):
            self._wait(E, ("cc", self.cc_val))


MK.barrier = _barrier


from contextlib import contextmanager


@contextmanager
def _scope(self):
    old = self.ctx
    self.gen += 1
    with ExitStack() as s:
        self.ctx = s
        yield
        self.barrier()
    self.ctx = old


MK.scope = _scope


def _collective(self, kind, rg, in_ap, out_ap, reads=(), writes=()):
    Q = "pool"
    if "cc" not in self.sem:
        self.sem["cc"] = self.ctx.enter_context(self.nc.semaphore("s_cc"))
        self.cc_val = 0
    self._deps(Q, reads, writes)
    self.cc_val += 1
    inst = self.nc.gpsimd.collective_compute(kind, ALU.bypass, replica_groups=rg, ins=[in_ap.opt()], outs=[out_ap.opt()])
    inst.then_inc(self.sem["cc"], 1)
    ev = ("cc", self.cc_val)
    self._mark(ev, reads, writes)
    self.nops += 1
    return ev


MK.collective = _collective


import numpy as np
from contextlib import ExitStack

D = 2048
NIN = 4672
TOK = 2048


def build_k0():
    nc = bass.Bass("TRN2", target_bir_lowering=False)
    NCOL = 1536
    cT = nc.dram_tensor("cT", [128, 16, 2], F32, kind="ExternalInput").ap()
    w = nc.dram_tensor("w", [2, D, NCOL], F32, kind="ExternalInput").ap()
    bb = nc.dram_tensor("bb", [2, 2, NCOL], F32, kind="ExternalInput").ap()
    out = nc.dram_tensor("mod", [2, 2, NCOL], F32, kind="ExternalOutput").ap()
    with ExitStack() as ctx:
        mk = MK(nc, ctx)
        ct = mk.sb("ct", [128, 16, 2])
        sct = mk.sb("sct", [128, 16, 2])
        wt = [mk.sb("wt%d" % i, [128, 16, 512]) for i in range(2)]
        bt = mk.sb("bt", [2, 2, NCOL])
        ot = mk.sb("ot", [2, 2, NCOL])
        P = [mk.ps("P%d" % i, [2, 512]) for i in range(2)]
        b_c, b_b, b_o = Buf(), Buf(), Buf()
        b_w = [Buf(), Buf()]
        b_p = [Buf(), Buf()]
        mk.dma("sp", ct[:], cT, writes=[b_c])
        mk.dma("sp", bt[:], bb.rearrange("l b c -> b l c"), writes=[b_b])
        mk.op("act", lambda: nc.scalar.activation(out=sct[:], in_=ct[:], func=AF.Silu), reads=[b_c], writes=[b_c])
        it = 0
        for l in range(2):
            for n in range(3):
                i = it % 2
                it += 1
                mk.dma("sp", wt[i][:], w[l, :, n * 512:(n + 1) * 512].rearrange("(k p) c -> p k c", p=128), writes=[b_w[i]])
                for k in range(16):
                    mk.op("pe", lambda: nc.tensor.matmul(P[i][:], lhsT=sct[:, k, :], rhs=wt[i][:, k, :], start=(k == 0), stop=(k == 15)),
                          reads=[b_c, b_w[i]], writes=[b_p[i]], skip_same=True)
                mk.op("dve", lambda: nc.vector.tensor_tensor(out=ot[:, l, n * 512:(n + 1) * 512], in0=P[i][:], in1=bt[:, l, n * 512:(n + 1) * 512], op=ALU.add),
                      reads=[b_p[i], b_b], writes=[b_o])
        mk.dma("sp", out.rearrange("l b c -> b l c"), ot[:], reads=[b_o], is_output=True)
        mk.finish("sp")
    return nc


def build_k1():
    nc = bass.Bass("TRN2", target_bir_lowering=False)
    x = nc.dram_tensor("x", [TOK, D], F32, kind="ExternalInput").ap()
    sc = nc.dram_tensor("sc", [128, D], F32, kind="ExternalInput").ap()
    sh = nc.dram_tensor("sh", [128, D], F32, kind="ExternalInput").ap()
    w_in = nc.dram_tensor("w_in", [D, NIN], F32, kind="ExternalInput").ap()
    ident = nc.dram_tensor("ident", [128, 128], F32, kind="ExternalInput").ap()
    pT = nc.dram_tensor("pT", [NIN, TOK], F32, kind="ExternalOutput").ap()
    with ExitStack() as ctx:
        mk = MK(nc, ctx)
        emit_k1(nc, mk, x, sc, sh, w_in, ident, pT)
        mk.finish("sp")
        print("k1 ops", mk.nops, "waits", mk.nwaits)
    return nc


def ln_stats(nc, mk, xt, bx, st, mv, rs, nmr, bs, eps=1e-5):
    for c in range(4):
        mk.op("dve", lambda: nc.vector.bn_stats(out=st[:, c, :], in_=xt[:, c * 512:(c + 1) * 512]), reads=[bx], writes=[bs])
    mk.op("dve", lambda: nc.vector.bn_aggr(out=mv[:], in_=st[:].rearrange("p a b -> p (a b)")), reads=[bs], writes=[bs])
    mk.op("act", lambda: nc.scalar.activation(out=rs[:], in_=mv[:, 1:2], func=AF.Sqrt, bias=eps, scale=1.0), reads=[bs], writes=[bs])
    mk.op("dve", lambda: nc.vector.reciprocal(out=rs[:], in_=rs[:]), reads=[bs], writes=[bs])
    mk.op("dve", lambda: nc.vector.tensor_scalar(out=nmr[:], in0=mv[:, 0:1], scalar1=rs[:, 0:1], scalar2=-1.0, op0=ALU.mult, op1=ALU.mult),
          reads=[bs], writes=[bs])


def emit_k1(nc, mk, x, sc, sh, w_in, ident, pT):
    NT = TOK // 128
    xt = [mk.sb("xt%d" % i, [128, D]) for i in range(2)]
    xn = mk.sb("xn", [128, D])
    h1 = mk.sb("h1", [128, D])
    hb = [mk.sb("hb%d" % i, [128, D], BF16) for i in range(2)]
    hT = mk.sb("hT", [128, 16, TOK], BF16)
    sct = mk.sb("sct", [128, D])
    sht = mk.sb("sht", [128, D])
    idf = mk.sb("idf", [128, 128])
    idb = mk.sb("idb", [128, 128], BF16)
    st = mk.sb("st", [128, 4, 6])
    mv = mk.sb("mv", [128, 2])
    rs = mk.sb("rs", [128, 1])
    nmr = mk.sb("nmr", [128, 1])
    wt = [mk.sb("wt%d" % i, [128, 16, 128], BF16) for i in range(2)]
    ot = [mk.sb("ot%d" % i, [128, TOK]) for i in range(2)]
    PT = [mk.ps("PT%d" % i, [128, 8, 128], BF16) for i in range(2)]
    PM = [mk.ps("PM%d" % i, [128, 512]) for i in range(4)]
    b_x = [Buf(), Buf()]
    b_xn, b_h1, b_s, b_sc, b_sh, b_id, b_hT = Buf(), Buf(), Buf(), Buf(), Buf(), Buf(), Buf()
    b_hb = [Buf(), Buf()]
    b_pt = [Buf(), Buf()]
    b_pm = [Buf() for _ in range(4)]
    b_w = [Buf(), Buf()]
    b_o = [Buf(), Buf()]

    mk.dma("sp", sct[:], sc, writes=[b_sc])
    mk.dma("sp", sht[:], sh, writes=[b_sh])
    mk.dma("sp", idf[:], ident, writes=[b_id])
    mk.op("dve", lambda: nc.vector.tensor_copy(out=idb[:], in_=idf[:]), reads=[b_id], writes=[b_id])
    mk.op("pool", lambda: nc.gpsimd.tensor_scalar(out=sct[:], in0=sct[:], scalar1=1.0, scalar2=None, op0=ALU.add), reads=[b_sc], writes=[b_sc])

    NCB = (NIN + 127) // 128

    def load_w(cb):
        j = cb % 2
        c0 = cb * 128
        cw = min(128, NIN - c0)
        mk.dma("pool", wt[j][:, :, 0:cw], w_in[:, c0:c0 + cw].rearrange("(k p) c -> p k c", p=128), writes=[b_w[j]])

    load_w(0)
    load_w(1)
    for t in range(NT):
        i = t % 2
        mk.dma("sp", xt[i][:], x[t * 128:(t + 1) * 128, :], writes=[b_x[i]])
        ln_stats(nc, mk, xt[i], b_x[i], st, mv, rs, nmr, b_s)
        mk.op("act", lambda: nc.scalar.activation(out=xn[:], in_=xt[i][:], func=AF.Identity, bias=nmr[:, 0:1], scale=rs[:, 0:1]),
              reads=[b_x[i], b_s], writes=[b_xn])
        mk.op("dve", lambda: nc.vector.tensor_tensor(out=h1[:], in0=xn[:], in1=sct[:], op=ALU.mult), reads=[b_xn, b_sc], writes=[b_h1])
        mk.op("pool", lambda: nc.gpsimd.tensor_tensor(out=hb[i][:], in0=h1[:], in1=sht[:], op=ALU.add), reads=[b_h1, b_sh], writes=[b_hb[i]])
        for half in range(2):
            for kk in range(8):
                k = half * 8 + kk
                mk.op("pe", lambda: nc.tensor.transpose(PT[half][:, kk, :], hb[i][:, k * 128:(k + 1) * 128], idb[:]),
                      reads=[b_hb[i], b_id], writes=[b_pt[half]], skip_same=True)
            eng = "act" if half == 0 else "dve"
            if eng == "act":
                mk.op("act", lambda: nc.scalar.copy(out=hT[:, half * 8:(half + 1) * 8, t * 128:(t + 1) * 128], in_=PT[half][:]),
                      reads=[b_pt[half]], writes=[b_hT])
            else:
                mk.op("dve", lambda: nc.vector.tensor_copy(out=hT[:, half * 8:(half + 1) * 8, t * 128:(t + 1) * 128], in_=PT[half][:]),
                      reads=[b_pt[half]], writes=[b_hT])
    pi = 0
    for cb in range(NCB):
        j = cb % 2
        c0 = cb * 128
        cw = min(128, NIN - c0)
        for tc in range(TOK // 512):
            q = pi % 4
            pi += 1
            for k in range(16):
                mk.op("pe", lambda: nc.tensor.matmul(PM[q][0:cw, :], lhsT=wt[j][:, k, 0:cw], rhs=hT[:, k, tc * 512:(tc + 1) * 512],
                                                     start=(k == 0), stop=(k == 15)),
                      reads=[b_w[j], b_hT], writes=[b_pm[q]], skip_same=True)
            if tc % 2 == 0:
                mk.op("act", lambda: nc.scalar.copy(out=ot[j][0:cw, tc * 512:(tc + 1) * 512], in_=PM[q][0:cw, :]), reads=[b_pm[q]], writes=[b_o[j]])
            else:
                mk.op("dve", lambda: nc.vector.tensor_copy(out=ot[j][0:cw, tc * 512:(tc + 1) * 512], in_=PM[q][0:cw, :]), reads=[b_pm[q]], writes=[b_o[j]])
        mk.dma("sp", pT[c0:c0 + cw, :], ot[j][0:cw, :], reads=[b_o[j]], is_output=True)
        if cb + 2 < NCB:
            load_w(cb + 2)


import numpy as np
from contextlib import ExitStack

C = 64
TB = 512
NCH = TB // C


def rwkv_consts():
    s = np.arange(64)[:, None]
    t = np.arange(64)[None, :]
    m_su = (s < t).astype(np.float32)
    m_ui = (s <= t).astype(np.float32)
    m1 = np.concatenate([m_su, m_ui], axis=1)
    mask1 = np.tile(m1, (1, 4))
    m_sl = (t < s).astype(np.float32)
    mask3 = np.tile(m_sl, (1, 4))
    seg = np.ones((128, TB), np.float32)
    seg[:, ::C] = 0.0
    return {"mask1": mask1, "mask3": mask3, "seg": seg, "ident": np.eye(128, dtype=np.float32)}


def emit_rwkv(nc, mk, T, rwin, par64, par128, w2, a2, g2, gnt, cst, yT, odt=F32):
    import os
    LVL = int(os.environ.get("RW_LVL", "9"))
    NB = T // TB
    V = lambda fn, r=(), w=(): mk.op("dve", fn, r, w)
    A = lambda fn, r=(), w=(): mk.op("act", fn, r, w)
    G = lambda fn, r=(), w=(): mk.op("pool", fn, r, w)
    PE = lambda fn, r=(), w=(), ss=True: mk.op("pe", fn, r, w, skip_same=ss)
    import os
    F32R = mybir.dt.float32r
    USE_R = os.environ.get("RW_F32R", "1") == "1"
    RR = (lambda a: a.bitcast(F32R)) if USE_R else (lambda a: a)
    sb = mk.sb
    p64 = sb("rw_p64", [64, 2, 11]); p128 = sb("rw_p128", [128, 3])
    w2t = sb("rw_w2", [96, 128]); a2t = sb("rw_a2", [96, 128]); g2t = sb("rw_g2", [128, 128])
    gn = sb("rw_gn", [64, 2, 2, 64])
    mask1 = sb("rw_mask1", [64, 512]); mask3 = sb("rw_mask3", [64, 256]); seg = sb("rw_seg", [128, TB])
    ident = sb("rw_ident", [128, 128])
    ones64 = sb("rw_ones", [64, 64])
    bc = Buf("const")
    for dst, src in ((p64, par64), (p128, par128), (w2t, w2), (a2t, a2), (g2t, g2), (gn, gnt),
                     (mask1, cst["mask1"]), (mask3, cst["mask3"]), (seg, cst["seg"]), (ident, cst["ident"])):
        mk.dma("sp", dst[:], src, writes=[bc])
    V(lambda: nc.vector.memset(ones64[:], 1.0), w=[bc])
    raw = {}
    for nm in ("r0", "k0", "v0", "r1", "k1", "v1"):
        raw[nm] = sb("rw_raw_" + nm, [64, TB + 1])
    raw["w"] = sb("rw_raw_w", [96, TB + 1]); raw["a"] = sb("rw_raw_a", [96, TB + 1]); raw["g"] = sb("rw_raw_g", [128, TB + 1])
    b_raw = Buf("raw")
    tmp = sb("rw_tmp", [128, TB]); b_tmp = Buf("tmp")
    ws = sb("rw_ws", [96, TB]); as_ = sb("rw_as", [96, TB]); gs = sb("rw_gs", [128, TB]); b_lo = Buf("lo")
    gate = [sb("rw_gate%d" % i_, [128, TB]) for i_ in range(2)]; b_gate = [Buf("gate0"), Buf("gate1")]
    H = []
    for h in range(2):
        d = {}
        for nm in ("rs", "ks", "vs", "lw", "asg", "kkn", "kp", "bv", "cum", "e1", "e2", "BT", "KT", "BH", "KH", "rkr"):
            d[nm] = sb("rw_%s%d" % (nm, h), [64, TB])
        d["AR"] = sb("rw_AR%d" % h, [64, NCH, 128])
        d["cC"] = sb("rw_cC%d" % h, [64, NCH]); d["gC"] = sb("rw_gC%d" % h, [64, NCH])
        d["b"] = Buf("H%d" % h)
        d["bo"] = [Buf("Ho%d_0" % h), Buf("Ho%d_1" % h)]
        d["Vt"] = sb("rw_Vt%d" % h, [64, NCH, 64]); d["BHt"] = sb("rw_BHt%d" % h, [64, NCH, 64]); d["KHt"] = sb("rw_KHt%d" % h, [64, NCH, 64])
        d["bt"] = [Buf("Ht%d_0" % h), Buf("Ht%d_1" % h)]
        for nm_, shp_ in (("AR", [64, NCH, 128]), ("BT", [64, TB]), ("KT", [64, TB]), ("gC", [64, NCH]), ("Vt", [64, NCH, 64]), ("BHt", [64, NCH, 64]), ("KHt", [64, NCH, 64])):
            d[nm_] = [d[nm_], sb("rw_%s%d_b" % (nm_, h), shp_)]
        d["NG"] = sb("rw_NG%d" % h, [64, NCH, 128]); d["LG"] = sb("rw_LG%d" % h, [64, NCH, 128])
        d["L"] = sb("rw_L%d" % h, [64, NCH, 64]); d["bA"] = Buf("A%d" % h)
        d["P"] = [sb("rw_P%d_%d" % (h, i), [64, NCH, 64]) for i in range(2)]
        d["PT"] = [sb("rw_PT%d_%d" % (h, i), [64, NCH, 64]) for i in range(2)]
        d["ST"] = [sb("rw_ST%d_%d" % (h, i), [64, NCH, 64]) for i in range(2)]
        d["bD"] = Buf("D%d" % h)
        d["M"] = sb("rw_M%d" % h, [64, 64]); d["bM"] = Buf("M%d" % h)
        d["X1"] = sb("rw_X1%d" % h, [64, 64]); d["U"] = sb("rw_U%d" % h, [64, 64]); d["bX"] = Buf("X%d" % h); d["bU"] = Buf("U%d" % h)
        H.append(d)
    Yb = sb("rw_Yb", [64, NCH, 2, 64]); b_Y = Buf("Y")
    Ysq = sb("rw_Ysq", [64, NCH, 2, 64])
    st1 = sb("rw_st1", [64, NCH * 2]); st2 = sb("rw_st2", [64, NCH * 2]); st3 = sb("rw_st3", [64, NCH * 2]); b_st = Buf("st")
    sbon = [sb("rw_sbon%d" % i_, [64, NCH, 2]) for i_ in range(2)]; b_sb = [Buf("sbon0"), Buf("sbon1")]
    yo = sb("rw_yo", [128, TB], odt); b_yo = Buf("yo")
    ps_lo = mk.ps("rw_ps_lo", [128, 512]); b_pl = Buf()
    ps_tr = mk.ps("rw_ps_tr", [128, 512]); b_ptr = Buf()
    ps_a1 = mk.ps("rw_ps_a1", [64, 512]); b_pa1 = Buf()
    ps_a2 = mk.ps("rw_ps_a2", [64, 512]); b_pa2 = Buf()
    ps_a3f = mk.ps("rw_ps_a3", [128, 512]); ps_a3 = ps_a3f[0:64, :]; b_pa3 = Buf()
    ps_d = mk.ps("rw_ps_d", [64, 512]); b_pd = Buf()
    ps_d2 = ps_a3; b_pd2 = b_pa3
    ps_sh = [mk.ps("rw_ps_s%d" % h, [64, 512]) for h in range(2)]
    b_psh = [Buf(), Buf()]

    for h in range(2):
        V(lambda: nc.vector.tensor_scalar(out=RR(H[h]["M"][:]), in0=ident[0:64, 0:64], scalar1=0.0, scalar2=None, op0=ALU.mult), r=[bc], w=[H[h]["bM"]])

    rows = {"r0": 0, "r1": 64, "k0": 128, "k1": 192, "v0": 256, "v1": 320, "w": 384, "a": 480, "g": 576}
    nrow = {"r0": 64, "r1": 64, "k0": 64, "k1": 64, "v0": 64, "v1": 64, "w": 96, "a": 96, "g": 128}

    def stage1(blk):
        t0 = blk * TB
        par = blk % 2
        for nm in rows:
            r0, n = rows[nm], nrow[nm]
            if blk == 0:
                V(lambda: nc.vector.memset(raw[nm][0:n, 0:1], 0.0), w=[b_raw])
                yield
                mk.dma("sp", raw[nm][0:n, 1:TB + 1], rwin[r0:r0 + n, 0:TB], writes=[b_raw])
                yield
            else:
                mk.dma("sp", raw[nm][0:n, :], rwin[r0:r0 + n, t0 - 1:t0 + TB], writes=[b_raw])
                yield

        def shift(dst, src, n, mu_ap, bdst):
            V(lambda: nc.vector.tensor_tensor(out=tmp[0:n, :], in0=src[0:n, 0:TB], in1=src[0:n, 1:TB + 1], op=ALU.subtract), r=[b_raw], w=[b_tmp])
            V(lambda: nc.vector.scalar_tensor_tensor(out=dst[0:n, :], in0=tmp[0:n, :], scalar=mu_ap, in1=src[0:n, 1:TB + 1], op0=ALU.mult, op1=ALU.add),
              r=[b_tmp, b_raw, bc], w=[bdst])

        shift(ws, raw["w"], 96, p128[0:96, 0:1], b_lo)
        yield
        shift(as_, raw["a"], 96, p128[0:96, 1:2], b_lo)
        yield
        shift(gs, raw["g"], 128, p128[:, 2:3], b_lo)
        yield
        A(lambda: nc.scalar.activation(out=ws[:], in_=ws[:], func=AF.Tanh), r=[b_lo], w=[b_lo])
        yield
        A(lambda: nc.scalar.activation(out=gs[:], in_=gs[:], func=AF.Sigmoid), r=[b_lo], w=[b_lo])
        yield
        PE(lambda: nc.tensor.matmul(ps_lo[:, :], lhsT=g2t[:, :], rhs=gs[:, :], start=True, stop=True), r=[bc, b_lo], w=[b_pl])
        yield
        A(lambda: nc.scalar.copy(out=gate[par][:], in_=ps_lo[:, :]), r=[b_pl], w=[b_gate[par]])
        yield
        for h in range(2):
            d = H[h]; b = d["b"]; bo = d["bo"][par]
            hs = slice(64 * h, 64 * h + 64)
            shift(d["rs"], raw["r%d" % h], 64, p64[:, h, 0:1], b)
            yield
            shift(d["ks"], raw["k%d" % h], 64, p64[:, h, 1:2], b)
            yield
            shift(d["vs"], raw["v%d" % h], 64, p64[:, h, 2:3], b)
            yield
            PE(lambda: nc.tensor.matmul(ps_lo[0:64, :], lhsT=w2t[:, hs], rhs=ws[:, :], start=True, stop=True), r=[bc, b_lo], w=[b_pl])
            yield
            A(lambda: nc.scalar.activation(out=d["lw"][:], in_=ps_lo[0:64, :], func=AF.Sigmoid, bias=p64[:, h, 3:4], scale=1.0), r=[b_pl, bc], w=[b])
            yield
            V(lambda: nc.vector.tensor_scalar(out=d["lw"][:], in0=d["lw"][:], scalar1=-0.6065306597126334, scalar2=None, op0=ALU.mult), r=[b], w=[b])
            yield
            PE(lambda: nc.tensor.matmul(ps_lo[0:64, :], lhsT=a2t[:, hs], rhs=as_[:, :], start=True, stop=True), r=[bc, b_lo], w=[b_pl])
            yield
            A(lambda: nc.scalar.activation(out=d["asg"][:], in_=ps_lo[0:64, :], func=AF.Sigmoid, bias=p64[:, h, 4:5], scale=1.0), r=[b_pl, bc], w=[b])
            yield
            V(lambda: nc.vector.tensor_scalar(out=d["kkn"][:], in0=d["ks"][:], scalar1=p64[:, h, 5:6], scalar2=None, op0=ALU.mult), r=[b, bc], w=[b])
            yield
            A(lambda: nc.scalar.activation(out=tmp[0:64, :], in_=d["kkn"][:], func=AF.Square), r=[b], w=[b_tmp])
            yield
            PE(lambda: nc.tensor.matmul(ps_lo[0:64, :], lhsT=ones64[:, :], rhs=tmp[0:64, :], start=True, stop=True), r=[bc, b_tmp], w=[b_pl])
            yield
            A(lambda: nc.scalar.activation(out=tmp[0:64, :], in_=ps_lo[0:64, :], func=AF.Sqrt), r=[b_pl], w=[b_tmp])
            yield
            V(lambda: nc.vector.tensor_scalar(out=tmp[0:64, :], in0=tmp[0:64, :], scalar1=1e-12, scalar2=None, op0=ALU.max), r=[b_tmp], w=[b_tmp])
            yield
            V(lambda: nc.vector.reciprocal(out=tmp[0:64, :], in_=tmp[0:64, :]), r=[b_tmp], w=[b_tmp])
            yield
            V(lambda: nc.vector.tensor_tensor(out=d["kkn"][:], in0=d["kkn"][:], in1=tmp[0:64, :], op=ALU.mult), r=[b, b_tmp], w=[b])
            yield
            V(lambda: nc.vector.tensor_scalar(out=tmp[0:64, :], in0=d["asg"][:], scalar1=-1.0, scalar2=p64[:, h, 6:7], op0=ALU.add, op1=ALU.mult), r=[b, bc], w=[b_tmp])
            yield
            V(lambda: nc.vector.scalar_tensor_tensor(out=d["kp"][:], in0=tmp[0:64, :], scalar=1.0, in1=d["ks"][:], op0=ALU.add, op1=ALU.mult), r=[b_tmp, b], w=[b])
            yield
            V(lambda: nc.vector.tensor_tensor(out=d["bv"][:], in0=d["kkn"][:], in1=d["asg"][:], op=ALU.mult), r=[b], w=[b])
            yield
            V(lambda: nc.vector.scalar_tensor_tensor(out=d["rkr"][:], in0=d["rs"][:], scalar=p64[:, h, 7:8], in1=d["kp"][:], op0=ALU.mult, op1=ALU.mult), r=[b, bc], w=[b])
            yield
            V(lambda: nc.vector.tensor_tensor_scan(out=d["cum"][:], data0=seg[0:64, :], data1=d["lw"][:], initial=0.0, op0=ALU.mult, op1=ALU.add), r=[b, bc], w=[b])
            yield
            cum3 = d["cum"][:].rearrange("p (c t) -> p c t", t=C)
            V(lambda: nc.vector.tensor_copy(out=d["cC"][:], in_=cum3[:, :, C - 1]), r=[b], w=[b])
            yield
            A(lambda: nc.scalar.activation(out=d["gC"][par][:], in_=d["cC"][:], func=AF.Exp), r=[b], w=[bo])
            yield
            A(lambda: nc.scalar.activation(out=d["e1"][:], in_=d["cum"][:], func=AF.Exp), r=[b], w=[b])
            yield
            A(lambda: nc.scalar.activation(out=d["e2"][:], in_=d["cum"][:], func=AF.Exp, scale=-1.0), r=[b], w=[b])
            yield
            AR = d["AR"][par]
            V(lambda: nc.vector.tensor_tensor(out=RR(AR[:, :, 64:128]), in0=d["rs"][:].rearrange("p (c t) -> p c t", t=C),
                                              in1=d["e1"][:].rearrange("p (c t) -> p c t", t=C), op=ALU.mult), r=[b], w=[bo])
            yield
            V(lambda: nc.vector.tensor_tensor(out=RR(d["BT"][par][:]), in0=d["bv"][:], in1=d["e2"][:], op=ALU.mult), r=[b], w=[bo])
            yield
            V(lambda: nc.vector.tensor_tensor(out=RR(d["KT"][par][:]), in0=d["kp"][:], in1=d["e2"][:], op=ALU.mult), r=[b], w=[bo])
            yield
            V(lambda: nc.vector.tensor_tensor(out=tmp[0:64, :], in0=d["cum"][:], in1=d["lw"][:], op=ALU.subtract), r=[b], w=[b_tmp])
            yield
            A(lambda: nc.scalar.activation(out=tmp[0:64, :], in_=tmp[0:64, :], func=AF.Exp), r=[b_tmp], w=[b_tmp])
            yield
            V(lambda: nc.vector.scalar_tensor_tensor(out=RR(AR[:, :, 0:64]), in0=d["kkn"][:].rearrange("p (c t) -> p c t", t=C), scalar=-1.0,
                                                     in1=tmp[0:64, :].rearrange("p (c t) -> p c t", t=C), op0=ALU.mult, op1=ALU.mult), r=[b, b_tmp], w=[bo])
            yield
            V(lambda: nc.vector.tensor_tensor(out=tmp[0:64, :].rearrange("p (c t) -> p c t", t=C), in0=d["cC"][:].unsqueeze(2).to_broadcast([64, NCH, C]),
                                              in1=cum3, op=ALU.subtract), r=[b], w=[b_tmp])
            yield
            A(lambda: nc.scalar.activation(out=tmp[0:64, :], in_=tmp[0:64, :], func=AF.Exp), r=[b_tmp], w=[b_tmp])
            yield
            V(lambda: nc.vector.tensor_tensor(out=d["BH"][:], in0=d["bv"][:], in1=tmp[0:64, :], op=ALU.mult), r=[b, b_tmp], w=[b])
            yield
            V(lambda: nc.vector.tensor_tensor(out=d["KH"][:], in0=d["kp"][:], in1=tmp[0:64, :], op=ALU.mult), r=[b, b_tmp], w=[b])
            yield
            for src, dstn in (("vs", "Vt"), ("BH", "BHt"), ("KH", "KHt")):
                for c in range(NCH):
                    PE(lambda: nc.tensor.transpose(ps_tr[0:64, c * 64:(c + 1) * 64], d[src][:, c * C:(c + 1) * C], ident[0:64, 0:64]), r=[b, bc], w=[b_ptr])
                    yield
                A(lambda: nc.scalar.copy(out=RR(d[dstn][par][:].rearrange("p c k -> p (c k)")), in_=ps_tr[0:64, :]), r=[b_ptr], w=[d["bt"][par]])
                yield
            for c in range(NCH):
                PE(lambda: nc.tensor.matmul(ps_tr[0:64, 2 * c:2 * c + 2], lhsT=d["rkr"][:, c * C:(c + 1) * C], rhs=ones64[:, 0:2], start=True, stop=True), r=[b, bc], w=[b_ptr])
                yield
            V(lambda: nc.vector.tensor_copy(out=sbon[par][:, :, h], in_=ps_tr[0:64, 0:2 * NCH:2]), r=[b_ptr], w=[b_sb[par]])
            yield

    def rest(blk, tick):
        t0 = blk * TB
        par = blk % 2
        for h in range(2):
            d = H[h]; b = d["bo"][par]; AR = d["AR"][par]
            tick()
            for half in range(2):
                for cc in range(4):
                    c = half * 4 + cc
                    PE(lambda: nc.tensor.matmul(ps_a1[:, cc * 128:(cc + 1) * 128], lhsT=RR(d["BT"][par][:, c * C:(c + 1) * C]), rhs=RR(AR[:, c, :]), start=True, stop=True), r=[b], w=[b_pa1])
                    PE(lambda: nc.tensor.matmul(ps_a2[:, cc * 128:(cc + 1) * 128], lhsT=RR(d["KT"][par][:, c * C:(c + 1) * C]), rhs=RR(AR[:, c, :]), start=True, stop=True), r=[b], w=[b_pa2])
                    PE(lambda: nc.tensor.matmul(ps_a3[:, cc * 64:(cc + 1) * 64], lhsT=RR(AR[:, c, 0:64]), rhs=RR(d["BT"][par][:, c * C:(c + 1) * C]), start=True, stop=True), r=[b], w=[b_pa3])
                V(lambda: nc.vector.tensor_tensor(out=RR(d["NG"][:, half * 4:half * 4 + 4, :].rearrange("p c k -> p (c k)")), in0=ps_a1[:, :], in1=mask1[:, :], op=ALU.mult), r=[b_pa1, bc], w=[d["bA"]])
                V(lambda: nc.vector.tensor_tensor(out=RR(d["LG"][:, half * 4:half * 4 + 4, :].rearrange("p c k -> p (c k)")), in0=ps_a2[:, :], in1=mask1[:, :], op=ALU.mult), r=[b_pa2, bc], w=[d["bA"]])
                V(lambda: nc.vector.tensor_tensor(out=d["L"][:, half * 4:half * 4 + 4, :].rearrange("p c k -> p (c k)"), in0=ps_a3[:, 0:256], in1=mask3[:, :], op=ALU.mult), r=[b_pa3, bc], w=[d["bA"]])
        DPS = [(ps_d, b_pd, ps_d2, b_pd2), (ps_a1, b_pa1, ps_a2, b_pa2)]
        for h in range(2):
            d = H[h]; P, PT, ST = d["P"], d["PT"], d["ST"]; bD = d["bD"]
            V(lambda: nc.vector.tensor_copy(out=RR(P[0][:]), in_=d["L"][:]), r=[d["bA"]], w=[bD])
            V(lambda: nc.vector.tensor_copy(out=RR(PT[0][:]), in_=d["NG"][:, :, 0:64]), r=[d["bA"]], w=[bD])
            V(lambda: nc.vector.tensor_tensor(out=RR(ST[0][:]), in0=d["NG"][:, :, 0:64], in1=ident[0:64, 0:64].unsqueeze(1).to_broadcast([64, NCH, 64]), op=ALU.add), r=[d["bA"], bc], w=[bD])
        cur = 0
        for lev in range(5):
            nxt = 1 - cur
            tick()
            for h in range(2):
                d = H[h]; P, PT, ST = d["P"], d["PT"], d["ST"]; bD = d["bD"]
                pd, bpd, pd2, bpd2 = DPS[h]
                for c in range(NCH):
                    PE(lambda: nc.tensor.matmul(pd[:, c * 64:(c + 1) * 64], lhsT=RR(PT[cur][:, c, :]), rhs=RR(P[cur][:, c, :]), start=True, stop=True), r=[bD], w=[bpd])
                for c in range(NCH):
                    PE(lambda: nc.tensor.matmul(pd2[:, c * 64:(c + 1) * 64], lhsT=RR(P[cur][:, c, :]), rhs=RR(PT[cur][:, c, :]), start=True, stop=True), r=[bD], w=[bpd2])
            tick()
            for h in range(2):
                d = H[h]; P, PT, ST = d["P"], d["PT"], d["ST"]; bD = d["bD"]
                pd, bpd, pd2, bpd2 = DPS[h]
                V(lambda: nc.vector.tensor_copy(out=RR(P[nxt][:].rearrange("p c k -> p (c k)")), in_=pd[:, :]), r=[], w=[bpd, bD])
                A(lambda: nc.scalar.copy(out=RR(PT[nxt][:].rearrange("p c k -> p (c k)")), in_=pd2[:, :]), r=[], w=[bpd2, bD])
            tick()
            for h in range(2):
                d = H[h]; P, PT, ST = d["P"], d["PT"], d["ST"]; bD = d["bD"]
                pd, bpd, pd2, bpd2 = DPS[h]
                for c in range(NCH):
                    PE(lambda: nc.tensor.matmul(pd[:, c * 64:(c + 1) * 64], lhsT=RR(P[nxt][:, c, :]), rhs=RR(ST[cur][:, c, :]), start=True, stop=True), r=[bD], w=[bpd])
            tick()
            for h in range(2):
                d = H[h]; P, PT, ST = d["P"], d["PT"], d["ST"]; bD = d["bD"]
                pd, bpd, pd2, bpd2 = DPS[h]
                V(lambda: nc.vector.tensor_tensor(out=RR(ST[nxt][:].rearrange("p c k -> p (c k)")), in0=pd[:, :], in1=ST[cur][:].rearrange("p c k -> p (c k)"), op=ALU.add), r=[bD], w=[bpd, bD])
            tick()
            cur = nxt
        for h in range(2):
            H[h]["STf"] = H[h]["ST"][cur]
        for c in range(NCH):
            pp = lambda h, i: ps_sh[h][:, i * 64:(i + 1) * 64]
            tick()
            for h in range(2):
                d = H[h]
                PE(lambda: nc.tensor.matmul(pp(h, 0), lhsT=RR(d["LG"][:, c, 0:64]), rhs=RR(d["Vt"][par][:, c, :]), start=True, stop=False), r=[d["bA"], d["bt"][par]], w=[b_psh[h]])
                PE(lambda: nc.tensor.matmul(pp(h, 0), lhsT=RR(d["AR"][par][:, c, 0:64]), rhs=RR(d["M"][:, :]), start=False, stop=True), r=[d["bo"][par], d["bM"]], w=[b_psh[h]])
            tick()
            for h in range(2):
                d = H[h]
                if h == 0:
                    A(lambda: nc.scalar.copy(out=RR(d["X1"][:]), in_=pp(h, 0)), r=[], w=[b_psh[h], d["bX"]])
                else:
                    V(lambda: nc.vector.tensor_copy(out=RR(d["X1"][:]), in_=pp(h, 0)), r=[], w=[b_psh[h], d["bX"]])
            tick()
            for h in range(2):
                d = H[h]
                PE(lambda: nc.tensor.matmul(pp(h, 1), lhsT=RR(d["STf"][:, c, :]), rhs=RR(d["X1"][:, :]), start=True, stop=True), r=[d["bD"], d["bX"]], w=[b_psh[h]])
            tick()
            for h in range(2):
                d = H[h]
                if h == 0:
                    V(lambda: nc.vector.tensor_copy(out=RR(d["U"][:]), in_=pp(h, 1)), r=[], w=[b_psh[h], d["bU"]])
                else:
                    A(lambda: nc.scalar.copy(out=RR(d["U"][:]), in_=pp(h, 1)), r=[], w=[b_psh[h], d["bU"]])
            tick()
            for h in range(2):
                d = H[h]
                PE(lambda: nc.tensor.matmul(pp(h, 2), lhsT=RR(d["AR"][par][:, c, 64:128]), rhs=RR(d["M"][:, :]), start=True, stop=False), r=[d["bo"][par], d["bM"]], w=[b_psh[h]])
                PE(lambda: nc.tensor.matmul(pp(h, 2), lhsT=RR(d["LG"][:, c, 64:128]), rhs=RR(d["Vt"][par][:, c, :]), start=False, stop=False), r=[d["bA"], d["bt"][par]], w=[b_psh[h]])
                PE(lambda: nc.tensor.matmul(pp(h, 2), lhsT=RR(d["NG"][:, c, 64:128]), rhs=RR(d["U"][:, :]), start=False, stop=True), r=[d["bA"], d["bU"]], w=[b_psh[h]])
                PE(lambda: nc.tensor.matmul(pp(h, 3), lhsT=RR(d["KHt"][par][:, c, :]), rhs=RR(d["Vt"][par][:, c, :]), start=True, stop=False), r=[d["bt"][par]], w=[b_psh[h]])
                PE(lambda: nc.tensor.matmul(pp(h, 3), lhsT=RR(d["BHt"][par][:, c, :]), rhs=RR(d["U"][:, :]), start=False, stop=True), r=[d["bt"][par], d["bU"]], w=[b_psh[h]])
            tick()
            for h in range(2):
                d = H[h]
                V(lambda: nc.vector.scalar_tensor_tensor(out=RR(d["M"][:]), in0=d["M"][:], scalar=d["gC"][par][:, c:c + 1], in1=pp(h, 3), op0=ALU.mult, op1=ALU.add), r=[d["bo"][par], d["bM"]], w=[b_psh[h], d["bM"]])
                A(lambda: nc.scalar.copy(out=Yb[:, c, h, :], in_=pp(h, 2)), r=[], w=[b_psh[h], b_Y])
        Y2 = Yb[:].rearrange("p c h v -> p (c h) v")
        V(lambda: nc.vector.tensor_reduce(out=st1[:], in_=Y2, axis=AX.X, op=ALU.add), r=[b_Y], w=[b_st])
        A(lambda: nc.scalar.activation(out=Ysq[:].rearrange("p c h v -> p (c h v)"), in_=Yb[:].rearrange("p c h v -> p (c h v)"), func=AF.Square), r=[b_Y], w=[b_tmp])
        V(lambda: nc.vector.tensor_reduce(out=st2[:], in_=Ysq[:].rearrange("p c h v -> p (c h) v"), axis=AX.X, op=ALU.add), r=[b_tmp], w=[b_st])
        V(lambda: nc.vector.tensor_scalar(out=st1[:], in0=st1[:], scalar1=1.0 / 64, scalar2=None, op0=ALU.mult), r=[b_st], w=[b_st])
        V(lambda: nc.vector.tensor_tensor(out=st3[:], in0=st1[:], in1=st1[:], op=ALU.mult), r=[b_st], w=[b_st])
        V(lambda: nc.vector.scalar_tensor_tensor(out=st2[:], in0=st2[:], scalar=1.0 / 64, in1=st3[:], op0=ALU.mult, op1=ALU.subtract), r=[b_st], w=[b_st])
        A(lambda: nc.scalar.activation(out=st2[:], in_=st2[:], func=AF.Sqrt, bias=64e-5, scale=1.0), r=[b_st], w=[b_st])
        V(lambda: nc.vector.reciprocal(out=st2[:], in_=st2[:]), r=[b_st], w=[b_st])
        V(lambda: nc.vector.tensor_tensor(out=Y2, in0=Y2, in1=st1[:].unsqueeze(2).to_broadcast([64, NCH * 2, 64]), op=ALU.subtract), r=[b_st, b_Y], w=[b_Y])
        V(lambda: nc.vector.tensor_tensor(out=Y2, in0=Y2, in1=st2[:].unsqueeze(2).to_broadcast([64, NCH * 2, 64]), op=ALU.mult), r=[b_st, b_Y], w=[b_Y])
        for h in range(2):
            V(lambda: nc.vector.tensor_tensor(out=Yb[:, :, h, :], in0=Yb[:, :, h, :], in1=gn[:, 0, h, :].unsqueeze(1).to_broadcast([64, NCH, 64]), op=ALU.mult), r=[b_Y, bc], w=[b_Y])
            V(lambda: nc.vector.tensor_tensor(out=Yb[:, :, h, :], in0=Yb[:, :, h, :], in1=gn[:, 1, h, :].unsqueeze(1).to_broadcast([64, NCH, 64]), op=ALU.add), r=[b_Y, bc], w=[b_Y])
            V(lambda: nc.vector.tensor_tensor(out=Ysq[:, :, h, :], in0=H[h]["Vt"][par][:], in1=sbon[par][:, :, h].unsqueeze(2).to_broadcast([64, NCH, 64]), op=ALU.mult), r=[H[h]["bt"][par], b_sb[par]], w=[b_tmp])
        V(lambda: nc.vector.tensor_tensor(out=Yb[:].rearrange("p c h v -> p (c h v)"), in0=Yb[:].rearrange("p c h v -> p (c h v)"), in1=Ysq[:].rearrange("p c h v -> p (c h v)"), op=ALU.add), r=[b_Y, b_tmp], w=[b_Y])
        for c in range(NCH):
            PE(lambda: nc.tensor.transpose(ps_a3f[:, c * 64:(c + 1) * 64], Yb[:, c, :, :].rearrange("p h v -> p (h v)"), ident[0:64, 0:64]), r=[b_Y, bc], w=[b_pa3])
        V(lambda: nc.vector.tensor_tensor(out=yo[:], in0=ps_a3f[:, :], in1=gate[par][:], op=ALU.mult), r=[b_gate[par]], w=[b_pa3, b_yo])
        mk.dma("sp", yT[:, t0:t0 + TB], yo[:], reads=[b_yo], is_output=True)

    for _ in stage1(0):
        pass
    for blk in range(NB):
        nx = stage1(blk + 1) if blk + 1 < NB else None

        def tick(k=2):
            if nx is not None:
                for _ in range(k):
                    if next(nx, "END") == "END":
                        break
        rest(blk, tick)
        if nx is not None:
            for _ in nx:
                pass


def build_rwkv(T):
    nc = bass.Bass("TRN2", target_bir_lowering=False)
    dt = lambda n, s, k="ExternalInput": nc.dram_tensor(n, s, F32, kind=k).ap()
    rwin = dt("rwin", [704, T]); par64 = dt("par64", [64, 2, 11]); par128 = dt("par128", [128, 3])
    w2 = dt("w2", [96, 128]); a2 = dt("a2", [96, 128]); g2 = dt("g2", [128, 128]); gnt = dt("gnt", [64, 2, 2, 64])
    cst = {"mask1": dt("mask1", [64, 512]), "mask3": dt("mask3", [64, 256]), "seg": dt("seg", [128, TB]), "ident": dt("ident", [128, 128])}
    yT = dt("yT", [128, T], "ExternalOutput")
    with ExitStack() as ctx:
        mk = MK(nc, ctx)
        emit_rwkv(nc, mk, T, rwin, par64, par128, w2, a2, g2, gnt, cst, yT)
        mk.finish("sp")
        print("rwkv ops", mk.nops, "waits", mk.nwaits)
    return nc


def rwkv_host_inputs(prm, l, q):
    G = 512
    cs = slice(128 * q, 128 * q + 128)
    mu = prm["rwkv_mu"][l]
    par64 = np.zeros((64, 2, 11), np.float32)
    for h in range(2):
        c0 = 128 * q + 64 * h
        par64[:, h, 0] = mu[0 * G + c0:0 * G + c0 + 64]
        par64[:, h, 1] = mu[1 * G + c0:1 * G + c0 + 64]
        par64[:, h, 2] = mu[2 * G + c0:2 * G + c0 + 64]
        par64[:, h, 3] = prm["rwkv_w0"][l][c0:c0 + 64]
        par64[:, h, 4] = prm["rwkv_a0"][l][c0:c0 + 64]
        par64[:, h, 5] = prm["rwkv_kk"][l][c0:c0 + 64]
        par64[:, h, 6] = prm["rwkv_ka"][l][c0:c0 + 64]
        par64[:, h, 7] = prm["rwkv_rk"][l][2 * q + h]
    par128 = np.zeros((128, 3), np.float32)
    par128[0:96, 0] = mu[3 * G:3 * G + 96]
    par128[0:96, 1] = mu[3 * G + 96:3 * G + 192]
    par128[:, 2] = mu[3 * G + 192:3 * G + 320]
    gnt = np.zeros((64, 2, 2, 64), np.float32)
    for h in range(2):
        c0 = 128 * q + 64 * h
        gnt[:, 0, h, :] = prm["rwkv_gn_g"][l][c0:c0 + 64][None]
        gnt[:, 1, h, :] = prm["rwkv_gn_b"][l][c0:c0 + 64][None]
    d = {"par64": par64, "par128": par128, "gnt": gnt,
         "w2": np.ascontiguousarray(prm["rwkv_w2"][l][:, cs]), "a2": np.ascontiguousarray(prm["rwkv_a2"][l][:, cs]),
         "g2": np.ascontiguousarray(prm["rwkv_g2"][l][:, cs])}
    d.update(rwkv_consts())
    return d


def rwkv_rows(q):
    G = 512
    idx = []
    for base in (0, G, 2 * G):
        idx += list(range(base + 128 * q, base + 128 * q + 64))
        idx += list(range(base + 128 * q + 64, base + 128 * q + 128))
    idx += list(range(3 * G, 3 * G + 320))
    return np.array(idx)


import math
import numpy as np
from contextlib import ExitStack


def emit_conv(nc, mk, T, cvin, cw, yT, TB=2048, odt=F32):
    V = lambda fn, r=(), w=(): mk.op("dve", fn, r, w)
    G = lambda fn, r=(), w=(): mk.op("pool", fn, r, w)
    cwt = mk.sb("cv_w", [128, 3]); bc = Buf()
    mk.dma("sp", cwt[:], cw, writes=[bc])
    Bt = mk.sb("cv_B", [128, TB]); Ct = mk.sb("cv_C", [128, TB + 2]); Ht = mk.sb("cv_H", [128, TB + 2])
    z = mk.sb("cv_z", [128, TB + 2]); y = mk.sb("cv_y", [128, TB]); o = mk.sb("cv_o", [128, TB], odt)
    b_in, b_z, b_y, b_o = Buf(), Buf(), Buf(), Buf()
    for blk in range(T // TB):
        t0 = blk * TB
        mk.dma("sp", Bt[:], cvin[0:128, t0:t0 + TB], writes=[b_in])
        if blk == 0:
            V(lambda: nc.vector.memset(Ct[:, 0:2], 0.0), w=[b_in])
            V(lambda: nc.vector.memset(Ht[:, 0:2], 0.0), w=[b_in])
            mk.dma("sp", Ct[:, 2:], cvin[128:256, 0:TB], writes=[b_in])
            mk.dma("sp", Ht[:, 2:], cvin[256:384, 0:TB], writes=[b_in])
        else:
            mk.dma("sp", Ct[:], cvin[128:256, t0 - 2:t0 + TB], writes=[b_in])
            mk.dma("sp", Ht[:], cvin[256:384, t0 - 2:t0 + TB], writes=[b_in])
        G(lambda: nc.gpsimd.tensor_tensor(out=z[:], in0=Ct[:], in1=Ht[:], op=ALU.mult), r=[b_in], w=[b_z])
        V(lambda: nc.vector.tensor_scalar(out=y[:], in0=z[:, 2:TB + 2], scalar1=cwt[:, 2:3], scalar2=None, op0=ALU.mult), r=[b_z, bc], w=[b_y])
        V(lambda: nc.vector.scalar_tensor_tensor(out=y[:], in0=z[:, 1:TB + 1], scalar=cwt[:, 1:2], in1=y[:], op0=ALU.mult, op1=ALU.add), r=[b_z, bc], w=[b_y])
        V(lambda: nc.vector.scalar_tensor_tensor(out=y[:], in0=z[:, 0:TB], scalar=cwt[:, 0:1], in1=y[:], op0=ALU.mult, op1=ALU.add), r=[b_z, bc], w=[b_y])
        G(lambda: nc.gpsimd.tensor_tensor(out=o[:], in0=y[:], in1=Bt[:], op=ALU.mult), r=[b_y, b_in], w=[b_o])
        mk.dma("sp", yT[:, t0:t0 + TB], o[:], reads=[b_o], is_output=True)


def t5_bucket_np(rel):
    n = np.maximum(rel, 0)
    max_exact = 16
    n_f = np.maximum(n, 1).astype(np.float32)
    large = max_exact + (np.log(n_f / max_exact) / math.log(128 / max_exact) * (32 - max_exact)).astype(np.int32)
    return np.where(n < max_exact, n, np.minimum(large, 31))


def attn_tables(rel_bias, sinks_l, q):
    qi = np.arange(128)[:, None]
    kj = np.arange(256)[None, :]
    rel = qi + 128 - kj
    bucket = t5_bucket_np(rel)
    valid = (rel >= 0) & (rel < 128)
    tab = np.zeros((2, 128, 2, 256), np.float32)
    for h in range(2):
        bias = rel_bias[bucket, 2 * q + h]
        full = np.where(valid, bias, np.float32(-30000.0))
        tab[0, :, h, :] = full
        f0 = full.copy()
        f0[:, 0:128] = -30000.0
        tab[1, :, h, :] = f0
    sk = np.broadcast_to(sinks_l[2 * q:2 * q + 2][None, :], (128, 2)).astype(np.float32).copy()
    return tab, sk


def emit_attn(nc, mk, T, qkv, btab, sinkt_d, ident_d, yT, odt=F32):
    V = lambda fn, r=(), w=(): mk.op("dve", fn, r, w)
    A = lambda fn, r=(), w=(): mk.op("act", fn, r, w)
    PE = lambda fn, r=(), w=(): mk.op("pe", fn, r, w, skip_same=True)
    NBK = T // 128
    bt = mk.sb("at_bt", [128, 2, 2, 256]); sk = mk.sb("at_sk", [128, 2]); ident = mk.sb("at_id", [128, 128]); bc = Buf()
    mk.dma("sp", bt[:, 0, :, :], btab[0], writes=[bc])
    mk.dma("sp", bt[:, 1, :, :], btab[1], writes=[bc])
    mk.dma("sp", sk[:], sinkt_d, writes=[bc])
    mk.dma("sp", ident[:], ident_d, writes=[bc])
    CH = 1024
    qt = mk.sb("at_q", [128, CH]); kt = mk.sb("at_k", [128, 128 + CH]); vt = mk.sb("at_v", [64, CH])
    vtok = mk.sb("at_vtok", [128, CH // 128 + 1, 64])
    b_q, b_k, b_v, b_vt = Buf(), Buf(), Buf(), Buf()
    sc = [mk.sb("at_sc%d" % h, [128, 256]) for h in range(2)]; b_sc = [Buf(), Buf()]
    pr = [mk.sb("at_p%d" % h, [128, 256]) for h in range(2)]; b_p = [Buf(), Buf()]
    pT = [mk.sb("at_pT%d" % h, [128, 256]) for h in range(2)]; b_pT = [Buf(), Buf()]
    sm = [mk.sb("at_sm%d" % h, [128, 8]) for h in range(2)]; b_sm = [Buf(), Buf()]
    ot = mk.sb("at_o", [128, 128]); b_o = Buf()
    yo = mk.sb("at_yo", [128, CH], odt); b_yo = Buf()
    ps_s = [mk.ps("at_ps_s%d" % h, [128, 512]) for h in range(2)]; b_ps = [Buf(), Buf()]
    ps_t = [mk.ps("at_ps_t%d" % h, [128, 512]) for h in range(2)]; b_pt = [Buf(), Buf()]
    ps_o = mk.ps("at_ps_o", [128, 512]); b_po = Buf()
    ps_v = mk.ps("at_ps_v", [128, 512]); b_pv = Buf()
    for ch in range(T // CH):
        c0 = ch * CH
        mk.dma("sp", qt[:], qkv[0:128, c0:c0 + CH], writes=[b_q])
        if ch == 0:
            V(lambda: nc.vector.memset(kt[:, 0:128], 0.0), w=[b_k])
            V(lambda: nc.vector.memset(vtok[:, 0, :], 0.0), w=[b_vt])
            for hh in range(2):
                mk.dma("sp", kt[64 * hh:64 * hh + 64, 128:], qkv[128:192, 0:CH], writes=[b_k])
        else:
            for hh in range(2):
                mk.dma("sp", kt[64 * hh:64 * hh + 64, :], qkv[128:192, c0 - 128:c0 + CH], writes=[b_k])
            V(lambda: nc.vector.tensor_copy(out=vtok[:, 0, :], in_=vtok[:, CH // 128, :]), r=[b_vt], w=[b_vt])
        mk.dma("sp", vt[:], qkv[192:256, c0:c0 + CH], writes=[b_v])
        for j in range(CH // 128):
            PE(lambda: nc.tensor.transpose(ps_v[:, j * 64:(j + 1) * 64], vt[:, j * 128:(j + 1) * 128], ident[0:64, 0:64]), r=[b_v, bc], w=[b_pv])
        A(lambda: nc.scalar.copy(out=vtok[:, 1:, :].rearrange("p j d -> p (j d)"), in_=ps_v[:, 0:(CH // 128) * 64]), r=[], w=[b_pv, b_vt])
        for j in range(CH // 128):
            first = 1 if (ch == 0 and j == 0) else 0
            HS = [slice(0, 64), slice(64, 128)]
            for h in range(2):
                PE(lambda: nc.tensor.matmul(ps_s[h][:, 0:256], lhsT=qt[HS[h], j * 128:(j + 1) * 128], rhs=kt[HS[h], j * 128:j * 128 + 256], start=True, stop=True),
                   r=[b_q, b_k], w=[b_ps[h]])
            for h in range(2):
                V(lambda: nc.vector.scalar_tensor_tensor(out=sc[h][:], in0=ps_s[h][:, 0:256], scalar=0.125, in1=bt[:, first, h, :], op0=ALU.mult, op1=ALU.add),
                  r=[bc], w=[b_ps[h], b_sc[h]])
                s = sm[h]
                V(lambda: nc.vector.reduce_max(out=s[:, 0:1], in_=sc[h][:], axis=AX.X), r=[b_sc[h]], w=[b_sm[h]])
                V(lambda: nc.vector.tensor_tensor(out=s[:, 0:1], in0=s[:, 0:1], in1=sk[:, h:h + 1], op=ALU.max), r=[bc], w=[b_sm[h]])
                V(lambda: nc.vector.tensor_scalar(out=s[:, 1:2], in0=s[:, 0:1], scalar1=-1.0, scalar2=None, op0=ALU.mult), r=[], w=[b_sm[h]])
            for h in range(2):
                s = sm[h]
                A(lambda: nc.scalar.activation(out=pr[h][:], in_=sc[h][:], func=AF.Exp, bias=s[:, 1:2], scale=1.0, accum_out=s[:, 2:3]), r=[b_sc[h]], w=[b_sm[h], b_p[h]])
                A(lambda: nc.scalar.activation(out=s[:, 3:4], in_=sk[:, h:h + 1], func=AF.Exp, bias=s[:, 1:2], scale=1.0), r=[bc], w=[b_sm[h]])
            for h in range(2):
                for kb in range(2):
                    PE(lambda: nc.tensor.transpose(ps_t[h][:, kb * 128:(kb + 1) * 128], pr[h][:, kb * 128:(kb + 1) * 128], ident[:, :]), r=[b_p[h], bc], w=[b_pt[h]])
            for h in range(2):
                s = sm[h]
                V(lambda: nc.vector.tensor_tensor(out=s[:, 4:5], in0=s[:, 2:3], in1=s[:, 3:4], op=ALU.add), r=[], w=[b_sm[h]])
                V(lambda: nc.vector.reciprocal(out=s[:, 5:6], in_=s[:, 4:5]), r=[], w=[b_sm[h]])
                V(lambda: nc.vector.tensor_copy(out=pT[h][:], in_=ps_t[h][:, 0:256]), r=[], w=[b_pt[h], b_pT[h]])
            for h in range(2):
                for kb in range(2):
                    PE(lambda: nc.tensor.matmul(ps_o[:, h * 64:(h + 1) * 64], lhsT=pT[h][:, kb * 128:(kb + 1) * 128], rhs=vtok[:, j + kb, :], start=(kb == 0), stop=(kb == 1)),
                       r=[b_pT[h], b_vt], w=[b_po])
            for h in range(2):
                s = sm[h]
                A(lambda: nc.scalar.activation(out=ot[:, h * 64:(h + 1) * 64], in_=ps_o[:, h * 64:(h + 1) * 64], func=AF.Copy, scale=s[:, 5:6]), r=[b_sm[h]], w=[b_po, b_o])
            PE(lambda: nc.tensor.transpose(ps_o[:, 128:256], ot[:, :], ident[:, :]), r=[b_o, bc], w=[b_po])
            V(lambda: nc.vector.tensor_copy(out=yo[:, j * 128:(j + 1) * 128], in_=ps_o[:, 128:256]), r=[], w=[b_po, b_yo])
        mk.dma("sp", yT[:, c0:c0 + CH], yo[:], reads=[b_yo], is_output=True)


CS = 512


def s5_host_inputs(prm, l, q):
    g0 = 8 * q
    par = np.zeros((128, 4, 3), np.float32)
    bb = np.zeros((128, 4, 2, 16), np.float32)
    cc = np.zeros((128, 4, 2, 16), np.float32)
    for j in range(4):
        for gl in range(2):
            g = g0 + 2 * j + gl
            ps = slice(64 * gl, 64 * gl + 64)
            par[ps, j, 0] = prm["s5_lambda_re"][l][g]
            par[ps, j, 1] = prm["s5_lambda_im"][l][g]
            par[ps, j, 2] = prm["s5_log_dt"][l][g]
            bb[ps, j, 0, :] = prm["s5_b_re"][l][g]
            bb[ps, j, 1, :] = prm["s5_b_im"][l][g]
            cc[ps, j, 0, :] = prm["s5_c_re"][l][g].T
            cc[ps, j, 1, :] = prm["s5_c_im"][l][g].T
    dsk = np.ascontiguousarray(prm["s5_d"][l][g0:g0 + 8].reshape(128, 1))
    iot = np.broadcast_to(np.arange(CS, dtype=np.float32)[None, :], (128, CS)).copy()
    return {"s5par": par, "s5bb": bb, "s5cc": cc, "s5d": dsk, "s5iota": iot, "ident": np.eye(128, dtype=np.float32)}


def emit_s5(nc, mk, T, uT, par_d, bb_d, cc_d, d_d, iota_d, ident_d, yT, odt=F32):
    V = lambda fn, r=(), w=(): mk.op("dve", fn, r, w)
    A = lambda fn, r=(), w=(): mk.op("act", fn, r, w)
    G = lambda fn, r=(), w=(): mk.op("pool", fn, r, w)
    PE = lambda fn, r=(), w=(): mk.op("pe", fn, r, w, skip_same=True)
    sb = mk.sb
    TWO_PI = 2.0 * math.pi
    par = sb("s5_par", [128, 4, 3]); bb = sb("s5_bb", [128, 4, 2, 16]); cc = sb("s5_cc", [128, 4, 2, 16])
    dsk = sb("s5_d", [128, 1]); iot = sb("s5_iota", [128, CS]); ident = sb("s5_id", [128, 128])
    bc = Buf("c")
    for dst, src in ((par, par_d), (bb, bb_d), (cc, cc_d), (dsk, d_d), (iot, iota_d), (ident, ident_d)):
        mk.dma("sp", dst[:], src, writes=[bc])
    P = {}
    for nm in ("dl", "mag", "th", "cs", "sn", "are", "aim", "den", "zre", "zim", "t1", "t2", "t3", "cC", "sC"):
        P[nm] = sb("s5_p_" + nm, [128, 4])
    ti = sb("s5_ti", [128, 4 * CS], I32)
    bp = Buf("p")
    lr, li, ldt = par[:, :, 0], par[:, :, 1], par[:, :, 2]

    def sincos(sin_out, cos_out, x, n, tmpa, tmpb, tint):
        def wrap(r):
            V(lambda: nc.vector.tensor_scalar(out=tmpb, in0=r, scalar1=0.5, scalar2=None, op0=ALU.is_gt), r=[bp], w=[bp])
            V(lambda: nc.vector.tensor_tensor(out=r, in0=r, in1=tmpb, op=ALU.subtract), r=[bp], w=[bp])
            V(lambda: nc.vector.tensor_scalar(out=tmpb, in0=r, scalar1=-0.5, scalar2=None, op0=ALU.is_lt), r=[bp], w=[bp])
            V(lambda: nc.vector.tensor_tensor(out=r, in0=r, in1=tmpb, op=ALU.add), r=[bp], w=[bp])
        V(lambda: nc.vector.tensor_copy(out=tint, in_=x), r=[bp, bc], w=[bp])
        V(lambda: nc.vector.tensor_copy(out=tmpa, in_=tint), r=[bp], w=[bp])
        V(lambda: nc.vector.tensor_tensor(out=tmpa, in0=x, in1=tmpa, op=ALU.subtract), r=[bp, bc], w=[bp])
        wrap(tmpa)
        A(lambda: nc.scalar.activation(out=sin_out, in_=tmpa, func=AF.Sin, scale=TWO_PI), r=[bp], w=[bp])
        V(lambda: nc.vector.tensor_scalar(out=tmpa, in0=tmpa, scalar1=0.25, scalar2=None, op0=ALU.add), r=[bp], w=[bp])
        wrap(tmpa)
        A(lambda: nc.scalar.activation(out=cos_out, in_=tmpa, func=AF.Sin, scale=TWO_PI), r=[bp], w=[bp])

    A(lambda: nc.scalar.activation(out=P["dl"][:], in_=ldt, func=AF.Exp), r=[bc], w=[bp])
    V(lambda: nc.vector.tensor_tensor(out=P["mag"][:], in0=lr, in1=P["dl"][:], op=ALU.mult), r=[bc, bp], w=[bp])
    A(lambda: nc.scalar.activation(out=P["mag"][:], in_=P["mag"][:], func=AF.Exp), r=[bp], w=[bp])
    V(lambda: nc.vector.tensor_tensor(out=P["th"][:], in0=li, in1=P["dl"][:], op=ALU.mult), r=[bc, bp], w=[bp])
    V(lambda: nc.vector.tensor_scalar(out=P["th"][:], in0=P["th"][:], scalar1=1.0 / TWO_PI, scalar2=None, op0=ALU.mult), r=[bp], w=[bp])
    V(lambda: nc.vector.tensor_copy(out=ti[:, 0:4], in_=P["th"][:]), r=[bp], w=[bp])
    V(lambda: nc.vector.tensor_copy(out=P["t1"][:], in_=ti[:, 0:4]), r=[bp], w=[bp])
    V(lambda: nc.vector.tensor_tensor(out=P["th"][:], in0=P["th"][:], in1=P["t1"][:], op=ALU.subtract), r=[bp], w=[bp])
    V(lambda: nc.vector.tensor_scalar(out=P["t1"][:], in0=P["th"][:], scalar1=0.5, scalar2=None, op0=ALU.is_gt), r=[bp], w=[bp])
    V(lambda: nc.vector.tensor_tensor(out=P["th"][:], in0=P["th"][:], in1=P["t1"][:], op=ALU.subtract), r=[bp], w=[bp])
    V(lambda: nc.vector.tensor_scalar(out=P["t1"][:], in0=P["th"][:], scalar1=-0.5, scalar2=None, op0=ALU.is_lt), r=[bp], w=[bp])
    V(lambda: nc.vector.tensor_tensor(out=P["th"][:], in0=P["th"][:], in1=P["t1"][:], op=ALU.add), r=[bp], w=[bp])
    sincos(P["sn"][:], P["cs"][:], P["th"][:], 4, P["t1"][:], P["t2"][:], ti[:, 0:4])
    V(lambda: nc.vector.tensor_tensor(out=P["are"][:], in0=P["mag"][:], in1=P["cs"][:], op=ALU.mult), r=[bp], w=[bp])
    V(lambda: nc.vector.tensor_tensor(out=P["aim"][:], in0=P["mag"][:], in1=P["sn"][:], op=ALU.mult), r=[bp], w=[bp])
    V(lambda: nc.vector.tensor_tensor(out=P["den"][:], in0=lr, in1=lr, op=ALU.mult), r=[bc], w=[bp])
    V(lambda: nc.vector.tensor_tensor(out=P["t1"][:], in0=li, in1=li, op=ALU.mult), r=[bc], w=[bp])
    V(lambda: nc.vector.tensor_tensor(out=P["den"][:], in0=P["den"][:], in1=P["t1"][:], op=ALU.add), r=[bp], w=[bp])
    V(lambda: nc.vector.reciprocal(out=P["den"][:], in_=P["den"][:]), r=[bp], w=[bp])
    V(lambda: nc.vector.tensor_scalar(out=P["t3"][:], in0=P["are"][:], scalar1=-1.0, scalar2=None, op0=ALU.add), r=[bp], w=[bp])
    V(lambda: nc.vector.tensor_tensor(out=P["t1"][:], in0=P["t3"][:], in1=lr, op=ALU.mult), r=[bp, bc], w=[bp])
    V(lambda: nc.vector.tensor_tensor(out=P["t2"][:], in0=P["aim"][:], in1=li, op=ALU.mult), r=[bp, bc], w=[bp])
    V(lambda: nc.vector.tensor_tensor(out=P["t1"][:], in0=P["t1"][:], in1=P["t2"][:], op=ALU.add), r=[bp], w=[bp])
    V(lambda: nc.vector.tensor_tensor(out=P["zre"][:], in0=P["t1"][:], in1=P["den"][:], op=ALU.mult), r=[bp], w=[bp])
    V(lambda: nc.vector.tensor_tensor(out=P["t1"][:], in0=P["aim"][:], in1=lr, op=ALU.mult), r=[bp, bc], w=[bp])
    V(lambda: nc.vector.tensor_tensor(out=P["t2"][:], in0=P["t3"][:], in1=li, op=ALU.mult), r=[bp, bc], w=[bp])
    V(lambda: nc.vector.tensor_tensor(out=P["t1"][:], in0=P["t1"][:], in1=P["t2"][:], op=ALU.subtract), r=[bp], w=[bp])
    V(lambda: nc.vector.tensor_tensor(out=P["zim"][:], in0=P["t1"][:], in1=P["den"][:], op=ALU.mult), r=[bp], w=[bp])
    bbar = sb("s5_bbar", [128, 4, 2, 16]); tb1 = sb("s5_tb1", [128, 4, 16]); tb2 = sb("s5_tb2", [128, 4, 16])
    zre_b = P["zre"][:].unsqueeze(2).to_broadcast([128, 4, 16]); zim_b = P["zim"][:].unsqueeze(2).to_broadcast([128, 4, 16])
    V(lambda: nc.vector.tensor_tensor(out=tb1[:], in0=bb[:, :, 0, :], in1=zre_b, op=ALU.mult), r=[bp, bc], w=[bp])
    V(lambda: nc.vector.tensor_tensor(out=tb2[:], in0=bb[:, :, 1, :], in1=zim_b, op=ALU.mult), r=[bp, bc], w=[bp])
    V(lambda: nc.vector.tensor_tensor(out=bbar[:, :, 0, :], in0=tb1[:], in1=tb2[:], op=ALU.subtract), r=[bp], w=[bp])
    V(lambda: nc.vector.tensor_tensor(out=tb1[:], in0=bb[:, :, 1, :], in1=zre_b, op=ALU.mult), r=[bp, bc], w=[bp])
    V(lambda: nc.vector.tensor_tensor(out=tb2[:], in0=bb[:, :, 0, :], in1=zim_b, op=ALU.mult), r=[bp, bc], w=[bp])
    V(lambda: nc.vector.tensor_tensor(out=bbar[:, :, 1, :], in0=tb1[:], in1=tb2[:], op=ALU.add), r=[bp], w=[bp])
    BD = sb("s5_BD", [128, 4, 2, 128]); CM = sb("s5_CM", [128, 4, 4, 128]); BbT = sb("s5_BbT", [128, 4, 2, 128])
    V(lambda: nc.vector.memset(BD[:].rearrange("p a b c -> p (a b c)"), 0.0), w=[bp])
    V(lambda: nc.vector.memset(CM[:].rearrange("p a b c -> p (a b c)"), 0.0), w=[bp])
    for j in range(4):
        for gl in range(2):
            ps_ = slice(64 * gl, 64 * gl + 64)
            c0 = 32 * j + 16 * gl
            for ri in range(2):
                V(lambda: nc.vector.tensor_copy(out=BD[ps_, j, ri, c0:c0 + 16], in_=bbar[ps_, j, ri, :]), r=[bp], w=[bp])
            V(lambda: nc.vector.tensor_copy(out=CM[ps_, j, 0, c0:c0 + 16], in_=cc[ps_, j, 0, :]), r=[bc], w=[bp])
            V(lambda: nc.vector.tensor_scalar(out=CM[ps_, j, 1, c0:c0 + 16], in0=cc[ps_, j, 0, :], scalar1=-1.0, scalar2=None, op0=ALU.mult), r=[bc], w=[bp])
            V(lambda: nc.vector.tensor_scalar(out=CM[ps_, j, 2, c0:c0 + 16], in0=cc[ps_, j, 1, :], scalar1=-1.0, scalar2=None, op0=ALU.mult), r=[bc], w=[bp])
            V(lambda: nc.vector.tensor_scalar(out=CM[ps_, j, 3, c0:c0 + 16], in0=cc[ps_, j, 1, :], scalar1=-1.0, scalar2=None, op0=ALU.mult), r=[bc], w=[bp])
    ps_tmp = mk.ps("s5_ps_tmp", [128, 512]); b_pt = Buf()
    for j in range(4):
        for ri in range(2):
            PE(lambda: nc.tensor.transpose(ps_tmp[:, ri * 128:(ri + 1) * 128], BD[:, j, ri, :], ident[:, :]), r=[bp, bc], w=[b_pt])
        V(lambda: nc.vector.tensor_copy(out=BbT[:, j, :, :].rearrange("p a b -> p (a b)"), in_=ps_tmp[:, 0:256]), r=[], w=[b_pt, bp])
    cosT = sb("s5_cosT", [128, 4, CS]); sinT = sb("s5_sinT", [128, 4, CS])
    xa = sb("s5_xa", [128, 4 * CS]); xb = sb("s5_xb", [128, 4 * CS]); xc = sb("s5_xc", [128, 4 * CS])
    for j in range(4):
        V(lambda: nc.vector.tensor_scalar(out=xc[:, j * CS:(j + 1) * CS], in0=iot[:], scalar1=P["th"][:, j:j + 1], scalar2=None, op0=ALU.mult), r=[bp, bc], w=[bp])
    sincos(sinT[:].rearrange("p a b -> p (a b)"), cosT[:].rearrange("p a b -> p (a b)"), xc[:], 4 * CS, xa[:], xb[:], ti[:])
    V(lambda: nc.vector.tensor_scalar(out=P["t3"][:], in0=P["th"][:], scalar1=float(CS), scalar2=None, op0=ALU.mult), r=[bp], w=[bp])
    sincos(P["sC"][:], P["cC"][:], P["t3"][:], 4, P["t1"][:], P["t2"][:], ti[:, 0:4])
    WW = []
    for par in range(2):
        Wd = {}
        for nm in ("t1", "t2", "t3", "t4", "br", "bi", "zr", "zi", "q1", "q2", "q3", "q4"):
            Wd[nm] = sb("s5_w%d_%s" % (par, nm), [128, CS])
        WW.append(Wd)
    b_wl = [Buf("w0"), Buf("w1")]; b_zl = [Buf("z0"), Buf("z1")]; b_ql = [Buf("q0"), Buf("q1")]
    init = sb("s5_init", [128, 4, 2]); itmp = sb("s5_itmp", [128, 4, 2]); b_il = [Buf("init%d" % j) for j in range(4)]
    for j in range(4):
        V(lambda: nc.vector.memset(init[:, j, :], 0.0), w=[b_il[j]])
    yv = sb("s5_yv", [128, CS]); y2 = sb("s5_y2", [128, CS]); yo = sb("s5_yo", [128, CS], odt); b_y = Buf("y")
    ps_al = [mk.ps("s5_ps_a%d" % i, [128, 512]) for i in range(2)]; ps_bl = [mk.ps("s5_ps_b%d" % i, [128, 512]) for i in range(2)]
    ps_y = mk.ps("s5_ps_y", [128, 512])
    b_pal, b_pbl, b_py = [Buf(), Buf()], [Buf(), Buf()], Buf()
    uts = [sb("s5_u%d" % i, [128, CS]) for i in range(2)]; b_ul = [Buf(), Buf()]
    NCK = T // CS

    def stage1(n):
        chk, j = n // 4, n % 4
        par = n % 2
        ut = uts[chk % 2]; b_u = b_ul[chk % 2]
        if j == 0:
            mk.dma("sp", ut[:], uT[:, chk * CS:(chk + 1) * CS], writes=[b_u])
        W = WW[par]; b_w = b_wl[par]
        ps_a = ps_al[par]; ps_b = ps_bl[par]; b_pa = b_pal[par]; b_pb = b_pbl[par]
        PE(lambda: nc.tensor.matmul(ps_a[:, :], lhsT=BbT[:, j, 0, :], rhs=ut[:, :], start=True, stop=True), r=[bp, b_u], w=[b_pa])
        PE(lambda: nc.tensor.matmul(ps_b[:, :], lhsT=BbT[:, j, 1, :], rhs=ut[:, :], start=True, stop=True), r=[bp, b_u], w=[b_pb])

    def stage1v(n):
        chk, j = n // 4, n % 4
        par = n % 2
        W = WW[par]; b_w = b_wl[par]
        ps_a = ps_al[par]; ps_b = ps_bl[par]; b_pa = b_pal[par]; b_pb = b_pbl[par]
        cj, sj = cosT[:, j, :], sinT[:, j, :]
        V(lambda: nc.vector.tensor_tensor(out=W["t1"][:], in0=ps_a[:, :], in1=cj, op=ALU.mult), r=[bp], w=[b_pa, b_w])
        V(lambda: nc.vector.tensor_tensor(out=W["t4"][:], in0=ps_a[:, :], in1=sj, op=ALU.mult), r=[bp], w=[b_pa, b_w])
        V(lambda: nc.vector.tensor_tensor(out=W["t2"][:], in0=ps_b[:, :], in1=sj, op=ALU.mult), r=[bp], w=[b_pb, b_w])
        V(lambda: nc.vector.tensor_tensor(out=W["t3"][:], in0=ps_b[:, :], in1=cj, op=ALU.mult), r=[bp], w=[b_pb, b_w])
        G(lambda: nc.gpsimd.tensor_tensor(out=W["br"][:], in0=W["t1"][:], in1=W["t2"][:], op=ALU.add), r=[b_w], w=[b_w])
        G(lambda: nc.gpsimd.tensor_tensor(out=W["bi"][:], in0=W["t3"][:], in1=W["t4"][:], op=ALU.subtract), r=[b_w], w=[b_w])

    def stage2(n):
        chk, j = n // 4, n % 4
        par = n % 2
        t0 = chk * CS
        ut = uts[chk % 2]; b_u = b_ul[chk % 2]
        W = WW[par]; b_w = b_wl[par]; b_z = b_zl[par]; b_q = b_ql[par]; b_i = b_il[j]
        cj, sj = cosT[:, j, :], sinT[:, j, :]
        rho = P["mag"][:, j:j + 1].to_broadcast([128, CS])
        V(lambda: nc.vector.tensor_tensor_scan(out=W["zr"][:], data0=rho, data1=W["br"][:], initial=init[:, j, 0:1], op0=ALU.mult, op1=ALU.add), r=[b_w, bp, b_i, b_q], w=[b_z])
        V(lambda: nc.vector.tensor_tensor_scan(out=W["zi"][:], data0=rho, data1=W["bi"][:], initial=init[:, j, 1:2], op0=ALU.mult, op1=ALU.add), r=[b_w, bp, b_i, b_q], w=[b_z])
        zrl, zil = W["zr"][:, CS - 1:CS], W["zi"][:, CS - 1:CS]
        cC, sC = P["cC"][:, j:j + 1], P["sC"][:, j:j + 1]
        V(lambda: nc.vector.tensor_tensor(out=itmp[:, j, 0:1], in0=zil, in1=sC, op=ALU.mult), r=[b_z, bp], w=[b_i])
        V(lambda: nc.vector.scalar_tensor_tensor(out=init[:, j, 0:1], in0=zrl, scalar=cC, in1=itmp[:, j, 0:1], op0=ALU.mult, op1=ALU.subtract), r=[b_z, bp], w=[b_i])
        V(lambda: nc.vector.tensor_tensor(out=itmp[:, j, 1:2], in0=zrl, in1=sC, op=ALU.mult), r=[b_z, bp], w=[b_i])
        V(lambda: nc.vector.scalar_tensor_tensor(out=init[:, j, 1:2], in0=zil, scalar=cC, in1=itmp[:, j, 1:2], op0=ALU.mult, op1=ALU.add), r=[b_z, bp], w=[b_i])
        G(lambda: nc.gpsimd.tensor_tensor(out=W["q1"][:], in0=W["zr"][:], in1=cj, op=ALU.mult), r=[b_z, bp], w=[b_q])
        G(lambda: nc.gpsimd.tensor_tensor(out=W["q2"][:], in0=W["zi"][:], in1=sj, op=ALU.mult), r=[b_z, bp], w=[b_q])
        V(lambda: nc.vector.tensor_tensor(out=W["q3"][:], in0=W["zi"][:], in1=cj, op=ALU.mult), r=[b_z, bp], w=[b_q])
        V(lambda: nc.vector.tensor_tensor(out=W["q4"][:], in0=W["zr"][:], in1=sj, op=ALU.mult), r=[b_z, bp], w=[b_q])
        for qi, nm in enumerate(("q1", "q2", "q3", "q4")):
            PE(lambda: nc.tensor.matmul(ps_y[:, :], lhsT=CM[:, j, qi, :], rhs=W[nm][:, :], start=(j == 0 and qi == 0), stop=(j == 3 and qi == 3)), r=[bp, b_q], w=[b_py])
        if j == 3:
            V(lambda: nc.vector.scalar_tensor_tensor(out=yv[:], in0=ut[:], scalar=dsk[:, 0:1], in1=ps_y[:, :], op0=ALU.mult, op1=ALU.add), r=[b_u, bc], w=[b_py, b_y])
            A(lambda: nc.scalar.activation(out=y2[:], in_=yv[:], func=AF.Square), r=[b_y], w=[b_y])
            V(lambda: nc.vector.tensor_scalar(out=y2[:], in0=y2[:], scalar1=0.044715, scalar2=1.0, op0=ALU.mult, op1=ALU.add), r=[b_y], w=[b_y])
            V(lambda: nc.vector.tensor_tensor(out=y2[:], in0=y2[:], in1=yv[:], op=ALU.mult), r=[b_y], w=[b_y])
            A(lambda: nc.scalar.activation(out=y2[:], in_=y2[:], func=AF.Tanh, scale=0.7978845608028654), r=[b_y], w=[b_y])
            V(lambda: nc.vector.scalar_tensor_tensor(out=y2[:], in0=y2[:], scalar=1.0, in1=yv[:], op0=ALU.add, op1=ALU.mult), r=[b_y], w=[b_y])
            A(lambda: nc.scalar.mul(out=yo[:], in_=y2[:], mul=0.5), r=[b_y], w=[b_y])
            mk.dma("sp", yT[:, t0:t0 + CS], yo[:], reads=[b_y], is_output=True)

    NTL_ = NCK * 4
    stage1(0)
    for n in range(NTL_ + 1):
        if n + 1 < NTL_:
            stage1(n + 1)
        if n < NTL_:
            stage1v(n)
        if n >= 1:
            stage2(n - 1)


def build(which, T):
    nc = bass.Bass("TRN2", target_bir_lowering=False)
    dt = lambda n, s, k="ExternalInput": nc.dram_tensor(n, s, F32, kind=k).ap()
    with ExitStack() as ctx:
        mk = MK(nc, ctx)
        if which == "conv":
            emit_conv(nc, mk, T, dt("cvin", [384, T]), dt("cw", [128, 3]), dt("yT", [128, T], "ExternalOutput"), TB=min(T, 2048))
        elif which == "attn":
            emit_attn(nc, mk, T, dt("qkv", [256, T]), dt("btab", [2, 128, 2, 256]), dt("sinkt", [128, 2]), dt("ident", [128, 128]), dt("yT", [128, T], "ExternalOutput"))
        elif which == "s5":
            emit_s5(nc, mk, T, dt("uT", [128, T]), dt("s5par", [128, 4, 3]), dt("s5bb", [128, 4, 2, 16]), dt("s5cc", [128, 4, 2, 16]),
                    dt("s5d", [128, 1]), dt("s5iota", [128, CS]), dt("ident", [128, 128]), dt("yT", [128, T], "ExternalOutput"))
        mk.finish("sp")
        print(which, "ops", mk.nops, "waits", mk.nwaits)
    return nc


import math
import numpy as np
from contextlib import ExitStack

D = 2048
TOK = 2048
NT = TOK // 128
NE = 32
CAP = 256
ALPHA = 4 ** 0.25
DE = 512


def k3_consts():
    tp = np.arange(128)[:, None]
    t = np.arange(128)[None, :]
    U = (tp < t).astype(np.float32)
    ecap = np.broadcast_to((np.arange(NE) * CAP).astype(np.float32)[None, :], (128, NE)).copy()
    return {"U": U, "ecap": ecap, "ident": np.eye(128, dtype=np.float32)}


def emit_k3(nc, mk, ymixT, x, w_out, glu_w, glu_b, rows, wr, br, w1, w3, w2, cst, x1s, Xg, Yg, xout, ymload=None, rowload=None):
    V = lambda fn, r=(), w=(): mk.op("dve", fn, r, w)
    A = lambda fn, r=(), w=(): mk.op("act", fn, r, w)
    G = lambda fn, r=(), w=(): mk.op("pool", fn, r, w)
    PE = lambda fn, r=(), w=(): mk.op("pe", fn, r, w, skip_same=True)
    gw = mk.sb("k3_gw", [128, NT, 2]); slot = mk.sb("k3_slot", [128, NT, 2], I32); b_rt = Buf("route")
    ident = mk.sb("k3_ident", [128, 128]); identb = mk.sb("k3_identb", [128, 128], BF16); bc = Buf("c")
    mk.dma("sp", ident[:], cst["ident"], writes=[bc])
    V(lambda: nc.vector.tensor_copy(out=identb[:], in_=ident[:]), r=[bc], w=[bc])
    with ExitStack() as pa:
        sb = lambda n, s, dt=F32: pa.enter_context(nc.sbuf_tensor("k3a%d_" % mk.gen + n, list(s), dt))
        ps = lambda n, s, dt=F32: pa.enter_context(nc.psum_tensor("k3a%d_" % mk.gen + n, list(s), dt))
        wo = sb("wo", [128, 16, D], BF16); gluw = sb("gluw", [128, 4, 512], BF16); glub = sb("glub", [128, 4])
        R = [sb("row%d" % i, [128, D]) for i in range(5)]
        wrt = sb("wr", [128, 16, 36]); brt = sb("br", [128, 36]); Ut = sb("U", [128, 128]); ones = sb("ones", [128, 128]); ecap = sb("ecap", [128, NE])
        Srun = sb("Srun", [128, NE]); b_S = Buf("S")
        if ymload is None:
            ym = [sb("ym%d" % i, [128, 16, 128], BF16) for i in range(2)]; b_ym = [Buf(), Buf()]
        else:
            ymbig = sb("ymbig", [128, 16, 1024], BF16); _bym = Buf()
            ym = None; b_ym = [_bym, _bym]
        _xt = sb("xt0", [128, D]); _bxt = Buf()
        xt = [_xt, _xt]; b_xt = [_bxt, _bxt]
        sg = sb("sg", [128, 4, 128]); b_sg = Buf()
        xr = sb("xr", [128, D]); b_xr = Buf()
        xn = xr; b_xn = b_xr
        _x1 = sb("x1_0", [128, D]); _bx1 = Buf()
        x1 = [_x1, _x1]; b_x1 = [_bx1, _bx1]
        h2l = [sb("h2_%d" % i_, [128, D]) for i_ in range(2)]; b_h2l = [Buf(), Buf()]
        hb = [sb("hb%d" % i, [128, D], BF16) for i in range(2)]; b_hb = [Buf(), Buf()]
        h2T = sb("h2T", [128, 16, 128]); b_h2T = Buf()
        st = sb("st", [128, 4, 6]); mv = sb("mv", [128, 2]); rs = sb("rs", [128, 1]); nmr = sb("nmr", [128, 1]); b_s = Buf()
        lg = sb("lg", [128, 36]); rt = sb("rt", [128, 16]); em = sb("em", [128, 32]); em2 = sb("em2", [128, 32])
        oh1 = sb("oh1", [128, 32]); oh2 = sb("oh2", [128, 32]); Mk = sb("Mk", [128, 32]); rank = sb("rank", [128, 32]); t32 = sb("t32", [128, 32]); pen = sb("pen", [128, 4]); ohg = sb("ohg", [128, 4]); eg = sb("eg", [128, 4])
        b_r = Buf("r")
        p_g = ps("p_g", [128, 512]); b_pg = Buf()
        p_o = [ps("p_o%d" % i, [128, 512]) for i in range(2)]; b_po = [Buf(), Buf()]
        p_t = [ps("p_t%d" % i, [128, 512]) for i in range(2)]; b_pt = [Buf(), Buf()]
        p_r = ps("p_r", [128, 512]); b_pr = Buf()
        p_k = ps("p_k", [128, 512]); b_pk = Buf()
        mk.dma("pool", wo[:], w_out.rearrange("(k p) c -> p k c", p=128), writes=[bc])
        mk.dma("pool", gluw[:], glu_w.rearrange("(k p) c -> p k c", p=128), writes=[bc])
        mk.dma("sp", glub[:], glu_b, writes=[bc])
        if rowload is None:
            rowload = lambda dst, ri, bcx: mk.dma("sp", dst[:], rows[ri], writes=[bcx])
        for i, ri in enumerate((0, 1, 2, 3, 4)):
            rowload(R[i], ri, bc)
        G(lambda: nc.gpsimd.tensor_scalar(out=R[0][:], in0=R[0][:], scalar1=1.0, scalar2=None, op0=ALU.add), r=[bc], w=[bc])
        G(lambda: nc.gpsimd.tensor_scalar(out=R[3][:], in0=R[3][:], scalar1=1.0, scalar2=None, op0=ALU.add), r=[bc], w=[bc])
        mk.dma("sp", wrt[:], wr.rearrange("(k p) c -> p k c", p=128), writes=[bc])
        mk.dma("sp", brt[:], br, writes=[bc])
        mk.dma("sp", Ut[:], cst["U"], writes=[bc])
        mk.dma("sp", ecap[:], cst["ecap"], writes=[bc])
        V(lambda: nc.vector.memset(ones[:], 1.0), w=[bc])
        V(lambda: nc.vector.memset(Srun[:], 0.0), w=[b_S])
        zt = hb[0]; b_z = b_hb[0]
        V(lambda: nc.vector.memset(zt[:], 0.0), w=[b_z])
        b_Xg = Buf("Xg")
        XgV = Xg.rearrange("(a p) c -> p a c", p=128)
        for a in range(NE * CAP // 128):
            mk.dma("sp", XgV[:, a, :], zt[:], reads=[b_z], writes=[b_Xg])

        def front(t, tick=lambda: None):
            i = t % 2
            ts_ = slice(t * 128, (t + 1) * 128)
            if ymload is None:
                mk.dma("pool", ym[i][:], ymixT[:, ts_].rearrange("(k p) t -> p k t", p=128), writes=[b_ym[i]])
                tick()
                ymt = ym[i]
            else:
                if t % 8 == 0:
                    ymload(t // 8, ymbig, b_ym[i])
                    tick()
                ymt = ymbig[:, :, (t % 8) * 128:(t % 8 + 1) * 128]
            mk.dma("sp", xt[i][:], x[ts_, :], writes=[b_xt[i]])
            tick()
            for oc in range(4):
                for kc in range(4):
                    PE(lambda: nc.tensor.matmul(p_g[:, oc * 128:(oc + 1) * 128], lhsT=gluw[:, kc, oc * 128:(oc + 1) * 128], rhs=ymt[:, 12 + kc, :], start=(kc == 0), stop=(kc == 3)),
                       r=[bc, b_ym[i]], w=[b_pg])
                    tick()
            for oc in range(4):
                A(lambda: nc.scalar.activation(out=sg[:, oc, :], in_=p_g[:, oc * 128:(oc + 1) * 128], func=AF.Sigmoid, bias=glub[:, oc:oc + 1], scale=1.0), r=[bc], w=[b_pg, b_sg])
                tick()
            V(lambda: nc.vector.tensor_tensor(out=ymt[:, 12:16, :], in0=ymt[:, 12:16, :], in1=sg[:], op=ALU.mult), r=[b_sg], w=[b_ym[i]])
            tick()
            for cc in range(4):
                j = cc % 2
                for k in range(16):
                    PE(lambda: nc.tensor.matmul(p_o[j][:, :], lhsT=ymt[:, k, :], rhs=wo[:, k, cc * 512:(cc + 1) * 512], start=(k == 0), stop=(k == 15)), r=[b_ym[i], bc], w=[b_po[j]])
                    tick()
                V(lambda: nc.vector.tensor_tensor(out=xr[:, cc * 512:(cc + 1) * 512], in0=p_o[j][:, :], in1=R[0][:, cc * 512:(cc + 1) * 512], op=ALU.mult), r=[bc], w=[b_po[j], b_xr])
                tick()
            V(lambda: nc.vector.scalar_tensor_tensor(out=xr[:], in0=xt[i][:], scalar=ALPHA, in1=xr[:], op0=ALU.mult, op1=ALU.add), r=[b_xt[i]], w=[b_xr])
            tick()
            ln_stats(nc, mk, xr, b_xr, st, mv, rs, nmr, b_s)
            tick()
            A(lambda: nc.scalar.activation(out=xn[:], in_=xr[:], func=AF.Identity, bias=nmr[:, 0:1], scale=rs[:, 0:1]), r=[b_s], w=[b_xn])
            tick()
            V(lambda: nc.vector.tensor_tensor(out=xn[:], in0=xn[:], in1=R[1][:], op=ALU.mult), r=[bc], w=[b_xn])
            tick()
            V(lambda: nc.vector.tensor_tensor(out=x1[i][:], in0=xn[:], in1=R[2][:], op=ALU.add), r=[b_xn, bc], w=[b_x1[i]])
            tick()
            mk.dma("sp", x1s[ts_, :], x1[i][:], reads=[b_x1[i]])
            tick()
            ln_stats(nc, mk, x1[i], b_x1[i], st, mv, rs, nmr, b_s)
            tick()
            A(lambda: nc.scalar.activation(out=xn[:], in_=x1[i][:], func=AF.Identity, bias=nmr[:, 0:1], scale=rs[:, 0:1]), r=[b_x1[i], b_s], w=[b_xn])
            tick()
            V(lambda: nc.vector.tensor_tensor(out=xn[:], in0=xn[:], in1=R[3][:], op=ALU.mult), r=[bc], w=[b_xn])
            tick()
            h2 = h2l[i]; b_h2 = b_h2l[i]
            V(lambda: nc.vector.tensor_tensor(out=h2[:], in0=xn[:], in1=R[4][:], op=ALU.add), r=[b_xn, bc], w=[b_h2])
            tick()
            A(lambda: nc.scalar.copy(out=hb[i][:], in_=h2[:]), r=[b_h2], w=[b_hb[i]])
            tick()
        def tail(t):
            i = t % 2
            ts_ = slice(t * 128, (t + 1) * 128)
            h2 = h2l[i]; b_h2 = b_h2l[i]
            for half in range(4):
                j = half % 2
                for kk in range(4):
                    k = half * 4 + kk
                    PE(lambda: nc.tensor.transpose(p_t[j][:, kk * 128:(kk + 1) * 128], h2[:, k * 128:(k + 1) * 128], ident[:, :]), r=[b_h2, bc], w=[b_pt[j]])
                    yield
                if j == 0:
                    A(lambda: nc.scalar.copy(out=h2T[:, half * 4:(half + 1) * 4, :].rearrange("p a b -> p (a b)"), in_=p_t[j][:, :]), r=[], w=[b_pt[j], b_h2T])
                    yield
                else:
                    V(lambda: nc.vector.tensor_copy(out=h2T[:, half * 4:(half + 1) * 4, :].rearrange("p a b -> p (a b)"), in_=p_t[j][:, :]), r=[], w=[b_pt[j], b_h2T])
                    yield
            for k in range(16):
                PE(lambda: nc.tensor.matmul(p_r[:, 0:36], lhsT=h2T[:, k, :], rhs=wrt[:, k, :], start=(k == 0), stop=(k == 15)), r=[b_h2T, bc], w=[b_pr])
                yield
            V(lambda: nc.vector.tensor_tensor(out=lg[:], in0=p_r[:, 0:36], in1=brt[:], op=ALU.add), r=[bc], w=[b_pr, b_r])
            yield
            R_ = lambda fn: V(fn, r=[b_r, bc], w=[b_r])
            R_(lambda: nc.vector.reduce_max(out=rt[:, 0:1], in_=lg[:, 0:4], axis=AX.X))
            yield
            R_(lambda: nc.vector.tensor_scalar(out=ohg[:], in0=lg[:, 0:4], scalar1=rt[:, 0:1], scalar2=None, op0=ALU.is_ge))
            yield
            R_(lambda: nc.vector.tensor_scalar(out=rt[:, 1:2], in0=rt[:, 0:1], scalar1=-1.0, scalar2=None, op0=ALU.mult))
            yield
            A(lambda: nc.scalar.activation(out=eg[:], in_=lg[:, 0:4], func=AF.Exp, bias=rt[:, 1:2], scale=1.0, accum_out=rt[:, 2:3]), r=[b_r], w=[b_r])
            yield
            R_(lambda: nc.vector.reciprocal(out=rt[:, 3:4], in_=rt[:, 2:3]))
            yield
            R_(lambda: nc.vector.tensor_scalar(out=pen[:], in0=ohg[:], scalar1=-1.0, scalar2=1e30, op0=ALU.add, op1=ALU.mult))
            yield
            R_(lambda: nc.vector.tensor_tensor(out=em[:].rearrange("p (g e) -> p g e", e=8), in0=lg[:, 4:36].rearrange("p (g e) -> p g e", e=8),
                                               in1=pen[:].unsqueeze(2).to_broadcast([128, 4, 8]), op=ALU.add))
            yield
            R_(lambda: nc.vector.reduce_max(out=rt[:, 4:5], in_=em[:], axis=AX.X))
            yield
            R_(lambda: nc.vector.tensor_scalar(out=oh1[:], in0=em[:], scalar1=rt[:, 4:5], scalar2=None, op0=ALU.is_ge))
            yield
            R_(lambda: nc.vector.scalar_tensor_tensor(out=em2[:], in0=oh1[:], scalar=-1e30, in1=em[:], op0=ALU.mult, op1=ALU.add))
            yield
            R_(lambda: nc.vector.reduce_max(out=rt[:, 5:6], in_=em2[:], axis=AX.X))
            yield
            R_(lambda: nc.vector.tensor_scalar(out=oh2[:], in0=em2[:], scalar1=rt[:, 5:6], scalar2=None, op0=ALU.is_ge))
            yield
            R_(lambda: nc.vector.tensor_tensor(out=rt[:, 6:7], in0=rt[:, 5:6], in1=rt[:, 4:5], op=ALU.subtract))
            yield
            A(lambda: nc.scalar.activation(out=rt[:, 7:8], in_=rt[:, 6:7], func=AF.Exp), r=[b_r], w=[b_r])
            yield
            R_(lambda: nc.vector.tensor_scalar(out=rt[:, 8:9], in0=rt[:, 7:8], scalar1=1.0, scalar2=None, op0=ALU.add))
            yield
            R_(lambda: nc.vector.reciprocal(out=rt[:, 8:9], in_=rt[:, 8:9]))
            yield
            R_(lambda: nc.vector.tensor_tensor(out=rt[:, 9:10], in0=rt[:, 7:8], in1=rt[:, 8:9], op=ALU.mult))
            yield
            R_(lambda: nc.vector.tensor_tensor(out=Mk[:], in0=oh1[:], in1=oh2[:], op=ALU.add))
            yield
            PE(lambda: nc.tensor.matmul(p_k[:, 0:32], lhsT=Ut[:, :], rhs=Mk[:, :], start=True, stop=True), r=[bc, b_r], w=[b_pk])
            yield
            PE(lambda: nc.tensor.matmul(p_k[:, 32:64], lhsT=ones[:, :], rhs=Mk[:, :], start=True, stop=True), r=[bc, b_r], w=[b_pk])
            yield
            V(lambda: nc.vector.tensor_tensor(out=rank[:], in0=p_k[:, 0:32], in1=Srun[:], op=ALU.add), r=[b_S, b_r], w=[b_pk, b_r])
            yield
            V(lambda: nc.vector.tensor_tensor(out=Srun[:], in0=p_k[:, 32:64], in1=Srun[:], op=ALU.add), r=[b_r], w=[b_pk, b_S])
            yield
            for kx, oh in enumerate((oh1, oh2)):
                R_(lambda: nc.vector.tensor_tensor(out=t32[:], in0=oh[:], in1=rank[:], op=ALU.mult))
                yield
                R_(lambda: nc.vector.reduce_sum(out=rt[:, 10:11], in_=t32[:], axis=AX.X))
                yield
                R_(lambda: nc.vector.tensor_tensor(out=t32[:], in0=oh[:], in1=ecap[:], op=ALU.mult))
                yield
                R_(lambda: nc.vector.reduce_sum(out=rt[:, 11:12], in_=t32[:], axis=AX.X))
                yield
                R_(lambda: nc.vector.tensor_scalar(out=rt[:, 12:13], in0=rt[:, 10:11], scalar1=float(CAP), scalar2=None, op0=ALU.is_ge))
                yield
                R_(lambda: nc.vector.tensor_tensor(out=rt[:, 11:12], in0=rt[:, 11:12], in1=rt[:, 10:11], op=ALU.add))
                yield
                R_(lambda: nc.vector.scalar_tensor_tensor(out=rt[:, 11:12], in0=rt[:, 12:13], scalar=1e6, in1=rt[:, 11:12], op0=ALU.mult, op1=ALU.add))
                yield
                V(lambda: nc.vector.tensor_copy(out=slot[:, t, kx:kx + 1], in_=rt[:, 11:12]), r=[b_r], w=[b_rt])
                yield
                R_(lambda: nc.vector.tensor_scalar(out=rt[:, 13:14], in0=rt[:, 12:13], scalar1=-1.0, scalar2=-1.0, op0=ALU.add, op1=ALU.mult))
                yield
                R_(lambda: nc.vector.tensor_tensor(out=rt[:, 13:14], in0=rt[:, 13:14], in1=rt[:, 3:4], op=ALU.mult))
                yield
                V(lambda: nc.vector.tensor_tensor(out=gw[:, t, kx:kx + 1], in0=rt[:, 13:14], in1=rt[:, 8 + kx:9 + kx], op=ALU.mult), r=[b_r], w=[b_rt])
                yield
                mk.idma(Xg, hb[i][:, :], slot[:, t, kx:kx + 1], True, NE * CAP - 1, reads=[b_hb[i], b_rt], writes=[b_Xg])
                yield
        for t in range(NT):
            if t == 0:
                front(0)
            g = tail(t)
            if t + 1 < NT:
                def tick(_g=g):
                    next(_g, None); next(_g, None); next(_g, None)
                front(t + 1, tick)
            for _ in g:
                pass
        mk.barrier()
    b_Yg = Buf("Yg")
    with ExitStack() as pb:
        sb = lambda n, s, dt=F32: pb.enter_context(nc.sbuf_tensor("k3b%d_" % mk.gen + n, list(s), dt))
        ps = lambda n, s, dt=F32: pb.enter_context(nc.psum_tensor("k3b%d_" % mk.gen + n, list(s), dt))
        W1 = [sb("w1_%d" % i, [128, 16, DE], BF16) for i in range(2)]
        W3 = [sb("w3_%d" % i, [128, 16, DE], BF16) for i in range(2)]
        W2 = [sb("w2_%d" % i, [128, 4, D], BF16) for i in range(2)]
        b_w = [Buf(), Buf()]
        NTL = CAP // 128
        xg = sb("xg", [128, NTL, D], BF16); b_xg = Buf()
        xgT = sb("xgT", [128, 16, CAP], BF16); b_xgT = Buf()
        ga = sb("ga", [128, CAP]); b_ga = Buf()
        gh = sb("gh", [128, 4, CAP], BF16); b_gh = Buf()
        yo = [sb("yo%d" % i, [128, D]) for i in range(2)]; b_yo = [Buf(), Buf()]
        p_t = [ps("p_t%d" % i, [128, 1024], BF16) for i in range(2)]; b_pt = [Buf(), Buf()]
        p_a = ps("p_a", [128, 512]); p_b = ps("p_b", [128, 512]); b_pa, b_pb = Buf(), Buf()
        p_o = [ps("p_o%d" % i, [128, 512]) for i in range(2)]; b_po = [Buf(), Buf()]

        def load_w(e):
            j = e % 2
            mk.dma("pool", W1[j][:], w1[e].rearrange("(p k) c -> p k c", k=16), writes=[b_w[j]])
            mk.dma("pool", W3[j][:], w3[e].rearrange("(p k) c -> p k c", k=16), writes=[b_w[j]])
            mk.dma("pool", W2[j][:], w2[e].rearrange("(k p) c -> p k c", p=128), writes=[b_w[j]])

        xgl = [xg, sb("xg_b", [128, NTL, D], BF16)]; b_xgl = [b_xg, Buf()]
        xgTl = [xgT, sb("xgT_b", [128, 16, CAP], BF16)]; b_xgTl = [b_xgT, Buf()]

        def prep(e):
            q_ = e % 2
            xg_, bxg_, xgT_, bxgT_ = xgl[q_], b_xgl[q_], xgTl[q_], b_xgTl[q_]
            mk.dma("sp", xg_[:], Xg[e * CAP:(e + 1) * CAP, :].rearrange("(a p) c -> p a c", p=128), reads=[b_Xg], writes=[bxg_])
            yield
            for a in range(NTL):
                for half in range(2):
                    for kk in range(8):
                        k = half * 8 + kk
                        PE(lambda: nc.tensor.transpose(p_t[half][:, kk * 128:(kk + 1) * 128], xg_[:, a, :].rearrange("t (p k) -> t k p", k=16)[:, k, :], identb[:, :]), r=[bxg_, bc], w=[b_pt[half]])
                    if half == 0:
                        A(lambda: nc.scalar.copy(out=xgT_[:, 0:8, a * 128:(a + 1) * 128], in_=p_t[half][:, :].rearrange("p (k t) -> p k t", t=128)), r=[], w=[b_pt[half], bxgT_])
                    else:
                        V(lambda: nc.vector.tensor_copy(out=xgT_[:, 8:16, a * 128:(a + 1) * 128], in_=p_t[half][:, :].rearrange("p (k t) -> p k t", t=128)), r=[], w=[b_pt[half], bxgT_])
                    yield

        load_w(0)
        yoi = 0
        for _ in prep(0):
            pass
        for e in range(NE):
            j = e % 2
            if e + 1 < NE:
                load_w(e + 1)
            nx = prep(e + 1) if e + 1 < NE else None

            def tick():
                if nx is not None:
                    next(nx, None)
            xgT_ = xgTl[e % 2]; bxgT_ = b_xgTl[e % 2]
            for hc in range(4):
                for k in range(16):
                    PE(lambda: nc.tensor.matmul(p_a[:, 0:CAP], lhsT=W1[j][:, k, hc * 128:(hc + 1) * 128], rhs=xgT_[:, k, :], start=(k == 0), stop=(k == 15)), r=[b_w[j], bxgT_], w=[b_pa])
                for k in range(16):
                    PE(lambda: nc.tensor.matmul(p_b[:, 0:CAP], lhsT=W3[j][:, k, hc * 128:(hc + 1) * 128], rhs=xgT_[:, k, :], start=(k == 0), stop=(k == 15)), r=[b_w[j], bxgT_], w=[b_pb])
                tick()
                A(lambda: nc.scalar.activation(out=ga[:], in_=p_a[:, 0:CAP], func=AF.Silu), r=[], w=[b_pa, b_ga])
                V(lambda: nc.vector.tensor_tensor(out=gh[:, hc, :], in0=p_b[:, 0:CAP], in1=ga[:], op=ALU.mult), r=[b_ga], w=[b_pb, b_gh])
            for a in range(NTL):
                y_ = yoi % 2
                yoi += 1
                for cc in range(4):
                    q = cc % 2
                    for hc in range(4):
                        PE(lambda: nc.tensor.matmul(p_o[q][:, :], lhsT=gh[:, hc, a * 128:(a + 1) * 128], rhs=W2[j][:, hc, cc * 512:(cc + 1) * 512], start=(hc == 0), stop=(hc == 3)), r=[b_gh, b_w[j]], w=[b_po[q]])
                    if q == 0:
                        A(lambda: nc.scalar.copy(out=yo[y_][:, cc * 512:(cc + 1) * 512], in_=p_o[q][:, :]), r=[], w=[b_po[q], b_yo[y_]])
                    else:
                        V(lambda: nc.vector.tensor_copy(out=yo[y_][:, cc * 512:(cc + 1) * 512], in_=p_o[q][:, :]), r=[], w=[b_po[q], b_yo[y_]])
                r0 = e * CAP + a * 128
                mk.dma("sp", Yg[r0:r0 + 128, :], yo[y_][:], reads=[b_yo[y_]], writes=[b_Yg])
            if nx is not None:
                for _ in nx:
                    pass
        mk.barrier()
    with ExitStack() as pc:
        sb = lambda n, s, dt=F32: pc.enter_context(nc.sbuf_tensor("k3c%d_" % mk.gen + n, list(s), dt))
        R = [sb("row%d" % i, [128, D]) for i in range(3)]
        for i, ri in enumerate((5, 6, 7)):
            rowload(R[i], ri, bc)
        G(lambda: nc.gpsimd.tensor_scalar(out=R[0][:], in0=R[0][:], scalar1=1.0, scalar2=None, op0=ALU.add), r=[bc], w=[bc])
        Y = [[sb("Y%d_%d" % (k, i), [128, D]) for i in range(2)] for k in range(2)]
        b_Y = [[Buf(), Buf()], [Buf(), Buf()]]
        for k in range(2):
            for i in range(2):
                V(lambda: nc.vector.memset(Y[k][i][:], 0.0), w=[b_Y[k][i]])
        x1t = [sb("x1t%d" % i, [128, D]) for i in range(2)]; b_x1 = [Buf(), Buf()]
        yml = [sb("ym%d" % i, [128, D]) for i in range(2)]; b_yml = [Buf(), Buf()]
        xnl = [sb("xn%d" % i, [128, D]) for i in range(2)]; b_xnl = [Buf(), Buf()]
        ot = [sb("ot%d" % i, [128, D]) for i in range(2)]; b_ot = [Buf(), Buf()]
        stl = [sb("st%d" % i, [128, 4, 6]) for i in range(2)]; mvl = [sb("mv%d" % i, [128, 2]) for i in range(2)]
        rsl = [sb("rs%d" % i, [128, 1]) for i in range(2)]; nmrl = [sb("nmr%d" % i, [128, 1]) for i in range(2)]; b_sl = [Buf(), Buf()]
        for t in range(NT):
            i = t % 2
            ts_ = slice(t * 128, (t + 1) * 128)
            ym = yml[i]; b_ym = b_yml[i]; xn = xnl[i]; b_xn = b_xnl[i]
            st, mv, rs, nmr, b_s = stl[i], mvl[i], rsl[i], nmrl[i], b_sl[i]
            for k in range(2):
                mk.idma(Y[k][i][:, :], Yg, slot[:, t, k:k + 1], False, NE * CAP - 1, reads=[b_Yg, b_rt], writes=[b_Y[k][i]])
            mk.dma("sp", x1t[i][:], x1s[ts_, :], writes=[b_x1[i]])
            A(lambda: nc.scalar.activation(out=ym[:], in_=Y[0][i][:], func=AF.Copy, scale=gw[:, t, 0:1]), r=[b_Y[0][i], b_rt], w=[b_ym])
            V(lambda: nc.vector.scalar_tensor_tensor(out=ym[:], in0=Y[1][i][:], scalar=gw[:, t, 1:2], in1=ym[:], op0=ALU.mult, op1=ALU.add), r=[b_Y[1][i], b_rt], w=[b_ym])
            V(lambda: nc.vector.tensor_tensor(out=ym[:], in0=ym[:], in1=R[0][:], op=ALU.mult), r=[bc], w=[b_ym])
            V(lambda: nc.vector.scalar_tensor_tensor(out=ym[:], in0=x1t[i][:], scalar=ALPHA, in1=ym[:], op0=ALU.mult, op1=ALU.add), r=[b_x1[i]], w=[b_ym])
            ln_stats(nc, mk, ym, b_ym, st, mv, rs, nmr, b_s)
            A(lambda: nc.scalar.activation(out=xn[:], in_=ym[:], func=AF.Identity, bias=nmr[:, 0:1], scale=rs[:, 0:1]), r=[b_ym, b_s], w=[b_xn])
            V(lambda: nc.vector.tensor_tensor(out=xn[:], in0=xn[:], in1=R[1][:], op=ALU.mult), r=[bc], w=[b_xn])
            V(lambda: nc.vector.tensor_tensor(out=ot[i][:], in0=xn[:], in1=R[2][:], op=ALU.add), r=[b_xn, bc], w=[b_ot[i]])
            mk.dma("sp", xout[ts_, :], ot[i][:], reads=[b_ot[i]], is_output=True)
        mk.barrier()


def build_k3():
    nc = bass.Bass("TRN2", target_bir_lowering=False)
    dt = lambda n, s, k="ExternalInput", d=F32: nc.dram_tensor(n, s, d, kind=k).ap()
    ymixT = dt("ymixT", [D, TOK]); x = dt("x", [TOK, D]); w_out = dt("w_out", [D, D]); glu_w = dt("glu_w", [512, 512]); glu_b = dt("glu_b", [128, 4])
    rows = dt("rows", [8, 128, D]); wr = dt("wr", [D, 36]); br = dt("br", [128, 36])
    w1 = dt("w1", [NE, D, DE]); w3 = dt("w3", [NE, D, DE]); w2 = dt("w2", [NE, DE, D])
    cst = {"U": dt("U", [128, 128]), "ecap": dt("ecap", [128, NE]), "ident": dt("ident", [128, 128])}
    x1s = dt("x1s", [TOK, D], "Internal"); Xg = dt("Xg", [NE * CAP, D], "Internal", BF16); Yg = dt("Yg", [NE * CAP, D], "Internal")
    xout = dt("xout", [TOK, D], "ExternalOutput")
    with ExitStack() as ctx:
        mk = MK(nc, ctx)
        emit_k3(nc, mk, ymixT, x, w_out, glu_w, glu_b, rows, wr, br, w1, w3, w2, cst, x1s, Xg, Yg, xout)
        mk.finish("sp")
        print("k3 ops", mk.nops, "waits", mk.nwaits)
    return nc


def k3_host_inputs(prm, mod, l, b):
    rep = lambda v: np.ascontiguousarray(np.broadcast_to(v[None, :], (128, v.shape[0])))
    sh1, sc1, gt1, sh2, sc2, gt2 = [mod[l, b, i * D:(i + 1) * D] for i in range(6)]
    rows = np.stack([rep(gt1), rep(prm["ln_g"][l, 0]), rep(prm["ln_b"][l, 0]), rep(sc2), rep(sh2), rep(gt2), rep(prm["ln_g"][l, 1]), rep(prm["ln_b"][l, 1])])
    wr = np.ascontiguousarray(np.concatenate([prm["router_group_w"][l], prm["router_expert_w"][l]], axis=1))
    br = rep(np.concatenate([prm["router_group_b"][l], prm["router_expert_b"][l]]))
    d = {"rows": rows, "wr": wr, "br": br, "w_out": prm["w_out"][l], "glu_w": prm["s5_glu_w"][l],
         "glu_b": np.ascontiguousarray(prm["s5_glu_b"][l].reshape(4, 128).T),
         "w1": prm["moe_w1"][l], "w3": prm["moe_w3"][l], "w2": prm["moe_w2"][l]}
    d.update(k3_consts())
    return d


G_ = 512
RW_OFF = 3 * G_
RW_COLS = 3 * G_ + 96 + 96 + 128
ATT_OFF = RW_OFF + RW_COLS
S5_OFF = ATT_OFF + 512 + 2 * 128
SEQ = 8192
RG4 = [[0, 1, 2, 3], [4, 5, 6, 7]]
NMINE = 1472
MODC = 3072


def emit_k0f(nc, mk, cT, w, bb, modin):
    ct = mk.sb("k0_ct", [128, 16, 2]); sct = mk.sb("k0_sct", [128, 16, 2])
    wt = [mk.sb("k0_wt%d" % i, [128, 16, 512]) for i in range(2)]
    bt = mk.sb("k0_bt", [2, 2, MODC]); ot = mk.sb("k0_ot", [2, 2, MODC])
    P = [mk.ps("k0_P%d" % i, [2, 512]) for i in range(2)]
    b_c, b_b, b_o = Buf(), Buf(), Buf()
    b_w = [Buf(), Buf()]; b_p = [Buf(), Buf()]
    mk.dma("sp", ct[:], cT, writes=[b_c])
    mk.dma("sp", bt[:], bb.rearrange("l b c -> b l c"), writes=[b_b])
    mk.op("act", lambda: nc.scalar.activation(out=sct[:], in_=ct[:], func=AF.Silu), reads=[b_c], writes=[b_c])
    it = 0
    for l in range(2):
        for n in range(MODC // 512):
            i = it % 2
            it += 1
            mk.dma("sp", wt[i][:], w[l, :, n * 512:(n + 1) * 512].rearrange("(k p) c -> p k c", p=128), writes=[b_w[i]])
            for k in range(16):
                mk.op("pe", lambda: nc.tensor.matmul(P[i][:], lhsT=sct[:, k, :], rhs=wt[i][:, k, :], start=(k == 0), stop=(k == 15)),
                      reads=[b_c, b_w[i]], writes=[b_p[i]], skip_same=True)
            mk.op("dve", lambda: nc.vector.tensor_tensor(out=ot[:, l, n * 512:(n + 1) * 512], in0=P[i][:], in1=bt[:, l, n * 512:(n + 1) * 512], op=ALU.add),
                  reads=[b_b], writes=[b_p[i], b_o])
    mk.dma("sp", modin.rearrange("(o l) c -> o l c", o=1), ot[0:1, :, :], reads=[b_o])


def mod_row_load(nc, mk, dst, modall, l, chunk, bc):
    c0 = chunk * 2048
    done = 0
    while done < 2048:
        col = c0 + done
        r = col // MODC
        off = col % MODC
        n = min(2048 - done, MODC - off)
        src = modall[r * 2 + l:r * 2 + l + 1, off:off + n].partition_broadcast(128)
        mk.dma("sp", dst[:, done:done + n], src, writes=[bc])
        done += n


def emit_k1a(nc, mk, x, modall, l, ident_d, hTs):
    NT_ = TOK // 128
    xt = [mk.sb("a_xt%d" % i, [128, D]) for i in range(2)]
    xn = mk.sb("a_xn", [128, D]); h1 = mk.sb("a_h1", [128, D])
    hb = [mk.sb("a_hb%d" % i, [128, D], BF16) for i in range(2)]
    hT = mk.sb("a_hT", [128, 16, TOK], BF16)
    sct = mk.sb("a_sct", [128, D]); sht = mk.sb("a_sht", [128, D])
    idf = mk.sb("a_idf", [128, 128]); idb = mk.sb("a_idb", [128, 128], BF16)
    st = mk.sb("a_st", [128, 4, 6]); mv = mk.sb("a_mv", [128, 2]); rs = mk.sb("a_rs", [128, 1]); nmr = mk.sb("a_nmr", [128, 1])
    PT = [mk.ps("a_PT%d" % i, [128, 8, 128], BF16) for i in range(2)]
    b_x = [Buf(), Buf()]
    b_xn, b_h1, b_s, b_sc, b_sh, b_id, b_hT = Buf(), Buf(), Buf(), Buf(), Buf(), Buf(), Buf()
    b_hb = [Buf(), Buf()]; b_pt = [Buf(), Buf()]
    mod_row_load(nc, mk, sct, modall, l, 1, b_sc)
    mod_row_load(nc, mk, sht, modall, l, 0, b_sh)
    mk.dma("sp", idf[:], ident_d, writes=[b_id])
    mk.op("dve", lambda: nc.vector.tensor_copy(out=idb[:], in_=idf[:]), reads=[b_id], writes=[b_id])
    mk.op("pool", lambda: nc.gpsimd.tensor_scalar(out=sct[:], in0=sct[:], scalar1=1.0, scalar2=None, op0=ALU.add), reads=[b_sc], writes=[b_sc])
    def ln_part(t):
        i = t % 2
        mk.dma("sp", xt[i][:], x[t * 128:(t + 1) * 128, :], writes=[b_x[i]])
        ln_stats(nc, mk, xt[i], b_x[i], st, mv, rs, nmr, b_s)
        mk.op("act", lambda: nc.scalar.activation(out=xn[:], in_=xt[i][:], func=AF.Identity, bias=nmr[:, 0:1], scale=rs[:, 0:1]),
              reads=[b_x[i], b_s], writes=[b_xn])
        mk.op("dve", lambda: nc.vector.tensor_tensor(out=h1[:], in0=xn[:], in1=sct[:], op=ALU.mult), reads=[b_xn, b_sc], writes=[b_h1])
        mk.op("dve", lambda: nc.vector.tensor_tensor(out=hb[i][:], in0=h1[:], in1=sht[:], op=ALU.add), reads=[b_h1, b_sh], writes=[b_hb[i]])
    def tr_part(t):
        i = t % 2
        for half in range(2):
            for kk in range(8):
                k = half * 8 + kk
                mk.op("pe", lambda: nc.tensor.transpose(PT[half][:, kk, :], hb[i][:, k * 128:(k + 1) * 128], idb[:]),
                      reads=[b_hb[i], b_id], writes=[b_pt[half]], skip_same=True)
            if half == 0:
                mk.op("act", lambda: nc.scalar.copy(out=hT[:, 0:8, t * 128:(t + 1) * 128], in_=PT[half][:]), reads=[], writes=[b_pt[half], b_hT])
            else:
                mk.op("dve", lambda: nc.vector.tensor_copy(out=hT[:, 8:16, t * 128:(t + 1) * 128], in_=PT[half][:]), reads=[], writes=[b_pt[half], b_hT])
    ln_part(0)
    for t in range(NT_):
        if t + 1 < NT_:
            ln_part(t + 1)
        tr_part(t)
    mk.dma("sp", hTs.rearrange("(k p) t -> p k t", p=128), hT[:], reads=[b_hT])


def emit_k1b(nc, mk, hTg, wmine, pmine):
    NB_ = (NMINE + 127) // 128
    wt = mk.sb("b_wt", [128, 16, NB_ * 128], BF16); b_w = Buf()
    ht = [mk.sb("b_ht%d" % i, [128, 16, 512], BF16) for i in range(2)]; b_h = [Buf(), Buf()]
    ot = [mk.sb("b_ot%d" % i, [128, 512]) for i in range(4)]; b_o = [Buf() for _ in range(4)]
    PM = [mk.ps("b_PM%d" % i, [128, 512]) for i in range(4)]; b_pm = [Buf() for _ in range(4)]
    for j in range(NB_):
        c0 = j * 128
        cw = min(128, NMINE - c0)
        mk.dma("pool", wt[:, :, c0:c0 + cw], wmine[:, c0:c0 + cw].rearrange("(k p) c -> p k c", p=128), writes=[b_w])
    pi = 0
    for tc in range(SEQ // 512):
        i = tc % 2
        r = tc // 4
        t0 = (tc % 4) * 512
        src = hTg.rearrange("(c r h p) t -> r p c h t", c=8, r=4, h=2, p=128)[r]
        for c in range(8):
            mk.dma("sp", ht[i][:, 2 * c:2 * c + 2, :], src[:, c, :, t0:t0 + 512], writes=[b_h[i]])
        for j in range(NB_):
            c0 = j * 128
            cw = min(128, NMINE - c0)
            q = pi % 4
            pi += 1
            for k in range(16):
                mk.op("pe", lambda: nc.tensor.matmul(PM[q][0:cw, :], lhsT=wt[:, k, c0:c0 + cw], rhs=ht[i][:, k, :], start=(k == 0), stop=(k == 15)),
                      reads=[b_w, b_h[i]], writes=[b_pm[q]], skip_same=True)
            if q % 2 == 0:
                mk.op("act", lambda: nc.scalar.copy(out=ot[q][0:cw, :], in_=PM[q][0:cw, :]), reads=[], writes=[b_pm[q], b_o[q]])
            else:
                mk.op("dve", lambda: nc.vector.tensor_copy(out=ot[q][0:cw, :], in_=PM[q][0:cw, :]), reads=[], writes=[b_pm[q], b_o[q]])
            mk.dma("sp", pmine[c0:c0 + cw, tc * 512:(tc + 1) * 512], ot[q][0:cw, :], reads=[b_o[q]])


def build_fused():
    nc = bass.Bass("TRN2", target_bir_lowering=False)
    T = SEQ
    din = lambda n, s, d=F32: nc.dram_tensor(n, s, d, kind="ExternalInput").ap()
    scr = lambda n, s, d=F32: nc.dram_tensor(n, s, d).ap()
    x_in = din("x", [TOK, D]); cT = din("cT", [128, 16, 2]); w_ada = din("w_ada", [2, D, MODC]); bb = din("bb", [2, 2, MODC])
    wmine = din("wmine", [2, D, NMINE]); w_out = din("w_out", [2, D, D]); lnp = din("lnp", [2, 4, D])
    cw = din("cw", [2, 128, 3]); btab = din("btab", [2, 2, 128, 2, 256]); sinkt = din("sinkt", [2, 128, 2]); ident = din("ident", [128, 128])
    s5par = din("s5par", [2, 128, 4, 3]); s5bb = din("s5bb", [2, 128, 4, 2, 16]); s5cc = din("s5cc", [2, 128, 4, 2, 16]); s5d = din("s5d", [2, 128, 1]); s5iota = din("s5iota", [128, CS])
    par64 = din("par64", [2, 64, 2, 11]); par128 = din("par128", [2, 128, 3]); w2 = din("w2", [2, 96, 128]); a2 = din("a2", [2, 96, 128]); g2 = din("g2", [2, 128, 128]); gnt = din("gnt", [2, 64, 2, 2, 64])
    cst = {"mask1": din("mask1", [64, 512]), "mask3": din("mask3", [64, 256]), "seg": din("seg", [128, TB]), "ident": ident, "U": din("U", [128, 128]), "ecap": din("ecap", [128, NE])}
    glu_w = din("glu_w", [2, 512, 512]); glu_b = din("glu_b", [2, 128, 4]); wr = din("wr", [2, D, 36]); br = din("br", [2, 128, 36])
    w1 = din("w1", [2, NE, D, DE]); w3 = din("w3", [2, NE, D, DE]); w2m = din("w2m", [2, NE, DE, D])
    ymidx_d = din("ymidx", [128, 16, 2], I32)
    y_out = nc.dram_tensor("y", [TOK, D], F32, kind="ExternalOutput").ap()
    modin = scr("modin", [2, MODC]); modall = scr("modall", [8, MODC])
    hTs = scr("hTs", [D, TOK], BF16); hTg = scr("hTg", [4 * D, TOK], BF16)
    pmine = scr("pmine", [NMINE, T])
    yT16 = scr("yT16", [512, T], BF16); ymg = scr("ymg", [2048, T], BF16)
    x1s = scr("x1s", [TOK, D]); Xg = scr("Xg", [NE * CAP, D], BF16); Yg = scr("Yg", [NE * CAP, D]); xcur = scr("xcur", [TOK, D])
    ymg_rows = ymg.rearrange("r (tb t) -> (r tb) t", t=1024)
    with ExitStack() as ctx:
        mk = MK(nc, ctx)
        bD = Buf("dram")
        with mk.scope():
            emit_k0f(nc, mk, cT, w_ada, bb, modin)
        mk.collective("AllGather", RG4, modin, modall, reads=[bD], writes=[bD])
        mk.barrier()
        for l in range(2):
            xsrc = x_in if l == 0 else xcur
            xdst = xcur if l == 0 else y_out
            with mk.scope():
                emit_k1a(nc, mk, xsrc, modall, l, ident, hTs)
            for c in range(8):
                mk.collective("AllGather", RG4, hTs[c * 256:(c + 1) * 256, :], hTg[c * 1024:(c + 1) * 1024, :], reads=[bD], writes=[bD])
            mk.barrier()
            with mk.scope():
                emit_k1b(nc, mk, hTg, wmine[l], pmine)
            def ag(chunks):
                for c in chunks:
                    mk.collective("AllGather", RG4, yT16[c * 64:(c + 1) * 64, :], ymg[c * 256:(c + 1) * 256, :], reads=[], writes=[Buf()])
            with mk.scope():
                emit_conv(nc, mk, T, pmine[0:384, :], cw[l], yT16[0:128, :], odt=BF16)
            ag((0, 1))
            with mk.scope():
                emit_attn(nc, mk, T, pmine[1088:1344, :], btab[l], sinkt[l], ident, yT16[256:384, :], odt=BF16)
            ag((4, 5))
            with mk.scope():
                emit_s5(nc, mk, T, pmine[1344:1472, :], s5par[l], s5bb[l], s5cc[l], s5d[l], s5iota, ident, yT16[384:512, :], odt=BF16)
            ag((6, 7))
            with mk.scope():
                emit_rwkv(nc, mk, T, pmine[384:1088, :], par64[l], par128[l], w2[l], a2[l], g2[l], gnt[l], cst, yT16[128:256, :], odt=BF16)
            for c in (2, 3):
                mk.collective("AllGather", RG4, yT16[c * 64:(c + 1) * 64, :], ymg[c * 256:(c + 1) * 256, :], reads=[bD], writes=[bD])
            mk.barrier()
            with mk.scope():
                ymidx = mk.sb("ymidx_sb", [128, 16, 2], I32); b_idx = Buf()
                mk.dma("sp", ymidx[:], ymidx_d, writes=[b_idx])

                def ymload(hf, ymt, b_ymt):
                    for k in range(16):
                        mk.idma(ymt[:, k, :], ymg_rows, ymidx[:, k, hf:hf + 1], False, 2048 * 8 - 1, reads=[b_idx], writes=[b_ymt])

                def rowload(dst, ri, bcx, _l=l):
                    if ri in (1, 2, 6, 7):
                        j = {1: 0, 2: 1, 6: 2, 7: 3}[ri]
                        mk.dma("sp", dst[:], lnp[_l, j:j + 1, :].partition_broadcast(128), writes=[bcx])
                    else:
                        chunk = {0: 2, 3: 4, 4: 3, 5: 5}[ri]
                        mod_row_load(nc, mk, dst, modall, _l, chunk, bcx)

                emit_k3(nc, mk, None, xsrc, w_out[l], glu_w[l], glu_b[l], None, wr[l], br[l], w1[l], w3[l], w2m[l], cst, x1s, Xg, Yg, xdst,
                        ymload=ymload, rowload=rowload)
        mk.finish("sp")
        mk.barrier()
        print("fused ops", mk.nops, "waits", mk.nwaits)
    return nc


_NC_CACHE = {}


def _get(name, fn):
    if name not in _NC_CACHE:
        _NC_CACHE[name] = fn()
    return _NC_CACHE[name]


def _fused_inputs(prm, core):
    b, q = core // 4, core % 4
    eye = np.eye(128, dtype=np.float32)
    d = {}
    d["x"] = np.ascontiguousarray(prm["x"][b, q * TOK:(q + 1) * TOK])
    cb = prm["c"][b]
    d["cT"] = np.ascontiguousarray(np.stack([cb.reshape(16, 128).T, cb.reshape(16, 128).T], axis=-1))
    sl = slice(q * MODC, (q + 1) * MODC)
    d["w_ada"] = np.ascontiguousarray(prm["w_ada"][:, :, sl])
    d["bb"] = np.ascontiguousarray(np.broadcast_to(prm["b_ada"][:, None, sl], (2, 2, MODC)))
    kv = q // 2
    cols = np.concatenate([np.arange(128 * q, 128 * q + 128), np.arange(G_ + 128 * q, G_ + 128 * q + 128), np.arange(2 * G_ + 128 * q, 2 * G_ + 128 * q + 128),
                           RW_OFF + rwkv_rows(q),
                           np.arange(ATT_OFF + 128 * q, ATT_OFF + 128 * q + 128), np.arange(ATT_OFF + 512 + 64 * kv, ATT_OFF + 512 + 64 * kv + 64),
                           np.arange(ATT_OFF + 640 + 64 * kv, ATT_OFF + 640 + 64 * kv + 64),
                           np.arange(S5_OFF + 128 * q, S5_OFF + 128 * q + 128)])
    assert cols.shape[0] == NMINE
    d["wmine"] = np.ascontiguousarray(prm["w_in"][:, :, cols])
    d["w_out"] = prm["w_out"]
    d["lnp"] = np.ascontiguousarray(np.stack([np.stack([prm["ln_g"][l, 0], prm["ln_b"][l, 0], prm["ln_g"][l, 1], prm["ln_b"][l, 1]]) for l in range(2)]))
    d["cw"] = np.ascontiguousarray(np.stack([prm["conv_w"][l][:, 128 * q:128 * q + 128].T for l in range(2)]))
    tabs = [attn_tables(prm["rel_bias"], prm["attn_sinks"][l], q) for l in range(2)]
    d["btab"] = np.ascontiguousarray(np.stack([t[0] for t in tabs])); d["sinkt"] = np.ascontiguousarray(np.stack([t[1] for t in tabs]))
    d["ident"] = eye
    s5 = [s5_host_inputs(prm, l, q) for l in range(2)]
    for k_ in ("s5par", "s5bb", "s5cc", "s5d"):
        d[k_] = np.ascontiguousarray(np.stack([s[k_] for s in s5]))
    d["s5iota"] = s5[0]["s5iota"]
    rw = [rwkv_host_inputs(prm, l, q) for l in range(2)]
    for k_ in ("par64", "par128", "gnt", "w2", "a2", "g2"):
        d[k_] = np.ascontiguousarray(np.stack([r[k_] for r in rw]))
    for k_ in ("mask1", "mask3", "seg"):
        d[k_] = rw[0][k_]
    kc = k3_consts()
    d["U"] = kc["U"]; d["ecap"] = kc["ecap"]
    d["glu_w"] = prm["s5_glu_w"]
    d["glu_b"] = np.ascontiguousarray(np.stack([prm["s5_glu_b"][l].reshape(4, 128).T for l in range(2)]))
    d["wr"] = np.ascontiguousarray(np.stack([np.concatenate([prm["router_group_w"][l], prm["router_expert_w"][l]], axis=1) for l in range(2)]))
    d["br"] = np.ascontiguousarray(np.stack([np.broadcast_to(np.concatenate([prm["router_group_b"][l], prm["router_expert_b"][l]])[None, :], (128, 36)) for l in range(2)]))
    d["w1"] = prm["moe_w1"]; d["w3"] = prm["moe_w3"]; d["w2m"] = prm["moe_w2"]
    p_ = np.arange(128)[:, None, None]; k_i = np.arange(16)[None, :, None]; t_ = np.arange(2)[None, None, :]
    src_row = ((k_i // 4) * 2 + p_ // 64) * 256 + (k_i % 4) * 64 + p_ % 64
    d["ymidx"] = np.ascontiguousarray((src_row * 8 + q * 2 + t_).astype(np.int32))
    return d


def kernel(**inp):
    prm = {k: np.ascontiguousarray(np.asarray(v, dtype=np.float32)) for k, v in inp.items()}
    cores = list(range(8))
    in_maps = [_fused_inputs(prm, c) for c in cores]
    res = run_bass_kernel_spmd(_get("fused", build_fused), in_maps, core_ids=cores)
    out = np.stack([np.concatenate([res.results[b * 4 + q]["y"] for q in range(4)], axis=0) for b in range(2)])
    return out.astype(np.float32)
```

```python
import numpy as np
from contextlib import ExitStack
import concourse.bass as bass
import concourse.mybir as mybir
from concourse.bass_utils import run_bass_kernel_spmd

F32 = mybir.dt.float32
BF16 = mybir.dt.bfloat16
I32 = mybir.dt.int32
U32 = mybir.dt.uint32
AF = mybir.ActivationFunctionType
ALU = mybir.AluOpType
AX = mybir.AxisListType

EPOCH = 1 << 20


class Buf:
    __slots__ = ("name", "w", "r")

    def __init__(self, name=""):
        self.name = name
        self.w = None
        self.r = {}


class MK:
    def __init__(self, nc, ctx, n_dma_sems=24):
        self.nc = nc
        self.ctx = ctx
        self.eng = {"pe": nc.tensor, "dve": nc.vector, "act": nc.scalar,
                    "pool": nc.gpsimd, "sp": nc.sync}
        self.sem = {}
        self.cnt = {e: 0 for e in self.eng}
        self.known = {e: {} for e in self.eng}
        for e in self.eng:
            self.sem[e] = ctx.enter_context(nc.semaphore("s_" + e))
        self.dma_keys = []
        self.dma_val = {}
        for i in range(n_dma_sems):
            k = ("dma", i)
            self.sem[k] = ctx.enter_context(nc.semaphore("s_dma%d" % i))
            self.dma_keys.append(k)
            self.dma_val[k] = 0
        self.dma_rr = 0
        self.nwaits = 0
        self.nops = 0
        self.out_events = []

    gen = 0

    def sb(self, name, shape, dt=F32):
        return self.ctx.enter_context(self.nc.sbuf_tensor("%s_g%d" % (name, self.gen), list(shape), dt))

    def ps(self, name, shape, dt=F32):
        return self.ctx.enter_context(self.nc.psum_tensor("%s_g%d" % (name, self.gen), list(shape), dt))

    def _wait(self, E, ev):
        if ev is None:
            return
        key, val = ev
        if self.known[E].get(key, 0) >= val:
            return
        self.eng[E].wait_ge(self.sem[key], val)
        self.known[E][key] = val
        self.nwaits += 1

    def _deps(self, E, reads, writes, skip_same=False):
        for b in reads:
            if b.w is not None and not (skip_same and b.w[0] == E):
                self._wait(E, b.w)
        for b in writes:
            if b.w is not None and not (skip_same and b.w[0] == E):
                self._wait(E, b.w)
            for ev in b.r.values():
                if not (skip_same and ev[0] == E):
                    self._wait(E, ev)

    def _mark(self, ev, reads, writes):
        for b in reads:
            b.r[ev[0]] = ev
        for b in writes:
            b.w = ev
            b.r = {}

    def op(self, E, fn, reads=(), writes=(), skip_same=False):
        self._deps(E, reads, writes, skip_same)
        inst = fn()
        self.cnt[E] += 1
        inst.then_inc(self.sem[E], 1)
        ev = (E, self.cnt[E])
        self._mark(ev, reads, writes)
        self.nops += 1
        return ev

    def dma(self, Q, out, in_, reads=(), writes=(), is_output=False, **kw):
        self._deps(Q, reads, writes)
        k = self.dma_keys[self.dma_rr]
        self.dma_rr = (self.dma_rr + 1) % len(self.dma_keys)
        self._wait(Q, (k, self.dma_val[k]) if self.dma_val[k] else None)
        self.dma_val[k] += 16
        inst = self.eng[Q].dma_start(out=out, in_=in_, **kw)
        inst.then_inc(self.sem[k], 16)
        ev = (k, self.dma_val[k])
        self._mark(ev, reads, writes)
        if is_output:
            self.out_events.append(ev)
        self.nops += 1
        return ev

    def finish(self, E="sp"):
        for k in self.dma_keys:
            if self.dma_val[k]:
                self._wait(E, (k, self.dma_val[k]))


def _idma(self, out, in_, idx_ap, scatter, bound, reads=(), writes=(), is_output=False):
    Q = "pool"
    self._deps(Q, reads, writes)
    k = self.dma_keys[self.dma_rr]
    self.dma_rr = (self.dma_rr + 1) % len(self.dma_keys)
    self._wait(Q, (k, self.dma_val[k]) if self.dma_val[k] else None)
    self.dma_val[k] += 16
    off = bass.IndirectOffsetOnAxis(ap=idx_ap, axis=0)
    if not hasattr(self, "_bregs"):
        self._bregs = {}
    if bound not in self._bregs:
        self._bregs[bound] = self.nc.gpsimd.to_reg(bound)
    bound = self._bregs[bound]
    if scatter:
        inst = self.nc.gpsimd.indirect_dma_start(out=out, out_offset=off, in_=in_, in_offset=None, bounds_check=bound, oob_is_err=False)
    else:
        inst = self.nc.gpsimd.indirect_dma_start(out=out, out_offset=None, in_=in_, in_offset=off, bounds_check=bound, oob_is_err=False)
    inst.then_inc(self.sem[k], 16)
    ev = (k, self.dma_val[k])
    self._mark(ev, reads, writes)
    if is_output:
        self.out_events.append(ev)
    self.nops += 1
    return ev


MK.idma = _idma


def _barrier(self):
    for E in self.eng:
        for F in self.eng:
            if self.cnt[F]:
                self._wait(E, (F, self.cnt[F]))
        for k in self.dma_keys:
            if self.dma_val[k]:
                self._wait(E, (k, self.dma_val[k]))
        if getattr(self, "cc_val", 0):
            self._wait(E, ("cc", self.cc_val))


MK.barrier = _barrier


from contextlib import contextmanager


@contextmanager
def _scope(self):
    old = self.ctx
    self.gen += 1
    with ExitStack() as s:
        self.ctx = s
        yield
        self.barrier()
    self.ctx = old


MK.scope = _scope


def _collective(self, kind, rg, in_ap, out_ap, reads=(), writes=()):
    Q = "pool"
    if "cc" not in self.sem:
        self.sem["cc"] = self.ctx.enter_context(self.nc.semaphore("s_cc"))
        self.cc_val = 0
    self._deps(Q, reads, writes)
    self.cc_val += 1
    inst = self.nc.gpsimd.collective_compute(kind, ALU.bypass, replica_groups=rg, ins=[in_ap.opt()], outs=[out_ap.opt()])
    inst.then_inc(self.sem["cc"], 1)
    ev = ("cc", self.cc_val)
    self._mark(ev, reads, writes)
    self.nops += 1
    return ev


MK.collective = _collective


import numpy as np
from contextlib import ExitStack

D = 2048
NIN = 4672
TOK = 2048


def build_k0():
    nc = bass.Bass("TRN2", target_bir_lowering=False)
    NCOL = 1536
    cT = nc.dram_tensor("cT", [128, 16, 2], F32, kind="ExternalInput").ap()
    w = nc.dram_tensor("w", [2, D, NCOL], F32, kind="ExternalInput").ap()
    bb = nc.dram_tensor("bb", [2, 2, NCOL], F32, kind="ExternalInput").ap()
    out = nc.dram_tensor("mod", [2, 2, NCOL], F32, kind="ExternalOutput").ap()
    with ExitStack() as ctx:
        mk = MK(nc, ctx)
        ct = mk.sb("ct", [128, 16, 2])
        sct = mk.sb("sct", [128, 16, 2])
        wt = [mk.sb("wt%d" % i, [128, 16, 512]) for i in range(2)]
        bt = mk.sb("bt", [2, 2, NCOL])
        ot = mk.sb("ot", [2, 2, NCOL])
        P = [mk.ps("P%d" % i, [2, 512]) for i in range(2)]
        b_c, b_b, b_o = Buf(), Buf(), Buf()
        b_w = [Buf(), Buf()]
        b_p = [Buf(), Buf()]
        mk.dma("sp", ct[:], cT, writes=[b_c])
        mk.dma("sp", bt[:], bb.rearrange("l b c -> b l c"), writes=[b_b])
        mk.op("act", lambda: nc.scalar.activation(out=sct[:], in_=ct[:], func=AF.Silu), reads=[b_c], writes=[b_c])
        it = 0
        for l in range(2):
            for n in range(3):
                i = it % 2
                it += 1
                mk.dma("sp", wt[i][:], w[l, :, n * 512:(n + 1) * 512].rearrange("(k p) c -> p k c", p=128), writes=[b_w[i]])
                for k in range(16):
                    mk.op("pe", lambda: nc.tensor.matmul(P[i][:], lhsT=sct[:, k, :], rhs=wt[i][:, k, :], start=(k == 0), stop=(k == 15)),
                          reads=[b_c, b_w[i]], writes=[b_p[i]], skip_same=True)
                mk.op("dve", lambda: nc.vector.tensor_tensor(out=ot[:, l, n * 512:(n + 1) * 512], in0=P[i][:], in1=bt[:, l, n * 512:(n + 1) * 512], op=ALU.add),
                      reads=[b_p[i], b_b], writes=[b_o])
        mk.dma("sp", out.rearrange("l b c -> b l c"), ot[:], reads=[b_o], is_output=True)
        mk.finish("sp")
    return nc


def build_k1():
    nc = bass.Bass("TRN2", target_bir_lowering=False)
    x = nc.dram_tensor("x", [TOK, D], F32, kind="ExternalInput").ap()
    sc = nc.dram_tensor("sc", [128, D], F32, kind="ExternalInput").ap()
    sh = nc.dram_tensor("sh", [128, D], F32, kind="ExternalInput").ap()
    w_in = nc.dram_tensor("w_in", [D, NIN], F32, kind="ExternalInput").ap()
    ident = nc.dram_tensor("ident", [128, 128], F32, kind="ExternalInput").ap()
    pT = nc.dram_tensor("pT", [NIN, TOK], F32, kind="ExternalOutput").ap()
    with ExitStack() as ctx:
        mk = MK(nc, ctx)
        emit_k1(nc, mk, x, sc, sh, w_in, ident, pT)
        mk.finish("sp")
        print("k1 ops", mk.nops, "waits", mk.nwaits)
    return nc


def ln_stats(nc, mk, xt, bx, st, mv, rs, nmr, bs, eps=1e-5):
    for c in range(4):
        mk.op("dve", lambda: nc.vector.bn_stats(out=st[:, c, :], in_=xt[:, c * 512:(c + 1) * 512]), reads=[bx], writes=[bs])
    mk.op("dve", lambda: nc.vector.bn_aggr(out=mv[:], in_=st[:].rearrange("p a b -> p (a b)")), reads=[bs], writes=[bs])
    mk.op("act", lambda: nc.scalar.activation(out=rs[:], in_=mv[:, 1:2], func=AF.Sqrt, bias=eps, scale=1.0), reads=[bs], writes=[bs])
    mk.op("dve", lambda: nc.vector.reciprocal(out=rs[:], in_=rs[:]), reads=[bs], writes=[bs])
    mk.op("dve", lambda: nc.vector.tensor_scalar(out=nmr[:], in0=mv[:, 0:1], scalar1=rs[:, 0:1], scalar2=-1.0, op0=ALU.mult, op1=ALU.mult),
          reads=[bs], writes=[bs])


def emit_k1(nc, mk, x, sc, sh, w_in, ident, pT):
    NT = TOK // 128
    xt = [mk.sb("xt%d" % i, [128, D]) for i in range(2)]
    xn = mk.sb("xn", [128, D])
    h1 = mk.sb("h1", [128, D])
    hb = [mk.sb("hb%d" % i, [128, D], BF16) for i in range(2)]
    hT = mk.sb("hT", [128, 16, TOK], BF16)
    sct = mk.sb("sct", [128, D])
    sht = mk.sb("sht", [128, D])
    idf = mk.sb("idf", [128, 128])
    idb = mk.sb("idb", [128, 128], BF16)
    st = mk.sb("st", [128, 4, 6])
    mv = mk.sb("mv", [128, 2])
    rs = mk.sb("rs", [128, 1])
    nmr = mk.sb("nmr", [128, 1])
    wt = [mk.sb("wt%d" % i, [128, 16, 128], BF16) for i in range(2)]
    ot = [mk.sb("ot%d" % i, [128, TOK]) for i in range(2)]
    PT = [mk.ps("PT%d" % i, [128, 8, 128], BF16) for i in range(2)]
    PM = [mk.ps("PM%d" % i, [128, 512]) for i in range(4)]
    b_x = [Buf(), Buf()]
    b_xn, b_h1, b_s, b_sc, b_sh, b_id, b_hT = Buf(), Buf(), Buf(), Buf(), Buf(), Buf(), Buf()
    b_hb = [Buf(), Buf()]
    b_pt = [Buf(), Buf()]
    b_pm = [Buf() for _ in range(4)]
    b_w = [Buf(), Buf()]
    b_o = [Buf(), Buf()]

    mk.dma("sp", sct[:], sc, writes=[b_sc])
    mk.dma("sp", sht[:], sh, writes=[b_sh])
    mk.dma("sp", idf[:], ident, writes=[b_id])
    mk.op("dve", lambda: nc.vector.tensor_copy(out=idb[:], in_=idf[:]), reads=[b_id], writes=[b_id])
    mk.op("pool", lambda: nc.gpsimd.tensor_scalar(out=sct[:], in0=sct[:], scalar1=1.0, scalar2=None, op0=ALU.add), reads=[b_sc], writes=[b_sc])

    NCB = (NIN + 127) // 128

    def load_w(cb):
        j = cb % 2
        c0 = cb * 128
        cw = min(128, NIN - c0)
        mk.dma("pool", wt[j][:, :, 0:cw], w_in[:, c0:c0 + cw].rearrange("(k p) c -> p k c", p=128), writes=[b_w[j]])

    load_w(0)
    load_w(1)
    for t in range(NT):
        i = t % 2
        mk.dma("sp", xt[i][:], x[t * 128:(t + 1) * 128, :], writes=[b_x[i]])
        ln_stats(nc, mk, xt[i], b_x[i], st, mv, rs, nmr, b_s)
        mk.op("act", lambda: nc.scalar.activation(out=xn[:], in_=xt[i][:], func=AF.Identity, bias=nmr[:, 0:1], scale=rs[:, 0:1]),
              reads=[b_x[i], b_s], writes=[b_xn])
        mk.op("dve", lambda: nc.vector.tensor_tensor(out=h1[:], in0=xn[:], in1=sct[:], op=ALU.mult), reads=[b_xn, b_sc], writes=[b_h1])
        mk.op("pool", lambda: nc.gpsimd.tensor_tensor(out=hb[i][:], in0=h1[:], in1=sht[:], op=ALU.add), reads=[b_h1, b_sh], writes=[b_hb[i]])
        for half in range(2):
            for kk in range(8):
                k = half * 8 + kk
                mk.op("pe", lambda: nc.tensor.transpose(PT[half][:, kk, :], hb[i][:, k * 128:(k + 1) * 128], idb[:]),
                      reads=[b_hb[i], b_id], writes=[b_pt[half]], skip_same=True)
            eng = "act" if half == 0 else "dve"
            if eng == "act":
                mk.op("act", lambda: nc.scalar.copy(out=hT[:, half * 8:(half + 1) * 8, t * 128:(t + 1) * 128], in_=PT[half][:]),
                      reads=[b_pt[half]], writes=[b_hT])
            else:
                mk.op("dve", lambda: nc.vector.tensor_copy(out=hT[:, half * 8:(half + 1) * 8, t * 128:(t + 1) * 128], in_=PT[half][:]),
                      reads=[b_pt[half]], writes=[b_hT])
    pi = 0
    for cb in range(NCB):
        j = cb % 2
        c0 = cb * 128
        cw = min(128, NIN - c0)
        for tc in range(TOK // 512):
            q = pi % 4
            pi += 1
            for k in range(16):
                mk.op("pe", lambda: nc.tensor.matmul(PM[q][0:cw, :], lhsT=wt[j][:, k, 0:cw], rhs=hT[:, k, tc * 512:(tc + 1) * 512],
                                                     start=(k == 0), stop=(k == 15)),
                      reads=[b_w[j], b_hT], writes=[b_pm[q]], skip_same=True)
            if tc % 2 == 0:
                mk.op("act", lambda: nc.scalar.copy(out=ot[j][0:cw, tc * 512:(tc + 1) * 512], in_=PM[q][0:cw, :]), reads=[b_pm[q]], writes=[b_o[j]])
            else:
                mk.op("dve", lambda: nc.vector.tensor_copy(out=ot[j][0:cw, tc * 512:(tc + 1) * 512], in_=PM[q][0:cw, :]), reads=[b_pm[q]], writes=[b_o[j]])
        mk.dma("sp", pT[c0:c0 + cw, :], ot[j][0:cw, :], reads=[b_o[j]], is_output=True)
        if cb + 2 < NCB:
            load_w(cb + 2)


import numpy as np
from contextlib import ExitStack

C = 64
TB = 512
NCH = TB // C


def rwkv_consts():
    s = np.arange(64)[:, None]
    t = np.arange(64)[None, :]
    m_su = (s < t).astype(np.float32)
    m_ui = (s <= t).astype(np.float32)
    m1 = np.concatenate([m_su, m_ui], axis=1)
    mask1 = np.tile(m1, (1, 4))
    m_sl = (t < s).astype(np.float32)
    mask3 = np.tile(m_sl, (1, 4))
    seg = np.ones((128, TB), np.float32)
    seg[:, ::C] = 0.0
    return {"mask1": mask1, "mask3": mask3, "seg": seg, "ident": np.eye(128, dtype=np.float32)}


def emit_rwkv(nc, mk, T, rwin, par64, par128, w2, a2, g2, gnt, cst, yT, odt=F32):
    import os
    LVL = int(os.environ.get("RW_LVL", "9"))
    NB = T // TB
    V = lambda fn, r=(), w=(): mk.op("dve", fn, r, w)
    A = lambda fn, r=(), w=(): mk.op("act", fn, r, w)
    G = lambda fn, r=(), w=(): mk.op("pool", fn, r, w)
    PE = lambda fn, r=(), w=(), ss=True: mk.op("pe", fn, r, w, skip_same=ss)
    import os
    F32R = mybir.dt.float32r
    USE_R = os.environ.get("RW_F32R", "1") == "1"
    RR = (lambda a: a.bitcast(F32R)) if USE_R else (lambda a: a)
    sb = mk.sb
    p64 = sb("rw_p64", [64, 2, 11]); p128 = sb("rw_p128", [128, 3])
    w2t = sb("rw_w2", [96, 128]); a2t = sb("rw_a2", [96, 128]); g2t = sb("rw_g2", [128, 128])
    gn = sb("rw_gn", [64, 2, 2, 64])
    mask1 = sb("rw_mask1", [64, 512]); mask3 = sb("rw_mask3", [64, 256]); seg = sb("rw_seg", [128, TB])
    ident = sb("rw_ident", [128, 128])
    ones64 = sb("rw_ones", [64, 64])
    bc = Buf("const")
    for dst, src in ((p64, par64), (p128, par128), (w2t, w2), (a2t, a2), (g2t, g2), (gn, gnt),
                     (mask1, cst["mask1"]), (mask3, cst["mask3"]), (seg, cst["seg"]), (ident, cst["ident"])):
        mk.dma("sp", dst[:], src, writes=[bc])
    V(lambda: nc.vector.memset(ones64[:], 1.0), w=[bc])
    raw = {}
    for nm in ("r0", "k0", "v0", "r1", "k1", "v1"):
        raw[nm] = sb("rw_raw_" + nm, [64, TB + 1])
    raw["w"] = sb("rw_raw_w", [96, TB + 1]); raw["a"] = sb("rw_raw_a", [96, TB + 1]); raw["g"] = sb("rw_raw_g", [128, TB + 1])
    b_raw = Buf("raw")
    tmp = sb("rw_tmp", [128, TB]); b_tmp = Buf("tmp")
    ws = sb("rw_ws", [96, TB]); as_ = sb("rw_as", [96, TB]); gs = sb("rw_gs", [128, TB]); b_lo = Buf("lo")
    gate = [sb("rw_gate%d" % i_, [128, TB]) for i_ in range(2)]; b_gate = [Buf("gate0"), Buf("gate1")]
    H = []
    for h in range(2):
        d = {}
        for nm in ("rs", "ks", "vs", "lw", "asg", "kkn", "kp", "bv", "cum", "e1", "e2", "BT", "KT", "BH", "KH", "rkr"):
            d[nm] = sb("rw_%s%d" % (nm, h), [64, TB])
        d["AR"] = sb("rw_AR%d" % h, [64, NCH, 128])
        d["cC"] = sb("rw_cC%d" % h, [64, NCH]); d["gC"] = sb("rw_gC%d" % h, [64, NCH])
        d["b"] = Buf("H%d" % h)
        d["bo"] = [Buf("Ho%d_0" % h), Buf("Ho%d_1" % h)]
        d["Vt"] = sb("rw_Vt%d" % h, [64, NCH, 64]); d["BHt"] = sb("rw_BHt%d" % h, [64, NCH, 64]); d["KHt"] = sb("rw_KHt%d" % h, [64, NCH, 64])
        d["bt"] = [Buf("Ht%d_0" % h), Buf("Ht%d_1" % h)]
        for nm_, shp_ in (("AR", [64, NCH, 128]), ("BT", [64, TB]), ("KT", [64, TB]), ("gC", [64, NCH]), ("Vt", [64, NCH, 64]), ("BHt", [64, NCH, 64]), ("KHt", [64, NCH, 64])):
            d[nm_] = [d[nm_], sb("rw_%s%d_b" % (nm_, h), shp_)]
        d["NG"] = sb("rw_NG%d" % h, [64, NCH, 128]); d["LG"] = sb("rw_LG%d" % h, [64, NCH, 128])
        d["L"] = sb("rw_L%d" % h, [64, NCH, 64]); d["bA"] = Buf("A%d" % h)
        d["P"] = [sb("rw_P%d_%d" % (h, i), [64, NCH, 64]) for i in range(2)]
        d["PT"] = [sb("rw_PT%d_%d" % (h, i), [64, NCH, 64]) for i in range(2)]
        d["ST"] = [sb("rw_ST%d_%d" % (h, i), [64, NCH, 64]) for i in range(2)]
        d["bD"] = Buf("D%d" % h)
        d["M"] = sb("rw_M%d" % h, [64, 64]); d["bM"] = Buf("M%d" % h)
        d["X1"] = sb("rw_X1%d" % h, [64, 64]); d["U"] = sb("rw_U%d" % h, [64, 64]); d["bX"] = Buf("X%d" % h); d["bU"] = Buf("U%d" % h)
        H.append(d)
    Yb = sb("rw_Yb", [64, NCH, 2, 64]); b_Y = Buf("Y")
    Ysq = sb("rw_Ysq", [64, NCH, 2, 64])
    st1 = sb("rw_st1", [64, NCH * 2]); st2 = sb("rw_st2", [64, NCH * 2]); st3 = sb("rw_st3", [64, NCH * 2]); b_st = Buf("st")
    sbon = [sb("rw_sbon%d" % i_, [64, NCH, 2]) for i_ in range(2)]; b_sb = [Buf("sbon0"), Buf("sbon1")]
    yo = sb("rw_yo", [128, TB], odt); b_yo = Buf("yo")
    ps_lo = mk.ps("rw_ps_lo", [128, 512]); b_pl = Buf()
    ps_tr = mk.ps("rw_ps_tr", [128, 512]); b_ptr = Buf()
    ps_a1 = mk.ps("rw_ps_a1", [64, 512]); b_pa1 = Buf()
    ps_a2 = mk.ps("rw_ps_a2", [64, 512]); b_pa2 = Buf()
    ps_a3f = mk.ps("rw_ps_a3", [128, 512]); ps_a3 = ps_a3f[0:64, :]; b_pa3 = Buf()
    ps_d = mk.ps("rw_ps_d", [64, 512]); b_pd = Buf()
    ps_d2 = ps_a3; b_pd2 = b_pa3
    ps_sh = [mk.ps("rw_ps_s%d" % h, [64, 512]) for h in range(2)]
    b_psh = [Buf(), Buf()]

    for h in range(2):
        V(lambda: nc.vector.tensor_scalar(out=RR(H[h]["M"][:]), in0=ident[0:64, 0:64], scalar1=0.0, scalar2=None, op0=ALU.mult), r=[bc], w=[H[h]["bM"]])

    rows = {"r0": 0, "r1": 64, "k0": 128, "k1": 192, "v0": 256, "v1": 320, "w": 384, "a": 480, "g": 576}
    nrow = {"r0": 64, "r1": 64, "k0": 64, "k1": 64, "v0": 64, "v1": 64, "w": 96, "a": 96, "g": 128}

    def stage1(blk):
        t0 = blk * TB
        par = blk % 2
        for nm in rows:
            r0, n = rows[nm], nrow[nm]
            if blk == 0:
                V(lambda: nc.vector.memset(raw[nm][0:n, 0:1], 0.0), w=[b_raw])
                yield
                mk.dma("sp", raw[nm][0:n, 1:TB + 1], rwin[r0:r0 + n, 0:TB], writes=[b_raw])
                yield
            else:
                mk.dma("sp", raw[nm][0:n, :], rwin[r0:r0 + n, t0 - 1:t0 + TB], writes=[b_raw])
                yield

        def shift(dst, src, n, mu_ap, bdst):
            V(lambda: nc.vector.tensor_tensor(out=tmp[0:n, :], in0=src[0:n, 0:TB], in1=src[0:n, 1:TB + 1], op=ALU.subtract), r=[b_raw], w=[b_tmp])
            V(lambda: nc.vector.scalar_tensor_tensor(out=dst[0:n, :], in0=tmp[0:n, :], scalar=mu_ap, in1=src[0:n, 1:TB + 1], op0=ALU.mult, op1=ALU.add),
              r=[b_tmp, b_raw, bc], w=[bdst])

        shift(ws, raw["w"], 96, p128[0:96, 0:1], b_lo)
        yield
        shift(as_, raw["a"], 96, p128[0:96, 1:2], b_lo)
        yield
        shift(gs, raw["g"], 128, p128[:, 2:3], b_lo)
        yield
        A(lambda: nc.scalar.activation(out=ws[:], in_=ws[:], func=AF.Tanh), r=[b_lo], w=[b_lo])
        yield
        A(lambda: nc.scalar.activation(out=gs[:], in_=gs[:], func=AF.Sigmoid), r=[b_lo], w=[b_lo])
        yield
        PE(lambda: nc.tensor.matmul(ps_lo[:, :], lhsT=g2t[:, :], rhs=gs[:, :], start=True, stop=True), r=[bc, b_lo], w=[b_pl])
        yield
        A(lambda: nc.scalar.copy(out=gate[par][:], in_=ps_lo[:, :]), r=[b_pl], w=[b_gate[par]])
        yield
        for h in range(2):
            d = H[h]; b = d["b"]; bo = d["bo"][par]
            hs = slice(64 * h, 64 * h + 64)
            shift(d["rs"], raw["r%d" % h], 64, p64[:, h, 0:1], b)
            yield
            shift(d["ks"], raw["k%d" % h], 64, p64[:, h, 1:2], b)
            yield
            shift(d["vs"], raw["v%d" % h], 64, p64[:, h, 2:3], b)
            yield
            PE(lambda: nc.tensor.matmul(ps_lo[0:64, :], lhsT=w2t[:, hs], rhs=ws[:, :], start=True, stop=True), r=[bc, b_lo], w=[b_pl])
            yield
            A(lambda: nc.scalar.activation(out=d["lw"][:], in_=ps_lo[0:64, :], func=AF.Sigmoid, bias=p64[:, h, 3:4], scale=1.0), r=[b_pl, bc], w=[b])
            yield
            V(lambda: nc.vector.tensor_scalar(out=d["lw"][:], in0=d["lw"][:], scalar1=-0.6065306597126334, scalar2=None, op0=ALU.mult), r=[b], w=[b])
            yield
            PE(lambda: nc.tensor.matmul(ps_lo[0:64, :], lhsT=a2t[:, hs], rhs=as_[:, :], start=True, stop=True), r=[bc, b_lo], w=[b_pl])
            yield
            A(lambda: nc.scalar.activation(out=d["asg"][:], in_=ps_lo[0:64, :], func=AF.Sigmoid, bias=p64[:, h, 4:5], scale=1.0), r=[b_pl, bc], w=[b])
            yield
            V(lambda: nc.vector.tensor_scalar(out=d["kkn"][:], in0=d["ks"][:], scalar1=p64[:, h, 5:6], scalar2=None, op0=ALU.mult), r=[b, bc], w=[b])
            yield
            A(lambda: nc.scalar.activation(out=tmp[0:64, :], in_=d["kkn"][:], func=AF.Square), r=[b], w=[b_tmp])
            yield
            PE(lambda: nc.tensor.matmul(ps_lo[0:64, :], lhsT=ones64[:, :], rhs=tmp[0:64, :], start=True, stop=True), r=[bc, b_tmp], w=[b_pl])
            yield
            A(lambda: nc.scalar.activation(out=tmp[0:64, :], in_=ps_lo[0:64, :], func=AF.Sqrt), r=[b_pl], w=[b_tmp])
            yield
            V(lambda: nc.vector.tensor_scalar(out=tmp[0:64, :], in0=tmp[0:64, :], scalar1=1e-12, scalar2=None, op0=ALU.max), r=[b_tmp], w=[b_tmp])
            yield
            V(lambda: nc.vector.reciprocal(out=tmp[0:64, :], in_=tmp[0:64, :]), r=[b_tmp], w=[b_tmp])
            yield
            V(lambda: nc.vector.tensor_tensor(out=d["kkn"][:], in0=d["kkn"][:], in1=tmp[0:64, :], op=ALU.mult), r=[b, b_tmp], w=[b])
            yield
            V(lambda: nc.vector.tensor_scalar(out=tmp[0:64, :], in0=d["asg"][:], scalar1=-1.0, scalar2=p64[:, h, 6:7], op0=ALU.add, op1=ALU.mult), r=[b, bc], w=[b_tmp])
            yield
            V(lambda: nc.vector.scalar_tensor_tensor(out=d["kp"][:], in0=tmp[0:64, :], scalar=1.0, in1=d["ks"][:], op0=ALU.add, op1=ALU.mult), r=[b_tmp, b], w=[b])
            yield
            V(lambda: nc.vector.tensor_tensor(out=d["bv"][:], in0=d["kkn"][:], in1=d["asg"][:], op=ALU.mult), r=[b], w=[b])
            yield
            V(lambda: nc.vector.scalar_tensor_tensor(out=d["rkr"][:], in0=d["rs"][:], scalar=p64[:, h, 7:8], in1=d["kp"][:], op0=ALU.mult, op1=ALU.mult), r=[b, bc], w=[b])
            yield
            V(lambda: nc.vector.tensor_tensor_scan(out=d["cum"][:], data0=seg[0:64, :], data1=d["lw"][:], initial=0.0, op0=ALU.mult, op1=ALU.add), r=[b, bc], w=[b])
            yield
            cum3 = d["cum"][:].rearrange("p (c t) -> p c t", t=C)
            V(lambda: nc.vector.tensor_copy(out=d["cC"][:], in_=cum3[:, :, C - 1]), r=[b], w=[b])
            yield
            A(lambda: nc.scalar.activation(out=d["gC"][par][:], in_=d["cC"][:], func=AF.Exp), r=[b], w=[bo])
            yield
            A(lambda: nc.scalar.activation(out=d["e1"][:], in_=d["cum"][:], func=AF.Exp), r=[b], w=[b])
            yield
            A(lambda: nc.scalar.activation(out=d["e2"][:], in_=d["cum"][:], func=AF.Exp, scale=-1.0), r=[b], w=[b])
            yield
            AR = d["AR"][par]
            V(lambda: nc.vector.tensor_tensor(out=RR(AR[:, :, 64:128]), in0=d["rs"][:].rearrange("p (c t) -> p c t", t=C),
                                              in1=d["e1"][:].rearrange("p (c t) -> p c t", t=C), op=ALU.mult), r=[b], w=[bo])
            yield
            V(lambda: nc.vector.tensor_tensor(out=RR(d["BT"][par][:]), in0=d["bv"][:], in1=d["e2"][:], op=ALU.mult), r=[b], w=[bo])
            yield
            V(lambda: nc.vector.tensor_tensor(out=RR(d["KT"][par][:]), in0=d["kp"][:], in1=d["e2"][:], op=ALU.mult), r=[b], w=[bo])
            yield
            V(lambda: nc.vector.tensor_tensor(out=tmp[0:64, :], in0=d["cum"][:], in1=d["lw"][:], op=ALU.subtract), r=[b], w=[b_tmp])
            yield
            A(lambda: nc.scalar.activation(out=tmp[0:64, :], in_=tmp[0:64, :], func=AF.Exp), r=[b_tmp], w=[b_tmp])
            yield
            V(lambda: nc.vector.scalar_tensor_tensor(out=RR(AR[:, :, 0:64]), in0=d["kkn"][:].rearrange("p (c t) -> p c t", t=C), scalar=-1.0,
                                                     in1=tmp[0:64, :].rearrange("p (c t) -> p c t", t=C), op0=ALU.mult, op1=ALU.mult), r=[b, b_tmp], w=[bo])
            yield
            V(lambda: nc.vector.tensor_tensor(out=tmp[0:64, :].rearrange("p (c t) -> p c t", t=C), in0=d["cC"][:].unsqueeze(2).to_broadcast([64, NCH, C]),
                                              in1=cum3, op=ALU.subtract), r=[b], w=[b_tmp])
            yield
            A(lambda: nc.scalar.activation(out=tmp[0:64, :], in_=tmp[0:64, :], func=AF.Exp), r=[b_tmp], w=[b_tmp])
            yield
            V(lambda: nc.vector.tensor_tensor(out=d["BH"][:], in0=d["bv"][:], in1=tmp[0:64, :], op=ALU.mult), r=[b, b_tmp], w=[b])
            yield
            V(lambda: nc.vector.tensor_tensor(out=d["KH"][:], in0=d["kp"][:], in1=tmp[0:64, :], op=ALU.mult), r=[b, b_tmp], w=[b])
            yield
            for src, dstn in (("vs", "Vt"), ("BH", "BHt"), ("KH", "KHt")):
                for c in range(NCH):
                    PE(lambda: nc.tensor.transpose(ps_tr[0:64, c * 64:(c + 1) * 64], d[src][:, c * C:(c + 1) * C], ident[0:64, 0:64]), r=[b, bc], w=[b_ptr])
                    yield
                A(lambda: nc.scalar.copy(out=RR(d[dstn][par][:].rearrange("p c k -> p (c k)")), in_=ps_tr[0:64, :]), r=[b_ptr], w=[d["bt"][par]])
                yield
            for c in range(NCH):
                PE(lambda: nc.tensor.matmul(ps_tr[0:64, 2 * c:2 * c + 2], lhsT=d["rkr"][:, c * C:(c + 1) * C], rhs=ones64[:, 0:2], start=True, stop=True), r=[b, bc], w=[b_ptr])
                yield
            V(lambda: nc.vector.tensor_copy(out=sbon[par][:, :, h], in_=ps_tr[0:64, 0:2 * NCH:2]), r=[b_ptr], w=[b_sb[par]])
            yield

    def rest(blk, tick):
        t0 = blk * TB
        par = blk % 2
        for h in range(2):
            d = H[h]; b = d["bo"][par]; AR = d["AR"][par]
            tick()
            for half in range(2):
                for cc in range(4):
                    c = half * 4 + cc
                    PE(lambda: nc.tensor.matmul(ps_a1[:, cc * 128:(cc + 1) * 128], lhsT=RR(d["BT"][par][:, c * C:(c + 1) * C]), rhs=RR(AR[:, c, :]), start=True, stop=True), r=[b], w=[b_pa1])
                    PE(lambda: nc.tensor.matmul(ps_a2[:, cc * 128:(cc + 1) * 128], lhsT=RR(d["KT"][par][:, c * C:(c + 1) * C]), rhs=RR(AR[:, c, :]), start=True, stop=True), r=[b], w=[b_pa2])
                    PE(lambda: nc.tensor.matmul(ps_a3[:, cc * 64:(cc + 1) * 64], lhsT=RR(AR[:, c, 0:64]), rhs=RR(d["BT"][par][:, c * C:(c + 1) * C]), start=True, stop=True), r=[b], w=[b_pa3])
                V(lambda: nc.vector.tensor_tensor(out=RR(d["NG"][:, half * 4:half * 4 + 4, :].rearrange("p c k -> p (c k)")), in0=ps_a1[:, :], in1=mask1[:, :], op=ALU.mult), r=[b_pa1, bc], w=[d["bA"]])
                V(lambda: nc.vector.tensor_tensor(out=RR(d["LG"][:, half * 4:half * 4 + 4, :].rearrange("p c k -> p (c k)")), in0=ps_a2[:, :], in1=mask1[:, :], op=ALU.mult), r=[b_pa2, bc], w=[d["bA"]])
                V(lambda: nc.vector.tensor_tensor(out=d["L"][:, half * 4:half * 4 + 4, :].rearrange("p c k -> p (c k)"), in0=ps_a3[:, 0:256], in1=mask3[:, :], op=ALU.mult), r=[b_pa3, bc], w=[d["bA"]])
        DPS = [(ps_d, b_pd, ps_d2, b_pd2), (ps_a1, b_pa1, ps_a2, b_pa2)]
        for h in range(2):
            d = H[h]; P, PT, ST = d["P"], d["PT"], d["ST"]; bD = d["bD"]
            V(lambda: nc.vector.tensor_copy(out=RR(P[0][:]), in_=d["L"][:]), r=[d["bA"]], w=[bD])
            V(lambda: nc.vector.tensor_copy(out=RR(PT[0][:]), in_=d["NG"][:, :, 0:64]), r=[d["bA"]], w=[bD])
            V(lambda: nc.vector.tensor_tensor(out=RR(ST[0][:]), in0=d["NG"][:, :, 0:64], in1=ident[0:64, 0:64].unsqueeze(1).to_broadcast([64, NCH, 64]), op=ALU.add), r=[d["bA"], bc], w=[bD])
        cur = 0
        for lev in range(5):
            nxt = 1 - cur
            tick()
            for h in range(2):
                d = H[h]; P, PT, ST = d["P"], d["PT"], d["ST"]; bD = d["bD"]
                pd, bpd, pd2, bpd2 = DPS[h]
                for c in range(NCH):
                    PE(lambda: nc.tensor.matmul(pd[:, c * 64:(c + 1) * 64], lhsT=RR(PT[cur][:, c, :]), rhs=RR(P[cur][:, c, :]), start=True, stop=True), r=[bD], w=[bpd])
                for c in range(NCH):
                    PE(lambda: nc.tensor.matmul(pd2[:, c * 64:(c + 1) * 64], lhsT=RR(P[cur][:, c, :]), rhs=RR(PT[cur][:, c, :]), start=True, stop=True), r=[bD], w=[bpd2])
            tick()
            for h in range(2):
                d = H[h]; P, PT, ST = d["P"], d["PT"], d["ST"]; bD = d["bD"]
                pd, bpd, pd2, bpd2 = DPS[h]
                V(lambda: nc.vector.tensor_copy(out=RR(P[nxt][:].rearrange("p c k -> p (c k)")), in_=pd[:, :]), r=[], w=[bpd, bD])
                A(lambda: nc.scalar.copy(out=RR(PT[nxt][:].rearrange("p c k -> p (c k)")), in_=pd2[:, :]), r=[], w=[bpd2, bD])
            tick()
            for h in range(2):
                d = H[h]; P, PT, ST = d["P"], d["PT"], d["ST"]; bD = d["bD"]
                pd, bpd, pd2, bpd2 = DPS[h]
                for c in range(NCH):
                    PE(lambda: nc.tensor.matmul(pd[:, c * 64:(c + 1) * 64], lhsT=RR(P[nxt][:, c, :]), rhs=RR(ST[cur][:, c, :]), start=True, stop=True), r=[bD], w=[bpd])
            tick()
            for h in range(2):
                d = H[h]; P, PT, ST = d["P"], d["PT"], d["ST"]; bD = d["bD"]
                pd, bpd, pd2, bpd2 = DPS[h]
                V(lambda: nc.vector.tensor_tensor(out=RR(ST[nxt][:].rearrange("p c k -> p (c k)")), in0=pd[:, :], in1=ST[cur][:].rearrange("p c k -> p (c k)"), op=ALU.add), r=[bD], w=[bpd, bD])
            tick()
            cur = nxt
        for h in range(2):
            H[h]["STf"] = H[h]["ST"][cur]
        for c in range(NCH):
            pp = lambda h, i: ps_sh[h][:, i * 64:(i + 1) * 64]
            tick()
            for h in range(2):
                d = H[h]
                PE(lambda: nc.tensor.matmul(pp(h, 0), lhsT=RR(d["LG"][:, c, 0:64]), rhs=RR(d["Vt"][par][:, c, :]), start=True, stop=False), r=[d["bA"], d["bt"][par]], w=[b_psh[h]])
                PE(lambda: nc.tensor.matmul(pp(h, 0), lhsT=RR(d["AR"][par][:, c, 0:64]), rhs=RR(d["M"][:, :]), start=False, stop=True), r=[d["bo"][par], d["bM"]], w=[b_psh[h]])
            tick()
            for h in range(2):
                d = H[h]
                if h == 0:
                    A(lambda: nc.scalar.copy(out=RR(d["X1"][:]), in_=pp(h, 0)), r=[], w=[b_psh[h], d["bX"]])
                else:
                    V(lambda: nc.vector.tensor_copy(out=RR(d["X1"][:]), in_=pp(h, 0)), r=[], w=[b_psh[h], d["bX"]])
            tick()
            for h in range(2):
                d = H[h]
                PE(lambda: nc.tensor.matmul(pp(h, 1), lhsT=RR(d["STf"][:, c, :]), rhs=RR(d["X1"][:, :]), start=True, stop=True), r=[d["bD"], d["bX"]], w=[b_psh[h]])
            tick()
            for h in range(2):
                d = H[h]
                if h == 0:
                    V(lambda: nc.vector.tensor_copy(out=RR(d["U"][:]), in_=pp(h, 1)), r=[], w=[b_psh[h], d["bU"]])
                else:
                    A(lambda: nc.scalar.copy(out=RR(d["U"][:]), in_=pp(h, 1)), r=[], w=[b_psh[h], d["bU"]])
            tick()
            for h in range(2):
                d = H[h]
                PE(lambda: nc.tensor.matmul(pp(h, 2), lhsT=RR(d["AR"][par][:, c, 64:128]), rhs=RR(d["M"][:, :]), start=True, stop=False), r=[d["bo"][par], d["bM"]], w=[b_psh[h]])
                PE(lambda: nc.tensor.matmul(pp(h, 2), lhsT=RR(d["LG"][:, c, 64:128]), rhs=RR(d["Vt"][par][:, c, :]), start=False, stop=False), r=[d["bA"], d["bt"][par]], w=[b_psh[h]])
                PE(lambda: nc.tensor.matmul(pp(h, 2), lhsT=RR(d["NG"][:, c, 64:128]), rhs=RR(d["U"][:, :]), start=False, stop=True), r=[d["bA"], d["bU"]], w=[b_psh[h]])
                PE(lambda: nc.tensor.matmul(pp(h, 3), lhsT=RR(d["KHt"][par][:, c, :]), rhs=RR(d["Vt"][par][:, c, :]), start=True, stop=False), r=[d["bt"][par]], w=[b_psh[h]])
                PE(lambda: nc.tensor.matmul(pp(h, 3), lhsT=RR(d["BHt"][par][:, c, :]), rhs=RR(d["U"][:, :]), start=False, stop=True), r=[d["bt"][par], d["bU"]], w=[b_psh[h]])
            tick()
            for h in range(2):
                d = H[h]
                V(lambda: nc.vector.scalar_tensor_tensor(out=RR(d["M"][:]), in0=d["M"][:], scalar=d["gC"][par][:, c:c + 1], in1=pp(h, 3), op0=ALU.mult, op1=ALU.add), r=[d["bo"][par], d["bM"]], w=[b_psh[h], d["bM"]])
                A(lambda: nc.scalar.copy(out=Yb[:, c, h, :], in_=pp(h, 2)), r=[], w=[b_psh[h], b_Y])
        Y2 = Yb[:].rearrange("p c h v -> p (c h) v")
        V(lambda: nc.vector.tensor_reduce(out=st1[:], in_=Y2, axis=AX.X, op=ALU.add), r=[b_Y], w=[b_st])
        A(lambda: nc.scalar.activation(out=Ysq[:].rearrange("p c h v -> p (c h v)"), in_=Yb[:].rearrange("p c h v -> p (c h v)"), func=AF.Square), r=[b_Y], w=[b_tmp])
        V(lambda: nc.vector.tensor_reduce(out=st2[:], in_=Ysq[:].rearrange("p c h v -> p (c h) v"), axis=AX.X, op=ALU.add), r=[b_tmp], w=[b_st])
        V(lambda: nc.vector.tensor_scalar(out=st1[:], in0=st1[:], scalar1=1.0 / 64, scalar2=None, op0=ALU.mult), r=[b_st], w=[b_st])
        V(lambda: nc.vector.tensor_tensor(out=st3[:], in0=st1[:], in1=st1[:], op=ALU.mult), r=[b_st], w=[b_st])
        V(lambda: nc.vector.scalar_tensor_tensor(out=st2[:], in0=st2[:], scalar=1.0 / 64, in1=st3[:], op0=ALU.mult, op1=ALU.subtract), r=[b_st], w=[b_st])
        A(lambda: nc.scalar.activation(out=st2[:], in_=st2[:], func=AF.Sqrt, bias=64e-5, scale=1.0), r=[b_st], w=[b_st])
        V(lambda: nc.vector.reciprocal(out=st2[:], in_=st2[:]), r=[b_st], w=[b_st])
        V(lambda: nc.vector.tensor_tensor(out=Y2, in0=Y2, in1=st1[:].unsqueeze(2).to_broadcast([64, NCH * 2, 64]), op=ALU.subtract), r=[b_st, b_Y], w=[b_Y])
        V(lambda: nc.vector.tensor_tensor(out=Y2, in0=Y2, in1=st2[:].unsqueeze(2).to_broadcast([64, NCH * 2, 64]), op=ALU.mult), r=[b_st, b_Y], w=[b_Y])
        for h in range(2):
            V(lambda: nc.vector.tensor_tensor(out=Yb[:, :, h, :], in0=Yb[:, :, h, :], in1=gn[:, 0, h, :].unsqueeze(1).to_broadcast([64, NCH, 64]), op=ALU.mult), r=[b_Y, bc], w=[b_Y])
            V(lambda: nc.vector.tensor_tensor(out=Yb[:, :, h, :], in0=Yb[:, :, h, :], in1=gn[:, 1, h, :].unsqueeze(1).to_broadcast([64, NCH, 64]), op=ALU.add), r=[b_Y, bc], w=[b_Y])
            V(lambda: nc.vector.tensor_tensor(out=Ysq[:, :, h, :], in0=H[h]["Vt"][par][:], in1=sbon[par][:, :, h].unsqueeze(2).to_broadcast([64, NCH, 64]), op=ALU.mult), r=[H[h]["bt"][par], b_sb[par]], w=[b_tmp])
        V(lambda: nc.vector.tensor_tensor(out=Yb[:].rearrange("p c h v -> p (c h v)"), in0=Yb[:].rearrange("p c h v -> p (c h v)"), in1=Ysq[:].rearrange("p c h v -> p (c h v)"), op=ALU.add), r=[b_Y, b_tmp], w=[b_Y])
        for c in range(NCH):
            PE(lambda: nc.tensor.transpose(ps_a3f[:, c * 64:(c + 1) * 64], Yb[:, c, :, :].rearrange("p h v -> p (h v)"), ident[0:64, 0:64]), r=[b_Y, bc], w=[b_pa3])
        V(lambda: nc.vector.tensor_tensor(out=yo[:], in0=ps_a3f[:, :], in1=gate[par][:], op=ALU.mult), r=[b_gate[par]], w=[b_pa3, b_yo])
        mk.dma("sp", yT[:, t0:t0 + TB], yo[:], reads=[b_yo], is_output=True)

    for _ in stage1(0):
        pass
    for blk in range(NB):
        nx = stage1(blk + 1) if blk + 1 < NB else None

        def tick(k=2):
            if nx is not None:
                for _ in range(k):
                    if next(nx, "END") == "END":
                        break
        rest(blk, tick)
        if nx is not None:
            for _ in nx:
                pass


def build_rwkv(T):
    nc = bass.Bass("TRN2", target_bir_lowering=False)
    dt = lambda n, s, k="ExternalInput": nc.dram_tensor(n, s, F32, kind=k).ap()
    rwin = dt("rwin", [704, T]); par64 = dt("par64", [64, 2, 11]); par128 = dt("par128", [128, 3])
    w2 = dt("w2", [96, 128]); a2 = dt("a2", [96, 128]); g2 = dt("g2", [128, 128]); gnt = dt("gnt", [64, 2, 2, 64])
    cst = {"mask1": dt("mask1", [64, 512]), "mask3": dt("mask3", [64, 256]), "seg": dt("seg", [128, TB]), "ident": dt("ident", [128, 128])}
    yT = dt("yT", [128, T], "ExternalOutput")
    with ExitStack() as ctx:
        mk = MK(nc, ctx)
        emit_rwkv(nc, mk, T, rwin, par64, par128, w2, a2, g2, gnt, cst, yT)
        mk.finish("sp")
        print("rwkv ops", mk.nops, "waits", mk.nwaits)
    return nc


def rwkv_host_inputs(prm, l, q):
    G = 512
    cs = slice(128 * q, 128 * q + 128)
    mu = prm["rwkv_mu"][l]
    par64 = np.zeros((64, 2, 11), np.float32)
    for h in range(2):
        c0 = 128 * q + 64 * h
        par64[:, h, 0] = mu[0 * G + c0:0 * G + c0 + 64]
        par64[:, h, 1] = mu[1 * G + c0:1 * G + c0 + 64]
        par64[:, h, 2] = mu[2 * G + c0:2 * G + c0 + 64]
        par64[:, h, 3] = prm["rwkv_w0"][l][c0:c0 + 64]
        par64[:, h, 4] = prm["rwkv_a0"][l][c0:c0 + 64]
        par64[:, h, 5] = prm["rwkv_kk"][l][c0:c0 + 64]
        par64[:, h, 6] = prm["rwkv_ka"][l][c0:c0 + 64]
        par64[:, h, 7] = prm["rwkv_rk"][l][2 * q + h]
    par128 = np.zeros((128, 3), np.float32)
    par128[0:96, 0] = mu[3 * G:3 * G + 96]
    par128[0:96, 1] = mu[3 * G + 96:3 * G + 192]
    par128[:, 2] = mu[3 * G + 192:3 * G + 320]
    gnt = np.zeros((64, 2, 2, 64), np.float32)
    for h in range(2):
        c0 = 128 * q + 64 * h
        gnt[:, 0, h, :] = prm["rwkv_gn_g"][l][c0:c0 + 64][None]
        gnt[:, 1, h, :] = prm["rwkv_gn_b"][l][c0:c0 + 64][None]
    d = {"par64": par64, "par128": par128, "gnt": gnt,
         "w2": np.ascontiguousarray(prm["rwkv_w2"][l][:, cs]), "a2": np.ascontiguousarray(prm["rwkv_a2"][l][:, cs]),
         "g2": np.ascontiguousarray(prm["rwkv_g2"][l][:, cs])}
    d.update(rwkv_consts())
    return d


def rwkv_rows(q):
    G = 512
    idx = []
    for base in (0, G, 2 * G):
        idx += list(range(base + 128 * q, base + 128 * q + 64))
        idx += list(range(base + 128 * q + 64, base + 128 * q + 128))
    idx += list(range(3 * G, 3 * G + 320))
    return np.array(idx)


import math
import numpy as np
from contextlib import ExitStack


def emit_conv_gen(nc, mk, T, cvin, cw, yT, TB=2048, odt=F32):
    V = lambda fn, r=(), w=(): mk.op("dve", fn, r, w)
    G = lambda fn, r=(), w=(): mk.op("pool", fn, r, w)
    cwt = mk.sb("cv_w", [128, 3]); bc = Buf()
    mk.dma("sp", cwt[:], cw, writes=[bc])
    yield
    Bt = mk.sb("cv_B", [128, TB]); Ct = mk.sb("cv_C", [128, TB + 2]); Ht = mk.sb("cv_H", [128, TB + 2])
    z = mk.sb("cv_z", [128, TB + 2]); y = mk.sb("cv_y", [128, TB]); o = mk.sb("cv_o", [128, TB], odt)
    b_in, b_z, b_y, b_o = Buf(), Buf(), Buf(), Buf()
    for blk in range(T // TB):
        t0 = blk * TB
        mk.dma("sp", Bt[:], cvin[0:128, t0:t0 + TB], writes=[b_in])
        yield
        if blk == 0:
            V(lambda: nc.vector.memset(Ct[:, 0:2], 0.0), w=[b_in])
            yield
            V(lambda: nc.vector.memset(Ht[:, 0:2], 0.0), w=[b_in])
            yield
            mk.dma("sp", Ct[:, 2:], cvin[128:256, 0:TB], writes=[b_in])
            yield
            mk.dma("sp", Ht[:, 2:], cvin[256:384, 0:TB], writes=[b_in])
            yield
        else:
            mk.dma("sp", Ct[:], cvin[128:256, t0 - 2:t0 + TB], writes=[b_in])
            yield
            mk.dma("sp", Ht[:], cvin[256:384, t0 - 2:t0 + TB], writes=[b_in])
            yield
        G(lambda: nc.gpsimd.tensor_tensor(out=z[:], in0=Ct[:], in1=Ht[:], op=ALU.mult), r=[b_in], w=[b_z])
        yield
        V(lambda: nc.vector.tensor_scalar(out=y[:], in0=z[:, 2:TB + 2], scalar1=cwt[:, 2:3], scalar2=None, op0=ALU.mult), r=[b_z, bc], w=[b_y])
        yield
        V(lambda: nc.vector.scalar_tensor_tensor(out=y[:], in0=z[:, 1:TB + 1], scalar=cwt[:, 1:2], in1=y[:], op0=ALU.mult, op1=ALU.add), r=[b_z, bc], w=[b_y])
        yield
        V(lambda: nc.vector.scalar_tensor_tensor(out=y[:], in0=z[:, 0:TB], scalar=cwt[:, 0:1], in1=y[:], op0=ALU.mult, op1=ALU.add), r=[b_z, bc], w=[b_y])
        yield
        G(lambda: nc.gpsimd.tensor_tensor(out=o[:], in0=y[:], in1=Bt[:], op=ALU.mult), r=[b_y, b_in], w=[b_o])
        yield
        mk.dma("sp", yT[:, t0:t0 + TB], o[:], reads=[b_o], is_output=True)
        yield


def emit_conv(nc, mk, T, cvin, cw, yT, TB=2048, odt=F32):
    for _ in emit_conv_gen(nc, mk, T, cvin, cw, yT, TB=TB, odt=odt):
        pass


def t5_bucket_np(rel):
    n = np.maximum(rel, 0)
    max_exact = 16
    n_f = np.maximum(n, 1).astype(np.float32)
    large = max_exact + (np.log(n_f / max_exact) / math.log(128 / max_exact) * (32 - max_exact)).astype(np.int32)
    return np.where(n < max_exact, n, np.minimum(large, 31))


def attn_tables(rel_bias, sinks_l, q):
    qi = np.arange(128)[:, None]
    kj = np.arange(256)[None, :]
    rel = qi + 128 - kj
    bucket = t5_bucket_np(rel)
    valid = (rel >= 0) & (rel < 128)
    tab = np.zeros((2, 128, 2, 256), np.float32)
    for h in range(2):
        bias = rel_bias[bucket, 2 * q + h]
        full = np.where(valid, bias, np.float32(-30000.0))
        tab[0, :, h, :] = full
        f0 = full.copy()
        f0[:, 0:128] = -30000.0
        tab[1, :, h, :] = f0
    sk = np.broadcast_to(sinks_l[2 * q:2 * q + 2][None, :], (128, 2)).astype(np.float32).copy()
    return tab, sk


def emit_attn(nc, mk, T, qkv, btab, sinkt_d, ident_d, yT, odt=F32, tick=None):
    V = lambda fn, r=(), w=(): mk.op("dve", fn, r, w)
    A = lambda fn, r=(), w=(): mk.op("act", fn, r, w)
    PE = lambda fn, r=(), w=(): mk.op("pe", fn, r, w, skip_same=True)
    NBK = T // 128
    bt = mk.sb("at_bt", [128, 2, 2, 256]); sk = mk.sb("at_sk", [128, 2]); ident = mk.sb("at_id", [128, 128]); bc = Buf()
    mk.dma("sp", bt[:, 0, :, :], btab[0], writes=[bc])
    mk.dma("sp", bt[:, 1, :, :], btab[1], writes=[bc])
    mk.dma("sp", sk[:], sinkt_d, writes=[bc])
    mk.dma("sp", ident[:], ident_d, writes=[bc])
    CH = 1024
    qt = mk.sb("at_q", [128, CH]); kt = mk.sb("at_k", [128, 128 + CH]); vt = mk.sb("at_v", [64, CH])
    vtok = mk.sb("at_vtok", [128, CH // 128 + 1, 64])
    b_q, b_k, b_v, b_vt = Buf(), Buf(), Buf(), Buf()
    sc = [mk.sb("at_sc%d" % h, [128, 256]) for h in range(2)]; b_sc = [Buf(), Buf()]
    pr = [mk.sb("at_p%d" % h, [128, 256]) for h in range(2)]; b_p = [Buf(), Buf()]
    pT = [mk.sb("at_pT%d" % h, [128, 256]) for h in range(2)]; b_pT = [Buf(), Buf()]
    sm = [mk.sb("at_sm%d" % h, [128, 8]) for h in range(2)]; b_sm = [Buf(), Buf()]
    ot = mk.sb("at_o", [128, 128]); b_o = Buf()
    yo = mk.sb("at_yo", [128, CH], odt); b_yo = Buf()
    ps_s = [mk.ps("at_ps_s%d" % h, [128, 512]) for h in range(2)]; b_ps = [Buf(), Buf()]
    ps_t = [mk.ps("at_ps_t%d" % h, [128, 512]) for h in range(2)]; b_pt = [Buf(), Buf()]
    ps_o = mk.ps("at_ps_o", [128, 512]); b_po = Buf()
    ps_v = mk.ps("at_ps_v", [128, 512]); b_pv = Buf()
    for ch in range(T // CH):
        c0 = ch * CH
        mk.dma("sp", qt[:], qkv[0:128, c0:c0 + CH], writes=[b_q])
        if ch == 0:
            V(lambda: nc.vector.memset(kt[:, 0:128], 0.0), w=[b_k])
            V(lambda: nc.vector.memset(vtok[:, 0, :], 0.0), w=[b_vt])
            for hh in range(2):
                mk.dma("sp", kt[64 * hh:64 * hh + 64, 128:], qkv[128:192, 0:CH], writes=[b_k])
        else:
            for hh in range(2):
                mk.dma("sp", kt[64 * hh:64 * hh + 64, :], qkv[128:192, c0 - 128:c0 + CH], writes=[b_k])
            V(lambda: nc.vector.tensor_copy(out=vtok[:, 0, :], in_=vtok[:, CH // 128, :]), r=[b_vt], w=[b_vt])
        mk.dma("sp", vt[:], qkv[192:256, c0:c0 + CH], writes=[b_v])
        for j in range(CH // 128):
            PE(lambda: nc.tensor.transpose(ps_v[:, j * 64:(j + 1) * 64], vt[:, j * 128:(j + 1) * 128], ident[0:64, 0:64]), r=[b_v, bc], w=[b_pv])
        A(lambda: nc.scalar.copy(out=vtok[:, 1:, :].rearrange("p j d -> p (j d)"), in_=ps_v[:, 0:(CH // 128) * 64]), r=[], w=[b_pv, b_vt])
        for j in range(CH // 128):
            first = 1 if (ch == 0 and j == 0) else 0
            if tick is not None:
                tick()
            HS = [slice(0, 64), slice(64, 128)]
            for h in range(2):
                PE(lambda: nc.tensor.matmul(ps_s[h][:, 0:256], lhsT=qt[HS[h], j * 128:(j + 1) * 128], rhs=kt[HS[h], j * 128:j * 128 + 256], start=True, stop=True),
                   r=[b_q, b_k], w=[b_ps[h]])
            for h in range(2):
                V(lambda: nc.vector.scalar_tensor_tensor(out=sc[h][:], in0=ps_s[h][:, 0:256], scalar=0.125, in1=bt[:, first, h, :], op0=ALU.mult, op1=ALU.add),
                  r=[bc], w=[b_ps[h], b_sc[h]])
                s = sm[h]
                V(lambda: nc.vector.reduce_max(out=s[:, 0:1], in_=sc[h][:], axis=AX.X), r=[b_sc[h]], w=[b_sm[h]])
                V(lambda: nc.vector.tensor_tensor(out=s[:, 0:1], in0=s[:, 0:1], in1=sk[:, h:h + 1], op=ALU.max), r=[bc], w=[b_sm[h]])
                V(lambda: nc.vector.tensor_scalar(out=s[:, 1:2], in0=s[:, 0:1], scalar1=-1.0, scalar2=None, op0=ALU.mult), r=[], w=[b_sm[h]])
            for h in range(2):
                s = sm[h]
                A(lambda: nc.scalar.activation(out=pr[h][:], in_=sc[h][:], func=AF.Exp, bias=s[:, 1:2], scale=1.0, accum_out=s[:, 2:3]), r=[b_sc[h]], w=[b_sm[h], b_p[h]])
                A(lambda: nc.scalar.activation(out=s[:, 3:4], in_=sk[:, h:h + 1], func=AF.Exp, bias=s[:, 1:2], scale=1.0), r=[bc], w=[b_sm[h]])
            for h in range(2):
                for kb in range(2):
                    PE(lambda: nc.tensor.transpose(ps_t[h][:, kb * 128:(kb + 1) * 128], pr[h][:, kb * 128:(kb + 1) * 128], ident[:, :]), r=[b_p[h], bc], w=[b_pt[h]])
            for h in range(2):
                s = sm[h]
                V(lambda: nc.vector.tensor_tensor(out=s[:, 4:5], in0=s[:, 2:3], in1=s[:, 3:4], op=ALU.add), r=[], w=[b_sm[h]])
                V(lambda: nc.vector.reciprocal(out=s[:, 5:6], in_=s[:, 4:5]), r=[], w=[b_sm[h]])
                V(lambda: nc.vector.tensor_copy(out=pT[h][:], in_=ps_t[h][:, 0:256]), r=[], w=[b_pt[h], b_pT[h]])
            for h in range(2):
                for kb in range(2):
                    PE(lambda: nc.tensor.matmul(ps_o[:, h * 64:(h + 1) * 64], lhsT=pT[h][:, kb * 128:(kb + 1) * 128], rhs=vtok[:, j + kb, :], start=(kb == 0), stop=(kb == 1)),
                       r=[b_pT[h], b_vt], w=[b_po])
            for h in range(2):
                s = sm[h]
                A(lambda: nc.scalar.activation(out=ot[:, h * 64:(h + 1) * 64], in_=ps_o[:, h * 64:(h + 1) * 64], func=AF.Copy, scale=s[:, 5:6]), r=[b_sm[h]], w=[b_po, b_o])
            PE(lambda: nc.tensor.transpose(ps_o[:, 128:256], ot[:, :], ident[:, :]), r=[b_o, bc], w=[b_po])
            V(lambda: nc.vector.tensor_copy(out=yo[:, j * 128:(j + 1) * 128], in_=ps_o[:, 128:256]), r=[], w=[b_po, b_yo])
        mk.dma("sp", yT[:, c0:c0 + CH], yo[:], reads=[b_yo], is_output=True)


CS = 512


def s5_host_inputs(prm, l, q):
    g0 = 8 * q
    par = np.zeros((128, 4, 3), np.float32)
    bb = np.zeros((128, 4, 2, 16), np.float32)
    cc = np.zeros((128, 4, 2, 16), np.float32)
    for j in range(4):
        for gl in range(2):
            g = g0 + 2 * j + gl
            ps = slice(64 * gl, 64 * gl + 64)
            par[ps, j, 0] = prm["s5_lambda_re"][l][g]
            par[ps, j, 1] = prm["s5_lambda_im"][l][g]
            par[ps, j, 2] = prm["s5_log_dt"][l][g]
            bb[ps, j, 0, :] = prm["s5_b_re"][l][g]
            bb[ps, j, 1, :] = prm["s5_b_im"][l][g]
            cc[ps, j, 0, :] = prm["s5_c_re"][l][g].T
            cc[ps, j, 1, :] = prm["s5_c_im"][l][g].T
    dsk = np.ascontiguousarray(prm["s5_d"][l][g0:g0 + 8].reshape(128, 1))
    iot = np.broadcast_to(np.arange(CS, dtype=np.float32)[None, :], (128, CS)).copy()
    return {"s5par": par, "s5bb": bb, "s5cc": cc, "s5d": dsk, "s5iota": iot, "ident": np.eye(128, dtype=np.float32)}


def emit_s5(nc, mk, T, uT, par_d, bb_d, cc_d, d_d, iota_d, ident_d, yT, odt=F32):
    V = lambda fn, r=(), w=(): mk.op("dve", fn, r, w)
    A = lambda fn, r=(), w=(): mk.op("act", fn, r, w)
    G = lambda fn, r=(), w=(): mk.op("pool", fn, r, w)
    PE = lambda fn, r=(), w=(): mk.op("pe", fn, r, w, skip_same=True)
    sb = mk.sb
    TWO_PI = 2.0 * math.pi
    par = sb("s5_par", [128, 4, 3]); bb = sb("s5_bb", [128, 4, 2, 16]); cc = sb("s5_cc", [128, 4, 2, 16])
    dsk = sb("s5_d", [128, 1]); iot = sb("s5_iota", [128, CS]); ident = sb("s5_id", [128, 128])
    bc = Buf("c")
    for dst, src in ((par, par_d), (bb, bb_d), (cc, cc_d), (dsk, d_d), (iot, iota_d), (ident, ident_d)):
        mk.dma("sp", dst[:], src, writes=[bc])
    P = {}
    for nm in ("dl", "mag", "th", "cs", "sn", "are", "aim", "den", "zre", "zim", "t1", "t2", "t3", "cC", "sC"):
        P[nm] = sb("s5_p_" + nm, [128, 4])
    ti = sb("s5_ti", [128, 4 * CS], I32)
    bp = Buf("p")
    lr, li, ldt = par[:, :, 0], par[:, :, 1], par[:, :, 2]

    def sincos(sin_out, cos_out, x, n, tmpa, tmpb, tint):
        def wrap(r):
            V(lambda: nc.vector.tensor_scalar(out=tmpb, in0=r, scalar1=0.5, scalar2=None, op0=ALU.is_gt), r=[bp], w=[bp])
            V(lambda: nc.vector.tensor_tensor(out=r, in0=r, in1=tmpb, op=ALU.subtract), r=[bp], w=[bp])
            V(lambda: nc.vector.tensor_scalar(out=tmpb, in0=r, scalar1=-0.5, scalar2=None, op0=ALU.is_lt), r=[bp], w=[bp])
            V(lambda: nc.vector.tensor_tensor(out=r, in0=r, in1=tmpb, op=ALU.add), r=[bp], w=[bp])
        V(lambda: nc.vector.tensor_copy(out=tint, in_=x), r=[bp, bc], w=[bp])
        V(lambda: nc.vector.tensor_copy(out=tmpa, in_=tint), r=[bp], w=[bp])
        V(lambda: nc.vector.tensor_tensor(out=tmpa, in0=x, in1=tmpa, op=ALU.subtract), r=[bp, bc], w=[bp])
        wrap(tmpa)
        A(lambda: nc.scalar.activation(out=sin_out, in_=tmpa, func=AF.Sin, scale=TWO_PI), r=[bp], w=[bp])
        V(lambda: nc.vector.tensor_scalar(out=tmpa, in0=tmpa, scalar1=0.25, scalar2=None, op0=ALU.add), r=[bp], w=[bp])
        wrap(tmpa)
        A(lambda: nc.scalar.activation(out=cos_out, in_=tmpa, func=AF.Sin, scale=TWO_PI), r=[bp], w=[bp])

    A(lambda: nc.scalar.activation(out=P["dl"][:], in_=ldt, func=AF.Exp), r=[bc], w=[bp])
    V(lambda: nc.vector.tensor_tensor(out=P["mag"][:], in0=lr, in1=P["dl"][:], op=ALU.mult), r=[bc, bp], w=[bp])
    A(lambda: nc.scalar.activation(out=P["mag"][:], in_=P["mag"][:], func=AF.Exp), r=[bp], w=[bp])
    V(lambda: nc.vector.tensor_tensor(out=P["th"][:], in0=li, in1=P["dl"][:], op=ALU.mult), r=[bc, bp], w=[bp])
    V(lambda: nc.vector.tensor_scalar(out=P["th"][:], in0=P["th"][:], scalar1=1.0 / TWO_PI, scalar2=None, op0=ALU.mult), r=[bp], w=[bp])
    V(lambda: nc.vector.tensor_copy(out=ti[:, 0:4], in_=P["th"][:]), r=[bp], w=[bp])
    V(lambda: nc.vector.tensor_copy(out=P["t1"][:], in_=ti[:, 0:4]), r=[bp], w=[bp])
    V(lambda: nc.vector.tensor_tensor(out=P["th"][:], in0=P["th"][:], in1=P["t1"][:], op=ALU.subtract), r=[bp], w=[bp])
    V(lambda: nc.vector.tensor_scalar(out=P["t1"][:], in0=P["th"][:], scalar1=0.5, scalar2=None, op0=ALU.is_gt), r=[bp], w=[bp])
    V(lambda: nc.vector.tensor_tensor(out=P["th"][:], in0=P["th"][:], in1=P["t1"][:], op=ALU.subtract), r=[bp], w=[bp])
    V(lambda: nc.vector.tensor_scalar(out=P["t1"][:], in0=P["th"][:], scalar1=-0.5, scalar2=None, op0=ALU.is_lt), r=[bp], w=[bp])
    V(lambda: nc.vector.tensor_tensor(out=P["th"][:], in0=P["th"][:], in1=P["t1"][:], op=ALU.add), r=[bp], w=[bp])
    sincos(P["sn"][:], P["cs"][:], P["th"][:], 4, P["t1"][:], P["t2"][:], ti[:, 0:4])
    V(lambda: nc.vector.tensor_tensor(out=P["are"][:], in0=P["mag"][:], in1=P["cs"][:], op=ALU.mult), r=[bp], w=[bp])
    V(lambda: nc.vector.tensor_tensor(out=P["aim"][:], in0=P["mag"][:], in1=P["sn"][:], op=ALU.mult), r=[bp], w=[bp])
    V(lambda: nc.vector.tensor_tensor(out=P["den"][:], in0=lr, in1=lr, op=ALU.mult), r=[bc], w=[bp])
    V(lambda: nc.vector.tensor_tensor(out=P["t1"][:], in0=li, in1=li, op=ALU.mult), r=[bc], w=[bp])
    V(lambda: nc.vector.tensor_tensor(out=P["den"][:], in0=P["den"][:], in1=P["t1"][:], op=ALU.add), r=[bp], w=[bp])
    V(lambda: nc.vector.reciprocal(out=P["den"][:], in_=P["den"][:]), r=[bp], w=[bp])
    V(lambda: nc.vector.tensor_scalar(out=P["t3"][:], in0=P["are"][:], scalar1=-1.0, scalar2=None, op0=ALU.add), r=[bp], w=[bp])
    V(lambda: nc.vector.tensor_tensor(out=P["t1"][:], in0=P["t3"][:], in1=lr, op=ALU.mult), r=[bp, bc], w=[bp])
    V(lambda: nc.vector.tensor_tensor(out=P["t2"][:], in0=P["aim"][:], in1=li, op=ALU.mult), r=[bp, bc], w=[bp])
    V(lambda: nc.vector.tensor_tensor(out=P["t1"][:], in0=P["t1"][:], in1=P["t2"][:], op=ALU.add), r=[bp], w=[bp])
    V(lambda: nc.vector.tensor_tensor(out=P["zre"][:], in0=P["t1"][:], in1=P["den"][:], op=ALU.mult), r=[bp], w=[bp])
    V(lambda: nc.vector.tensor_tensor(out=P["t1"][:], in0=P["aim"][:], in1=lr, op=ALU.mult), r=[bp, bc], w=[bp])
    V(lambda: nc.vector.tensor_tensor(out=P["t2"][:], in0=P["t3"][:], in1=li, op=ALU.mult), r=[bp, bc], w=[bp])
    V(lambda: nc.vector.tensor_tensor(out=P["t1"][:], in0=P["t1"][:], in1=P["t2"][:], op=ALU.subtract), r=[bp], w=[bp])
    V(lambda: nc.vector.tensor_tensor(out=P["zim"][:], in0=P["t1"][:], in1=P["den"][:], op=ALU.mult), r=[bp], w=[bp])
    bbar = sb("s5_bbar", [128, 4, 2, 16]); tb1 = sb("s5_tb1", [128, 4, 16]); tb2 = sb("s5_tb2", [128, 4, 16])
    zre_b = P["zre"][:].unsqueeze(2).to_broadcast([128, 4, 16]); zim_b = P["zim"][:].unsqueeze(2).to_broadcast([128, 4, 16])
    V(lambda: nc.vector.tensor_tensor(out=tb1[:], in0=bb[:, :, 0, :], in1=zre_b, op=ALU.mult), r=[bp, bc], w=[bp])
    V(lambda: nc.vector.tensor_tensor(out=tb2[:], in0=bb[:, :, 1, :], in1=zim_b, op=ALU.mult), r=[bp, bc], w=[bp])
    V(lambda: nc.vector.tensor_tensor(out=bbar[:, :, 0, :], in0=tb1[:], in1=tb2[:], op=ALU.subtract), r=[bp], w=[bp])
    V(lambda: nc.vector.tensor_tensor(out=tb1[:], in0=bb[:, :, 1, :], in1=zre_b, op=ALU.mult), r=[bp, bc], w=[bp])
    V(lambda: nc.vector.tensor_tensor(out=tb2[:], in0=bb[:, :, 0, :], in1=zim_b, op=ALU.mult), r=[bp, bc], w=[bp])
    V(lambda: nc.vector.tensor_tensor(out=bbar[:, :, 1, :], in0=tb1[:], in1=tb2[:], op=ALU.add), r=[bp], w=[bp])
    BD = sb("s5_BD", [128, 4, 2, 128]); CM = sb("s5_CM", [128, 4, 4, 128]); BbT = sb("s5_BbT", [128, 4, 2, 128])
    V(lambda: nc.vector.memset(BD[:].rearrange("p a b c -> p (a b c)"), 0.0), w=[bp])
    V(lambda: nc.vector.memset(CM[:].rearrange("p a b c -> p (a b c)"), 0.0), w=[bp])
    for j in range(4):
        for gl in range(2):
            ps_ = slice(64 * gl, 64 * gl + 64)
            c0 = 32 * j + 16 * gl
            for ri in range(2):
                V(lambda: nc.vector.tensor_copy(out=BD[ps_, j, ri, c0:c0 + 16], in_=bbar[ps_, j, ri, :]), r=[bp], w=[bp])
            V(lambda: nc.vector.tensor_copy(out=CM[ps_, j, 0, c0:c0 + 16], in_=cc[ps_, j, 0, :]), r=[bc], w=[bp])
            V(lambda: nc.vector.tensor_scalar(out=CM[ps_, j, 1, c0:c0 + 16], in0=cc[ps_, j, 0, :], scalar1=-1.0, scalar2=None, op0=ALU.mult), r=[bc], w=[bp])
            V(lambda: nc.vector.tensor_scalar(out=CM[ps_, j, 2, c0:c0 + 16], in0=cc[ps_, j, 1, :], scalar1=-1.0, scalar2=None, op0=ALU.mult), r=[bc], w=[bp])
            V(lambda: nc.vector.tensor_scalar(out=CM[ps_, j, 3, c0:c0 + 16], in0=cc[ps_, j, 1, :], scalar1=-1.0, scalar2=None, op0=ALU.mult), r=[bc], w=[bp])
    ps_tmp = mk.ps("s5_ps_tmp", [128, 512]); b_pt = Buf()
    for j in range(4):
        for ri in range(2):
            PE(lambda: nc.tensor.transpose(ps_tmp[:, ri * 128:(ri + 1) * 128], BD[:, j, ri, :], ident[:, :]), r=[bp, bc], w=[b_pt])
        V(lambda: nc.vector.tensor_copy(out=BbT[:, j, :, :].rearrange("p a b -> p (a b)"), in_=ps_tmp[:, 0:256]), r=[], w=[b_pt, bp])
    cosT = sb("s5_cosT", [128, 4, CS]); sinT = sb("s5_sinT", [128, 4, CS])
    xa = sb("s5_xa", [128, 4 * CS]); xb = sb("s5_xb", [128, 4 * CS]); xc = sb("s5_xc", [128, 4 * CS])
    for j in range(4):
        V(lambda: nc.vector.tensor_scalar(out=xc[:, j * CS:(j + 1) * CS], in0=iot[:], scalar1=P["th"][:, j:j + 1], scalar2=None, op0=ALU.mult), r=[bp, bc], w=[bp])
    sincos(sinT[:].rearrange("p a b -> p (a b)"), cosT[:].rearrange("p a b -> p (a b)"), xc[:], 4 * CS, xa[:], xb[:], ti[:])
    V(lambda: nc.vector.tensor_scalar(out=P["t3"][:], in0=P["th"][:], scalar1=float(CS), scalar2=None, op0=ALU.mult), r=[bp], w=[bp])
    sincos(P["sC"][:], P["cC"][:], P["t3"][:], 4, P["t1"][:], P["t2"][:], ti[:, 0:4])
    WW = []
    for par in range(2):
        Wd = {}
        for nm in ("t1", "t2", "t3", "t4", "br", "bi", "zr", "zi", "q1", "q2", "q3", "q4"):
            Wd[nm] = sb("s5_w%d_%s" % (par, nm), [128, CS])
        WW.append(Wd)
    b_wl = [Buf("w0"), Buf("w1")]; b_zl = [Buf("z0"), Buf("z1")]; b_ql = [Buf("q0"), Buf("q1")]
    init = sb("s5_init", [128, 4, 2]); itmp = sb("s5_itmp", [128, 4, 2]); b_il = [Buf("init%d" % j) for j in range(4)]
    for j in range(4):
        V(lambda: nc.vector.memset(init[:, j, :], 0.0), w=[b_il[j]])
    yv = sb("s5_yv", [128, CS]); y2 = sb("s5_y2", [128, CS]); yo = sb("s5_yo", [128, CS], odt); b_y = Buf("y")
    ps_al = [mk.ps("s5_ps_a%d" % i, [128, 512]) for i in range(2)]; ps_bl = [mk.ps("s5_ps_b%d" % i, [128, 512]) for i in range(2)]
    ps_y = mk.ps("s5_ps_y", [128, 512])
    b_pal, b_pbl, b_py = [Buf(), Buf()], [Buf(), Buf()], Buf()
    uts = [sb("s5_u%d" % i, [128, CS]) for i in range(2)]; b_ul = [Buf(), Buf()]
    NCK = T // CS

    def stage1(n):
        chk, j = n // 4, n % 4
        par = n % 2
        ut = uts[chk % 2]; b_u = b_ul[chk % 2]
        if j == 0:
            mk.dma("sp", ut[:], uT[:, chk * CS:(chk + 1) * CS], writes=[b_u])
        W = WW[par]; b_w = b_wl[par]
        ps_a = ps_al[par]; ps_b = ps_bl[par]; b_pa = b_pal[par]; b_pb = b_pbl[par]
        PE(lambda: nc.tensor.matmul(ps_a[:, :], lhsT=BbT[:, j, 0, :], rhs=ut[:, :], start=True, stop=True), r=[bp, b_u], w=[b_pa])
        PE(lambda: nc.tensor.matmul(ps_b[:, :], lhsT=BbT[:, j, 1, :], rhs=ut[:, :], start=True, stop=True), r=[bp, b_u], w=[b_pb])

    def stage1v(n):
        chk, j = n // 4, n % 4
        par = n % 2
        W = WW[par]; b_w = b_wl[par]
        ps_a = ps_al[par]; ps_b = ps_bl[par]; b_pa = b_pal[par]; b_pb = b_pbl[par]
        cj, sj = cosT[:, j, :], sinT[:, j, :]
        V(lambda: nc.vector.tensor_tensor(out=W["t1"][:], in0=ps_a[:, :], in1=cj, op=ALU.mult), r=[bp], w=[b_pa, b_w])
        V(lambda: nc.vector.tensor_tensor(out=W["t4"][:], in0=ps_a[:, :], in1=sj, op=ALU.mult), r=[bp], w=[b_pa, b_w])
        V(lambda: nc.vector.tensor_tensor(out=W["t2"][:], in0=ps_b[:, :], in1=sj, op=ALU.mult), r=[bp], w=[b_pb, b_w])
        V(lambda: nc.vector.tensor_tensor(out=W["t3"][:], in0=ps_b[:, :], in1=cj, op=ALU.mult), r=[bp], w=[b_pb, b_w])
        G(lambda: nc.gpsimd.tensor_tensor(out=W["br"][:], in0=W["t1"][:], in1=W["t2"][:], op=ALU.add), r=[b_w], w=[b_w])
        G(lambda: nc.gpsimd.tensor_tensor(out=W["bi"][:], in0=W["t3"][:], in1=W["t4"][:], op=ALU.subtract), r=[b_w], w=[b_w])

    def stage2(n):
        chk, j = n // 4, n % 4
        par = n % 2
        t0 = chk * CS
        ut = uts[chk % 2]; b_u = b_ul[chk % 2]
        W = WW[par]; b_w = b_wl[par]; b_z = b_zl[par]; b_q = b_ql[par]; b_i = b_il[j]
        cj, sj = cosT[:, j, :], sinT[:, j, :]
        rho = P["mag"][:, j:j + 1].to_broadcast([128, CS])
        V(lambda: nc.vector.tensor_tensor_scan(out=W["zr"][:], data0=rho, data1=W["br"][:], initial=init[:, j, 0:1], op0=ALU.mult, op1=ALU.add), r=[b_w, bp, b_i, b_q], w=[b_z])
        V(lambda: nc.vector.tensor_tensor_scan(out=W["zi"][:], data0=rho, data1=W["bi"][:], initial=init[:, j, 1:2], op0=ALU.mult, op1=ALU.add), r=[b_w, bp, b_i, b_q], w=[b_z])
        zrl, zil = W["zr"][:, CS - 1:CS], W["zi"][:, CS - 1:CS]
        cC, sC = P["cC"][:, j:j + 1], P["sC"][:, j:j + 1]
        V(lambda: nc.vector.tensor_tensor(out=itmp[:, j, 0:1], in0=zil, in1=sC, op=ALU.mult), r=[b_z, bp], w=[b_i])
        V(lambda: nc.vector.scalar_tensor_tensor(out=init[:, j, 0:1], in0=zrl, scalar=cC, in1=itmp[:, j, 0:1], op0=ALU.mult, op1=ALU.subtract), r=[b_z, bp], w=[b_i])
        V(lambda: nc.vector.tensor_tensor(out=itmp[:, j, 1:2], in0=zrl, in1=sC, op=ALU.mult), r=[b_z, bp], w=[b_i])
        V(lambda: nc.vector.scalar_tensor_tensor(out=init[:, j, 1:2], in0=zil, scalar=cC, in1=itmp[:, j, 1:2], op0=ALU.mult, op1=ALU.add), r=[b_z, bp], w=[b_i])
        G(lambda: nc.gpsimd.tensor_tensor(out=W["q1"][:], in0=W["zr"][:], in1=cj, op=ALU.mult), r=[b_z, bp], w=[b_q])
        G(lambda: nc.gpsimd.tensor_tensor(out=W["q2"][:], in0=W["zi"][:], in1=sj, op=ALU.mult), r=[b_z, bp], w=[b_q])
        V(lambda: nc.vector.tensor_tensor(out=W["q3"][:], in0=W["zi"][:], in1=cj, op=ALU.mult), r=[b_z, bp], w=[b_q])
        V(lambda: nc.vector.tensor_tensor(out=W["q4"][:], in0=W["zr"][:], in1=sj, op=ALU.mult), r=[b_z, bp], w=[b_q])
        for qi, nm in enumerate(("q1", "q2", "q3", "q4")):
            PE(lambda: nc.tensor.matmul(ps_y[:, :], lhsT=CM[:, j, qi, :], rhs=W[nm][:, :], start=(j == 0 and qi == 0), stop=(j == 3 and qi == 3)), r=[bp, b_q], w=[b_py])
        if j == 3:
            V(lambda: nc.vector.scalar_tensor_tensor(out=yv[:], in0=ut[:], scalar=dsk[:, 0:1], in1=ps_y[:, :], op0=ALU.mult, op1=ALU.add), r=[b_u, bc], w=[b_py, b_y])
            A(lambda: nc.scalar.activation(out=y2[:], in_=yv[:], func=AF.Square), r=[b_y], w=[b_y])
            V(lambda: nc.vector.tensor_scalar(out=y2[:], in0=y2[:], scalar1=0.044715, scalar2=1.0, op0=ALU.mult, op1=ALU.add), r=[b_y], w=[b_y])
            V(lambda: nc.vector.tensor_tensor(out=y2[:], in0=y2[:], in1=yv[:], op=ALU.mult), r=[b_y], w=[b_y])
            A(lambda: nc.scalar.activation(out=y2[:], in_=y2[:], func=AF.Tanh, scale=0.7978845608028654), r=[b_y], w=[b_y])
            V(lambda: nc.vector.scalar_tensor_tensor(out=y2[:], in0=y2[:], scalar=1.0, in1=yv[:], op0=ALU.add, op1=ALU.mult), r=[b_y], w=[b_y])
            A(lambda: nc.scalar.mul(out=yo[:], in_=y2[:], mul=0.5), r=[b_y], w=[b_y])
            mk.dma("sp", yT[:, t0:t0 + CS], yo[:], reads=[b_y], is_output=True)

    NTL_ = NCK * 4
    stage1(0)
    for n in range(NTL_ + 1):
        if n + 1 < NTL_:
            stage1(n + 1)
        if n < NTL_:
            stage1v(n)
        if n >= 1:
            stage2(n - 1)


def build(which, T):
    nc = bass.Bass("TRN2", target_bir_lowering=False)
    dt = lambda n, s, k="ExternalInput": nc.dram_tensor(n, s, F32, kind=k).ap()
    with ExitStack() as ctx:
        mk = MK(nc, ctx)
        if which == "conv":
            emit_conv(nc, mk, T, dt("cvin", [384, T]), dt("cw", [128, 3]), dt("yT", [128, T], "ExternalOutput"), TB=min(T, 2048))
        elif which == "attn":
            emit_attn(nc, mk, T, dt("qkv", [256, T]), dt("btab", [2, 128, 2, 256]), dt("sinkt", [128, 2]), dt("ident", [128, 128]), dt("yT", [128, T], "ExternalOutput"))
        elif which == "s5":
            emit_s5(nc, mk, T, dt("uT", [128, T]), dt("s5par", [128, 4, 3]), dt("s5bb", [128, 4, 2, 16]), dt("s5cc", [128, 4, 2, 16]),
                    dt("s5d", [128, 1]), dt("s5iota", [128, CS]), dt("ident", [128, 128]), dt("yT", [128, T], "ExternalOutput"))
        mk.finish("sp")
        print(which, "ops", mk.nops, "waits", mk.nwaits)
    return nc


import math
import numpy as np
from contextlib import ExitStack

D = 2048
TOK = 2048
NT = TOK // 128
NE = 32
CAP = 256
ALPHA = 4 ** 0.25
DE = 512


def k3_consts():
    tp = np.arange(128)[:, None]
    t = np.arange(128)[None, :]
    U = (tp < t).astype(np.float32)
    ecap = np.broadcast_to((np.arange(NE) * CAP).astype(np.float32)[None, :], (128, NE)).copy()
    return {"U": U, "ecap": ecap, "ident": np.eye(128, dtype=np.float32)}


def emit_k3(nc, mk, ymixT, x, w_out, glu_w, glu_b, rows, wr, br, w1, w3, w2, cst, x1s, Xg, Yg, xout, ymload=None, rowload=None):
    V = lambda fn, r=(), w=(): mk.op("dve", fn, r, w)
    A = lambda fn, r=(), w=(): mk.op("act", fn, r, w)
    G = lambda fn, r=(), w=(): mk.op("pool", fn, r, w)
    PE = lambda fn, r=(), w=(): mk.op("pe", fn, r, w, skip_same=True)
    gw = mk.sb("k3_gw", [128, NT, 2]); slot = mk.sb("k3_slot", [128, NT, 2], I32); b_rt = Buf("route")
    ident = mk.sb("k3_ident", [128, 128]); identb = mk.sb("k3_identb", [128, 128], BF16); bc = Buf("c")
    mk.dma("sp", ident[:], cst["ident"], writes=[bc])
    V(lambda: nc.vector.tensor_copy(out=identb[:], in_=ident[:]), r=[bc], w=[bc])
    with ExitStack() as pa:
        sb = lambda n, s, dt=F32: pa.enter_context(nc.sbuf_tensor("k3a%d_" % mk.gen + n, list(s), dt))
        ps = lambda n, s, dt=F32: pa.enter_context(nc.psum_tensor("k3a%d_" % mk.gen + n, list(s), dt))
        wo = sb("wo", [128, 16, D], BF16); gluw = sb("gluw", [128, 4, 512], BF16); glub = sb("glub", [128, 4])
        R = [sb("row%d" % i, [128, D]) for i in range(5)]
        wrt = sb("wr", [128, 16, 36]); brt = sb("br", [128, 36]); Ut = sb("U", [128, 128]); ones = sb("ones", [128, 128]); ecap = sb("ecap", [128, NE])
        Srun = sb("Srun", [128, NE]); b_S = Buf("S")
        if ymload is None:
            ym = [sb("ym%d" % i, [128, 16, 128], BF16) for i in range(2)]; b_ym = [Buf(), Buf()]
        else:
            ymbig = sb("ymbig", [128, 16, 1024], BF16); _bym = Buf()
            ym = None; b_ym = [_bym, _bym]
        _xt = sb("xt0", [128, D]); _bxt = Buf()
        xt = [_xt, _xt]; b_xt = [_bxt, _bxt]
        sg = sb("sg", [128, 4, 128]); b_sg = Buf()
        xr = sb("xr", [128, D]); b_xr = Buf()
        xn = xr; b_xn = b_xr
        _x1 = sb("x1_0", [128, D]); _bx1 = Buf()
        x1 = [_x1, _x1]; b_x1 = [_bx1, _bx1]
        h2l = [sb("h2_%d" % i_, [128, D]) for i_ in range(2)]; b_h2l = [Buf(), Buf()]
        hb = [sb("hb%d" % i, [128, D], BF16) for i in range(2)]; b_hb = [Buf(), Buf()]
        h2T = sb("h2T", [128, 16, 128]); b_h2T = Buf()
        st = sb("st", [128, 4, 6]); mv = sb("mv", [128, 2]); rs = sb("rs", [128, 1]); nmr = sb("nmr", [128, 1]); b_s = Buf()
        lg = sb("lg", [128, 36]); rt = sb("rt", [128, 16]); em = sb("em", [128, 32]); em2 = sb("em2", [128, 32])
        oh1 = sb("oh1", [128, 32]); oh2 = sb("oh2", [128, 32]); Mk = sb("Mk", [128, 32]); rank = sb("rank", [128, 32]); t32 = sb("t32", [128, 32]); pen = sb("pen", [128, 4]); ohg = sb("ohg", [128, 4]); eg = sb("eg", [128, 4])
        b_r = Buf("r")
        p_g = ps("p_g", [128, 512]); b_pg = Buf()
        p_o = [ps("p_o%d" % i, [128, 512]) for i in range(2)]; b_po = [Buf(), Buf()]
        p_t = [ps("p_t%d" % i, [128, 512]) for i in range(2)]; b_pt = [Buf(), Buf()]
        p_r = ps("p_r", [128, 512]); b_pr = Buf()
        p_k = ps("p_k", [128, 512]); b_pk = Buf()
        mk.dma("pool", wo[:], w_out.rearrange("(k p) c -> p k c", p=128), writes=[bc])
        mk.dma("pool", gluw[:], glu_w.rearrange("(k p) c -> p k c", p=128), writes=[bc])
        mk.dma("sp", glub[:], glu_b, writes=[bc])
        if rowload is None:
            rowload = lambda dst, ri, bcx: mk.dma("sp", dst[:], rows[ri], writes=[bcx])
        for i, ri in enumerate((0, 1, 2, 3, 4)):
            rowload(R[i], ri, bc)
        G(lambda: nc.gpsimd.tensor_scalar(out=R[0][:], in0=R[0][:], scalar1=1.0, scalar2=None, op0=ALU.add), r=[bc], w=[bc])
        G(lambda: nc.gpsimd.tensor_scalar(out=R[3][:], in0=R[3][:], scalar1=1.0, scalar2=None, op0=ALU.add), r=[bc], w=[bc])
        mk.dma("sp", wrt[:], wr.rearrange("(k p) c -> p k c", p=128), writes=[bc])
        mk.dma("sp", brt[:], br, writes=[bc])
        mk.dma("sp", Ut[:], cst["U"], writes=[bc])
        mk.dma("sp", ecap[:], cst["ecap"], writes=[bc])
        V(lambda: nc.vector.memset(ones[:], 1.0), w=[bc])
        V(lambda: nc.vector.memset(Srun[:], 0.0), w=[b_S])
        zt = hb[0]; b_z = b_hb[0]
        V(lambda: nc.vector.memset(zt[:], 0.0), w=[b_z])
        b_Xg = Buf("Xg")
        XgV = Xg.rearrange("(a p) c -> p a c", p=128)
        for a in range(NE * CAP // 128):
            mk.dma("sp", XgV[:, a, :], zt[:], reads=[b_z], writes=[b_Xg])

        def front(t, tick=lambda: None):
            i = t % 2
            ts_ = slice(t * 128, (t + 1) * 128)
            if ymload is None:
                mk.dma("pool", ym[i][:], ymixT[:, ts_].rearrange("(k p) t -> p k t", p=128), writes=[b_ym[i]])
                tick()
                ymt = ym[i]
            else:
                if t % 8 == 0:
                    ymload(t // 8, ymbig, b_ym[i])
                    tick()
                ymt = ymbig[:, :, (t % 8) * 128:(t % 8 + 1) * 128]
            mk.dma("sp", xt[i][:], x[ts_, :], writes=[b_xt[i]])
            tick()
            for oc in range(4):
                for kc in range(4):
                    PE(lambda: nc.tensor.matmul(p_g[:, oc * 128:(oc + 1) * 128], lhsT=gluw[:, kc, oc * 128:(oc + 1) * 128], rhs=ymt[:, 12 + kc, :], start=(kc == 0), stop=(kc == 3)),
                       r=[bc, b_ym[i]], w=[b_pg])
                    tick()
            for oc in range(4):
                A(lambda: nc.scalar.activation(out=sg[:, oc, :], in_=p_g[:, oc * 128:(oc + 1) * 128], func=AF.Sigmoid, bias=glub[:, oc:oc + 1], scale=1.0), r=[bc], w=[b_pg, b_sg])
                tick()
            V(lambda: nc.vector.tensor_tensor(out=ymt[:, 12:16, :], in0=ymt[:, 12:16, :], in1=sg[:], op=ALU.mult), r=[b_sg], w=[b_ym[i]])
            tick()
            for cc in range(4):
                j = cc % 2
                for k in range(16):
                    PE(lambda: nc.tensor.matmul(p_o[j][:, :], lhsT=ymt[:, k, :], rhs=wo[:, k, cc * 512:(cc + 1) * 512], start=(k == 0), stop=(k == 15)), r=[b_ym[i], bc], w=[b_po[j]])
                    tick()
                V(lambda: nc.vector.tensor_tensor(out=xr[:, cc * 512:(cc + 1) * 512], in0=p_o[j][:, :], in1=R[0][:, cc * 512:(cc + 1) * 512], op=ALU.mult), r=[bc], w=[b_po[j], b_xr])
                tick()
            V(lambda: nc.vector.scalar_tensor_tensor(out=xr[:], in0=xt[i][:], scalar=ALPHA, in1=xr[:], op0=ALU.mult, op1=ALU.add), r=[b_xt[i]], w=[b_xr])
            tick()
            ln_stats(nc, mk, xr, b_xr, st, mv, rs, nmr, b_s)
            tick()
            A(lambda: nc.scalar.activation(out=xn[:], in_=xr[:], func=AF.Identity, bias=nmr[:, 0:1], scale=rs[:, 0:1]), r=[b_s], w=[b_xn])
            tick()
            V(lambda: nc.vector.tensor_tensor(out=xn[:], in0=xn[:], in1=R[1][:], op=ALU.mult), r=[bc], w=[b_xn])
            tick()
            V(lambda: nc.vector.tensor_tensor(out=x1[i][:], in0=xn[:], in1=R[2][:], op=ALU.add), r=[b_xn, bc], w=[b_x1[i]])
            tick()
            mk.dma("sp", x1s[ts_, :], x1[i][:], reads=[b_x1[i]])
            tick()
            ln_stats(nc, mk, x1[i], b_x1[i], st, mv, rs, nmr, b_s)
            tick()
            A(lambda: nc.scalar.activation(out=xn[:], in_=x1[i][:], func=AF.Identity, bias=nmr[:, 0:1], scale=rs[:, 0:1]), r=[b_x1[i], b_s], w=[b_xn])
            tick()
            V(lambda: nc.vector.tensor_tensor(out=xn[:], in0=xn[:], in1=R[3][:], op=ALU.mult), r=[bc], w=[b_xn])
            tick()
            h2 = h2l[i]; b_h2 = b_h2l[i]
            V(lambda: nc.vector.tensor_tensor(out=h2[:], in0=xn[:], in1=R[4][:], op=ALU.add), r=[b_xn, bc], w=[b_h2])
            tick()
            A(lambda: nc.scalar.copy(out=hb[i][:], in_=h2[:]), r=[b_h2], w=[b_hb[i]])
            tick()
        def tail(t):
            i = t % 2
            ts_ = slice(t * 128, (t + 1) * 128)
            h2 = h2l[i]; b_h2 = b_h2l[i]
            for half in range(4):
                j = half % 2
                for kk in range(4):
                    k = half * 4 + kk
                    PE(lambda: nc.tensor.transpose(p_t[j][:, kk * 128:(kk + 1) * 128], h2[:, k * 128:(k + 1) * 128], ident[:, :]), r=[b_h2, bc], w=[b_pt[j]])
                    yield
                if j == 0:
                    A(lambda: nc.scalar.copy(out=h2T[:, half * 4:(half + 1) * 4, :].rearrange("p a b -> p (a b)"), in_=p_t[j][:, :]), r=[], w=[b_pt[j], b_h2T])
                    yield
                else:
                    V(lambda: nc.vector.tensor_copy(out=h2T[:, half * 4:(half + 1) * 4, :].rearrange("p a b -> p (a b)"), in_=p_t[j][:, :]), r=[], w=[b_pt[j], b_h2T])
                    yield
            for k in range(16):
                PE(lambda: nc.tensor.matmul(p_r[:, 0:36], lhsT=h2T[:, k, :], rhs=wrt[:, k, :], start=(k == 0), stop=(k == 15)), r=[b_h2T, bc], w=[b_pr])
                yield
            V(lambda: nc.vector.tensor_tensor(out=lg[:], in0=p_r[:, 0:36], in1=brt[:], op=ALU.add), r=[bc], w=[b_pr, b_r])
            yield
            R_ = lambda fn: V(fn, r=[b_r, bc], w=[b_r])
            R_(lambda: nc.vector.reduce_max(out=rt[:, 0:1], in_=lg[:, 0:4], axis=AX.X))
            yield
            R_(lambda: nc.vector.tensor_scalar(out=ohg[:], in0=lg[:, 0:4], scalar1=rt[:, 0:1], scalar2=None, op0=ALU.is_ge))
            yield
            R_(lambda: nc.vector.tensor_scalar(out=rt[:, 1:2], in0=rt[:, 0:1], scalar1=-1.0, scalar2=None, op0=ALU.mult))
            yield
            A(lambda: nc.scalar.activation(out=eg[:], in_=lg[:, 0:4], func=AF.Exp, bias=rt[:, 1:2], scale=1.0, accum_out=rt[:, 2:3]), r=[b_r], w=[b_r])
            yield
            R_(lambda: nc.vector.reciprocal(out=rt[:, 3:4], in_=rt[:, 2:3]))
            yield
            R_(lambda: nc.vector.tensor_scalar(out=pen[:], in0=ohg[:], scalar1=-1.0, scalar2=1e30, op0=ALU.add, op1=ALU.mult))
            yield
            R_(lambda: nc.vector.tensor_tensor(out=em[:].rearrange("p (g e) -> p g e", e=8), in0=lg[:, 4:36].rearrange("p (g e) -> p g e", e=8),
                                               in1=pen[:].unsqueeze(2).to_broadcast([128, 4, 8]), op=ALU.add))
            yield
            R_(lambda: nc.vector.reduce_max(out=rt[:, 4:5], in_=em[:], axis=AX.X))
            yield
            R_(lambda: nc.vector.tensor_scalar(out=oh1[:], in0=em[:], scalar1=rt[:, 4:5], scalar2=None, op0=ALU.is_ge))
            yield
            R_(lambda: nc.vector.scalar_tensor_tensor(out=em2[:], in0=oh1[:], scalar=-1e30, in1=em[:], op0=ALU.mult, op1=ALU.add))
            yield
            R_(lambda: nc.vector.reduce_max(out=rt[:, 5:6], in_=em2[:], axis=AX.X))
            yield
            R_(lambda: nc.vector.tensor_scalar(out=oh2[:], in0=em2[:], scalar1=rt[:, 5:6], scalar2=None, op0=ALU.is_ge))
            yield
            R_(lambda: nc.vector.tensor_tensor(out=rt[:, 6:7], in0=rt[:, 5:6], in1=rt[:, 4:5], op=ALU.subtract))
            yield
            A(lambda: nc.scalar.activation(out=rt[:, 7:8], in_=rt[:, 6:7], func=AF.Exp), r=[b_r], w=[b_r])
            yield
            R_(lambda: nc.vector.tensor_scalar(out=rt[:, 8:9], in0=rt[:, 7:8], scalar1=1.0, scalar2=None, op0=ALU.add))
            yield
            R_(lambda: nc.vector.reciprocal(out=rt[:, 8:9], in_=rt[:, 8:9]))
            yield
            R_(lambda: nc.vector.tensor_tensor(out=rt[:, 9:10], in0=rt[:, 7:8], in1=rt[:, 8:9], op=ALU.mult))
            yield
            R_(lambda: nc.vector.tensor_tensor(out=Mk[:], in0=oh1[:], in1=oh2[:], op=ALU.add))
            yield
            PE(lambda: nc.tensor.matmul(p_k[:, 0:32], lhsT=Ut[:, :], rhs=Mk[:, :], start=True, stop=True), r=[bc, b_r], w=[b_pk])
            yield
            PE(lambda: nc.tensor.matmul(p_k[:, 32:64], lhsT=ones[:, :], rhs=Mk[:, :], start=True, stop=True), r=[bc, b_r], w=[b_pk])
            yield
            V(lambda: nc.vector.tensor_tensor(out=rank[:], in0=p_k[:, 0:32], in1=Srun[:], op=ALU.add), r=[b_S, b_r], w=[b_pk, b_r])
            yield
            V(lambda: nc.vector.tensor_tensor(out=Srun[:], in0=p_k[:, 32:64], in1=Srun[:], op=ALU.add), r=[b_r], w=[b_pk, b_S])
            yield
            for kx, oh in enumerate((oh1, oh2)):
                R_(lambda: nc.vector.tensor_tensor(out=t32[:], in0=oh[:], in1=rank[:], op=ALU.mult))
                yield
                R_(lambda: nc.vector.reduce_sum(out=rt[:, 10:11], in_=t32[:], axis=AX.X))
                yield
                R_(lambda: nc.vector.tensor_tensor(out=t32[:], in0=oh[:], in1=ecap[:], op=ALU.mult))
                yield
                R_(lambda: nc.vector.reduce_sum(out=rt[:, 11:12], in_=t32[:], axis=AX.X))
                yield
                R_(lambda: nc.vector.tensor_scalar(out=rt[:, 12:13], in0=rt[:, 10:11], scalar1=float(CAP), scalar2=None, op0=ALU.is_ge))
                yield
                R_(lambda: nc.vector.tensor_tensor(out=rt[:, 11:12], in0=rt[:, 11:12], in1=rt[:, 10:11], op=ALU.add))
                yield
                R_(lambda: nc.vector.scalar_tensor_tensor(out=rt[:, 11:12], in0=rt[:, 12:13], scalar=1e6, in1=rt[:, 11:12], op0=ALU.mult, op1=ALU.add))
                yield
                V(lambda: nc.vector.tensor_copy(out=slot[:, t, kx:kx + 1], in_=rt[:, 11:12]), r=[b_r], w=[b_rt])
                yield
                R_(lambda: nc.vector.tensor_scalar(out=rt[:, 13:14], in0=rt[:, 12:13], scalar1=-1.0, scalar2=-1.0, op0=ALU.add, op1=ALU.mult))
                yield
                R_(lambda: nc.vector.tensor_tensor(out=rt[:, 13:14], in0=rt[:, 13:14], in1=rt[:, 3:4], op=ALU.mult))
                yield
                V(lambda: nc.vector.tensor_tensor(out=gw[:, t, kx:kx + 1], in0=rt[:, 13:14], in1=rt[:, 8 + kx:9 + kx], op=ALU.mult), r=[b_r], w=[b_rt])
                yield
                mk.idma(Xg, hb[i][:, :], slot[:, t, kx:kx + 1], True, NE * CAP - 1, reads=[b_hb[i], b_rt], writes=[b_Xg])
                yield
        for t in range(NT):
            if t == 0:
                front(0)
            g = tail(t)
            if t + 1 < NT:
                def tick(_g=g):
                    next(_g, None); next(_g, None); next(_g, None)
                front(t + 1, tick)
            for _ in g:
                pass
        mk.barrier()
    b_Yg = Buf("Yg")
    with ExitStack() as pb:
        sb = lambda n, s, dt=F32: pb.enter_context(nc.sbuf_tensor("k3b%d_" % mk.gen + n, list(s), dt))
        ps = lambda n, s, dt=F32: pb.enter_context(nc.psum_tensor("k3b%d_" % mk.gen + n, list(s), dt))
        W1 = [sb("w1_%d" % i, [128, 16, DE], BF16) for i in range(2)]
        W3 = [sb("w3_%d" % i, [128, 16, DE], BF16) for i in range(2)]
        W2 = [sb("w2_%d" % i, [128, 4, D], BF16) for i in range(2)]
        b_w = [Buf(), Buf()]
        NTL = CAP // 128
        xg = sb("xg", [128, NTL, D], BF16); b_xg = Buf()
        xgT = sb("xgT", [128, 16, CAP], BF16); b_xgT = Buf()
        ga = sb("ga", [128, CAP]); b_ga = Buf()
        gh = sb("gh", [128, 4, CAP], BF16); b_gh = Buf()
        yo = [sb("yo%d" % i, [128, D]) for i in range(2)]; b_yo = [Buf(), Buf()]
        p_t = [ps("p_t%d" % i, [128, 1024], BF16) for i in range(2)]; b_pt = [Buf(), Buf()]
        p_a = ps("p_a", [128, 512]); p_b = ps("p_b", [128, 512]); b_pa, b_pb = Buf(), Buf()
        p_o = [ps("p_o%d" % i, [128, 512]) for i in range(2)]; b_po = [Buf(), Buf()]

        def load_w(e):
            j = e % 2
            mk.dma("pool", W1[j][:], w1[e].rearrange("(p k) c -> p k c", k=16), writes=[b_w[j]])
            mk.dma("pool", W3[j][:], w3[e].rearrange("(p k) c -> p k c", k=16), writes=[b_w[j]])
            mk.dma("pool", W2[j][:], w2[e].rearrange("(k p) c -> p k c", p=128), writes=[b_w[j]])

        xgl = [xg, sb("xg_b", [128, NTL, D], BF16)]; b_xgl = [b_xg, Buf()]
        xgTl = [xgT, sb("xgT_b", [128, 16, CAP], BF16)]; b_xgTl = [b_xgT, Buf()]

        def prep(e):
            q_ = e % 2
            xg_, bxg_, xgT_, bxgT_ = xgl[q_], b_xgl[q_], xgTl[q_], b_xgTl[q_]
            mk.dma("sp", xg_[:], Xg[e * CAP:(e + 1) * CAP, :].rearrange("(a p) c -> p a c", p=128), reads=[b_Xg], writes=[bxg_])
            yield
            for a in range(NTL):
                for half in range(2):
                    for kk in range(8):
                        k = half * 8 + kk
                        PE(lambda: nc.tensor.transpose(p_t[half][:, kk * 128:(kk + 1) * 128], xg_[:, a, :].rearrange("t (p k) -> t k p", k=16)[:, k, :], identb[:, :]), r=[bxg_, bc], w=[b_pt[half]])
                    if half == 0:
                        A(lambda: nc.scalar.copy(out=xgT_[:, 0:8, a * 128:(a + 1) * 128], in_=p_t[half][:, :].rearrange("p (k t) -> p k t", t=128)), r=[], w=[b_pt[half], bxgT_])
                    else:
                        V(lambda: nc.vector.tensor_copy(out=xgT_[:, 8:16, a * 128:(a + 1) * 128], in_=p_t[half][:, :].rearrange("p (k t) -> p k t", t=128)), r=[], w=[b_pt[half], bxgT_])
                    yield

        load_w(0)
        yoi = 0
        for _ in prep(0):
            pass
        for e in range(NE):
            j = e % 2
            if e + 1 < NE:
                load_w(e + 1)
            nx = prep(e + 1) if e + 1 < NE else None

            def tick():
                if nx is not None:
                    next(nx, None)
            xgT_ = xgTl[e % 2]; bxgT_ = b_xgTl[e % 2]
            for hc in range(4):
                for k in range(16):
                    PE(lambda: nc.tensor.matmul(p_a[:, 0:CAP], lhsT=W1[j][:, k, hc * 128:(hc + 1) * 128], rhs=xgT_[:, k, :], start=(k == 0), stop=(k == 15)), r=[b_w[j], bxgT_], w=[b_pa])
                for k in range(16):
                    PE(lambda: nc.tensor.matmul(p_b[:, 0:CAP], lhsT=W3[j][:, k, hc * 128:(hc + 1) * 128], rhs=xgT_[:, k, :], start=(k == 0), stop=(k == 15)), r=[b_w[j], bxgT_], w=[b_pb])
                tick()
                A(lambda: nc.scalar.activation(out=ga[:], in_=p_a[:, 0:CAP], func=AF.Silu), r=[], w=[b_pa, b_ga])
                V(lambda: nc.vector.tensor_tensor(out=gh[:, hc, :], in0=p_b[:, 0:CAP], in1=ga[:], op=ALU.mult), r=[b_ga], w=[b_pb, b_gh])
            for a in range(NTL):
                y_ = yoi % 2
                yoi += 1
                for cc in range(4):
                    q = cc % 2
                    for hc in range(4):
                        PE(lambda: nc.tensor.matmul(p_o[q][:, :], lhsT=gh[:, hc, a * 128:(a + 1) * 128], rhs=W2[j][:, hc, cc * 512:(cc + 1) * 512], start=(hc == 0), stop=(hc == 3)), r=[b_gh, b_w[j]], w=[b_po[q]])
                    if q == 0:
                        A(lambda: nc.scalar.copy(out=yo[y_][:, cc * 512:(cc + 1) * 512], in_=p_o[q][:, :]), r=[], w=[b_po[q], b_yo[y_]])
                    else:
                        V(lambda: nc.vector.tensor_copy(out=yo[y_][:, cc * 512:(cc + 1) * 512], in_=p_o[q][:, :]), r=[], w=[b_po[q], b_yo[y_]])
                r0 = e * CAP + a * 128
                mk.dma("sp", Yg[r0:r0 + 128, :], yo[y_][:], reads=[b_yo[y_]], writes=[b_Yg])
            if nx is not None:
                for _ in nx:
                    pass
        mk.barrier()
    with ExitStack() as pc:
        sb = lambda n, s, dt=F32: pc.enter_context(nc.sbuf_tensor("k3c%d_" % mk.gen + n, list(s), dt))
        R = [sb("row%d" % i, [128, D]) for i in range(3)]
        for i, ri in enumerate((5, 6, 7)):
            rowload(R[i], ri, bc)
        G(lambda: nc.gpsimd.tensor_scalar(out=R[0][:], in0=R[0][:], scalar1=1.0, scalar2=None, op0=ALU.add), r=[bc], w=[bc])
        Y = [[sb("Y%d_%d" % (k, i), [128, D]) for i in range(2)] for k in range(2)]
        b_Y = [[Buf(), Buf()], [Buf(), Buf()]]
        for k in range(2):
            for i in range(2):
                V(lambda: nc.vector.memset(Y[k][i][:], 0.0), w=[b_Y[k][i]])
        x1t = [sb("x1t%d" % i, [128, D]) for i in range(2)]; b_x1 = [Buf(), Buf()]
        yml = [sb("ym%d" % i, [128, D]) for i in range(2)]; b_yml = [Buf(), Buf()]
        xnl = [sb("xn%d" % i, [128, D]) for i in range(2)]; b_xnl = [Buf(), Buf()]
        ot = [sb("ot%d" % i, [128, D]) for i in range(2)]; b_ot = [Buf(), Buf()]
        stl = [sb("st%d" % i, [128, 4, 6]) for i in range(2)]; mvl = [sb("mv%d" % i, [128, 2]) for i in range(2)]
        rsl = [sb("rs%d" % i, [128, 1]) for i in range(2)]; nmrl = [sb("nmr%d" % i, [128, 1]) for i in range(2)]; b_sl = [Buf(), Buf()]
        for t in range(NT):
            i = t % 2
            ts_ = slice(t * 128, (t + 1) * 128)
            ym = yml[i]; b_ym = b_yml[i]; xn = xnl[i]; b_xn = b_xnl[i]
            st, mv, rs, nmr, b_s = stl[i], mvl[i], rsl[i], nmrl[i], b_sl[i]
            for k in range(2):
                mk.idma(Y[k][i][:, :], Yg, slot[:, t, k:k + 1], False, NE * CAP - 1, reads=[b_Yg, b_rt], writes=[b_Y[k][i]])
            mk.dma("sp", x1t[i][:], x1s[ts_, :], writes=[b_x1[i]])
            A(lambda: nc.scalar.activation(out=ym[:], in_=Y[0][i][:], func=AF.Copy, scale=gw[:, t, 0:1]), r=[b_Y[0][i], b_rt], w=[b_ym])
            V(lambda: nc.vector.scalar_tensor_tensor(out=ym[:], in0=Y[1][i][:], scalar=gw[:, t, 1:2], in1=ym[:], op0=ALU.mult, op1=ALU.add), r=[b_Y[1][i], b_rt], w=[b_ym])
            V(lambda: nc.vector.tensor_tensor(out=ym[:], in0=ym[:], in1=R[0][:], op=ALU.mult), r=[bc], w=[b_ym])
            V(lambda: nc.vector.scalar_tensor_tensor(out=ym[:], in0=x1t[i][:], scalar=ALPHA, in1=ym[:], op0=ALU.mult, op1=ALU.add), r=[b_x1[i]], w=[b_ym])
            ln_stats(nc, mk, ym, b_ym, st, mv, rs, nmr, b_s)
            A(lambda: nc.scalar.activation(out=xn[:], in_=ym[:], func=AF.Identity, bias=nmr[:, 0:1], scale=rs[:, 0:1]), r=[b_ym, b_s], w=[b_xn])
            V(lambda: nc.vector.tensor_tensor(out=xn[:], in0=xn[:], in1=R[1][:], op=ALU.mult), r=[bc], w=[b_xn])
            V(lambda: nc.vector.tensor_tensor(out=ot[i][:], in0=xn[:], in1=R[2][:], op=ALU.add), r=[b_xn, bc], w=[b_ot[i]])
            mk.dma("sp", xout[ts_, :], ot[i][:], reads=[b_ot[i]], is_output=True)
        mk.barrier()


def build_k3():
    nc = bass.Bass("TRN2", target_bir_lowering=False)
    dt = lambda n, s, k="ExternalInput", d=F32: nc.dram_tensor(n, s, d, kind=k).ap()
    ymixT = dt("ymixT", [D, TOK]); x = dt("x", [TOK, D]); w_out = dt("w_out", [D, D]); glu_w = dt("glu_w", [512, 512]); glu_b = dt("glu_b", [128, 4])
    rows = dt("rows", [8, 128, D]); wr = dt("wr", [D, 36]); br = dt("br", [128, 36])
    w1 = dt("w1", [NE, D, DE]); w3 = dt("w3", [NE, D, DE]); w2 = dt("w2", [NE, DE, D])
    cst = {"U": dt("U", [128, 128]), "ecap": dt("ecap", [128, NE]), "ident": dt("ident", [128, 128])}
    x1s = dt("x1s", [TOK, D], "Internal"); Xg = dt("Xg", [NE * CAP, D], "Internal", BF16); Yg = dt("Yg", [NE * CAP, D], "Internal")
    xout = dt("xout", [TOK, D], "ExternalOutput")
    with ExitStack() as ctx:
        mk = MK(nc, ctx)
        emit_k3(nc, mk, ymixT, x, w_out, glu_w, glu_b, rows, wr, br, w1, w3, w2, cst, x1s, Xg, Yg, xout)
        mk.finish("sp")
        print("k3 ops", mk.nops, "waits", mk.nwaits)
    return nc


def k3_host_inputs(prm, mod, l, b):
    rep = lambda v: np.ascontiguousarray(np.broadcast_to(v[None, :], (128, v.shape[0])))
    sh1, sc1, gt1, sh2, sc2, gt2 = [mod[l, b, i * D:(i + 1) * D] for i in range(6)]
    rows = np.stack([rep(gt1), rep(prm["ln_g"][l, 0]), rep(prm["ln_b"][l, 0]), rep(sc2), rep(sh2), rep(gt2), rep(prm["ln_g"][l, 1]), rep(prm["ln_b"][l, 1])])
    wr = np.ascontiguousarray(np.concatenate([prm["router_group_w"][l], prm["router_expert_w"][l]], axis=1))
    br = rep(np.concatenate([prm["router_group_b"][l], prm["router_expert_b"][l]]))
    d = {"rows": rows, "wr": wr, "br": br, "w_out": prm["w_out"][l], "glu_w": prm["s5_glu_w"][l],
         "glu_b": np.ascontiguousarray(prm["s5_glu_b"][l].reshape(4, 128).T),
         "w1": prm["moe_w1"][l], "w3": prm["moe_w3"][l], "w2": prm["moe_w2"][l]}
    d.update(k3_consts())
    return d


G_ = 512
RW_OFF = 3 * G_
RW_COLS = 3 * G_ + 96 + 96 + 128
ATT_OFF = RW_OFF + RW_COLS
S5_OFF = ATT_OFF + 512 + 2 * 128
SEQ = 8192
RG4 = [[0, 1, 2, 3], [4, 5, 6, 7]]
NMINE = 1472
MODC = 3072


def emit_k0f(nc, mk, cT, w, bb, modin):
    ct = mk.sb("k0_ct", [128, 16, 2]); sct = mk.sb("k0_sct", [128, 16, 2])
    wt = [mk.sb("k0_wt%d" % i, [128, 16, 512]) for i in range(2)]
    bt = mk.sb("k0_bt", [2, 2, MODC]); ot = mk.sb("k0_ot", [2, 2, MODC])
    P = [mk.ps("k0_P%d" % i, [2, 512]) for i in range(2)]
    b_c, b_b, b_o = Buf(), Buf(), Buf()
    b_w = [Buf(), Buf()]; b_p = [Buf(), Buf()]
    mk.dma("sp", ct[:], cT, writes=[b_c])
    mk.dma("sp", bt[:], bb.rearrange("l b c -> b l c"), writes=[b_b])
    mk.op("act", lambda: nc.scalar.activation(out=sct[:], in_=ct[:], func=AF.Silu), reads=[b_c], writes=[b_c])
    it = 0
    for l in range(2):
        for n in range(MODC // 512):
            i = it % 2
            it += 1
            mk.dma("sp", wt[i][:], w[l, :, n * 512:(n + 1) * 512].rearrange("(k p) c -> p k c", p=128), writes=[b_w[i]])
            for k in range(16):
                mk.op("pe", lambda: nc.tensor.matmul(P[i][:], lhsT=sct[:, k, :], rhs=wt[i][:, k, :], start=(k == 0), stop=(k == 15)),
                      reads=[b_c, b_w[i]], writes=[b_p[i]], skip_same=True)
            mk.op("dve", lambda: nc.vector.tensor_tensor(out=ot[:, l, n * 512:(n + 1) * 512], in0=P[i][:], in1=bt[:, l, n * 512:(n + 1) * 512], op=ALU.add),
                  reads=[b_b], writes=[b_p[i], b_o])
    mk.dma("sp", modin.rearrange("(o l) c -> o l c", o=1), ot[0:1, :, :], reads=[b_o])


def mod_row_load(nc, mk, dst, modall, l, chunk, bc):
    c0 = chunk * 2048
    done = 0
    while done < 2048:
        col = c0 + done
        r = col // MODC
        off = col % MODC
        n = min(2048 - done, MODC - off)
        src = modall[r * 2 + l:r * 2 + l + 1, off:off + n].partition_broadcast(128)
        mk.dma("sp", dst[:, done:done + n], src, writes=[bc])
        done += n


def emit_k1a(nc, mk, x, modall, l, ident_d, hTs):
    NT_ = TOK // 128
    xt = [mk.sb("a_xt%d" % i, [128, D]) for i in range(2)]
    xn = mk.sb("a_xn", [128, D]); h1 = mk.sb("a_h1", [128, D])
    hb = [mk.sb("a_hb%d" % i, [128, D], BF16) for i in range(2)]
    hT = mk.sb("a_hT", [128, 16, TOK], BF16)
    sct = mk.sb("a_sct", [128, D]); sht = mk.sb("a_sht", [128, D])
    idf = mk.sb("a_idf", [128, 128]); idb = mk.sb("a_idb", [128, 128], BF16)
    st = mk.sb("a_st", [128, 4, 6]); mv = mk.sb("a_mv", [128, 2]); rs = mk.sb("a_rs", [128, 1]); nmr = mk.sb("a_nmr", [128, 1])
    PT = [mk.ps("a_PT%d" % i, [128, 8, 128], BF16) for i in range(2)]
    b_x = [Buf(), Buf()]
    b_xn, b_h1, b_s, b_sc, b_sh, b_id, b_hT = Buf(), Buf(), Buf(), Buf(), Buf(), Buf(), Buf()
    b_hb = [Buf(), Buf()]; b_pt = [Buf(), Buf()]
    mod_row_load(nc, mk, sct, modall, l, 1, b_sc)
    mod_row_load(nc, mk, sht, modall, l, 0, b_sh)
    mk.dma("sp", idf[:], ident_d, writes=[b_id])
    mk.op("dve", lambda: nc.vector.tensor_copy(out=idb[:], in_=idf[:]), reads=[b_id], writes=[b_id])
    mk.op("pool", lambda: nc.gpsimd.tensor_scalar(out=sct[:], in0=sct[:], scalar1=1.0, scalar2=None, op0=ALU.add), reads=[b_sc], writes=[b_sc])
    def ln_part(t):
        i = t % 2
        mk.dma("sp", xt[i][:], x[t * 128:(t + 1) * 128, :], writes=[b_x[i]])
        ln_stats(nc, mk, xt[i], b_x[i], st, mv, rs, nmr, b_s)
        mk.op("act", lambda: nc.scalar.activation(out=xn[:], in_=xt[i][:], func=AF.Identity, bias=nmr[:, 0:1], scale=rs[:, 0:1]),
              reads=[b_x[i], b_s], writes=[b_xn])
        mk.op("dve", lambda: nc.vector.tensor_tensor(out=h1[:], in0=xn[:], in1=sct[:], op=ALU.mult), reads=[b_xn, b_sc], writes=[b_h1])
        mk.op("dve", lambda: nc.vector.tensor_tensor(out=hb[i][:], in0=h1[:], in1=sht[:], op=ALU.add), reads=[b_h1, b_sh], writes=[b_hb[i]])
    def tr_part(t):
        i = t % 2
        for half in range(2):
            for kk in range(8):
                k = half * 8 + kk
                mk.op("pe", lambda: nc.tensor.transpose(PT[half][:, kk, :], hb[i][:, k * 128:(k + 1) * 128], idb[:]),
                      reads=[b_hb[i], b_id], writes=[b_pt[half]], skip_same=True)
            if half == 0:
                mk.op("act", lambda: nc.scalar.copy(out=hT[:, 0:8, t * 128:(t + 1) * 128], in_=PT[half][:]), reads=[], writes=[b_pt[half], b_hT])
            else:
                mk.op("dve", lambda: nc.vector.tensor_copy(out=hT[:, 8:16, t * 128:(t + 1) * 128], in_=PT[half][:]), reads=[], writes=[b_pt[half], b_hT])
    ln_part(0)
    for t in range(NT_):
        if t + 1 < NT_:
            ln_part(t + 1)
        tr_part(t)
    mk.dma("sp", hTs.rearrange("(k p) t -> p k t", p=128), hT[:], reads=[b_hT])


def emit_k1b(nc, mk, hTg, wmine, pmine):
    NB_ = (NMINE + 127) // 128
    wt = mk.sb("b_wt", [128, 16, NB_ * 128], BF16); b_w = Buf()
    ht = [mk.sb("b_ht%d" % i, [128, 16, 512], BF16) for i in range(2)]; b_h = [Buf(), Buf()]
    ot = [mk.sb("b_ot%d" % i, [128, 512]) for i in range(4)]; b_o = [Buf() for _ in range(4)]
    PM = [mk.ps("b_PM%d" % i, [128, 512]) for i in range(4)]; b_pm = [Buf() for _ in range(4)]
    for j in range(NB_):
        c0 = j * 128
        cw = min(128, NMINE - c0)
        mk.dma("pool", wt[:, :, c0:c0 + cw], wmine[:, c0:c0 + cw].rearrange("(k p) c -> p k c", p=128), writes=[b_w])
    pi = 0
    for tc in range(SEQ // 512):
        i = tc % 2
        r = tc // 4
        t0 = (tc % 4) * 512
        src = hTg.rearrange("(c r h p) t -> r p c h t", c=8, r=4, h=2, p=128)[r]
        for c in range(8):
            mk.dma("sp", ht[i][:, 2 * c:2 * c + 2, :], src[:, c, :, t0:t0 + 512], writes=[b_h[i]])
        for j in range(NB_):
            c0 = j * 128
            cw = min(128, NMINE - c0)
            q = pi % 4
            pi += 1
            for k in range(16):
                mk.op("pe", lambda: nc.tensor.matmul(PM[q][0:cw, :], lhsT=wt[:, k, c0:c0 + cw], rhs=ht[i][:, k, :], start=(k == 0), stop=(k == 15)),
                      reads=[b_w, b_h[i]], writes=[b_pm[q]], skip_same=True)
            if q % 2 == 0:
                mk.op("act", lambda: nc.scalar.copy(out=ot[q][0:cw, :], in_=PM[q][0:cw, :]), reads=[], writes=[b_pm[q], b_o[q]])
            else:
                mk.op("dve", lambda: nc.vector.tensor_copy(out=ot[q][0:cw, :], in_=PM[q][0:cw, :]), reads=[], writes=[b_pm[q], b_o[q]])
            mk.dma("sp", pmine[c0:c0 + cw, tc * 512:(tc + 1) * 512], ot[q][0:cw, :], reads=[b_o[q]])


def build_fused():
    nc = bass.Bass("TRN2", target_bir_lowering=False)
    T = SEQ
    din = lambda n, s, d=F32: nc.dram_tensor(n, s, d, kind="ExternalInput").ap()
    scr = lambda n, s, d=F32: nc.dram_tensor(n, s, d).ap()
    x_in = din("x", [TOK, D]); cT = din("cT", [128, 16, 2]); w_ada = din("w_ada", [2, D, MODC]); bb = din("bb", [2, 2, MODC])
    wmine = din("wmine", [2, D, NMINE]); w_out = din("w_out", [2, D, D]); lnp = din("lnp", [2, 4, D])
    cw = din("cw", [2, 128, 3]); btab = din("btab", [2, 2, 128, 2, 256]); sinkt = din("sinkt", [2, 128, 2]); ident = din("ident", [128, 128])
    s5par = din("s5par", [2, 128, 4, 3]); s5bb = din("s5bb", [2, 128, 4, 2, 16]); s5cc = din("s5cc", [2, 128, 4, 2, 16]); s5d = din("s5d", [2, 128, 1]); s5iota = din("s5iota", [128, CS])
    par64 = din("par64", [2, 64, 2, 11]); par128 = din("par128", [2, 128, 3]); w2 = din("w2", [2, 96, 128]); a2 = din("a2", [2, 96, 128]); g2 = din("g2", [2, 128, 128]); gnt = din("gnt", [2, 64, 2, 2, 64])
    cst = {"mask1": din("mask1", [64, 512]), "mask3": din("mask3", [64, 256]), "seg": din("seg", [128, TB]), "ident": ident, "U": din("U", [128, 128]), "ecap": din("ecap", [128, NE])}
    glu_w = din("glu_w", [2, 512, 512]); glu_b = din("glu_b", [2, 128, 4]); wr = din("wr", [2, D, 36]); br = din("br", [2, 128, 36])
    w1 = din("w1", [2, NE, D, DE]); w3 = din("w3", [2, NE, D, DE]); w2m = din("w2m", [2, NE, DE, D])
    ymidx_d = din("ymidx", [128, 16, 2], I32)
    y_out = nc.dram_tensor("y", [TOK, D], F32, kind="ExternalOutput").ap()
    modin = scr("modin", [2, MODC]); modall = scr("modall", [8, MODC])
    hTs = scr("hTs", [D, TOK], BF16); hTg = scr("hTg", [4 * D, TOK], BF16)
    pmine = scr("pmine", [NMINE, T])
    yT16 = scr("yT16", [512, T], BF16); ymg = scr("ymg", [2048, T], BF16)
    x1s = scr("x1s", [TOK, D]); Xg = scr("Xg", [NE * CAP, D], BF16); Yg = scr("Yg", [NE * CAP, D]); xcur = scr("xcur", [TOK, D])
    ymg_rows = ymg.rearrange("r (tb t) -> (r tb) t", t=1024)
    with ExitStack() as ctx:
        mk = MK(nc, ctx)
        bD = Buf("dram")
        with mk.scope():
            emit_k0f(nc, mk, cT, w_ada, bb, modin)
        mk.collective("AllGather", RG4, modin, modall, reads=[bD], writes=[bD])
        mk.barrier()
        for l in range(2):
            xsrc = x_in if l == 0 else xcur
            xdst = xcur if l == 0 else y_out
            with mk.scope():
                emit_k1a(nc, mk, xsrc, modall, l, ident, hTs)
            for c in range(8):
                mk.collective("AllGather", RG4, hTs[c * 256:(c + 1) * 256, :], hTg[c * 1024:(c + 1) * 1024, :], reads=[bD], writes=[bD])
            mk.barrier()
            with mk.scope():
                emit_k1b(nc, mk, hTg, wmine[l], pmine)
            def ag(chunks):
                for c in chunks:
                    mk.collective("AllGather", RG4, yT16[c * 64:(c + 1) * 64, :], ymg[c * 256:(c + 1) * 256, :], reads=[], writes=[Buf()])
            with mk.scope():
                cg = emit_conv_gen(nc, mk, T, pmine[0:384, :], cw[l], yT16[0:128, :], odt=BF16)
                emit_attn(nc, mk, T, pmine[1088:1344, :], btab[l], sinkt[l], ident, yT16[256:384, :], odt=BF16, tick=lambda: next(cg, None))
                for _ in cg:
                    pass
            ag((0, 1))
            ag((4, 5))
            with mk.scope():
                emit_s5(nc, mk, T, pmine[1344:1472, :], s5par[l], s5bb[l], s5cc[l], s5d[l], s5iota, ident, yT16[384:512, :], odt=BF16)
            ag((6, 7))
            with mk.scope():
                emit_rwkv(nc, mk, T, pmine[384:1088, :], par64[l], par128[l], w2[l], a2[l], g2[l], gnt[l], cst, yT16[128:256, :], odt=BF16)
            for c in (2, 3):
                mk.collective("AllGather", RG4, yT16[c * 64:(c + 1) * 64, :], ymg[c * 256:(c + 1) * 256, :], reads=[bD], writes=[bD])
            mk.barrier()
            with mk.scope():
                ymidx = mk.sb("ymidx_sb", [128, 16, 2], I32); b_idx = Buf()
                mk.dma("sp", ymidx[:], ymidx_d, writes=[b_idx])

                def ymload(hf, ymt, b_ymt):
                    for k in range(16):
                        mk.idma(ymt[:, k, :], ymg_rows, ymidx[:, k, hf:hf + 1], False, 2048 * 8 - 1, reads=[b_idx], writes=[b_ymt])

                def rowload(dst, ri, bcx, _l=l):
                    if ri in (1, 2, 6, 7):
                        j = {1: 0, 2: 1, 6: 2, 7: 3}[ri]
                        mk.dma("sp", dst[:], lnp[_l, j:j + 1, :].partition_broadcast(128), writes=[bcx])
                    else:
                        chunk = {0: 2, 3: 4, 4: 3, 5: 5}[ri]
                        mod_row_load(nc, mk, dst, modall, _l, chunk, bcx)

                emit_k3(nc, mk, None, xsrc, w_out[l], glu_w[l], glu_b[l], None, wr[l], br[l], w1[l], w3[l], w2m[l], cst, x1s, Xg, Yg, xdst,
                        ymload=ymload, rowload=rowload)
        mk.finish("sp")
        mk.barrier()
        print("fused ops", mk.nops, "waits", mk.nwaits)
    return nc


_NC_CACHE = {}


def _get(name, fn):
    if name not in _NC_CACHE:
        _NC_CACHE[name] = fn()
    return _NC_CACHE[name]


def _fused_inputs(prm, core):
    b, q = core // 4, core % 4
    eye = np.eye(128, dtype=np.float32)
    d = {}
    d["x"] = np.ascontiguousarray(prm["x"][b, q * TOK:(q + 1) * TOK])
    cb = prm["c"][b]
    d["cT"] = np.ascontiguousarray(np.stack([cb.reshape(16, 128).T, cb.reshape(16, 128).T], axis=-1))
    sl = slice(q * MODC, (q + 1) * MODC)
    d["w_ada"] = np.ascontiguousarray(prm["w_ada"][:, :, sl])
    d["bb"] = np.ascontiguousarray(np.broadcast_to(prm["b_ada"][:, None, sl], (2, 2, MODC)))
    kv = q // 2
    cols = np.concatenate([np.arange(128 * q, 128 * q + 128), np.arange(G_ + 128 * q, G_ + 128 * q + 128), np.arange(2 * G_ + 128 * q, 2 * G_ + 128 * q + 128),
                           RW_OFF + rwkv_rows(q),
                           np.arange(ATT_OFF + 128 * q, ATT_OFF + 128 * q + 128), np.arange(ATT_OFF + 512 + 64 * kv, ATT_OFF + 512 + 64 * kv + 64),
                           np.arange(ATT_OFF + 640 + 64 * kv, ATT_OFF + 640 + 64 * kv + 64),
                           np.arange(S5_OFF + 128 * q, S5_OFF + 128 * q + 128)])
    assert cols.shape[0] == NMINE
    d["wmine"] = np.ascontiguousarray(prm["w_in"][:, :, cols])
    d["w_out"] = prm["w_out"]
    d["lnp"] = np.ascontiguousarray(np.stack([np.stack([prm["ln_g"][l, 0], prm["ln_b"][l, 0], prm["ln_g"][l, 1], prm["ln_b"][l, 1]]) for l in range(2)]))
    d["cw"] = np.ascontiguousarray(np.stack([prm["conv_w"][l][:, 128 * q:128 * q + 128].T for l in range(2)]))
    tabs = [attn_tables(prm["rel_bias"], prm["attn_sinks"][l], q) for l in range(2)]
    d["btab"] = np.ascontiguousarray(np.stack([t[0] for t in tabs])); d["sinkt"] = np.ascontiguousarray(np.stack([t[1] for t in tabs]))
    d["ident"] = eye
    s5 = [s5_host_inputs(prm, l, q) for l in range(2)]
    for k_ in ("s5par", "s5bb", "s5cc", "s5d"):
        d[k_] = np.ascontiguousarray(np.stack([s[k_] for s in s5]))
    d["s5iota"] = s5[0]["s5iota"]
    rw = [rwkv_host_inputs(prm, l, q) for l in range(2)]
    for k_ in ("par64", "par128", "gnt", "w2", "a2", "g2"):
        d[k_] = np.ascontiguousarray(np.stack([r[k_] for r in rw]))
    for k_ in ("mask1", "mask3", "seg"):
        d[k_] = rw[0][k_]
    kc = k3_consts()
    d["U"] = kc["U"]; d["ecap"] = kc["ecap"]
    d["glu_w"] = prm["s5_glu_w"]
    d["glu_b"] = np.ascontiguousarray(np.stack([prm["s5_glu_b"][l].reshape(4, 128).T for l in range(2)]))
    d["wr"] = np.ascontiguousarray(np.stack([np.concatenate([prm["router_group_w"][l], prm["router_expert_w"][l]], axis=1) for l in range(2)]))
    d["br"] = np.ascontiguousarray(np.stack([np.broadcast_to(np.concatenate([prm["router_group_b"][l], prm["router_expert_b"][l]])[None, :], (128, 36)) for l in range(2)]))
    d["w1"] = prm["moe_w1"]; d["w3"] = prm["moe_w3"]; d["w2m"] = prm["moe_w2"]
    p_ = np.arange(128)[:, None, None]; k_i = np.arange(16)[None, :, None]; t_ = np.arange(2)[None, None, :]
    src_row = ((k_i // 4) * 2 + p_ // 64) * 256 + (k_i % 4) * 64 + p_ % 64
    d["ymidx"] = np.ascontiguousarray((src_row * 8 + q * 2 + t_).astype(np.int32))
    return d


def kernel(**inp):
    prm = {k: np.ascontiguousarray(np.asarray(v, dtype=np.float32)) for k, v in inp.items()}
    cores = list(range(8))
    in_maps = [_fused_inputs(prm, c) for c in cores]
    res = run_bass_kernel_spmd(_get("fused", build_fused), in_maps, core_ids=cores)
    out = np.stack([np.concatenate([res.results[b * 4 + q]["y"] for q in range(4)], axis=0) for b in range(2)])
    return out.astype(np.float32)
```

```python
import numpy as np
from contextlib import ExitStack
import concourse.bass as bass
import concourse.mybir as mybir
from concourse.bass_utils import run_bass_kernel_spmd

F32 = mybir.dt.float32
BF16 = mybir.dt.bfloat16
I32 = mybir.dt.int32
U32 = mybir.dt.uint32
AF = mybir.ActivationFunctionType
ALU = mybir.AluOpType
AX = mybir.AxisListType

EPOCH = 1 << 20


class Buf:
    __slots__ = ("name", "w", "r")

    def __init__(self, name=""):
        self.name = name
        self.w = None
        self.r = {}


class MK:
    def __init__(self, nc, ctx, n_dma_sems=24):
        self.nc = nc
        self.ctx = ctx
        self.eng = {"pe": nc.tensor, "dve": nc.vector, "act": nc.scalar,
                    "pool": nc.gpsimd, "sp": nc.sync}
        self.sem = {}
        self.cnt = {e: 0 for e in self.eng}
        self.known = {e: {} for e in self.eng}
        for e in self.eng:
            self.sem[e] = ctx.enter_context(nc.semaphore("s_" + e))
        self.dma_keys = []
        self.dma_val = {}
        for i in range(n_dma_sems):
            k = ("dma", i)
            self.sem[k] = ctx.enter_context(nc.semaphore("s_dma%d" % i))
            self.dma_keys.append(k)
            self.dma_val[k] = 0
        self.dma_rr = 0
        self.nwaits = 0
        self.nops = 0
        self.out_events = []

    gen = 0

    def sb(self, name, shape, dt=F32):
        return self.ctx.enter_context(self.nc.sbuf_tensor("%s_g%d" % (name, self.gen), list(shape), dt))

    def ps(self, name, shape, dt=F32):
        return self.ctx.enter_context(self.nc.psum_tensor("%s_g%d" % (name, self.gen), list(shape), dt))

    def _wait(self, E, ev):
        if ev is None:
            return
        key, val = ev
        if self.known[E].get(key, 0) >= val:
            return
        self.eng[E].wait_ge(self.sem[key], val)
        self.known[E][key] = val
        self.nwaits += 1

    def _deps(self, E, reads, writes, skip_same=False):
        for b in reads:
            if b.w is not None and not (skip_same and b.w[0] == E):
                self._wait(E, b.w)
        for b in writes:
            if b.w is not None and not (skip_same and b.w[0] == E):
                self._wait(E, b.w)
            for ev in b.r.values():
                if not (skip_same and ev[0] == E):
                    self._wait(E, ev)

    def _mark(self, ev, reads, writes):
        for b in reads:
            b.r[ev[0]] = ev
        for b in writes:
            b.w = ev
            b.r = {}

    def op(self, E, fn, reads=(), writes=(), skip_same=False):
        self._deps(E, reads, writes, skip_same)
        inst = fn()
        self.cnt[E] += 1
        inst.then_inc(self.sem[E], 1)
        ev = (E, self.cnt[E])
        self._mark(ev, reads, writes)
        self.nops += 1
        return ev

    def dma(self, Q, out, in_, reads=(), writes=(), is_output=False, **kw):
        self._deps(Q, reads, writes)
        k = self.dma_keys[self.dma_rr]
        self.dma_rr = (self.dma_rr + 1) % len(self.dma_keys)
        self._wait(Q, (k, self.dma_val[k]) if self.dma_val[k] else None)
        self.dma_val[k] += 16
        inst = self.eng[Q].dma_start(out=out, in_=in_, **kw)
        inst.then_inc(self.sem[k], 16)
        ev = (k, self.dma_val[k])
        self._mark(ev, reads, writes)
        if is_output:
            self.out_events.append(ev)
        self.nops += 1
        return ev

    def finish(self, E="sp"):
        for k in self.dma_keys:
            if self.dma_val[k]:
                self._wait(E, (k, self.dma_val[k]))


def _idma(self, out, in_, idx_ap, scatter, bound, reads=(), writes=(), is_output=False):
    Q = "pool"
    self._deps(Q, reads, writes)
    k = self.dma_keys[self.dma_rr]
    self.dma_rr = (self.dma_rr + 1) % len(self.dma_keys)
    self._wait(Q, (k, self.dma_val[k]) if self.dma_val[k] else None)
    self.dma_val[k] += 16
    off = bass.IndirectOffsetOnAxis(ap=idx_ap, axis=0)
    if not hasattr(self, "_bregs"):
        self._bregs = {}
    if bound not in self._bregs:
        self._bregs[bound] = self.nc.gpsimd.to_reg(bound)
    bound = self._bregs[bound]
    if scatter:
        inst = self.nc.gpsimd.indirect_dma_start(out=out, out_offset=off, in_=in_, in_offset=None, bounds_check=bound, oob_is_err=False)
    else:
        inst = self.nc.gpsimd.indirect_dma_start(out=out, out_offset=None, in_=in_, in_offset=off, bounds_check=bound, oob_is_err=False)
    inst.then_inc(self.sem[k], 16)
    ev = (k, self.dma_val[k])
    self._mark(ev, reads, writes)
    if is_output:
        self.out_events.append(ev)
    self.nops += 1
    return ev


MK.idma = _idma


def _barrier(self):
    for E in self.eng:
        for F in self.eng:
            if self.cnt[F]:
                self._wait(E, (F, self.cnt[F]))
        for k in self.dma_keys:
            if self.dma_val[k]:
                self._wait(E, (k, self.dma_val[k]))
        if getattr(self, "cc_val", 0):
            self._wait(E, ("cc", self.cc_val))


MK.barrier = _barrier


from contextlib import contextmanager


@contextmanager
def _scope(self):
    old = self.ctx
    self.gen += 1
    with ExitStack() as s:
        self.ctx = s
        yield
        self.barrier()
    self.ctx = old


MK.scope = _scope


def _collective(self, kind, rg, in_ap, out_ap, reads=(), writes=()):
    Q = "pool"
    if "cc" not in self.sem:
        self.sem["cc"] = self.ctx.enter_context(self.nc.semaphore("s_cc"))
        self.cc_val = 0
    self._deps(Q, reads, writes)
    self.cc_val += 1
    inst = self.nc.gpsimd.collective_compute(kind, ALU.bypass, replica_groups=rg, ins=[in_ap.opt()], outs=[out_ap.opt()])
    inst.then_inc(self.sem["cc"], 1)
    ev = ("cc", self.cc_val)
    self._mark(ev, reads, writes)
    self.nops += 1
    return ev


MK.collective = _collective


import numpy as np
from contextlib import ExitStack

D = 2048
NIN = 4672
TOK = 2048


def build_k0():
    nc = bass.Bass("TRN2", target_bir_lowering=False)
    NCOL = 1536
    cT = nc.dram_tensor("cT", [128, 16, 2], F32, kind="ExternalInput").ap()
    w = nc.dram_tensor("w", [2, D, NCOL], F32, kind="ExternalInput").ap()
    bb = nc.dram_tensor("bb", [2, 2, NCOL], F32, kind="ExternalInput").ap()
    out = nc.dram_tensor("mod", [2, 2, NCOL], F32, kind="ExternalOutput").ap()
    with ExitStack() as ctx:
        mk = MK(nc, ctx)
        ct = mk.sb("ct", [128, 16, 2])
        sct = mk.sb("sct", [128, 16, 2])
        wt = [mk.sb("wt%d" % i, [128, 16, 512]) for i in range(2)]
        bt = mk.sb("bt", [2, 2, NCOL])
        ot = mk.sb("ot", [2, 2, NCOL])
        P = [mk.ps("P%d" % i, [2, 512]) for i in range(2)]
        b_c, b_b, b_o = Buf(), Buf(), Buf()
        b_w = [Buf(), Buf()]
        b_p = [Buf(), Buf()]
        mk.dma("sp", ct[:], cT, writes=[b_c])
        mk.dma("sp", bt[:], bb.rearrange("l b c -> b l c"), writes=[b_b])
        mk.op("act", lambda: nc.scalar.activation(out=sct[:], in_=ct[:], func=AF.Silu), reads=[b_c], writes=[b_c])
        it = 0
        for l in range(2):
            for n in range(3):
                i = it % 2
                it += 1
                mk.dma("sp", wt[i][:], w[l, :, n * 512:(n + 1) * 512].rearrange("(k p) c -> p k c", p=128), writes=[b_w[i]])
                for k in range(16):
                    mk.op("pe", lambda: nc.tensor.matmul(P[i][:], lhsT=sct[:, k, :], rhs=wt[i][:, k, :], start=(k == 0), stop=(k == 15)),
                          reads=[b_c, b_w[i]], writes=[b_p[i]], skip_same=True)
                mk.op("dve", lambda: nc.vector.tensor_tensor(out=ot[:, l, n * 512:(n + 1) * 512], in0=P[i][:], in1=bt[:, l, n * 512:(n + 1) * 512], op=ALU.add),
                      reads=[b_p[i], b_b], writes=[b_o])
        mk.dma("sp", out.rearrange("l b c -> b l c"), ot[:], reads=[b_o], is_output=True)
        mk.finish("sp")
    return nc


def build_k1():
    nc = bass.Bass("TRN2", target_bir_lowering=False)
    x = nc.dram_tensor("x", [TOK, D], F32, kind="ExternalInput").ap()
    sc = nc.dram_tensor("sc", [128, D], F32, kind="ExternalInput").ap()
    sh = nc.dram_tensor("sh", [128, D], F32, kind="ExternalInput").ap()
    w_in = nc.dram_tensor("w_in", [D, NIN], F32, kind="ExternalInput").ap()
    ident = nc.dram_tensor("ident", [128, 128], F32, kind="ExternalInput").ap()
    pT = nc.dram_tensor("pT", [NIN, TOK], F32, kind="ExternalOutput").ap()
    with ExitStack() as ctx:
        mk = MK(nc, ctx)
        emit_k1(nc, mk, x, sc, sh, w_in, ident, pT)
        mk.finish("sp")
        print("k1 ops", mk.nops, "waits", mk.nwaits)
    return nc


def ln_stats(nc, mk, xt, bx, st, mv, rs, nmr, bs, eps=1e-5):
    for c in range(4):
        mk.op("dve", lambda: nc.vector.bn_stats(out=st[:, c, :], in_=xt[:, c * 512:(c + 1) * 512]), reads=[bx], writes=[bs])
    mk.op("dve", lambda: nc.vector.bn_aggr(out=mv[:], in_=st[:].rearrange("p a b -> p (a b)")), reads=[bs], writes=[bs])
    mk.op("act", lambda: nc.scalar.activation(out=rs[:], in_=mv[:, 1:2], func=AF.Sqrt, bias=eps, scale=1.0), reads=[bs], writes=[bs])
    mk.op("dve", lambda: nc.vector.reciprocal(out=rs[:], in_=rs[:]), reads=[bs], writes=[bs])
    mk.op("dve", lambda: nc.vector.tensor_scalar(out=nmr[:], in0=mv[:, 0:1], scalar1=rs[:, 0:1], scalar2=-1.0, op0=ALU.mult, op1=ALU.mult),
          reads=[bs], writes=[bs])


def emit_k1(nc, mk, x, sc, sh, w_in, ident, pT):
    NT = TOK // 128
    xt = [mk.sb("xt%d" % i, [128, D]) for i in range(2)]
    xn = mk.sb("xn", [128, D])
    h1 = mk.sb("h1", [128, D])
    hb = [mk.sb("hb%d" % i, [128, D], BF16) for i in range(2)]
    hT = mk.sb("hT", [128, 16, TOK], BF16)
    sct = mk.sb("sct", [128, D])
    sht = mk.sb("sht", [128, D])
    idf = mk.sb("idf", [128, 128])
    idb = mk.sb("idb", [128, 128], BF16)
    st = mk.sb("st", [128, 4, 6])
    mv = mk.sb("mv", [128, 2])
    rs = mk.sb("rs", [128, 1])
    nmr = mk.sb("nmr", [128, 1])
    wt = [mk.sb("wt%d" % i, [128, 16, 128], BF16) for i in range(2)]
    ot = [mk.sb("ot%d" % i, [128, TOK]) for i in range(2)]
    PT = [mk.ps("PT%d" % i, [128, 8, 128], BF16) for i in range(2)]
    PM = [mk.ps("PM%d" % i, [128, 512]) for i in range(4)]
    b_x = [Buf(), Buf()]
    b_xn, b_h1, b_s, b_sc, b_sh, b_id, b_hT = Buf(), Buf(), Buf(), Buf(), Buf(), Buf(), Buf()
    b_hb = [Buf(), Buf()]
    b_pt = [Buf(), Buf()]
    b_pm = [Buf() for _ in range(4)]
    b_w = [Buf(), Buf()]
    b_o = [Buf(), Buf()]

    mk.dma("sp", sct[:], sc, writes=[b_sc])
    mk.dma("sp", sht[:], sh, writes=[b_sh])
    mk.dma("sp", idf[:], ident, writes=[b_id])
    mk.op("dve", lambda: nc.vector.tensor_copy(out=idb[:], in_=idf[:]), reads=[b_id], writes=[b_id])
    mk.op("pool", lambda: nc.gpsimd.tensor_scalar(out=sct[:], in0=sct[:], scalar1=1.0, scalar2=None, op0=ALU.add), reads=[b_sc], writes=[b_sc])

    NCB = (NIN + 127) // 128

    def load_w(cb):
        j = cb % 2
        c0 = cb * 128
        cw = min(128, NIN - c0)
        mk.dma("pool", wt[j][:, :, 0:cw], w_in[:, c0:c0 + cw].rearrange("(k p) c -> p k c", p=128), writes=[b_w[j]])

    load_w(0)
    load_w(1)
    for t in range(NT):
        i = t % 2
        mk.dma("sp", xt[i][:], x[t * 128:(t + 1) * 128, :], writes=[b_x[i]])
        ln_stats(nc, mk, xt[i], b_x[i], st, mv, rs, nmr, b_s)
        mk.op("act", lambda: nc.scalar.activation(out=xn[:], in_=xt[i][:], func=AF.Identity, bias=nmr[:, 0:1], scale=rs[:, 0:1]),
              reads=[b_x[i], b_s], writes=[b_xn])
        mk.op("dve", lambda: nc.vector.tensor_tensor(out=h1[:], in0=xn[:], in1=sct[:], op=ALU.mult), reads=[b_xn, b_sc], writes=[b_h1])
        mk.op("pool", lambda: nc.gpsimd.tensor_tensor(out=hb[i][:], in0=h1[:], in1=sht[:], op=ALU.add), reads=[b_h1, b_sh], writes=[b_hb[i]])
        for half in range(2):
            for kk in range(8):
                k = half * 8 + kk
                mk.op("pe", lambda: nc.tensor.transpose(PT[half][:, kk, :], hb[i][:, k * 128:(k + 1) * 128], idb[:]),
                      reads=[b_hb[i], b_id], writes=[b_pt[half]], skip_same=True)
            eng = "act" if half == 0 else "dve"
            if eng == "act":
                mk.op("act", lambda: nc.scalar.copy(out=hT[:, half * 8:(half + 1) * 8, t * 128:(t + 1) * 128], in_=PT[half][:]),
                      reads=[b_pt[half]], writes=[b_hT])
            else:
                mk.op("dve", lambda: nc.vector.tensor_copy(out=hT[:, half * 8:(half + 1) * 8, t * 128:(t + 1) * 128], in_=PT[half][:]),
                      reads=[b_pt[half]], writes=[b_hT])
    pi = 0
    for cb in range(NCB):
        j = cb % 2
        c0 = cb * 128
        cw = min(128, NIN - c0)
        for tc in range(TOK // 512):
            q = pi % 4
            pi += 1
            for k in range(16):
                mk.op("pe", lambda: nc.tensor.matmul(PM[q][0:cw, :], lhsT=wt[j][:, k, 0:cw], rhs=hT[:, k, tc * 512:(tc + 1) * 512],
                                                     start=(k == 0), stop=(k == 15)),
                      reads=[b_w[j], b_hT], writes=[b_pm[q]], skip_same=True)
            if tc % 2 == 0:
                mk.op("act", lambda: nc.scalar.copy(out=ot[j][0:cw, tc * 512:(tc + 1) * 512], in_=PM[q][0:cw, :]), reads=[b_pm[q]], writes=[b_o[j]])
            else:
                mk.op("dve", lambda: nc.vector.tensor_copy(out=ot[j][0:cw, tc * 512:(tc + 1) * 512], in_=PM[q][0:cw, :]), reads=[b_pm[q]], writes=[b_o[j]])
        mk.dma("sp", pT[c0:c0 + cw, :], ot[j][0:cw, :], reads=[b_o[j]], is_output=True)
        if cb + 2 < NCB:
            load_w(cb + 2)


import numpy as np
from contextlib import ExitStack

C = 64
TB = 512
NCH = TB // C


def rwkv_consts():
    s = np.arange(64)[:, None]
    t = np.arange(64)[None, :]
    m_su = (s < t).astype(np.float32)
    m_ui = (s <= t).astype(np.float32)
    m1 = np.concatenate([m_su, m_ui], axis=1)
    mask1 = np.tile(m1, (1, 4))
    m_sl = (t < s).astype(np.float32)
    mask3 = np.tile(m_sl, (1, 4))
    seg = np.ones((128, TB), np.float32)
    seg[:, ::C] = 0.0
    return {"mask1": mask1, "mask3": mask3, "seg": seg, "ident": np.eye(128, dtype=np.float32)}


def emit_rwkv(nc, mk, T, rwin, par64, par128, w2, a2, g2, gnt, cst, yT, odt=F32):
    import os
    LVL = int(os.environ.get("RW_LVL", "9"))
    NB = T // TB
    V = lambda fn, r=(), w=(): mk.op("dve", fn, r, w)
    A = lambda fn, r=(), w=(): mk.op("act", fn, r, w)
    G = lambda fn, r=(), w=(): mk.op("pool", fn, r, w)
    PE = lambda fn, r=(), w=(), ss=True: mk.op("pe", fn, r, w, skip_same=ss)
    import os
    F32R = mybir.dt.float32r
    USE_R = os.environ.get("RW_F32R", "1") == "1"
    RR = (lambda a: a.bitcast(F32R)) if USE_R else (lambda a: a)
    sb = mk.sb
    p64 = sb("rw_p64", [64, 2, 11]); p128 = sb("rw_p128", [128, 3])
    w2t = sb("rw_w2", [96, 128]); a2t = sb("rw_a2", [96, 128]); g2t = sb("rw_g2", [128, 128])
    gn = sb("rw_gn", [64, 2, 2, 64])
    mask1 = sb("rw_mask1", [64, 512]); mask3 = sb("rw_mask3", [64, 256]); seg = sb("rw_seg", [128, TB])
    ident = sb("rw_ident", [128, 128])
    ones64 = sb("rw_ones", [64, 64])
    bc = Buf("const")
    for dst, src in ((p64, par64), (p128, par128), (w2t, w2), (a2t, a2), (g2t, g2), (gn, gnt),
                     (mask1, cst["mask1"]), (mask3, cst["mask3"]), (seg, cst["seg"]), (ident, cst["ident"])):
        mk.dma("sp", dst[:], src, writes=[bc])
    V(lambda: nc.vector.memset(ones64[:], 1.0), w=[bc])
    raw = {}
    for nm in ("r0", "k0", "v0", "r1", "k1", "v1"):
        raw[nm] = sb("rw_raw_" + nm, [64, TB + 1])
    raw["w"] = sb("rw_raw_w", [96, TB + 1]); raw["a"] = sb("rw_raw_a", [96, TB + 1]); raw["g"] = sb("rw_raw_g", [128, TB + 1])
    b_raw = Buf("raw")
    tmp = sb("rw_tmp", [128, TB]); b_tmp = Buf("tmp")
    ws = sb("rw_ws", [96, TB]); as_ = sb("rw_as", [96, TB]); gs = sb("rw_gs", [128, TB]); b_lo = Buf("lo")
    gate = [sb("rw_gate%d" % i_, [128, TB]) for i_ in range(2)]; b_gate = [Buf("gate0"), Buf("gate1")]
    H = []
    for h in range(2):
        d = {}
        for nm in ("rs", "ks", "vs", "lw", "asg", "kkn", "kp", "bv", "cum", "e1", "e2", "BT", "KT", "BH", "KH", "rkr"):
            d[nm] = sb("rw_%s%d" % (nm, h), [64, TB])
        d["AR"] = sb("rw_AR%d" % h, [64, NCH, 128])
        d["cC"] = sb("rw_cC%d" % h, [64, NCH]); d["gC"] = sb("rw_gC%d" % h, [64, NCH])
        d["b"] = Buf("H%d" % h)
        d["bo"] = [Buf("Ho%d_0" % h), Buf("Ho%d_1" % h)]
        d["Vt"] = sb("rw_Vt%d" % h, [64, NCH, 64]); d["BHt"] = sb("rw_BHt%d" % h, [64, NCH, 64]); d["KHt"] = sb("rw_KHt%d" % h, [64, NCH, 64])
        d["bt"] = [Buf("Ht%d_0" % h), Buf("Ht%d_1" % h)]
        for nm_, shp_ in (("AR", [64, NCH, 128]), ("BT", [64, TB]), ("KT", [64, TB]), ("gC", [64, NCH]), ("Vt", [64, NCH, 64]), ("BHt", [64, NCH, 64]), ("KHt", [64, NCH, 64])):
            d[nm_] = [d[nm_], sb("rw_%s%d_b" % (nm_, h), shp_)]
        d["NG"] = sb("rw_NG%d" % h, [64, NCH, 128]); d["LG"] = sb("rw_LG%d" % h, [64, NCH, 128])
        d["L"] = sb("rw_L%d" % h, [64, NCH, 64]); d["bA"] = Buf("A%d" % h)
        d["P"] = [sb("rw_P%d_%d" % (h, i), [64, NCH, 64]) for i in range(2)]
        d["PT"] = [sb("rw_PT%d_%d" % (h, i), [64, NCH, 64]) for i in range(2)]
        d["ST"] = [sb("rw_ST%d_%d" % (h, i), [64, NCH, 64]) for i in range(2)]
        d["bD"] = Buf("D%d" % h)
        d["M"] = sb("rw_M%d" % h, [64, 64]); d["bM"] = Buf("M%d" % h)
        d["X1"] = sb("rw_X1%d" % h, [64, 64]); d["U"] = sb("rw_U%d" % h, [64, 64]); d["bX"] = Buf("X%d" % h); d["bU"] = Buf("U%d" % h)
        H.append(d)
    Yb = sb("rw_Yb", [64, NCH, 2, 64]); b_Y = Buf("Y")
    Ysq = sb("rw_Ysq", [64, NCH, 2, 64])
    st1 = sb("rw_st1", [64, NCH * 2]); st2 = sb("rw_st2", [64, NCH * 2]); st3 = sb("rw_st3", [64, NCH * 2]); b_st = Buf("st")
    sbon = [sb("rw_sbon%d" % i_, [64, NCH, 2]) for i_ in range(2)]; b_sb = [Buf("sbon0"), Buf("sbon1")]
    yo = sb("rw_yo", [128, TB], odt); b_yo = Buf("yo")
    ps_lo = mk.ps("rw_ps_lo", [128, 512]); b_pl = Buf()
    ps_tr = mk.ps("rw_ps_tr", [128, 512]); b_ptr = Buf()
    ps_a1 = mk.ps("rw_ps_a1", [64, 512]); b_pa1 = Buf()
    ps_a2 = mk.ps("rw_ps_a2", [64, 512]); b_pa2 = Buf()
    ps_a3f = mk.ps("rw_ps_a3", [128, 512]); ps_a3 = ps_a3f[0:64, :]; b_pa3 = Buf()
    ps_d = mk.ps("rw_ps_d", [64, 512]); b_pd = Buf()
    ps_d2 = ps_a3; b_pd2 = b_pa3
    ps_sh = [mk.ps("rw_ps_s%d" % h, [64, 512]) for h in range(2)]
    b_psh = [Buf(), Buf()]

    for h in range(2):
        V(lambda: nc.vector.tensor_scalar(out=RR(H[h]["M"][:]), in0=ident[0:64, 0:64], scalar1=0.0, scalar2=None, op0=ALU.mult), r=[bc], w=[H[h]["bM"]])

    rows = {"r0": 0, "r1": 64, "k0": 128, "k1": 192, "v0": 256, "v1": 320, "w": 384, "a": 480, "g": 576}
    nrow = {"r0": 64, "r1": 64, "k0": 64, "k1": 64, "v0": 64, "v1": 64, "w": 96, "a": 96, "g": 128}

    def stage1(blk):
        t0 = blk * TB
        par = blk % 2
        for nm in rows:
            r0, n = rows[nm], nrow[nm]
            if blk == 0:
                V(lambda: nc.vector.memset(raw[nm][0:n, 0:1], 0.0), w=[b_raw])
                yield
                mk.dma("sp", raw[nm][0:n, 1:TB + 1], rwin[r0:r0 + n, 0:TB], writes=[b_raw])
                yield
            else:
                mk.dma("sp", raw[nm][0:n, :], rwin[r0:r0 + n, t0 - 1:t0 + TB], writes=[b_raw])
                yield

        def shift(dst, src, n, mu_ap, bdst):
            V(lambda: nc.vector.tensor_tensor(out=tmp[0:n, :], in0=src[0:n, 0:TB], in1=src[0:n, 1:TB + 1], op=ALU.subtract), r=[b_raw], w=[b_tmp])
            V(lambda: nc.vector.scalar_tensor_tensor(out=dst[0:n, :], in0=tmp[0:n, :], scalar=mu_ap, in1=src[0:n, 1:TB + 1], op0=ALU.mult, op1=ALU.add),
              r=[b_tmp, b_raw, bc], w=[bdst])

        shift(ws, raw["w"], 96, p128[0:96, 0:1], b_lo)
        yield
        shift(as_, raw["a"], 96, p128[0:96, 1:2], b_lo)
        yield
        shift(gs, raw["g"], 128, p128[:, 2:3], b_lo)
        yield
        A(lambda: nc.scalar.activation(out=ws[:], in_=ws[:], func=AF.Tanh), r=[b_lo], w=[b_lo])
        yield
        A(lambda: nc.scalar.activation(out=gs[:], in_=gs[:], func=AF.Sigmoid), r=[b_lo], w=[b_lo])
        yield
        PE(lambda: nc.tensor.matmul(ps_lo[:, :], lhsT=g2t[:, :], rhs=gs[:, :], start=True, stop=True), r=[bc, b_lo], w=[b_pl])
        yield
        A(lambda: nc.scalar.copy(out=gate[par][:], in_=ps_lo[:, :]), r=[b_pl], w=[b_gate[par]])
        yield
        for h in range(2):
            d = H[h]; b = d["b"]; bo = d["bo"][par]
            hs = slice(64 * h, 64 * h + 64)
            shift(d["rs"], raw["r%d" % h], 64, p64[:, h, 0:1], b)
            yield
            shift(d["ks"], raw["k%d" % h], 64, p64[:, h, 1:2], b)
            yield
            shift(d["vs"], raw["v%d" % h], 64, p64[:, h, 2:3], b)
            yield
            PE(lambda: nc.tensor.matmul(ps_lo[0:64, :], lhsT=w2t[:, hs], rhs=ws[:, :], start=True, stop=True), r=[bc, b_lo], w=[b_pl])
            yield
            A(lambda: nc.scalar.activation(out=d["lw"][:], in_=ps_lo[0:64, :], func=AF.Sigmoid, bias=p64[:, h, 3:4], scale=1.0), r=[b_pl, bc], w=[b])
            yield
            V(lambda: nc.vector.tensor_scalar(out=d["lw"][:], in0=d["lw"][:], scalar1=-0.6065306597126334, scalar2=None, op0=ALU.mult), r=[b], w=[b])
            yield
            PE(lambda: nc.tensor.matmul(ps_lo[0:64, :], lhsT=a2t[:, hs], rhs=as_[:, :], start=True, stop=True), r=[bc, b_lo], w=[b_pl])
            yield
            A(lambda: nc.scalar.activation(out=d["asg"][:], in_=ps_lo[0:64, :], func=AF.Sigmoid, bias=p64[:, h, 4:5], scale=1.0), r=[b_pl, bc], w=[b])
            yield
            V(lambda: nc.vector.tensor_scalar(out=d["kkn"][:], in0=d["ks"][:], scalar1=p64[:, h, 5:6], scalar2=None, op0=ALU.mult), r=[b, bc], w=[b])
            yield
            A(lambda: nc.scalar.activation(out=tmp[0:64, :], in_=d["kkn"][:], func=AF.Square), r=[b], w=[b_tmp])
            yield
            PE(lambda: nc.tensor.matmul(ps_lo[0:64, :], lhsT=ones64[:, :], rhs=tmp[0:64, :], start=True, stop=True), r=[bc, b_tmp], w=[b_pl])
            yield
            A(lambda: nc.scalar.activation(out=tmp[0:64, :], in_=ps_lo[0:64, :], func=AF.Sqrt), r=[b_pl], w=[b_tmp])
            yield
            V(lambda: nc.vector.tensor_scalar(out=tmp[0:64, :], in0=tmp[0:64, :], scalar1=1e-12, scalar2=None, op0=ALU.max), r=[b_tmp], w=[b_tmp])
            yield
            V(lambda: nc.vector.reciprocal(out=tmp[0:64, :], in_=tmp[0:64, :]), r=[b_tmp], w=[b_tmp])
            yield
            V(lambda: nc.vector.tensor_tensor(out=d["kkn"][:], in0=d["kkn"][:], in1=tmp[0:64, :], op=ALU.mult), r=[b, b_tmp], w=[b])
            yield
            V(lambda: nc.vector.tensor_scalar(out=tmp[0:64, :], in0=d["asg"][:], scalar1=-1.0, scalar2=p64[:, h, 6:7], op0=ALU.add, op1=ALU.mult), r=[b, bc], w=[b_tmp])
            yield
            V(lambda: nc.vector.scalar_tensor_tensor(out=d["kp"][:], in0=tmp[0:64, :], scalar=1.0, in1=d["ks"][:], op0=ALU.add, op1=ALU.mult), r=[b_tmp, b], w=[b])
            yield
            V(lambda: nc.vector.tensor_tensor(out=d["bv"][:], in0=d["kkn"][:], in1=d["asg"][:], op=ALU.mult), r=[b], w=[b])
            yield
            V(lambda: nc.vector.scalar_tensor_tensor(out=d["rkr"][:], in0=d["rs"][:], scalar=p64[:, h, 7:8], in1=d["kp"][:], op0=ALU.mult, op1=ALU.mult), r=[b, bc], w=[b])
            yield
            V(lambda: nc.vector.tensor_tensor_scan(out=d["cum"][:], data0=seg[0:64, :], data1=d["lw"][:], initial=0.0, op0=ALU.mult, op1=ALU.add), r=[b, bc], w=[b])
            yield
            cum3 = d["cum"][:].rearrange("p (c t) -> p c t", t=C)
            V(lambda: nc.vector.tensor_copy(out=d["cC"][:], in_=cum3[:, :, C - 1]), r=[b], w=[b])
            yield
            A(lambda: nc.scalar.activation(out=d["gC"][par][:], in_=d["cC"][:], func=AF.Exp), r=[b], w=[bo])
            yield
            A(lambda: nc.scalar.activation(out=d["e1"][:], in_=d["cum"][:], func=AF.Exp), r=[b], w=[b])
            yield
            A(lambda: nc.scalar.activation(out=d["e2"][:], in_=d["cum"][:], func=AF.Exp, scale=-1.0), r=[b], w=[b])
            yield
            AR = d["AR"][par]
            V(lambda: nc.vector.tensor_tensor(out=RR(AR[:, :, 64:128]), in0=d["rs"][:].rearrange("p (c t) -> p c t", t=C),
                                              in1=d["e1"][:].rearrange("p (c t) -> p c t", t=C), op=ALU.mult), r=[b], w=[bo])
            yield
            V(lambda: nc.vector.tensor_tensor(out=RR(d["BT"][par][:]), in0=d["bv"][:], in1=d["e2"][:], op=ALU.mult), r=[b], w=[bo])
            yield
            V(lambda: nc.vector.tensor_tensor(out=RR(d["KT"][par][:]), in0=d["kp"][:], in1=d["e2"][:], op=ALU.mult), r=[b], w=[bo])
            yield
            V(lambda: nc.vector.tensor_tensor(out=tmp[0:64, :], in0=d["cum"][:], in1=d["lw"][:], op=ALU.subtract), r=[b], w=[b_tmp])
            yield
            A(lambda: nc.scalar.activation(out=tmp[0:64, :], in_=tmp[0:64, :], func=AF.Exp), r=[b_tmp], w=[b_tmp])
            yield
            V(lambda: nc.vector.scalar_tensor_tensor(out=RR(AR[:, :, 0:64]), in0=d["kkn"][:].rearrange("p (c t) -> p c t", t=C), scalar=-1.0,
                                                     in1=tmp[0:64, :].rearrange("p (c t) -> p c t", t=C), op0=ALU.mult, op1=ALU.mult), r=[b, b_tmp], w=[bo])
            yield
            V(lambda: nc.vector.tensor_tensor(out=tmp[0:64, :].rearrange("p (c t) -> p c t", t=C), in0=d["cC"][:].unsqueeze(2).to_broadcast([64, NCH, C]),
                                              in1=cum3, op=ALU.subtract), r=[b], w=[b_tmp])
            yield
            A(lambda: nc.scalar.activation(out=tmp[0:64, :], in_=tmp[0:64, :], func=AF.Exp), r=[b_tmp], w=[b_tmp])
            yield
            V(lambda: nc.vector.tensor_tensor(out=d["BH"][:], in0=d["bv"][:], in1=tmp[0:64, :], op=ALU.mult), r=[b, b_tmp], w=[b])
            yield
            V(lambda: nc.vector.tensor_tensor(out=d["KH"][:], in0=d["kp"][:], in1=tmp[0:64, :], op=ALU.mult), r=[b, b_tmp], w=[b])
            yield
            for src, dstn in (("vs", "Vt"), ("BH", "BHt"), ("KH", "KHt")):
                for c in range(NCH):
                    PE(lambda: nc.tensor.transpose(ps_tr[0:64, c * 64:(c + 1) * 64], d[src][:, c * C:(c + 1) * C], ident[0:64, 0:64]), r=[b, bc], w=[b_ptr])
                    yield
                A(lambda: nc.scalar.copy(out=RR(d[dstn][par][:].rearrange("p c k -> p (c k)")), in_=ps_tr[0:64, :]), r=[b_ptr], w=[d["bt"][par]])
                yield
            for c in range(NCH):
                PE(lambda: nc.tensor.matmul(ps_tr[0:64, 2 * c:2 * c + 2], lhsT=d["rkr"][:, c * C:(c + 1) * C], rhs=ones64[:, 0:2], start=True, stop=True), r=[b, bc], w=[b_ptr])
                yield
            V(lambda: nc.vector.tensor_copy(out=sbon[par][:, :, h], in_=ps_tr[0:64, 0:2 * NCH:2]), r=[b_ptr], w=[b_sb[par]])
            yield

    def rest(blk, tick):
        t0 = blk * TB
        par = blk % 2
        for h in range(2):
            d = H[h]; b = d["bo"][par]; AR = d["AR"][par]
            tick()
            for half in range(2):
                for cc in range(4):
                    c = half * 4 + cc
                    PE(lambda: nc.tensor.matmul(ps_a1[:, cc * 128:(cc + 1) * 128], lhsT=RR(d["BT"][par][:, c * C:(c + 1) * C]), rhs=RR(AR[:, c, :]), start=True, stop=True), r=[b], w=[b_pa1])
                    PE(lambda: nc.tensor.matmul(ps_a2[:, cc * 128:(cc + 1) * 128], lhsT=RR(d["KT"][par][:, c * C:(c + 1) * C]), rhs=RR(AR[:, c, :]), start=True, stop=True), r=[b], w=[b_pa2])
                    PE(lambda: nc.tensor.matmul(ps_a3[:, cc * 64:(cc + 1) * 64], lhsT=RR(AR[:, c, 0:64]), rhs=RR(d["BT"][par][:, c * C:(c + 1) * C]), start=True, stop=True), r=[b], w=[b_pa3])
                V(lambda: nc.vector.tensor_tensor(out=RR(d["NG"][:, half * 4:half * 4 + 4, :].rearrange("p c k -> p (c k)")), in0=ps_a1[:, :], in1=mask1[:, :], op=ALU.mult), r=[b_pa1, bc], w=[d["bA"]])
                V(lambda: nc.vector.tensor_tensor(out=RR(d["LG"][:, half * 4:half * 4 + 4, :].rearrange("p c k -> p (c k)")), in0=ps_a2[:, :], in1=mask1[:, :], op=ALU.mult), r=[b_pa2, bc], w=[d["bA"]])
                V(lambda: nc.vector.tensor_tensor(out=d["L"][:, half * 4:half * 4 + 4, :].rearrange("p c k -> p (c k)"), in0=ps_a3[:, 0:256], in1=mask3[:, :], op=ALU.mult), r=[b_pa3, bc], w=[d["bA"]])
        DPS = [(ps_d, b_pd, ps_d2, b_pd2), (ps_a1, b_pa1, ps_a2, b_pa2)]
        for h in range(2):
            d = H[h]; P, PT, ST = d["P"], d["PT"], d["ST"]; bD = d["bD"]
            V(lambda: nc.vector.tensor_copy(out=RR(P[0][:]), in_=d["L"][:]), r=[d["bA"]], w=[bD])
            V(lambda: nc.vector.tensor_copy(out=RR(PT[0][:]), in_=d["NG"][:, :, 0:64]), r=[d["bA"]], w=[bD])
            V(lambda: nc.vector.tensor_tensor(out=RR(ST[0][:]), in0=d["NG"][:, :, 0:64], in1=ident[0:64, 0:64].unsqueeze(1).to_broadcast([64, NCH, 64]), op=ALU.add), r=[d["bA"], bc], w=[bD])
        cur = 0
        for lev in range(5):
            nxt = 1 - cur
            tick()
            for h in range(2):
                d = H[h]; P, PT, ST = d["P"], d["PT"], d["ST"]; bD = d["bD"]
                pd, bpd, pd2, bpd2 = DPS[h]
                for c in range(NCH):
                    PE(lambda: nc.tensor.matmul(pd[:, c * 64:(c + 1) * 64], lhsT=RR(PT[cur][:, c, :]), rhs=RR(P[cur][:, c, :]), start=True, stop=True), r=[bD], w=[bpd])
                for c in range(NCH):
                    PE(lambda: nc.tensor.matmul(pd2[:, c * 64:(c + 1) * 64], lhsT=RR(P[cur][:, c, :]), rhs=RR(PT[cur][:, c, :]), start=True, stop=True), r=[bD], w=[bpd2])
            tick()
            for h in range(2):
                d = H[h]; P, PT, ST = d["P"], d["PT"], d["ST"]; bD = d["bD"]
                pd, bpd, pd2, bpd2 = DPS[h]
                V(lambda: nc.vector.tensor_copy(out=RR(P[nxt][:].rearrange("p c k -> p (c k)")), in_=pd[:, :]), r=[], w=[bpd, bD])
                A(lambda: nc.scalar.copy(out=RR(PT[nxt][:].rearrange("p c k -> p (c k)")), in_=pd2[:, :]), r=[], w=[bpd2, bD])
            tick()
            for h in range(2):
                d = H[h]; P, PT, ST = d["P"], d["PT"], d["ST"]; bD = d["bD"]
                pd, bpd, pd2, bpd2 = DPS[h]
                for c in range(NCH):
                    PE(lambda: nc.tensor.matmul(pd[:, c * 64:(c + 1) * 64], lhsT=RR(P[nxt][:, c, :]), rhs=RR(ST[cur][:, c, :]), start=True, stop=True), r=[bD], w=[bpd])
            tick()
            for h in range(2):
                d = H[h]; P, PT, ST = d["P"], d["PT"], d["ST"]; bD = d["bD"]
                pd, bpd, pd2, bpd2 = DPS[h]
                V(lambda: nc.vector.tensor_tensor(out=RR(ST[nxt][:].rearrange("p c k -> p (c k)")), in0=pd[:, :], in1=ST[cur][:].rearrange("p c k -> p (c k)"), op=ALU.add), r=[bD], w=[bpd, bD])
            tick()
            cur = nxt
        for h in range(2):
            H[h]["STf"] = H[h]["ST"][cur]
        for c in range(NCH):
            pp = lambda h, i: ps_sh[h][:, i * 64:(i + 1) * 64]
            tick()
            for h in range(2):
                d = H[h]
                PE(lambda: nc.tensor.matmul(pp(h, 0), lhsT=RR(d["LG"][:, c, 0:64]), rhs=RR(d["Vt"][par][:, c, :]), start=True, stop=False), r=[d["bA"], d["bt"][par]], w=[b_psh[h]])
                PE(lambda: nc.tensor.matmul(pp(h, 0), lhsT=RR(d["AR"][par][:, c, 0:64]), rhs=RR(d["M"][:, :]), start=False, stop=True), r=[d["bo"][par], d["bM"]], w=[b_psh[h]])
            tick()
            for h in range(2):
                d = H[h]
                if h == 0:
                    A(lambda: nc.scalar.copy(out=RR(d["X1"][:]), in_=pp(h, 0)), r=[], w=[b_psh[h], d["bX"]])
                else:
                    V(lambda: nc.vector.tensor_copy(out=RR(d["X1"][:]), in_=pp(h, 0)), r=[], w=[b_psh[h], d["bX"]])
            tick()
            for h in range(2):
                d = H[h]
                PE(lambda: nc.tensor.matmul(pp(h, 1), lhsT=RR(d["STf"][:, c, :]), rhs=RR(d["X1"][:, :]), start=True, stop=True), r=[d["bD"], d["bX"]], w=[b_psh[h]])
            tick()
            for h in range(2):
                d = H[h]
                if h == 0:
                    V(lambda: nc.vector.tensor_copy(out=RR(d["U"][:]), in_=pp(h, 1)), r=[], w=[b_psh[h], d["bU"]])
                else:
                    A(lambda: nc.scalar.copy(out=RR(d["U"][:]), in_=pp(h, 1)), r=[], w=[b_psh[h], d["bU"]])
            tick()
            for h in range(2):
                d = H[h]
                PE(lambda: nc.tensor.matmul(pp(h, 2), lhsT=RR(d["AR"][par][:, c, 64:128]), rhs=RR(d["M"][:, :]), start=True, stop=False), r=[d["bo"][par], d["bM"]], w=[b_psh[h]])
                PE(lambda: nc.tensor.matmul(pp(h, 2), lhsT=RR(d["LG"][:, c, 64:128]), rhs=RR(d["Vt"][par][:, c, :]), start=False, stop=False), r=[d["bA"], d["bt"][par]], w=[b_psh[h]])
                PE(lambda: nc.tensor.matmul(pp(h, 2), lhsT=RR(d["NG"][:, c, 64:128]), rhs=RR(d["U"][:, :]), start=False, stop=True), r=[d["bA"], d["bU"]], w=[b_psh[h]])
                PE(lambda: nc.tensor.matmul(pp(h, 3), lhsT=RR(d["KHt"][par][:, c, :]), rhs=RR(d["Vt"][par][:, c, :]), start=True, stop=False), r=[d["bt"][par]], w=[b_psh[h]])
                PE(lambda: nc.tensor.matmul(pp(h, 3), lhsT=RR(d["BHt"][par][:, c, :]), rhs=RR(d["U"][:, :]), start=False, stop=True), r=[d["bt"][par], d["bU"]], w=[b_psh[h]])
            tick()
            for h in range(2):
                d = H[h]
                V(lambda: nc.vector.scalar_tensor_tensor(out=RR(d["M"][:]), in0=d["M"][:], scalar=d["gC"][par][:, c:c + 1], in1=pp(h, 3), op0=ALU.mult, op1=ALU.add), r=[d["bo"][par], d["bM"]], w=[b_psh[h], d["bM"]])
                A(lambda: nc.scalar.copy(out=Yb[:, c, h, :], in_=pp(h, 2)), r=[], w=[b_psh[h], b_Y])
        Y2 = Yb[:].rearrange("p c h v -> p (c h) v")
        V(lambda: nc.vector.tensor_reduce(out=st1[:], in_=Y2, axis=AX.X, op=ALU.add), r=[b_Y], w=[b_st])
        A(lambda: nc.scalar.activation(out=Ysq[:].rearrange("p c h v -> p (c h v)"), in_=Yb[:].rearrange("p c h v -> p (c h v)"), func=AF.Square), r=[b_Y], w=[b_tmp])
        V(lambda: nc.vector.tensor_reduce(out=st2[:], in_=Ysq[:].rearrange("p c h v -> p (c h) v"), axis=AX.X, op=ALU.add), r=[b_tmp], w=[b_st])
        V(lambda: nc.vector.tensor_scalar(out=st1[:], in0=st1[:], scalar1=1.0 / 64, scalar2=None, op0=ALU.mult), r=[b_st], w=[b_st])
        V(lambda: nc.vector.tensor_tensor(out=st3[:], in0=st1[:], in1=st1[:], op=ALU.mult), r=[b_st], w=[b_st])
        V(lambda: nc.vector.scalar_tensor_tensor(out=st2[:], in0=st2[:], scalar=1.0 / 64, in1=st3[:], op0=ALU.mult, op1=ALU.subtract), r=[b_st], w=[b_st])
        A(lambda: nc.scalar.activation(out=st2[:], in_=st2[:], func=AF.Sqrt, bias=64e-5, scale=1.0), r=[b_st], w=[b_st])
        V(lambda: nc.vector.reciprocal(out=st2[:], in_=st2[:]), r=[b_st], w=[b_st])
        V(lambda: nc.vector.tensor_tensor(out=Y2, in0=Y2, in1=st1[:].unsqueeze(2).to_broadcast([64, NCH * 2, 64]), op=ALU.subtract), r=[b_st, b_Y], w=[b_Y])
        V(lambda: nc.vector.tensor_tensor(out=Y2, in0=Y2, in1=st2[:].unsqueeze(2).to_broadcast([64, NCH * 2, 64]), op=ALU.mult), r=[b_st, b_Y], w=[b_Y])
        for h in range(2):
            V(lambda: nc.vector.tensor_tensor(out=Yb[:, :, h, :], in0=Yb[:, :, h, :], in1=gn[:, 0, h, :].unsqueeze(1).to_broadcast([64, NCH, 64]), op=ALU.mult), r=[b_Y, bc], w=[b_Y])
            V(lambda: nc.vector.tensor_tensor(out=Yb[:, :, h, :], in0=Yb[:, :, h, :], in1=gn[:, 1, h, :].unsqueeze(1).to_broadcast([64, NCH, 64]), op=ALU.add), r=[b_Y, bc], w=[b_Y])
            V(lambda: nc.vector.tensor_tensor(out=Ysq[:, :, h, :], in0=H[h]["Vt"][par][:], in1=sbon[par][:, :, h].unsqueeze(2).to_broadcast([64, NCH, 64]), op=ALU.mult), r=[H[h]["bt"][par], b_sb[par]], w=[b_tmp])
        V(lambda: nc.vector.tensor_tensor(out=Yb[:].rearrange("p c h v -> p (c h v)"), in0=Yb[:].rearrange("p c h v -> p (c h v)"), in1=Ysq[:].rearrange("p c h v -> p (c h v)"), op=ALU.add), r=[b_Y, b_tmp], w=[b_Y])
        for c in range(NCH):
            PE(lambda: nc.tensor.transpose(ps_a3f[:, c * 64:(c + 1) * 64], Yb[:, c, :, :].rearrange("p h v -> p (h v)"), ident[0:64, 0:64]), r=[b_Y, bc], w=[b_pa3])
        V(lambda: nc.vector.tensor_tensor(out=yo[:], in0=ps_a3f[:, :], in1=gate[par][:], op=ALU.mult), r=[b_gate[par]], w=[b_pa3, b_yo])
        mk.dma("sp", yT[:, t0:t0 + TB], yo[:], reads=[b_yo], is_output=True)

    for _ in stage1(0):
        pass
    for blk in range(NB):
        nx = stage1(blk + 1) if blk + 1 < NB else None

        def tick(k=2):
            if nx is not None:
                for _ in range(k):
                    if next(nx, "END") == "END":
                        break
        rest(blk, tick)
        if nx is not None:
            for _ in nx:
                pass


def build_rwkv(T):
    nc = bass.Bass("TRN2", target_bir_lowering=False)
    dt = lambda n, s, k="ExternalInput": nc.dram_tensor(n, s, F32, kind=k).ap()
    rwin = dt("rwin", [704, T]); par64 = dt("par64", [64, 2, 11]); par128 = dt("par128", [128, 3])
    w2 = dt("w2", [96, 128]); a2 = dt("a2", [96, 128]); g2 = dt("g2", [128, 128]); gnt = dt("gnt", [64, 2, 2, 64])
    cst = {"mask1": dt("mask1", [64, 512]), "mask3": dt("mask3", [64, 256]), "seg": dt("seg", [128, TB]), "ident": dt("ident", [128, 128])}
    yT = dt("yT", [128, T], "ExternalOutput")
    with ExitStack() as ctx:
        mk = MK(nc, ctx)
        emit_rwkv(nc, mk, T, rwin, par64, par128, w2, a2, g2, gnt, cst, yT)
        mk.finish("sp")
        print("rwkv ops", mk.nops, "waits", mk.nwaits)
    return nc


def rwkv_host_inputs(prm, l, q):
    G = 512
    cs = slice(128 * q, 128 * q + 128)
    mu = prm["rwkv_mu"][l]
    par64 = np.zeros((64, 2, 11), np.float32)
    for h in range(2):
        c0 = 128 * q + 64 * h
        par64[:, h, 0] = mu[0 * G + c0:0 * G + c0 + 64]
        par64[:, h, 1] = mu[1 * G + c0:1 * G + c0 + 64]
        par64[:, h, 2] = mu[2 * G + c0:2 * G + c0 + 64]
        par64[:, h, 3] = prm["rwkv_w0"][l][c0:c0 + 64]
        par64[:, h, 4] = prm["rwkv_a0"][l][c0:c0 + 64]
        par64[:, h, 5] = prm["rwkv_kk"][l][c0:c0 + 64]
        par64[:, h, 6] = prm["rwkv_ka"][l][c0:c0 + 64]
        par64[:, h, 7] = prm["rwkv_rk"][l][2 * q + h]
    par128 = np.zeros((128, 3), np.float32)
    par128[0:96, 0] = mu[3 * G:3 * G + 96]
    par128[0:96, 1] = mu[3 * G + 96:3 * G + 192]
    par128[:, 2] = mu[3 * G + 192:3 * G + 320]
    gnt = np.zeros((64, 2, 2, 64), np.float32)
    for h in range(2):
        c0 = 128 * q + 64 * h
        gnt[:, 0, h, :] = prm["rwkv_gn_g"][l][c0:c0 + 64][None]
        gnt[:, 1, h, :] = prm["rwkv_gn_b"][l][c0:c0 + 64][None]
    d = {"par64": par64, "par128": par128, "gnt": gnt,
         "w2": np.ascontiguousarray(prm["rwkv_w2"][l][:, cs]), "a2": np.ascontiguousarray(prm["rwkv_a2"][l][:, cs]),
         "g2": np.ascontiguousarray(prm["rwkv_g2"][l][:, cs])}
    d.update(rwkv_consts())
    return d


def rwkv_rows(q):
    G = 512
    idx = []
    for base in (0, G, 2 * G):
        idx += list(range(base + 128 * q, base + 128 * q + 64))
        idx += list(range(base + 128 * q + 64, base + 128 * q + 128))
    idx += list(range(3 * G, 3 * G + 320))
    return np.array(idx)


import math
import numpy as np
from contextlib import ExitStack


def emit_conv_gen(nc, mk, T, cvin, cw, yT, TB=2048, odt=F32):
    V = lambda fn, r=(), w=(): mk.op("dve", fn, r, w)
    G = lambda fn, r=(), w=(): mk.op("pool", fn, r, w)
    cwt = mk.sb("cv_w", [128, 3]); bc = Buf()
    mk.dma("sp", cwt[:], cw, writes=[bc])
    yield
    Bt = mk.sb("cv_B", [128, TB]); Ct = mk.sb("cv_C", [128, TB + 2]); Ht = mk.sb("cv_H", [128, TB + 2])
    z = mk.sb("cv_z", [128, TB + 2]); y = mk.sb("cv_y", [128, TB]); o = mk.sb("cv_o", [128, TB], odt)
    b_in, b_z, b_y, b_o = Buf(), Buf(), Buf(), Buf()
    for blk in range(T // TB):
        t0 = blk * TB
        mk.dma("sp", Bt[:], cvin[0:128, t0:t0 + TB], writes=[b_in])
        yield
        if blk == 0:
            V(lambda: nc.vector.memset(Ct[:, 0:2], 0.0), w=[b_in])
            yield
            V(lambda: nc.vector.memset(Ht[:, 0:2], 0.0), w=[b_in])
            yield
            mk.dma("sp", Ct[:, 2:], cvin[128:256, 0:TB], writes=[b_in])
            yield
            mk.dma("sp", Ht[:, 2:], cvin[256:384, 0:TB], writes=[b_in])
            yield
        else:
            mk.dma("sp", Ct[:], cvin[128:256, t0 - 2:t0 + TB], writes=[b_in])
            yield
            mk.dma("sp", Ht[:], cvin[256:384, t0 - 2:t0 + TB], writes=[b_in])
            yield
        G(lambda: nc.gpsimd.tensor_tensor(out=z[:], in0=Ct[:], in1=Ht[:], op=ALU.mult), r=[b_in], w=[b_z])
        yield
        V(lambda: nc.vector.tensor_scalar(out=y[:], in0=z[:, 2:TB + 2], scalar1=cwt[:, 2:3], scalar2=None, op0=ALU.mult), r=[b_z, bc], w=[b_y])
        yield
        V(lambda: nc.vector.scalar_tensor_tensor(out=y[:], in0=z[:, 1:TB + 1], scalar=cwt[:, 1:2], in1=y[:], op0=ALU.mult, op1=ALU.add), r=[b_z, bc], w=[b_y])
        yield
        V(lambda: nc.vector.scalar_tensor_tensor(out=y[:], in0=z[:, 0:TB], scalar=cwt[:, 0:1], in1=y[:], op0=ALU.mult, op1=ALU.add), r=[b_z, bc], w=[b_y])
        yield
        G(lambda: nc.gpsimd.tensor_tensor(out=o[:], in0=y[:], in1=Bt[:], op=ALU.mult), r=[b_y, b_in], w=[b_o])
        yield
        mk.dma("sp", yT[:, t0:t0 + TB], o[:], reads=[b_o], is_output=True)
        yield


def emit_conv(nc, mk, T, cvin, cw, yT, TB=2048, odt=F32):
    for _ in emit_conv_gen(nc, mk, T, cvin, cw, yT, TB=TB, odt=odt):
        pass


def t5_bucket_np(rel):
    n = np.maximum(rel, 0)
    max_exact = 16
    n_f = np.maximum(n, 1).astype(np.float32)
    large = max_exact + (np.log(n_f / max_exact) / math.log(128 / max_exact) * (32 - max_exact)).astype(np.int32)
    return np.where(n < max_exact, n, np.minimum(large, 31))


def attn_tables(rel_bias, sinks_l, q):
    qi = np.arange(128)[:, None]
    kj = np.arange(256)[None, :]
    rel = qi + 128 - kj
    bucket = t5_bucket_np(rel)
    valid = (rel >= 0) & (rel < 128)
    tab = np.zeros((2, 128, 2, 256), np.float32)
    for h in range(2):
        bias = rel_bias[bucket, 2 * q + h]
        full = np.where(valid, bias, np.float32(-30000.0))
        tab[0, :, h, :] = full
        f0 = full.copy()
        f0[:, 0:128] = -30000.0
        tab[1, :, h, :] = f0
    sk = np.broadcast_to(sinks_l[2 * q:2 * q + 2][None, :], (128, 2)).astype(np.float32).copy()
    return tab, sk


def emit_attn(nc, mk, T, qkv, btab, sinkt_d, ident_d, yT, odt=F32, tick=None):
    V = lambda fn, r=(), w=(): mk.op("dve", fn, r, w)
    A = lambda fn, r=(), w=(): mk.op("act", fn, r, w)
    PE = lambda fn, r=(), w=(): mk.op("pe", fn, r, w, skip_same=True)
    NBK = T // 128
    bt = mk.sb("at_bt", [128, 2, 2, 256]); sk = mk.sb("at_sk", [128, 2]); ident = mk.sb("at_id", [128, 128]); bc = Buf()
    mk.dma("sp", bt[:, 0, :, :], btab[0], writes=[bc])
    mk.dma("sp", bt[:, 1, :, :], btab[1], writes=[bc])
    mk.dma("sp", sk[:], sinkt_d, writes=[bc])
    mk.dma("sp", ident[:], ident_d, writes=[bc])
    CH = 1024
    qt = mk.sb("at_q", [128, CH]); kt = mk.sb("at_k", [128, 128 + CH]); vt = mk.sb("at_v", [64, CH])
    vtok = mk.sb("at_vtok", [128, CH // 128 + 1, 64])
    b_q, b_k, b_v, b_vt = Buf(), Buf(), Buf(), Buf()
    sc = [mk.sb("at_sc%d" % h, [128, 256]) for h in range(2)]; b_sc = [Buf(), Buf()]
    pr = [mk.sb("at_p%d" % h, [128, 256]) for h in range(2)]; b_p = [Buf(), Buf()]
    pT = [mk.sb("at_pT%d" % h, [128, 256]) for h in range(2)]; b_pT = [Buf(), Buf()]
    sm = [mk.sb("at_sm%d" % h, [128, 8]) for h in range(2)]; b_sm = [Buf(), Buf()]
    ot = mk.sb("at_o", [128, 128]); b_o = Buf()
    yo = mk.sb("at_yo", [128, CH], odt); b_yo = Buf()
    ps_s = [mk.ps("at_ps_s%d" % h, [128, 512]) for h in range(2)]; b_ps = [Buf(), Buf()]
    ps_t = [mk.ps("at_ps_t%d" % h, [128, 512]) for h in range(2)]; b_pt = [Buf(), Buf()]
    ps_o = mk.ps("at_ps_o", [128, 512]); b_po = Buf()
    ps_v = mk.ps("at_ps_v", [128, 512]); b_pv = Buf()
    for ch in range(T // CH):
        c0 = ch * CH
        mk.dma("sp", qt[:], qkv[0:128, c0:c0 + CH], writes=[b_q])
        if ch == 0:
            V(lambda: nc.vector.memset(kt[:, 0:128], 0.0), w=[b_k])
            V(lambda: nc.vector.memset(vtok[:, 0, :], 0.0), w=[b_vt])
            for hh in range(2):
                mk.dma("sp", kt[64 * hh:64 * hh + 64, 128:], qkv[128:192, 0:CH], writes=[b_k])
        else:
            for hh in range(2):
                mk.dma("sp", kt[64 * hh:64 * hh + 64, :], qkv[128:192, c0 - 128:c0 + CH], writes=[b_k])
            V(lambda: nc.vector.tensor_copy(out=vtok[:, 0, :], in_=vtok[:, CH // 128, :]), r=[b_vt], w=[b_vt])
        mk.dma("sp", vt[:], qkv[192:256, c0:c0 + CH], writes=[b_v])
        for j in range(CH // 128):
            PE(lambda: nc.tensor.transpose(ps_v[:, j * 64:(j + 1) * 64], vt[:, j * 128:(j + 1) * 128], ident[0:64, 0:64]), r=[b_v, bc], w=[b_pv])
        A(lambda: nc.scalar.copy(out=vtok[:, 1:, :].rearrange("p j d -> p (j d)"), in_=ps_v[:, 0:(CH // 128) * 64]), r=[], w=[b_pv, b_vt])
        for j in range(CH // 128):
            first = 1 if (ch == 0 and j == 0) else 0
            if tick is not None:
                tick()
            HS = [slice(0, 64), slice(64, 128)]
            for h in range(2):
                PE(lambda: nc.tensor.matmul(ps_s[h][:, 0:256], lhsT=qt[HS[h], j * 128:(j + 1) * 128], rhs=kt[HS[h], j * 128:j * 128 + 256], start=True, stop=True),
                   r=[b_q, b_k], w=[b_ps[h]])
            for h in range(2):
                V(lambda: nc.vector.scalar_tensor_tensor(out=sc[h][:], in0=ps_s[h][:, 0:256], scalar=0.125, in1=bt[:, first, h, :], op0=ALU.mult, op1=ALU.add),
                  r=[bc], w=[b_ps[h], b_sc[h]])
                s = sm[h]
                V(lambda: nc.vector.reduce_max(out=s[:, 0:1], in_=sc[h][:], axis=AX.X), r=[b_sc[h]], w=[b_sm[h]])
                V(lambda: nc.vector.tensor_tensor(out=s[:, 0:1], in0=s[:, 0:1], in1=sk[:, h:h + 1], op=ALU.max), r=[bc], w=[b_sm[h]])
                V(lambda: nc.vector.tensor_scalar(out=s[:, 1:2], in0=s[:, 0:1], scalar1=-1.0, scalar2=None, op0=ALU.mult), r=[], w=[b_sm[h]])
            for h in range(2):
                s = sm[h]
                A(lambda: nc.scalar.activation(out=pr[h][:], in_=sc[h][:], func=AF.Exp, bias=s[:, 1:2], scale=1.0, accum_out=s[:, 2:3]), r=[b_sc[h]], w=[b_sm[h], b_p[h]])
                A(lambda: nc.scalar.activation(out=s[:, 3:4], in_=sk[:, h:h + 1], func=AF.Exp, bias=s[:, 1:2], scale=1.0), r=[bc], w=[b_sm[h]])
            for h in range(2):
                for kb in range(2):
                    PE(lambda: nc.tensor.transpose(ps_t[h][:, kb * 128:(kb + 1) * 128], pr[h][:, kb * 128:(kb + 1) * 128], ident[:, :]), r=[b_p[h], bc], w=[b_pt[h]])
            for h in range(2):
                s = sm[h]
                V(lambda: nc.vector.tensor_tensor(out=s[:, 4:5], in0=s[:, 2:3], in1=s[:, 3:4], op=ALU.add), r=[], w=[b_sm[h]])
                V(lambda: nc.vector.reciprocal(out=s[:, 5:6], in_=s[:, 4:5]), r=[], w=[b_sm[h]])
                V(lambda: nc.vector.tensor_copy(out=pT[h][:], in_=ps_t[h][:, 0:256]), r=[], w=[b_pt[h], b_pT[h]])
            for h in range(2):
                for kb in range(2):
                    PE(lambda: nc.tensor.matmul(ps_o[:, h * 64:(h + 1) * 64], lhsT=pT[h][:, kb * 128:(kb + 1) * 128], rhs=vtok[:, j + kb, :], start=(kb == 0), stop=(kb == 1)),
                       r=[b_pT[h], b_vt], w=[b_po])
            for h in range(2):
                s = sm[h]
                A(lambda: nc.scalar.activation(out=ot[:, h * 64:(h + 1) * 64], in_=ps_o[:, h * 64:(h + 1) * 64], func=AF.Copy, scale=s[:, 5:6]), r=[b_sm[h]], w=[b_po, b_o])
            PE(lambda: nc.tensor.transpose(ps_o[:, 128:256], ot[:, :], ident[:, :]), r=[b_o, bc], w=[b_po])
            V(lambda: nc.vector.tensor_copy(out=yo[:, j * 128:(j + 1) * 128], in_=ps_o[:, 128:256]), r=[], w=[b_po, b_yo])
        mk.dma("sp", yT[:, c0:c0 + CH], yo[:], reads=[b_yo], is_output=True)


CS = 512


def s5_host_inputs(prm, l, q):
    g0 = 8 * q
    par = np.zeros((128, 4, 3), np.float32)
    bb = np.zeros((128, 4, 2, 16), np.float32)
    cc = np.zeros((128, 4, 2, 16), np.float32)
    for j in range(4):
        for gl in range(2):
            g = g0 + 2 * j + gl
            ps = slice(64 * gl, 64 * gl + 64)
            par[ps, j, 0] = prm["s5_lambda_re"][l][g]
            par[ps, j, 1] = prm["s5_lambda_im"][l][g]
            par[ps, j, 2] = prm["s5_log_dt"][l][g]
            bb[ps, j, 0, :] = prm["s5_b_re"][l][g]
            bb[ps, j, 1, :] = prm["s5_b_im"][l][g]
            cc[ps, j, 0, :] = prm["s5_c_re"][l][g].T
            cc[ps, j, 1, :] = prm["s5_c_im"][l][g].T
    dsk = np.ascontiguousarray(prm["s5_d"][l][g0:g0 + 8].reshape(128, 1))
    iot = np.broadcast_to(np.arange(CS, dtype=np.float32)[None, :], (128, CS)).copy()
    return {"s5par": par, "s5bb": bb, "s5cc": cc, "s5d": dsk, "s5iota": iot, "ident": np.eye(128, dtype=np.float32)}


def emit_s5(nc, mk, T, uT, par_d, bb_d, cc_d, d_d, iota_d, ident_d, yT, odt=F32):
    V = lambda fn, r=(), w=(): mk.op("dve", fn, r, w)
    A = lambda fn, r=(), w=(): mk.op("act", fn, r, w)
    G = lambda fn, r=(), w=(): mk.op("pool", fn, r, w)
    PE = lambda fn, r=(), w=(): mk.op("pe", fn, r, w, skip_same=True)
    sb = mk.sb
    TWO_PI = 2.0 * math.pi
    par = sb("s5_par", [128, 4, 3]); bb = sb("s5_bb", [128, 4, 2, 16]); cc = sb("s5_cc", [128, 4, 2, 16])
    dsk = sb("s5_d", [128, 1]); iot = sb("s5_iota", [128, CS]); ident = sb("s5_id", [128, 128])
    bc = Buf("c")
    for dst, src in ((par, par_d), (bb, bb_d), (cc, cc_d), (dsk, d_d), (iot, iota_d), (ident, ident_d)):
        mk.dma("sp", dst[:], src, writes=[bc])
    P = {}
    for nm in ("dl", "mag", "th", "cs", "sn", "are", "aim", "den", "zre", "zim", "t1", "t2", "t3", "cC", "sC"):
        P[nm] = sb("s5_p_" + nm, [128, 4])
    ti = sb("s5_ti", [128, 4 * CS], I32)
    bp = Buf("p")
    lr, li, ldt = par[:, :, 0], par[:, :, 1], par[:, :, 2]

    def sincos(sin_out, cos_out, x, n, tmpa, tmpb, tint):
        def wrap(r):
            V(lambda: nc.vector.tensor_scalar(out=tmpb, in0=r, scalar1=0.5, scalar2=None, op0=ALU.is_gt), r=[bp], w=[bp])
            V(lambda: nc.vector.tensor_tensor(out=r, in0=r, in1=tmpb, op=ALU.subtract), r=[bp], w=[bp])
            V(lambda: nc.vector.tensor_scalar(out=tmpb, in0=r, scalar1=-0.5, scalar2=None, op0=ALU.is_lt), r=[bp], w=[bp])
            V(lambda: nc.vector.tensor_tensor(out=r, in0=r, in1=tmpb, op=ALU.add), r=[bp], w=[bp])
        V(lambda: nc.vector.tensor_copy(out=tint, in_=x), r=[bp, bc], w=[bp])
        V(lambda: nc.vector.tensor_copy(out=tmpa, in_=tint), r=[bp], w=[bp])
        V(lambda: nc.vector.tensor_tensor(out=tmpa, in0=x, in1=tmpa, op=ALU.subtract), r=[bp, bc], w=[bp])
        wrap(tmpa)
        A(lambda: nc.scalar.activation(out=sin_out, in_=tmpa, func=AF.Sin, scale=TWO_PI), r=[bp], w=[bp])
        V(lambda: nc.vector.tensor_scalar(out=tmpa, in0=tmpa, scalar1=0.25, scalar2=None, op0=ALU.add), r=[bp], w=[bp])
        wrap(tmpa)
        A(lambda: nc.scalar.activation(out=cos_out, in_=tmpa, func=AF.Sin, scale=TWO_PI), r=[bp], w=[bp])

    A(lambda: nc.scalar.activation(out=P["dl"][:], in_=ldt, func=AF.Exp), r=[bc], w=[bp])
    V(lambda: nc.vector.tensor_tensor(out=P["mag"][:], in0=lr, in1=P["dl"][:], op=ALU.mult), r=[bc, bp], w=[bp])
    A(lambda: nc.scalar.activation(out=P["mag"][:], in_=P["mag"][:], func=AF.Exp), r=[bp], w=[bp])
    V(lambda: nc.vector.tensor_tensor(out=P["th"][:], in0=li, in1=P["dl"][:], op=ALU.mult), r=[bc, bp], w=[bp])
    V(lambda: nc.vector.tensor_scalar(out=P["th"][:], in0=P["th"][:], scalar1=1.0 / TWO_PI, scalar2=None, op0=ALU.mult), r=[bp], w=[bp])
    V(lambda: nc.vector.tensor_copy(out=ti[:, 0:4], in_=P["th"][:]), r=[bp], w=[bp])
    V(lambda: nc.vector.tensor_copy(out=P["t1"][:], in_=ti[:, 0:4]), r=[bp], w=[bp])
    V(lambda: nc.vector.tensor_tensor(out=P["th"][:], in0=P["th"][:], in1=P["t1"][:], op=ALU.subtract), r=[bp], w=[bp])
    V(lambda: nc.vector.tensor_scalar(out=P["t1"][:], in0=P["th"][:], scalar1=0.5, scalar2=None, op0=ALU.is_gt), r=[bp], w=[bp])
    V(lambda: nc.vector.tensor_tensor(out=P["th"][:], in0=P["th"][:], in1=P["t1"][:], op=ALU.subtract), r=[bp], w=[bp])
    V(lambda: nc.vector.tensor_scalar(out=P["t1"][:], in0=P["th"][:], scalar1=-0.5, scalar2=None, op0=ALU.is_lt), r=[bp], w=[bp])
    V(lambda: nc.vector.tensor_tensor(out=P["th"][:], in0=P["th"][:], in1=P["t1"][:], op=ALU.add), r=[bp], w=[bp])
    sincos(P["sn"][:], P["cs"][:], P["th"][:], 4, P["t1"][:], P["t2"][:], ti[:, 0:4])
    V(lambda: nc.vector.tensor_tensor(out=P["are"][:], in0=P["mag"][:], in1=P["cs"][:], op=ALU.mult), r=[bp], w=[bp])
    V(lambda: nc.vector.tensor_tensor(out=P["aim"][:], in0=P["mag"][:], in1=P["sn"][:], op=ALU.mult), r=[bp], w=[bp])
    V(lambda: nc.vector.tensor_tensor(out=P["den"][:], in0=lr, in1=lr, op=ALU.mult), r=[bc], w=[bp])
    V(lambda: nc.vector.tensor_tensor(out=P["t1"][:], in0=li, in1=li, op=ALU.mult), r=[bc], w=[bp])
    V(lambda: nc.vector.tensor_tensor(out=P["den"][:], in0=P["den"][:], in1=P["t1"][:], op=ALU.add), r=[bp], w=[bp])
    V(lambda: nc.vector.reciprocal(out=P["den"][:], in_=P["den"][:]), r=[bp], w=[bp])
    V(lambda: nc.vector.tensor_scalar(out=P["t3"][:], in0=P["are"][:], scalar1=-1.0, scalar2=None, op0=ALU.add), r=[bp], w=[bp])
    V(lambda: nc.vector.tensor_tensor(out=P["t1"][:], in0=P["t3"][:], in1=lr, op=ALU.mult), r=[bp, bc], w=[bp])
    V(lambda: nc.vector.tensor_tensor(out=P["t2"][:], in0=P["aim"][:], in1=li, op=ALU.mult), r=[bp, bc], w=[bp])
    V(lambda: nc.vector.tensor_tensor(out=P["t1"][:], in0=P["t1"][:], in1=P["t2"][:], op=ALU.add), r=[bp], w=[bp])
    V(lambda: nc.vector.tensor_tensor(out=P["zre"][:], in0=P["t1"][:], in1=P["den"][:], op=ALU.mult), r=[bp], w=[bp])
    V(lambda: nc.vector.tensor_tensor(out=P["t1"][:], in0=P["aim"][:], in1=lr, op=ALU.mult), r=[bp, bc], w=[bp])
    V(lambda: nc.vector.tensor_tensor(out=P["t2"][:], in0=P["t3"][:], in1=li, op=ALU.mult), r=[bp, bc], w=[bp])
    V(lambda: nc.vector.tensor_tensor(out=P["t1"][:], in0=P["t1"][:], in1=P["t2"][:], op=ALU.subtract), r=[bp], w=[bp])
    V(lambda: nc.vector.tensor_tensor(out=P["zim"][:], in0=P["t1"][:], in1=P["den"][:], op=ALU.mult), r=[bp], w=[bp])
    bbar = sb("s5_bbar", [128, 4, 2, 16]); tb1 = sb("s5_tb1", [128, 4, 16]); tb2 = sb("s5_tb2", [128, 4, 16])
    zre_b = P["zre"][:].unsqueeze(2).to_broadcast([128, 4, 16]); zim_b = P["zim"][:].unsqueeze(2).to_broadcast([128, 4, 16])
    V(lambda: nc.vector.tensor_tensor(out=tb1[:], in0=bb[:, :, 0, :], in1=zre_b, op=ALU.mult), r=[bp, bc], w=[bp])
    V(lambda: nc.vector.tensor_tensor(out=tb2[:], in0=bb[:, :, 1, :], in1=zim_b, op=ALU.mult), r=[bp, bc], w=[bp])
    V(lambda: nc.vector.tensor_tensor(out=bbar[:, :, 0, :], in0=tb1[:], in1=tb2[:], op=ALU.subtract), r=[bp], w=[bp])
    V(lambda: nc.vector.tensor_tensor(out=tb1[:], in0=bb[:, :, 1, :], in1=zre_b, op=ALU.mult), r=[bp, bc], w=[bp])
    V(lambda: nc.vector.tensor_tensor(out=tb2[:], in0=bb[:, :, 0, :], in1=zim_b, op=ALU.mult), r=[bp, bc], w=[bp])
    V(lambda: nc.vector.tensor_tensor(out=bbar[:, :, 1, :], in0=tb1[:], in1=tb2[:], op=ALU.add), r=[bp], w=[bp])
    BD = sb("s5_BD", [128, 4, 2, 128]); CM = sb("s5_CM", [128, 4, 4, 128]); BbT = sb("s5_BbT", [128, 4, 2, 128])
    V(lambda: nc.vector.memset(BD[:].rearrange("p a b c -> p (a b c)"), 0.0), w=[bp])
    V(lambda: nc.vector.memset(CM[:].rearrange("p a b c -> p (a b c)"), 0.0), w=[bp])
    for j in range(4):
        for gl in range(2):
            ps_ = slice(64 * gl, 64 * gl + 64)
            c0 = 32 * j + 16 * gl
            for ri in range(2):
                V(lambda: nc.vector.tensor_copy(out=BD[ps_, j, ri, c0:c0 + 16], in_=bbar[ps_, j, ri, :]), r=[bp], w=[bp])
            V(lambda: nc.vector.tensor_copy(out=CM[ps_, j, 0, c0:c0 + 16], in_=cc[ps_, j, 0, :]), r=[bc], w=[bp])
            V(lambda: nc.vector.tensor_scalar(out=CM[ps_, j, 1, c0:c0 + 16], in0=cc[ps_, j, 0, :], scalar1=-1.0, scalar2=None, op0=ALU.mult), r=[bc], w=[bp])
            V(lambda: nc.vector.tensor_scalar(out=CM[ps_, j, 2, c0:c0 + 16], in0=cc[ps_, j, 1, :], scalar1=-1.0, scalar2=None, op0=ALU.mult), r=[bc], w=[bp])
            V(lambda: nc.vector.tensor_scalar(out=CM[ps_, j, 3, c0:c0 + 16], in0=cc[ps_, j, 1, :], scalar1=-1.0, scalar2=None, op0=ALU.mult), r=[bc], w=[bp])
    ps_tmp = mk.ps("s5_ps_tmp", [128, 512]); b_pt = Buf()
    for j in range(4):
        for ri in range(2):
            PE(lambda: nc.tensor.transpose(ps_tmp[:, ri * 128:(ri + 1) * 128], BD[:, j, ri, :], ident[:, :]), r=[bp, bc], w=[b_pt])
        V(lambda: nc.vector.tensor_copy(out=BbT[:, j, :, :].rearrange("p a b -> p (a b)"), in_=ps_tmp[:, 0:256]), r=[], w=[b_pt, bp])
    cosT = sb("s5_cosT", [128, 4, CS]); sinT = sb("s5_sinT", [128, 4, CS])
    xa = sb("s5_xa", [128, 4 * CS]); xb = sb("s5_xb", [128, 4 * CS]); xc = sb("s5_xc", [128, 4 * CS])
    for j in range(4):
        V(lambda: nc.vector.tensor_scalar(out=xc[:, j * CS:(j + 1) * CS], in0=iot[:], scalar1=P["th"][:, j:j + 1], scalar2=None, op0=ALU.mult), r=[bp, bc], w=[bp])
    sincos(sinT[:].rearrange("p a b -> p (a b)"), cosT[:].rearrange("p a b -> p (a b)"), xc[:], 4 * CS, xa[:], xb[:], ti[:])
    V(lambda: nc.vector.tensor_scalar(out=P["t3"][:], in0=P["th"][:], scalar1=float(CS), scalar2=None, op0=ALU.mult), r=[bp], w=[bp])
    sincos(P["sC"][:], P["cC"][:], P["t3"][:], 4, P["t1"][:], P["t2"][:], ti[:, 0:4])
    WW = []
    for par in range(2):
        Wd = {}
        for nm in ("t1", "t2", "t3", "t4", "br", "bi", "zr", "zi", "q1", "q2", "q3", "q4"):
            Wd[nm] = sb("s5_w%d_%s" % (par, nm), [128, CS])
        WW.append(Wd)
    b_wl = [Buf("w0"), Buf("w1")]; b_zl = [Buf("z0"), Buf("z1")]; b_ql = [Buf("q0"), Buf("q1")]
    init = sb("s5_init", [128, 4, 2]); itmp = sb("s5_itmp", [128, 4, 2]); b_il = [Buf("init%d" % j) for j in range(4)]
    for j in range(4):
        V(lambda: nc.vector.memset(init[:, j, :], 0.0), w=[b_il[j]])
    yv = sb("s5_yv", [128, CS]); y2 = sb("s5_y2", [128, CS]); yo = sb("s5_yo", [128, CS], odt); b_y = Buf("y")
    ps_al = [mk.ps("s5_ps_a%d" % i, [128, 512]) for i in range(2)]; ps_bl = [mk.ps("s5_ps_b%d" % i, [128, 512]) for i in range(2)]
    ps_y = mk.ps("s5_ps_y", [128, 512])
    b_pal, b_pbl, b_py = [Buf(), Buf()], [Buf(), Buf()], Buf()
    uts = [sb("s5_u%d" % i, [128, CS]) for i in range(2)]; b_ul = [Buf(), Buf()]
    NCK = T // CS

    def stage1(n):
        chk, j = n // 4, n % 4
        par = n % 2
        ut = uts[chk % 2]; b_u = b_ul[chk % 2]
        if j == 0:
            mk.dma("sp", ut[:], uT[:, chk * CS:(chk + 1) * CS], writes=[b_u])
        W = WW[par]; b_w = b_wl[par]
        ps_a = ps_al[par]; ps_b = ps_bl[par]; b_pa = b_pal[par]; b_pb = b_pbl[par]
        PE(lambda: nc.tensor.matmul(ps_a[:, :], lhsT=BbT[:, j, 0, :], rhs=ut[:, :], start=True, stop=True), r=[bp, b_u], w=[b_pa])
        PE(lambda: nc.tensor.matmul(ps_b[:, :], lhsT=BbT[:, j, 1, :], rhs=ut[:, :], start=True, stop=True), r=[bp, b_u], w=[b_pb])

    def stage1v(n):
        chk, j = n // 4, n % 4
        par = n % 2
        W = WW[par]; b_w = b_wl[par]
        ps_a = ps_al[par]; ps_b = ps_bl[par]; b_pa = b_pal[par]; b_pb = b_pbl[par]
        cj, sj = cosT[:, j, :], sinT[:, j, :]
        V(lambda: nc.vector.tensor_tensor(out=W["t1"][:], in0=ps_a[:, :], in1=cj, op=ALU.mult), r=[bp], w=[b_pa, b_w])
        V(lambda: nc.vector.tensor_tensor(out=W["t4"][:], in0=ps_a[:, :], in1=sj, op=ALU.mult), r=[bp], w=[b_pa, b_w])
        V(lambda: nc.vector.tensor_tensor(out=W["t2"][:], in0=ps_b[:, :], in1=sj, op=ALU.mult), r=[bp], w=[b_pb, b_w])
        V(lambda: nc.vector.tensor_tensor(out=W["t3"][:], in0=ps_b[:, :], in1=cj, op=ALU.mult), r=[bp], w=[b_pb, b_w])
        G(lambda: nc.gpsimd.tensor_tensor(out=W["br"][:], in0=W["t1"][:], in1=W["t2"][:], op=ALU.add), r=[b_w], w=[b_w])
        G(lambda: nc.gpsimd.tensor_tensor(out=W["bi"][:], in0=W["t3"][:], in1=W["t4"][:], op=ALU.subtract), r=[b_w], w=[b_w])

    def stage2(n):
        chk, j = n // 4, n % 4
        par = n % 2
        t0 = chk * CS
        ut = uts[chk % 2]; b_u = b_ul[chk % 2]
        W = WW[par]; b_w = b_wl[par]; b_z = b_zl[par]; b_q = b_ql[par]; b_i = b_il[j]
        cj, sj = cosT[:, j, :], sinT[:, j, :]
        rho = P["mag"][:, j:j + 1].to_broadcast([128, CS])
        V(lambda: nc.vector.tensor_tensor_scan(out=W["zr"][:], data0=rho, data1=W["br"][:], initial=init[:, j, 0:1], op0=ALU.mult, op1=ALU.add), r=[b_w, bp, b_i, b_q], w=[b_z])
        V(lambda: nc.vector.tensor_tensor_scan(out=W["zi"][:], data0=rho, data1=W["bi"][:], initial=init[:, j, 1:2], op0=ALU.mult, op1=ALU.add), r=[b_w, bp, b_i, b_q], w=[b_z])
        zrl, zil = W["zr"][:, CS - 1:CS], W["zi"][:, CS - 1:CS]
        cC, sC = P["cC"][:, j:j + 1], P["sC"][:, j:j + 1]
        V(lambda: nc.vector.tensor_tensor(out=itmp[:, j, 0:1], in0=zil, in1=sC, op=ALU.mult), r=[b_z, bp], w=[b_i])
        V(lambda: nc.vector.scalar_tensor_tensor(out=init[:, j, 0:1], in0=zrl, scalar=cC, in1=itmp[:, j, 0:1], op0=ALU.mult, op1=ALU.subtract), r=[b_z, bp], w=[b_i])
        V(lambda: nc.vector.tensor_tensor(out=itmp[:, j, 1:2], in0=zrl, in1=sC, op=ALU.mult), r=[b_z, bp], w=[b_i])
        V(lambda: nc.vector.scalar_tensor_tensor(out=init[:, j, 1:2], in0=zil, scalar=cC, in1=itmp[:, j, 1:2], op0=ALU.mult, op1=ALU.add), r=[b_z, bp], w=[b_i])
        G(lambda: nc.gpsimd.tensor_tensor(out=W["q1"][:], in0=W["zr"][:], in1=cj, op=ALU.mult), r=[b_z, bp], w=[b_q])
        G(lambda: nc.gpsimd.tensor_tensor(out=W["q2"][:], in0=W["zi"][:], in1=sj, op=ALU.mult), r=[b_z, bp], w=[b_q])
        V(lambda: nc.vector.tensor_tensor(out=W["q3"][:], in0=W["zi"][:], in1=cj, op=ALU.mult), r=[b_z, bp], w=[b_q])
        V(lambda: nc.vector.tensor_tensor(out=W["q4"][:], in0=W["zr"][:], in1=sj, op=ALU.mult), r=[b_z, bp], w=[b_q])
        for qi, nm in enumerate(("q1", "q2", "q3", "q4")):
            PE(lambda: nc.tensor.matmul(ps_y[:, :], lhsT=CM[:, j, qi, :], rhs=W[nm][:, :], start=(j == 0 and qi == 0), stop=(j == 3 and qi == 3)), r=[bp, b_q], w=[b_py])
        if j == 3:
            V(lambda: nc.vector.scalar_tensor_tensor(out=yv[:], in0=ut[:], scalar=dsk[:, 0:1], in1=ps_y[:, :], op0=ALU.mult, op1=ALU.add), r=[b_u, bc], w=[b_py, b_y])
            A(lambda: nc.scalar.activation(out=y2[:], in_=yv[:], func=AF.Square), r=[b_y], w=[b_y])
            V(lambda: nc.vector.tensor_scalar(out=y2[:], in0=y2[:], scalar1=0.044715, scalar2=1.0, op0=ALU.mult, op1=ALU.add), r=[b_y], w=[b_y])
            V(lambda: nc.vector.tensor_tensor(out=y2[:], in0=y2[:], in1=yv[:], op=ALU.mult), r=[b_y], w=[b_y])
            A(lambda: nc.scalar.activation(out=y2[:], in_=y2[:], func=AF.Tanh, scale=0.7978845608028654), r=[b_y], w=[b_y])
            V(lambda: nc.vector.scalar_tensor_tensor(out=y2[:], in0=y2[:], scalar=1.0, in1=yv[:], op0=ALU.add, op1=ALU.mult), r=[b_y], w=[b_y])
            A(lambda: nc.scalar.mul(out=yo[:], in_=y2[:], mul=0.5), r=[b_y], w=[b_y])
            mk.dma("sp", yT[:, t0:t0 + CS], yo[:], reads=[b_y], is_output=True)

    NTL_ = NCK * 4
    stage1(0)
    for n in range(NTL_ + 1):
        if n + 1 < NTL_:
            stage1(n + 1)
        if n < NTL_:
            stage1v(n)
        if n >= 1:
            stage2(n - 1)


def build(which, T):
    nc = bass.Bass("TRN2", target_bir_lowering=False)
    dt = lambda n, s, k="ExternalInput": nc.dram_tensor(n, s, F32, kind=k).ap()
    with ExitStack() as ctx:
        mk = MK(nc, ctx)
        if which == "conv":
            emit_conv(nc, mk, T, dt("cvin", [384, T]), dt("cw", [128, 3]), dt("yT", [128, T], "ExternalOutput"), TB=min(T, 2048))
        elif which == "attn":
            emit_attn(nc, mk, T, dt("qkv", [256, T]), dt("btab", [2, 128, 2, 256]), dt("sinkt", [128, 2]), dt("ident", [128, 128]), dt("yT", [128, T], "ExternalOutput"))
        elif which == "s5":
            emit_s5(nc, mk, T, dt("uT", [128, T]), dt("s5par", [128, 4, 3]), dt("s5bb", [128, 4, 2, 16]), dt("s5cc", [128, 4, 2, 16]),
                    dt("s5d", [128, 1]), dt("s5iota", [128, CS]), dt("ident", [128, 128]), dt("yT", [128, T], "ExternalOutput"))
        mk.finish("sp")
        print(which, "ops", mk.nops, "waits", mk.nwaits)
    return nc


import math
import numpy as np
from contextlib import ExitStack

D = 2048
TOK = 2048
NT = TOK // 128
NE = 32
CAP = 256
ALPHA = 4 ** 0.25
DE = 512


def k3_consts():
    tp = np.arange(128)[:, None]
    t = np.arange(128)[None, :]
    U = (tp < t).astype(np.float32)
    ecap = np.broadcast_to((np.arange(NE) * CAP).astype(np.float32)[None, :], (128, NE)).copy()
    return {"U": U, "ecap": ecap, "ident": np.eye(128, dtype=np.float32)}


def emit_k3(nc, mk, ymixT, x, w_out, glu_w, glu_b, rows, wr, br, w1, w3, w2, cst, x1s, Xg, Yg, xout, ymload=None, rowload=None):
    V = lambda fn, r=(), w=(): mk.op("dve", fn, r, w)
    A = lambda fn, r=(), w=(): mk.op("act", fn, r, w)
    G = lambda fn, r=(), w=(): mk.op("pool", fn, r, w)
    PE = lambda fn, r=(), w=(): mk.op("pe", fn, r, w, skip_same=True)
    gw = mk.sb("k3_gw", [128, NT, 2]); slot = mk.sb("k3_slot", [128, NT, 2], I32); b_rt = Buf("route")
    ident = mk.sb("k3_ident", [128, 128]); identb = mk.sb("k3_identb", [128, 128], BF16); bc = Buf("c")
    mk.dma("sp", ident[:], cst["ident"], writes=[bc])
    V(lambda: nc.vector.tensor_copy(out=identb[:], in_=ident[:]), r=[bc], w=[bc])
    with ExitStack() as pa:
        sb = lambda n, s, dt=F32: pa.enter_context(nc.sbuf_tensor("k3a%d_" % mk.gen + n, list(s), dt))
        ps = lambda n, s, dt=F32: pa.enter_context(nc.psum_tensor("k3a%d_" % mk.gen + n, list(s), dt))
        wo = sb("wo", [128, 16, D], BF16); gluw = sb("gluw", [128, 4, 512], BF16); glub = sb("glub", [128, 4])
        R = [sb("row%d" % i, [128, D]) for i in range(5)]
        wrt = sb("wr", [128, 16, 36]); brt = sb("br", [128, 36]); Ut = sb("U", [128, 128]); ones = sb("ones", [128, 128]); ecap = sb("ecap", [128, NE])
        Srun = sb("Srun", [128, NE]); b_S = Buf("S")
        if ymload is None:
            ym = [sb("ym%d" % i, [128, 16, 128], BF16) for i in range(2)]; b_ym = [Buf(), Buf()]
        else:
            ymbig = sb("ymbig", [128, 16, 1024], BF16); _bym = Buf()
            ym = None; b_ym = [_bym, _bym]
        _xt = sb("xt0", [128, D]); _bxt = Buf()
        xt = [_xt, _xt]; b_xt = [_bxt, _bxt]
        sg = sb("sg", [128, 4, 128]); b_sg = Buf()
        xr = sb("xr", [128, D]); b_xr = Buf()
        xn = xr; b_xn = b_xr
        _x1 = sb("x1_0", [128, D]); _bx1 = Buf()
        x1 = [_x1, _x1]; b_x1 = [_bx1, _bx1]
        h2l = [sb("h2_%d" % i_, [128, D]) for i_ in range(2)]; b_h2l = [Buf(), Buf()]
        hb = [sb("hb%d" % i, [128, D], BF16) for i in range(2)]; b_hb = [Buf(), Buf()]
        h2T = sb("h2T", [128, 16, 128]); b_h2T = Buf()
        st = sb("st", [128, 4, 6]); mv = sb("mv", [128, 2]); rs = sb("rs", [128, 1]); nmr = sb("nmr", [128, 1]); b_s = Buf()
        lg = sb("lg", [128, 36]); rt = sb("rt", [128, 16]); em = sb("em", [128, 32]); em2 = sb("em2", [128, 32])
        oh1 = sb("oh1", [128, 32]); oh2 = sb("oh2", [128, 32]); Mk = sb("Mk", [128, 32]); rank = sb("rank", [128, 32]); t32 = sb("t32", [128, 32]); pen = sb("pen", [128, 4]); ohg = sb("ohg", [128, 4]); eg = sb("eg", [128, 4])
        b_r = Buf("r")
        p_g = ps("p_g", [128, 512]); b_pg = Buf()
        p_o = [ps("p_o%d" % i, [128, 512]) for i in range(2)]; b_po = [Buf(), Buf()]
        p_t = [ps("p_t%d" % i, [128, 512]) for i in range(2)]; b_pt = [Buf(), Buf()]
        p_r = ps("p_r", [128, 512]); b_pr = Buf()
        p_k = ps("p_k", [128, 512]); b_pk = Buf()
        mk.dma("pool", wo[:], w_out.rearrange("(k p) c -> p k c", p=128), writes=[bc])
        mk.dma("pool", gluw[:], glu_w.rearrange("(k p) c -> p k c", p=128), writes=[bc])
        mk.dma("sp", glub[:], glu_b, writes=[bc])
        if rowload is None:
            rowload = lambda dst, ri, bcx: mk.dma("sp", dst[:], rows[ri], writes=[bcx])
        for i, ri in enumerate((0, 1, 2, 3, 4)):
            rowload(R[i], ri, bc)
        G(lambda: nc.gpsimd.tensor_scalar(out=R[0][:], in0=R[0][:], scalar1=1.0, scalar2=None, op0=ALU.add), r=[bc], w=[bc])
        G(lambda: nc.gpsimd.tensor_scalar(out=R[3][:], in0=R[3][:], scalar1=1.0, scalar2=None, op0=ALU.add), r=[bc], w=[bc])
        mk.dma("sp", wrt[:], wr.rearrange("(k p) c -> p k c", p=128), writes=[bc])
        mk.dma("sp", brt[:], br, writes=[bc])
        mk.dma("sp", Ut[:], cst["U"], writes=[bc])
        mk.dma("sp", ecap[:], cst["ecap"], writes=[bc])
        V(lambda: nc.vector.memset(ones[:], 1.0), w=[bc])
        V(lambda: nc.vector.memset(Srun[:], 0.0), w=[b_S])
        zt = hb[0]; b_z = b_hb[0]
        V(lambda: nc.vector.memset(zt[:], 0.0), w=[b_z])
        b_Xg = Buf("Xg")
        XgV = Xg.rearrange("(a p) c -> p a c", p=128)
        for a in range(NE * CAP // 128):
            mk.dma("sp", XgV[:, a, :], zt[:], reads=[b_z], writes=[b_Xg])

        def front(t, tick=lambda: None):
            i = t % 2
            ts_ = slice(t * 128, (t + 1) * 128)
            if ymload is None:
                mk.dma("pool", ym[i][:], ymixT[:, ts_].rearrange("(k p) t -> p k t", p=128), writes=[b_ym[i]])
                tick()
                ymt = ym[i]
            else:
                if t % 8 == 0:
                    ymload(t // 8, ymbig, b_ym[i])
                    tick()
                ymt = ymbig[:, :, (t % 8) * 128:(t % 8 + 1) * 128]
            mk.dma("sp", xt[i][:], x[ts_, :], writes=[b_xt[i]])
            tick()
            for oc in range(4):
                for kc in range(4):
                    PE(lambda: nc.tensor.matmul(p_g[:, oc * 128:(oc + 1) * 128], lhsT=gluw[:, kc, oc * 128:(oc + 1) * 128], rhs=ymt[:, 12 + kc, :], start=(kc == 0), stop=(kc == 3)),
                       r=[bc, b_ym[i]], w=[b_pg])
                    tick()
            for oc in range(4):
                A(lambda: nc.scalar.activation(out=sg[:, oc, :], in_=p_g[:, oc * 128:(oc + 1) * 128], func=AF.Sigmoid, bias=glub[:, oc:oc + 1], scale=1.0), r=[bc], w=[b_pg, b_sg])
                tick()
            V(lambda: nc.vector.tensor_tensor(out=ymt[:, 12:16, :], in0=ymt[:, 12:16, :], in1=sg[:], op=ALU.mult), r=[b_sg], w=[b_ym[i]])
            tick()
            for cc in range(4):
                j = cc % 2
                for k in range(16):
                    PE(lambda: nc.tensor.matmul(p_o[j][:, :], lhsT=ymt[:, k, :], rhs=wo[:, k, cc * 512:(cc + 1) * 512], start=(k == 0), stop=(k == 15)), r=[b_ym[i], bc], w=[b_po[j]])
                    tick()
                V(lambda: nc.vector.tensor_tensor(out=xr[:, cc * 512:(cc + 1) * 512], in0=p_o[j][:, :], in1=R[0][:, cc * 512:(cc + 1) * 512], op=ALU.mult), r=[bc], w=[b_po[j], b_xr])
                tick()
            V(lambda: nc.vector.scalar_tensor_tensor(out=xr[:], in0=xt[i][:], scalar=ALPHA, in1=xr[:], op0=ALU.mult, op1=ALU.add), r=[b_xt[i]], w=[b_xr])
            tick()
            ln_stats(nc, mk, xr, b_xr, st, mv, rs, nmr, b_s)
            tick()
            A(lambda: nc.scalar.activation(out=xn[:], in_=xr[:], func=AF.Identity, bias=nmr[:, 0:1], scale=rs[:, 0:1]), r=[b_s], w=[b_xn])
            tick()
            V(lambda: nc.vector.tensor_tensor(out=xn[:], in0=xn[:], in1=R[1][:], op=ALU.mult), r=[bc], w=[b_xn])
            tick()
            V(lambda: nc.vector.tensor_tensor(out=x1[i][:], in0=xn[:], in1=R[2][:], op=ALU.add), r=[b_xn, bc], w=[b_x1[i]])
            tick()
            mk.dma("sp", x1s[ts_, :], x1[i][:], reads=[b_x1[i]])
            tick()
            ln_stats(nc, mk, x1[i], b_x1[i], st, mv, rs, nmr, b_s)
            tick()
            A(lambda: nc.scalar.activation(out=xn[:], in_=x1[i][:], func=AF.Identity, bias=nmr[:, 0:1], scale=rs[:, 0:1]), r=[b_x1[i], b_s], w=[b_xn])
            tick()
            V(lambda: nc.vector.tensor_tensor(out=xn[:], in0=xn[:], in1=R[3][:], op=ALU.mult), r=[bc], w=[b_xn])
            tick()
            h2 = h2l[i]; b_h2 = b_h2l[i]
            V(lambda: nc.vector.tensor_tensor(out=h2[:], in0=xn[:], in1=R[4][:], op=ALU.add), r=[b_xn, bc], w=[b_h2])
            tick()
            A(lambda: nc.scalar.copy(out=hb[i][:], in_=h2[:]), r=[b_h2], w=[b_hb[i]])
            tick()
        def tail(t):
            i = t % 2
            ts_ = slice(t * 128, (t + 1) * 128)
            h2 = h2l[i]; b_h2 = b_h2l[i]
            for half in range(4):
                j = half % 2
                for kk in range(4):
                    k = half * 4 + kk
                    PE(lambda: nc.tensor.transpose(p_t[j][:, kk * 128:(kk + 1) * 128], h2[:, k * 128:(k + 1) * 128], ident[:, :]), r=[b_h2, bc], w=[b_pt[j]])
                    yield
                if j == 0:
                    A(lambda: nc.scalar.copy(out=h2T[:, half * 4:(half + 1) * 4, :].rearrange("p a b -> p (a b)"), in_=p_t[j][:, :]), r=[], w=[b_pt[j], b_h2T])
                    yield
                else:
                    V(lambda: nc.vector.tensor_copy(out=h2T[:, half * 4:(half + 1) * 4, :].rearrange("p a b -> p (a b)"), in_=p_t[j][:, :]), r=[], w=[b_pt[j], b_h2T])
                    yield
            for k in range(16):
                PE(lambda: nc.tensor.matmul(p_r[:, 0:36], lhsT=h2T[:, k, :], rhs=wrt[:, k, :], start=(k == 0), stop=(k == 15)), r=[b_h2T, bc], w=[b_pr])
                yield
            V(lambda: nc.vector.tensor_tensor(out=lg[:], in0=p_r[:, 0:36], in1=brt[:], op=ALU.add), r=[bc], w=[b_pr, b_r])
            yield
            R_ = lambda fn: V(fn, r=[b_r, bc], w=[b_r])
            R_(lambda: nc.vector.reduce_max(out=rt[:, 0:1], in_=lg[:, 0:4], axis=AX.X))
            yield
            R_(lambda: nc.vector.tensor_scalar(out=ohg[:], in0=lg[:, 0:4], scalar1=rt[:, 0:1], scalar2=None, op0=ALU.is_ge))
            yield
            R_(lambda: nc.vector.tensor_scalar(out=rt[:, 1:2], in0=rt[:, 0:1], scalar1=-1.0, scalar2=None, op0=ALU.mult))
            yield
            A(lambda: nc.scalar.activation(out=eg[:], in_=lg[:, 0:4], func=AF.Exp, bias=rt[:, 1:2], scale=1.0, accum_out=rt[:, 2:3]), r=[b_r], w=[b_r])
            yield
            R_(lambda: nc.vector.reciprocal(out=rt[:, 3:4], in_=rt[:, 2:3]))
            yield
            R_(lambda: nc.vector.tensor_scalar(out=pen[:], in0=ohg[:], scalar1=-1.0, scalar2=1e30, op0=ALU.add, op1=ALU.mult))
            yield
            R_(lambda: nc.vector.tensor_tensor(out=em[:].rearrange("p (g e) -> p g e", e=8), in0=lg[:, 4:36].rearrange("p (g e) -> p g e", e=8),
                                               in1=pen[:].unsqueeze(2).to_broadcast([128, 4, 8]), op=ALU.add))
            yield
            R_(lambda: nc.vector.reduce_max(out=rt[:, 4:5], in_=em[:], axis=AX.X))
            yield
            R_(lambda: nc.vector.tensor_scalar(out=oh1[:], in0=em[:], scalar1=rt[:, 4:5], scalar2=None, op0=ALU.is_ge))
            yield
            R_(lambda: nc.vector.scalar_tensor_tensor(out=em2[:], in0=oh1[:], scalar=-1e30, in1=em[:], op0=ALU.mult, op1=ALU.add))
            yield
            R_(lambda: nc.vector.reduce_max(out=rt[:, 5:6], in_=em2[:], axis=AX.X))
            yield
            R_(lambda: nc.vector.tensor_scalar(out=oh2[:], in0=em2[:], scalar1=rt[:, 5:6], scalar2=None, op0=ALU.is_ge))
            yield
            R_(lambda: nc.vector.tensor_tensor(out=rt[:, 6:7], in0=rt[:, 5:6], in1=rt[:, 4:5], op=ALU.subtract))
            yield
            A(lambda: nc.scalar.activation(out=rt[:, 7:8], in_=rt[:, 6:7], func=AF.Exp), r=[b_r], w=[b_r])
            yield
            R_(lambda: nc.vector.tensor_scalar(out=rt[:, 8:9], in0=rt[:, 7:8], scalar1=1.0, scalar2=None, op0=ALU.add))
            yield
            R_(lambda: nc.vector.reciprocal(out=rt[:, 8:9], in_=rt[:, 8:9]))
            yield
            R_(lambda: nc.vector.tensor_tensor(out=rt[:, 9:10], in0=rt[:, 7:8], in1=rt[:, 8:9], op=ALU.mult))
            yield
            R_(lambda: nc.vector.tensor_tensor(out=Mk[:], in0=oh1[:], in1=oh2[:], op=ALU.add))
            yield
            PE(lambda: nc.tensor.matmul(p_k[:, 0:32], lhsT=Ut[:, :], rhs=Mk[:, :], start=True, stop=True), r=[bc, b_r], w=[b_pk])
            yield
            PE(lambda: nc.tensor.matmul(p_k[:, 32:64], lhsT=ones[:, :], rhs=Mk[:, :], start=True, stop=True), r=[bc, b_r], w=[b_pk])
            yield
            V(lambda: nc.vector.tensor_tensor(out=rank[:], in0=p_k[:, 0:32], in1=Srun[:], op=ALU.add), r=[b_S, b_r], w=[b_pk, b_r])
            yield
            V(lambda: nc.vector.tensor_tensor(out=Srun[:], in0=p_k[:, 32:64], in1=Srun[:], op=ALU.add), r=[b_r], w=[b_pk, b_S])
            yield
            for kx, oh in enumerate((oh1, oh2)):
                R_(lambda: nc.vector.tensor_tensor(out=t32[:], in0=oh[:], in1=rank[:], op=ALU.mult))
                yield
                R_(lambda: nc.vector.reduce_sum(out=rt[:, 10:11], in_=t32[:], axis=AX.X))
                yield
                R_(lambda: nc.vector.tensor_tensor(out=t32[:], in0=oh[:], in1=ecap[:], op=ALU.mult))
                yield
                R_(lambda: nc.vector.reduce_sum(out=rt[:, 11:12], in_=t32[:], axis=AX.X))
                yield
                R_(lambda: nc.vector.tensor_scalar(out=rt[:, 12:13], in0=rt[:, 10:11], scalar1=float(CAP), scalar2=None, op0=ALU.is_ge))
                yield
                R_(lambda: nc.vector.tensor_tensor(out=rt[:, 11:12], in0=rt[:, 11:12], in1=rt[:, 10:11], op=ALU.add))
                yield
                R_(lambda: nc.vector.scalar_tensor_tensor(out=rt[:, 11:12], in0=rt[:, 12:13], scalar=1e6, in1=rt[:, 11:12], op0=ALU.mult, op1=ALU.add))
                yield
                V(lambda: nc.vector.tensor_copy(out=slot[:, t, kx:kx + 1], in_=rt[:, 11:12]), r=[b_r], w=[b_rt])
                yield
                R_(lambda: nc.vector.tensor_scalar(out=rt[:, 13:14], in0=rt[:, 12:13], scalar1=-1.0, scalar2=-1.0, op0=ALU.add, op1=ALU.mult))
                yield
                R_(lambda: nc.vector.tensor_tensor(out=rt[:, 13:14], in0=rt[:, 13:14], in1=rt[:, 3:4], op=ALU.mult))
                yield
                V(lambda: nc.vector.tensor_tensor(out=gw[:, t, kx:kx + 1], in0=rt[:, 13:14], in1=rt[:, 8 + kx:9 + kx], op=ALU.mult), r=[b_r], w=[b_rt])
                yield
                mk.idma(Xg, hb[i][:, :], slot[:, t, kx:kx + 1], True, NE * CAP - 1, reads=[b_hb[i], b_rt], writes=[b_Xg])
                yield
        for t in range(NT):
            if t == 0:
                front(0)
            g = tail(t)
            if t + 1 < NT:
                def tick(_g=g):
                    next(_g, None); next(_g, None); next(_g, None)
                front(t + 1, tick)
            for _ in g:
                pass
        mk.barrier()
    b_Yg = Buf("Yg")
    with ExitStack() as pb:
        sb = lambda n, s, dt=F32: pb.enter_context(nc.sbuf_tensor("k3b%d_" % mk.gen + n, list(s), dt))
        ps = lambda n, s, dt=F32: pb.enter_context(nc.psum_tensor("k3b%d_" % mk.gen + n, list(s), dt))
        W1 = [sb("w1_%d" % i, [128, 16, DE], BF16) for i in range(2)]
        W3 = [sb("w3_%d" % i, [128, 16, DE], BF16) for i in range(2)]
        W2 = [sb("w2_%d" % i, [128, 4, D], BF16) for i in range(2)]
        b_w = [Buf(), Buf()]
        NTL = CAP // 128
        xg = sb("xg", [128, NTL, D], BF16); b_xg = Buf()
        xgT = sb("xgT", [128, 16, CAP], BF16); b_xgT = Buf()
        ga = sb("ga", [128, CAP]); b_ga = Buf()
        gh = sb("gh", [128, 4, CAP], BF16); b_gh = Buf()
        yo = [sb("yo%d" % i, [128, D]) for i in range(2)]; b_yo = [Buf(), Buf()]
        p_t = [ps("p_t%d" % i, [128, 1024], BF16) for i in range(2)]; b_pt = [Buf(), Buf()]
        p_a = ps("p_a", [128, 512]); p_b = ps("p_b", [128, 512]); b_pa, b_pb = Buf(), Buf()
        p_o = [ps("p_o%d" % i, [128, 512]) for i in range(2)]; b_po = [Buf(), Buf()]

        def load_w(e):
            j = e % 2
            mk.dma("pool", W1[j][:], w1[e].rearrange("(p k) c -> p k c", k=16), writes=[b_w[j]])
            mk.dma("pool", W3[j][:], w3[e].rearrange("(p k) c -> p k c", k=16), writes=[b_w[j]])
            mk.dma("pool", W2[j][:], w2[e].rearrange("(k p) c -> p k c", p=128), writes=[b_w[j]])

        xgl = [xg, sb("xg_b", [128, NTL, D], BF16)]; b_xgl = [b_xg, Buf()]
        xgTl = [xgT, sb("xgT_b", [128, 16, CAP], BF16)]; b_xgTl = [b_xgT, Buf()]

        def prep(e):
            q_ = e % 2
            xg_, bxg_, xgT_, bxgT_ = xgl[q_], b_xgl[q_], xgTl[q_], b_xgTl[q_]
            mk.dma("sp", xg_[:], Xg[e * CAP:(e + 1) * CAP, :].rearrange("(a p) c -> p a c", p=128), reads=[b_Xg], writes=[bxg_])
            yield
            for a in range(NTL):
                for half in range(2):
                    for kk in range(8):
                        k = half * 8 + kk
                        PE(lambda: nc.tensor.transpose(p_t[half][:, kk * 128:(kk + 1) * 128], xg_[:, a, :].rearrange("t (p k) -> t k p", k=16)[:, k, :], identb[:, :]), r=[bxg_, bc], w=[b_pt[half]])
                    if half == 0:
                        A(lambda: nc.scalar.copy(out=xgT_[:, 0:8, a * 128:(a + 1) * 128], in_=p_t[half][:, :].rearrange("p (k t) -> p k t", t=128)), r=[], w=[b_pt[half], bxgT_])
                    else:
                        V(lambda: nc.vector.tensor_copy(out=xgT_[:, 8:16, a * 128:(a + 1) * 128], in_=p_t[half][:, :].rearrange("p (k t) -> p k t", t=128)), r=[], w=[b_pt[half], bxgT_])
                    yield

        load_w(0)
        yoi = 0
        for _ in prep(0):
            pass
        for e in range(NE):
            j = e % 2
            if e + 1 < NE:
                load_w(e + 1)
            nx = prep(e + 1) if e + 1 < NE else None

            def tick():
                if nx is not None:
                    next(nx, None)
            xgT_ = xgTl[e % 2]; bxgT_ = b_xgTl[e % 2]
            for hc in range(4):
                for k in range(16):
                    PE(lambda: nc.tensor.matmul(p_a[:, 0:CAP], lhsT=W1[j][:, k, hc * 128:(hc + 1) * 128], rhs=xgT_[:, k, :], start=(k == 0), stop=(k == 15)), r=[b_w[j], bxgT_], w=[b_pa])
                for k in range(16):
                    PE(lambda: nc.tensor.matmul(p_b[:, 0:CAP], lhsT=W3[j][:, k, hc * 128:(hc + 1) * 128], rhs=xgT_[:, k, :], start=(k == 0), stop=(k == 15)), r=[b_w[j], bxgT_], w=[b_pb])
                tick()
                A(lambda: nc.scalar.activation(out=ga[:], in_=p_a[:, 0:CAP], func=AF.Silu), r=[], w=[b_pa, b_ga])
                V(lambda: nc.vector.tensor_tensor(out=gh[:, hc, :], in0=p_b[:, 0:CAP], in1=ga[:], op=ALU.mult), r=[b_ga], w=[b_pb, b_gh])
            for a in range(NTL):
                y_ = yoi % 2
                yoi += 1
                for cc in range(4):
                    q = cc % 2
                    for hc in range(4):
                        PE(lambda: nc.tensor.matmul(p_o[q][:, :], lhsT=gh[:, hc, a * 128:(a + 1) * 128], rhs=W2[j][:, hc, cc * 512:(cc + 1) * 512], start=(hc == 0), stop=(hc == 3)), r=[b_gh, b_w[j]], w=[b_po[q]])
                    if q == 0:
                        A(lambda: nc.scalar.copy(out=yo[y_][:, cc * 512:(cc + 1) * 512], in_=p_o[q][:, :]), r=[], w=[b_po[q], b_yo[y_]])
                    else:
                        V(lambda: nc.vector.tensor_copy(out=yo[y_][:, cc * 512:(cc + 1) * 512], in_=p_o[q][:, :]), r=[], w=[b_po[q], b_yo[y_]])
                r0 = e * CAP + a * 128
                mk.dma("sp", Yg[r0:r0 + 128, :], yo[y_][:], reads=[b_yo[y_]], writes=[b_Yg])
            if nx is not None:
                for _ in nx:
                    pass
        mk.barrier()
    with ExitStack() as pc:
        sb = lambda n, s, dt=F32: pc.enter_context(nc.sbuf_tensor("k3c%d_" % mk.gen + n, list(s), dt))
        R = [sb("row%d" % i, [128, D]) for i in range(3)]
        for i, ri in enumerate((5, 6, 7)):
            rowload(R[i], ri, bc)
        G(lambda: nc.gpsimd.tensor_scalar(out=R[0][:], in0=R[0][:], scalar1=1.0, scalar2=None, op0=ALU.add), r=[bc], w=[bc])
        Y = [[sb("Y%d_%d" % (k, i), [128, D]) for i in range(2)] for k in range(2)]
        b_Y = [[Buf(), Buf()], [Buf(), Buf()]]
        for k in range(2):
            for i in range(2):
                V(lambda: nc.vector.memset(Y[k][i][:], 0.0), w=[b_Y[k][i]])
        x1t = [sb("x1t%d" % i, [128, D]) for i in range(2)]; b_x1 = [Buf(), Buf()]
        yml = [sb("ym%d" % i, [128, D]) for i in range(2)]; b_yml = [Buf(), Buf()]
        xnl = [sb("xn%d" % i, [128, D]) for i in range(2)]; b_xnl = [Buf(), Buf()]
        ot = [sb("ot%d" % i, [128, D]) for i in range(2)]; b_ot = [Buf(), Buf()]
        stl = [sb("st%d" % i, [128, 4, 6]) for i in range(2)]; mvl = [sb("mv%d" % i, [128, 2]) for i in range(2)]
        rsl = [sb("rs%d" % i, [128, 1]) for i in range(2)]; nmrl = [sb("nmr%d" % i, [128, 1]) for i in range(2)]; b_sl = [Buf(), Buf()]
        for t in range(NT):
            i = t % 2
            ts_ = slice(t * 128, (t + 1) * 128)
            ym = yml[i]; b_ym = b_yml[i]; xn = xnl[i]; b_xn = b_xnl[i]
            st, mv, rs, nmr, b_s = stl[i], mvl[i], rsl[i], nmrl[i], b_sl[i]
            for k in range(2):
                mk.idma(Y[k][i][:, :], Yg, slot[:, t, k:k + 1], False, NE * CAP - 1, reads=[b_Yg, b_rt], writes=[b_Y[k][i]])
            mk.dma("sp", x1t[i][:], x1s[ts_, :], writes=[b_x1[i]])
            A(lambda: nc.scalar.activation(out=ym[:], in_=Y[0][i][:], func=AF.Copy, scale=gw[:, t, 0:1]), r=[b_Y[0][i], b_rt], w=[b_ym])
            V(lambda: nc.vector.scalar_tensor_tensor(out=ym[:], in0=Y[1][i][:], scalar=gw[:, t, 1:2], in1=ym[:], op0=ALU.mult, op1=ALU.add), r=[b_Y[1][i], b_rt], w=[b_ym])
            V(lambda: nc.vector.tensor_tensor(out=ym[:], in0=ym[:], in1=R[0][:], op=ALU.mult), r=[bc], w=[b_ym])
            V(lambda: nc.vector.scalar_tensor_tensor(out=ym[:], in0=x1t[i][:], scalar=ALPHA, in1=ym[:], op0=ALU.mult, op1=ALU.add), r=[b_x1[i]], w=[b_ym])
            ln_stats(nc, mk, ym, b_ym, st, mv, rs, nmr, b_s)
            A(lambda: nc.scalar.activation(out=xn[:], in_=ym[:], func=AF.Identity, bias=nmr[:, 0:1], scale=rs[:, 0:1]), r=[b_ym, b_s], w=[b_xn])
            V(lambda: nc.vector.tensor_tensor(out=xn[:], in0=xn[:], in1=R[1][:], op=ALU.mult), r=[bc], w=[b_xn])
            V(lambda: nc.vector.tensor_tensor(out=ot[i][:], in0=xn[:], in1=R[2][:], op=ALU.add), r=[b_xn, bc], w=[b_ot[i]])
            mk.dma("sp", xout[ts_, :], ot[i][:], reads=[b_ot[i]], is_output=True)
        mk.barrier()


def build_k3():
    nc = bass.Bass("TRN2", target_bir_lowering=False)
    dt = lambda n, s, k="ExternalInput", d=F32: nc.dram_tensor(n, s, d, kind=k).ap()
    ymixT = dt("ymixT", [D, TOK]); x = dt("x", [TOK, D]); w_out = dt("w_out", [D, D]); glu_w = dt("glu_w", [512, 512]); glu_b = dt("glu_b", [128, 4])
    rows = dt("rows", [8, 128, D]); wr = dt("wr", [D, 36]); br = dt("br", [128, 36])
    w1 = dt("w1", [NE, D, DE]); w3 = dt("w3", [NE, D, DE]); w2 = dt("w2", [NE, DE, D])
    cst = {"U": dt("U", [128, 128]), "ecap": dt("ecap", [128, NE]), "ident": dt("ident", [128, 128])}
    x1s = dt("x1s", [TOK, D], "Internal"); Xg = dt("Xg", [NE * CAP, D], "Internal", BF16); Yg = dt("Yg", [NE * CAP, D], "Internal")
    xout = dt("xout", [TOK, D], "ExternalOutput")
    with ExitStack() as ctx:
        mk = MK(nc, ctx)
        emit_k3(nc, mk, ymixT, x, w_out, glu_w, glu_b, rows, wr, br, w1, w3, w2, cst, x1s, Xg, Yg, xout)
        mk.finish("sp")
        print("k3 ops", mk.nops, "waits", mk.nwaits)
    return nc


def k3_host_inputs(prm, mod, l, b):
    rep = lambda v: np.ascontiguousarray(np.broadcast_to(v[None, :], (128, v.shape[0])))
    sh1, sc1, gt1, sh2, sc2, gt2 = [mod[l, b, i * D:(i + 1) * D] for i in range(6)]
    rows = np.stack([rep(gt1), rep(prm["ln_g"][l, 0]), rep(prm["ln_b"][l, 0]), rep(sc2), rep(sh2), rep(gt2), rep(prm["ln_g"][l, 1]), rep(prm["ln_b"][l, 1])])
    wr = np.ascontiguousarray(np.concatenate([prm["router_group_w"][l], prm["router_expert_w"][l]], axis=1))
    br = rep(np.concatenate([prm["router_group_b"][l], prm["router_expert_b"][l]]))
    d = {"rows": rows, "wr": wr, "br": br, "w_out": prm["w_out"][l], "glu_w": prm["s5_glu_w"][l],
         "glu_b": np.ascontiguousarray(prm["s5_glu_b"][l].reshape(4, 128).T),
         "w1": prm["moe_w1"][l], "w3": prm["moe_w3"][l], "w2": prm["moe_w2"][l]}
    d.update(k3_consts())
    return d


G_ = 512
RW_OFF = 3 * G_
RW_COLS = 3 * G_ + 96 + 96 + 128
ATT_OFF = RW_OFF + RW_COLS
S5_OFF = ATT_OFF + 512 + 2 * 128
SEQ = 8192
RG4 = [[0, 1, 2, 3], [4, 5, 6, 7]]
NMINE = 1472
MODC = 3072


def emit_k0f(nc, mk, cT, w, bb, modin):
    ct = mk.sb("k0_ct", [128, 16, 2]); sct = mk.sb("k0_sct", [128, 16, 2])
    wt = [mk.sb("k0_wt%d" % i, [128, 16, 512]) for i in range(2)]
    bt = mk.sb("k0_bt", [2, 2, MODC]); ot = mk.sb("k0_ot", [2, 2, MODC])
    P = [mk.ps("k0_P%d" % i, [2, 512]) for i in range(2)]
    b_c, b_b, b_o = Buf(), Buf(), Buf()
    b_w = [Buf(), Buf()]; b_p = [Buf(), Buf()]
    mk.dma("sp", ct[:], cT, writes=[b_c])
    mk.dma("sp", bt[:], bb.rearrange("l b c -> b l c"), writes=[b_b])
    mk.op("act", lambda: nc.scalar.activation(out=sct[:], in_=ct[:], func=AF.Silu), reads=[b_c], writes=[b_c])
    it = 0
    for l in range(2):
        for n in range(MODC // 512):
            i = it % 2
            it += 1
            mk.dma("sp", wt[i][:], w[l, :, n * 512:(n + 1) * 512].rearrange("(k p) c -> p k c", p=128), writes=[b_w[i]])
            for k in range(16):
                mk.op("pe", lambda: nc.tensor.matmul(P[i][:], lhsT=sct[:, k, :], rhs=wt[i][:, k, :], start=(k == 0), stop=(k == 15)),
                      reads=[b_c, b_w[i]], writes=[b_p[i]], skip_same=True)
            mk.op("dve", lambda: nc.vector.tensor_tensor(out=ot[:, l, n * 512:(n + 1) * 512], in0=P[i][:], in1=bt[:, l, n * 512:(n + 1) * 512], op=ALU.add),
                  reads=[b_b], writes=[b_p[i], b_o])
    mk.dma("sp", modin.rearrange("(o l) c -> o l c", o=1), ot[0:1, :, :], reads=[b_o])


def mod_row_load(nc, mk, dst, modall, l, chunk, bc):
    c0 = chunk * 2048
    done = 0
    while done < 2048:
        col = c0 + done
        r = col // MODC
        off = col % MODC
        n = min(2048 - done, MODC - off)
        src = modall[r * 2 + l:r * 2 + l + 1, off:off + n].partition_broadcast(128)
        mk.dma("sp", dst[:, done:done + n], src, writes=[bc])
        done += n


def emit_k1a(nc, mk, x, modall, l, ident_d, hTs):
    NT_ = TOK // 128
    xt = [mk.sb("a_xt%d" % i, [128, D]) for i in range(2)]
    xn = mk.sb("a_xn", [128, D]); h1 = mk.sb("a_h1", [128, D])
    hb = [mk.sb("a_hb%d" % i, [128, D], BF16) for i in range(2)]
    hT = mk.sb("a_hT", [128, 16, TOK], BF16)
    sct = mk.sb("a_sct", [128, D]); sht = mk.sb("a_sht", [128, D])
    idf = mk.sb("a_idf", [128, 128]); idb = mk.sb("a_idb", [128, 128], BF16)
    st = mk.sb("a_st", [128, 4, 6]); mv = mk.sb("a_mv", [128, 2]); rs = mk.sb("a_rs", [128, 1]); nmr = mk.sb("a_nmr", [128, 1])
    PT = [mk.ps("a_PT%d" % i, [128, 8, 128], BF16) for i in range(2)]
    b_x = [Buf(), Buf()]
    b_xn, b_h1, b_s, b_sc, b_sh, b_id, b_hT = Buf(), Buf(), Buf(), Buf(), Buf(), Buf(), Buf()
    b_hb = [Buf(), Buf()]; b_pt = [Buf(), Buf()]
    mod_row_load(nc, mk, sct, modall, l, 1, b_sc)
    mod_row_load(nc, mk, sht, modall, l, 0, b_sh)
    mk.dma("sp", idf[:], ident_d, writes=[b_id])
    mk.op("dve", lambda: nc.vector.tensor_copy(out=idb[:], in_=idf[:]), reads=[b_id], writes=[b_id])
    mk.op("pool", lambda: nc.gpsimd.tensor_scalar(out=sct[:], in0=sct[:], scalar1=1.0, scalar2=None, op0=ALU.add), reads=[b_sc], writes=[b_sc])
    def ln_part(t):
        i = t % 2
        mk.dma("sp", xt[i][:], x[t * 128:(t + 1) * 128, :], writes=[b_x[i]])
        ln_stats(nc, mk, xt[i], b_x[i], st, mv, rs, nmr, b_s)
        mk.op("act", lambda: nc.scalar.activation(out=xn[:], in_=xt[i][:], func=AF.Identity, bias=nmr[:, 0:1], scale=rs[:, 0:1]),
              reads=[b_x[i], b_s], writes=[b_xn])
        mk.op("dve", lambda: nc.vector.tensor_tensor(out=h1[:], in0=xn[:], in1=sct[:], op=ALU.mult), reads=[b_xn, b_sc], writes=[b_h1])
        mk.op("dve", lambda: nc.vector.tensor_tensor(out=hb[i][:], in0=h1[:], in1=sht[:], op=ALU.add), reads=[b_h1, b_sh], writes=[b_hb[i]])
    def tr_part(t):
        i = t % 2
        for half in range(2):
            for kk in range(8):
                k = half * 8 + kk
                mk.op("pe", lambda: nc.tensor.transpose(PT[half][:, kk, :], hb[i][:, k * 128:(k + 1) * 128], idb[:]),
                      reads=[b_hb[i], b_id], writes=[b_pt[half]], skip_same=True)
            if half == 0:
                mk.op("act", lambda: nc.scalar.copy(out=hT[:, 0:8, t * 128:(t + 1) * 128], in_=PT[half][:]), reads=[], writes=[b_pt[half], b_hT])
            else:
                mk.op("dve", lambda: nc.vector.tensor_copy(out=hT[:, 8:16, t * 128:(t + 1) * 128], in_=PT[half][:]), reads=[], writes=[b_pt[half], b_hT])
    ln_part(0)
    for t in range(NT_):
        if t + 1 < NT_:
            ln_part(t + 1)
        tr_part(t)
    mk.dma("sp", hTs.rearrange("(k p) t -> p k t", p=128), hT[:], reads=[b_hT])


def emit_k1b(nc, mk, hTg, wmine, pmine):
    NB_ = (NMINE + 127) // 128
    wt = mk.sb("b_wt", [128, 16, NB_ * 128], BF16); b_w = Buf()
    ht = [mk.sb("b_ht%d" % i, [128, 16, 512], BF16) for i in range(2)]; b_h = [Buf(), Buf()]
    ot = [mk.sb("b_ot%d" % i, [128, 512]) for i in range(4)]; b_o = [Buf() for _ in range(4)]
    PM = [mk.ps("b_PM%d" % i, [128, 512]) for i in range(4)]; b_pm = [Buf() for _ in range(4)]
    for j in range(NB_):
        c0 = j * 128
        cw = min(128, NMINE - c0)
        mk.dma("pool", wt[:, :, c0:c0 + cw], wmine[:, c0:c0 + cw].rearrange("(k p) c -> p k c", p=128), writes=[b_w])
    pi = 0
    for tc in range(SEQ // 512):
        i = tc % 2
        r = tc // 4
        t0 = (tc % 4) * 512
        src = hTg.rearrange("(c r h p) t -> r p c h t", c=8, r=4, h=2, p=128)[r]
        for c in range(8):
            mk.dma("sp", ht[i][:, 2 * c:2 * c + 2, :], src[:, c, :, t0:t0 + 512], writes=[b_h[i]])
        for j in range(NB_):
            c0 = j * 128
            cw = min(128, NMINE - c0)
            q = pi % 4
            pi += 1
            for k in range(16):
                mk.op("pe", lambda: nc.tensor.matmul(PM[q][0:cw, :], lhsT=wt[:, k, c0:c0 + cw], rhs=ht[i][:, k, :], start=(k == 0), stop=(k == 15)),
                      reads=[b_w, b_h[i]], writes=[b_pm[q]], skip_same=True)
            if q % 2 == 0:
                mk.op("act", lambda: nc.scalar.copy(out=ot[q][0:cw, :], in_=PM[q][0:cw, :]), reads=[], writes=[b_pm[q], b_o[q]])
            else:
                mk.op("dve", lambda: nc.vector.tensor_copy(out=ot[q][0:cw, :], in_=PM[q][0:cw, :]), reads=[], writes=[b_pm[q], b_o[q]])
            mk.dma("sp", pmine[c0:c0 + cw, tc * 512:(tc + 1) * 512], ot[q][0:cw, :], reads=[b_o[q]])


def build_fused():
    nc = bass.Bass("TRN2", target_bir_lowering=False)
    T = SEQ
    din = lambda n, s, d=F32: nc.dram_tensor(n, s, d, kind="ExternalInput").ap()
    scr = lambda n, s, d=F32: nc.dram_tensor(n, s, d).ap()
    x_in = din("x", [TOK, D]); cT = din("cT", [128, 16, 2]); w_ada = din("w_ada", [2, D, MODC]); bb = din("bb", [2, 2, MODC])
    wmine = din("wmine", [2, D, NMINE]); w_out = din("w_out", [2, D, D]); lnp = din("lnp", [2, 4, D])
    cw = din("cw", [2, 128, 3]); btab = din("btab", [2, 2, 128, 2, 256]); sinkt = din("sinkt", [2, 128, 2]); ident = din("ident", [128, 128])
    s5par = din("s5par", [2, 128, 4, 3]); s5bb = din("s5bb", [2, 128, 4, 2, 16]); s5cc = din("s5cc", [2, 128, 4, 2, 16]); s5d = din("s5d", [2, 128, 1]); s5iota = din("s5iota", [128, CS])
    par64 = din("par64", [2, 64, 2, 11]); par128 = din("par128", [2, 128, 3]); w2 = din("w2", [2, 96, 128]); a2 = din("a2", [2, 96, 128]); g2 = din("g2", [2, 128, 128]); gnt = din("gnt", [2, 64, 2, 2, 64])
    cst = {"mask1": din("mask1", [64, 512]), "mask3": din("mask3", [64, 256]), "seg": din("seg", [128, TB]), "ident": ident, "U": din("U", [128, 128]), "ecap": din("ecap", [128, NE])}
    glu_w = din("glu_w", [2, 512, 512]); glu_b = din("glu_b", [2, 128, 4]); wr = din("wr", [2, D, 36]); br = din("br", [2, 128, 36])
    w1 = din("w1", [2, NE, D, DE]); w3 = din("w3", [2, NE, D, DE]); w2m = din("w2m", [2, NE, DE, D])
    ymidx_d = din("ymidx", [128, 16, 2], I32)
    y_out = nc.dram_tensor("y", [TOK, D], F32, kind="ExternalOutput").ap()
    modin = scr("modin", [2, MODC]); modall = scr("modall", [8, MODC])
    hTs = scr("hTs", [D, TOK], BF16); hTg = scr("hTg", [4 * D, TOK], BF16)
    pmine = scr("pmine", [NMINE, T])
    yT16 = scr("yT16", [512, T], BF16); ymg = scr("ymg", [2048, T], BF16)
    x1s = scr("x1s", [TOK, D]); Xg = scr("Xg", [NE * CAP, D], BF16); Yg = scr("Yg", [NE * CAP, D]); xcur = scr("xcur", [TOK, D])
    ymg_rows = ymg.rearrange("r (tb t) -> (r tb) t", t=1024)
    with ExitStack() as ctx:
        mk = MK(nc, ctx)
        bD = Buf("dram")
        with mk.scope():
            emit_k0f(nc, mk, cT, w_ada, bb, modin)
        mk.collective("AllGather", RG4, modin, modall, reads=[bD], writes=[bD])
        mk.barrier()
        for l in range(2):
            xsrc = x_in if l == 0 else xcur
            xdst = xcur if l == 0 else y_out
            with mk.scope():
                emit_k1a(nc, mk, xsrc, modall, l, ident, hTs)
            for c in range(8):
                mk.collective("AllGather", RG4, hTs[c * 256:(c + 1) * 256, :], hTg[c * 1024:(c + 1) * 1024, :], reads=[], writes=[Buf()])
            mk.barrier()
            with mk.scope():
                emit_k1b(nc, mk, hTg, wmine[l], pmine)
            def ag(chunks):
                for c in chunks:
                    mk.collective("AllGather", RG4, yT16[c * 64:(c + 1) * 64, :], ymg[c * 256:(c + 1) * 256, :], reads=[], writes=[Buf()])
            with mk.scope():
                cg = emit_conv_gen(nc, mk, T, pmine[0:384, :], cw[l], yT16[0:128, :], odt=BF16)
                emit_attn(nc, mk, T, pmine[1088:1344, :], btab[l], sinkt[l], ident, yT16[256:384, :], odt=BF16, tick=lambda: next(cg, None))
                for _ in cg:
                    pass
            ag((0, 1))
            ag((4, 5))
            with mk.scope():
                emit_s5(nc, mk, T, pmine[1344:1472, :], s5par[l], s5bb[l], s5cc[l], s5d[l], s5iota, ident, yT16[384:512, :], odt=BF16)
            ag((6, 7))
            with mk.scope():
                emit_rwkv(nc, mk, T, pmine[384:1088, :], par64[l], par128[l], w2[l], a2[l], g2[l], gnt[l], cst, yT16[128:256, :], odt=BF16)
            ag((2, 3))
            mk.barrier()
            with mk.scope():
                ymidx = mk.sb("ymidx_sb", [128, 16, 2], I32); b_idx = Buf()
                mk.dma("sp", ymidx[:], ymidx_d, writes=[b_idx])

                def ymload(hf, ymt, b_ymt):
                    for k in range(16):
                        mk.idma(ymt[:, k, :], ymg_rows, ymidx[:, k, hf:hf + 1], False, 2048 * 8 - 1, reads=[b_idx], writes=[b_ymt])

                def rowload(dst, ri, bcx, _l=l):
                    if ri in (1, 2, 6, 7):
                        j = {1: 0, 2: 1, 6: 2, 7: 3}[ri]
                        mk.dma("sp", dst[:], lnp[_l, j:j + 1, :].partition_broadcast(128), writes=[bcx])
                    else:
                        chunk = {0: 2, 3: 4, 4: 3, 5: 5}[ri]
                        mod_row_load(nc, mk, dst, modall, _l, chunk, bcx)

                emit_k3(nc, mk, None, xsrc, w_out[l], glu_w[l], glu_b[l], None, wr[l], br[l], w1[l], w3[l], w2m[l], cst, x1s, Xg, Yg, xdst,
                        ymload=ymload, rowload=rowload)
        mk.finish("sp")
        mk.barrier()
        print("fused ops", mk.nops, "waits", mk.nwaits)
    return nc


_NC_CACHE = {}


def _get(name, fn):
    if name not in _NC_CACHE:
        _NC_CACHE[name] = fn()
    return _NC_CACHE[name]


def _fused_inputs(prm, core):
    b, q = core // 4, core % 4
    eye = np.eye(128, dtype=np.float32)
    d = {}
    d["x"] = np.ascontiguousarray(prm["x"][b, q * TOK:(q + 1) * TOK])
    cb = prm["c"][b]
    d["cT"] = np.ascontiguousarray(np.stack([cb.reshape(16, 128).T, cb.reshape(16, 128).T], axis=-1))
    sl = slice(q * MODC, (q + 1) * MODC)
    d["w_ada"] = np.ascontiguousarray(prm["w_ada"][:, :, sl])
    d["bb"] = np.ascontiguousarray(np.broadcast_to(prm["b_ada"][:, None, sl], (2, 2, MODC)))
    kv = q // 2
    cols = np.concatenate([np.arange(128 * q, 128 * q + 128), np.arange(G_ + 128 * q, G_ + 128 * q + 128), np.arange(2 * G_ + 128 * q, 2 * G_ + 128 * q + 128),
                           RW_OFF + rwkv_rows(q),
                           np.arange(ATT_OFF + 128 * q, ATT_OFF + 128 * q + 128), np.arange(ATT_OFF + 512 + 64 * kv, ATT_OFF + 512 + 64 * kv + 64),
                           np.arange(ATT_OFF + 640 + 64 * kv, ATT_OFF + 640 + 64 * kv + 64),
                           np.arange(S5_OFF + 128 * q, S5_OFF + 128 * q + 128)])
    assert cols.shape[0] == NMINE
    d["wmine"] = np.ascontiguousarray(prm["w_in"][:, :, cols])
    d["w_out"] = prm["w_out"]
    d["lnp"] = np.ascontiguousarray(np.stack([np.stack([prm["ln_g"][l, 0], prm["ln_b"][l, 0], prm["ln_g"][l, 1], prm["ln_b"][l, 1]]) for l in range(2)]))
    d["cw"] = np.ascontiguousarray(np.stack([prm["conv_w"][l][:, 128 * q:128 * q + 128].T for l in range(2)]))
    tabs = [attn_tables(prm["rel_bias"], prm["attn_sinks"][l], q) for l in range(2)]
    d["btab"] = np.ascontiguousarray(np.stack([t[0] for t in tabs])); d["sinkt"] = np.ascontiguousarray(np.stack([t[1] for t in tabs]))
    d["ident"] = eye
    s5 = [s5_host_inputs(prm, l, q) for l in range(2)]
    for k_ in ("s5par", "s5bb", "s5cc", "s5d"):
        d[k_] = np.ascontiguousarray(np.stack([s[k_] for s in s5]))
    d["s5iota"] = s5[0]["s5iota"]
    rw = [rwkv_host_inputs(prm, l, q) for l in range(2)]
    for k_ in ("par64", "par128", "gnt", "w2", "a2", "g2"):
        d[k_] = np.ascontiguousarray(np.stack([r[k_] for r in rw]))
    for k_ in ("mask1", "mask3", "seg"):
        d[k_] = rw[0][k_]
    kc = k3_consts()
    d["U"] = kc["U"]; d["ecap"] = kc["ecap"]
    d["glu_w"] = prm["s5_glu_w"]
    d["glu_b"] = np.ascontiguousarray(np.stack([prm["s5_glu_b"][l].reshape(4, 128).T for l in range(2)]))
    d["wr"] = np.ascontiguousarray(np.stack([np.concatenate([prm["router_group_w"][l], prm["router_expert_w"][l]], axis=1) for l in range(2)]))
    d["br"] = np.ascontiguousarray(np.stack([np.broadcast_to(np.concatenate([prm["router_group_b"][l], prm["router_expert_b"][l]])[None, :], (128, 36)) for l in range(2)]))
    d["w1"] = prm["moe_w1"]; d["w3"] = prm["moe_w3"]; d["w2m"] = prm["moe_w2"]
    p_ = np.arange(128)[:, None, None]; k_i = np.arange(16)[None, :, None]; t_ = np.arange(2)[None, None, :]
    src_row = ((k_i // 4) * 2 + p_ // 64) * 256 + (k_i % 4) * 64 + p_ % 64
    d["ymidx"] = np.ascontiguousarray((src_row * 8 + q * 2 + t_).astype(np.int32))
    return d


def kernel(**inp):
    prm = {k: np.ascontiguousarray(np.asarray(v, dtype=np.float32)) for k, v in inp.items()}
    cores = list(range(8))
    in_maps = [_fused_inputs(prm, c) for c in cores]
    res = run_bass_kernel_spmd(_get("fused", build_fused), in_maps, core_ids=cores)
    out = np.stack([np.concatenate([res.results[b * 4 + q]["y"] for q in range(4)], axis=0) for b in range(2)])
    return out.astype(np.float32)
```
